# Optimizing a Trainium2 kernel written in Bass

```python
import jax
import jax.numpy as jnp
from jax import lax
import numpy as np

D_MODEL = 1024
BATCH = 8
SEQ = 2048
DEPTH = 2
DEC_BATCH = 32
DEC_SEQ = 4
PAST_LEN = 16384
PAGE_SIZE = 128

N_EVEN = (DEPTH + 1) // 2
N_ODD = DEPTH // 2
POOL_DIM = D_MODEL // 2
POOL_WINDOWS = (2, 4, 8, 16)
POOL_GROUP = POOL_DIM // len(POOL_WINDOWS)
POOL_HIST = max(POOL_WINDOWS) - 1
RET_HEADS = 4
RET_DK = 128
RET_DV = 128
RET_CHUNK = 128
EVEN_IN = POOL_DIM + 2 * RET_HEADS * RET_DK + 2 * RET_HEADS * RET_DV
EVEN_OUT = POOL_DIM + RET_HEADS * RET_DV
MLA_HEADS = 8
QK_NOPE = 128
QK_ROPE = 64
V_DIM = 128
Q_LORA = 512
KV_LORA = 256
MLA_SCALE = (QK_NOPE + QK_ROPE) ** -0.5
ATTN_Q_BLOCK = 128
D_FF = 2816
CONV_W = 3
ALPHA = (2.0 * DEPTH) ** 0.25
BETA = (8.0 * DEPTH) ** -0.25
ROPE_THETA = 10000.0
LN_EPS = 1e-5
RMS_EPS = 1e-6
GN_EPS = 1e-6

kernel_name = 'hybrid_pool_retention_mla_convffn_step'

F32 = jnp.float32


def layer_norm(x, g, b):
    xf = x.astype(F32)
    mu = jnp.mean(xf, axis=-1, keepdims=True)
    var = jnp.mean(jnp.square(xf - mu), axis=-1, keepdims=True)
    return ((xf - mu) * lax.rsqrt(var + LN_EPS) * g.astype(F32) + b.astype(F32)).astype(x.dtype)


def rms_norm(x, g):
    xf = x.astype(F32)
    ms = jnp.mean(jnp.square(xf), axis=-1, keepdims=True)
    return (xf * lax.rsqrt(ms + RMS_EPS) * g.astype(F32)).astype(x.dtype)


def rope(x, pos):
    half = x.shape[-1] // 2
    inv_freq = ROPE_THETA ** (-jnp.arange(half, dtype=F32) / half)
    ang = pos[:, None] * inv_freq[None, :]
    shape = (pos.shape[0],) + (1,) * (x.ndim - 3) + (half,)
    cos = jnp.cos(ang).reshape(shape)
    sin = jnp.sin(ang).reshape(shape)
    xf = x.astype(F32)
    x1, x2 = xf[..., :half], xf[..., half:]
    return jnp.concatenate([x1 * cos - x2 * sin, x2 * cos + x1 * sin], axis=-1).astype(x.dtype)


def pool_mixer(u, hist, pos, pool_w, pool_scale):
    B, L, _ = u.shape
    u_ext = jnp.concatenate([hist.astype(u.dtype), u], axis=1)
    cs = jnp.cumsum(u_ext.astype(F32), axis=1)
    cs = jnp.concatenate([jnp.zeros_like(cs[:, :1]), cs], axis=1)
    end = cs[:, POOL_HIST + 1:]
    outs = []
    for gi, w in enumerate(POOL_WINDOWS):
        sl = slice(gi * POOL_GROUP, (gi + 1) * POOL_GROUP)
        start = cs[:, POOL_HIST + 1 - w:POOL_HIST + 1 - w + L, sl]
        cnt = jnp.minimum(float(w), pos + 1.0)[None, :, None]
        outs.append((end[..., sl] - start) / cnt - u[..., sl].astype(F32))
    pooled = jnp.stack(outs, axis=2).astype(u.dtype)
    mixed = jnp.einsum('blgc,gcd->blgd', pooled, pool_w).reshape(B, L, POOL_DIM)
    return mixed * pool_scale, u_ext[:, -POOL_HIST:]


def retention_chunk(q, k, v, S, log_gamma):
    C = q.shape[1]
    idx = jnp.arange(C, dtype=F32)
    diff = idx[:, None] - idx[None, :]
    decay = jnp.where(diff >= 0, jnp.exp(jnp.maximum(diff, 0.0)[None] * log_gamma[:, None, None]), 0.0).astype(q.dtype)
    scores = jnp.einsum('blhk,bmhk->bhlm', q, k) * decay
    o = jnp.einsum('bhlm,bmhv->blhv', scores, v)
    q_dec = jnp.exp((idx + 1.0)[:, None] * log_gamma[None, :]).astype(q.dtype)
    o = o + jnp.einsum('blhk,bhkv->blhv', q * q_dec[None, :, :, None], S)
    k_dec = jnp.exp((C - 1.0 - idx)[:, None] * log_gamma[None, :]).astype(q.dtype)
    S_new = jnp.exp(C * log_gamma)[None, :, None, None].astype(S.dtype) * S + jnp.einsum('blhk,blhv->bhkv', k * k_dec[None, :, :, None], v)
    return o, S_new.astype(S.dtype)


def retention(q, k, v, S0, log_gamma):
    B, L = q.shape[:2]
    C = RET_CHUNK if L % RET_CHUNK == 0 else L
    nc = L // C

    def to_chunks(t):
        return jnp.moveaxis(t.reshape((B, nc, C) + t.shape[2:]), 1, 0)

    def step(S, blk):
        qc, kc, vc = blk
        o, S = retention_chunk(qc, kc, vc, S, log_gamma)
        return S, o

    S, o = lax.scan(step, S0, (to_chunks(q), to_chunks(k), to_chunks(v)))
    o = jnp.moveaxis(o, 0, 1).reshape(B, L, RET_HEADS, RET_DV)
    return o, S


def even_mixer(x, pos, pool_hist, ret_state, w_in, pool_w, pool_scale, gn_g, w_o):
    B, L, _ = x.shape
    h = x @ w_in
    qk = RET_HEADS * RET_DK
    vd = RET_HEADS * RET_DV
    o0 = POOL_DIM
    u = h[..., :o0]
    q = h[..., o0:o0 + qk].reshape(B, L, RET_HEADS, RET_DK)
    k = h[..., o0 + qk:o0 + 2 * qk].reshape(B, L, RET_HEADS, RET_DK)
    v = h[..., o0 + 2 * qk:o0 + 2 * qk + vd].reshape(B, L, RET_HEADS, RET_DV)
    g = h[..., o0 + 2 * qk + vd:]
    pool_out, pool_hist_new = pool_mixer(u, pool_hist, pos, pool_w, pool_scale)
    q = rope(q, pos)
    k = rope(k, pos) * (RET_DK ** -0.5)
    log_gamma = jnp.log(1.0 - 2.0 ** (-5.0 - jnp.arange(RET_HEADS, dtype=F32)))
    o, S = retention(q, k, v, ret_state.astype(x.dtype), log_gamma)
    of = o.astype(F32)
    mu = jnp.mean(of, axis=-1, keepdims=True)
    var = jnp.mean(jnp.square(of - mu), axis=-1, keepdims=True)
    on = (of - mu) * lax.rsqrt(var + GN_EPS) * gn_g.astype(F32).reshape(RET_HEADS, RET_DV)
    ret_out = (jax.nn.silu(g.astype(F32)) * on.reshape(B, L, vd)).astype(x.dtype)
    y = jnp.concatenate([pool_out.astype(x.dtype), ret_out], axis=-1) @ w_o
    return y, pool_hist_new, S


def mla_project(x, pos, w_dq, q_norm_g, w_uq, w_dkv, kv_norm_g):
    B, L, _ = x.shape
    q = (rms_norm(x @ w_dq, q_norm_g) @ w_uq).reshape(B, L, MLA_HEADS, QK_NOPE + QK_ROPE)
    q_nope = q[..., :QK_NOPE]
    q_pe = rope(q[..., QK_NOPE:], pos)
    kv = x @ w_dkv
    c_kv = rms_norm(kv[..., :KV_LORA], kv_norm_g)
    k_pe = rope(kv[..., KV_LORA:], pos)
    return q_nope, q_pe, c_kv, k_pe


def mla_prompt(x, pos, w_dq, q_norm_g, w_uq, w_dkv, kv_norm_g, w_uk, w_uv, w_o):
    B, L, _ = x.shape
    q_nope, q_pe, c_kv, k_pe = mla_project(x, pos, w_dq, q_norm_g, w_uq, w_dkv, kv_norm_g)
    k_nope = jnp.einsum('bsc,chd->bshd', c_kv, w_uk)
    v = jnp.einsum('bsc,chd->bshd', c_kv, w_uv)
    QB = ATTN_Q_BLOCK if L % ATTN_Q_BLOCK == 0 else L
    nb = L // QB
    key_pos = jnp.arange(L)

    def to_blocks(t):
        return jnp.moveaxis(t.reshape((B, nb, QB) + t.shape[2:]), 1, 0)

    def block(args):
        qn, qp, i = args
        s = jnp.einsum('bqhd,bkhd->bhqk', qn, k_nope) + jnp.einsum('bqhr,bkr->bhqk', qp, k_pe)
        s = s.astype(F32) * MLA_SCALE
        q_pos = i * QB + jnp.arange(QB)
        s = jnp.where(key_pos[None, :] <= q_pos[:, None], s, -jnp.inf)
        p = jax.nn.softmax(s, axis=-1).astype(v.dtype)
        return jnp.einsum('bhqk,bkhd->bqhd', p, v)

    o = lax.map(block, (to_blocks(q_nope), to_blocks(q_pe), jnp.arange(nb)))
    o = jnp.moveaxis(o, 0, 1).reshape(B, L, MLA_HEADS * V_DIM)
    return o @ w_o, c_kv, k_pe


def mla_sample(x, pos, cache_ckv, cache_kpe, layer_idx, page_table, w_dq, q_norm_g, w_uq, w_dkv, kv_norm_g, w_uk, w_uv, w_o):
    B, L, _ = x.shape
    q_nope, q_pe, c_kv, k_pe = mla_project(x, pos, w_dq, q_norm_g, w_uq, w_dkv, kv_norm_g)
    n_pages = page_table.shape[1]
    P = n_pages * PAGE_SIZE
    ckv_past = cache_ckv[layer_idx, page_table].reshape(B, P, KV_LORA).astype(x.dtype)
    kpe_past = cache_kpe[layer_idx, page_table].reshape(B, P, QK_ROPE).astype(x.dtype)
    q_lat = jnp.einsum('bqhd,chd->bqhc', q_nope, w_uk)
    s_past = jnp.einsum('bqhc,bkc->bhqk', q_lat, ckv_past) + jnp.einsum('bqhr,bkr->bhqk', q_pe, kpe_past)
    s_new = jnp.einsum('bqhc,bkc->bhqk', q_lat, c_kv) + jnp.einsum('bqhr,bkr->bhqk', q_pe, k_pe)
    s = jnp.concatenate([s_past, s_new], axis=-1).astype(F32) * MLA_SCALE
    causal = jnp.arange(L)[None, :] <= jnp.arange(L)[:, None]
    mask = jnp.concatenate([jnp.ones((L, P), dtype=bool), causal], axis=-1)
    s = jnp.where(mask, s, -jnp.inf)
    p = jax.nn.softmax(s, axis=-1).astype(x.dtype)
    o_lat = jnp.einsum('bhqk,bkc->bqhc', p[..., :P], ckv_past) + jnp.einsum('bhqk,bkc->bqhc', p[..., P:], c_kv)
    o = jnp.einsum('bqhc,chd->bqhd', o_lat, w_uv).reshape(B, L, MLA_HEADS * V_DIM)
    return o @ w_o, c_kv, k_pe


def conv_ffn(x, hist, w_up, conv_w, conv_b, w_down):
    L = x.shape[1]
    h = x @ w_up
    a, b = h[..., :D_FF], h[..., D_FF:]
    a_ext = jnp.concatenate([hist.astype(a.dtype), a], axis=1)
    conv = conv_b
    for j in range(CONV_W):
        conv = conv + conv_w[j] * a_ext[:, j:j + L]
    y = (jax.nn.silu(conv) * b) @ w_down
    return y, a_ext[:, L:]


def setup_inputs(seed: int = 0) -> dict:
    key = jax.random.key(seed)
    ks = jax.random.split(key, 40)

    def nrm(i, shape, scale):
        return jax.random.normal(ks[i], shape, F32) * scale

    n_pages = PAST_LEN // PAGE_SIZE
    n_pool = (DEC_BATCH * n_pages * 5) // 4
    page_table = jax.random.permutation(ks[0], n_pool)[:DEC_BATCH * n_pages].reshape(DEC_BATCH, n_pages).astype(jnp.int32)
    return {
        'x_prompt': nrm(1, (BATCH, SEQ, D_MODEL), 1.0),
        'x_sample': nrm(2, (DEC_BATCH, DEC_SEQ, D_MODEL), 1.0),
        'state_pool': nrm(3, (N_EVEN, DEC_BATCH, POOL_HIST, POOL_DIM), 1.0),
        'state_ret': nrm(4, (N_EVEN, DEC_BATCH, RET_HEADS, RET_DK, RET_DV), 0.3),
        'cache_ckv': nrm(5, (N_ODD, n_pool, PAGE_SIZE, KV_LORA), 1.0),
        'cache_kpe': nrm(6, (N_ODD, n_pool, PAGE_SIZE, QK_ROPE), 1.0),
        'state_conv': nrm(7, (DEPTH, DEC_BATCH, CONV_W - 1, D_FF), 1.0),
        'page_table': page_table,
        'w_in_even': nrm(8, (N_EVEN, D_MODEL, EVEN_IN), D_MODEL ** -0.5),
        'pool_w': nrm(9, (N_EVEN, len(POOL_WINDOWS), POOL_GROUP, POOL_GROUP), POOL_GROUP ** -0.5),
        'pool_scale': 1.0 + nrm(10, (N_EVEN, POOL_DIM), 0.1),
        'ret_gn_g': 1.0 + nrm(11, (N_EVEN, RET_HEADS * RET_DV), 0.1),
        'w_o_even': nrm(12, (N_EVEN, EVEN_OUT, D_MODEL), EVEN_OUT ** -0.5 * BETA),
        'w_dq': nrm(13, (N_ODD, D_MODEL, Q_LORA), D_MODEL ** -0.5),
        'q_norm_g': 1.0 + nrm(14, (N_ODD, Q_LORA), 0.1),
        'w_uq': nrm(15, (N_ODD, Q_LORA, MLA_HEADS * (QK_NOPE + QK_ROPE)), Q_LORA ** -0.5),
        'w_dkv': nrm(16, (N_ODD, D_MODEL, KV_LORA + QK_ROPE), D_MODEL ** -0.5),
        'kv_norm_g': 1.0 + nrm(17, (N_ODD, KV_LORA), 0.1),
        'w_uk': nrm(18, (N_ODD, KV_LORA, MLA_HEADS, QK_NOPE), KV_LORA ** -0.5),
        'w_uv': nrm(19, (N_ODD, KV_LORA, MLA_HEADS, V_DIM), KV_LORA ** -0.5),
        'w_o_mla': nrm(20, (N_ODD, MLA_HEADS * V_DIM, D_MODEL), (MLA_HEADS * V_DIM) ** -0.5 * BETA),
        'w_up': nrm(21, (DEPTH, D_MODEL, 2 * D_FF), D_MODEL ** -0.5),
        'conv_w': nrm(22, (DEPTH, CONV_W, D_FF), 0.5),
        'conv_b': nrm(23, (DEPTH, D_FF), 0.02),
        'w_down': nrm(24, (DEPTH, D_FF, D_MODEL), D_FF ** -0.5 * BETA),
        'ln_mix_g': 1.0 + nrm(25, (DEPTH, D_MODEL), 0.05),
        'ln_mix_b': nrm(26, (DEPTH, D_MODEL), 0.02),
        'ln_ffn_g': 1.0 + nrm(27, (DEPTH, D_MODEL), 0.05),
        'ln_ffn_b': nrm(28, (DEPTH, D_MODEL), 0.02),
    }


def reference(x_prompt, x_sample, state_pool, state_ret, cache_ckv, cache_kpe, state_conv, page_table,
              w_in_even, pool_w, pool_scale, ret_gn_g, w_o_even,
              w_dq, q_norm_g, w_uq, w_dkv, kv_norm_g, w_uk, w_uv, w_o_mla,
              w_up, conv_w, conv_b, w_down, ln_mix_g, ln_mix_b, ln_ffn_g, ln_ffn_b):
    xp, xs = x_prompt, x_sample
    Bp, Lp, _ = xp.shape
    past_len = page_table.shape[1] * PAGE_SIZE
    pos_p = jnp.arange(Lp, dtype=F32)
    pos_s = past_len + jnp.arange(xs.shape[1], dtype=F32)
    pool_p, pool_s, ret_p, ret_s = [], [], [], []
    ckv_p, ckv_s, kpe_p, kpe_s = [], [], [], []
    conv_p, conv_s = [], []
    for layer in range(DEPTH):
        if layer % 2 == 0:
            e = layer // 2
            hist0 = jnp.zeros((Bp, POOL_HIST, POOL_DIM), xp.dtype)
            S0 = jnp.zeros((Bp, RET_HEADS, RET_DK, RET_DV), xp.dtype)
            mp, hp, sp = even_mixer(xp, pos_p, hist0, S0, w_in_even[e], pool_w[e], pool_scale[e], ret_gn_g[e], w_o_even[e])
            ms, hs, ss = even_mixer(xs, pos_s, state_pool[e], state_ret[e], w_in_even[e], pool_w[e], pool_scale[e], ret_gn_g[e], w_o_even[e])
            pool_p.append(hp)
            pool_s.append(hs)
            ret_p.append(sp)
            ret_s.append(ss)
        else:
            o = layer // 2
            mp, cp, kp = mla_prompt(xp, pos_p, w_dq[o], q_norm_g[o], w_uq[o], w_dkv[o], kv_norm_g[o], w_uk[o], w_uv[o], w_o_mla[o])
            ms, cs, ksmp = mla_sample(xs, pos_s, cache_ckv, cache_kpe, o, page_table, w_dq[o], q_norm_g[o], w_uq[o], w_dkv[o], kv_norm_g[o], w_uk[o], w_uv[o], w_o_mla[o])
            ckv_p.append(cp)
            ckv_s.append(cs)
            kpe_p.append(kp)
            kpe_s.append(ksmp)
        xp = layer_norm(ALPHA * xp + mp, ln_mix_g[layer], ln_mix_b[layer])
        xs = layer_norm(ALPHA * xs + ms, ln_mix_g[layer], ln_mix_b[layer])
        conv0 = jnp.zeros((Bp, CONV_W - 1, D_FF), xp.dtype)
        fp, hcp = conv_ffn(xp, conv0, w_up[layer], conv_w[layer], conv_b[layer], w_down[layer])
        fs, hcs = conv_ffn(xs, state_conv[layer], w_up[layer], conv_w[layer], conv_b[layer], w_down[layer])
        conv_p.append(hcp)
        conv_s.append(hcs)
        xp = layer_norm(ALPHA * xp + fp, ln_ffn_g[layer], ln_ffn_b[layer])
        xs = layer_norm(ALPHA * xs + fs, ln_ffn_g[layer], ln_ffn_b[layer])
    return (xp, xs,
            jnp.stack(pool_p), jnp.stack(pool_s),
            jnp.stack(ret_p), jnp.stack(ret_s),
            jnp.stack(ckv_p), jnp.stack(ckv_s),
            jnp.stack(kpe_p), jnp.stack(kpe_s),
            jnp.stack(conv_p), jnp.stack(conv_s))
```

```python
import contextlib
import math
import numpy as np
import concourse.bass as bass
import concourse.mybir as mybir
from concourse.bass_utils import run_bass_kernel_spmd

F32 = mybir.dt.float32
BF16 = mybir.dt.bfloat16
I32 = mybir.dt.int32
AF = mybir.ActivationFunctionType
ALU = mybir.AluOpType
AX = mybir.AxisListType

ALL_Q = ("pe", "act", "dve", "pool", "sp")
SAME_ENGINE_SYNC = True

D = 1024
SEQ = 2048
NT = SEQ // 128
DEPTH = 2
NS = 16
SB = 4
SL = 4
PAST = 16384
NPAGES = 128
POOL_W = (2, 4, 8, 16)
RH = 4
DFF = 2816
NJ = DFF // 128
ALPHA = (2.0 * DEPTH) ** 0.25
LN_EPS = 1e-5
RMS_EPS = 1e-6
GN_EPS = 1e-6
MLA_SCALE = (128 + 64) ** -0.5
KSCALE = 128 ** -0.5
NEG = -30000.0


class Res:
    __slots__ = ("name", "w", "r")

    def __init__(self, name):
        self.name = name
        self.w = None
        self.r = []


class Op:
    __slots__ = ("q", "fn", "waits", "signal", "dma", "idx")

    def __init__(self, q, fn, dma=None):
        self.q = q
        self.fn = fn
        self.waits = []
        self.signal = False
        self.dma = dma
        self.idx = None


class Sched:
    def __init__(self, nc, stack):
        self.nc = nc
        self.stack = stack
        self.ops = {q: [] for q in ALL_Q}
        self.dma_sems = {}
        self.eng_sems = {}
        self.waited = {q: {} for q in ALL_Q}
        self.dwaited = {q: {} for q in ALL_Q}
        self.nres = 0

    def res(self, name=None):
        self.nres += 1
        return Res(name or f"r{self.nres}")

    def _add_wait(self, op, tok):
        if tok is None:
            return
        q = op.q
        if tok[0] == "eng":
            _, sq, sidx = tok
            if sq == q and (q == "pe" or q == "sp" or not SAME_ENGINE_SYNC):
                return
            if self.waited[q].get(sq, -1) >= sidx:
                return
            self.waited[q][sq] = sidx
            op.waits.append(tok)
            self.ops[sq][sidx].signal = True
        else:
            _, key, val = tok
            if self.dwaited[q].get(key, -1) >= val:
                return
            self.dwaited[q][key] = val
            op.waits.append(tok)

    def _deps(self, o, r, w, tok):
        toks = []
        for res in r:
            if res.w is not None:
                toks.append(res.w)
        for res in w:
            if res.w is not None:
                toks.append(res.w)
            toks.extend(res.r)
        best = {}
        for t in toks:
            k = (t[0], t[1])
            if k not in best or best[k][2] < t[2]:
                best[k] = t
        for t in best.values():
            self._add_wait(o, t)
        for res in r:
            res.r.append(tok)
        for res in w:
            res.w = tok
            res.r = []

    def op(self, q, fn, r=(), w=()):
        o = Op(q, fn)
        o.idx = len(self.ops[q])
        self._deps(o, r, w, ("eng", q, o.idx))
        self.ops[q].append(o)
        return o

    def dma(self, q, fn, r=(), w=(), semkey=None):
        res0 = (list(w) + list(r))[0] if (w or r) else None
        key = semkey if semkey is not None else id(res0)
        if key not in self.dma_sems:
            self.dma_sems[key] = [None, 0]
        ent = self.dma_sems[key]
        ent[1] += 16
        o = Op(q, fn, dma=(key, ent[1]))
        o.idx = len(self.ops[q])
        tok = ("dma", key, ent[1])
        self._deps(o, r, w, tok)
        self.ops[q].append(o)
        return tok

    def finish(self, out_tokens):
        o = Op("sp", None)
        o.idx = len(self.ops["sp"])
        for t in out_tokens:
            self._add_wait(o, t)
        self.ops["sp"].append(o)

    def emit(self):
        nc = self.nc
        st = self.stack
        for q in ALL_Q:
            self.eng_sems[q] = st.enter_context(nc.semaphore(f"es_{q}"))
        for i, key in enumerate(self.dma_sems):
            self.dma_sems[key][0] = st.enter_context(nc.semaphore(f"ds_{i}"))
        cnt = {}
        for q in ALL_Q:
            c = 0
            arr = []
            for o in self.ops[q]:
                if o.signal:
                    c += 1
                arr.append(c)
            cnt[q] = arr
        block = st.enter_context(nc.Block())

        def run(q, eng):
            for o in self.ops[q]:
                for t in o.waits:
                    if t[0] == "eng":
                        eng.wait_ge(self.eng_sems[t[1]], cnt[t[1]][t[2]])
                    else:
                        eng.wait_ge(self.dma_sems[t[1]][0], t[2])
                if o.fn is None:
                    continue
                ins = o.fn(eng)
                if o.dma is not None:
                    ins.then_inc(self.dma_sems[o.dma[0]][0], 16)
                elif o.signal:
                    ins.then_inc(self.eng_sems[q], 1)

        @block.tensor
        def _(eng):
            run("pe", eng)

        @block.scalar
        def _(eng):
            run("act", eng)

        @block.vector
        def _(eng):
            run("dve", eng)

        @block.gpsimd
        def _(eng):
            run("pool", eng)

        @block.sync
        def _(eng):
            run("sp", eng)

    def stats(self):
        return {q: (len(self.ops[q]), sum(len(o.waits) for o in self.ops[q])) for q in ALL_Q}


def _rope_tables(half, pos):
    inv = (np.float32(10000.0) ** (-(np.arange(half, dtype=np.float32)) / np.float32(half))).astype(np.float32)
    ang = (pos.astype(np.float32)[:, None] * inv[None, :]).astype(np.float32)
    return np.cos(ang).astype(np.float32), np.sin(ang).astype(np.float32)


def host_tables():
    t = {}
    pos_p = np.arange(SEQ, dtype=np.float32)
    pos_s = (PAST + np.arange(SL)).astype(np.float32)
    c, s = _rope_tables(64, pos_p)
    t["c2p"] = np.ascontiguousarray(np.concatenate([c, c], 1).T)
    t["ssp"] = np.ascontiguousarray(np.concatenate([s, -s], 1).T)
    c, s = _rope_tables(64, pos_s)
    t["c2s"] = np.ascontiguousarray(np.tile(np.concatenate([c, c], 1).T, (1, SB)))
    t["sss"] = np.ascontiguousarray(np.tile(np.concatenate([s, -s], 1).T, (1, SB)))
    c, s = _rope_tables(32, pos_p)
    cc = np.concatenate([c, c], 1).reshape(NT, 128, 64).transpose(1, 0, 2)
    ss = np.concatenate([-s, s], 1).reshape(NT, 128, 64).transpose(1, 0, 2)
    t["cc1"] = np.ascontiguousarray(cc)
    t["ss1"] = np.ascontiguousarray(ss)
    c, s = _rope_tables(32, pos_s)
    t["cc1s"] = np.ascontiguousarray(np.tile(np.concatenate([c, c], 1), (SB, 1)))
    t["ss1s"] = np.ascontiguousarray(np.tile(np.concatenate([-s, s], 1), (SB, 1)))
    lg = np.log(np.float32(1.0) - np.float32(2.0) ** (np.float32(-5.0) - np.arange(RH, dtype=np.float32))).astype(np.float32)
    for name, C in (("p", 128), ("s", SL)):
        idx = np.arange(C, dtype=np.float32)
        diff = idx[:, None] - idx[None, :]
        dec = np.where(diff >= 0, np.exp(np.maximum(diff, 0.0)[None] * lg[:, None, None]), 0.0).astype(np.float32)
        t["decT" + name] = np.ascontiguousarray((dec * np.float32(KSCALE)).transpose(2, 0, 1))
        qd = np.exp((idx + 1.0)[:, None] * lg[None, :]).astype(np.float32)
        t["qdec" + name] = np.ascontiguousarray(np.broadcast_to(qd.T[None], (128, RH, C)))
        kd = np.exp((C - 1.0 - idx)[:, None] * lg[None, :]).astype(np.float32) * np.float32(KSCALE)
        t["kdec" + name] = np.ascontiguousarray(kd)
        t["gC" + name] = [float(np.exp(np.float32(C) * lg[h])) for h in range(RH)]
    ic = np.zeros((128, 4, 15), np.float32)
    for g, w in enumerate(POOL_W):
        ic[:, g, :] = 1.0 / np.minimum(float(w), np.arange(15) + 1.0)
    t["invcnt"] = ic
    qi = np.arange(128)
    t["maskd"] = np.where(qi[None, :] <= qi[:, None], 0.0, NEG).astype(np.float32)
    t["ident"] = np.eye(128, dtype=np.float32)
    r = np.arange(32) % 4
    t["masks"] = np.where(np.arange(4)[None, :] <= r[:, None], 0.0, NEG).astype(np.float32)
    mb = np.full((32, SB, NS), NEG, np.float32)
    for bb in range(SB):
        for l2 in range(SL):
            mb[:, bb, bb * SL + l2] = np.where(l2 <= r, 0.0, NEG)
    t["masksb"] = np.ascontiguousarray(mb.reshape(32, SB * NS))
    return t


ARENA_BYTES = 142144


class PsumRing:
    def __init__(self, S):
        self.res = [S.res(f"bank{i}") for i in range(8)]
        self.nxt = 0

    def get(self, n=1):
        if n == 2 and self.nxt % 2:
            self.nxt = (self.nxt + 1) % 8
        if n == 4 and self.nxt % 4:
            self.nxt = (self.nxt + 4 - self.nxt % 4) % 8
        b = self.nxt
        self.nxt = (self.nxt + n) % 8
        return b, [self.res[b + i] for i in range(n)]


class Arena:
    def __init__(self, S, buf):
        self.S = S
        self.buf = buf
        self.hist = []
        self.cur = 0

    def at(self, off):
        assert off % 4 == 0
        self.cur = off

    def get(self, nelem, dt=BF16, name=None, nres=None):
        size = 2 if dt == BF16 else 4
        nb = (nelem * size + 3) // 4 * 4
        s, e = self.cur, self.cur + nb
        assert e <= ARENA_BYTES, (name, e)
        self.cur = e
        pend = []
        for (s2, e2, r2) in self.hist:
            if s2 < e and s < e2:
                if r2.w is not None:
                    pend.append(r2.w)
                pend.extend(r2.r)
        best = {}
        for t in pend:
            k = (t[0], t[1])
            if k not in best or best[k][2] < t[2]:
                best[k] = t
        pend = list(best.values())
        rs = []
        for i in range(nres or 1):
            r = self.S.res(f"{name}_{i}")
            r.r = list(pend)
            self.hist.append((s, e, r))
            rs.append(r)
        v = self.buf[:, s // 2:s // 2 + nb // 2]
        if dt != BF16:
            v = v.bitcast(dt)
        return v[:, 0:nelem], (rs if nres else rs[0])


class _Stop(Exception):
    pass


def build_program(tabs, with_sample=True, stop=None):
    nc = bass.Bass("TRN2", target_bir_lowering=False)

    def din(name, shape, dt=F32):
        return nc.dram_tensor(name, list(shape), dt, kind="ExternalInput").ap()

    def dout(name, shape):
        return nc.dram_tensor(name, list(shape), F32, kind="ExternalOutput").ap()

    xp = din("xp", [SEQ, D])
    xs = din("xs", [NS, D])
    spool = din("spool", [SB * 15, 512])
    sret = din("sret", [SB * RH, 128, 128])
    sconv = din("sconv", [DEPTH, SB * 2, DFF])
    ptab = din("ptab", [128, SB], I32)
    if with_sample:
        cckv = din("cckv", [5120, 128 * 256])
        ckpe = din("ckpe", [5120, 128 * 64])
    w_in = din("w_in", [D, 2560])
    pool_w = din("pool_w", [128, 4, 128])
    pool_scale = din("pool_scale", [128, 4])
    gn_g = din("gn_g", [512])
    w_o0 = din("w_o0", [D, D])
    w_dq = din("w_dq", [D, 512])
    qn_g = din("qn_g", [512])
    w_uq = din("w_uq", [512, 1536])
    w_dkv = din("w_dkv", [D, 320])
    kvn_g = din("kvn_g", [256])
    w_ukT = din("w_ukT", [128, 8, 256])
    w_uv = din("w_uv", [256, 1024])
    w_o1 = din("w_o1", [D, D])
    w_up = din("w_up", [DEPTH, NJ, 128, 8, 256])
    w_down = din("w_down", [DEPTH, DFF, D])
    conv_w = din("conv_w", [DEPTH, 128, NJ, 3])
    conv_b = din("conv_b", [DEPTH, 128, NJ])
    ln_mix_g = din("ln_mix_g", [DEPTH, D])
    ln_mix_b = din("ln_mix_b", [DEPTH, D])
    ln_ffn_g = din("ln_ffn_g", [DEPTH, D])
    ln_ffn_b = din("ln_ffn_b", [DEPTH, D])
    tdr = {k: din("t_" + k, v.shape) for k, v in tabs.items() if isinstance(v, np.ndarray)}

    yp = dout("yp", [SEQ, D])
    ys = dout("ys", [NS, D])
    poolp = dout("poolp", [15, 512])
    pools = dout("pools", [SB * 15, 512])
    retp = dout("retp", [RH, 128, 128])
    rets = dout("rets", [SB * RH, 128, 128])
    ckvp = dout("ckvp", [SEQ, 256])
    ckvs = dout("ckvs", [NS, 256])
    kpep = dout("kpep", [SEQ, 64])
    kpes = dout("kpes", [NS, 64])
    convp = dout("convp", [DEPTH, 2, DFF])
    convs = dout("convs", [DEPTH, SB * 2, DFF])

    out_toks = []
    st = contextlib.ExitStack()
    with st:
        S = Sched(nc, st)
        _n = [0]

        def sb(shape, dt, name):
            return st.enter_context(nc.sbuf_tensor(name, list(shape), dt))

        P = st.enter_context(nc.psum_tensor("P", [128, 8, 512], F32))
        ring = PsumRing(S)

        def bank(b, n=1):
            return P[:, b:b + n, :].rearrange("p a b -> p (a b)")

        def bankbf(b, n=1):
            return P[:, b:b + n, :].rearrange("p a b -> p (a b)").bitcast(BF16)

        X = sb([128, NT, D], F32, "X")
        rX = [S.res(f"X{t}") for t in range(NT)]
        BIG = sb([128, ARENA_BYTES // 2], BF16, "BIG")
        A = Arena(S, BIG)
        ident_f = sb([128, 128], F32, "ident_f"); r_idf = S.res()
        ident_b = sb([128, 128], BF16, "ident_b"); r_idb = S.res()
        maskd = sb([128, 128], BF16, "maskd"); r_maskd = S.res()
        eps_t = sb([128, 4], F32, "eps_t"); r_eps = S.res()
        XS = sb([NS, D], F32, "XS"); r_XS = S.res()
        S.dma("sp", lambda e: e.dma_start(out=ident_f[:], in_=tdr["ident"]), w=[r_idf])
        S.dma("pool", lambda e: e.dma_start(out=ident_b[:], in_=tdr["ident"]), w=[r_idb])
        S.dma("pool", lambda e: e.dma_start(out=maskd[:], in_=tdr["maskd"]), w=[r_maskd])
        S.op("pool", lambda e: e.memset(eps_t[:, 0:1], LN_EPS), w=[r_eps])
        S.op("pool", lambda e: e.memset(eps_t[:, 1:2], RMS_EPS), w=[r_eps])
        S.op("pool", lambda e: e.memset(eps_t[:, 2:3], GN_EPS), w=[r_eps])
        eps_ln, eps_rms, eps_gn = eps_t[:, 0:1], eps_t[:, 1:2], eps_t[:, 2:3]

        def evac(q, out_ap, in_ap, r, w):
            if q == "act":
                S.op("act", lambda e: e.copy(out=out_ap, in_=in_ap), r=r, w=w)
            else:
                S.op(q, lambda e: e.tensor_copy(out=out_ap, in_=in_ap), r=r, w=w)

        def load_ln(lnt, r_lnt, g_ap, b_ap):
            S.dma("sp", lambda e: e.dma_start(out=lnt[:, 0:D], in_=g_ap.partition_broadcast(128)), w=[r_lnt])
            S.dma("sp", lambda e: e.dma_start(out=lnt[:, D:2 * D], in_=b_ap.partition_broadcast(128)), w=[r_lnt])

        def layer_norm_tile(z_ap, z_res, out_ap, out_res, npart, scr, scr_res, lnt, r_lnt):
            for c in range(2):
                S.op("dve", lambda e, c=c: e.bn_stats(out=scr[0:npart, c * 6:(c + 1) * 6], in_=z_ap[:, c * 512:(c + 1) * 512]), r=[z_res], w=[scr_res])
            mv = scr[0:npart, 12:14]
            S.op("dve", lambda e: e.bn_aggr(out=mv, in_=scr[0:npart, 0:12]), r=[scr_res], w=[scr_res])
            sd = scr[0:npart, 14:15]
            S.op("act", lambda e: e.activation(out=sd, in_=scr[0:npart, 13:14], func=AF.Sqrt, bias=eps_ln[0:npart], scale=1.0), r=[scr_res, r_eps], w=[scr_res])
            S.op("dve", lambda e: e.reciprocal(out=sd, in_=sd), r=[scr_res], w=[scr_res])
            nmr = scr[0:npart, 15:16]
            S.op("dve", lambda e: e.tensor_scalar(out=nmr, in0=scr[0:npart, 12:13], scalar1=sd, scalar2=-1.0, op0=ALU.mult, op1=ALU.mult), r=[scr_res], w=[scr_res])
            S.op("act", lambda e: e.activation(out=z_ap, in_=z_ap, func=AF.Identity, bias=nmr, scale=sd), r=[scr_res, z_res], w=[z_res])
            S.op("pool", lambda e: e.tensor_tensor(out=z_ap, in0=z_ap, in1=lnt[0:npart, 0:D], op=ALU.mult), r=[z_res, r_lnt], w=[z_res])
            S.op("pool", lambda e: e.tensor_tensor(out=out_ap, in0=z_ap, in1=lnt[0:npart, D:2 * D], op=ALU.add), r=[z_res, r_lnt], w=[out_res])

        def transpose_x(src_ap, src_res, dst_ap, dst_res, npart=128):
            for half in range(2):
                b, br = ring.get(1)
                pv = bank(b)[:, 0:4 * npart].rearrange("p (a b) -> p a b", a=4)
                for c in range(4):
                    cc = half * 4 + c
                    S.op("pe", lambda e, c=c, cc=cc, pv=pv: e.transpose(out=pv[:, c, :], in_=src_ap[:, cc * 128:(cc + 1) * 128], identity=ident_f[0:npart, 0:npart]),
                         r=[src_res, r_idf], w=br)
                evac("act" if half == 0 else "dve", dst_ap[:, half * 4:half * 4 + 4, :], pv, br, [dst_res])

        def chk(tag):
            if stop == tag:
                raise _Stop()

        try:
            for t in range(NT):
                S.dma("sp", lambda e, t=t: e.dma_start(out=X[:, t, :], in_=xp[t * 128:(t + 1) * 128, :]), w=[rX[t]])
            S.dma("sp", lambda e: e.dma_start(out=XS[:], in_=xs), w=[r_XS])

            A.at(0)
            w_in_sb, r_win = A.get(8 * 2560, BF16, "w_in"); w_in_sb = w_in_sb.rearrange("p (k n) -> p k n", k=8)
            w_o_sb, r_wo = A.get(8 * 1024, BF16, "w_o0"); w_o_sb = w_o_sb.rearrange("p (k n) -> p k n", k=8)
            poolw_sb, r_poolw = A.get(512, BF16, "poolw"); poolw_sb = poolw_sb.rearrange("p (g d) -> p g d", g=4)
            c0buf, r_c0 = A.get(1024 + 1024 + 4 + 60 + 4 + 512, F32, "c0")
            decT = c0buf[:, 0:512].rearrange("p (h l) -> p h l", h=4)
            qdec = c0buf[:, 512:1024].rearrange("p (h l) -> p h l", h=4)
            kdec = c0buf[:, 1024:1028]
            invcnt = c0buf[:, 1028:1088].rearrange("p (g t) -> p g t", g=4)
            pscale = c0buf[:, 1088:1092]
            gng = c0buf[:, 1092:1604]
            lnt0, r_lnt0 = A.get(2 * D, F32, "lnt0")
            w_in_v = w_in.rearrange("(k p) n -> p k n", p=128)
            for hh in range(2):
                S.dma("pool", lambda e, hh=hh: e.dma_start(out=w_in_sb[:, :, hh * 1280:(hh + 1) * 1280], in_=w_in_v[:, :, hh * 1280:(hh + 1) * 1280]), w=[r_win])
            S.dma("pool", lambda e: e.dma_start(out=poolw_sb, in_=pool_w), w=[r_poolw])
            S.dma("pool", lambda e: e.dma_start(out=w_o_sb, in_=w_o0.rearrange("(k p) n -> p k n", p=128)), w=[r_wo])
            S.dma("sp", lambda e: e.dma_start(out=decT, in_=tdr["decTp"]), w=[r_c0])
            S.dma("sp", lambda e: e.dma_start(out=qdec, in_=tdr["qdecp"]), w=[r_c0])
            S.dma("sp", lambda e: e.dma_start(out=kdec, in_=tdr["kdecp"]), w=[r_c0])
            S.dma("sp", lambda e: e.dma_start(out=invcnt, in_=tdr["invcnt"]), w=[r_c0])
            S.dma("sp", lambda e: e.dma_start(out=pscale, in_=pool_scale), w=[r_c0])
            S.dma("sp", lambda e: e.dma_start(out=gng, in_=gn_g.partition_broadcast(128)), w=[r_c0])
            load_ln(lnt0, r_lnt0, ln_mix_g[0], ln_mix_b[0])
            L0_BUF = A.cur

            def dbl(nelem, dt, name, shape=None, **kw):
                out = []
                for i in range(2):
                    v, r = A.get(nelem, dt, f"{name}{i}")
                    if shape:
                        v = v.rearrange(shape, **kw)
                    out.append((v, r))
                return [x[0] for x in out], [x[1] for x in out]

            xT, r_xT = dbl(1024, BF16, "xT", "p (k t) -> p k t", k=8)
            ropet, r_ropet = dbl(256, F32, "ropet")
            uext, r_uext = dbl(4 * 143 + 1, F32, "uext")
            uext = [u[:, 0:572].rearrange("p (g t) -> p g t", g=4) for u in uext]
            tA, r_tA = A.get(573, F32, "tA"); tA = tA[:, 0:572].rearrange("p (g t) -> p g t", g=4)
            tB, r_tB = A.get(573, F32, "tB"); tB = tB[:, 0:572].rearrange("p (g t) -> p g t", g=4)
            pooledT, r_pooled = dbl(512, BF16, "pooledT", "p (g t) -> p g t", g=4)
            qk_f, r_qkf = A.get(1024, F32, "qk_f"); qk_f = qk_f.rearrange("p (h t) -> p h t", h=8)
            ropeA, r_ropeA = A.get(1024, F32, "ropeA"); ropeA = ropeA.rearrange("p (h t) -> p h t", h=8)
            ropeB, r_ropeB = A.get(1024, F32, "ropeB"); ropeB = ropeB.rearrange("p (h t) -> p h t", h=8)
            qkT, r_qkT = dbl(1024, BF16, "qkT", "p (h t) -> p h t", h=8)
            qdT, r_qdT = dbl(512, BF16, "qdT", "p (h t) -> p h t", h=4)
            v_sb, r_v = dbl(512, BF16, "v")
            sgg, r_sgg = dbl(512, F32, "sgg")
            sT_bf, r_sT = dbl(512, BF16, "sT", "p (h t) -> p h t", h=4)
            kp_bf, r_kp = dbl(512, BF16, "kp", "p (h t) -> p h t", h=4)
            S_f, r_Sf = A.get(512, F32, "S_f"); S_f = S_f.rearrange("p (h t) -> p h t", h=4)
            S_b, r_Sb = A.get(512, BF16, "S_b"); S_b = S_b.rearrange("p (h t) -> p h t", h=4)
            on1, r_on1 = A.get(512, F32, "on1")
            ret_bf, r_ret = dbl(512, BF16, "ret")
            catT, r_cat = dbl(1024, BF16, "catT", "p (c t) -> p c t", c=8)
            z0_, r_z0_ = A.get(D, F32, "z"); z = [z0_, z0_]; r_z = [r_z0_, r_z0_]
            scr, r_scr = dbl(16, F32, "scr")
            gst, r_gst = A.get(32, F32, "gst")
            poolo, r_poolo = A.get(512, F32, "poolo")
            L0_END = A.cur
            print("layer0 arena bytes", L0_END)

            S.op("pool", lambda e: e.memset(uext[1][:, :, 128:143], 0.0), w=[r_uext[1]])

            for t in range(NT):
                pb = t % 2
                tok = slice(t * 128, (t + 1) * 128)
                S.dma("sp", lambda e, pb=pb, tok=tok: e.dma_start(out=ropet[pb][:, 0:128], in_=tdr["c2p"][:, tok]), w=[r_ropet[pb]])
                S.dma("sp", lambda e, pb=pb, tok=tok: e.dma_start(out=ropet[pb][:, 128:256], in_=tdr["ssp"][:, tok]), w=[r_ropet[pb]])
                transpose_x(X[:, t, :], rX[t], xT[pb], r_xT[pb])
                bu, ru = ring.get(1)
                bq, rq = ring.get(1)
                bk, rk = ring.get(1)
                bv, rv = ring.get(1)
                bg, rg = ring.get(1)
                for (bb, rr, c0) in ((bu, ru, 0), (bq, rq, 512), (bk, rk, 1024)):
                    pv = bank(bb).rearrange("p (h t) -> p h t", h=4)
                    for h in range(4):
                        for k in range(8):
                            S.op("pe", lambda e, pv=pv, h=h, k=k, c0=c0, pb=pb: e.matmul(pv[:, h, :], lhsT=w_in_sb[:, k, c0 + h * 128:c0 + (h + 1) * 128], rhs=xT[pb][:, k, :], start=(k == 0), stop=(k == 7)),
                                 r=[r_win, r_xT[pb]], w=rr)
                for (bb, rr, c0) in ((bv, rv, 1536), (bg, rg, 2048)):
                    for k in range(8):
                        S.op("pe", lambda e, bb=bb, k=k, c0=c0, pb=pb: e.matmul(bank(bb), lhsT=xT[pb][:, k, :], rhs=w_in_sb[:, k, c0:c0 + 512], start=(k == 0), stop=(k == 7)),
                             r=[r_win, r_xT[pb]], w=rr)
                ue = uext[pb]
                S.op("act", lambda e, ue=ue, bu=bu: e.copy(out=ue[:, :, 15:143], in_=bank(bu).rearrange("p (h t) -> p h t", h=4)), r=ru, w=[r_uext[pb]])
                S.op("pool", lambda e, ue=ue, pb=pb: e.tensor_copy(out=ue[:, :, 0:15], in_=uext[1 - pb][:, :, 128:143]), r=[r_uext[1 - pb]], w=[r_uext[pb]])
                S.op("pool", lambda e, ue=ue: e.tensor_tensor(out=tA[:, :, 1:143], in0=ue[:, :, 1:143], in1=ue[:, :, 0:142], op=ALU.add), r=[r_uext[pb]], w=[r_tA])
                S.op("pool", lambda e: e.tensor_tensor(out=tB[:, 1:4, 3:143], in0=tA[:, 1:4, 3:143], in1=tA[:, 1:4, 1:141], op=ALU.add), r=[r_tA], w=[r_tB])
                S.op("pool", lambda e: e.tensor_tensor(out=tA[:, 2:4, 7:143], in0=tB[:, 2:4, 7:143], in1=tB[:, 2:4, 3:139], op=ALU.add), r=[r_tB], w=[r_tA])
                S.op("pool", lambda e: e.tensor_tensor(out=tB[:, 3:4, 15:143], in0=tA[:, 3:4, 15:143], in1=tA[:, 3:4, 7:135], op=ALU.add), r=[r_tA], w=[r_tB])
                fin = (tA, tB, tA, tB)
                for g in range(4):
                    S.op("dve", lambda e, g=g, ue=ue, pb=pb: e.scalar_tensor_tensor(out=pooledT[pb][:, g, :], in0=fin[g][:, g, 15:143], scalar=1.0 / POOL_W[g], in1=ue[:, g, 15:143], op0=ALU.mult, op1=ALU.subtract),
                         r=[r_tA, r_tB, r_uext[pb]], w=[r_pooled[pb]])
                if t == 0:
                    for g in range(4):
                        S.op("dve", lambda e, g=g: e.tensor_tensor(out=ropeA[:, g, 0:15], in0=fin[g][:, g, 15:30], in1=invcnt[:, g, :], op=ALU.mult), r=[r_tA, r_tB, r_c0], w=[r_ropeA])
                        S.op("dve", lambda e, g=g, ue=ue, pb=pb: e.tensor_tensor(out=pooledT[pb][:, g, 0:15], in0=ropeA[:, g, 0:15], in1=ue[:, g, 15:30], op=ALU.subtract), r=[r_ropeA, r_uext[pb]], w=[r_pooled[pb]])
                bpm, rpm = ring.get(1)
                pmv = bank(bpm).rearrange("p (h t) -> p h t", h=4)
                for g in range(4):
                    S.op("pe", lambda e, g=g, pmv=pmv, pb=pb: e.matmul(pmv[:, g, :], lhsT=poolw_sb[:, g, :], rhs=pooledT[pb][:, g, :], start=True, stop=True), r=[r_poolw, r_pooled[pb]], w=rpm)
                for g in range(4):
                    S.op("act", lambda e, g=g, pmv=pmv, pb=pb: e.activation(out=catT[pb][:, g, :], in_=pmv[:, g, :], func=AF.Identity, bias=0.0, scale=pscale[:, g:g + 1]), r=rpm + [r_c0], w=[r_cat[pb]])
                if t == NT - 1:
                    bpo, rpo = ring.get(1)
                    for g in range(4):
                        S.op("pe", lambda e, g=g, ue=ue, bpo=bpo: e.transpose(out=bank(bpo)[0:15, g * 128:(g + 1) * 128], in_=ue[:, g, 128:143], identity=ident_f[:]), r=[r_uext[pb], r_idf], w=rpo)
                    S.op("act", lambda e, bpo=bpo: e.copy(out=poolo[0:15, :], in_=bank(bpo)[0:15, :]), r=rpo, w=[r_poolo])
                    out_toks.append(S.dma("sp", lambda e: e.dma_start(out=poolp, in_=poolo[0:15, :]), r=[r_poolo]))
                S.op("act", lambda e, bq=bq: e.copy(out=qk_f[:, 0:4, :], in_=bank(bq).rearrange("p (h t) -> p h t", h=4)), r=rq, w=[r_qkf])
                S.op("act", lambda e, bk=bk: e.copy(out=qk_f[:, 4:8, :], in_=bank(bk).rearrange("p (h t) -> p h t", h=4)), r=rk, w=[r_qkf])
                c2b = ropet[pb][:, 0:128].unsqueeze(1)
                ssb = ropet[pb][:, 128:256].unsqueeze(1)
                S.op("dve", lambda e, c2b=c2b: e.tensor_tensor(out=ropeA, in0=qk_f, in1=c2b.to_broadcast([128, 8, 128]), op=ALU.mult), r=[r_qkf, r_ropet[pb]], w=[r_ropeA])
                S.op("pool", lambda e, ssb=ssb: e.tensor_tensor(out=ropeB[0:64], in0=qk_f[64:128], in1=ssb[64:128].to_broadcast([64, 8, 128]), op=ALU.mult), r=[r_qkf, r_ropet[pb]], w=[r_ropeB])
                S.op("pool", lambda e, ssb=ssb: e.tensor_tensor(out=ropeB[64:128], in0=qk_f[0:64], in1=ssb[0:64].to_broadcast([64, 8, 128]), op=ALU.mult), r=[r_qkf, r_ropet[pb]], w=[r_ropeB])
                S.op("dve", lambda e, pb=pb: e.tensor_tensor(out=qkT[pb], in0=ropeA, in1=ropeB, op=ALU.add), r=[r_ropeA, r_ropeB], w=[r_qkT[pb]])
                if t > 0:
                    S.op("pool", lambda e, pb=pb: e.tensor_tensor(out=qdT[pb], in0=qkT[pb][:, 0:4, :], in1=qdec, op=ALU.mult), r=[r_qkT[pb], r_c0], w=[r_qdT[pb]])
                S.op("act", lambda e, pb=pb, bv=bv: e.copy(out=v_sb[pb], in_=bank(bv)), r=rv, w=[r_v[pb]])
                S.op("act", lambda e, pb=pb, bg=bg: e.activation(out=sgg[pb], in_=bank(bg), func=AF.Silu), r=rg, w=[r_sgg[pb]])
                S.op("pool", lambda e, pb=pb: e.tensor_tensor(out=sgg[pb], in0=sgg[pb], in1=gng, op=ALU.mult), r=[r_sgg[pb], r_c0], w=[r_sgg[pb]])
                bs, rs = ring.get(1)
                sv = bank(bs).rearrange("p (h t) -> p h t", h=4)
                for h in range(4):
                    S.op("pe", lambda e, h=h, sv=sv, pb=pb: e.matmul(sv[:, h, :], lhsT=qkT[pb][:, 4 + h, :], rhs=qkT[pb][:, h, :], start=True, stop=True), r=[r_qkT[pb]], w=rs)
                S.op("dve", lambda e, sv=sv, pb=pb: e.tensor_tensor(out=sT_bf[pb], in0=sv, in1=decT, op=ALU.mult), r=rs + [r_c0], w=[r_sT[pb]])
                bo, ro = ring.get(1)
                ov = bank(bo).rearrange("p (h t) -> p h t", h=4)
                for h in range(4):
                    S.op("pe", lambda e, h=h, ov=ov, pb=pb, t=t: e.matmul(ov[:, h, :], lhsT=sT_bf[pb][:, h, :], rhs=v_sb[pb][:, h * 128:(h + 1) * 128], start=True, stop=(t == 0)), r=[r_sT[pb], r_v[pb]], w=ro)
                    if t > 0:
                        S.op("pe", lambda e, h=h, ov=ov, pb=pb: e.matmul(ov[:, h, :], lhsT=qdT[pb][:, h, :], rhs=S_b[:, h, :], start=False, stop=True), r=[r_qdT[pb], r_Sb], w=ro)
                bkt, rkt = ring.get(1)
                ktv = bankbf(bkt)[:, 0:512].rearrange("p (h t) -> p h t", h=4)
                for h in range(4):
                    S.op("pe", lambda e, h=h, ktv=ktv, pb=pb: e.transpose(out=ktv[:, h, :], in_=qkT[pb][:, 4 + h, :], identity=ident_b[:]), r=[r_qkT[pb], r_idb], w=rkt)
                for h in range(4):
                    S.op("act", lambda e, h=h, ktv=ktv, pb=pb: e.activation(out=kp_bf[pb][:, h, :], in_=ktv[:, h, :], func=AF.Identity, bias=0.0, scale=kdec[:, h:h + 1]), r=rkt + [r_c0], w=[r_kp[pb]])
                bds, rds = ring.get(1)
                dsv = bank(bds).rearrange("p (h t) -> p h t", h=4)
                for h in range(4):
                    S.op("pe", lambda e, h=h, dsv=dsv, pb=pb: e.matmul(dsv[:, h, :], lhsT=kp_bf[pb][:, h, :], rhs=v_sb[pb][:, h * 128:(h + 1) * 128], start=True, stop=True), r=[r_kp[pb], r_v[pb]], w=rds)
                if t == 0:
                    S.op("dve", lambda e, dsv=dsv: e.tensor_copy(out=S_f, in_=dsv), r=rds, w=[r_Sf])
                else:
                    for h in range(4):
                        S.op("dve", lambda e, h=h, dsv=dsv: e.scalar_tensor_tensor(out=S_f[:, h, :], in0=S_f[:, h, :], scalar=tabs["gCp"][h], in1=dsv[:, h, :], op0=ALU.mult, op1=ALU.add), r=rds + [r_Sf], w=[r_Sf])
                if t < NT - 1:
                    S.op("pool", lambda e: e.tensor_copy(out=S_b, in_=S_f), r=[r_Sf], w=[r_Sb])
                else:
                    out_toks.append(S.dma("sp", lambda e: e.dma_start(out=retp.rearrange("h k v -> k h v"), in_=S_f), r=[r_Sf]))
                for h in range(4):
                    S.op("dve", lambda e, h=h, ov=ov: e.bn_stats(out=gst[:, h * 6:(h + 1) * 6], in_=ov[:, h, :]), r=ro, w=[r_gst])
                    S.op("dve", lambda e, h=h: e.bn_aggr(out=gst[:, 24 + 2 * h:26 + 2 * h], in_=gst[:, h * 6:(h + 1) * 6]), r=[r_gst], w=[r_gst])
                mvv = gst[:, 24:32].rearrange("p (h two) -> p h two", two=2)
                S.op("act", lambda e, pb=pb: e.activation(out=scr[pb][:, 0:4], in_=mvv[:, :, 1], func=AF.Sqrt, bias=eps_gn, scale=1.0), r=[r_gst, r_eps], w=[r_scr[pb]])
                S.op("dve", lambda e, pb=pb: e.reciprocal(out=scr[pb][:, 0:4], in_=scr[pb][:, 0:4]), r=[r_scr[pb]], w=[r_scr[pb]])
                S.op("dve", lambda e, pb=pb: e.scalar_tensor_tensor(out=scr[pb][:, 4:8], in0=mvv[:, :, 0], scalar=-1.0, in1=scr[pb][:, 0:4], op0=ALU.mult, op1=ALU.mult), r=[r_gst, r_scr[pb]], w=[r_scr[pb]])
                for h in range(4):
                    S.op("act", lambda e, h=h, ov=ov, pb=pb: e.activation(out=on1[:, h * 128:(h + 1) * 128], in_=ov[:, h, :], func=AF.Identity, bias=scr[pb][:, 4 + h:5 + h], scale=scr[pb][:, h:h + 1]), r=ro + [r_scr[pb]], w=[r_on1])
                S.op("pool", lambda e, pb=pb: e.tensor_tensor(out=ret_bf[pb], in0=on1, in1=sgg[pb], op=ALU.mult), r=[r_on1, r_sgg[pb]], w=[r_ret[pb]])
                brt, rrt = ring.get(1)
                rtv = bankbf(brt)[:, 0:512].rearrange("p (h t) -> p h t", h=4)
                for h in range(4):
                    S.op("pe", lambda e, h=h, rtv=rtv, pb=pb: e.transpose(out=rtv[:, h, :], in_=ret_bf[pb][:, h * 128:(h + 1) * 128], identity=ident_b[:]), r=[r_ret[pb], r_idb], w=rrt)
                S.op("dve", lambda e, rtv=rtv, pb=pb: e.tensor_copy(out=catT[pb][:, 4:8, :], in_=rtv), r=rrt, w=[r_cat[pb]])
                by, ry = ring.get(2)
                for hf in range(2):
                    for c in range(8):
                        S.op("pe", lambda e, hf=hf, c=c, by=by, pb=pb: e.matmul(bank(by + hf), lhsT=catT[pb][:, c, :], rhs=w_o_sb[:, c, hf * 512:(hf + 1) * 512], start=(c == 0), stop=(c == 7)), r=[r_cat[pb], r_wo], w=[ry[hf]])
                S.op("dve", lambda e, by=by, pb=pb, t=t: e.scalar_tensor_tensor(out=z[pb], in0=X[:, t, :], scalar=ALPHA, in1=bank(by, 2), op0=ALU.mult, op1=ALU.add), r=[rX[t]] + ry, w=[r_z[pb]])
                layer_norm_tile(z[pb], r_z[pb], X[:, t, :], rX[t], 128, scr[pb], r_scr[pb], lnt0, r_lnt0)
                chk(f'l0t{t}')

            def ffn(layer, final):
                A.at(0)
                gated, r_gated = A.get(NJ * 1024, BF16, "gated", nres=NJ); gated = gated.rearrange("p (j n) -> p j n", j=NJ)
                x1T, r_x1T = A.get(8 * 1024, BF16, "x1T", nres=8); x1T = x1T.rearrange("p (k n) -> p k n", k=8)
                cb0_, r_cb0_ = A.get(512, F32, "cbuf"); cbuf = [cb0_, cb0_]; r_cbuf = [r_cb0_, r_cb0_]
                sb0_, r_sb0_ = A.get(512, F32, "sbf"); sbf = [sb0_, sb0_]; r_sbf = [r_sb0_, r_sb0_]
                zf0, r_zf0 = A.get(D, F32, "zf"); zf = [zf0, zf0]; r_zf = [r_zf0, r_zf0]
                scf, r_scf = dbl(16, F32, "scf")
                cwb, r_cw = A.get(NJ * 4, F32, "cw"); cw = cwb[:, 0:NJ * 3].rearrange("p (j c) -> p j c", j=NJ); cb = cwb[:, NJ * 3:NJ * 4]
                hist, r_hist = A.get(NJ * 2, F32, "hist"); hist = hist.rearrange("p (j c) -> p j c", j=NJ)
                cst, r_cst = A.get(NJ * 2, F32, "cst"); cst = cst.rearrange("p (j c) -> p j c", j=NJ)
                cso, r_cso = A.get(512, F32, "cso")
                lnt, r_lnt = A.get(2 * D, F32, "lntf")
                wup = []
                r_wup = []
                for i in range(2):
                    v, r = A.get(8 * 256, BF16, f"wup{i}")
                    wup.append(v.rearrange("p (k n) -> p k n", k=8)); r_wup.append(r)
                wdn, r_wdn = A.get(NJ * 1024, BF16, "wdn"); wdn = wdn.rearrange("p (j n) -> p j n", j=NJ)
                if with_sample:
                    xs1T, r_xs1T = A.get(128, BF16, "xs1T"); xs1T = xs1T.rearrange("p (k t) -> p k t", k=8)
                    aext, r_aext = A.get(NJ * 24, F32, "aext"); aext = aext.rearrange("p (j b r) -> p j b r", j=NJ, b=4)
                    b_s, r_bs = A.get(NJ * 16, F32, "b_s"); b_s = b_s.rearrange("p (j t) -> p j t", j=NJ)
                    scT, r_scT = A.get(NJ * 8, F32, "scT"); scT = scT.rearrange("p (j r) -> p j r", j=NJ)
                    c_s, r_cs = A.get(NJ * 16, F32, "c_s"); c_s = c_s.rearrange("p (j b l) -> p j b l", j=NJ, b=4)
                    t_s, r_ts = A.get(NJ * 16, F32, "t_s"); t_s = t_s.rearrange("p (j b l) -> p j b l", j=NJ, b=4)
                    gated_s, r_gs = A.get(NJ * 16, BF16, "gated_s"); gated_s = gated_s.rearrange("p (j t) -> p j t", j=NJ)
                print("ffn arena bytes", A.cur)
                S.dma("sp", lambda e: e.dma_start(out=cw, in_=conv_w[layer]), w=[r_cw])
                S.dma("sp", lambda e: e.dma_start(out=cb, in_=conv_b[layer]), w=[r_cw])
                load_ln(lnt, r_lnt, ln_ffn_g[layer], ln_ffn_b[layer])
                wdn_v = w_down[layer].rearrange("(j p) n -> p j n", p=128)
                wdn_loaded = [False]

                def load_wdn():
                    for jj in range(0, NJ, 2):
                        S.dma("pool", lambda e, jj=jj: e.dma_start(out=wdn[:, jj:jj + 2, :], in_=wdn_v[:, jj:jj + 2, :]), w=[r_wdn])

                def load_wup(j, slot):
                    S.dma("pool", lambda e, j=j, slot=slot: e.dma_start(out=wup[slot], in_=w_up[layer, j]), w=[r_wup[slot]])

                if with_sample:
                    transpose_x(XS[:], r_XS, xs1T, r_xs1T, npart=NS)
                    for jg in range(0, NJ, 4):
                        n = min(4, NJ - jg)
                        S.dma("sp", lambda e, jg=jg, n=n: e.dma_start(out=cso[0:8, 0:n * 128], in_=sconv[layer][:, jg * 128:(jg + n) * 128]), w=[r_cso])
                        bst, rst = ring.get(1)
                        for i_ in range(n):
                            S.op("pe", lambda e, i_=i_, bst=bst: e.transpose(out=bank(bst)[:, i_ * 8:(i_ + 1) * 8], in_=cso[0:8, i_ * 128:(i_ + 1) * 128], identity=ident_f[0:8, 0:8]), r=[r_cso, r_idf], w=rst)
                        S.op("act", lambda e, jg=jg, n=n, bst=bst: e.copy(out=scT[:, jg:jg + n, :], in_=bank(bst)[:, 0:n * 8].rearrange("p (j r) -> p j r", j=n)), r=rst, w=[r_scT])
                    S.op("pool", lambda e: e.tensor_copy(out=aext[:, :, :, 0:2], in_=scT.rearrange("p j (b r) -> p j b r", b=4)), r=[r_scT], w=[r_aext])
                cnt = 0
                for half in range(2):
                    for tt in range(8):
                        t = half * 8 + tt
                        transpose_x(X[:, t, :], rX[t], x1T[:, :, tt * 128:(tt + 1) * 128], r_x1T[tt])
                    load_wup(0, cnt % 2)
                    for j in range(NJ):
                        slot = cnt % 2
                        if j + 1 < NJ:
                            load_wup(j + 1, (cnt + 1) % 2)
                        elif half == 0:
                            pass
                        if half == 0 and j == 1:
                            load_wdn()
                        cnt += 1
                        for grp in range(2):
                            ba, ra = ring.get(1)
                            bb_, rb = ring.get(1)
                            cols = slice(grp * 512, (grp + 1) * 512)
                            for (bx, rx, c0) in ((ba, ra, 0), (bb_, rb, 128)):
                                for k in range(8):
                                    S.op("pe", lambda e, bx=bx, k=k, c0=c0, slot=slot, cols=cols: e.matmul(bank(bx), lhsT=wup[slot][:, k, c0:c0 + 128], rhs=x1T[:, k, cols], start=(k == 0), stop=(k == 7)),
                                         r=[r_wup[slot]] + r_x1T[grp * 4:grp * 4 + 4], w=rx)
                            pa = bank(ba)
                            cp = (j * 2 + grp) % 2
                            c_ = cbuf[cp]
                            s_ = sbf[cp]
                            S.op("act", lambda e, pa=pa, c_=c_, j=j: e.activation(out=c_, in_=pa, func=AF.Identity, bias=cb[:, j:j + 1], scale=cw[:, j, 2:3]), r=ra + [r_cw], w=[r_cbuf[cp]])
                            S.op("dve", lambda e, pa=pa, c_=c_, j=j: e.scalar_tensor_tensor(out=c_[:, 1:512], in0=pa[:, 0:511], scalar=cw[:, j, 1:2], in1=c_[:, 1:512], op0=ALU.mult, op1=ALU.add), r=ra + [r_cw, r_cbuf[cp]], w=[r_cbuf[cp]])
                            S.op("dve", lambda e, pa=pa, c_=c_, j=j: e.scalar_tensor_tensor(out=c_[:, 2:512], in0=pa[:, 0:510], scalar=cw[:, j, 0:1], in1=c_[:, 2:512], op0=ALU.mult, op1=ALU.add), r=ra + [r_cw, r_cbuf[cp]], w=[r_cbuf[cp]])
                            if not (half == 0 and grp == 0):
                                S.op("dve", lambda e, c_=c_, j=j: e.scalar_tensor_tensor(out=c_[:, 0:1], in0=hist[:, j, 1:2], scalar=cw[:, j, 1:2], in1=c_[:, 0:1], op0=ALU.mult, op1=ALU.add), r=[r_hist, r_cw, r_cbuf[cp]], w=[r_cbuf[cp]])
                                S.op("dve", lambda e, c_=c_, j=j: e.scalar_tensor_tensor(out=c_[:, 0:2], in0=hist[:, j, 0:2], scalar=cw[:, j, 0:1], in1=c_[:, 0:2], op0=ALU.mult, op1=ALU.add), r=[r_hist, r_cw, r_cbuf[cp]], w=[r_cbuf[cp]])
                            if half == 1 and grp == 1:
                                S.op("act", lambda e, pa=pa, j=j: e.copy(out=cst[:, j, :], in_=pa[:, 510:512]), r=ra, w=[r_cst])
                            else:
                                S.op("act", lambda e, pa=pa, j=j: e.copy(out=hist[:, j, :], in_=pa[:, 510:512]), r=ra, w=[r_hist])
                            S.op("act", lambda e, c_=c_, s_=s_: e.activation(out=s_, in_=c_, func=AF.Silu), r=[r_cbuf[cp]], w=[r_sbf[cp]])
                            S.op("dve", lambda e, s_=s_, bb_=bb_, j=j, cols=cols: e.tensor_tensor(out=gated[:, j, cols], in0=s_, in1=bank(bb_), op=ALU.mult), r=[r_sbf[cp]] + rb, w=[r_gated[j]])
                        if with_sample and half == 0:
                            bsa, rsa = ring.get(1)
                            for (c0, col) in ((0, 0), (128, 16)):
                                for k in range(8):
                                    S.op("pe", lambda e, k=k, c0=c0, col=col, slot=slot, bsa=bsa: e.matmul(bank(bsa)[:, col:col + 16], lhsT=wup[slot][:, k, c0:c0 + 128], rhs=xs1T[:, k, :], start=(k == 0), stop=(k == 7)), r=[r_wup[slot], r_xs1T], w=rsa)
                            S.op("act", lambda e, j=j, bsa=bsa: e.copy(out=aext[:, j, :, 2:6], in_=bank(bsa)[:, 0:16].rearrange("p (b l) -> p b l", b=4)), r=rsa, w=[r_aext])
                            S.op("dve", lambda e, j=j, bsa=bsa: e.tensor_copy(out=b_s[:, j, :], in_=bank(bsa)[:, 16:32]), r=rsa, w=[r_bs])
                    if with_sample and half == 0:
                        def cwb(c):
                            return cw[:, :, c:c + 1].unsqueeze(3).to_broadcast([128, NJ, 4, 4])
                        S.op("dve", lambda e: e.tensor_tensor(out=c_s, in0=aext[:, :, :, 2:6], in1=cwb(2), op=ALU.mult), r=[r_aext, r_cw], w=[r_cs])
                        S.op("pool", lambda e: e.tensor_tensor(out=t_s, in0=aext[:, :, :, 1:5], in1=cwb(1), op=ALU.mult), r=[r_aext, r_cw], w=[r_ts])
                        S.op("dve", lambda e: e.tensor_tensor(out=c_s, in0=c_s, in1=t_s, op=ALU.add), r=[r_cs, r_ts], w=[r_cs])
                        S.op("pool", lambda e: e.tensor_tensor(out=t_s, in0=aext[:, :, :, 0:4], in1=cwb(0), op=ALU.mult), r=[r_aext, r_cw], w=[r_ts])
                        S.op("dve", lambda e: e.tensor_tensor(out=c_s, in0=c_s, in1=t_s, op=ALU.add), r=[r_cs, r_ts], w=[r_cs])
                        S.op("dve", lambda e: e.tensor_tensor(out=c_s, in0=c_s, in1=cb.unsqueeze(2).unsqueeze(3).to_broadcast([128, NJ, 4, 4]), op=ALU.add), r=[r_cs, r_cw], w=[r_cs])
                        S.op("act", lambda e: e.activation(out=c_s, in_=c_s, func=AF.Silu), r=[r_cs], w=[r_cs])
                        S.op("dve", lambda e: e.tensor_tensor(out=gated_s, in0=c_s.rearrange("p j b l -> p j (b l)"), in1=b_s, op=ALU.mult), r=[r_cs, r_bs], w=[r_gs])
                        S.op("pool", lambda e: e.tensor_copy(out=scT.rearrange("p j (b r) -> p j b r", b=4), in_=aext[:, :, :, 4:6]), r=[r_aext], w=[r_scT])
                        for jg in range(0, NJ, 4):
                            n = min(4, NJ - jg)
                            bst, rst = ring.get(1)
                            for i_ in range(n):
                                S.op("pe", lambda e, i_=i_, jg=jg, bst=bst: e.transpose(out=bank(bst)[0:8, i_ * 128:(i_ + 1) * 128], in_=scT[:, jg + i_, :], identity=ident_f[:]), r=[r_scT, r_idf], w=rst)
                            S.op("act", lambda e, n=n, bst=bst: e.copy(out=cso[0:8, 0:n * 128], in_=bank(bst)[0:8, 0:n * 128]), r=rst, w=[r_cso])
                            out_toks.append(S.dma("sp", lambda e, jg=jg, n=n: e.dma_start(out=convs[layer][:, jg * 128:(jg + n) * 128], in_=cso[0:8, 0:n * 128]), r=[r_cso]))
                    for tt in range(8):
                        t = half * 8 + tt
                        pb = tt % 2
                        by, ry = ring.get(2)
                        for hf in range(2):
                            for j in range(NJ):
                                S.op("pe", lambda e, hf=hf, j=j, by=by, tt=tt: e.matmul(bank(by + hf), lhsT=gated[:, j, tt * 128:(tt + 1) * 128], rhs=wdn[:, j, hf * 512:(hf + 1) * 512], start=(j == 0), stop=(j == NJ - 1)),
                                     r=[r_gated[j], r_wdn], w=[ry[hf]])
                        S.op("dve", lambda e, by=by, pb=pb, t=t: e.scalar_tensor_tensor(out=zf[pb], in0=X[:, t, :], scalar=ALPHA, in1=bank(by, 2), op0=ALU.mult, op1=ALU.add), r=[rX[t]] + ry, w=[r_zf[pb]])
                        layer_norm_tile(zf[pb], r_zf[pb], X[:, t, :], rX[t], 128, scf[pb], r_scf[pb], lnt, r_lnt)
                        if final:
                            out_toks.append(S.dma("sp", lambda e, t=t: e.dma_start(out=yp[t * 128:(t + 1) * 128, :], in_=X[:, t, :]), r=[rX[t]]))
                if with_sample:
                    bys, rys = ring.get(2)
                    for hf in range(2):
                        for j in range(NJ):
                            S.op("pe", lambda e, hf=hf, j=j, bys=bys: e.matmul(bank(bys + hf)[0:NS, :], lhsT=gated_s[:, j, :], rhs=wdn[:, j, hf * 512:(hf + 1) * 512], start=(j == 0), stop=(j == NJ - 1)), r=[r_gs, r_wdn], w=[rys[hf]])
                    S.op("dve", lambda e, bys=bys: e.scalar_tensor_tensor(out=zf[0][0:NS, :], in0=XS[:], scalar=ALPHA, in1=bank(bys, 2)[0:NS, :], op0=ALU.mult, op1=ALU.add), r=[r_XS] + rys, w=[r_zf[0]])
                    layer_norm_tile(zf[0][0:NS, :], r_zf[0], XS[:], r_XS, NS, scf[0], r_scf[0], lnt, r_lnt)
                    if final:
                        out_toks.append(S.dma("sp", lambda e: e.dma_start(out=ys, in_=XS[:]), r=[r_XS]))
                bc, rc = ring.get(1)
                for j in range(NJ):
                    S.op("pe", lambda e, j=j, bc=bc: e.transpose(out=bank(bc)[0:2, (j % 4) * 128:(j % 4 + 1) * 128], in_=cst[:, j, :], identity=ident_f[:]), r=[r_cst, r_idf], w=rc)
                    if j % 4 == 3 or j == NJ - 1:
                        j0 = j - j % 4
                        n = j - j0 + 1
                        S.op("act", lambda e, bc=bc, n=n: e.copy(out=cso[0:2, 0:n * 128], in_=bank(bc)[0:2, 0:n * 128]), r=rc, w=[r_cso])
                        out_toks.append(S.dma("sp", lambda e, j0=j0, n=n: e.dma_start(out=convp[layer][:, j0 * 128:(j0 + n) * 128], in_=cso[0:2, 0:n * 128]), r=[r_cso]))
                        if j != NJ - 1:
                            bc, rc = ring.get(1)


            def sample_l0():
                A.at(L0_BUF)
                xsT, r_xsT = A.get(128, BF16, "xsT"); xsT = xsT.rearrange("p (k t) -> p k t", k=8)
                tb, r_tb = A.get(68, F32, "stab")
                c2s = tb[:, 0:16]; sss = tb[:, 16:32]
                decTs = tb[0:4, 32:48]
                qdecs = tb[:, 48:64].rearrange("p (h l) -> p h l", h=4)
                kdecs = tb[0:4, 64:68]
                S.dma("sp", lambda e: e.dma_start(out=c2s, in_=tdr["c2s"]), w=[r_tb])
                S.dma("sp", lambda e: e.dma_start(out=sss, in_=tdr["sss"]), w=[r_tb])
                S.dma("sp", lambda e: e.dma_start(out=decTs, in_=tdr["decTs"].rearrange("m h l -> m (h l)")), w=[r_tb])
                S.dma("sp", lambda e: e.dma_start(out=qdecs, in_=tdr["qdecs"]), w=[r_tb])
                S.dma("sp", lambda e: e.dma_start(out=kdecs, in_=tdr["kdecs"]), w=[r_tb])
                sp_sb, r_sp = A.get(512, F32, "sp_sb")
                S.dma("sp", lambda e: e.dma_start(out=sp_sb[0:60, :], in_=spool), w=[r_sp])
                S0f, r_S0f = A.get(16 * 128, F32, "S0f"); S0f = S0f.rearrange("p (n v) -> p n v", n=16)
                S0b, r_S0b = A.get(16 * 128, BF16, "S0b"); S0b = S0b.rearrange("p (n v) -> p n v", n=16)
                S.dma("sp", lambda e: e.dma_start(out=S0f, in_=sret.rearrange("n k v -> k n v")), w=[r_S0f])
                S.op("pool", lambda e: e.tensor_copy(out=S0b, in_=S0f), r=[r_S0f], w=[r_S0b])
                for bb in range(SB):
                    out_toks.append(S.dma("sp", lambda e, bb=bb: e.dma_start(out=pools[bb * 15:bb * 15 + 11, :], in_=spool[bb * 15 + 4:bb * 15 + 15, :]), semkey="d2d"))
                transpose_x(XS[:], r_XS, xsT, r_xsT, npart=NS)
                bq_, rq_ = ring.get(1)
                pqk = bank(bq_)[:, 0:192].rearrange("p (g t) -> p g t", g=12)
                for g in range(12):
                    for k in range(8):
                        S.op("pe", lambda e, g=g, k=k: e.matmul(pqk[:, g, :], lhsT=w_in_sb[:, k, g * 128:(g + 1) * 128], rhs=xsT[:, k, :], start=(k == 0), stop=(k == 7)), r=[r_win, r_xsT], w=rq_)
                bu_, ru_ = ring.get(1)
                for k in range(8):
                    S.op("pe", lambda e, k=k: e.matmul(bank(bu_)[0:NS, :], lhsT=xsT[:, k, :], rhs=w_in_sb[:, k, 0:512], start=(k == 0), stop=(k == 7)), r=[r_win, r_xsT], w=ru_)
                utok, r_utok = A.get(512, F32, "utok")
                S.op("act", lambda e: e.copy(out=utok[0:NS, :], in_=bank(bu_)[0:NS, :]), r=ru_, w=[r_utok])
                for bb in range(SB):
                    out_toks.append(S.dma("sp", lambda e, bb=bb: e.dma_start(out=pools[bb * 15 + 11:bb * 15 + 15, :], in_=utok[bb * 4:bb * 4 + 4, :]), r=[r_utok]))
                bh_, rh_ = ring.get(1)
                for g in range(4):
                    S.op("pe", lambda e, g=g: e.transpose(out=bank(bh_)[:, g * 64:g * 64 + 60], in_=sp_sb[0:60, g * 128:(g + 1) * 128], identity=ident_f[0:60, 0:60]), r=[r_sp, r_idf], w=rh_)
                ue, r_ue = A.get(16 * 19, F32, "ues"); ue4 = ue.rearrange("p (g b r) -> p g b r", g=4, b=4); ue3 = ue.rearrange("p (n r) -> p n r", n=16)
                hv = bank(bh_)[:, 0:256].rearrange("p (g c) -> p g c", g=4)[:, :, 0:60].rearrange("p g (b r) -> p g b r", b=4)
                for g in range(4):
                    S.op("act", lambda e, g=g: e.copy(out=ue4[:, g, :, 0:15], in_=hv[:, g]), r=rh_, w=[r_ue])
                    S.op("act", lambda e, g=g: e.copy(out=ue4[:, g, :, 15:19], in_=pqk[:, g, :].rearrange("p (b l) -> p b l", b=4)), r=rq_, w=[r_ue])
                tAs, r_tAs = A.get(16 * 19, F32, "tAs"); tAs = tAs.rearrange("p (n r) -> p n r", n=16)
                tBs, r_tBs = A.get(16 * 19, F32, "tBs"); tBs = tBs.rearrange("p (n r) -> p n r", n=16)
                S.op("pool", lambda e: e.tensor_tensor(out=tAs[:, :, 1:19], in0=ue3[:, :, 1:19], in1=ue3[:, :, 0:18], op=ALU.add), r=[r_ue], w=[r_tAs])
                S.op("pool", lambda e: e.tensor_tensor(out=tBs[:, 4:16, 3:19], in0=tAs[:, 4:16, 3:19], in1=tAs[:, 4:16, 1:17], op=ALU.add), r=[r_tAs], w=[r_tBs])
                S.op("pool", lambda e: e.tensor_tensor(out=tAs[:, 8:16, 7:19], in0=tBs[:, 8:16, 7:19], in1=tBs[:, 8:16, 3:15], op=ALU.add), r=[r_tBs], w=[r_tAs])
                S.op("pool", lambda e: e.tensor_tensor(out=tBs[:, 12:16, 15:19], in0=tAs[:, 12:16, 15:19], in1=tAs[:, 12:16, 7:11], op=ALU.add), r=[r_tAs], w=[r_tBs])
                pooleds, r_pooleds = A.get(64, BF16, "pooleds"); pooleds = pooleds.rearrange("p (g b l) -> p g b l", g=4, b=4)
                fins = (tAs, tBs, tAs, tBs)
                for g in range(4):
                    S.op("dve", lambda e, g=g: e.scalar_tensor_tensor(out=pooleds[:, g], in0=fins[g][:, g * 4:(g + 1) * 4, 15:19], scalar=1.0 / POOL_W[g], in1=ue3[:, g * 4:(g + 1) * 4, 15:19], op0=ALU.mult, op1=ALU.subtract), r=[r_tAs, r_tBs, r_ue], w=[r_pooleds])
                catTs, r_catTs = A.get(128, BF16, "catTs"); catTs = catTs.rearrange("p (c t) -> p c t", c=8)
                bpm_, rpm_ = ring.get(1)
                pms = bank(bpm_)[:, 0:64].rearrange("p (g t) -> p g t", g=4)
                for g in range(4):
                    S.op("pe", lambda e, g=g: e.matmul(pms[:, g, :], lhsT=poolw_sb[:, g, :], rhs=pooleds[:, g].rearrange("p b l -> p (b l)"), start=True, stop=True), r=[r_poolw, r_pooleds], w=rpm_)
                for g in range(4):
                    S.op("act", lambda e, g=g: e.activation(out=catTs[:, g, :], in_=pms[:, g, :], func=AF.Identity, bias=0.0, scale=pscale[:, g:g + 1]), r=rpm_ + [r_c0], w=[r_catTs])
                qkfs, r_qkfs = A.get(128, F32, "qkfs"); qkfs = qkfs.rearrange("p (h t) -> p h t", h=8)
                rAs, r_rAs = A.get(128, F32, "rAs"); rAs = rAs.rearrange("p (h t) -> p h t", h=8)
                rBs, r_rBs = A.get(128, F32, "rBs"); rBs = rBs.rearrange("p (h t) -> p h t", h=8)
                qkTs, r_qkTs = A.get(128, BF16, "qkTs"); qkTs = qkTs.rearrange("p (h t) -> p h t", h=8)
                qdTs, r_qdTs = A.get(64, BF16, "qdTs"); qdTs = qdTs.rearrange("p (h t) -> p h t", h=4)
                S.op("act", lambda e: e.copy(out=qkfs, in_=pqk[:, 4:12, :]), r=rq_, w=[r_qkfs])
                S.op("dve", lambda e: e.tensor_tensor(out=rAs, in0=qkfs, in1=c2s.unsqueeze(1).to_broadcast([128, 8, 16]), op=ALU.mult), r=[r_qkfs, r_tb], w=[r_rAs])
                S.op("pool", lambda e: e.tensor_tensor(out=rBs[0:64], in0=qkfs[64:128], in1=sss[64:128].unsqueeze(1).to_broadcast([64, 8, 16]), op=ALU.mult), r=[r_qkfs, r_tb], w=[r_rBs])
                S.op("pool", lambda e: e.tensor_tensor(out=rBs[64:128], in0=qkfs[0:64], in1=sss[0:64].unsqueeze(1).to_broadcast([64, 8, 16]), op=ALU.mult), r=[r_qkfs, r_tb], w=[r_rBs])
                S.op("dve", lambda e: e.tensor_tensor(out=qkTs, in0=rAs, in1=rBs, op=ALU.add), r=[r_rAs, r_rBs], w=[r_qkTs])
                S.op("pool", lambda e: e.tensor_tensor(out=qdTs.rearrange("p h (b l) -> p h b l", b=4), in0=qkTs[:, 0:4, :].rearrange("p h (b l) -> p h b l", b=4), in1=qdecs.unsqueeze(2).to_broadcast([128, 4, 4, 4]), op=ALU.mult), r=[r_qkTs, r_tb], w=[r_qdTs])
                v_s, r_vs = A.get(SB * 512, BF16, "v_s"); v_s = v_s.rearrange("p (b n) -> p b n", b=SB)
                sTs, r_sTs = A.get(64, BF16, "sTs"); sTs = sTs.rearrange("p (b h l) -> p b h l", b=4, h=4)
                kps, r_kps = A.get(512, BF16, "kps"); kps = kps.rearrange("p (h d) -> p h d", h=4)
                S.op("pool", lambda e: e.memset(v_s, 0.0), w=[r_vs])
                S.op("pool", lambda e: e.memset(sTs, 0.0), w=[r_sTs])
                S.op("pool", lambda e: e.memset(kps, 0.0), w=[r_kps])
                sggs, r_sggs = A.get(SB * 512, F32, "sggs"); sggs = sggs.rearrange("p (b n) -> p b n", b=SB)
                for bb in range(SB):
                    bv_, rv_ = ring.get(1)
                    bg_, rg_ = ring.get(1)
                    for (bx, rx, c0) in ((bv_, rv_, 1536), (bg_, rg_, 2048)):
                        for k in range(8):
                            S.op("pe", lambda e, bx=bx, k=k, c0=c0, bb=bb: e.matmul(bank(bx)[0:4, :], lhsT=xsT[:, k, bb * 4:bb * 4 + 4], rhs=w_in_sb[:, k, c0:c0 + 512], start=(k == 0), stop=(k == 7)), r=[r_win, r_xsT], w=rx)
                    S.op("act", lambda e, bv_=bv_, bb=bb: e.copy(out=v_s[0:4, bb, :], in_=bank(bv_)[0:4, :]), r=rv_, w=[r_vs])
                    S.op("act", lambda e, bg_=bg_, bb=bb: e.activation(out=sggs[0:4, bb, :], in_=bank(bg_)[0:4, :], func=AF.Silu), r=rg_, w=[r_sggs])
                    S.op("pool", lambda e, bb=bb: e.tensor_tensor(out=sggs[0:4, bb, :], in0=sggs[0:4, bb, :], in1=gng[0:4], op=ALU.mult), r=[r_sggs, r_c0], w=[r_sggs])
                bs_, rs_ = ring.get(1)
                for bb in range(SB):
                    for h in range(4):
                        c0 = (bb * 4 + h) * 4
                        S.op("pe", lambda e, bb=bb, h=h, c0=c0: e.matmul(bank(bs_)[0:4, c0:c0 + 4], lhsT=qkTs[:, 4 + h, bb * 4:bb * 4 + 4], rhs=qkTs[:, h, bb * 4:bb * 4 + 4], start=True, stop=True), r=[r_qkTs], w=rs_)
                S.op("dve", lambda e: e.tensor_tensor(out=sTs[0:4].rearrange("p b h l -> p b (h l)"), in0=bank(bs_)[0:4, 0:64].rearrange("p (b n) -> p b n", b=4), in1=decTs.unsqueeze(1).to_broadcast([4, 4, 16]), op=ALU.mult), r=rs_ + [r_tb], w=[r_sTs])
                Snew, r_Snew = dbl(512, F32, "Snew", "p (h v) -> p h v", h=4)
                gsts, r_gsts = A.get(40, F32, "gsts")
                on1s, r_on1s = A.get(512, F32, "on1s")
                rets_bf, r_retsb = A.get(512, BF16, "retsb")
                for bb in range(SB):
                    bo_, ro_ = ring.get(1)
                    ovs = bank(bo_)[0:4, :].rearrange("p (h t) -> p h t", h=4)
                    for h in range(4):
                        S.op("pe", lambda e, h=h, bb=bb, ovs=ovs: e.matmul(ovs[:, h, :], lhsT=sTs[:, bb, h, :], rhs=v_s[:, bb, h * 128:(h + 1) * 128], start=True, stop=False), r=[r_sTs, r_vs], w=ro_)
                        S.op("pe", lambda e, h=h, bb=bb, ovs=ovs: e.matmul(ovs[:, h, :], lhsT=qdTs[:, h, bb * 4:bb * 4 + 4], rhs=S0b[:, bb * 4 + h, :], start=False, stop=True), r=[r_qdTs, r_S0b], w=ro_)
                    bkt_, rkt_ = ring.get(1)
                    ktvs = bankbf(bkt_)[0:4, 0:512].rearrange("p (h t) -> p h t", h=4)
                    for h in range(4):
                        S.op("pe", lambda e, h=h, bb=bb, ktvs=ktvs: e.transpose(out=ktvs[:, h, :], in_=qkTs[:, 4 + h, bb * 4:bb * 4 + 4], identity=ident_b[:]), r=[r_qkTs, r_idb], w=rkt_)
                    for h in range(4):
                        S.op("act", lambda e, h=h, ktvs=ktvs: e.activation(out=kps[0:4, h, :], in_=ktvs[:, h, :], func=AF.Identity, bias=0.0, scale=kdecs[:, h:h + 1]), r=rkt_ + [r_tb], w=[r_kps])
                    bds_, rds_ = ring.get(1)
                    dss = bank(bds_).rearrange("p (h t) -> p h t", h=4)
                    for h in range(4):
                        S.op("pe", lambda e, h=h, bb=bb, dss=dss: e.matmul(dss[:, h, :], lhsT=kps[:, h, :], rhs=v_s[:, bb, h * 128:(h + 1) * 128], start=True, stop=True), r=[r_kps, r_vs], w=rds_)
                    sn = Snew[bb % 2]
                    for h in range(4):
                        S.op("dve", lambda e, h=h, bb=bb, dss=dss, sn=sn: e.scalar_tensor_tensor(out=sn[:, h, :], in0=S0f[:, bb * 4 + h, :], scalar=tabs["gCs"][h], in1=dss[:, h, :], op0=ALU.mult, op1=ALU.add), r=rds_ + [r_S0f], w=[r_Snew[bb % 2]])
                    out_toks.append(S.dma("sp", lambda e, bb=bb, sn=sn: e.dma_start(out=rets[bb * 4:(bb + 1) * 4].rearrange("h k v -> k h v"), in_=sn), r=[r_Snew[bb % 2]]))
                    for h in range(4):
                        S.op("dve", lambda e, h=h, ovs=ovs: e.bn_stats(out=gsts[0:4, h * 6:(h + 1) * 6], in_=ovs[:, h, :]), r=ro_, w=[r_gsts])
                        S.op("dve", lambda e, h=h: e.bn_aggr(out=gsts[0:4, 24 + 2 * h:26 + 2 * h], in_=gsts[0:4, h * 6:(h + 1) * 6]), r=[r_gsts], w=[r_gsts])
                    mvs = gsts[0:4, 24:32].rearrange("p (h two) -> p h two", two=2)
                    S.op("act", lambda e: e.activation(out=gsts[0:4, 32:36], in_=mvs[:, :, 1], func=AF.Sqrt, bias=eps_gn[0:4], scale=1.0), r=[r_gsts, r_eps], w=[r_gsts])
                    S.op("dve", lambda e: e.reciprocal(out=gsts[0:4, 32:36], in_=gsts[0:4, 32:36]), r=[r_gsts], w=[r_gsts])
                    S.op("dve", lambda e: e.scalar_tensor_tensor(out=gsts[0:4, 36:40], in0=mvs[:, :, 0], scalar=-1.0, in1=gsts[0:4, 32:36], op0=ALU.mult, op1=ALU.mult), r=[r_gsts], w=[r_gsts])
                    for h in range(4):
                        S.op("act", lambda e, h=h, ovs=ovs: e.activation(out=on1s[0:4, h * 128:(h + 1) * 128], in_=ovs[:, h, :], func=AF.Identity, bias=gsts[0:4, 36 + h:37 + h], scale=gsts[0:4, 32 + h:33 + h]), r=ro_ + [r_gsts], w=[r_on1s])
                    S.op("pool", lambda e, bb=bb: e.tensor_tensor(out=rets_bf[0:4, :], in0=on1s[0:4, :], in1=sggs[0:4, bb, :], op=ALU.mult), r=[r_on1s, r_sggs], w=[r_retsb])
                    brt_, rrt_ = ring.get(1)
                    rtvs = bankbf(brt_)[:, 0:16].rearrange("p (h t) -> p h t", h=4)
                    for h in range(4):
                        S.op("pe", lambda e, h=h, rtvs=rtvs: e.transpose(out=rtvs[:, h, :], in_=rets_bf[0:4, h * 128:(h + 1) * 128], identity=ident_b[0:4, 0:4]), r=[r_retsb, r_idb], w=rrt_)
                    S.op("dve", lambda e, rtvs=rtvs, bb=bb: e.tensor_copy(out=catTs[:, 4:8, bb * 4:bb * 4 + 4], in_=rtvs), r=rrt_, w=[r_catTs])
                by_, ry_ = ring.get(2)
                for hf in range(2):
                    for c in range(8):
                        S.op("pe", lambda e, hf=hf, c=c: e.matmul(bank(by_ + hf)[0:NS, :], lhsT=catTs[:, c, :], rhs=w_o_sb[:, c, hf * 512:(hf + 1) * 512], start=(c == 0), stop=(c == 7)), r=[r_catTs, r_wo], w=[ry_[hf]])
                zs, r_zs = A.get(D, F32, "zs")
                scs, r_scs = A.get(16, F32, "scs")
                S.op("dve", lambda e: e.scalar_tensor_tensor(out=zs[0:NS, :], in0=XS[:], scalar=ALPHA, in1=bank(by_, 2)[0:NS, :], op0=ALU.mult, op1=ALU.add), r=[r_XS] + ry_, w=[r_zs])
                layer_norm_tile(zs[0:NS, :], r_zs, XS[:], r_XS, NS, scs, r_scs, lnt0, r_lnt0)

            if with_sample:
                sample_l0()
            chk('l0')
            ffn(0, False)
            chk('ffn0')

            A.at(0)
            wdq_sb, r_wdq = A.get(8 * 512, BF16, "wdq"); wdq_sb = wdq_sb.rearrange("p (k n) -> p k n", k=8)
            wuq_sb, r_wuq = A.get(4 * 1536, BF16, "wuq"); wuq_sb = wuq_sb.rearrange("p (k n) -> p k n", k=4)
            wdkv_sb, r_wdkv = A.get(8 * 320, BF16, "wdkv"); wdkv_sb = wdkv_sb.rearrange("p (k n) -> p k n", k=8)
            wukT_sb, r_wukT = A.get(8 * 256, BF16, "wukT"); wukT_sb = wukT_sb.rearrange("p (h c) -> p h c", h=8)
            wuv_sb, r_wuv = A.get(2 * 1024, BF16, "wuv"); wuv_sb = wuv_sb.rearrange("p (k n) -> p k n", k=2)
            wo1_sb, r_wo1 = A.get(8 * 1024, BF16, "wo1"); wo1_sb = wo1_sb.rearrange("p (k n) -> p k n", k=8)
            c1buf, r_c1 = A.get(512 + 256, F32, "c1")
            qng = c1buf[:, 0:512]
            kvng = c1buf[:, 512:768]
            lnt1, r_lnt1 = A.get(2 * D, F32, "lnt1")
            ckvT, r_ckvT = A.get(2 * SEQ, BF16, "ckvT", nres=NT); ckvT = ckvT.rearrange("p (k n) -> p k n", k=2)
            ckvK, r_ckvK = A.get(NT * 256, BF16, "ckvK", nres=NT); ckvK = ckvK.rearrange("p (t c) -> p t c", t=NT)
            kpeT, r_kpeT = A.get(SEQ, BF16, "kpeT", nres=NT)
            S.dma("pool", lambda e: e.dma_start(out=wdq_sb, in_=w_dq.rearrange("(k p) n -> p k n", p=128)), w=[r_wdq])
            S.dma("pool", lambda e: e.dma_start(out=wdkv_sb, in_=w_dkv.rearrange("(k p) n -> p k n", p=128)), w=[r_wdkv])
            S.dma("pool", lambda e: e.dma_start(out=wuq_sb, in_=w_uq.rearrange("(k p) n -> p k n", p=128)), w=[r_wuq])
            S.dma("pool", lambda e: e.dma_start(out=wukT_sb, in_=w_ukT), w=[r_wukT])
            S.dma("pool", lambda e: e.dma_start(out=wuv_sb, in_=w_uv.rearrange("(k p) n -> p k n", p=128)), w=[r_wuv])
            S.dma("pool", lambda e: e.dma_start(out=wo1_sb, in_=w_o1.rearrange("(k p) n -> p k n", p=128)), w=[r_wo1])
            S.dma("sp", lambda e: e.dma_start(out=qng, in_=qn_g.partition_broadcast(128)), w=[r_c1])
            S.dma("sp", lambda e: e.dma_start(out=kvng, in_=kvn_g.partition_broadcast(128)), w=[r_c1])
            load_ln(lnt1, r_lnt1, ln_mix_g[1], ln_mix_b[1])
            L1_BUF = A.cur

            xT1, r_xT1 = dbl(1024, BF16, "xT1", "p (k t) -> p k t", k=8)
            rt1, r_rt1 = dbl(128, F32, "rt1")
            sc1, r_sc1 = dbl(16, F32, "sc1")
            cqn, r_cqn = dbl(512, BF16, "cqn")
            cqnT, r_cqnT = dbl(512, BF16, "cqnT", "p (k t) -> p k t", k=4)
            ckv_f, r_ckvf = dbl(256, F32, "ckv_f")
            kpe_f, r_kpef = dbl(64, F32, "kpe_f")
            kpe_t, r_kpet = A.get(128, F32, "kpe_t")
            kpe_b, r_kpeb = dbl(128, BF16, "kpe_b")
            qnT, r_qnT = dbl(1024, BF16, "qnT", "p (h t) -> p h t", h=8)
            qpe_t, r_qpet = A.get(1024, F32, "qpe_t")
            junk, r_junk = qpe_t[:, 0:512], r_qpet
            qpe_b, r_qpeb = dbl(512, BF16, "qpe_b")
            qpeT, r_qpeT = dbl(1024, BF16, "qpeT", "p (h t) -> p h t", h=8)
            qlT0, r_qlT0 = A.get(2048, BF16, "qlT"); qlT0 = qlT0.rearrange("p (h k t) -> p h k t", h=8, k=2); qlT = [qlT0, qlT0]; r_qlT = [r_qlT0, r_qlT0]
            p_sb, r_p = dbl(SEQ, BF16, "p_sb")
            pT_sb, r_pT = dbl(SEQ, BF16, "pT_sb")
            st1, r_st1 = dbl(8, F32, "st1")
            ol_b, r_olb = dbl(256, BF16, "ol_b")
            olT, r_olT = dbl(256, BF16, "olT", "p (k t) -> p k t", k=2)
            oT1, r_oT1 = dbl(1024, BF16, "oT1", "p (h t) -> p h t", h=8)
            z10, r_z10 = A.get(D, F32, "z1"); z1 = [z10, z10]; r_z1 = [r_z10, r_z10]
            print("layer1 arena bytes", A.cur)

            for i_ in range(2):
                S.op("pool", lambda e, i_=i_: e.memset(qpeT[i_], 0.0), w=[r_qpeT[i_]])
            hcnt = 0
            for t in range(NT):
                pb = t % 2
                tok = slice(t * 128, (t + 1) * 128)
                nk = (t + 1) * 128
                S.dma("sp", lambda e, pb=pb, t=t: e.dma_start(out=rt1[pb][:, 0:64], in_=tdr["cc1"][:, t, :]), w=[r_rt1[pb]])
                S.dma("sp", lambda e, pb=pb, t=t: e.dma_start(out=rt1[pb][:, 64:128], in_=tdr["ss1"][:, t, :]), w=[r_rt1[pb]])
                transpose_x(X[:, t, :], rX[t], xT1[pb], r_xT1[pb])
                bcq, rcq = ring.get(1)
                bkv, rkv = ring.get(1)
                for k in range(8):
                    S.op("pe", lambda e, k=k, pb=pb, bcq=bcq: e.matmul(bank(bcq), lhsT=xT1[pb][:, k, :], rhs=wdq_sb[:, k, :], start=(k == 0), stop=(k == 7)), r=[r_xT1[pb], r_wdq], w=rcq)
                for k in range(8):
                    S.op("pe", lambda e, k=k, pb=pb, bkv=bkv: e.matmul(bank(bkv)[:, 0:320], lhsT=xT1[pb][:, k, :], rhs=wdkv_sb[:, k, :], start=(k == 0), stop=(k == 7)), r=[r_xT1[pb], r_wdkv], w=rkv)
                sc = sc1[pb]
                S.op("act", lambda e, bcq=bcq, sc=sc: e.activation(out=junk, in_=bank(bcq), func=AF.Square, accum_out=sc[:, 0:1]), r=rcq, w=[r_junk, r_sc1[pb]])
                S.op("act", lambda e, sc=sc: e.activation(out=sc[:, 1:2], in_=sc[:, 0:1], func=AF.Sqrt, bias=eps_rms, scale=1.0 / 512), r=[r_sc1[pb], r_eps], w=[r_sc1[pb]])
                S.op("dve", lambda e, sc=sc: e.reciprocal(out=sc[:, 1:2], in_=sc[:, 1:2]), r=[r_sc1[pb]], w=[r_sc1[pb]])
                S.op("dve", lambda e, bcq=bcq, sc=sc, pb=pb: e.scalar_tensor_tensor(out=cqn[pb], in0=bank(bcq), scalar=sc[:, 1:2], in1=qng, op0=ALU.mult, op1=ALU.mult), r=rcq + [r_sc1[pb], r_c1], w=[r_cqn[pb]])
                S.op("act", lambda e, bkv=bkv, sc=sc: e.activation(out=junk[:, 0:256], in_=bank(bkv)[:, 0:256], func=AF.Square, accum_out=sc[:, 2:3]), r=rkv, w=[r_junk, r_sc1[pb]])
                S.op("act", lambda e, sc=sc: e.activation(out=sc[:, 3:4], in_=sc[:, 2:3], func=AF.Sqrt, bias=eps_rms, scale=1.0 / 256), r=[r_sc1[pb], r_eps], w=[r_sc1[pb]])
                S.op("dve", lambda e, sc=sc: e.reciprocal(out=sc[:, 3:4], in_=sc[:, 3:4]), r=[r_sc1[pb]], w=[r_sc1[pb]])
                S.op("dve", lambda e, bkv=bkv, sc=sc, pb=pb: e.scalar_tensor_tensor(out=ckv_f[pb], in0=bank(bkv)[:, 0:256], scalar=sc[:, 3:4], in1=kvng, op0=ALU.mult, op1=ALU.mult), r=rkv + [r_sc1[pb], r_c1], w=[r_ckvf[pb]])
                out_toks.append(S.dma("sp", lambda e, pb=pb, tok=tok: e.dma_start(out=ckvp[tok, :], in_=ckv_f[pb]), r=[r_ckvf[pb]]))
                S.op("pool", lambda e, pb=pb, t=t: e.tensor_copy(out=ckvK[:, t, :], in_=ckv_f[pb]), r=[r_ckvf[pb]], w=[r_ckvK[t]])
                S.op("dve", lambda e, bkv=bkv, pb=pb: e.tensor_tensor(out=kpe_t[:, 0:64], in0=bank(bkv)[:, 256:320], in1=rt1[pb][:, 0:64], op=ALU.mult), r=rkv + [r_rt1[pb]], w=[r_kpet])
                S.op("dve", lambda e, bkv=bkv, pb=pb: e.tensor_tensor(out=kpe_t[:, 64:96], in0=bank(bkv)[:, 288:320], in1=rt1[pb][:, 64:96], op=ALU.mult), r=rkv + [r_rt1[pb]], w=[r_kpet])
                S.op("dve", lambda e, bkv=bkv, pb=pb: e.tensor_tensor(out=kpe_t[:, 96:128], in0=bank(bkv)[:, 256:288], in1=rt1[pb][:, 96:128], op=ALU.mult), r=rkv + [r_rt1[pb]], w=[r_kpet])
                S.op("pool", lambda e, pb=pb: e.tensor_tensor(out=kpe_f[pb], in0=kpe_t[:, 0:64], in1=kpe_t[:, 64:128], op=ALU.add), r=[r_kpet], w=[r_kpef[pb]])
                out_toks.append(S.dma("sp", lambda e, pb=pb, tok=tok: e.dma_start(out=kpep[tok, :], in_=kpe_f[pb]), r=[r_kpef[pb]]))
                S.op("pool", lambda e, pb=pb: e.tensor_copy(out=kpe_b[pb][:, 0:64], in_=kpe_f[pb]), r=[r_kpef[pb]], w=[r_kpeb[pb]])
                S.op("pool", lambda e, pb=pb: e.tensor_copy(out=kpe_b[pb][:, 64:128], in_=kpe_f[pb]), r=[r_kpef[pb]], w=[r_kpeb[pb]])
                chk('l1a')
                btr, rtr = ring.get(2)
                trv = bankbf(btr, 2)
                for k in range(4):
                    S.op("pe", lambda e, k=k, trv=trv, pb=pb: e.transpose(out=trv[:, k * 128:(k + 1) * 128], in_=cqn[pb][:, k * 128:(k + 1) * 128], identity=ident_b[:]), r=[r_cqn[pb], r_idb], w=rtr)
                for k in range(2):
                    S.op("pe", lambda e, k=k, trv=trv, t=t: e.transpose(out=trv[:, 1024 + k * 128:1024 + (k + 1) * 128], in_=ckvK[:, t, k * 128:(k + 1) * 128], identity=ident_b[:]), r=[r_ckvK[t], r_idb], w=rtr)
                S.op("pe", lambda e, trv=trv, pb=pb: e.transpose(out=trv[:, 1280:1408], in_=kpe_b[pb], identity=ident_b[:]), r=[r_kpeb[pb], r_idb], w=rtr)
                S.op("act", lambda e, trv=trv, pb=pb: e.copy(out=cqnT[pb], in_=trv[:, 0:512].rearrange("p (k t) -> p k t", k=4)), r=rtr, w=[r_cqnT[pb]])
                S.op("dve", lambda e, trv=trv, tok=tok, t=t: e.tensor_copy(out=ckvT[:, :, tok], in_=trv[:, 1024:1280].rearrange("p (k t) -> p k t", k=2)), r=rtr, w=[r_ckvT[t]])
                S.op("dve", lambda e, trv=trv, tok=tok, t=t: e.tensor_copy(out=kpeT[:, tok], in_=trv[:, 1280:1408]), r=rtr, w=[r_kpeT[t]])
                chk('l1b')
                for hh in range(2):
                    bqn, rqn = ring.get(1)
                    qv = bank(bqn).rearrange("p (h t) -> p h t", h=4)
                    for h4 in range(4):
                        h = hh * 4 + h4
                        for k in range(4):
                            S.op("pe", lambda e, qv=qv, h4=h4, h=h, k=k, pb=pb: e.matmul(qv[:, h4, :], lhsT=wuq_sb[:, k, h * 128:(h + 1) * 128], rhs=cqnT[pb][:, k, :], start=(k == 0), stop=(k == 3)), r=[r_wuq, r_cqnT[pb]], w=rqn)
                    evac("act" if hh == 0 else "dve", qnT[pb][:, hh * 4:hh * 4 + 4, :], qv, rqn, [r_qnT[pb]])
                bqp, rqp = ring.get(1)
                for k in range(4):
                    S.op("pe", lambda e, k=k, bqp=bqp, pb=pb: e.matmul(bank(bqp), lhsT=cqnT[pb][:, k, :], rhs=wuq_sb[:, k, 1024:1536], start=(k == 0), stop=(k == 3)), r=[r_wuq, r_cqnT[pb]], w=rqp)
                qpv = bank(bqp).rearrange("p (h d) -> p h d", h=8)
                qA = qpe_t[:, 0:512].rearrange("p (h d) -> p h d", h=8)
                qB = qpe_t[:, 512:1024].rearrange("p (h d) -> p h d", h=8)
                ccb = rt1[pb][:, 0:64].unsqueeze(1)
                ssb1 = rt1[pb][:, 64:96].unsqueeze(1)
                ssb2 = rt1[pb][:, 96:128].unsqueeze(1)
                S.op("dve", lambda e, qpv=qpv, ccb=ccb: e.tensor_tensor(out=qA, in0=qpv, in1=ccb.to_broadcast([128, 8, 64]), op=ALU.mult), r=rqp + [r_rt1[pb]], w=[r_qpet])
                S.op("dve", lambda e, qpv=qpv, ssb1=ssb1: e.tensor_tensor(out=qB[:, :, 0:32], in0=qpv[:, :, 32:64], in1=ssb1.to_broadcast([128, 8, 32]), op=ALU.mult), r=rqp + [r_rt1[pb]], w=[r_qpet])
                S.op("dve", lambda e, qpv=qpv, ssb2=ssb2: e.tensor_tensor(out=qB[:, :, 32:64], in0=qpv[:, :, 0:32], in1=ssb2.to_broadcast([128, 8, 32]), op=ALU.mult), r=rqp + [r_rt1[pb]], w=[r_qpet])
                S.op("pool", lambda e, pb=pb: e.tensor_tensor(out=qpe_b[pb], in0=qpe_t[:, 0:512], in1=qpe_t[:, 512:1024], op=ALU.add), r=[r_qpet], w=[r_qpeb[pb]])
                btq, rtq = ring.get(1)
                tqv = bankbf(btq)[:, 0:512].rearrange("p (a t) -> p a t", a=4)
                for a in range(4):
                    S.op("pe", lambda e, a=a, tqv=tqv, pb=pb: e.transpose(out=tqv[:, a, :], in_=qpe_b[pb][:, a * 128:(a + 1) * 128], identity=ident_b[:]), r=[r_qpeb[pb], r_idb], w=rtq)
                qz = qpeT[pb].rearrange("p (a two) t -> p a two t", two=2)
                S.op("act", lambda e, tqv=tqv, qz=qz: e.copy(out=qz[0:64, :, 0, :], in_=tqv[0:64]), r=rtq, w=[r_qpeT[pb]])
                S.op("dve", lambda e, tqv=tqv, qz=qz: e.tensor_copy(out=qz[64:128, :, 1, :], in_=tqv[64:128]), r=rtq, w=[r_qpeT[pb]])
                chk('l1c')
                for hh in range(4):
                    bql, rql = ring.get(1)
                    qlv = bank(bql).rearrange("p (h k t) -> p h k t", h=2, k=2)
                    for h2 in range(2):
                        h = hh * 2 + h2
                        for k in range(2):
                            S.op("pe", lambda e, qlv=qlv, h2=h2, h=h, k=k, pb=pb: e.matmul(qlv[:, h2, k, :], lhsT=wukT_sb[:, h, k * 128:(k + 1) * 128], rhs=qnT[pb][:, h, :], start=True, stop=True), r=[r_wukT, r_qnT[pb]], w=rql)
                    evac("act" if hh % 2 == 0 else "dve", qlT[pb][:, hh * 2:hh * 2 + 2, :, :], qlv, rql, [r_qlT[pb]])
                    chk('l1d')
                for h in range(8):
                    hp = hcnt % 2
                    hcnt += 1
                    nb = (nk + 511) // 512
                    nbanks = 1 if nb == 1 else (2 if nb == 2 else 4)
                    bs, rs = ring.get(nbanks)
                    sfull = bank(bs, nbanks)
                    for gi in range(nb):
                        k0 = gi * 512
                        kn = min(512, nk - k0)
                        ks = slice(k0, k0 + kn)
                        kres = r_ckvT[gi * 4:gi * 4 + (kn + 127) // 128]
                        kpres = r_kpeT[gi * 4:gi * 4 + (kn + 127) // 128]
                        last_diag = (gi == nb - 1)
                        S.op("pe", lambda e, sfull=sfull, ks=ks, h=h, pb=pb: e.matmul(sfull[:, ks], lhsT=qlT[pb][:, h, 0, :], rhs=ckvT[:, 0, ks], start=True, stop=False), r=[r_qlT[pb]] + kres, w=[rs[gi]])
                        S.op("pe", lambda e, sfull=sfull, ks=ks, h=h, pb=pb: e.matmul(sfull[:, ks], lhsT=qlT[pb][:, h, 1, :], rhs=ckvT[:, 1, ks], start=False, stop=False), r=[r_qlT[pb]] + kres, w=[rs[gi]])
                        S.op("pe", lambda e, sfull=sfull, ks=ks, h=h, pb=pb, last_diag=last_diag: e.matmul(sfull[:, ks], lhsT=qpeT[pb][:, h, :], rhs=kpeT[:, ks], start=False, stop=(not last_diag)), r=[r_qpeT[pb]] + kpres, w=[rs[gi]])
                        if last_diag:
                            S.op("pe", lambda e, sfull=sfull, nk=nk: e.matmul(sfull[:, nk - 128:nk], lhsT=ident_b[:], rhs=maskd[:], start=False, stop=True), r=[r_idb, r_maskd], w=[rs[gi]])
                    stt = st1[hp]
                    chs = [(c0_, min(nk, c0_ + 1024)) for c0_ in range(0, nk, 1024)]
                    for ci, (a_, b_) in enumerate(chs):
                        S.op("dve", lambda e, sfull=sfull, a_=a_, b_=b_, ci=ci, stt=stt: e.reduce_max(out=stt[:, 4 + ci:5 + ci], in_=sfull[:, a_:b_], axis=AX.X), r=rs[0:nb], w=[r_st1[hp]])
                    if len(chs) == 2:
                        S.op("dve", lambda e, stt=stt: e.tensor_tensor(out=stt[:, 4:5], in0=stt[:, 4:5], in1=stt[:, 5:6], op=ALU.max), r=[r_st1[hp]], w=[r_st1[hp]])
                    S.op("dve", lambda e, stt=stt: e.tensor_scalar(out=stt[:, 1:2], in0=stt[:, 4:5], scalar1=-MLA_SCALE, scalar2=None, op0=ALU.mult), r=[r_st1[hp]], w=[r_st1[hp]])
                    for ci, (a_, b_) in enumerate(chs):
                        S.op("act", lambda e, sfull=sfull, a_=a_, b_=b_, ci=ci, stt=stt, hp=hp: e.activation(out=p_sb[hp][:, a_:b_], in_=sfull[:, a_:b_], func=AF.Exp, bias=stt[:, 1:2], scale=MLA_SCALE, accum_out=stt[:, 6 + ci:7 + ci]), r=rs[0:nb] + [r_st1[hp]], w=[r_p[hp], r_st1[hp]])
                    if len(chs) == 2:
                        S.op("dve", lambda e, stt=stt: e.tensor_tensor(out=stt[:, 6:7], in0=stt[:, 6:7], in1=stt[:, 7:8], op=ALU.add), r=[r_st1[hp]], w=[r_st1[hp]])
                    S.op("dve", lambda e, stt=stt: e.reciprocal(out=stt[:, 3:4], in_=stt[:, 6:7]), r=[r_st1[hp]], w=[r_st1[hp]])
                    chk('l1e')
                    nkt = t + 1
                    for g0 in range(0, nkt, 4):
                        bpt, rpt = ring.get(1)
                        ptv = bankbf(bpt)
                        n_ = min(4, nkt - g0)
                        for i_ in range(n_):
                            S.op("pe", lambda e, i_=i_, g0=g0, ptv=ptv, hp=hp: e.transpose(out=ptv[:, i_ * 128:(i_ + 1) * 128], in_=p_sb[hp][:, (g0 + i_) * 128:(g0 + i_ + 1) * 128], identity=ident_b[:]), r=[r_p[hp], r_idb], w=rpt)
                        evac("dve" if (g0 // 4 + h) % 2 == 0 else "act", pT_sb[hp][:, g0 * 128:(g0 + n_) * 128], ptv[:, 0:n_ * 128], rpt, [r_pT[hp]])
                    bol, rol = ring.get(1)
                    for kt in range(nkt):
                        S.op("pe", lambda e, kt=kt, bol=bol, hp=hp: e.matmul(bank(bol)[:, 0:256], lhsT=pT_sb[hp][:, kt * 128:(kt + 1) * 128], rhs=ckvK[:, kt, :], start=(kt == 0), stop=(kt == nkt - 1)), r=[r_pT[hp], r_ckvK[kt]], w=rol)
                    S.op("act", lambda e, bol=bol, stt=stt, hp=hp: e.activation(out=ol_b[hp], in_=bank(bol)[:, 0:256], func=AF.Identity, bias=0.0, scale=stt[:, 3:4]), r=rol + [r_st1[hp]], w=[r_olb[hp]])
                    bot, rot = ring.get(1)
                    otv = bankbf(bot)[:, 0:256].rearrange("p (k t) -> p k t", k=2)
                    for k in range(2):
                        S.op("pe", lambda e, k=k, otv=otv, hp=hp: e.transpose(out=otv[:, k, :], in_=ol_b[hp][:, k * 128:(k + 1) * 128], identity=ident_b[:]), r=[r_olb[hp], r_idb], w=rot)
                    S.op("dve", lambda e, otv=otv, hp=hp: e.tensor_copy(out=olT[hp], in_=otv), r=rot, w=[r_olT[hp]])
                    boh, roh = ring.get(1)
                    for k in range(2):
                        S.op("pe", lambda e, k=k, boh=boh, h=h, hp=hp: e.matmul(bank(boh)[:, 0:128], lhsT=wuv_sb[:, k, h * 128:(h + 1) * 128], rhs=olT[hp][:, k, :], start=(k == 0), stop=(k == 1)), r=[r_wuv, r_olT[hp]], w=roh)
                    evac("act" if h % 2 == 0 else "dve", oT1[pb][:, h, :], bank(boh)[:, 0:128], roh, [r_oT1[pb]])
                    chk('l1f')
                by, ry = ring.get(2)
                for hf in range(2):
                    for c in range(8):
                        S.op("pe", lambda e, hf=hf, c=c, by=by, pb=pb: e.matmul(bank(by + hf), lhsT=oT1[pb][:, c, :], rhs=wo1_sb[:, c, hf * 512:(hf + 1) * 512], start=(c == 0), stop=(c == 7)), r=[r_oT1[pb], r_wo1], w=[ry[hf]])
                S.op("dve", lambda e, by=by, pb=pb, t=t: e.scalar_tensor_tensor(out=z1[pb], in0=X[:, t, :], scalar=ALPHA, in1=bank(by, 2), op0=ALU.mult, op1=ALU.add), r=[rX[t]] + ry, w=[r_z1[pb]])
                layer_norm_tile(z1[pb], r_z1[pb], X[:, t, :], rX[t], 128, sc1[pb], r_sc1[pb], lnt1, r_lnt1)
                chk(f'l1t{t}')


            def sample_l1():
                A.at(L1_BUF)
                NSTEP = NPAGES * 128 // 1024
                tb1, r_tb1 = A.get(128, F32, "tb1")
                cc1s = tb1[0:NS, 0:64]; ss1s = tb1[0:NS, 64:128]
                S.dma("sp", lambda e: e.dma_start(out=cc1s, in_=tdr["cc1s"]), w=[r_tb1])
                S.dma("sp", lambda e: e.dma_start(out=ss1s, in_=tdr["ss1s"]), w=[r_tb1])
                msk, r_msk = A.get(SB * NS, BF16, "msk")
                S.dma("pool", lambda e: e.dma_start(out=msk[0:32, :], in_=tdr["masksb"]), w=[r_msk])
                mskv = msk.rearrange("p (b t) -> p b t", b=SB)
                pt_sb, r_pt = A.get(SB, I32, "pt_sb")
                S.dma("sp", lambda e: e.dma_start(out=pt_sb, in_=ptab), w=[r_pt])
                xsT1, r_xsT1 = A.get(128, BF16, "xsT1"); xsT1 = xsT1.rearrange("p (k t) -> p k t", k=8)
                transpose_x(XS[:], r_XS, xsT1, r_xsT1, npart=NS)
                bcq, rcq = ring.get(1)
                bkv, rkv = ring.get(1)
                for k in range(8):
                    S.op("pe", lambda e, k=k: e.matmul(bank(bcq)[0:NS, :], lhsT=xsT1[:, k, :], rhs=wdq_sb[:, k, :], start=(k == 0), stop=(k == 7)), r=[r_xsT1, r_wdq], w=rcq)
                for k in range(8):
                    S.op("pe", lambda e, k=k: e.matmul(bank(bkv)[0:NS, 0:320], lhsT=xsT1[:, k, :], rhs=wdkv_sb[:, k, :], start=(k == 0), stop=(k == 7)), r=[r_xsT1, r_wdkv], w=rkv)
                jk, r_jk = A.get(512, F32, "jk")
                scq, r_scq = A.get(16, F32, "scq")
                cqns, r_cqns = A.get(512, BF16, "cqns")
                ckvsf, r_ckvsf = A.get(256, F32, "ckvsf")
                ckvsb, r_ckvsb = A.get(256, BF16, "ckvsb")
                kpet, r_kpets = A.get(128, F32, "kpets")
                kpesf, r_kpesf = A.get(64, F32, "kpesf")
                kpesb, r_kpesb = A.get(128, BF16, "kpesb")
                sc = scq[0:NS]
                S.op("act", lambda e: e.activation(out=jk[0:NS, :], in_=bank(bcq)[0:NS, :], func=AF.Square, accum_out=sc[:, 0:1]), r=rcq, w=[r_jk, r_scq])
                S.op("act", lambda e: e.activation(out=sc[:, 1:2], in_=sc[:, 0:1], func=AF.Sqrt, bias=eps_rms[0:NS], scale=1.0 / 512), r=[r_scq, r_eps], w=[r_scq])
                S.op("dve", lambda e: e.reciprocal(out=sc[:, 1:2], in_=sc[:, 1:2]), r=[r_scq], w=[r_scq])
                S.op("dve", lambda e: e.scalar_tensor_tensor(out=cqns[0:NS, :], in0=bank(bcq)[0:NS, :], scalar=sc[:, 1:2], in1=qng[0:NS], op0=ALU.mult, op1=ALU.mult), r=rcq + [r_scq, r_c1], w=[r_cqns])
                S.op("act", lambda e: e.activation(out=jk[0:NS, 0:256], in_=bank(bkv)[0:NS, 0:256], func=AF.Square, accum_out=sc[:, 2:3]), r=rkv, w=[r_jk, r_scq])
                S.op("act", lambda e: e.activation(out=sc[:, 3:4], in_=sc[:, 2:3], func=AF.Sqrt, bias=eps_rms[0:NS], scale=1.0 / 256), r=[r_scq, r_eps], w=[r_scq])
                S.op("dve", lambda e: e.reciprocal(out=sc[:, 3:4], in_=sc[:, 3:4]), r=[r_scq], w=[r_scq])
                S.op("dve", lambda e: e.scalar_tensor_tensor(out=ckvsf[0:NS, :], in0=bank(bkv)[0:NS, 0:256], scalar=sc[:, 3:4], in1=kvng[0:NS], op0=ALU.mult, op1=ALU.mult), r=rkv + [r_scq, r_c1], w=[r_ckvsf])
                out_toks.append(S.dma("sp", lambda e: e.dma_start(out=ckvs, in_=ckvsf[0:NS, :]), r=[r_ckvsf]))
                S.op("pool", lambda e: e.tensor_copy(out=ckvsb[0:NS, :], in_=ckvsf[0:NS, :]), r=[r_ckvsf], w=[r_ckvsb])
                S.op("dve", lambda e: e.tensor_tensor(out=kpet[0:NS, 0:64], in0=bank(bkv)[0:NS, 256:320], in1=cc1s, op=ALU.mult), r=rkv + [r_tb1], w=[r_kpets])
                S.op("dve", lambda e: e.tensor_tensor(out=kpet[0:NS, 64:96], in0=bank(bkv)[0:NS, 288:320], in1=ss1s[:, 0:32], op=ALU.mult), r=rkv + [r_tb1], w=[r_kpets])
                S.op("dve", lambda e: e.tensor_tensor(out=kpet[0:NS, 96:128], in0=bank(bkv)[0:NS, 256:288], in1=ss1s[:, 32:64], op=ALU.mult), r=rkv + [r_tb1], w=[r_kpets])
                S.op("pool", lambda e: e.tensor_tensor(out=kpesf[0:NS, :], in0=kpet[0:NS, 0:64], in1=kpet[0:NS, 64:128], op=ALU.add), r=[r_kpets], w=[r_kpesf])
                out_toks.append(S.dma("sp", lambda e: e.dma_start(out=kpes, in_=kpesf[0:NS, :]), r=[r_kpesf]))
                S.op("pool", lambda e: e.tensor_copy(out=kpesb[0:NS, 0:64], in_=kpesf[0:NS, :]), r=[r_kpesf], w=[r_kpesb])
                S.op("pool", lambda e: e.tensor_copy(out=kpesb[0:NS, 64:128], in_=kpesf[0:NS, :]), r=[r_kpesf], w=[r_kpesb])
                btr, rtr = ring.get(1)
                trv = bankbf(btr)
                for k in range(4):
                    S.op("pe", lambda e, k=k: e.transpose(out=trv[:, k * 16:(k + 1) * 16], in_=cqns[0:NS, k * 128:(k + 1) * 128], identity=ident_b[0:NS, 0:NS]), r=[r_cqns, r_idb], w=rtr)
                for k in range(2):
                    S.op("pe", lambda e, k=k: e.transpose(out=trv[:, 64 + k * 16:64 + (k + 1) * 16], in_=ckvsb[0:NS, k * 128:(k + 1) * 128], identity=ident_b[0:NS, 0:NS]), r=[r_ckvsb, r_idb], w=rtr)
                S.op("pe", lambda e: e.transpose(out=trv[:, 96:112], in_=kpesb[0:NS, :], identity=ident_b[0:NS, 0:NS]), r=[r_kpesb, r_idb], w=rtr)
                trs, r_trs = A.get(112, BF16, "trs")
                S.op("act", lambda e: e.copy(out=trs, in_=trv[:, 0:112]), r=rtr, w=[r_trs])
                cqnTs = trs[:, 0:64].rearrange("p (k t) -> p k t", k=4)
                ckvTs = trs[:, 64:96].rearrange("p (k t) -> p k t", k=2)
                kpeTs = trs[:, 96:112]
                bqn, rqn = ring.get(1)
                qnv = bank(bqn)[:, 0:128].rearrange("p (h t) -> p h t", h=8)
                for h in range(8):
                    for k in range(4):
                        S.op("pe", lambda e, h=h, k=k: e.matmul(qnv[:, h, :], lhsT=wuq_sb[:, k, h * 128:(h + 1) * 128], rhs=cqnTs[:, k, :], start=(k == 0), stop=(k == 3)), r=[r_wuq, r_trs], w=rqn)
                qnTs, r_qnTs = A.get(128, BF16, "qnTs"); qnTs = qnTs.rearrange("p (h t) -> p h t", h=8)
                S.op("act", lambda e: e.copy(out=qnTs, in_=qnv), r=rqn, w=[r_qnTs])
                bqp, rqp = ring.get(1)
                for k in range(4):
                    S.op("pe", lambda e, k=k: e.matmul(bank(bqp)[0:NS, :], lhsT=cqnTs[:, k, :], rhs=wuq_sb[:, k, 1024:1536], start=(k == 0), stop=(k == 3)), r=[r_wuq, r_trs], w=rqp)
                qpv = bank(bqp)[0:NS, :].rearrange("p (h d) -> p h d", h=8)
                qpt, r_qpt = A.get(1024, F32, "qpt")
                qA = qpt[0:NS, 0:512].rearrange("p (h d) -> p h d", h=8)
                qB = qpt[0:NS, 512:1024].rearrange("p (h d) -> p h d", h=8)
                S.op("dve", lambda e: e.tensor_tensor(out=qA, in0=qpv, in1=cc1s.unsqueeze(1).to_broadcast([NS, 8, 64]), op=ALU.mult), r=rqp + [r_tb1], w=[r_qpt])
                S.op("dve", lambda e: e.tensor_tensor(out=qB[:, :, 0:32], in0=qpv[:, :, 32:64], in1=ss1s[:, 0:32].unsqueeze(1).to_broadcast([NS, 8, 32]), op=ALU.mult), r=rqp + [r_tb1], w=[r_qpt])
                S.op("dve", lambda e: e.tensor_tensor(out=qB[:, :, 32:64], in0=qpv[:, :, 0:32], in1=ss1s[:, 32:64].unsqueeze(1).to_broadcast([NS, 8, 32]), op=ALU.mult), r=rqp + [r_tb1], w=[r_qpt])
                qpsb, r_qpsb = A.get(512, BF16, "qpsb")
                S.op("pool", lambda e: e.tensor_tensor(out=qpsb[0:NS, :], in0=qpt[0:NS, 0:512], in1=qpt[0:NS, 512:1024], op=ALU.add), r=[r_qpt], w=[r_qpsb])
                btq, rtq = ring.get(1)
                tqv = bankbf(btq)[:, 0:64].rearrange("p (a t) -> p a t", a=4)
                for a in range(4):
                    S.op("pe", lambda e, a=a: e.transpose(out=tqv[:, a, :], in_=qpsb[0:NS, a * 128:(a + 1) * 128], identity=ident_b[0:NS, 0:NS]), r=[r_qpsb, r_idb], w=rtq)
                qpTs, r_qpTs = A.get(SB * 32, BF16, "qpTs")
                S.op("pool", lambda e: e.memset(qpTs, 0.0), w=[r_qpTs])
                qpT5 = qpTs.rearrange("p (b a two l) -> p b a two l", b=4, a=4, two=2)
                tq4 = tqv.rearrange("p a (b l) -> p a b l", b=4)
                for bb in range(SB):
                    S.op("act", lambda e, bb=bb: e.copy(out=qpT5[0:64, bb, :, 0, :], in_=tq4[0:64, :, bb, :]), r=rtq, w=[r_qpTs])
                    S.op("dve", lambda e, bb=bb: e.tensor_copy(out=qpT5[64:128, bb, :, 1, :], in_=tq4[64:128, :, bb, :]), r=rtq, w=[r_qpTs])
                qpTb = qpTs.rearrange("p (b n) -> p b n", b=4)
                bql, rql = ring.get(1)
                qlv = bank(bql)[:, 0:256].rearrange("p (h k t) -> p h k t", h=8, k=2)
                for h in range(8):
                    for k in range(2):
                        S.op("pe", lambda e, h=h, k=k: e.matmul(qlv[:, h, k, :], lhsT=wukT_sb[:, h, k * 128:(k + 1) * 128], rhs=qnTs[:, h, :], start=True, stop=True), r=[r_wukT, r_qnTs], w=rql)
                qlTs, r_qlTs = A.get(2 * SB * 32, BF16, "qlTs")
                ql5 = qlTs.rearrange("p (k b h l) -> p k b h l", k=2, b=4, h=8)
                qlv5 = qlv.rearrange("p h k (b l) -> p h k b l", b=4)
                for k in range(2):
                    for bb in range(SB):
                        S.op("act" if (k + bb) % 2 == 0 else "dve",
                             (lambda e, k=k, bb=bb: e.copy(out=ql5[:, k, bb], in_=qlv5[:, :, k, bb, :])) if (k + bb) % 2 == 0 else
                             (lambda e, k=k, bb=bb: e.tensor_copy(out=ql5[:, k, bb], in_=qlv5[:, :, k, bb, :])), r=rql, w=[r_qlTs])
                qlTb = qlTs.rearrange("p (k b n) -> p k b n", k=2, b=4)
                gk, r_gk = dbl(2048, BF16, "gk", "p (s c) -> p s c", s=8)
                gp, r_gp = dbl(512, BF16, "gp", "p (s c) -> p s c", s=8)
                gp2, r_gp2 = dbl(1024, BF16, "gp2", "p (s c) -> p s c", s=8)
                ckTc, r_ckTc = dbl(2048, BF16, "ckTc", "p (k n) -> p k n", k=2)
                kpTc, r_kpTc = dbl(1024, BF16, "kpTc")
                p_s, r_ps = dbl(1024, BF16, "p_s")
                pTc, r_pTc = dbl(256, BF16, "pTc", "p (i t) -> p i t", i=8)
                Oacc, r_Oacc = dbl(256, F32, "Oacc")
                ms, r_ms = dbl(8, F32, "ms")
                olb, r_olb = A.get(256, BF16, "olb")
                olTs, r_olTs = A.get(64, BF16, "olTs"); olTs = olTs.rearrange("p (k n) -> p k n", k=2)
                oTs, r_oTs = A.get(128, BF16, "oTs"); oTs = oTs.rearrange("p (h t) -> p h t", h=8)
                print("sample l1 arena bytes", A.cur)
                gcount = 0
                for bb in range(SB):
                    mb_ = ms[bb % 2][0:32]
                    oa = Oacc[bb % 2][0:32]
                    r_m = r_ms[bb % 2]
                    r_o = r_Oacc[bb % 2]
                    bsn, rsn = ring.get(1)
                    sn = bank(bsn)[0:32, 0:NS]
                    S.op("pe", lambda e, bb=bb, sn=sn: e.matmul(sn, lhsT=qlTb[:, 0, bb, :], rhs=ckvTs[:, 0, :], start=True, stop=False), r=[r_qlTs, r_trs], w=rsn)
                    S.op("pe", lambda e, bb=bb, sn=sn: e.matmul(sn, lhsT=qlTb[:, 1, bb, :], rhs=ckvTs[:, 1, :], start=False, stop=False), r=[r_qlTs, r_trs], w=rsn)
                    S.op("pe", lambda e, bb=bb, sn=sn: e.matmul(sn, lhsT=qpTb[:, bb, :], rhs=kpeTs, start=False, stop=False), r=[r_qpTs, r_trs], w=rsn)
                    S.op("pe", lambda e, bb=bb, sn=sn: e.matmul(sn, lhsT=ident_b[0:32, 0:32], rhs=mskv[0:32, bb, :], start=False, stop=True), r=[r_idb, r_msk], w=rsn)
                    S.op("dve", lambda e, sn=sn, mb_=mb_: e.reduce_max(out=mb_[:, 0:1], in_=sn, axis=AX.X), r=rsn, w=[r_m])
                    S.op("dve", lambda e, mb_=mb_: e.tensor_scalar(out=mb_[:, 5:6], in0=mb_[:, 0:1], scalar1=-MLA_SCALE, scalar2=None, op0=ALU.mult), r=[r_m], w=[r_m])
                    psl = p_s[0][0:32]
                    S.op("act", lambda e, sn=sn, mb_=mb_, psl=psl: e.activation(out=psl[:, 0:NS], in_=sn, func=AF.Exp, bias=mb_[:, 5:6], scale=MLA_SCALE, accum_out=mb_[:, 1:2]), r=rsn + [r_m], w=[r_ps[0], r_m])
                    bpt, rpt = ring.get(1)
                    S.op("pe", lambda e, psl=psl, bpt=bpt: e.transpose(out=bankbf(bpt)[0:NS, 0:32], in_=psl[:, 0:NS], identity=ident_b[0:32, 0:32]), r=[r_ps[0], r_idb], w=rpt)
                    S.op("act", lambda e, bpt=bpt: e.copy(out=pTc[0][0:NS, 0, :], in_=bankbf(bpt)[0:NS, 0:32]), r=rpt, w=[r_pTc[0]])
                    bo_, ro_ = ring.get(1)
                    S.op("pe", lambda e, bo_=bo_: e.matmul(bank(bo_)[0:32, 0:256], lhsT=pTc[0][0:NS, 0, :], rhs=ckvsb[0:NS, :], start=True, stop=True), r=[r_pTc[0], r_ckvsb], w=ro_)
                    S.op("dve", lambda e, bo_=bo_, oa=oa: e.tensor_copy(out=oa, in_=bank(bo_)[0:32, 0:256]), r=ro_, w=[r_o])
                    for stp in range(NSTEP):
                        gs = gcount % 2
                        gcount += 1
                        S.dma("pool", lambda e, gs=gs, bb=bb, stp=stp: e.indirect_dma_start(out=gk[gs].rearrange("p s c -> p (s c)"), out_offset=None, in_=cckv[:, :], element_offset=stp * 2048,
                                                                                             in_offset=bass.IndirectOffsetOnAxis(ap=pt_sb[:, bb:bb + 1], axis=0)), r=[r_pt], w=[r_gk[gs]])
                        S.dma("pool", lambda e, gs=gs, bb=bb, stp=stp: e.indirect_dma_start(out=gp[gs].rearrange("p s c -> p (s c)"), out_offset=None, in_=ckpe[:, :], element_offset=stp * 512,
                                                                                             in_offset=bass.IndirectOffsetOnAxis(ap=pt_sb[:, bb:bb + 1], axis=0)), r=[r_pt], w=[r_gp[gs]])
                        S.op("pool", lambda e, gs=gs: e.tensor_copy(out=gp2[gs][:, :, 0:64], in_=gp[gs]), r=[r_gp[gs]], w=[r_gp2[gs]])
                        S.op("pool", lambda e, gs=gs: e.tensor_copy(out=gp2[gs][:, :, 64:128], in_=gp[gs]), r=[r_gp[gs]], w=[r_gp2[gs]])
                        for hf in range(2):
                            sl = hf * 4
                            bA, rA = ring.get(1)
                            bB, rB = ring.get(1)
                            bC, rC = ring.get(1)
                            for i_ in range(4):
                                S.op("pe", lambda e, i_=i_, sl=sl, gs=gs, bA=bA: e.transpose(out=bankbf(bA)[:, i_ * 128:(i_ + 1) * 128], in_=gk[gs][:, sl + i_, 0:128], identity=ident_b[:]), r=[r_gk[gs], r_idb], w=rA)
                            for i_ in range(4):
                                S.op("pe", lambda e, i_=i_, sl=sl, gs=gs, bB=bB: e.transpose(out=bankbf(bB)[:, i_ * 128:(i_ + 1) * 128], in_=gk[gs][:, sl + i_, 128:256], identity=ident_b[:]), r=[r_gk[gs], r_idb], w=rB)
                            for i_ in range(4):
                                S.op("pe", lambda e, i_=i_, sl=sl, gs=gs, bC=bC: e.transpose(out=bankbf(bC)[:, i_ * 128:(i_ + 1) * 128], in_=gp2[gs][:, sl + i_, :], identity=ident_b[:]), r=[r_gp2[gs], r_idb], w=rC)
                            cs_ = slice(hf * 512, (hf + 1) * 512)
                            S.op("act", lambda e, gs=gs, bA=bA, cs_=cs_: e.copy(out=ckTc[gs][:, 0, cs_], in_=bankbf(bA)[:, 0:512]), r=rA, w=[r_ckTc[gs]])
                            S.op("dve", lambda e, gs=gs, bB=bB, cs_=cs_: e.tensor_copy(out=ckTc[gs][:, 1, cs_], in_=bankbf(bB)[:, 0:512]), r=rB, w=[r_ckTc[gs]])
                            if hf == 0:
                                S.op("act", lambda e, gs=gs, bC=bC, cs_=cs_: e.copy(out=kpTc[gs][:, cs_], in_=bankbf(bC)[:, 0:512]), r=rC, w=[r_kpTc[gs]])
                            else:
                                S.op("dve", lambda e, gs=gs, bC=bC, cs_=cs_: e.tensor_copy(out=kpTc[gs][:, cs_], in_=bankbf(bC)[:, 0:512]), r=rC, w=[r_kpTc[gs]])
                        bs2, rs2 = ring.get(2)
                        sfull = bank(bs2, 2)[0:32, :]
                        for hf in range(2):
                            cs_ = slice(hf * 512, (hf + 1) * 512)
                            S.op("pe", lambda e, bb=bb, gs=gs, cs_=cs_, sfull=sfull: e.matmul(sfull[:, cs_], lhsT=qlTb[:, 0, bb, :], rhs=ckTc[gs][:, 0, cs_], start=True, stop=False), r=[r_qlTs, r_ckTc[gs]], w=[rs2[hf]])
                            S.op("pe", lambda e, bb=bb, gs=gs, cs_=cs_, sfull=sfull: e.matmul(sfull[:, cs_], lhsT=qlTb[:, 1, bb, :], rhs=ckTc[gs][:, 1, cs_], start=False, stop=False), r=[r_qlTs, r_ckTc[gs]], w=[rs2[hf]])
                            S.op("pe", lambda e, bb=bb, gs=gs, cs_=cs_, sfull=sfull: e.matmul(sfull[:, cs_], lhsT=qpTb[:, bb, :], rhs=kpTc[gs][:, cs_], start=False, stop=True), r=[r_qpTs, r_kpTc[gs]], w=[rs2[hf]])
                        S.op("dve", lambda e, sfull=sfull, mb_=mb_: e.reduce_max(out=mb_[:, 2:3], in_=sfull, axis=AX.X), r=rs2, w=[r_m])
                        S.op("dve", lambda e, mb_=mb_: e.tensor_tensor(out=mb_[:, 3:4], in0=mb_[:, 0:1], in1=mb_[:, 2:3], op=ALU.max), r=[r_m], w=[r_m])
                        S.op("dve", lambda e, mb_=mb_: e.tensor_tensor(out=mb_[:, 4:5], in0=mb_[:, 0:1], in1=mb_[:, 3:4], op=ALU.subtract), r=[r_m], w=[r_m])
                        S.op("act", lambda e, mb_=mb_: e.activation(out=mb_[:, 4:5], in_=mb_[:, 4:5], func=AF.Exp, scale=MLA_SCALE), r=[r_m], w=[r_m])
                        S.op("dve", lambda e, mb_=mb_: e.tensor_scalar(out=mb_[:, 5:6], in0=mb_[:, 3:4], scalar1=-MLA_SCALE, scalar2=None, op0=ALU.mult), r=[r_m], w=[r_m])
                        pq = gs
                        psl = p_s[pq][0:32]
                        S.op("act", lambda e, sfull=sfull, mb_=mb_, psl=psl: e.activation(out=psl, in_=sfull, func=AF.Exp, bias=mb_[:, 5:6], scale=MLA_SCALE, accum_out=mb_[:, 6:7]), r=rs2 + [r_m], w=[r_ps[pq], r_m])
                        S.op("dve", lambda e, mb_=mb_: e.scalar_tensor_tensor(out=mb_[:, 1:2], in0=mb_[:, 1:2], scalar=mb_[:, 4:5], in1=mb_[:, 6:7], op0=ALU.mult, op1=ALU.add), r=[r_m], w=[r_m])
                        S.op("pool", lambda e, mb_=mb_: e.tensor_copy(out=mb_[:, 0:1], in_=mb_[:, 3:4]), r=[r_m], w=[r_m])
                        bpt, rpt = ring.get(1)
                        ptv = bankbf(bpt)[:, 0:256].rearrange("p (i t) -> p i t", i=8)
                        for i_ in range(8):
                            S.op("pe", lambda e, i_=i_, psl=psl, ptv=ptv: e.transpose(out=ptv[:, i_, :], in_=psl[:, i_ * 128:(i_ + 1) * 128], identity=ident_b[0:32, 0:32]), r=[r_ps[pq], r_idb], w=rpt)
                        S.op("act", lambda e, ptv=ptv, pq=pq: e.copy(out=pTc[pq], in_=ptv), r=rpt, w=[r_pTc[pq]])
                        bo_, ro_ = ring.get(1)
                        for i_ in range(8):
                            S.op("pe", lambda e, i_=i_, bo_=bo_, pq=pq, gs=gs: e.matmul(bank(bo_)[0:32, 0:256], lhsT=pTc[pq][:, i_, :], rhs=gk[gs][:, i_, :], start=(i_ == 0), stop=(i_ == 7)), r=[r_pTc[pq], r_gk[gs]], w=ro_)
                        S.op("dve", lambda e, bo_=bo_, oa=oa, mb_=mb_: e.scalar_tensor_tensor(out=oa, in0=oa, scalar=mb_[:, 4:5], in1=bank(bo_)[0:32, 0:256], op0=ALU.mult, op1=ALU.add), r=ro_ + [r_o, r_m], w=[r_o])
                    S.op("dve", lambda e, mb_=mb_: e.reciprocal(out=mb_[:, 7:8], in_=mb_[:, 1:2]), r=[r_m], w=[r_m])
                    S.op("act", lambda e, oa=oa, mb_=mb_: e.activation(out=olb[0:32, :], in_=oa, func=AF.Identity, bias=0.0, scale=mb_[:, 7:8]), r=[r_o, r_m], w=[r_olb])
                    bot, rot = ring.get(1)
                    otv = bankbf(bot)[:, 0:64].rearrange("p (k n) -> p k n", k=2)
                    for k in range(2):
                        S.op("pe", lambda e, k=k, otv=otv: e.transpose(out=otv[:, k, :], in_=olb[0:32, k * 128:(k + 1) * 128], identity=ident_b[0:32, 0:32]), r=[r_olb, r_idb], w=rot)
                    S.op("dve", lambda e, otv=otv: e.tensor_copy(out=olTs, in_=otv), r=rot, w=[r_olTs])
                    boh, roh = ring.get(1)
                    ohv = bank(boh)[:, 0:32].rearrange("p (h l) -> p h l", h=8)
                    for h in range(8):
                        for k in range(2):
                            S.op("pe", lambda e, h=h, k=k, ohv=ohv: e.matmul(ohv[:, h, :], lhsT=wuv_sb[:, k, h * 128:(h + 1) * 128], rhs=olTs[:, k, h * 4:(h + 1) * 4], start=(k == 0), stop=(k == 1)), r=[r_wuv, r_olTs], w=roh)
                    S.op("act", lambda e, ohv=ohv, bb=bb: e.copy(out=oTs[:, :, bb * 4:bb * 4 + 4], in_=ohv), r=roh, w=[r_oTs])
                by_, ry_ = ring.get(2)
                for hf in range(2):
                    for c in range(8):
                        S.op("pe", lambda e, hf=hf, c=c: e.matmul(bank(by_ + hf)[0:NS, :], lhsT=oTs[:, c, :], rhs=wo1_sb[:, c, hf * 512:(hf + 1) * 512], start=(c == 0), stop=(c == 7)), r=[r_oTs, r_wo1], w=[ry_[hf]])
                zs1, r_zs1 = A.get(D, F32, "zs1")
                scs1, r_scs1 = A.get(16, F32, "scs1")
                S.op("dve", lambda e: e.scalar_tensor_tensor(out=zs1[0:NS, :], in0=XS[:], scalar=ALPHA, in1=bank(by_, 2)[0:NS, :], op0=ALU.mult, op1=ALU.add), r=[r_XS] + ry_, w=[r_zs1])
                layer_norm_tile(zs1[0:NS, :], r_zs1, XS[:], r_XS, NS, scs1, r_scs1, lnt1, r_lnt1)

            if with_sample:
                sample_l1()
            chk('l1')
            ffn(1, True)


        except _Stop:
            print('stopped at', stop)
        S.finish(out_toks)
        S.emit()
        print("ops", S.stats())
    return nc


_CACHE = {}


def _prep_shared(inp, tabs):
    f = lambda a: np.ascontiguousarray(np.asarray(a, dtype=np.float32))
    sh = {}
    sh["w_in"] = f(inp["w_in_even"][0])
    sh["pool_w"] = f(np.transpose(inp["pool_w"][0], (1, 0, 2)))
    sh["pool_scale"] = f(inp["pool_scale"][0].reshape(4, 128).T)
    sh["gn_g"] = f(inp["ret_gn_g"][0])
    sh["w_o0"] = f(inp["w_o_even"][0])
    sh["w_dq"] = f(inp["w_dq"][0])
    sh["qn_g"] = f(inp["q_norm_g"][0])
    wuq = np.asarray(inp["w_uq"][0]).reshape(512, 8, 192)
    sh["w_uq"] = f(np.concatenate([wuq[:, :, :128].reshape(512, 1024), wuq[:, :, 128:].reshape(512, 512)], axis=1))
    sh["w_dkv"] = f(inp["w_dkv"][0])
    sh["kvn_g"] = f(inp["kv_norm_g"][0])
    sh["w_ukT"] = f(np.transpose(inp["w_uk"][0], (2, 1, 0)))
    sh["w_uv"] = f(np.asarray(inp["w_uv"][0]).reshape(256, 1024))
    sh["w_o1"] = f(inp["w_o_mla"][0])
    wup = np.asarray(inp["w_up"])
    a = wup[:, :, :DFF].reshape(DEPTH, 8, 128, NJ, 128)
    b = wup[:, :, DFF:].reshape(DEPTH, 8, 128, NJ, 128)
    ab = np.concatenate([a, b], axis=4)
    sh["w_up"] = f(np.transpose(ab, (0, 3, 2, 1, 4)))
    sh["w_down"] = f(inp["w_down"])
    sh["conv_w"] = f(np.transpose(np.asarray(inp["conv_w"]).reshape(DEPTH, 3, NJ, 128), (0, 3, 2, 1)))
    sh["conv_b"] = f(np.transpose(np.asarray(inp["conv_b"]).reshape(DEPTH, NJ, 128), (0, 2, 1)))
    for k in ("ln_mix_g", "ln_mix_b", "ln_ffn_g", "ln_ffn_b"):
        sh[k] = f(inp[k])
    sh["cckv"] = np.asarray(inp["cache_ckv"], dtype=np.float32).reshape(5120, 128 * 256)
    sh["ckpe"] = np.asarray(inp["cache_kpe"], dtype=np.float32).reshape(5120, 128 * 64)
    for k, v in tabs.items():
        if isinstance(v, np.ndarray):
            sh["t_" + k] = v
    return sh


def kernel(**inp):
    tabs = host_tables()
    if "nc" not in _CACHE:
        _CACHE["nc"] = build_program(tabs)
    nc = _CACHE["nc"]
    sh = _prep_shared(inp, tabs)
    f = lambda a: np.ascontiguousarray(np.asarray(a, dtype=np.float32))
    in_maps = []
    for c in range(8):
        m = dict(sh)
        bs = slice(SB * c, SB * (c + 1))
        m["xp"] = f(inp["x_prompt"][c])
        m["xs"] = f(np.asarray(inp["x_sample"][bs]).reshape(NS, D))
        m["spool"] = f(np.asarray(inp["state_pool"][0, bs]).reshape(SB * 15, 512))
        m["sret"] = f(np.asarray(inp["state_ret"][0, bs]).reshape(SB * RH, 128, 128))
        m["sconv"] = f(np.asarray(inp["state_conv"][:, bs]).reshape(DEPTH, SB * 2, DFF))
        m["ptab"] = np.ascontiguousarray(np.asarray(inp["page_table"][bs]).T.astype(np.int32))
        in_maps.append(m)
    res = run_bass_kernel_spmd(nc, in_maps, core_ids=list(range(8)))
    R = res.results
    cat = lambda k: np.stack([np.asarray(R[c][k]) for c in range(8)])
    y_p = cat("yp")
    y_s = cat("ys").reshape(32, SL, D)
    pool_p = cat("poolp")[None]
    pool_s = cat("pools").reshape(1, 32, 15, 512)
    ret_p = cat("retp")[None]
    ret_s = cat("rets").reshape(1, 32, RH, 128, 128)
    ckv_p = cat("ckvp")[None]
    ckv_s = cat("ckvs").reshape(1, 32, SL, 256)
    kpe_p = cat("kpep")[None]
    kpe_s = cat("kpes").reshape(1, 32, SL, 64)
    conv_p = np.transpose(cat("convp"), (1, 0, 2, 3))
    conv_s = np.transpose(cat("convs").reshape(8, DEPTH, SB, 2, DFF), (1, 0, 2, 3, 4)).reshape(DEPTH, 32, 2, DFF)
    outs = (y_p, y_s, pool_p, pool_s, ret_p, ret_s, ckv_p, ckv_s, kpe_p, kpe_s, conv_p, conv_s)
    return tuple(np.ascontiguousarray(o.astype(np.float32)) for o in outs)
```

```python
import contextlib
import math
import numpy as np
import concourse.bass as bass
import concourse.mybir as mybir
from concourse.bass_utils import run_bass_kernel_spmd

F32 = mybir.dt.float32
BF16 = mybir.dt.bfloat16
I32 = mybir.dt.int32
AF = mybir.ActivationFunctionType
ALU = mybir.AluOpType
AX = mybir.AxisListType

ALL_Q = ("pe", "act", "dve", "pool", "sp")
SAME_ENGINE_SYNC = True

D = 1024
SEQ = 2048
NT = SEQ // 128
DEPTH = 2
NS = 16
SB = 4
SL = 4
PAST = 16384
NPAGES = 128
POOL_W = (2, 4, 8, 16)
RH = 4
DFF = 2816
NJ = DFF // 128
ALPHA = (2.0 * DEPTH) ** 0.25
LN_EPS = 1e-5
RMS_EPS = 1e-6
GN_EPS = 1e-6
MLA_SCALE = (128 + 64) ** -0.5
KSCALE = 128 ** -0.5
NEG = -30000.0


class Res:
    __slots__ = ("name", "w", "r")

    def __init__(self, name):
        self.name = name
        self.w = None
        self.r = []


class Op:
    __slots__ = ("q", "fn", "waits", "signal", "dma", "idx")

    def __init__(self, q, fn, dma=None):
        self.q = q
        self.fn = fn
        self.waits = []
        self.signal = False
        self.dma = dma
        self.idx = None


class Sched:
    def __init__(self, nc, stack):
        self.nc = nc
        self.stack = stack
        self.ops = {q: [] for q in ALL_Q}
        self.dma_sems = {}
        self.eng_sems = {}
        self.waited = {q: {} for q in ALL_Q}
        self.dwaited = {q: {} for q in ALL_Q}
        self.nres = 0

    def res(self, name=None):
        self.nres += 1
        return Res(name or f"r{self.nres}")

    def _add_wait(self, op, tok):
        if tok is None:
            return
        q = op.q
        if tok[0] == "eng":
            _, sq, sidx = tok
            if sq == q and (q == "pe" or q == "sp" or not SAME_ENGINE_SYNC):
                return
            if self.waited[q].get(sq, -1) >= sidx:
                return
            self.waited[q][sq] = sidx
            op.waits.append(tok)
            self.ops[sq][sidx].signal = True
        else:
            _, key, val = tok
            if self.dwaited[q].get(key, -1) >= val:
                return
            self.dwaited[q][key] = val
            op.waits.append(tok)

    def _deps(self, o, r, w, tok):
        toks = []
        for res in r:
            if res.w is not None:
                toks.append(res.w)
        for res in w:
            if res.w is not None:
                toks.append(res.w)
            toks.extend(res.r)
        best = {}
        for t in toks:
            k = (t[0], t[1])
            if k not in best or best[k][2] < t[2]:
                best[k] = t
        for t in best.values():
            self._add_wait(o, t)
        for res in r:
            res.r.append(tok)
        for res in w:
            res.w = tok
            res.r = []

    def op(self, q, fn, r=(), w=()):
        o = Op(q, fn)
        o.idx = len(self.ops[q])
        self._deps(o, r, w, ("eng", q, o.idx))
        self.ops[q].append(o)
        return o

    def dma(self, q, fn, r=(), w=(), semkey=None):
        res0 = (list(w) + list(r))[0] if (w or r) else None
        key = semkey if semkey is not None else id(res0)
        if key not in self.dma_sems:
            self.dma_sems[key] = [None, 0]
        ent = self.dma_sems[key]
        ent[1] += 16
        o = Op(q, fn, dma=(key, ent[1]))
        o.idx = len(self.ops[q])
        tok = ("dma", key, ent[1])
        self._deps(o, r, w, tok)
        self.ops[q].append(o)
        return tok

    def finish(self, out_tokens):
        o = Op("sp", None)
        o.idx = len(self.ops["sp"])
        for t in out_tokens:
            self._add_wait(o, t)
        self.ops["sp"].append(o)

    def emit(self):
        nc = self.nc
        st = self.stack
        for q in ALL_Q:
            self.eng_sems[q] = st.enter_context(nc.semaphore(f"es_{q}"))
        for i, key in enumerate(self.dma_sems):
            self.dma_sems[key][0] = st.enter_context(nc.semaphore(f"ds_{i}"))
        cnt = {}
        for q in ALL_Q:
            c = 0
            arr = []
            for o in self.ops[q]:
                if o.signal:
                    c += 1
                arr.append(c)
            cnt[q] = arr
        block = st.enter_context(nc.Block())

        def run(q, eng):
            for o in self.ops[q]:
                for t in o.waits:
                    if t[0] == "eng":
                        eng.wait_ge(self.eng_sems[t[1]], cnt[t[1]][t[2]])
                    else:
                        eng.wait_ge(self.dma_sems[t[1]][0], t[2])
                if o.fn is None:
                    continue
                ins = o.fn(eng)
                if o.dma is not None:
                    ins.then_inc(self.dma_sems[o.dma[0]][0], 16)
                elif o.signal:
                    ins.then_inc(self.eng_sems[q], 1)

        @block.tensor
        def _(eng):
            run("pe", eng)

        @block.scalar
        def _(eng):
            run("act", eng)

        @block.vector
        def _(eng):
            run("dve", eng)

        @block.gpsimd
        def _(eng):
            run("pool", eng)

        @block.sync
        def _(eng):
            run("sp", eng)

    def stats(self):
        return {q: (len(self.ops[q]), sum(len(o.waits) for o in self.ops[q])) for q in ALL_Q}


def _rope_tables(half, pos):
    inv = (np.float32(10000.0) ** (-(np.arange(half, dtype=np.float32)) / np.float32(half))).astype(np.float32)
    ang = (pos.astype(np.float32)[:, None] * inv[None, :]).astype(np.float32)
    return np.cos(ang).astype(np.float32), np.sin(ang).astype(np.float32)


def host_tables():
    t = {}
    pos_p = np.arange(SEQ, dtype=np.float32)
    pos_s = (PAST + np.arange(SL)).astype(np.float32)
    c, s = _rope_tables(64, pos_p)
    t["c2p"] = np.ascontiguousarray(np.concatenate([c, c], 1).T)
    t["ssp"] = np.ascontiguousarray(np.concatenate([s, -s], 1).T)
    c, s = _rope_tables(64, pos_s)
    t["c2s"] = np.ascontiguousarray(np.tile(np.concatenate([c, c], 1).T, (1, SB)))
    t["sss"] = np.ascontiguousarray(np.tile(np.concatenate([s, -s], 1).T, (1, SB)))
    c, s = _rope_tables(32, pos_p)
    cc = np.concatenate([c, c], 1).reshape(NT, 128, 64).transpose(1, 0, 2)
    ss = np.concatenate([-s, s], 1).reshape(NT, 128, 64).transpose(1, 0, 2)
    t["cc1"] = np.ascontiguousarray(cc)
    t["ss1"] = np.ascontiguousarray(ss)
    c, s = _rope_tables(32, pos_s)
    t["cc1s"] = np.ascontiguousarray(np.tile(np.concatenate([c, c], 1), (SB, 1)))
    t["ss1s"] = np.ascontiguousarray(np.tile(np.concatenate([-s, s], 1), (SB, 1)))
    lg = np.log(np.float32(1.0) - np.float32(2.0) ** (np.float32(-5.0) - np.arange(RH, dtype=np.float32))).astype(np.float32)
    for name, C in (("p", 128), ("s", SL)):
        idx = np.arange(C, dtype=np.float32)
        diff = idx[:, None] - idx[None, :]
        dec = np.where(diff >= 0, np.exp(np.maximum(diff, 0.0)[None] * lg[:, None, None]), 0.0).astype(np.float32)
        t["decT" + name] = np.ascontiguousarray((dec * np.float32(KSCALE)).transpose(2, 0, 1))
        qd = np.exp((idx + 1.0)[:, None] * lg[None, :]).astype(np.float32)
        t["qdec" + name] = np.ascontiguousarray(np.broadcast_to(qd.T[None], (128, RH, C)))
        kd = np.exp((C - 1.0 - idx)[:, None] * lg[None, :]).astype(np.float32) * np.float32(KSCALE)
        t["kdec" + name] = np.ascontiguousarray(kd)
        t["gC" + name] = [float(np.exp(np.float32(C) * lg[h])) for h in range(RH)]
    ic = np.zeros((128, 4, 15), np.float32)
    for g, w in enumerate(POOL_W):
        ic[:, g, :] = 1.0 / np.minimum(float(w), np.arange(15) + 1.0)
    t["invcnt"] = ic
    qi = np.arange(128)
    t["maskd"] = np.where(qi[None, :] <= qi[:, None], 0.0, NEG).astype(np.float32)
    t["ident"] = np.eye(128, dtype=np.float32)
    r = np.arange(32) % 4
    t["masks"] = np.where(np.arange(4)[None, :] <= r[:, None], 0.0, NEG).astype(np.float32)
    mb = np.full((32, SB, NS), NEG, np.float32)
    for bb in range(SB):
        for l2 in range(SL):
            mb[:, bb, bb * SL + l2] = np.where(l2 <= r, 0.0, NEG)
    t["masksb"] = np.ascontiguousarray(mb.reshape(32, SB * NS))
    return t


ARENA_BYTES = 142144


class PsumRing:
    def __init__(self, S):
        self.res = [S.res(f"bank{i}") for i in range(8)]
        self.nxt = 0

    def get(self, n=1):
        if n == 2 and self.nxt % 2:
            self.nxt = (self.nxt + 1) % 8
        if n == 4 and self.nxt % 4:
            self.nxt = (self.nxt + 4 - self.nxt % 4) % 8
        b = self.nxt
        self.nxt = (self.nxt + n) % 8
        return b, [self.res[b + i] for i in range(n)]


class Arena:
    def __init__(self, S, buf):
        self.S = S
        self.buf = buf
        self.hist = []
        self.cur = 0

    def at(self, off):
        assert off % 4 == 0
        self.cur = off

    def get(self, nelem, dt=BF16, name=None, nres=None):
        size = 2 if dt == BF16 else 4
        nb = (nelem * size + 3) // 4 * 4
        s, e = self.cur, self.cur + nb
        assert e <= ARENA_BYTES, (name, e)
        self.cur = e
        pend = []
        for (s2, e2, r2) in self.hist:
            if s2 < e and s < e2:
                if r2.w is not None:
                    pend.append(r2.w)
                pend.extend(r2.r)
        best = {}
        for t in pend:
            k = (t[0], t[1])
            if k not in best or best[k][2] < t[2]:
                best[k] = t
        pend = list(best.values())
        rs = []
        for i in range(nres or 1):
            r = self.S.res(f"{name}_{i}")
            r.r = list(pend)
            self.hist.append((s, e, r))
            rs.append(r)
        v = self.buf[:, s // 2:s // 2 + nb // 2]
        if dt != BF16:
            v = v.bitcast(dt)
        return v[:, 0:nelem], (rs if nres else rs[0])


class _Stop(Exception):
    pass


def build_program(tabs, with_sample=True, stop=None):
    nc = bass.Bass("TRN2", target_bir_lowering=False)

    def din(name, shape, dt=F32):
        return nc.dram_tensor(name, list(shape), dt, kind="ExternalInput").ap()

    def dout(name, shape):
        return nc.dram_tensor(name, list(shape), F32, kind="ExternalOutput").ap()

    xp = din("xp", [SEQ, D])
    xs = din("xs", [NS, D])
    spool = din("spool", [SB * 15, 512])
    sret = din("sret", [SB * RH, 128, 128])
    sconv = din("sconv", [DEPTH, SB * 2, DFF])
    ptab = din("ptab", [128, SB], I32)
    if with_sample:
        cckv = din("cckv", [5120, 128 * 256])
        ckpe = din("ckpe", [5120, 128 * 64])
    w_in = din("w_in", [D, 2560])
    pool_w = din("pool_w", [128, 4, 128])
    pool_scale = din("pool_scale", [128, 4])
    gn_g = din("gn_g", [512])
    w_o0 = din("w_o0", [D, D])
    w_dq = din("w_dq", [D, 512])
    qn_g = din("qn_g", [512])
    w_uq = din("w_uq", [512, 1536])
    w_dkv = din("w_dkv", [D, 320])
    kvn_g = din("kvn_g", [256])
    w_ukT = din("w_ukT", [128, 8, 256])
    w_uv = din("w_uv", [256, 1024])
    w_o1 = din("w_o1", [D, D])
    w_up = din("w_up", [DEPTH, NJ, 128, 8, 256])
    w_down = din("w_down", [DEPTH, DFF, D])
    conv_w = din("conv_w", [DEPTH, 128, NJ, 3])
    conv_b = din("conv_b", [DEPTH, 128, NJ])
    ln_mix_g = din("ln_mix_g", [DEPTH, D])
    ln_mix_b = din("ln_mix_b", [DEPTH, D])
    ln_ffn_g = din("ln_ffn_g", [DEPTH, D])
    ln_ffn_b = din("ln_ffn_b", [DEPTH, D])
    tdr = {k: din("t_" + k, v.shape) for k, v in tabs.items() if isinstance(v, np.ndarray)}

    yp = dout("yp", [SEQ, D])
    ys = dout("ys", [NS, D])
    poolp = dout("poolp", [15, 512])
    pools = dout("pools", [SB * 15, 512])
    retp = dout("retp", [RH, 128, 128])
    rets = dout("rets", [SB * RH, 128, 128])
    ckvp = dout("ckvp", [SEQ, 256])
    ckvs = dout("ckvs", [NS, 256])
    kpep = dout("kpep", [SEQ, 64])
    kpes = dout("kpes", [NS, 64])
    convp = dout("convp", [DEPTH, 2, DFF])
    convs = dout("convs", [DEPTH, SB * 2, DFF])

    out_toks = []
    st = contextlib.ExitStack()
    with st:
        S = Sched(nc, st)
        _n = [0]

        def sb(shape, dt, name):
            return st.enter_context(nc.sbuf_tensor(name, list(shape), dt))

        P = st.enter_context(nc.psum_tensor("P", [128, 8, 512], F32))
        ring = PsumRing(S)

        def bank(b, n=1):
            return P[:, b:b + n, :].rearrange("p a b -> p (a b)")

        def bankbf(b, n=1):
            return P[:, b:b + n, :].rearrange("p a b -> p (a b)").bitcast(BF16)

        X = sb([128, NT, D], F32, "X")
        rX = [S.res(f"X{t}") for t in range(NT)]
        BIG = sb([128, ARENA_BYTES // 2], BF16, "BIG")
        A = Arena(S, BIG)
        ident_f = sb([128, 128], F32, "ident_f"); r_idf = S.res()
        ident_b = sb([128, 128], BF16, "ident_b"); r_idb = S.res()
        maskd = sb([128, 128], BF16, "maskd"); r_maskd = S.res()
        eps_t = sb([128, 4], F32, "eps_t"); r_eps = S.res()
        XS = sb([NS, D], F32, "XS"); r_XS = S.res()
        S.dma("sp", lambda e: e.dma_start(out=ident_f[:], in_=tdr["ident"]), w=[r_idf])
        S.dma("pool", lambda e: e.dma_start(out=ident_b[:], in_=tdr["ident"]), w=[r_idb])
        S.dma("pool", lambda e: e.dma_start(out=maskd[:], in_=tdr["maskd"]), w=[r_maskd])
        S.op("pool", lambda e: e.memset(eps_t[:, 0:1], LN_EPS), w=[r_eps])
        S.op("pool", lambda e: e.memset(eps_t[:, 1:2], RMS_EPS), w=[r_eps])
        S.op("pool", lambda e: e.memset(eps_t[:, 2:3], GN_EPS), w=[r_eps])
        eps_ln, eps_rms, eps_gn = eps_t[:, 0:1], eps_t[:, 1:2], eps_t[:, 2:3]

        def evac(q, out_ap, in_ap, r, w):
            if q == "act":
                S.op("act", lambda e: e.copy(out=out_ap, in_=in_ap), r=r, w=w)
            else:
                S.op(q, lambda e: e.tensor_copy(out=out_ap, in_=in_ap), r=r, w=w)

        def load_ln(lnt, r_lnt, g_ap, b_ap):
            S.dma("sp", lambda e: e.dma_start(out=lnt[:, 0:D], in_=g_ap.partition_broadcast(128)), w=[r_lnt])
            S.dma("sp", lambda e: e.dma_start(out=lnt[:, D:2 * D], in_=b_ap.partition_broadcast(128)), w=[r_lnt])

        def layer_norm_tile(z_ap, z_res, out_ap, out_res, npart, scr, scr_res, lnt, r_lnt):
            for c in range(2):
                S.op("dve", lambda e, c=c: e.bn_stats(out=scr[0:npart, c * 6:(c + 1) * 6], in_=z_ap[:, c * 512:(c + 1) * 512]), r=[z_res], w=[scr_res])
            mv = scr[0:npart, 12:14]
            S.op("dve", lambda e: e.bn_aggr(out=mv, in_=scr[0:npart, 0:12]), r=[scr_res], w=[scr_res])
            sd = scr[0:npart, 14:15]
            S.op("act", lambda e: e.activation(out=sd, in_=scr[0:npart, 13:14], func=AF.Sqrt, bias=eps_ln[0:npart], scale=1.0), r=[scr_res, r_eps], w=[scr_res])
            S.op("dve", lambda e: e.reciprocal(out=sd, in_=sd), r=[scr_res], w=[scr_res])
            nmr = scr[0:npart, 15:16]
            S.op("dve", lambda e: e.tensor_scalar(out=nmr, in0=scr[0:npart, 12:13], scalar1=sd, scalar2=-1.0, op0=ALU.mult, op1=ALU.mult), r=[scr_res], w=[scr_res])
            S.op("act", lambda e: e.activation(out=z_ap, in_=z_ap, func=AF.Identity, bias=nmr, scale=sd), r=[scr_res, z_res], w=[z_res])
            S.op("pool", lambda e: e.tensor_tensor(out=z_ap, in0=z_ap, in1=lnt[0:npart, 0:D], op=ALU.mult), r=[z_res, r_lnt], w=[z_res])
            S.op("pool", lambda e: e.tensor_tensor(out=out_ap, in0=z_ap, in1=lnt[0:npart, D:2 * D], op=ALU.add), r=[z_res, r_lnt], w=[out_res])

        def transpose_x(src_ap, src_res, dst_ap, dst_res, npart=128):
            for half in range(2):
                b, br = ring.get(1)
                pv = bank(b)[:, 0:4 * npart].rearrange("p (a b) -> p a b", a=4)
                for c in range(4):
                    cc = half * 4 + c
                    S.op("pe", lambda e, c=c, cc=cc, pv=pv: e.transpose(out=pv[:, c, :], in_=src_ap[:, cc * 128:(cc + 1) * 128], identity=ident_f[0:npart, 0:npart]),
                         r=[src_res, r_idf], w=br)
                evac("act" if half == 0 else "dve", dst_ap[:, half * 4:half * 4 + 4, :], pv, br, [dst_res])

        def chk(tag):
            if stop == tag:
                raise _Stop()

        try:
            for t in range(NT):
                S.dma("sp", lambda e, t=t: e.dma_start(out=X[:, t, :], in_=xp[t * 128:(t + 1) * 128, :]), w=[rX[t]])
            S.dma("sp", lambda e: e.dma_start(out=XS[:], in_=xs), w=[r_XS])

            A.at(0)
            w_in_sb, r_win = A.get(8 * 2560, BF16, "w_in"); w_in_sb = w_in_sb.rearrange("p (k n) -> p k n", k=8)
            w_o_sb, r_wo = A.get(8 * 1024, BF16, "w_o0"); w_o_sb = w_o_sb.rearrange("p (k n) -> p k n", k=8)
            poolw_sb, r_poolw = A.get(512, BF16, "poolw"); poolw_sb = poolw_sb.rearrange("p (g d) -> p g d", g=4)
            c0buf, r_c0 = A.get(1024 + 1024 + 4 + 60 + 4 + 512, F32, "c0")
            decT = c0buf[:, 0:512].rearrange("p (h l) -> p h l", h=4)
            qdec = c0buf[:, 512:1024].rearrange("p (h l) -> p h l", h=4)
            kdec = c0buf[:, 1024:1028]
            invcnt = c0buf[:, 1028:1088].rearrange("p (g t) -> p g t", g=4)
            pscale = c0buf[:, 1088:1092]
            gng = c0buf[:, 1092:1604]
            lnt0, r_lnt0 = A.get(2 * D, F32, "lnt0")
            w_in_v = w_in.rearrange("(k p) n -> p k n", p=128)
            for hh in range(2):
                S.dma("pool", lambda e, hh=hh: e.dma_start(out=w_in_sb[:, :, hh * 1280:(hh + 1) * 1280], in_=w_in_v[:, :, hh * 1280:(hh + 1) * 1280]), w=[r_win])
            S.dma("pool", lambda e: e.dma_start(out=poolw_sb, in_=pool_w), w=[r_poolw])
            S.dma("pool", lambda e: e.dma_start(out=w_o_sb, in_=w_o0.rearrange("(k p) n -> p k n", p=128)), w=[r_wo])
            S.dma("sp", lambda e: e.dma_start(out=decT, in_=tdr["decTp"]), w=[r_c0])
            S.dma("sp", lambda e: e.dma_start(out=qdec, in_=tdr["qdecp"]), w=[r_c0])
            S.dma("sp", lambda e: e.dma_start(out=kdec, in_=tdr["kdecp"]), w=[r_c0])
            S.dma("sp", lambda e: e.dma_start(out=invcnt, in_=tdr["invcnt"]), w=[r_c0])
            S.dma("sp", lambda e: e.dma_start(out=pscale, in_=pool_scale), w=[r_c0])
            S.dma("sp", lambda e: e.dma_start(out=gng, in_=gn_g.partition_broadcast(128)), w=[r_c0])
            load_ln(lnt0, r_lnt0, ln_mix_g[0], ln_mix_b[0])
            L0_BUF = A.cur

            def dbl(nelem, dt, name, shape=None, **kw):
                out = []
                for i in range(2):
                    v, r = A.get(nelem, dt, f"{name}{i}")
                    if shape:
                        v = v.rearrange(shape, **kw)
                    out.append((v, r))
                return [x[0] for x in out], [x[1] for x in out]

            xT, r_xT = dbl(1024, BF16, "xT", "p (k t) -> p k t", k=8)
            ropet, r_ropet = dbl(256, F32, "ropet")
            uext, r_uext = dbl(4 * 143 + 1, F32, "uext")
            uext = [u[:, 0:572].rearrange("p (g t) -> p g t", g=4) for u in uext]
            tA, r_tA = A.get(573, F32, "tA"); tA = tA[:, 0:572].rearrange("p (g t) -> p g t", g=4)
            tB, r_tB = A.get(573, F32, "tB"); tB = tB[:, 0:572].rearrange("p (g t) -> p g t", g=4)
            pooledT, r_pooled = dbl(512, BF16, "pooledT", "p (g t) -> p g t", g=4)
            qk_f, r_qkf = A.get(1024, F32, "qk_f"); qk_f = qk_f.rearrange("p (h t) -> p h t", h=8)
            ropeA, r_ropeA = A.get(1024, F32, "ropeA"); ropeA = ropeA.rearrange("p (h t) -> p h t", h=8)
            ropeB, r_ropeB = A.get(1024, F32, "ropeB"); ropeB = ropeB.rearrange("p (h t) -> p h t", h=8)
            qkT, r_qkT = dbl(1024, BF16, "qkT", "p (h t) -> p h t", h=8)
            qdT, r_qdT = dbl(512, BF16, "qdT", "p (h t) -> p h t", h=4)
            v_sb, r_v = dbl(512, BF16, "v")
            sgg, r_sgg = dbl(512, F32, "sgg")
            sT_bf, r_sT = dbl(512, BF16, "sT", "p (h t) -> p h t", h=4)
            kp_bf, r_kp = dbl(512, BF16, "kp", "p (h t) -> p h t", h=4)
            S_f, r_Sf = A.get(512, F32, "S_f"); S_f = S_f.rearrange("p (h t) -> p h t", h=4)
            S_b, r_Sb = A.get(512, BF16, "S_b"); S_b = S_b.rearrange("p (h t) -> p h t", h=4)
            on1, r_on1 = A.get(512, F32, "on1")
            ret_bf, r_ret = dbl(512, BF16, "ret")
            catT, r_cat = dbl(1024, BF16, "catT", "p (c t) -> p c t", c=8)
            z0_, r_z0_ = A.get(D, F32, "z"); z = [z0_, z0_]; r_z = [r_z0_, r_z0_]
            scr, r_scr = dbl(16, F32, "scr")
            gst, r_gst = A.get(32, F32, "gst")
            poolo, r_poolo = A.get(512, F32, "poolo")
            L0_END = A.cur
            print("layer0 arena bytes", L0_END)

            S.op("pool", lambda e: e.memset(uext[1][:, :, 128:143], 0.0), w=[r_uext[1]])

            def l0A(t):
                pb = t % 2
                tok = slice(t * 128, (t + 1) * 128)
                S.dma("sp", lambda e, pb=pb, tok=tok: e.dma_start(out=ropet[pb][:, 0:128], in_=tdr["c2p"][:, tok]), w=[r_ropet[pb]])
                S.dma("sp", lambda e, pb=pb, tok=tok: e.dma_start(out=ropet[pb][:, 128:256], in_=tdr["ssp"][:, tok]), w=[r_ropet[pb]])
                transpose_x(X[:, t, :], rX[t], xT[pb], r_xT[pb])
                bu, ru = ring.get(1)
                bq, rq = ring.get(1)
                bk, rk = ring.get(1)
                bv, rv = ring.get(1)
                bg, rg = ring.get(1)
                for (bb, rr, c0) in ((bu, ru, 0), (bq, rq, 512), (bk, rk, 1024)):
                    pv = bank(bb).rearrange("p (h t) -> p h t", h=4)
                    for h in range(4):
                        for k in range(8):
                            S.op("pe", lambda e, pv=pv, h=h, k=k, c0=c0, pb=pb: e.matmul(pv[:, h, :], lhsT=w_in_sb[:, k, c0 + h * 128:c0 + (h + 1) * 128], rhs=xT[pb][:, k, :], start=(k == 0), stop=(k == 7)),
                                 r=[r_win, r_xT[pb]], w=rr)
                for (bb, rr, c0) in ((bv, rv, 1536), (bg, rg, 2048)):
                    for k in range(8):
                        S.op("pe", lambda e, bb=bb, k=k, c0=c0, pb=pb: e.matmul(bank(bb), lhsT=xT[pb][:, k, :], rhs=w_in_sb[:, k, c0:c0 + 512], start=(k == 0), stop=(k == 7)),
                             r=[r_win, r_xT[pb]], w=rr)
                ue = uext[pb]
                S.op("act", lambda e, ue=ue, bu=bu: e.copy(out=ue[:, :, 15:143], in_=bank(bu).rearrange("p (h t) -> p h t", h=4)), r=ru, w=[r_uext[pb]])
                S.op("pool", lambda e, ue=ue, pb=pb: e.tensor_copy(out=ue[:, :, 0:15], in_=uext[1 - pb][:, :, 128:143]), r=[r_uext[1 - pb]], w=[r_uext[pb]])
                S.op("pool", lambda e, ue=ue: e.tensor_tensor(out=tA[:, :, 1:143], in0=ue[:, :, 1:143], in1=ue[:, :, 0:142], op=ALU.add), r=[r_uext[pb]], w=[r_tA])
                S.op("pool", lambda e: e.tensor_tensor(out=tB[:, 1:4, 3:143], in0=tA[:, 1:4, 3:143], in1=tA[:, 1:4, 1:141], op=ALU.add), r=[r_tA], w=[r_tB])
                S.op("pool", lambda e: e.tensor_tensor(out=tA[:, 2:4, 7:143], in0=tB[:, 2:4, 7:143], in1=tB[:, 2:4, 3:139], op=ALU.add), r=[r_tB], w=[r_tA])
                S.op("pool", lambda e: e.tensor_tensor(out=tB[:, 3:4, 15:143], in0=tA[:, 3:4, 15:143], in1=tA[:, 3:4, 7:135], op=ALU.add), r=[r_tA], w=[r_tB])
                fin = (tA, tB, tA, tB)
                for g in range(4):
                    S.op("dve", lambda e, g=g, ue=ue, pb=pb: e.scalar_tensor_tensor(out=pooledT[pb][:, g, :], in0=fin[g][:, g, 15:143], scalar=1.0 / POOL_W[g], in1=ue[:, g, 15:143], op0=ALU.mult, op1=ALU.subtract),
                         r=[r_tA, r_tB, r_uext[pb]], w=[r_pooled[pb]])
                if t == 0:
                    for g in range(4):
                        S.op("dve", lambda e, g=g: e.tensor_tensor(out=ropeA[:, g, 0:15], in0=fin[g][:, g, 15:30], in1=invcnt[:, g, :], op=ALU.mult), r=[r_tA, r_tB, r_c0], w=[r_ropeA])
                        S.op("dve", lambda e, g=g, ue=ue, pb=pb: e.tensor_tensor(out=pooledT[pb][:, g, 0:15], in0=ropeA[:, g, 0:15], in1=ue[:, g, 15:30], op=ALU.subtract), r=[r_ropeA, r_uext[pb]], w=[r_pooled[pb]])
                bpm, rpm = ring.get(1)
                pmv = bank(bpm).rearrange("p (h t) -> p h t", h=4)
                for g in range(4):
                    S.op("pe", lambda e, g=g, pmv=pmv, pb=pb: e.matmul(pmv[:, g, :], lhsT=poolw_sb[:, g, :], rhs=pooledT[pb][:, g, :], start=True, stop=True), r=[r_poolw, r_pooled[pb]], w=rpm)
                for g in range(4):
                    S.op("act", lambda e, g=g, pmv=pmv, pb=pb: e.activation(out=catT[pb][:, g, :], in_=pmv[:, g, :], func=AF.Identity, bias=0.0, scale=pscale[:, g:g + 1]), r=rpm + [r_c0], w=[r_cat[pb]])
                if t == NT - 1:
                    bpo, rpo = ring.get(1)
                    for g in range(4):
                        S.op("pe", lambda e, g=g, ue=ue, bpo=bpo: e.transpose(out=bank(bpo)[0:15, g * 128:(g + 1) * 128], in_=ue[:, g, 128:143], identity=ident_f[:]), r=[r_uext[pb], r_idf], w=rpo)
                    S.op("act", lambda e, bpo=bpo: e.copy(out=poolo[0:15, :], in_=bank(bpo)[0:15, :]), r=rpo, w=[r_poolo])
                    out_toks.append(S.dma("sp", lambda e: e.dma_start(out=poolp, in_=poolo[0:15, :]), r=[r_poolo]))
                S.op("act", lambda e, bq=bq: e.copy(out=qk_f[:, 0:4, :], in_=bank(bq).rearrange("p (h t) -> p h t", h=4)), r=rq, w=[r_qkf])
                S.op("act", lambda e, bk=bk: e.copy(out=qk_f[:, 4:8, :], in_=bank(bk).rearrange("p (h t) -> p h t", h=4)), r=rk, w=[r_qkf])
                c2b = ropet[pb][:, 0:128].unsqueeze(1)
                ssb = ropet[pb][:, 128:256].unsqueeze(1)
                S.op("dve", lambda e, c2b=c2b: e.tensor_tensor(out=ropeA, in0=qk_f, in1=c2b.to_broadcast([128, 8, 128]), op=ALU.mult), r=[r_qkf, r_ropet[pb]], w=[r_ropeA])
                S.op("pool", lambda e, ssb=ssb: e.tensor_tensor(out=ropeB[0:64], in0=qk_f[64:128], in1=ssb[64:128].to_broadcast([64, 8, 128]), op=ALU.mult), r=[r_qkf, r_ropet[pb]], w=[r_ropeB])
                S.op("pool", lambda e, ssb=ssb: e.tensor_tensor(out=ropeB[64:128], in0=qk_f[0:64], in1=ssb[0:64].to_broadcast([64, 8, 128]), op=ALU.mult), r=[r_qkf, r_ropet[pb]], w=[r_ropeB])
                S.op("dve", lambda e, pb=pb: e.tensor_tensor(out=qkT[pb], in0=ropeA, in1=ropeB, op=ALU.add), r=[r_ropeA, r_ropeB], w=[r_qkT[pb]])
                if t > 0:
                    S.op("pool", lambda e, pb=pb: e.tensor_tensor(out=qdT[pb], in0=qkT[pb][:, 0:4, :], in1=qdec, op=ALU.mult), r=[r_qkT[pb], r_c0], w=[r_qdT[pb]])
                S.op("act", lambda e, pb=pb, bv=bv: e.copy(out=v_sb[pb], in_=bank(bv)), r=rv, w=[r_v[pb]])
                S.op("act", lambda e, pb=pb, bg=bg: e.activation(out=sgg[pb], in_=bank(bg), func=AF.Silu), r=rg, w=[r_sgg[pb]])
                S.op("pool", lambda e, pb=pb: e.tensor_tensor(out=sgg[pb], in0=sgg[pb], in1=gng, op=ALU.mult), r=[r_sgg[pb], r_c0], w=[r_sgg[pb]])
            def l0B(t):
                pb = t % 2
                bs, rs = ring.get(1)
                sv = bank(bs).rearrange("p (h t) -> p h t", h=4)
                for h in range(4):
                    S.op("pe", lambda e, h=h, sv=sv, pb=pb: e.matmul(sv[:, h, :], lhsT=qkT[pb][:, 4 + h, :], rhs=qkT[pb][:, h, :], start=True, stop=True), r=[r_qkT[pb]], w=rs)
                S.op("dve", lambda e, sv=sv, pb=pb: e.tensor_tensor(out=sT_bf[pb], in0=sv, in1=decT, op=ALU.mult), r=rs + [r_c0], w=[r_sT[pb]])
                bo, ro = ring.get(1)
                ov = bank(bo).rearrange("p (h t) -> p h t", h=4)
                for h in range(4):
                    S.op("pe", lambda e, h=h, ov=ov, pb=pb, t=t: e.matmul(ov[:, h, :], lhsT=sT_bf[pb][:, h, :], rhs=v_sb[pb][:, h * 128:(h + 1) * 128], start=True, stop=(t == 0)), r=[r_sT[pb], r_v[pb]], w=ro)
                    if t > 0:
                        S.op("pe", lambda e, h=h, ov=ov, pb=pb: e.matmul(ov[:, h, :], lhsT=qdT[pb][:, h, :], rhs=S_b[:, h, :], start=False, stop=True), r=[r_qdT[pb], r_Sb], w=ro)
                bkt, rkt = ring.get(1)
                ktv = bankbf(bkt)[:, 0:512].rearrange("p (h t) -> p h t", h=4)
                for h in range(4):
                    S.op("pe", lambda e, h=h, ktv=ktv, pb=pb: e.transpose(out=ktv[:, h, :], in_=qkT[pb][:, 4 + h, :], identity=ident_b[:]), r=[r_qkT[pb], r_idb], w=rkt)
                for h in range(4):
                    S.op("act", lambda e, h=h, ktv=ktv, pb=pb: e.activation(out=kp_bf[pb][:, h, :], in_=ktv[:, h, :], func=AF.Identity, bias=0.0, scale=kdec[:, h:h + 1]), r=rkt + [r_c0], w=[r_kp[pb]])
                bds, rds = ring.get(1)
                dsv = bank(bds).rearrange("p (h t) -> p h t", h=4)
                for h in range(4):
                    S.op("pe", lambda e, h=h, dsv=dsv, pb=pb: e.matmul(dsv[:, h, :], lhsT=kp_bf[pb][:, h, :], rhs=v_sb[pb][:, h * 128:(h + 1) * 128], start=True, stop=True), r=[r_kp[pb], r_v[pb]], w=rds)
                if t == 0:
                    S.op("dve", lambda e, dsv=dsv: e.tensor_copy(out=S_f, in_=dsv), r=rds, w=[r_Sf])
                else:
                    for h in range(4):
                        S.op("dve", lambda e, h=h, dsv=dsv: e.scalar_tensor_tensor(out=S_f[:, h, :], in0=S_f[:, h, :], scalar=tabs["gCp"][h], in1=dsv[:, h, :], op0=ALU.mult, op1=ALU.add), r=rds + [r_Sf], w=[r_Sf])
                if t < NT - 1:
                    S.op("pool", lambda e: e.tensor_copy(out=S_b, in_=S_f), r=[r_Sf], w=[r_Sb])
                else:
                    out_toks.append(S.dma("sp", lambda e: e.dma_start(out=retp.rearrange("h k v -> k h v"), in_=S_f), r=[r_Sf]))
                for h in range(4):
                    S.op("dve", lambda e, h=h, ov=ov: e.bn_stats(out=gst[:, h * 6:(h + 1) * 6], in_=ov[:, h, :]), r=ro, w=[r_gst])
                    S.op("dve", lambda e, h=h: e.bn_aggr(out=gst[:, 24 + 2 * h:26 + 2 * h], in_=gst[:, h * 6:(h + 1) * 6]), r=[r_gst], w=[r_gst])
                mvv = gst[:, 24:32].rearrange("p (h two) -> p h two", two=2)
                S.op("act", lambda e, pb=pb: e.activation(out=scr[pb][:, 0:4], in_=mvv[:, :, 1], func=AF.Sqrt, bias=eps_gn, scale=1.0), r=[r_gst, r_eps], w=[r_scr[pb]])
                S.op("dve", lambda e, pb=pb: e.reciprocal(out=scr[pb][:, 0:4], in_=scr[pb][:, 0:4]), r=[r_scr[pb]], w=[r_scr[pb]])
                S.op("dve", lambda e, pb=pb: e.scalar_tensor_tensor(out=scr[pb][:, 4:8], in0=mvv[:, :, 0], scalar=-1.0, in1=scr[pb][:, 0:4], op0=ALU.mult, op1=ALU.mult), r=[r_gst, r_scr[pb]], w=[r_scr[pb]])
                for h in range(4):
                    S.op("act", lambda e, h=h, ov=ov, pb=pb: e.activation(out=on1[:, h * 128:(h + 1) * 128], in_=ov[:, h, :], func=AF.Identity, bias=scr[pb][:, 4 + h:5 + h], scale=scr[pb][:, h:h + 1]), r=ro + [r_scr[pb]], w=[r_on1])
                S.op("pool", lambda e, pb=pb: e.tensor_tensor(out=ret_bf[pb], in0=on1, in1=sgg[pb], op=ALU.mult), r=[r_on1, r_sgg[pb]], w=[r_ret[pb]])
                brt, rrt = ring.get(1)
                rtv = bankbf(brt)[:, 0:512].rearrange("p (h t) -> p h t", h=4)
                for h in range(4):
                    S.op("pe", lambda e, h=h, rtv=rtv, pb=pb: e.transpose(out=rtv[:, h, :], in_=ret_bf[pb][:, h * 128:(h + 1) * 128], identity=ident_b[:]), r=[r_ret[pb], r_idb], w=rrt)
                S.op("dve", lambda e, rtv=rtv, pb=pb: e.tensor_copy(out=catT[pb][:, 4:8, :], in_=rtv), r=rrt, w=[r_cat[pb]])
                by, ry = ring.get(2)
                for hf in range(2):
                    for c in range(8):
                        S.op("pe", lambda e, hf=hf, c=c, by=by, pb=pb: e.matmul(bank(by + hf), lhsT=catT[pb][:, c, :], rhs=w_o_sb[:, c, hf * 512:(hf + 1) * 512], start=(c == 0), stop=(c == 7)), r=[r_cat[pb], r_wo], w=[ry[hf]])
                S.op("dve", lambda e, by=by, pb=pb, t=t: e.scalar_tensor_tensor(out=z[pb], in0=X[:, t, :], scalar=ALPHA, in1=bank(by, 2), op0=ALU.mult, op1=ALU.add), r=[rX[t]] + ry, w=[r_z[pb]])
                layer_norm_tile(z[pb], r_z[pb], X[:, t, :], rX[t], 128, scr[pb], r_scr[pb], lnt0, r_lnt0)

            l0A(0)
            for t in range(NT):
                if t + 1 < NT:
                    l0A(t + 1)
                l0B(t)

            def ffn(layer, final):
                A.at(0)
                gated, r_gated = A.get(NJ * 1024, BF16, "gated", nres=NJ); gated = gated.rearrange("p (j n) -> p j n", j=NJ)
                x1T, r_x1T = A.get(8 * 1024, BF16, "x1T", nres=8); x1T = x1T.rearrange("p (k n) -> p k n", k=8)
                cb0_, r_cb0_ = A.get(512, F32, "cbuf"); cbuf = [cb0_, cb0_]; r_cbuf = [r_cb0_, r_cb0_]
                sb0_, r_sb0_ = A.get(512, F32, "sbf"); sbf = [sb0_, sb0_]; r_sbf = [r_sb0_, r_sb0_]
                zf0, r_zf0 = A.get(D, F32, "zf"); zf = [zf0, zf0]; r_zf = [r_zf0, r_zf0]
                scf, r_scf = dbl(16, F32, "scf")
                cwb, r_cw = A.get(NJ * 4, F32, "cw"); cw = cwb[:, 0:NJ * 3].rearrange("p (j c) -> p j c", j=NJ); cb = cwb[:, NJ * 3:NJ * 4]
                hist, r_hist = A.get(NJ * 2, F32, "hist"); hist = hist.rearrange("p (j c) -> p j c", j=NJ)
                cst, r_cst = A.get(NJ * 2, F32, "cst"); cst = cst.rearrange("p (j c) -> p j c", j=NJ)
                cso, r_cso = A.get(512, F32, "cso")
                lnt, r_lnt = A.get(2 * D, F32, "lntf")
                wup = []
                r_wup = []
                for i in range(2):
                    v, r = A.get(8 * 256, BF16, f"wup{i}")
                    wup.append(v.rearrange("p (k n) -> p k n", k=8)); r_wup.append(r)
                wdn, r_wdn = A.get(NJ * 1024, BF16, "wdn"); wdn = wdn.rearrange("p (j n) -> p j n", j=NJ)
                if with_sample:
                    xs1T, r_xs1T = A.get(128, BF16, "xs1T"); xs1T = xs1T.rearrange("p (k t) -> p k t", k=8)
                    aext, r_aext = A.get(NJ * 24, F32, "aext"); aext = aext.rearrange("p (j b r) -> p j b r", j=NJ, b=4)
                    b_s, r_bs = A.get(NJ * 16, F32, "b_s"); b_s = b_s.rearrange("p (j t) -> p j t", j=NJ)
                    scT, r_scT = A.get(NJ * 8, F32, "scT"); scT = scT.rearrange("p (j r) -> p j r", j=NJ)
                    c_s, r_cs = A.get(NJ * 16, F32, "c_s"); c_s = c_s.rearrange("p (j b l) -> p j b l", j=NJ, b=4)
                    t_s, r_ts = A.get(NJ * 16, F32, "t_s"); t_s = t_s.rearrange("p (j b l) -> p j b l", j=NJ, b=4)
                    gated_s, r_gs = A.get(NJ * 16, BF16, "gated_s"); gated_s = gated_s.rearrange("p (j t) -> p j t", j=NJ)
                print("ffn arena bytes", A.cur)
                S.dma("sp", lambda e: e.dma_start(out=cw, in_=conv_w[layer]), w=[r_cw])
                S.dma("sp", lambda e: e.dma_start(out=cb, in_=conv_b[layer]), w=[r_cw])
                load_ln(lnt, r_lnt, ln_ffn_g[layer], ln_ffn_b[layer])
                wdn_v = w_down[layer].rearrange("(j p) n -> p j n", p=128)
                wdn_loaded = [False]

                def load_wdn():
                    for jj in range(0, NJ, 2):
                        S.dma("pool", lambda e, jj=jj: e.dma_start(out=wdn[:, jj:jj + 2, :], in_=wdn_v[:, jj:jj + 2, :]), w=[r_wdn])

                def load_wup(j, slot):
                    S.dma("pool", lambda e, j=j, slot=slot: e.dma_start(out=wup[slot], in_=w_up[layer, j]), w=[r_wup[slot]])

                if with_sample:
                    transpose_x(XS[:], r_XS, xs1T, r_xs1T, npart=NS)
                    for jg in range(0, NJ, 4):
                        n = min(4, NJ - jg)
                        S.dma("sp", lambda e, jg=jg, n=n: e.dma_start(out=cso[0:8, 0:n * 128], in_=sconv[layer][:, jg * 128:(jg + n) * 128]), w=[r_cso])
                        bst, rst = ring.get(1)
                        for i_ in range(n):
                            S.op("pe", lambda e, i_=i_, bst=bst: e.transpose(out=bank(bst)[:, i_ * 8:(i_ + 1) * 8], in_=cso[0:8, i_ * 128:(i_ + 1) * 128], identity=ident_f[0:8, 0:8]), r=[r_cso, r_idf], w=rst)
                        S.op("act", lambda e, jg=jg, n=n, bst=bst: e.copy(out=scT[:, jg:jg + n, :], in_=bank(bst)[:, 0:n * 8].rearrange("p (j r) -> p j r", j=n)), r=rst, w=[r_scT])
                    S.op("pool", lambda e: e.tensor_copy(out=aext[:, :, :, 0:2], in_=scT.rearrange("p j (b r) -> p j b r", b=4)), r=[r_scT], w=[r_aext])
                cnt = 0
                for half in range(2):
                    for tt in range(8):
                        t = half * 8 + tt
                        transpose_x(X[:, t, :], rX[t], x1T[:, :, tt * 128:(tt + 1) * 128], r_x1T[tt])
                    load_wup(0, cnt % 2)
                    for j in range(NJ):
                        slot = cnt % 2
                        if j + 1 < NJ:
                            load_wup(j + 1, (cnt + 1) % 2)
                        elif half == 0:
                            pass
                        if half == 0 and j == 1:
                            load_wdn()
                        cnt += 1
                        for grp in range(2):
                            ba, ra = ring.get(1)
                            bb_, rb = ring.get(1)
                            cols = slice(grp * 512, (grp + 1) * 512)
                            for (bx, rx, c0) in ((ba, ra, 0), (bb_, rb, 128)):
                                for k in range(8):
                                    S.op("pe", lambda e, bx=bx, k=k, c0=c0, slot=slot, cols=cols: e.matmul(bank(bx), lhsT=wup[slot][:, k, c0:c0 + 128], rhs=x1T[:, k, cols], start=(k == 0), stop=(k == 7)),
                                         r=[r_wup[slot]] + r_x1T[grp * 4:grp * 4 + 4], w=rx)
                            pa = bank(ba)
                            cp = (j * 2 + grp) % 2
                            c_ = cbuf[cp]
                            s_ = sbf[cp]
                            S.op("act", lambda e, pa=pa, c_=c_, j=j: e.activation(out=c_, in_=pa, func=AF.Identity, bias=cb[:, j:j + 1], scale=cw[:, j, 2:3]), r=ra + [r_cw], w=[r_cbuf[cp]])
                            S.op("dve", lambda e, pa=pa, c_=c_, j=j: e.scalar_tensor_tensor(out=c_[:, 1:512], in0=pa[:, 0:511], scalar=cw[:, j, 1:2], in1=c_[:, 1:512], op0=ALU.mult, op1=ALU.add), r=ra + [r_cw, r_cbuf[cp]], w=[r_cbuf[cp]])
                            S.op("dve", lambda e, pa=pa, c_=c_, j=j: e.scalar_tensor_tensor(out=c_[:, 2:512], in0=pa[:, 0:510], scalar=cw[:, j, 0:1], in1=c_[:, 2:512], op0=ALU.mult, op1=ALU.add), r=ra + [r_cw, r_cbuf[cp]], w=[r_cbuf[cp]])
                            if not (half == 0 and grp == 0):
                                S.op("dve", lambda e, c_=c_, j=j: e.scalar_tensor_tensor(out=c_[:, 0:1], in0=hist[:, j, 1:2], scalar=cw[:, j, 1:2], in1=c_[:, 0:1], op0=ALU.mult, op1=ALU.add), r=[r_hist, r_cw, r_cbuf[cp]], w=[r_cbuf[cp]])
                                S.op("dve", lambda e, c_=c_, j=j: e.scalar_tensor_tensor(out=c_[:, 0:2], in0=hist[:, j, 0:2], scalar=cw[:, j, 0:1], in1=c_[:, 0:2], op0=ALU.mult, op1=ALU.add), r=[r_hist, r_cw, r_cbuf[cp]], w=[r_cbuf[cp]])
                            if half == 1 and grp == 1:
                                S.op("act", lambda e, pa=pa, j=j: e.copy(out=cst[:, j, :], in_=pa[:, 510:512]), r=ra, w=[r_cst])
                            else:
                                S.op("act", lambda e, pa=pa, j=j: e.copy(out=hist[:, j, :], in_=pa[:, 510:512]), r=ra, w=[r_hist])
                            S.op("act", lambda e, c_=c_, s_=s_: e.activation(out=s_, in_=c_, func=AF.Silu), r=[r_cbuf[cp]], w=[r_sbf[cp]])
                            S.op("dve", lambda e, s_=s_, bb_=bb_, j=j, cols=cols: e.tensor_tensor(out=gated[:, j, cols], in0=s_, in1=bank(bb_), op=ALU.mult), r=[r_sbf[cp]] + rb, w=[r_gated[j]])
                        if with_sample and half == 0:
                            bsa, rsa = ring.get(1)
                            for (c0, col) in ((0, 0), (128, 16)):
                                for k in range(8):
                                    S.op("pe", lambda e, k=k, c0=c0, col=col, slot=slot, bsa=bsa: e.matmul(bank(bsa)[:, col:col + 16], lhsT=wup[slot][:, k, c0:c0 + 128], rhs=xs1T[:, k, :], start=(k == 0), stop=(k == 7)), r=[r_wup[slot], r_xs1T], w=rsa)
                            S.op("act", lambda e, j=j, bsa=bsa: e.copy(out=aext[:, j, :, 2:6], in_=bank(bsa)[:, 0:16].rearrange("p (b l) -> p b l", b=4)), r=rsa, w=[r_aext])
                            S.op("dve", lambda e, j=j, bsa=bsa: e.tensor_copy(out=b_s[:, j, :], in_=bank(bsa)[:, 16:32]), r=rsa, w=[r_bs])
                    if with_sample and half == 0:
                        def cwb(c):
                            return cw[:, :, c:c + 1].unsqueeze(3).to_broadcast([128, NJ, 4, 4])
                        S.op("dve", lambda e: e.tensor_tensor(out=c_s, in0=aext[:, :, :, 2:6], in1=cwb(2), op=ALU.mult), r=[r_aext, r_cw], w=[r_cs])
                        S.op("pool", lambda e: e.tensor_tensor(out=t_s, in0=aext[:, :, :, 1:5], in1=cwb(1), op=ALU.mult), r=[r_aext, r_cw], w=[r_ts])
                        S.op("dve", lambda e: e.tensor_tensor(out=c_s, in0=c_s, in1=t_s, op=ALU.add), r=[r_cs, r_ts], w=[r_cs])
                        S.op("pool", lambda e: e.tensor_tensor(out=t_s, in0=aext[:, :, :, 0:4], in1=cwb(0), op=ALU.mult), r=[r_aext, r_cw], w=[r_ts])
                        S.op("dve", lambda e: e.tensor_tensor(out=c_s, in0=c_s, in1=t_s, op=ALU.add), r=[r_cs, r_ts], w=[r_cs])
                        S.op("dve", lambda e: e.tensor_tensor(out=c_s, in0=c_s, in1=cb.unsqueeze(2).unsqueeze(3).to_broadcast([128, NJ, 4, 4]), op=ALU.add), r=[r_cs, r_cw], w=[r_cs])
                        S.op("act", lambda e: e.activation(out=c_s, in_=c_s, func=AF.Silu), r=[r_cs], w=[r_cs])
                        S.op("dve", lambda e: e.tensor_tensor(out=gated_s, in0=c_s.rearrange("p j b l -> p j (b l)"), in1=b_s, op=ALU.mult), r=[r_cs, r_bs], w=[r_gs])
                        S.op("pool", lambda e: e.tensor_copy(out=scT.rearrange("p j (b r) -> p j b r", b=4), in_=aext[:, :, :, 4:6]), r=[r_aext], w=[r_scT])
                        for jg in range(0, NJ, 4):
                            n = min(4, NJ - jg)
                            bst, rst = ring.get(1)
                            for i_ in range(n):
                                S.op("pe", lambda e, i_=i_, jg=jg, bst=bst: e.transpose(out=bank(bst)[0:8, i_ * 128:(i_ + 1) * 128], in_=scT[:, jg + i_, :], identity=ident_f[:]), r=[r_scT, r_idf], w=rst)
                            S.op("act", lambda e, n=n, bst=bst: e.copy(out=cso[0:8, 0:n * 128], in_=bank(bst)[0:8, 0:n * 128]), r=rst, w=[r_cso])
                            out_toks.append(S.dma("sp", lambda e, jg=jg, n=n: e.dma_start(out=convs[layer][:, jg * 128:(jg + n) * 128], in_=cso[0:8, 0:n * 128]), r=[r_cso]))
                    for tt in range(8):
                        t = half * 8 + tt
                        pb = tt % 2
                        by, ry = ring.get(2)
                        for hf in range(2):
                            for j in range(NJ):
                                S.op("pe", lambda e, hf=hf, j=j, by=by, tt=tt: e.matmul(bank(by + hf), lhsT=gated[:, j, tt * 128:(tt + 1) * 128], rhs=wdn[:, j, hf * 512:(hf + 1) * 512], start=(j == 0), stop=(j == NJ - 1)),
                                     r=[r_gated[j], r_wdn], w=[ry[hf]])
                        S.op("dve", lambda e, by=by, pb=pb, t=t: e.scalar_tensor_tensor(out=zf[pb], in0=X[:, t, :], scalar=ALPHA, in1=bank(by, 2), op0=ALU.mult, op1=ALU.add), r=[rX[t]] + ry, w=[r_zf[pb]])
                        layer_norm_tile(zf[pb], r_zf[pb], X[:, t, :], rX[t], 128, scf[pb], r_scf[pb], lnt, r_lnt)
                        if final:
                            out_toks.append(S.dma("sp", lambda e, t=t: e.dma_start(out=yp[t * 128:(t + 1) * 128, :], in_=X[:, t, :]), r=[rX[t]]))
                if with_sample:
                    bys, rys = ring.get(2)
                    for hf in range(2):
                        for j in range(NJ):
                            S.op("pe", lambda e, hf=hf, j=j, bys=bys: e.matmul(bank(bys + hf)[0:NS, :], lhsT=gated_s[:, j, :], rhs=wdn[:, j, hf * 512:(hf + 1) * 512], start=(j == 0), stop=(j == NJ - 1)), r=[r_gs, r_wdn], w=[rys[hf]])
                    S.op("dve", lambda e, bys=bys: e.scalar_tensor_tensor(out=zf[0][0:NS, :], in0=XS[:], scalar=ALPHA, in1=bank(bys, 2)[0:NS, :], op0=ALU.mult, op1=ALU.add), r=[r_XS] + rys, w=[r_zf[0]])
                    layer_norm_tile(zf[0][0:NS, :], r_zf[0], XS[:], r_XS, NS, scf[0], r_scf[0], lnt, r_lnt)
                    if final:
                        out_toks.append(S.dma("sp", lambda e: e.dma_start(out=ys, in_=XS[:]), r=[r_XS]))
                bc, rc = ring.get(1)
                for j in range(NJ):
                    S.op("pe", lambda e, j=j, bc=bc: e.transpose(out=bank(bc)[0:2, (j % 4) * 128:(j % 4 + 1) * 128], in_=cst[:, j, :], identity=ident_f[:]), r=[r_cst, r_idf], w=rc)
                    if j % 4 == 3 or j == NJ - 1:
                        j0 = j - j % 4
                        n = j - j0 + 1
                        S.op("act", lambda e, bc=bc, n=n: e.copy(out=cso[0:2, 0:n * 128], in_=bank(bc)[0:2, 0:n * 128]), r=rc, w=[r_cso])
                        out_toks.append(S.dma("sp", lambda e, j0=j0, n=n: e.dma_start(out=convp[layer][:, j0 * 128:(j0 + n) * 128], in_=cso[0:2, 0:n * 128]), r=[r_cso]))
                        if j != NJ - 1:
                            bc, rc = ring.get(1)


            def sample_l0():
                A.at(L0_BUF)
                xsT, r_xsT = A.get(128, BF16, "xsT"); xsT = xsT.rearrange("p (k t) -> p k t", k=8)
                tb, r_tb = A.get(68, F32, "stab")
                c2s = tb[:, 0:16]; sss = tb[:, 16:32]
                decTs = tb[0:4, 32:48]
                qdecs = tb[:, 48:64].rearrange("p (h l) -> p h l", h=4)
                kdecs = tb[0:4, 64:68]
                S.dma("sp", lambda e: e.dma_start(out=c2s, in_=tdr["c2s"]), w=[r_tb])
                S.dma("sp", lambda e: e.dma_start(out=sss, in_=tdr["sss"]), w=[r_tb])
                S.dma("sp", lambda e: e.dma_start(out=decTs, in_=tdr["decTs"].rearrange("m h l -> m (h l)")), w=[r_tb])
                S.dma("sp", lambda e: e.dma_start(out=qdecs, in_=tdr["qdecs"]), w=[r_tb])
                S.dma("sp", lambda e: e.dma_start(out=kdecs, in_=tdr["kdecs"]), w=[r_tb])
                sp_sb, r_sp = A.get(512, F32, "sp_sb")
                S.dma("sp", lambda e: e.dma_start(out=sp_sb[0:60, :], in_=spool), w=[r_sp])
                S0f, r_S0f = A.get(16 * 128, F32, "S0f"); S0f = S0f.rearrange("p (n v) -> p n v", n=16)
                S0b, r_S0b = A.get(16 * 128, BF16, "S0b"); S0b = S0b.rearrange("p (n v) -> p n v", n=16)
                S.dma("sp", lambda e: e.dma_start(out=S0f, in_=sret.rearrange("n k v -> k n v")), w=[r_S0f])
                S.op("pool", lambda e: e.tensor_copy(out=S0b, in_=S0f), r=[r_S0f], w=[r_S0b])
                for bb in range(SB):
                    out_toks.append(S.dma("sp", lambda e, bb=bb: e.dma_start(out=pools[bb * 15:bb * 15 + 11, :], in_=spool[bb * 15 + 4:bb * 15 + 15, :]), semkey="d2d"))
                transpose_x(XS[:], r_XS, xsT, r_xsT, npart=NS)
                bq_, rq_ = ring.get(1)
                pqk = bank(bq_)[:, 0:192].rearrange("p (g t) -> p g t", g=12)
                for g in range(12):
                    for k in range(8):
                        S.op("pe", lambda e, g=g, k=k: e.matmul(pqk[:, g, :], lhsT=w_in_sb[:, k, g * 128:(g + 1) * 128], rhs=xsT[:, k, :], start=(k == 0), stop=(k == 7)), r=[r_win, r_xsT], w=rq_)
                bu_, ru_ = ring.get(1)
                for k in range(8):
                    S.op("pe", lambda e, k=k: e.matmul(bank(bu_)[0:NS, :], lhsT=xsT[:, k, :], rhs=w_in_sb[:, k, 0:512], start=(k == 0), stop=(k == 7)), r=[r_win, r_xsT], w=ru_)
                utok, r_utok = A.get(512, F32, "utok")
                S.op("act", lambda e: e.copy(out=utok[0:NS, :], in_=bank(bu_)[0:NS, :]), r=ru_, w=[r_utok])
                for bb in range(SB):
                    out_toks.append(S.dma("sp", lambda e, bb=bb: e.dma_start(out=pools[bb * 15 + 11:bb * 15 + 15, :], in_=utok[bb * 4:bb * 4 + 4, :]), r=[r_utok]))
                bh_, rh_ = ring.get(1)
                for g in range(4):
                    S.op("pe", lambda e, g=g: e.transpose(out=bank(bh_)[:, g * 64:g * 64 + 60], in_=sp_sb[0:60, g * 128:(g + 1) * 128], identity=ident_f[0:60, 0:60]), r=[r_sp, r_idf], w=rh_)
                ue, r_ue = A.get(16 * 19, F32, "ues"); ue4 = ue.rearrange("p (g b r) -> p g b r", g=4, b=4); ue3 = ue.rearrange("p (n r) -> p n r", n=16)
                hv = bank(bh_)[:, 0:256].rearrange("p (g c) -> p g c", g=4)[:, :, 0:60].rearrange("p g (b r) -> p g b r", b=4)
                for g in range(4):
                    S.op("act", lambda e, g=g: e.copy(out=ue4[:, g, :, 0:15], in_=hv[:, g]), r=rh_, w=[r_ue])
                    S.op("act", lambda e, g=g: e.copy(out=ue4[:, g, :, 15:19], in_=pqk[:, g, :].rearrange("p (b l) -> p b l", b=4)), r=rq_, w=[r_ue])
                tAs, r_tAs = A.get(16 * 19, F32, "tAs"); tAs = tAs.rearrange("p (n r) -> p n r", n=16)
                tBs, r_tBs = A.get(16 * 19, F32, "tBs"); tBs = tBs.rearrange("p (n r) -> p n r", n=16)
                S.op("pool", lambda e: e.tensor_tensor(out=tAs[:, :, 1:19], in0=ue3[:, :, 1:19], in1=ue3[:, :, 0:18], op=ALU.add), r=[r_ue], w=[r_tAs])
                S.op("pool", lambda e: e.tensor_tensor(out=tBs[:, 4:16, 3:19], in0=tAs[:, 4:16, 3:19], in1=tAs[:, 4:16, 1:17], op=ALU.add), r=[r_tAs], w=[r_tBs])
                S.op("pool", lambda e: e.tensor_tensor(out=tAs[:, 8:16, 7:19], in0=tBs[:, 8:16, 7:19], in1=tBs[:, 8:16, 3:15], op=ALU.add), r=[r_tBs], w=[r_tAs])
                S.op("pool", lambda e: e.tensor_tensor(out=tBs[:, 12:16, 15:19], in0=tAs[:, 12:16, 15:19], in1=tAs[:, 12:16, 7:11], op=ALU.add), r=[r_tAs], w=[r_tBs])
                pooleds, r_pooleds = A.get(64, BF16, "pooleds"); pooleds = pooleds.rearrange("p (g b l) -> p g b l", g=4, b=4)
                fins = (tAs, tBs, tAs, tBs)
                for g in range(4):
                    S.op("dve", lambda e, g=g: e.scalar_tensor_tensor(out=pooleds[:, g], in0=fins[g][:, g * 4:(g + 1) * 4, 15:19], scalar=1.0 / POOL_W[g], in1=ue3[:, g * 4:(g + 1) * 4, 15:19], op0=ALU.mult, op1=ALU.subtract), r=[r_tAs, r_tBs, r_ue], w=[r_pooleds])
                catTs, r_catTs = A.get(128, BF16, "catTs"); catTs = catTs.rearrange("p (c t) -> p c t", c=8)
                bpm_, rpm_ = ring.get(1)
                pms = bank(bpm_)[:, 0:64].rearrange("p (g t) -> p g t", g=4)
                for g in range(4):
                    S.op("pe", lambda e, g=g: e.matmul(pms[:, g, :], lhsT=poolw_sb[:, g, :], rhs=pooleds[:, g].rearrange("p b l -> p (b l)"), start=True, stop=True), r=[r_poolw, r_pooleds], w=rpm_)
                for g in range(4):
                    S.op("act", lambda e, g=g: e.activation(out=catTs[:, g, :], in_=pms[:, g, :], func=AF.Identity, bias=0.0, scale=pscale[:, g:g + 1]), r=rpm_ + [r_c0], w=[r_catTs])
                qkfs, r_qkfs = A.get(128, F32, "qkfs"); qkfs = qkfs.rearrange("p (h t) -> p h t", h=8)
                rAs, r_rAs = A.get(128, F32, "rAs"); rAs = rAs.rearrange("p (h t) -> p h t", h=8)
                rBs, r_rBs = A.get(128, F32, "rBs"); rBs = rBs.rearrange("p (h t) -> p h t", h=8)
                qkTs, r_qkTs = A.get(128, BF16, "qkTs"); qkTs = qkTs.rearrange("p (h t) -> p h t", h=8)
                qdTs, r_qdTs = A.get(64, BF16, "qdTs"); qdTs = qdTs.rearrange("p (h t) -> p h t", h=4)
                S.op("act", lambda e: e.copy(out=qkfs, in_=pqk[:, 4:12, :]), r=rq_, w=[r_qkfs])
                S.op("dve", lambda e: e.tensor_tensor(out=rAs, in0=qkfs, in1=c2s.unsqueeze(1).to_broadcast([128, 8, 16]), op=ALU.mult), r=[r_qkfs, r_tb], w=[r_rAs])
                S.op("pool", lambda e: e.tensor_tensor(out=rBs[0:64], in0=qkfs[64:128], in1=sss[64:128].unsqueeze(1).to_broadcast([64, 8, 16]), op=ALU.mult), r=[r_qkfs, r_tb], w=[r_rBs])
                S.op("pool", lambda e: e.tensor_tensor(out=rBs[64:128], in0=qkfs[0:64], in1=sss[0:64].unsqueeze(1).to_broadcast([64, 8, 16]), op=ALU.mult), r=[r_qkfs, r_tb], w=[r_rBs])
                S.op("dve", lambda e: e.tensor_tensor(out=qkTs, in0=rAs, in1=rBs, op=ALU.add), r=[r_rAs, r_rBs], w=[r_qkTs])
                S.op("pool", lambda e: e.tensor_tensor(out=qdTs.rearrange("p h (b l) -> p h b l", b=4), in0=qkTs[:, 0:4, :].rearrange("p h (b l) -> p h b l", b=4), in1=qdecs.unsqueeze(2).to_broadcast([128, 4, 4, 4]), op=ALU.mult), r=[r_qkTs, r_tb], w=[r_qdTs])
                v_s, r_vs = A.get(SB * 512, BF16, "v_s"); v_s = v_s.rearrange("p (b n) -> p b n", b=SB)
                sTs, r_sTs = A.get(64, BF16, "sTs"); sTs = sTs.rearrange("p (b h l) -> p b h l", b=4, h=4)
                kps, r_kps = A.get(512, BF16, "kps"); kps = kps.rearrange("p (h d) -> p h d", h=4)
                S.op("pool", lambda e: e.memset(v_s, 0.0), w=[r_vs])
                S.op("pool", lambda e: e.memset(sTs, 0.0), w=[r_sTs])
                S.op("pool", lambda e: e.memset(kps, 0.0), w=[r_kps])
                sggs, r_sggs = A.get(SB * 512, F32, "sggs"); sggs = sggs.rearrange("p (b n) -> p b n", b=SB)
                for bb in range(SB):
                    bv_, rv_ = ring.get(1)
                    bg_, rg_ = ring.get(1)
                    for (bx, rx, c0) in ((bv_, rv_, 1536), (bg_, rg_, 2048)):
                        for k in range(8):
                            S.op("pe", lambda e, bx=bx, k=k, c0=c0, bb=bb: e.matmul(bank(bx)[0:4, :], lhsT=xsT[:, k, bb * 4:bb * 4 + 4], rhs=w_in_sb[:, k, c0:c0 + 512], start=(k == 0), stop=(k == 7)), r=[r_win, r_xsT], w=rx)
                    S.op("act", lambda e, bv_=bv_, bb=bb: e.copy(out=v_s[0:4, bb, :], in_=bank(bv_)[0:4, :]), r=rv_, w=[r_vs])
                    S.op("act", lambda e, bg_=bg_, bb=bb: e.activation(out=sggs[0:4, bb, :], in_=bank(bg_)[0:4, :], func=AF.Silu), r=rg_, w=[r_sggs])
                    S.op("pool", lambda e, bb=bb: e.tensor_tensor(out=sggs[0:4, bb, :], in0=sggs[0:4, bb, :], in1=gng[0:4], op=ALU.mult), r=[r_sggs, r_c0], w=[r_sggs])
                bs_, rs_ = ring.get(1)
                for bb in range(SB):
                    for h in range(4):
                        c0 = (bb * 4 + h) * 4
                        S.op("pe", lambda e, bb=bb, h=h, c0=c0: e.matmul(bank(bs_)[0:4, c0:c0 + 4], lhsT=qkTs[:, 4 + h, bb * 4:bb * 4 + 4], rhs=qkTs[:, h, bb * 4:bb * 4 + 4], start=True, stop=True), r=[r_qkTs], w=rs_)
                S.op("dve", lambda e: e.tensor_tensor(out=sTs[0:4].rearrange("p b h l -> p b (h l)"), in0=bank(bs_)[0:4, 0:64].rearrange("p (b n) -> p b n", b=4), in1=decTs.unsqueeze(1).to_broadcast([4, 4, 16]), op=ALU.mult), r=rs_ + [r_tb], w=[r_sTs])
                Snew, r_Snew = dbl(512, F32, "Snew", "p (h v) -> p h v", h=4)
                gsts, r_gsts = A.get(40, F32, "gsts")
                on1s, r_on1s = A.get(512, F32, "on1s")
                rets_bf, r_retsb = A.get(512, BF16, "retsb")
                for bb in range(SB):
                    bo_, ro_ = ring.get(1)
                    ovs = bank(bo_)[0:4, :].rearrange("p (h t) -> p h t", h=4)
                    for h in range(4):
                        S.op("pe", lambda e, h=h, bb=bb, ovs=ovs: e.matmul(ovs[:, h, :], lhsT=sTs[:, bb, h, :], rhs=v_s[:, bb, h * 128:(h + 1) * 128], start=True, stop=False), r=[r_sTs, r_vs], w=ro_)
                        S.op("pe", lambda e, h=h, bb=bb, ovs=ovs: e.matmul(ovs[:, h, :], lhsT=qdTs[:, h, bb * 4:bb * 4 + 4], rhs=S0b[:, bb * 4 + h, :], start=False, stop=True), r=[r_qdTs, r_S0b], w=ro_)
                    bkt_, rkt_ = ring.get(1)
                    ktvs = bankbf(bkt_)[0:4, 0:512].rearrange("p (h t) -> p h t", h=4)
                    for h in range(4):
                        S.op("pe", lambda e, h=h, bb=bb, ktvs=ktvs: e.transpose(out=ktvs[:, h, :], in_=qkTs[:, 4 + h, bb * 4:bb * 4 + 4], identity=ident_b[:]), r=[r_qkTs, r_idb], w=rkt_)
                    for h in range(4):
                        S.op("act", lambda e, h=h, ktvs=ktvs: e.activation(out=kps[0:4, h, :], in_=ktvs[:, h, :], func=AF.Identity, bias=0.0, scale=kdecs[:, h:h + 1]), r=rkt_ + [r_tb], w=[r_kps])
                    bds_, rds_ = ring.get(1)
                    dss = bank(bds_).rearrange("p (h t) -> p h t", h=4)
                    for h in range(4):
                        S.op("pe", lambda e, h=h, bb=bb, dss=dss: e.matmul(dss[:, h, :], lhsT=kps[:, h, :], rhs=v_s[:, bb, h * 128:(h + 1) * 128], start=True, stop=True), r=[r_kps, r_vs], w=rds_)
                    sn = Snew[bb % 2]
                    for h in range(4):
                        S.op("dve", lambda e, h=h, bb=bb, dss=dss, sn=sn: e.scalar_tensor_tensor(out=sn[:, h, :], in0=S0f[:, bb * 4 + h, :], scalar=tabs["gCs"][h], in1=dss[:, h, :], op0=ALU.mult, op1=ALU.add), r=rds_ + [r_S0f], w=[r_Snew[bb % 2]])
                    out_toks.append(S.dma("sp", lambda e, bb=bb, sn=sn: e.dma_start(out=rets[bb * 4:(bb + 1) * 4].rearrange("h k v -> k h v"), in_=sn), r=[r_Snew[bb % 2]]))
                    for h in range(4):
                        S.op("dve", lambda e, h=h, ovs=ovs: e.bn_stats(out=gsts[0:4, h * 6:(h + 1) * 6], in_=ovs[:, h, :]), r=ro_, w=[r_gsts])
                        S.op("dve", lambda e, h=h: e.bn_aggr(out=gsts[0:4, 24 + 2 * h:26 + 2 * h], in_=gsts[0:4, h * 6:(h + 1) * 6]), r=[r_gsts], w=[r_gsts])
                    mvs = gsts[0:4, 24:32].rearrange("p (h two) -> p h two", two=2)
                    S.op("act", lambda e: e.activation(out=gsts[0:4, 32:36], in_=mvs[:, :, 1], func=AF.Sqrt, bias=eps_gn[0:4], scale=1.0), r=[r_gsts, r_eps], w=[r_gsts])
                    S.op("dve", lambda e: e.reciprocal(out=gsts[0:4, 32:36], in_=gsts[0:4, 32:36]), r=[r_gsts], w=[r_gsts])
                    S.op("dve", lambda e: e.scalar_tensor_tensor(out=gsts[0:4, 36:40], in0=mvs[:, :, 0], scalar=-1.0, in1=gsts[0:4, 32:36], op0=ALU.mult, op1=ALU.mult), r=[r_gsts], w=[r_gsts])
                    for h in range(4):
                        S.op("act", lambda e, h=h, ovs=ovs: e.activation(out=on1s[0:4, h * 128:(h + 1) * 128], in_=ovs[:, h, :], func=AF.Identity, bias=gsts[0:4, 36 + h:37 + h], scale=gsts[0:4, 32 + h:33 + h]), r=ro_ + [r_gsts], w=[r_on1s])
                    S.op("pool", lambda e, bb=bb: e.tensor_tensor(out=rets_bf[0:4, :], in0=on1s[0:4, :], in1=sggs[0:4, bb, :], op=ALU.mult), r=[r_on1s, r_sggs], w=[r_retsb])
                    brt_, rrt_ = ring.get(1)
                    rtvs = bankbf(brt_)[:, 0:16].rearrange("p (h t) -> p h t", h=4)
                    for h in range(4):
                        S.op("pe", lambda e, h=h, rtvs=rtvs: e.transpose(out=rtvs[:, h, :], in_=rets_bf[0:4, h * 128:(h + 1) * 128], identity=ident_b[0:4, 0:4]), r=[r_retsb, r_idb], w=rrt_)
                    S.op("dve", lambda e, rtvs=rtvs, bb=bb: e.tensor_copy(out=catTs[:, 4:8, bb * 4:bb * 4 + 4], in_=rtvs), r=rrt_, w=[r_catTs])
                by_, ry_ = ring.get(2)
                for hf in range(2):
                    for c in range(8):
                        S.op("pe", lambda e, hf=hf, c=c: e.matmul(bank(by_ + hf)[0:NS, :], lhsT=catTs[:, c, :], rhs=w_o_sb[:, c, hf * 512:(hf + 1) * 512], start=(c == 0), stop=(c == 7)), r=[r_catTs, r_wo], w=[ry_[hf]])
                zs, r_zs = A.get(D, F32, "zs")
                scs, r_scs = A.get(16, F32, "scs")
                S.op("dve", lambda e: e.scalar_tensor_tensor(out=zs[0:NS, :], in0=XS[:], scalar=ALPHA, in1=bank(by_, 2)[0:NS, :], op0=ALU.mult, op1=ALU.add), r=[r_XS] + ry_, w=[r_zs])
                layer_norm_tile(zs[0:NS, :], r_zs, XS[:], r_XS, NS, scs, r_scs, lnt0, r_lnt0)

            if with_sample:
                sample_l0()
            chk('l0')
            ffn(0, False)
            chk('ffn0')

            A.at(0)
            wdq_sb, r_wdq = A.get(8 * 512, BF16, "wdq"); wdq_sb = wdq_sb.rearrange("p (k n) -> p k n", k=8)
            wuq_sb, r_wuq = A.get(4 * 1536, BF16, "wuq"); wuq_sb = wuq_sb.rearrange("p (k n) -> p k n", k=4)
            wdkv_sb, r_wdkv = A.get(8 * 320, BF16, "wdkv"); wdkv_sb = wdkv_sb.rearrange("p (k n) -> p k n", k=8)
            wukT_sb, r_wukT = A.get(8 * 256, BF16, "wukT"); wukT_sb = wukT_sb.rearrange("p (h c) -> p h c", h=8)
            wuv_sb, r_wuv = A.get(2 * 1024, BF16, "wuv"); wuv_sb = wuv_sb.rearrange("p (k n) -> p k n", k=2)
            wo1_sb, r_wo1 = A.get(8 * 1024, BF16, "wo1"); wo1_sb = wo1_sb.rearrange("p (k n) -> p k n", k=8)
            c1buf, r_c1 = A.get(512 + 256, F32, "c1")
            qng = c1buf[:, 0:512]
            kvng = c1buf[:, 512:768]
            lnt1, r_lnt1 = A.get(2 * D, F32, "lnt1")
            ckvT, r_ckvT = A.get(2 * SEQ, BF16, "ckvT", nres=NT); ckvT = ckvT.rearrange("p (k n) -> p k n", k=2)
            ckvK, r_ckvK = A.get(NT * 256, BF16, "ckvK", nres=NT); ckvK = ckvK.rearrange("p (t c) -> p t c", t=NT)
            kpeT, r_kpeT = A.get(SEQ, BF16, "kpeT", nres=NT)
            S.dma("pool", lambda e: e.dma_start(out=wdq_sb, in_=w_dq.rearrange("(k p) n -> p k n", p=128)), w=[r_wdq])
            S.dma("pool", lambda e: e.dma_start(out=wdkv_sb, in_=w_dkv.rearrange("(k p) n -> p k n", p=128)), w=[r_wdkv])
            S.dma("pool", lambda e: e.dma_start(out=wuq_sb, in_=w_uq.rearrange("(k p) n -> p k n", p=128)), w=[r_wuq])
            S.dma("pool", lambda e: e.dma_start(out=wukT_sb, in_=w_ukT), w=[r_wukT])
            S.dma("pool", lambda e: e.dma_start(out=wuv_sb, in_=w_uv.rearrange("(k p) n -> p k n", p=128)), w=[r_wuv])
            S.dma("pool", lambda e: e.dma_start(out=wo1_sb, in_=w_o1.rearrange("(k p) n -> p k n", p=128)), w=[r_wo1])
            S.dma("sp", lambda e: e.dma_start(out=qng, in_=qn_g.partition_broadcast(128)), w=[r_c1])
            S.dma("sp", lambda e: e.dma_start(out=kvng, in_=kvn_g.partition_broadcast(128)), w=[r_c1])
            load_ln(lnt1, r_lnt1, ln_mix_g[1], ln_mix_b[1])
            L1_BUF = A.cur

            xT1, r_xT1 = dbl(1024, BF16, "xT1", "p (k t) -> p k t", k=8)
            rt1, r_rt1 = dbl(128, F32, "rt1")
            sc1, r_sc1 = dbl(16, F32, "sc1")
            cqn, r_cqn = dbl(512, BF16, "cqn")
            cqnT, r_cqnT = dbl(512, BF16, "cqnT", "p (k t) -> p k t", k=4)
            ckv_f, r_ckvf = dbl(256, F32, "ckv_f")
            kpe_f, r_kpef = dbl(64, F32, "kpe_f")
            kpe_t, r_kpet = A.get(128, F32, "kpe_t")
            kpe_b, r_kpeb = dbl(128, BF16, "kpe_b")
            qnT0_, r_qnT0_ = A.get(1024, BF16, "qnT"); qnT0_ = qnT0_.rearrange("p (h t) -> p h t", h=8); qnT = [qnT0_, qnT0_]; r_qnT = [r_qnT0_, r_qnT0_]
            qpe_t, r_qpet = A.get(1024, F32, "qpe_t")
            junk, r_junk = qpe_t[:, 0:512], r_qpet
            qpe_b, r_qpeb = dbl(512, BF16, "qpe_b")
            qpeT, r_qpeT = dbl(1024, BF16, "qpeT", "p (h t) -> p h t", h=8)
            qlT, r_qlT = dbl(2048, BF16, "qlT", "p (h k t) -> p h k t", h=8, k=2)
            p_sb, r_p = dbl(SEQ, BF16, "p_sb")
            pT_sb, r_pT = dbl(SEQ, BF16, "pT_sb")
            st1, r_st1 = dbl(8, F32, "st1")
            ol_b, r_olb = dbl(256, BF16, "ol_b")
            olT, r_olT = dbl(256, BF16, "olT", "p (k t) -> p k t", k=2)
            oT10_, r_oT10_ = A.get(1024, BF16, "oT1"); oT10_ = oT10_.rearrange("p (h t) -> p h t", h=8); oT1 = [oT10_, oT10_]; r_oT1 = [r_oT10_, r_oT10_]
            z10, r_z10 = A.get(D, F32, "z1"); z1 = [z10, z10]; r_z1 = [r_z10, r_z10]
            print("layer1 arena bytes", A.cur)

            for i_ in range(2):
                S.op("pool", lambda e, i_=i_: e.memset(qpeT[i_], 0.0), w=[r_qpeT[i_]])
            def l1A(t):
                pb = t % 2
                tok = slice(t * 128, (t + 1) * 128)
                S.dma("sp", lambda e, pb=pb, t=t: e.dma_start(out=rt1[pb][:, 0:64], in_=tdr["cc1"][:, t, :]), w=[r_rt1[pb]])
                S.dma("sp", lambda e, pb=pb, t=t: e.dma_start(out=rt1[pb][:, 64:128], in_=tdr["ss1"][:, t, :]), w=[r_rt1[pb]])
                transpose_x(X[:, t, :], rX[t], xT1[pb], r_xT1[pb])
                bcq, rcq = ring.get(1)
                bkv, rkv = ring.get(1)
                for k in range(8):
                    S.op("pe", lambda e, k=k, pb=pb, bcq=bcq: e.matmul(bank(bcq), lhsT=xT1[pb][:, k, :], rhs=wdq_sb[:, k, :], start=(k == 0), stop=(k == 7)), r=[r_xT1[pb], r_wdq], w=rcq)
                for k in range(8):
                    S.op("pe", lambda e, k=k, pb=pb, bkv=bkv: e.matmul(bank(bkv)[:, 0:320], lhsT=xT1[pb][:, k, :], rhs=wdkv_sb[:, k, :], start=(k == 0), stop=(k == 7)), r=[r_xT1[pb], r_wdkv], w=rkv)
                sc = sc1[pb]
                S.op("act", lambda e, bcq=bcq, sc=sc: e.activation(out=junk, in_=bank(bcq), func=AF.Square, accum_out=sc[:, 0:1]), r=rcq, w=[r_junk, r_sc1[pb]])
                S.op("act", lambda e, sc=sc: e.activation(out=sc[:, 1:2], in_=sc[:, 0:1], func=AF.Sqrt, bias=eps_rms, scale=1.0 / 512), r=[r_sc1[pb], r_eps], w=[r_sc1[pb]])
                S.op("dve", lambda e, sc=sc: e.reciprocal(out=sc[:, 1:2], in_=sc[:, 1:2]), r=[r_sc1[pb]], w=[r_sc1[pb]])
                S.op("dve", lambda e, bcq=bcq, sc=sc, pb=pb: e.scalar_tensor_tensor(out=cqn[pb], in0=bank(bcq), scalar=sc[:, 1:2], in1=qng, op0=ALU.mult, op1=ALU.mult), r=rcq + [r_sc1[pb], r_c1], w=[r_cqn[pb]])
                S.op("act", lambda e, bkv=bkv, sc=sc: e.activation(out=junk[:, 0:256], in_=bank(bkv)[:, 0:256], func=AF.Square, accum_out=sc[:, 2:3]), r=rkv, w=[r_junk, r_sc1[pb]])
                S.op("act", lambda e, sc=sc: e.activation(out=sc[:, 3:4], in_=sc[:, 2:3], func=AF.Sqrt, bias=eps_rms, scale=1.0 / 256), r=[r_sc1[pb], r_eps], w=[r_sc1[pb]])
                S.op("dve", lambda e, sc=sc: e.reciprocal(out=sc[:, 3:4], in_=sc[:, 3:4]), r=[r_sc1[pb]], w=[r_sc1[pb]])
                S.op("dve", lambda e, bkv=bkv, sc=sc, pb=pb: e.scalar_tensor_tensor(out=ckv_f[pb], in0=bank(bkv)[:, 0:256], scalar=sc[:, 3:4], in1=kvng, op0=ALU.mult, op1=ALU.mult), r=rkv + [r_sc1[pb], r_c1], w=[r_ckvf[pb]])
                out_toks.append(S.dma("sp", lambda e, pb=pb, tok=tok: e.dma_start(out=ckvp[tok, :], in_=ckv_f[pb]), r=[r_ckvf[pb]]))
                S.op("pool", lambda e, pb=pb, t=t: e.tensor_copy(out=ckvK[:, t, :], in_=ckv_f[pb]), r=[r_ckvf[pb]], w=[r_ckvK[t]])
                S.op("dve", lambda e, bkv=bkv, pb=pb: e.tensor_tensor(out=kpe_t[:, 0:64], in0=bank(bkv)[:, 256:320], in1=rt1[pb][:, 0:64], op=ALU.mult), r=rkv + [r_rt1[pb]], w=[r_kpet])
                S.op("dve", lambda e, bkv=bkv, pb=pb: e.tensor_tensor(out=kpe_t[:, 64:96], in0=bank(bkv)[:, 288:320], in1=rt1[pb][:, 64:96], op=ALU.mult), r=rkv + [r_rt1[pb]], w=[r_kpet])
                S.op("dve", lambda e, bkv=bkv, pb=pb: e.tensor_tensor(out=kpe_t[:, 96:128], in0=bank(bkv)[:, 256:288], in1=rt1[pb][:, 96:128], op=ALU.mult), r=rkv + [r_rt1[pb]], w=[r_kpet])
                S.op("pool", lambda e, pb=pb: e.tensor_tensor(out=kpe_f[pb], in0=kpe_t[:, 0:64], in1=kpe_t[:, 64:128], op=ALU.add), r=[r_kpet], w=[r_kpef[pb]])
                out_toks.append(S.dma("sp", lambda e, pb=pb, tok=tok: e.dma_start(out=kpep[tok, :], in_=kpe_f[pb]), r=[r_kpef[pb]]))
                S.op("pool", lambda e, pb=pb: e.tensor_copy(out=kpe_b[pb][:, 0:64], in_=kpe_f[pb]), r=[r_kpef[pb]], w=[r_kpeb[pb]])
                S.op("pool", lambda e, pb=pb: e.tensor_copy(out=kpe_b[pb][:, 64:128], in_=kpe_f[pb]), r=[r_kpef[pb]], w=[r_kpeb[pb]])
                btr, rtr = ring.get(2)
                trv = bankbf(btr, 2)
                for k in range(4):
                    S.op("pe", lambda e, k=k, trv=trv, pb=pb: e.transpose(out=trv[:, k * 128:(k + 1) * 128], in_=cqn[pb][:, k * 128:(k + 1) * 128], identity=ident_b[:]), r=[r_cqn[pb], r_idb], w=rtr)
                for k in range(2):
                    S.op("pe", lambda e, k=k, trv=trv, t=t: e.transpose(out=trv[:, 1024 + k * 128:1024 + (k + 1) * 128], in_=ckvK[:, t, k * 128:(k + 1) * 128], identity=ident_b[:]), r=[r_ckvK[t], r_idb], w=rtr)
                S.op("pe", lambda e, trv=trv, pb=pb: e.transpose(out=trv[:, 1280:1408], in_=kpe_b[pb], identity=ident_b[:]), r=[r_kpeb[pb], r_idb], w=rtr)
                S.op("act", lambda e, trv=trv, pb=pb: e.copy(out=cqnT[pb], in_=trv[:, 0:512].rearrange("p (k t) -> p k t", k=4)), r=rtr, w=[r_cqnT[pb]])
                S.op("dve", lambda e, trv=trv, tok=tok, t=t: e.tensor_copy(out=ckvT[:, :, tok], in_=trv[:, 1024:1280].rearrange("p (k t) -> p k t", k=2)), r=rtr, w=[r_ckvT[t]])
                S.op("dve", lambda e, trv=trv, tok=tok, t=t: e.tensor_copy(out=kpeT[:, tok], in_=trv[:, 1280:1408]), r=rtr, w=[r_kpeT[t]])
                for hh in range(2):
                    bqn, rqn = ring.get(1)
                    qv = bank(bqn).rearrange("p (h t) -> p h t", h=4)
                    for h4 in range(4):
                        h = hh * 4 + h4
                        for k in range(4):
                            S.op("pe", lambda e, qv=qv, h4=h4, h=h, k=k, pb=pb: e.matmul(qv[:, h4, :], lhsT=wuq_sb[:, k, h * 128:(h + 1) * 128], rhs=cqnT[pb][:, k, :], start=(k == 0), stop=(k == 3)), r=[r_wuq, r_cqnT[pb]], w=rqn)
                    evac("act" if hh == 0 else "dve", qnT[pb][:, hh * 4:hh * 4 + 4, :], qv, rqn, [r_qnT[pb]])
                bqp, rqp = ring.get(1)
                for k in range(4):
                    S.op("pe", lambda e, k=k, bqp=bqp, pb=pb: e.matmul(bank(bqp), lhsT=cqnT[pb][:, k, :], rhs=wuq_sb[:, k, 1024:1536], start=(k == 0), stop=(k == 3)), r=[r_wuq, r_cqnT[pb]], w=rqp)
                qpv = bank(bqp).rearrange("p (h d) -> p h d", h=8)
                qA = qpe_t[:, 0:512].rearrange("p (h d) -> p h d", h=8)
                qB = qpe_t[:, 512:1024].rearrange("p (h d) -> p h d", h=8)
                ccb = rt1[pb][:, 0:64].unsqueeze(1)
                ssb1 = rt1[pb][:, 64:96].unsqueeze(1)
                ssb2 = rt1[pb][:, 96:128].unsqueeze(1)
                S.op("dve", lambda e, qpv=qpv, ccb=ccb: e.tensor_tensor(out=qA, in0=qpv, in1=ccb.to_broadcast([128, 8, 64]), op=ALU.mult), r=rqp + [r_rt1[pb]], w=[r_qpet])
                S.op("dve", lambda e, qpv=qpv, ssb1=ssb1: e.tensor_tensor(out=qB[:, :, 0:32], in0=qpv[:, :, 32:64], in1=ssb1.to_broadcast([128, 8, 32]), op=ALU.mult), r=rqp + [r_rt1[pb]], w=[r_qpet])
                S.op("dve", lambda e, qpv=qpv, ssb2=ssb2: e.tensor_tensor(out=qB[:, :, 32:64], in0=qpv[:, :, 0:32], in1=ssb2.to_broadcast([128, 8, 32]), op=ALU.mult), r=rqp + [r_rt1[pb]], w=[r_qpet])
                S.op("pool", lambda e, pb=pb: e.tensor_tensor(out=qpe_b[pb], in0=qpe_t[:, 0:512], in1=qpe_t[:, 512:1024], op=ALU.add), r=[r_qpet], w=[r_qpeb[pb]])
                btq, rtq = ring.get(1)
                tqv = bankbf(btq)[:, 0:512].rearrange("p (a t) -> p a t", a=4)
                for a in range(4):
                    S.op("pe", lambda e, a=a, tqv=tqv, pb=pb: e.transpose(out=tqv[:, a, :], in_=qpe_b[pb][:, a * 128:(a + 1) * 128], identity=ident_b[:]), r=[r_qpeb[pb], r_idb], w=rtq)
                qz = qpeT[pb].rearrange("p (a two) t -> p a two t", two=2)
                S.op("act", lambda e, tqv=tqv, qz=qz: e.copy(out=qz[0:64, :, 0, :], in_=tqv[0:64]), r=rtq, w=[r_qpeT[pb]])
                S.op("dve", lambda e, tqv=tqv, qz=qz: e.tensor_copy(out=qz[64:128, :, 1, :], in_=tqv[64:128]), r=rtq, w=[r_qpeT[pb]])
                for hh in range(4):
                    bql, rql = ring.get(1)
                    qlv = bank(bql).rearrange("p (h k t) -> p h k t", h=2, k=2)
                    for h2 in range(2):
                        h = hh * 2 + h2
                        for k in range(2):
                            S.op("pe", lambda e, qlv=qlv, h2=h2, h=h, k=k, pb=pb: e.matmul(qlv[:, h2, k, :], lhsT=wukT_sb[:, h, k * 128:(k + 1) * 128], rhs=qnT[pb][:, h, :], start=True, stop=True), r=[r_wukT, r_qnT[pb]], w=rql)
                    evac("act" if hh % 2 == 0 else "dve", qlT[pb][:, hh * 2:hh * 2 + 2, :, :], qlv, rql, [r_qlT[pb]])

            l1ctx = {}
            def l1P1a(t, h):
                pb = t % 2
                hp = h % 2
                nk = (t + 1) * 128
                nb = (nk + 511) // 512
                nbanks = 1 if nb == 1 else (2 if nb == 2 else 4)
                bs, rs = ring.get(nbanks)
                sfull = bank(bs, nbanks)
                for gi in range(nb):
                    k0 = gi * 512
                    kn = min(512, nk - k0)
                    ks = slice(k0, k0 + kn)
                    kres = r_ckvT[gi * 4:gi * 4 + (kn + 127) // 128]
                    kpres = r_kpeT[gi * 4:gi * 4 + (kn + 127) // 128]
                    last_diag = (gi == nb - 1)
                    S.op("pe", lambda e, sfull=sfull, ks=ks, h=h, pb=pb: e.matmul(sfull[:, ks], lhsT=qlT[pb][:, h, 0, :], rhs=ckvT[:, 0, ks], start=True, stop=False), r=[r_qlT[pb]] + kres, w=[rs[gi]])
                    S.op("pe", lambda e, sfull=sfull, ks=ks, h=h, pb=pb: e.matmul(sfull[:, ks], lhsT=qlT[pb][:, h, 1, :], rhs=ckvT[:, 1, ks], start=False, stop=False), r=[r_qlT[pb]] + kres, w=[rs[gi]])
                    S.op("pe", lambda e, sfull=sfull, ks=ks, h=h, pb=pb, last_diag=last_diag: e.matmul(sfull[:, ks], lhsT=qpeT[pb][:, h, :], rhs=kpeT[:, ks], start=False, stop=(not last_diag)), r=[r_qpeT[pb]] + kpres, w=[rs[gi]])
                    if last_diag:
                        S.op("pe", lambda e, sfull=sfull, nk=nk: e.matmul(sfull[:, nk - 128:nk], lhsT=ident_b[:], rhs=maskd[:], start=False, stop=True), r=[r_idb, r_maskd], w=[rs[gi]])
                stt = st1[hp]
                chs = [(c0_, min(nk, c0_ + 1024)) for c0_ in range(0, nk, 1024)]
                for ci, (a_, b_) in enumerate(chs):
                    S.op("dve", lambda e, sfull=sfull, a_=a_, b_=b_, ci=ci, stt=stt: e.reduce_max(out=stt[:, 4 + ci:5 + ci], in_=sfull[:, a_:b_], axis=AX.X), r=rs[0:nb], w=[r_st1[hp]])
                if len(chs) == 2:
                    S.op("dve", lambda e, stt=stt: e.tensor_tensor(out=stt[:, 4:5], in0=stt[:, 4:5], in1=stt[:, 5:6], op=ALU.max), r=[r_st1[hp]], w=[r_st1[hp]])
                S.op("dve", lambda e, stt=stt: e.tensor_scalar(out=stt[:, 1:2], in0=stt[:, 4:5], scalar1=-MLA_SCALE, scalar2=None, op0=ALU.mult), r=[r_st1[hp]], w=[r_st1[hp]])
                l1ctx[(t, h)] = (sfull, rs, nb, chs)

            def l1P1b(t, h):
                pb = t % 2
                hp = h % 2
                nk = (t + 1) * 128
                stt = st1[hp]
                sfull, rs, nb, chs = l1ctx[(t, h)]
                for ci, (a_, b_) in enumerate(chs):
                    S.op("act", lambda e, sfull=sfull, a_=a_, b_=b_, ci=ci, stt=stt, hp=hp: e.activation(out=p_sb[hp][:, a_:b_], in_=sfull[:, a_:b_], func=AF.Exp, bias=stt[:, 1:2], scale=MLA_SCALE, accum_out=stt[:, 6 + ci:7 + ci]), r=rs[0:nb] + [r_st1[hp]], w=[r_p[hp], r_st1[hp]])
                if len(chs) == 2:
                    S.op("dve", lambda e, stt=stt: e.tensor_tensor(out=stt[:, 6:7], in0=stt[:, 6:7], in1=stt[:, 7:8], op=ALU.add), r=[r_st1[hp]], w=[r_st1[hp]])
                S.op("dve", lambda e, stt=stt: e.reciprocal(out=stt[:, 3:4], in_=stt[:, 6:7]), r=[r_st1[hp]], w=[r_st1[hp]])


            def l1P2a(t, h):
                pb = t % 2
                hp = h % 2
                nk = (t + 1) * 128
                stt = st1[hp]
                nkt = t + 1
                for g0 in range(0, nkt, 4):
                    bpt, rpt = ring.get(1)
                    ptv = bankbf(bpt)
                    n_ = min(4, nkt - g0)
                    for i_ in range(n_):
                        S.op("pe", lambda e, i_=i_, g0=g0, ptv=ptv, hp=hp: e.transpose(out=ptv[:, i_ * 128:(i_ + 1) * 128], in_=p_sb[hp][:, (g0 + i_) * 128:(g0 + i_ + 1) * 128], identity=ident_b[:]), r=[r_p[hp], r_idb], w=rpt)
                    evac("act" if g0 < 8 else "dve", pT_sb[hp][:, g0 * 128:(g0 + n_) * 128], ptv[:, 0:n_ * 128], rpt, [r_pT[hp]])

            def l1P2b(t, h):
                pb = t % 2
                hp = h % 2
                nk = (t + 1) * 128
                stt = st1[hp]
                nkt = t + 1
                bol, rol = ring.get(1)
                for kt in range(nkt):
                    S.op("pe", lambda e, kt=kt, bol=bol, hp=hp: e.matmul(bank(bol)[:, 0:256], lhsT=pT_sb[hp][:, kt * 128:(kt + 1) * 128], rhs=ckvK[:, kt, :], start=(kt == 0), stop=(kt == nkt - 1)), r=[r_pT[hp], r_ckvK[kt]], w=rol)
                S.op("act", lambda e, bol=bol, stt=stt, hp=hp: e.activation(out=ol_b[hp], in_=bank(bol)[:, 0:256], func=AF.Identity, bias=0.0, scale=stt[:, 3:4]), r=rol + [r_st1[hp]], w=[r_olb[hp]])
                bot, rot = ring.get(1)
                otv = bankbf(bot)[:, 0:256].rearrange("p (k t) -> p k t", k=2)
                for k in range(2):
                    S.op("pe", lambda e, k=k, otv=otv, hp=hp: e.transpose(out=otv[:, k, :], in_=ol_b[hp][:, k * 128:(k + 1) * 128], identity=ident_b[:]), r=[r_olb[hp], r_idb], w=rot)
                S.op("dve", lambda e, otv=otv, hp=hp: e.tensor_copy(out=olT[hp], in_=otv), r=rot, w=[r_olT[hp]])
                boh, roh = ring.get(1)
                for k in range(2):
                    S.op("pe", lambda e, k=k, boh=boh, h=h, hp=hp: e.matmul(bank(boh)[:, 0:128], lhsT=wuv_sb[:, k, h * 128:(h + 1) * 128], rhs=olT[hp][:, k, :], start=(k == 0), stop=(k == 1)), r=[r_wuv, r_olT[hp]], w=roh)
                evac("act" if h % 2 == 0 else "dve", oT1[pb][:, h, :], bank(boh)[:, 0:128], roh, [r_oT1[pb]])


            def l1out(t):
                pb = t % 2
                by, ry = ring.get(2)
                for hf in range(2):
                    for c in range(8):
                        S.op("pe", lambda e, hf=hf, c=c, by=by, pb=pb: e.matmul(bank(by + hf), lhsT=oT1[pb][:, c, :], rhs=wo1_sb[:, c, hf * 512:(hf + 1) * 512], start=(c == 0), stop=(c == 7)), r=[r_oT1[pb], r_wo1], w=[ry[hf]])
                S.op("dve", lambda e, by=by, pb=pb, t=t: e.scalar_tensor_tensor(out=z1[pb], in0=X[:, t, :], scalar=ALPHA, in1=bank(by, 2), op0=ALU.mult, op1=ALU.add), r=[rX[t]] + ry, w=[r_z1[pb]])
                layer_norm_tile(z1[pb], r_z1[pb], X[:, t, :], rX[t], 128, sc1[pb], r_sc1[pb], lnt1, r_lnt1)


            l1A(0)
            for t in range(NT):
                l1P1a(t, 0)
                l1P1b(t, 0)
                for h in range(1, 8):
                    l1P1a(t, h)
                    l1P2a(t, h - 1)
                    l1P1b(t, h)
                    l1P2b(t, h - 1)
                    if h == 4 and t + 1 < NT:
                        l1A(t + 1)
                l1P2a(t, 7)
                l1P2b(t, 7)
                l1out(t)
                chk(f'l1t{t}')

            def sample_l1():
                A.at(L1_BUF)
                NSTEP = NPAGES * 128 // 1024
                tb1, r_tb1 = A.get(128, F32, "tb1")
                cc1s = tb1[0:NS, 0:64]; ss1s = tb1[0:NS, 64:128]
                S.dma("sp", lambda e: e.dma_start(out=cc1s, in_=tdr["cc1s"]), w=[r_tb1])
                S.dma("sp", lambda e: e.dma_start(out=ss1s, in_=tdr["ss1s"]), w=[r_tb1])
                msk, r_msk = A.get(SB * NS, BF16, "msk")
                S.dma("pool", lambda e: e.dma_start(out=msk[0:32, :], in_=tdr["masksb"]), w=[r_msk])
                mskv = msk.rearrange("p (b t) -> p b t", b=SB)
                pt_sb, r_pt = A.get(SB, I32, "pt_sb")
                S.dma("sp", lambda e: e.dma_start(out=pt_sb, in_=ptab), w=[r_pt])
                xsT1, r_xsT1 = A.get(128, BF16, "xsT1"); xsT1 = xsT1.rearrange("p (k t) -> p k t", k=8)
                transpose_x(XS[:], r_XS, xsT1, r_xsT1, npart=NS)
                bcq, rcq = ring.get(1)
                bkv, rkv = ring.get(1)
                for k in range(8):
                    S.op("pe", lambda e, k=k: e.matmul(bank(bcq)[0:NS, :], lhsT=xsT1[:, k, :], rhs=wdq_sb[:, k, :], start=(k == 0), stop=(k == 7)), r=[r_xsT1, r_wdq], w=rcq)
                for k in range(8):
                    S.op("pe", lambda e, k=k: e.matmul(bank(bkv)[0:NS, 0:320], lhsT=xsT1[:, k, :], rhs=wdkv_sb[:, k, :], start=(k == 0), stop=(k == 7)), r=[r_xsT1, r_wdkv], w=rkv)
                jk, r_jk = A.get(512, F32, "jk")
                scq, r_scq = A.get(16, F32, "scq")
                cqns, r_cqns = A.get(512, BF16, "cqns")
                ckvsf, r_ckvsf = A.get(256, F32, "ckvsf")
                ckvsb, r_ckvsb = A.get(256, BF16, "ckvsb")
                kpet, r_kpets = A.get(128, F32, "kpets")
                kpesf, r_kpesf = A.get(64, F32, "kpesf")
                kpesb, r_kpesb = A.get(128, BF16, "kpesb")
                sc = scq[0:NS]
                S.op("act", lambda e: e.activation(out=jk[0:NS, :], in_=bank(bcq)[0:NS, :], func=AF.Square, accum_out=sc[:, 0:1]), r=rcq, w=[r_jk, r_scq])
                S.op("act", lambda e: e.activation(out=sc[:, 1:2], in_=sc[:, 0:1], func=AF.Sqrt, bias=eps_rms[0:NS], scale=1.0 / 512), r=[r_scq, r_eps], w=[r_scq])
                S.op("dve", lambda e: e.reciprocal(out=sc[:, 1:2], in_=sc[:, 1:2]), r=[r_scq], w=[r_scq])
                S.op("dve", lambda e: e.scalar_tensor_tensor(out=cqns[0:NS, :], in0=bank(bcq)[0:NS, :], scalar=sc[:, 1:2], in1=qng[0:NS], op0=ALU.mult, op1=ALU.mult), r=rcq + [r_scq, r_c1], w=[r_cqns])
                S.op("act", lambda e: e.activation(out=jk[0:NS, 0:256], in_=bank(bkv)[0:NS, 0:256], func=AF.Square, accum_out=sc[:, 2:3]), r=rkv, w=[r_jk, r_scq])
                S.op("act", lambda e: e.activation(out=sc[:, 3:4], in_=sc[:, 2:3], func=AF.Sqrt, bias=eps_rms[0:NS], scale=1.0 / 256), r=[r_scq, r_eps], w=[r_scq])
                S.op("dve", lambda e: e.reciprocal(out=sc[:, 3:4], in_=sc[:, 3:4]), r=[r_scq], w=[r_scq])
                S.op("dve", lambda e: e.scalar_tensor_tensor(out=ckvsf[0:NS, :], in0=bank(bkv)[0:NS, 0:256], scalar=sc[:, 3:4], in1=kvng[0:NS], op0=ALU.mult, op1=ALU.mult), r=rkv + [r_scq, r_c1], w=[r_ckvsf])
                out_toks.append(S.dma("sp", lambda e: e.dma_start(out=ckvs, in_=ckvsf[0:NS, :]), r=[r_ckvsf]))
                S.op("pool", lambda e: e.tensor_copy(out=ckvsb[0:NS, :], in_=ckvsf[0:NS, :]), r=[r_ckvsf], w=[r_ckvsb])
                S.op("dve", lambda e: e.tensor_tensor(out=kpet[0:NS, 0:64], in0=bank(bkv)[0:NS, 256:320], in1=cc1s, op=ALU.mult), r=rkv + [r_tb1], w=[r_kpets])
                S.op("dve", lambda e: e.tensor_tensor(out=kpet[0:NS, 64:96], in0=bank(bkv)[0:NS, 288:320], in1=ss1s[:, 0:32], op=ALU.mult), r=rkv + [r_tb1], w=[r_kpets])
                S.op("dve", lambda e: e.tensor_tensor(out=kpet[0:NS, 96:128], in0=bank(bkv)[0:NS, 256:288], in1=ss1s[:, 32:64], op=ALU.mult), r=rkv + [r_tb1], w=[r_kpets])
                S.op("pool", lambda e: e.tensor_tensor(out=kpesf[0:NS, :], in0=kpet[0:NS, 0:64], in1=kpet[0:NS, 64:128], op=ALU.add), r=[r_kpets], w=[r_kpesf])
                out_toks.append(S.dma("sp", lambda e: e.dma_start(out=kpes, in_=kpesf[0:NS, :]), r=[r_kpesf]))
                S.op("pool", lambda e: e.tensor_copy(out=kpesb[0:NS, 0:64], in_=kpesf[0:NS, :]), r=[r_kpesf], w=[r_kpesb])
                S.op("pool", lambda e: e.tensor_copy(out=kpesb[0:NS, 64:128], in_=kpesf[0:NS, :]), r=[r_kpesf], w=[r_kpesb])
                btr, rtr = ring.get(1)
                trv = bankbf(btr)
                for k in range(4):
                    S.op("pe", lambda e, k=k: e.transpose(out=trv[:, k * 16:(k + 1) * 16], in_=cqns[0:NS, k * 128:(k + 1) * 128], identity=ident_b[0:NS, 0:NS]), r=[r_cqns, r_idb], w=rtr)
                for k in range(2):
                    S.op("pe", lambda e, k=k: e.transpose(out=trv[:, 64 + k * 16:64 + (k + 1) * 16], in_=ckvsb[0:NS, k * 128:(k + 1) * 128], identity=ident_b[0:NS, 0:NS]), r=[r_ckvsb, r_idb], w=rtr)
                S.op("pe", lambda e: e.transpose(out=trv[:, 96:112], in_=kpesb[0:NS, :], identity=ident_b[0:NS, 0:NS]), r=[r_kpesb, r_idb], w=rtr)
                trs, r_trs = A.get(112, BF16, "trs")
                S.op("act", lambda e: e.copy(out=trs, in_=trv[:, 0:112]), r=rtr, w=[r_trs])
                cqnTs = trs[:, 0:64].rearrange("p (k t) -> p k t", k=4)
                ckvTs = trs[:, 64:96].rearrange("p (k t) -> p k t", k=2)
                kpeTs = trs[:, 96:112]
                bqn, rqn = ring.get(1)
                qnv = bank(bqn)[:, 0:128].rearrange("p (h t) -> p h t", h=8)
                for h in range(8):
                    for k in range(4):
                        S.op("pe", lambda e, h=h, k=k: e.matmul(qnv[:, h, :], lhsT=wuq_sb[:, k, h * 128:(h + 1) * 128], rhs=cqnTs[:, k, :], start=(k == 0), stop=(k == 3)), r=[r_wuq, r_trs], w=rqn)
                qnTs, r_qnTs = A.get(128, BF16, "qnTs"); qnTs = qnTs.rearrange("p (h t) -> p h t", h=8)
                S.op("act", lambda e: e.copy(out=qnTs, in_=qnv), r=rqn, w=[r_qnTs])
                bqp, rqp = ring.get(1)
                for k in range(4):
                    S.op("pe", lambda e, k=k: e.matmul(bank(bqp)[0:NS, :], lhsT=cqnTs[:, k, :], rhs=wuq_sb[:, k, 1024:1536], start=(k == 0), stop=(k == 3)), r=[r_wuq, r_trs], w=rqp)
                qpv = bank(bqp)[0:NS, :].rearrange("p (h d) -> p h d", h=8)
                qpt, r_qpt = A.get(1024, F32, "qpt")
                qA = qpt[0:NS, 0:512].rearrange("p (h d) -> p h d", h=8)
                qB = qpt[0:NS, 512:1024].rearrange("p (h d) -> p h d", h=8)
                S.op("dve", lambda e: e.tensor_tensor(out=qA, in0=qpv, in1=cc1s.unsqueeze(1).to_broadcast([NS, 8, 64]), op=ALU.mult), r=rqp + [r_tb1], w=[r_qpt])
                S.op("dve", lambda e: e.tensor_tensor(out=qB[:, :, 0:32], in0=qpv[:, :, 32:64], in1=ss1s[:, 0:32].unsqueeze(1).to_broadcast([NS, 8, 32]), op=ALU.mult), r=rqp + [r_tb1], w=[r_qpt])
                S.op("dve", lambda e: e.tensor_tensor(out=qB[:, :, 32:64], in0=qpv[:, :, 0:32], in1=ss1s[:, 32:64].unsqueeze(1).to_broadcast([NS, 8, 32]), op=ALU.mult), r=rqp + [r_tb1], w=[r_qpt])
                qpsb, r_qpsb = A.get(512, BF16, "qpsb")
                S.op("pool", lambda e: e.tensor_tensor(out=qpsb[0:NS, :], in0=qpt[0:NS, 0:512], in1=qpt[0:NS, 512:1024], op=ALU.add), r=[r_qpt], w=[r_qpsb])
                btq, rtq = ring.get(1)
                tqv = bankbf(btq)[:, 0:64].rearrange("p (a t) -> p a t", a=4)
                for a in range(4):
                    S.op("pe", lambda e, a=a: e.transpose(out=tqv[:, a, :], in_=qpsb[0:NS, a * 128:(a + 1) * 128], identity=ident_b[0:NS, 0:NS]), r=[r_qpsb, r_idb], w=rtq)
                qpTs, r_qpTs = A.get(SB * 32, BF16, "qpTs")
                S.op("pool", lambda e: e.memset(qpTs, 0.0), w=[r_qpTs])
                qpT5 = qpTs.rearrange("p (b a two l) -> p b a two l", b=4, a=4, two=2)
                tq4 = tqv.rearrange("p a (b l) -> p a b l", b=4)
                for bb in range(SB):
                    S.op("act", lambda e, bb=bb: e.copy(out=qpT5[0:64, bb, :, 0, :], in_=tq4[0:64, :, bb, :]), r=rtq, w=[r_qpTs])
                    S.op("dve", lambda e, bb=bb: e.tensor_copy(out=qpT5[64:128, bb, :, 1, :], in_=tq4[64:128, :, bb, :]), r=rtq, w=[r_qpTs])
                qpTb = qpTs.rearrange("p (b n) -> p b n", b=4)
                bql, rql = ring.get(1)
                qlv = bank(bql)[:, 0:256].rearrange("p (h k t) -> p h k t", h=8, k=2)
                for h in range(8):
                    for k in range(2):
                        S.op("pe", lambda e, h=h, k=k: e.matmul(qlv[:, h, k, :], lhsT=wukT_sb[:, h, k * 128:(k + 1) * 128], rhs=qnTs[:, h, :], start=True, stop=True), r=[r_wukT, r_qnTs], w=rql)
                qlTs, r_qlTs = A.get(2 * SB * 32, BF16, "qlTs")
                ql5 = qlTs.rearrange("p (k b h l) -> p k b h l", k=2, b=4, h=8)
                qlv5 = qlv.rearrange("p h k (b l) -> p h k b l", b=4)
                for k in range(2):
                    for bb in range(SB):
                        S.op("act" if (k + bb) % 2 == 0 else "dve",
                             (lambda e, k=k, bb=bb: e.copy(out=ql5[:, k, bb], in_=qlv5[:, :, k, bb, :])) if (k + bb) % 2 == 0 else
                             (lambda e, k=k, bb=bb: e.tensor_copy(out=ql5[:, k, bb], in_=qlv5[:, :, k, bb, :])), r=rql, w=[r_qlTs])
                qlTb = qlTs.rearrange("p (k b n) -> p k b n", k=2, b=4)
                gk = []; r_gk = []; gp = []; r_gp = []
                for i_ in range(3):
                    v_, r_ = A.get(2048, BF16, f"gk{i_}"); gk.append(v_.rearrange("p (s c) -> p s c", s=8)); r_gk.append(r_)
                    v_, r_ = A.get(512, BF16, f"gp{i_}"); gp.append(v_.rearrange("p (s c) -> p s c", s=8)); r_gp.append(r_)
                gp2, r_gp2 = dbl(1024, BF16, "gp2", "p (s c) -> p s c", s=8)
                ckTc, r_ckTc = dbl(2048, BF16, "ckTc", "p (k n) -> p k n", k=2)
                kpTc, r_kpTc = dbl(1024, BF16, "kpTc")
                p_s, r_ps = dbl(1024, BF16, "p_s")
                pTc, r_pTc = dbl(256, BF16, "pTc", "p (i t) -> p i t", i=8)
                Oacc, r_Oacc = dbl(256, F32, "Oacc")
                ms, r_ms = dbl(8, F32, "ms")
                olb, r_olb = A.get(256, BF16, "olb")
                olTs, r_olTs = A.get(64, BF16, "olTs"); olTs = olTs.rearrange("p (k n) -> p k n", k=2)
                oTs, r_oTs = A.get(128, BF16, "oTs"); oTs = oTs.rearrange("p (h t) -> p h t", h=8)
                print("sample l1 arena bytes", A.cur)
                sctx = {}
                def sI(bb):
                    mb_ = ms[bb % 2][0:32]
                    oa = Oacc[bb % 2][0:32]
                    r_m = r_ms[bb % 2]
                    r_o = r_Oacc[bb % 2]
                    bsn, rsn = ring.get(1)
                    sn = bank(bsn)[0:32, 0:NS]
                    S.op("pe", lambda e, bb=bb, sn=sn: e.matmul(sn, lhsT=qlTb[:, 0, bb, :], rhs=ckvTs[:, 0, :], start=True, stop=False), r=[r_qlTs, r_trs], w=rsn)
                    S.op("pe", lambda e, bb=bb, sn=sn: e.matmul(sn, lhsT=qlTb[:, 1, bb, :], rhs=ckvTs[:, 1, :], start=False, stop=False), r=[r_qlTs, r_trs], w=rsn)
                    S.op("pe", lambda e, bb=bb, sn=sn: e.matmul(sn, lhsT=qpTb[:, bb, :], rhs=kpeTs, start=False, stop=False), r=[r_qpTs, r_trs], w=rsn)
                    S.op("pe", lambda e, bb=bb, sn=sn: e.matmul(sn, lhsT=ident_b[0:32, 0:32], rhs=mskv[0:32, bb, :], start=False, stop=True), r=[r_idb, r_msk], w=rsn)
                    S.op("dve", lambda e, sn=sn, mb_=mb_: e.reduce_max(out=mb_[:, 0:1], in_=sn, axis=AX.X), r=rsn, w=[r_m])
                    S.op("dve", lambda e, mb_=mb_: e.tensor_scalar(out=mb_[:, 5:6], in0=mb_[:, 0:1], scalar1=-MLA_SCALE, scalar2=None, op0=ALU.mult), r=[r_m], w=[r_m])
                    psl = p_s[0][0:32]
                    S.op("act", lambda e, sn=sn, mb_=mb_, psl=psl: e.activation(out=psl[:, 0:NS], in_=sn, func=AF.Exp, bias=mb_[:, 5:6], scale=MLA_SCALE, accum_out=mb_[:, 1:2]), r=rsn + [r_m], w=[r_ps[0], r_m])
                    bpt, rpt = ring.get(1)
                    S.op("pe", lambda e, psl=psl, bpt=bpt: e.transpose(out=bankbf(bpt)[0:NS, 0:32], in_=psl[:, 0:NS], identity=ident_b[0:32, 0:32]), r=[r_ps[0], r_idb], w=rpt)
                    S.op("act", lambda e, bpt=bpt: e.copy(out=pTc[0][0:NS, 0, :], in_=bankbf(bpt)[0:NS, 0:32]), r=rpt, w=[r_pTc[0]])
                    bo_, ro_ = ring.get(1)
                    S.op("pe", lambda e, bo_=bo_: e.matmul(bank(bo_)[0:32, 0:256], lhsT=pTc[0][0:NS, 0, :], rhs=ckvsb[0:NS, :], start=True, stop=True), r=[r_pTc[0], r_ckvsb], w=ro_)
                    S.op("dve", lambda e, bo_=bo_, oa=oa: e.tensor_copy(out=oa, in_=bank(bo_)[0:32, 0:256]), r=ro_, w=[r_o])

                def sG(bb, stp, g3):
                    S.dma("pool", lambda e, g3=g3, bb=bb, stp=stp: e.indirect_dma_start(out=gk[g3].rearrange("p s c -> p (s c)"), out_offset=None, in_=cckv[:, :], element_offset=stp * 2048,
                                                                                         in_offset=bass.IndirectOffsetOnAxis(ap=pt_sb[:, bb:bb + 1], axis=0)), r=[r_pt], w=[r_gk[g3]])
                    S.dma("pool", lambda e, g3=g3, bb=bb, stp=stp: e.indirect_dma_start(out=gp[g3].rearrange("p s c -> p (s c)"), out_offset=None, in_=ckpe[:, :], element_offset=stp * 512,
                                                                                         in_offset=bass.IndirectOffsetOnAxis(ap=pt_sb[:, bb:bb + 1], axis=0)), r=[r_pt], w=[r_gp[g3]])

                def sT(bb, stp, gs, g3):
                    S.op("act", lambda e, gs=gs, g3=g3: e.copy(out=gp2[gs][:, :, 0:64], in_=gp[g3]), r=[r_gp[g3]], w=[r_gp2[gs]])
                    S.op("dve", lambda e, gs=gs, g3=g3: e.tensor_copy(out=gp2[gs][:, :, 64:128], in_=gp[g3]), r=[r_gp[g3]], w=[r_gp2[gs]])
                    for hf in range(2):
                        sl = hf * 4
                        bA, rA = ring.get(1)
                        bB, rB = ring.get(1)
                        bC, rC = ring.get(1)
                        for i_ in range(4):
                            S.op("pe", lambda e, i_=i_, sl=sl, g3=g3, bA=bA: e.transpose(out=bankbf(bA)[:, i_ * 128:(i_ + 1) * 128], in_=gk[g3][:, sl + i_, 0:128], identity=ident_b[:]), r=[r_gk[g3], r_idb], w=rA)
                        for i_ in range(4):
                            S.op("pe", lambda e, i_=i_, sl=sl, g3=g3, bB=bB: e.transpose(out=bankbf(bB)[:, i_ * 128:(i_ + 1) * 128], in_=gk[g3][:, sl + i_, 128:256], identity=ident_b[:]), r=[r_gk[g3], r_idb], w=rB)
                        for i_ in range(4):
                            S.op("pe", lambda e, i_=i_, sl=sl, gs=gs, bC=bC: e.transpose(out=bankbf(bC)[:, i_ * 128:(i_ + 1) * 128], in_=gp2[gs][:, sl + i_, :], identity=ident_b[:]), r=[r_gp2[gs], r_idb], w=rC)
                        cs_ = slice(hf * 512, (hf + 1) * 512)
                        S.op("act", lambda e, gs=gs, bA=bA, cs_=cs_: e.copy(out=ckTc[gs][:, 0, cs_], in_=bankbf(bA)[:, 0:512]), r=rA, w=[r_ckTc[gs]])
                        S.op("dve", lambda e, gs=gs, bB=bB, cs_=cs_: e.tensor_copy(out=ckTc[gs][:, 1, cs_], in_=bankbf(bB)[:, 0:512]), r=rB, w=[r_ckTc[gs]])
                        if hf == 0:
                            S.op("act", lambda e, gs=gs, bC=bC, cs_=cs_: e.copy(out=kpTc[gs][:, cs_], in_=bankbf(bC)[:, 0:512]), r=rC, w=[r_kpTc[gs]])
                        else:
                            S.op("dve", lambda e, gs=gs, bC=bC, cs_=cs_: e.tensor_copy(out=kpTc[gs][:, cs_], in_=bankbf(bC)[:, 0:512]), r=rC, w=[r_kpTc[gs]])

                def sU1(bb, stp, gs):
                    mb_ = ms[bb % 2][0:32]
                    oa = Oacc[bb % 2][0:32]
                    r_m = r_ms[bb % 2]
                    r_o = r_Oacc[bb % 2]
                    bs2, rs2 = ring.get(2)
                    sfull = bank(bs2, 2)[0:32, :]
                    for hf in range(2):
                        cs_ = slice(hf * 512, (hf + 1) * 512)
                        S.op("pe", lambda e, bb=bb, gs=gs, cs_=cs_, sfull=sfull: e.matmul(sfull[:, cs_], lhsT=qlTb[:, 0, bb, :], rhs=ckTc[gs][:, 0, cs_], start=True, stop=False), r=[r_qlTs, r_ckTc[gs]], w=[rs2[hf]])
                        S.op("pe", lambda e, bb=bb, gs=gs, cs_=cs_, sfull=sfull: e.matmul(sfull[:, cs_], lhsT=qlTb[:, 1, bb, :], rhs=ckTc[gs][:, 1, cs_], start=False, stop=False), r=[r_qlTs, r_ckTc[gs]], w=[rs2[hf]])
                        S.op("pe", lambda e, bb=bb, gs=gs, cs_=cs_, sfull=sfull: e.matmul(sfull[:, cs_], lhsT=qpTb[:, bb, :], rhs=kpTc[gs][:, cs_], start=False, stop=True), r=[r_qpTs, r_kpTc[gs]], w=[rs2[hf]])
                    S.op("dve", lambda e, sfull=sfull, mb_=mb_: e.reduce_max(out=mb_[:, 2:3], in_=sfull, axis=AX.X), r=rs2, w=[r_m])
                    S.op("dve", lambda e, mb_=mb_: e.tensor_tensor(out=mb_[:, 3:4], in0=mb_[:, 0:1], in1=mb_[:, 2:3], op=ALU.max), r=[r_m], w=[r_m])
                    S.op("dve", lambda e, mb_=mb_: e.tensor_tensor(out=mb_[:, 4:5], in0=mb_[:, 0:1], in1=mb_[:, 3:4], op=ALU.subtract), r=[r_m], w=[r_m])
                    S.op("act", lambda e, mb_=mb_: e.activation(out=mb_[:, 4:5], in_=mb_[:, 4:5], func=AF.Exp, scale=MLA_SCALE), r=[r_m], w=[r_m])
                    S.op("dve", lambda e, mb_=mb_: e.tensor_scalar(out=mb_[:, 5:6], in0=mb_[:, 3:4], scalar1=-MLA_SCALE, scalar2=None, op0=ALU.mult), r=[r_m], w=[r_m])
                    pq = gs
                    psl = p_s[pq][0:32]
                    S.op("act", lambda e, sfull=sfull, mb_=mb_, psl=psl: e.activation(out=psl, in_=sfull, func=AF.Exp, bias=mb_[:, 5:6], scale=MLA_SCALE, accum_out=mb_[:, 6:7]), r=rs2 + [r_m], w=[r_ps[pq], r_m])
                    S.op("dve", lambda e, mb_=mb_: e.scalar_tensor_tensor(out=mb_[:, 1:2], in0=mb_[:, 1:2], scalar=mb_[:, 4:5], in1=mb_[:, 6:7], op0=ALU.mult, op1=ALU.add), r=[r_m], w=[r_m])
                    S.op("dve", lambda e, mb_=mb_: e.tensor_copy(out=mb_[:, 0:1], in_=mb_[:, 3:4]), r=[r_m], w=[r_m])

                def sU2(bb, stp, gs, g3):
                    mb_ = ms[bb % 2][0:32]
                    oa = Oacc[bb % 2][0:32]
                    r_m = r_ms[bb % 2]
                    r_o = r_Oacc[bb % 2]
                    pq = gs
                    psl = p_s[pq][0:32]
                    bpt, rpt = ring.get(1)
                    ptv = bankbf(bpt)[:, 0:256].rearrange("p (i t) -> p i t", i=8)
                    for i_ in range(8):
                        S.op("pe", lambda e, i_=i_, psl=psl, ptv=ptv: e.transpose(out=ptv[:, i_, :], in_=psl[:, i_ * 128:(i_ + 1) * 128], identity=ident_b[0:32, 0:32]), r=[r_ps[pq], r_idb], w=rpt)
                    S.op("act", lambda e, ptv=ptv, pq=pq: e.copy(out=pTc[pq], in_=ptv), r=rpt, w=[r_pTc[pq]])
                    bo_, ro_ = ring.get(1)
                    for i_ in range(8):
                        S.op("pe", lambda e, i_=i_, bo_=bo_, pq=pq, g3=g3: e.matmul(bank(bo_)[0:32, 0:256], lhsT=pTc[pq][:, i_, :], rhs=gk[g3][:, i_, :], start=(i_ == 0), stop=(i_ == 7)), r=[r_pTc[pq], r_gk[g3]], w=ro_)
                    S.op("dve", lambda e, bo_=bo_, oa=oa, mb_=mb_: e.scalar_tensor_tensor(out=oa, in0=oa, scalar=mb_[:, 4:5], in1=bank(bo_)[0:32, 0:256], op0=ALU.mult, op1=ALU.add), r=ro_ + [r_o, r_m], w=[r_o])


                def sF(bb):
                    mb_ = ms[bb % 2][0:32]
                    oa = Oacc[bb % 2][0:32]
                    r_m = r_ms[bb % 2]
                    r_o = r_Oacc[bb % 2]
                    S.op("dve", lambda e, mb_=mb_: e.reciprocal(out=mb_[:, 7:8], in_=mb_[:, 1:2]), r=[r_m], w=[r_m])
                    S.op("act", lambda e, oa=oa, mb_=mb_: e.activation(out=olb[0:32, :], in_=oa, func=AF.Identity, bias=0.0, scale=mb_[:, 7:8]), r=[r_o, r_m], w=[r_olb])
                    bot, rot = ring.get(1)
                    otv = bankbf(bot)[:, 0:64].rearrange("p (k n) -> p k n", k=2)
                    for k in range(2):
                        S.op("pe", lambda e, k=k, otv=otv: e.transpose(out=otv[:, k, :], in_=olb[0:32, k * 128:(k + 1) * 128], identity=ident_b[0:32, 0:32]), r=[r_olb, r_idb], w=rot)
                    S.op("dve", lambda e, otv=otv: e.tensor_copy(out=olTs, in_=otv), r=rot, w=[r_olTs])
                    boh, roh = ring.get(1)
                    ohv = bank(boh)[:, 0:32].rearrange("p (h l) -> p h l", h=8)
                    for h in range(8):
                        for k in range(2):
                            S.op("pe", lambda e, h=h, k=k, ohv=ohv: e.matmul(ohv[:, h, :], lhsT=wuv_sb[:, k, h * 128:(h + 1) * 128], rhs=olTs[:, k, h * 4:(h + 1) * 4], start=(k == 0), stop=(k == 1)), r=[r_wuv, r_olTs], w=roh)
                    S.op("act", lambda e, ohv=ohv, bb=bb: e.copy(out=oTs[:, :, bb * 4:bb * 4 + 4], in_=ohv), r=roh, w=[r_oTs])

                NTOT = SB * NSTEP
                sG(0, 0, 0)
                sG(0, 1, 1)
                sT(0, 0, 0, 0)
                for idx in range(NTOT):
                    bb, stp = divmod(idx, NSTEP)
                    if idx + 2 < NTOT:
                        b3, s3 = divmod(idx + 2, NSTEP)
                        sG(b3, s3, (idx + 2) % 3)
                    if stp == 0:
                        sI(bb)
                    sU1(bb, stp, idx % 2)
                    if idx + 1 < NTOT:
                        b2, s2 = divmod(idx + 1, NSTEP)
                        sT(b2, s2, (idx + 1) % 2, (idx + 1) % 3)
                    sU2(bb, stp, idx % 2, idx % 3)
                    if stp == NSTEP - 1:
                        sF(bb)
                by_, ry_ = ring.get(2)
                for hf in range(2):
                    for c in range(8):
                        S.op("pe", lambda e, hf=hf, c=c: e.matmul(bank(by_ + hf)[0:NS, :], lhsT=oTs[:, c, :], rhs=wo1_sb[:, c, hf * 512:(hf + 1) * 512], start=(c == 0), stop=(c == 7)), r=[r_oTs, r_wo1], w=[ry_[hf]])
                zs1, r_zs1 = A.get(D, F32, "zs1")
                scs1, r_scs1 = A.get(16, F32, "scs1")
                S.op("dve", lambda e: e.scalar_tensor_tensor(out=zs1[0:NS, :], in0=XS[:], scalar=ALPHA, in1=bank(by_, 2)[0:NS, :], op0=ALU.mult, op1=ALU.add), r=[r_XS] + ry_, w=[r_zs1])
                layer_norm_tile(zs1[0:NS, :], r_zs1, XS[:], r_XS, NS, scs1, r_scs1, lnt1, r_lnt1)

            if with_sample:
                sample_l1()
            chk('l1')
            ffn(1, True)


        except _Stop:
            print('stopped at', stop)
        S.finish(out_toks)
        S.emit()
        print("ops", S.stats())
    return nc


_CACHE = {}


def _prep_shared(inp, tabs):
    f = lambda a: np.ascontiguousarray(np.asarray(a, dtype=np.float32))
    sh = {}
    sh["w_in"] = f(inp["w_in_even"][0])
    sh["pool_w"] = f(np.transpose(inp["pool_w"][0], (1, 0, 2)))
    sh["pool_scale"] = f(inp["pool_scale"][0].reshape(4, 128).T)
    sh["gn_g"] = f(inp["ret_gn_g"][0])
    sh["w_o0"] = f(inp["w_o_even"][0])
    sh["w_dq"] = f(inp["w_dq"][0])
    sh["qn_g"] = f(inp["q_norm_g"][0])
    wuq = np.asarray(inp["w_uq"][0]).reshape(512, 8, 192)
    sh["w_uq"] = f(np.concatenate([wuq[:, :, :128].reshape(512, 1024), wuq[:, :, 128:].reshape(512, 512)], axis=1))
    sh["w_dkv"] = f(inp["w_dkv"][0])
    sh["kvn_g"] = f(inp["kv_norm_g"][0])
    sh["w_ukT"] = f(np.transpose(inp["w_uk"][0], (2, 1, 0)))
    sh["w_uv"] = f(np.asarray(inp["w_uv"][0]).reshape(256, 1024))
    sh["w_o1"] = f(inp["w_o_mla"][0])
    wup = np.asarray(inp["w_up"])
    a = wup[:, :, :DFF].reshape(DEPTH, 8, 128, NJ, 128)
    b = wup[:, :, DFF:].reshape(DEPTH, 8, 128, NJ, 128)
    ab = np.concatenate([a, b], axis=4)
    sh["w_up"] = f(np.transpose(ab, (0, 3, 2, 1, 4)))
    sh["w_down"] = f(inp["w_down"])
    sh["conv_w"] = f(np.transpose(np.asarray(inp["conv_w"]).reshape(DEPTH, 3, NJ, 128), (0, 3, 2, 1)))
    sh["conv_b"] = f(np.transpose(np.asarray(inp["conv_b"]).reshape(DEPTH, NJ, 128), (0, 2, 1)))
    for k in ("ln_mix_g", "ln_mix_b", "ln_ffn_g", "ln_ffn_b"):
        sh[k] = f(inp[k])
    sh["cckv"] = np.asarray(inp["cache_ckv"], dtype=np.float32).reshape(5120, 128 * 256)
    sh["ckpe"] = np.asarray(inp["cache_kpe"], dtype=np.float32).reshape(5120, 128 * 64)
    for k, v in tabs.items():
        if isinstance(v, np.ndarray):
            sh["t_" + k] = v
    return sh


def kernel(**inp):
    tabs = host_tables()
    if "nc" not in _CACHE:
        _CACHE["nc"] = build_program(tabs)
    nc = _CACHE["nc"]
    sh = _prep_shared(inp, tabs)
    f = lambda a: np.ascontiguousarray(np.asarray(a, dtype=np.float32))
    in_maps = []
    for c in range(8):
        m = dict(sh)
        bs = slice(SB * c, SB * (c + 1))
        m["xp"] = f(inp["x_prompt"][c])
        m["xs"] = f(np.asarray(inp["x_sample"][bs]).reshape(NS, D))
        m["spool"] = f(np.asarray(inp["state_pool"][0, bs]).reshape(SB * 15, 512))
        m["sret"] = f(np.asarray(inp["state_ret"][0, bs]).reshape(SB * RH, 128, 128))
        m["sconv"] = f(np.asarray(inp["state_conv"][:, bs]).reshape(DEPTH, SB * 2, DFF))
        m["ptab"] = np.ascontiguousarray(np.asarray(inp["page_table"][bs]).T.astype(np.int32))
        in_maps.append(m)
    res = run_bass_kernel_spmd(nc, in_maps, core_ids=list(range(8)))
    R = res.results
    cat = lambda k: np.stack([np.asarray(R[c][k]) for c in range(8)])
    y_p = cat("yp")
    y_s = cat("ys").reshape(32, SL, D)
    pool_p = cat("poolp")[None]
    pool_s = cat("pools").reshape(1, 32, 15, 512)
    ret_p = cat("retp")[None]
    ret_s = cat("rets").reshape(1, 32, RH, 128, 128)
    ckv_p = cat("ckvp")[None]
    ckv_s = cat("ckvs").reshape(1, 32, SL, 256)
    kpe_p = cat("kpep")[None]
    kpe_s = cat("kpes").reshape(1, 32, SL, 64)
    conv_p = np.transpose(cat("convp"), (1, 0, 2, 3))
    conv_s = np.transpose(cat("convs").reshape(8, DEPTH, SB, 2, DFF), (1, 0, 2, 3, 4)).reshape(DEPTH, 32, 2, DFF)
    outs = (y_p, y_s, pool_p, pool_s, ret_p, ret_s, ckv_p, ckv_s, kpe_p, kpe_s, conv_p, conv_s)
    return tuple(np.ascontiguousarray(o.astype(np.float32)) for o in outs)
```

```python
import contextlib
import math
import numpy as np
import concourse.bass as bass
import concourse.mybir as mybir
from concourse.bass_utils import run_bass_kernel_spmd

F32 = mybir.dt.float32
BF16 = mybir.dt.bfloat16
I32 = mybir.dt.int32
AF = mybir.ActivationFunctionType
ALU = mybir.AluOpType
AX = mybir.AxisListType

ALL_Q = ("pe", "act", "dve", "pool", "sp")
SAME_ENGINE_SYNC = True

D = 1024
SEQ = 2048
NT = SEQ // 128
DEPTH = 2
NS = 16
SB = 4
SL = 4
PAST = 16384
NPAGES = 128
POOL_W = (2, 4, 8, 16)
RH = 4
DFF = 2816
NJ = DFF // 128
ALPHA = (2.0 * DEPTH) ** 0.25
LN_EPS = 1e-5
RMS_EPS = 1e-6
GN_EPS = 1e-6
MLA_SCALE = (128 + 64) ** -0.5
KSCALE = 128 ** -0.5
NEG = -30000.0


class Res:
    __slots__ = ("name", "w", "r")

    def __init__(self, name):
        self.name = name
        self.w = None
        self.r = []


class Op:
    __slots__ = ("q", "fn", "waits", "signal", "dma", "idx")

    def __init__(self, q, fn, dma=None):
        self.q = q
        self.fn = fn
        self.waits = []
        self.signal = False
        self.dma = dma
        self.idx = None


class Sched:
    def __init__(self, nc, stack):
        self.nc = nc
        self.stack = stack
        self.ops = {q: [] for q in ALL_Q}
        self.dma_sems = {}
        self.eng_sems = {}
        self.waited = {q: {} for q in ALL_Q}
        self.dwaited = {q: {} for q in ALL_Q}
        self.nres = 0

    def res(self, name=None):
        self.nres += 1
        return Res(name or f"r{self.nres}")

    def _add_wait(self, op, tok):
        if tok is None:
            return
        q = op.q
        if tok[0] == "eng":
            _, sq, sidx = tok
            if sq == q and (q == "pe" or q == "sp" or not SAME_ENGINE_SYNC):
                return
            if self.waited[q].get(sq, -1) >= sidx:
                return
            self.waited[q][sq] = sidx
            op.waits.append(tok)
            self.ops[sq][sidx].signal = True
        else:
            _, key, val = tok
            if self.dwaited[q].get(key, -1) >= val:
                return
            self.dwaited[q][key] = val
            op.waits.append(tok)

    def _deps(self, o, r, w, tok):
        toks = []
        for res in r:
            if res.w is not None:
                toks.append(res.w)
        for res in w:
            if res.w is not None:
                toks.append(res.w)
            toks.extend(res.r)
        best = {}
        for t in toks:
            k = (t[0], t[1])
            if k not in best or best[k][2] < t[2]:
                best[k] = t
        for t in best.values():
            self._add_wait(o, t)
        for res in r:
            res.r.append(tok)
        for res in w:
            res.w = tok
            res.r = []

    def op(self, q, fn, r=(), w=()):
        o = Op(q, fn)
        o.idx = len(self.ops[q])
        self._deps(o, r, w, ("eng", q, o.idx))
        self.ops[q].append(o)
        return o

    def dma(self, q, fn, r=(), w=(), semkey=None):
        res0 = (list(w) + list(r))[0] if (w or r) else None
        key = semkey if semkey is not None else id(res0)
        if key not in self.dma_sems:
            self.dma_sems[key] = [None, 0]
        ent = self.dma_sems[key]
        ent[1] += 16
        o = Op(q, fn, dma=(key, ent[1]))
        o.idx = len(self.ops[q])
        tok = ("dma", key, ent[1])
        self._deps(o, r, w, tok)
        self.ops[q].append(o)
        return tok

    def finish(self, out_tokens):
        o = Op("sp", None)
        o.idx = len(self.ops["sp"])
        for t in out_tokens:
            self._add_wait(o, t)
        self.ops["sp"].append(o)

    def emit(self):
        nc = self.nc
        st = self.stack
        for q in ALL_Q:
            self.eng_sems[q] = st.enter_context(nc.semaphore(f"es_{q}"))
        for i, key in enumerate(self.dma_sems):
            self.dma_sems[key][0] = st.enter_context(nc.semaphore(f"ds_{i}"))
        cnt = {}
        for q in ALL_Q:
            c = 0
            arr = []
            for o in self.ops[q]:
                if o.signal:
                    c += 1
                arr.append(c)
            cnt[q] = arr
        block = st.enter_context(nc.Block())

        def run(q, eng):
            for o in self.ops[q]:
                for t in o.waits:
                    if t[0] == "eng":
                        eng.wait_ge(self.eng_sems[t[1]], cnt[t[1]][t[2]])
                    else:
                        eng.wait_ge(self.dma_sems[t[1]][0], t[2])
                if o.fn is None:
                    continue
                ins = o.fn(eng)
                if o.dma is not None:
                    ins.then_inc(self.dma_sems[o.dma[0]][0], 16)
                elif o.signal:
                    ins.then_inc(self.eng_sems[q], 1)

        @block.tensor
        def _(eng):
            run("pe", eng)

        @block.scalar
        def _(eng):
            run("act", eng)

        @block.vector
        def _(eng):
            run("dve", eng)

        @block.gpsimd
        def _(eng):
            run("pool", eng)

        @block.sync
        def _(eng):
            run("sp", eng)

    def stats(self):
        return {q: (len(self.ops[q]), sum(len(o.waits) for o in self.ops[q])) for q in ALL_Q}


def _rope_tables(half, pos):
    inv = (np.float32(10000.0) ** (-(np.arange(half, dtype=np.float32)) / np.float32(half))).astype(np.float32)
    ang = (pos.astype(np.float32)[:, None] * inv[None, :]).astype(np.float32)
    return np.cos(ang).astype(np.float32), np.sin(ang).astype(np.float32)


def host_tables():
    t = {}
    pos_p = np.arange(SEQ, dtype=np.float32)
    pos_s = (PAST + np.arange(SL)).astype(np.float32)
    c, s = _rope_tables(64, pos_p)
    t["c2p"] = np.ascontiguousarray(np.concatenate([c, c], 1).T)
    t["ssp"] = np.ascontiguousarray(np.concatenate([s, -s], 1).T)
    c, s = _rope_tables(64, pos_s)
    t["c2s"] = np.ascontiguousarray(np.tile(np.concatenate([c, c], 1).T, (1, SB)))
    t["sss"] = np.ascontiguousarray(np.tile(np.concatenate([s, -s], 1).T, (1, SB)))
    c, s = _rope_tables(32, pos_p)
    cc = np.concatenate([c, c], 1).reshape(NT, 128, 64).transpose(1, 0, 2)
    ss = np.concatenate([-s, s], 1).reshape(NT, 128, 64).transpose(1, 0, 2)
    t["cc1"] = np.ascontiguousarray(cc)
    t["ss1"] = np.ascontiguousarray(ss)
    c, s = _rope_tables(32, pos_s)
    t["cc1s"] = np.ascontiguousarray(np.tile(np.concatenate([c, c], 1), (SB, 1)))
    t["ss1s"] = np.ascontiguousarray(np.tile(np.concatenate([-s, s], 1), (SB, 1)))
    lg = np.log(np.float32(1.0) - np.float32(2.0) ** (np.float32(-5.0) - np.arange(RH, dtype=np.float32))).astype(np.float32)
    for name, C in (("p", 128), ("s", SL)):
        idx = np.arange(C, dtype=np.float32)
        diff = idx[:, None] - idx[None, :]
        dec = np.where(diff >= 0, np.exp(np.maximum(diff, 0.0)[None] * lg[:, None, None]), 0.0).astype(np.float32)
        t["decT" + name] = np.ascontiguousarray((dec * np.float32(KSCALE)).transpose(2, 0, 1))
        qd = np.exp((idx + 1.0)[:, None] * lg[None, :]).astype(np.float32)
        t["qdec" + name] = np.ascontiguousarray(np.broadcast_to(qd.T[None], (128, RH, C)))
        kd = np.exp((C - 1.0 - idx)[:, None] * lg[None, :]).astype(np.float32) * np.float32(KSCALE)
        t["kdec" + name] = np.ascontiguousarray(kd)
        t["gC" + name] = [float(np.exp(np.float32(C) * lg[h])) for h in range(RH)]
    ic = np.zeros((128, 4, 15), np.float32)
    for g, w in enumerate(POOL_W):
        ic[:, g, :] = 1.0 / np.minimum(float(w), np.arange(15) + 1.0)
    t["invcnt"] = ic
    qi = np.arange(128)
    t["maskd"] = np.where(qi[None, :] <= qi[:, None], 0.0, NEG).astype(np.float32)
    t["ident"] = np.eye(128, dtype=np.float32)
    r = np.arange(32) % 4
    t["masks"] = np.where(np.arange(4)[None, :] <= r[:, None], 0.0, NEG).astype(np.float32)
    mb = np.full((32, SB, NS), NEG, np.float32)
    for bb in range(SB):
        for l2 in range(SL):
            mb[:, bb, bb * SL + l2] = np.where(l2 <= r, 0.0, NEG)
    t["masksb"] = np.ascontiguousarray(mb.reshape(32, SB * NS))
    return t


ARENA_BYTES = 142144


class PsumRing:
    def __init__(self, S):
        self.res = [S.res(f"bank{i}") for i in range(8)]
        self.nxt = 0

    def get(self, n=1):
        if n == 2 and self.nxt % 2:
            self.nxt = (self.nxt + 1) % 8
        if n == 4 and self.nxt % 4:
            self.nxt = (self.nxt + 4 - self.nxt % 4) % 8
        b = self.nxt
        self.nxt = (self.nxt + n) % 8
        return b, [self.res[b + i] for i in range(n)]


class Arena:
    def __init__(self, S, buf):
        self.S = S
        self.buf = buf
        self.hist = []
        self.cur = 0

    def at(self, off):
        assert off % 4 == 0
        self.cur = off

    def get(self, nelem, dt=BF16, name=None, nres=None):
        size = 2 if dt == BF16 else 4
        nb = (nelem * size + 3) // 4 * 4
        s, e = self.cur, self.cur + nb
        assert e <= ARENA_BYTES, (name, e)
        self.cur = e
        pend = []
        for (s2, e2, r2) in self.hist:
            if s2 < e and s < e2:
                if r2.w is not None:
                    pend.append(r2.w)
                pend.extend(r2.r)
        best = {}
        for t in pend:
            k = (t[0], t[1])
            if k not in best or best[k][2] < t[2]:
                best[k] = t
        pend = list(best.values())
        rs = []
        for i in range(nres or 1):
            r = self.S.res(f"{name}_{i}")
            r.r = list(pend)
            self.hist.append((s, e, r))
            rs.append(r)
        v = self.buf[:, s // 2:s // 2 + nb // 2]
        if dt != BF16:
            v = v.bitcast(dt)
        return v[:, 0:nelem], (rs if nres else rs[0])


class _Stop(Exception):
    pass


def build_program(tabs, with_sample=True, stop=None):
    nc = bass.Bass("TRN2", target_bir_lowering=False)

    def din(name, shape, dt=F32):
        return nc.dram_tensor(name, list(shape), dt, kind="ExternalInput").ap()

    def dout(name, shape):
        return nc.dram_tensor(name, list(shape), F32, kind="ExternalOutput").ap()

    xp = din("xp", [SEQ, D])
    xs = din("xs", [NS, D])
    spool = din("spool", [SB * 15, 512])
    sret = din("sret", [SB * RH, 128, 128])
    sconv = din("sconv", [DEPTH, SB * 2, DFF])
    ptab = din("ptab", [128, SB], I32)
    if with_sample:
        cckv = din("cckv", [5120, 128 * 256])
        ckpe = din("ckpe", [5120, 128 * 64])
    w_in = din("w_in", [D, 2560])
    pool_w = din("pool_w", [128, 4, 128])
    pool_scale = din("pool_scale", [128, 4])
    gn_g = din("gn_g", [512])
    w_o0 = din("w_o0", [D, D])
    w_dq = din("w_dq", [D, 512])
    qn_g = din("qn_g", [512])
    w_uq = din("w_uq", [512, 1536])
    w_dkv = din("w_dkv", [D, 320])
    kvn_g = din("kvn_g", [256])
    w_ukT = din("w_ukT", [128, 8, 256])
    w_uv = din("w_uv", [256, 1024])
    w_o1 = din("w_o1", [D, D])
    w_up = din("w_up", [DEPTH, NJ, 128, 8, 256])
    w_down = din("w_down", [DEPTH, DFF, D])
    conv_w = din("conv_w", [DEPTH, 128, NJ, 3])
    conv_b = din("conv_b", [DEPTH, 128, NJ])
    ln_mix_g = din("ln_mix_g", [DEPTH, D])
    ln_mix_b = din("ln_mix_b", [DEPTH, D])
    ln_ffn_g = din("ln_ffn_g", [DEPTH, D])
    ln_ffn_b = din("ln_ffn_b", [DEPTH, D])
    tdr = {k: din("t_" + k, v.shape) for k, v in tabs.items() if isinstance(v, np.ndarray)}

    yp = dout("yp", [SEQ, D])
    ys = dout("ys", [NS, D])
    poolp = dout("poolp", [15, 512])
    pools = dout("pools", [SB * 15, 512])
    retp = dout("retp", [RH, 128, 128])
    rets = dout("rets", [SB * RH, 128, 128])
    ckvp = dout("ckvp", [SEQ, 256])
    ckvs = dout("ckvs", [NS, 256])
    kpep = dout("kpep", [SEQ, 64])
    kpes = dout("kpes", [NS, 64])
    convp = dout("convp", [DEPTH, 2, DFF])
    convs = dout("convs", [DEPTH, SB * 2, DFF])

    out_toks = []
    st = contextlib.ExitStack()
    with st:
        S = Sched(nc, st)
        _n = [0]

        def sb(shape, dt, name):
            return st.enter_context(nc.sbuf_tensor(name, list(shape), dt))

        P = st.enter_context(nc.psum_tensor("P", [128, 8, 512], F32))
        ring = PsumRing(S)

        def bank(b, n=1):
            return P[:, b:b + n, :].rearrange("p a b -> p (a b)")

        def bankbf(b, n=1):
            return P[:, b:b + n, :].rearrange("p a b -> p (a b)").bitcast(BF16)

        X = sb([128, NT, D], F32, "X")
        rX = [S.res(f"X{t}") for t in range(NT)]
        BIG = sb([128, ARENA_BYTES // 2], BF16, "BIG")
        A = Arena(S, BIG)
        ident_f = sb([128, 128], F32, "ident_f"); r_idf = S.res()
        ident_b = sb([128, 128], BF16, "ident_b"); r_idb = S.res()
        maskd = sb([128, 128], BF16, "maskd"); r_maskd = S.res()
        eps_t = sb([128, 4], F32, "eps_t"); r_eps = S.res()
        XS = sb([NS, D], F32, "XS"); r_XS = S.res()
        S.dma("sp", lambda e: e.dma_start(out=ident_f[:], in_=tdr["ident"]), w=[r_idf])
        S.dma("pool", lambda e: e.dma_start(out=ident_b[:], in_=tdr["ident"]), w=[r_idb])
        S.dma("pool", lambda e: e.dma_start(out=maskd[:], in_=tdr["maskd"]), w=[r_maskd])
        S.op("pool", lambda e: e.memset(eps_t[:, 0:1], LN_EPS), w=[r_eps])
        S.op("pool", lambda e: e.memset(eps_t[:, 1:2], RMS_EPS), w=[r_eps])
        S.op("pool", lambda e: e.memset(eps_t[:, 2:3], GN_EPS), w=[r_eps])
        eps_ln, eps_rms, eps_gn = eps_t[:, 0:1], eps_t[:, 1:2], eps_t[:, 2:3]

        def evac(q, out_ap, in_ap, r, w):
            if q == "act":
                S.op("act", lambda e: e.copy(out=out_ap, in_=in_ap), r=r, w=w)
            else:
                S.op(q, lambda e: e.tensor_copy(out=out_ap, in_=in_ap), r=r, w=w)

        def load_ln(lnt, r_lnt, g_ap, b_ap):
            S.dma("sp", lambda e: e.dma_start(out=lnt[:, 0:D], in_=g_ap.partition_broadcast(128)), w=[r_lnt])
            S.dma("sp", lambda e: e.dma_start(out=lnt[:, D:2 * D], in_=b_ap.partition_broadcast(128)), w=[r_lnt])

        def layer_norm_tile(z_ap, z_res, out_ap, out_res, npart, scr, scr_res, lnt, r_lnt):
            for c in range(2):
                S.op("dve", lambda e, c=c: e.bn_stats(out=scr[0:npart, c * 6:(c + 1) * 6], in_=z_ap[:, c * 512:(c + 1) * 512]), r=[z_res], w=[scr_res])
            mv = scr[0:npart, 12:14]
            S.op("dve", lambda e: e.bn_aggr(out=mv, in_=scr[0:npart, 0:12]), r=[scr_res], w=[scr_res])
            sd = scr[0:npart, 14:15]
            S.op("act", lambda e: e.activation(out=sd, in_=scr[0:npart, 13:14], func=AF.Sqrt, bias=eps_ln[0:npart], scale=1.0), r=[scr_res, r_eps], w=[scr_res])
            S.op("dve", lambda e: e.reciprocal(out=sd, in_=sd), r=[scr_res], w=[scr_res])
            nmr = scr[0:npart, 15:16]
            S.op("dve", lambda e: e.tensor_scalar(out=nmr, in0=scr[0:npart, 12:13], scalar1=sd, scalar2=-1.0, op0=ALU.mult, op1=ALU.mult), r=[scr_res], w=[scr_res])
            S.op("act", lambda e: e.activation(out=z_ap, in_=z_ap, func=AF.Identity, bias=nmr, scale=sd), r=[scr_res, z_res], w=[z_res])
            S.op("pool", lambda e: e.tensor_tensor(out=z_ap, in0=z_ap, in1=lnt[0:npart, 0:D], op=ALU.mult), r=[z_res, r_lnt], w=[z_res])
            S.op("pool", lambda e: e.tensor_tensor(out=out_ap, in0=z_ap, in1=lnt[0:npart, D:2 * D], op=ALU.add), r=[z_res, r_lnt], w=[out_res])

        def transpose_x(src_ap, src_res, dst_ap, dst_res, npart=128):
            for half in range(2):
                b, br = ring.get(1)
                pv = bank(b)[:, 0:4 * npart].rearrange("p (a b) -> p a b", a=4)
                for c in range(4):
                    cc = half * 4 + c
                    S.op("pe", lambda e, c=c, cc=cc, pv=pv: e.transpose(out=pv[:, c, :], in_=src_ap[:, cc * 128:(cc + 1) * 128], identity=ident_f[0:npart, 0:npart]),
                         r=[src_res, r_idf], w=br)
                evac("act" if half == 0 else "dve", dst_ap[:, half * 4:half * 4 + 4, :], pv, br, [dst_res])

        def chk(tag):
            if stop == tag:
                raise _Stop()

        try:
            for t in range(NT):
                S.dma("sp", lambda e, t=t: e.dma_start(out=X[:, t, :], in_=xp[t * 128:(t + 1) * 128, :]), w=[rX[t]])
            S.dma("sp", lambda e: e.dma_start(out=XS[:], in_=xs), w=[r_XS])

            A.at(0)
            w_in_sb, r_win = A.get(8 * 2560, BF16, "w_in"); w_in_sb = w_in_sb.rearrange("p (k n) -> p k n", k=8)
            w_o_sb, r_wo = A.get(8 * 1024, BF16, "w_o0"); w_o_sb = w_o_sb.rearrange("p (k n) -> p k n", k=8)
            poolw_sb, r_poolw = A.get(512, BF16, "poolw"); poolw_sb = poolw_sb.rearrange("p (g d) -> p g d", g=4)
            c0buf, r_c0 = A.get(1024 + 1024 + 4 + 60 + 4 + 512, F32, "c0")
            decT = c0buf[:, 0:512].rearrange("p (h l) -> p h l", h=4)
            qdec = c0buf[:, 512:1024].rearrange("p (h l) -> p h l", h=4)
            kdec = c0buf[:, 1024:1028]
            invcnt = c0buf[:, 1028:1088].rearrange("p (g t) -> p g t", g=4)
            pscale = c0buf[:, 1088:1092]
            gng = c0buf[:, 1092:1604]
            lnt0, r_lnt0 = A.get(2 * D, F32, "lnt0")
            w_in_v = w_in.rearrange("(k p) n -> p k n", p=128)
            for hh in range(2):
                S.dma("pool", lambda e, hh=hh: e.dma_start(out=w_in_sb[:, :, hh * 1280:(hh + 1) * 1280], in_=w_in_v[:, :, hh * 1280:(hh + 1) * 1280]), w=[r_win])
            S.dma("pool", lambda e: e.dma_start(out=poolw_sb, in_=pool_w), w=[r_poolw])
            S.dma("pool", lambda e: e.dma_start(out=w_o_sb, in_=w_o0.rearrange("(k p) n -> p k n", p=128)), w=[r_wo])
            S.dma("sp", lambda e: e.dma_start(out=decT, in_=tdr["decTp"]), w=[r_c0])
            S.dma("sp", lambda e: e.dma_start(out=qdec, in_=tdr["qdecp"]), w=[r_c0])
            S.dma("sp", lambda e: e.dma_start(out=kdec, in_=tdr["kdecp"]), w=[r_c0])
            S.dma("sp", lambda e: e.dma_start(out=invcnt, in_=tdr["invcnt"]), w=[r_c0])
            S.dma("sp", lambda e: e.dma_start(out=pscale, in_=pool_scale), w=[r_c0])
            S.dma("sp", lambda e: e.dma_start(out=gng, in_=gn_g.partition_broadcast(128)), w=[r_c0])
            load_ln(lnt0, r_lnt0, ln_mix_g[0], ln_mix_b[0])
            L0_BUF = A.cur

            def dbl(nelem, dt, name, shape=None, **kw):
                out = []
                for i in range(2):
                    v, r = A.get(nelem, dt, f"{name}{i}")
                    if shape:
                        v = v.rearrange(shape, **kw)
                    out.append((v, r))
                return [x[0] for x in out], [x[1] for x in out]

            xT, r_xT = dbl(1024, BF16, "xT", "p (k t) -> p k t", k=8)
            ropet, r_ropet = dbl(256, F32, "ropet")
            uext, r_uext = dbl(4 * 143 + 1, F32, "uext")
            uext = [u[:, 0:572].rearrange("p (g t) -> p g t", g=4) for u in uext]
            tA, r_tA = A.get(573, F32, "tA"); tA = tA[:, 0:572].rearrange("p (g t) -> p g t", g=4)
            tB, r_tB = A.get(573, F32, "tB"); tB = tB[:, 0:572].rearrange("p (g t) -> p g t", g=4)
            pooledT, r_pooled = dbl(512, BF16, "pooledT", "p (g t) -> p g t", g=4)
            qk_f, r_qkf = A.get(1024, F32, "qk_f"); qk_f = qk_f.rearrange("p (h t) -> p h t", h=8)
            ropeA, r_ropeA = A.get(1024, F32, "ropeA"); ropeA = ropeA.rearrange("p (h t) -> p h t", h=8)
            ropeB, r_ropeB = A.get(1024, F32, "ropeB"); ropeB = ropeB.rearrange("p (h t) -> p h t", h=8)
            qkT, r_qkT = dbl(1024, BF16, "qkT", "p (h t) -> p h t", h=8)
            qdT, r_qdT = dbl(512, BF16, "qdT", "p (h t) -> p h t", h=4)
            v_sb, r_v = dbl(512, BF16, "v")
            sgg, r_sgg = dbl(512, F32, "sgg")
            sT_bf, r_sT = dbl(512, BF16, "sT", "p (h t) -> p h t", h=4)
            kp_bf, r_kp = dbl(512, BF16, "kp", "p (h t) -> p h t", h=4)
            S_f, r_Sf = A.get(512, F32, "S_f"); S_f = S_f.rearrange("p (h t) -> p h t", h=4)
            S_b, r_Sb = A.get(512, BF16, "S_b"); S_b = S_b.rearrange("p (h t) -> p h t", h=4)
            on1, r_on1 = A.get(512, F32, "on1")
            ret_bf, r_ret = dbl(512, BF16, "ret")
            catT, r_cat = dbl(1024, BF16, "catT", "p (c t) -> p c t", c=8)
            z0_, r_z0_ = A.get(D, F32, "z"); z = [z0_, z0_]; r_z = [r_z0_, r_z0_]
            scr, r_scr = dbl(16, F32, "scr")
            gst, r_gst = A.get(32, F32, "gst")
            poolo, r_poolo = A.get(512, F32, "poolo")
            L0_END = A.cur
            print("layer0 arena bytes", L0_END)

            S.op("pool", lambda e: e.memset(uext[1][:, :, 128:143], 0.0), w=[r_uext[1]])

            def l0A(t):
                pb = t % 2
                tok = slice(t * 128, (t + 1) * 128)
                S.dma("sp", lambda e, pb=pb, tok=tok: e.dma_start(out=ropet[pb][:, 0:128], in_=tdr["c2p"][:, tok]), w=[r_ropet[pb]])
                S.dma("sp", lambda e, pb=pb, tok=tok: e.dma_start(out=ropet[pb][:, 128:256], in_=tdr["ssp"][:, tok]), w=[r_ropet[pb]])
                transpose_x(X[:, t, :], rX[t], xT[pb], r_xT[pb])
                bu, ru = ring.get(1)
                bq, rq = ring.get(1)
                bk, rk = ring.get(1)
                bv, rv = ring.get(1)
                bg, rg = ring.get(1)
                for (bb, rr, c0) in ((bu, ru, 0), (bq, rq, 512), (bk, rk, 1024)):
                    pv = bank(bb).rearrange("p (h t) -> p h t", h=4)
                    for h in range(4):
                        for k in range(8):
                            S.op("pe", lambda e, pv=pv, h=h, k=k, c0=c0, pb=pb: e.matmul(pv[:, h, :], lhsT=w_in_sb[:, k, c0 + h * 128:c0 + (h + 1) * 128], rhs=xT[pb][:, k, :], start=(k == 0), stop=(k == 7)),
                                 r=[r_win, r_xT[pb]], w=rr)
                for (bb, rr, c0) in ((bv, rv, 1536), (bg, rg, 2048)):
                    for k in range(8):
                        S.op("pe", lambda e, bb=bb, k=k, c0=c0, pb=pb: e.matmul(bank(bb), lhsT=xT[pb][:, k, :], rhs=w_in_sb[:, k, c0:c0 + 512], start=(k == 0), stop=(k == 7)),
                             r=[r_win, r_xT[pb]], w=rr)
                ue = uext[pb]
                S.op("act", lambda e, ue=ue, bu=bu: e.copy(out=ue[:, :, 15:143], in_=bank(bu).rearrange("p (h t) -> p h t", h=4)), r=ru, w=[r_uext[pb]])
                S.op("pool", lambda e, ue=ue, pb=pb: e.tensor_copy(out=ue[:, :, 0:15], in_=uext[1 - pb][:, :, 128:143]), r=[r_uext[1 - pb]], w=[r_uext[pb]])
                S.op("pool", lambda e, ue=ue: e.tensor_tensor(out=tA[:, :, 1:143], in0=ue[:, :, 1:143], in1=ue[:, :, 0:142], op=ALU.add), r=[r_uext[pb]], w=[r_tA])
                S.op("pool", lambda e: e.tensor_tensor(out=tB[:, 1:4, 3:143], in0=tA[:, 1:4, 3:143], in1=tA[:, 1:4, 1:141], op=ALU.add), r=[r_tA], w=[r_tB])
                S.op("pool", lambda e: e.tensor_tensor(out=tA[:, 2:4, 7:143], in0=tB[:, 2:4, 7:143], in1=tB[:, 2:4, 3:139], op=ALU.add), r=[r_tB], w=[r_tA])
                S.op("pool", lambda e: e.tensor_tensor(out=tB[:, 3:4, 15:143], in0=tA[:, 3:4, 15:143], in1=tA[:, 3:4, 7:135], op=ALU.add), r=[r_tA], w=[r_tB])
                fin = (tA, tB, tA, tB)
                for g in range(4):
                    S.op("dve", lambda e, g=g, ue=ue, pb=pb: e.scalar_tensor_tensor(out=pooledT[pb][:, g, :], in0=fin[g][:, g, 15:143], scalar=1.0 / POOL_W[g], in1=ue[:, g, 15:143], op0=ALU.mult, op1=ALU.subtract),
                         r=[r_tA, r_tB, r_uext[pb]], w=[r_pooled[pb]])
                if t == 0:
                    for g in range(4):
                        S.op("dve", lambda e, g=g: e.tensor_tensor(out=ropeA[:, g, 0:15], in0=fin[g][:, g, 15:30], in1=invcnt[:, g, :], op=ALU.mult), r=[r_tA, r_tB, r_c0], w=[r_ropeA])
                        S.op("dve", lambda e, g=g, ue=ue, pb=pb: e.tensor_tensor(out=pooledT[pb][:, g, 0:15], in0=ropeA[:, g, 0:15], in1=ue[:, g, 15:30], op=ALU.subtract), r=[r_ropeA, r_uext[pb]], w=[r_pooled[pb]])
                bpm, rpm = ring.get(1)
                pmv = bank(bpm).rearrange("p (h t) -> p h t", h=4)
                for g in range(4):
                    S.op("pe", lambda e, g=g, pmv=pmv, pb=pb: e.matmul(pmv[:, g, :], lhsT=poolw_sb[:, g, :], rhs=pooledT[pb][:, g, :], start=True, stop=True), r=[r_poolw, r_pooled[pb]], w=rpm)
                for g in range(4):
                    S.op("act", lambda e, g=g, pmv=pmv, pb=pb: e.activation(out=catT[pb][:, g, :], in_=pmv[:, g, :], func=AF.Identity, bias=0.0, scale=pscale[:, g:g + 1]), r=rpm + [r_c0], w=[r_cat[pb]])
                if t == NT - 1:
                    bpo, rpo = ring.get(1)
                    for g in range(4):
                        S.op("pe", lambda e, g=g, ue=ue, bpo=bpo: e.transpose(out=bank(bpo)[0:15, g * 128:(g + 1) * 128], in_=ue[:, g, 128:143], identity=ident_f[:]), r=[r_uext[pb], r_idf], w=rpo)
                    S.op("act", lambda e, bpo=bpo: e.copy(out=poolo[0:15, :], in_=bank(bpo)[0:15, :]), r=rpo, w=[r_poolo])
                    out_toks.append(S.dma("sp", lambda e: e.dma_start(out=poolp, in_=poolo[0:15, :]), r=[r_poolo]))
                S.op("act", lambda e, bq=bq: e.copy(out=qk_f[:, 0:4, :], in_=bank(bq).rearrange("p (h t) -> p h t", h=4)), r=rq, w=[r_qkf])
                S.op("act", lambda e, bk=bk: e.copy(out=qk_f[:, 4:8, :], in_=bank(bk).rearrange("p (h t) -> p h t", h=4)), r=rk, w=[r_qkf])
                c2b = ropet[pb][:, 0:128].unsqueeze(1)
                ssb = ropet[pb][:, 128:256].unsqueeze(1)
                S.op("dve", lambda e, c2b=c2b: e.tensor_tensor(out=ropeA, in0=qk_f, in1=c2b.to_broadcast([128, 8, 128]), op=ALU.mult), r=[r_qkf, r_ropet[pb]], w=[r_ropeA])
                S.op("pool", lambda e, ssb=ssb: e.tensor_tensor(out=ropeB[0:64], in0=qk_f[64:128], in1=ssb[64:128].to_broadcast([64, 8, 128]), op=ALU.mult), r=[r_qkf, r_ropet[pb]], w=[r_ropeB])
                S.op("pool", lambda e, ssb=ssb: e.tensor_tensor(out=ropeB[64:128], in0=qk_f[0:64], in1=ssb[0:64].to_broadcast([64, 8, 128]), op=ALU.mult), r=[r_qkf, r_ropet[pb]], w=[r_ropeB])
                S.op("dve", lambda e, pb=pb: e.tensor_tensor(out=qkT[pb], in0=ropeA, in1=ropeB, op=ALU.add), r=[r_ropeA, r_ropeB], w=[r_qkT[pb]])
                if t > 0:
                    S.op("pool", lambda e, pb=pb: e.tensor_tensor(out=qdT[pb], in0=qkT[pb][:, 0:4, :], in1=qdec, op=ALU.mult), r=[r_qkT[pb], r_c0], w=[r_qdT[pb]])
                S.op("act", lambda e, pb=pb, bv=bv: e.copy(out=v_sb[pb], in_=bank(bv)), r=rv, w=[r_v[pb]])
                S.op("act", lambda e, pb=pb, bg=bg: e.activation(out=sgg[pb], in_=bank(bg), func=AF.Silu), r=rg, w=[r_sgg[pb]])
                S.op("pool", lambda e, pb=pb: e.tensor_tensor(out=sgg[pb], in0=sgg[pb], in1=gng, op=ALU.mult), r=[r_sgg[pb], r_c0], w=[r_sgg[pb]])
            def l0B(t):
                pb = t % 2
                bs, rs = ring.get(1)
                sv = bank(bs).rearrange("p (h t) -> p h t", h=4)
                for h in range(4):
                    S.op("pe", lambda e, h=h, sv=sv, pb=pb: e.matmul(sv[:, h, :], lhsT=qkT[pb][:, 4 + h, :], rhs=qkT[pb][:, h, :], start=True, stop=True), r=[r_qkT[pb]], w=rs)
                S.op("dve", lambda e, sv=sv, pb=pb: e.tensor_tensor(out=sT_bf[pb], in0=sv, in1=decT, op=ALU.mult), r=rs + [r_c0], w=[r_sT[pb]])
                bo, ro = ring.get(1)
                ov = bank(bo).rearrange("p (h t) -> p h t", h=4)
                for h in range(4):
                    S.op("pe", lambda e, h=h, ov=ov, pb=pb, t=t: e.matmul(ov[:, h, :], lhsT=sT_bf[pb][:, h, :], rhs=v_sb[pb][:, h * 128:(h + 1) * 128], start=True, stop=(t == 0)), r=[r_sT[pb], r_v[pb]], w=ro)
                    if t > 0:
                        S.op("pe", lambda e, h=h, ov=ov, pb=pb: e.matmul(ov[:, h, :], lhsT=qdT[pb][:, h, :], rhs=S_b[:, h, :], start=False, stop=True), r=[r_qdT[pb], r_Sb], w=ro)
                bkt, rkt = ring.get(1)
                ktv = bankbf(bkt)[:, 0:512].rearrange("p (h t) -> p h t", h=4)
                for h in range(4):
                    S.op("pe", lambda e, h=h, ktv=ktv, pb=pb: e.transpose(out=ktv[:, h, :], in_=qkT[pb][:, 4 + h, :], identity=ident_b[:]), r=[r_qkT[pb], r_idb], w=rkt)
                for h in range(4):
                    S.op("act", lambda e, h=h, ktv=ktv, pb=pb: e.activation(out=kp_bf[pb][:, h, :], in_=ktv[:, h, :], func=AF.Identity, bias=0.0, scale=kdec[:, h:h + 1]), r=rkt + [r_c0], w=[r_kp[pb]])
                bds, rds = ring.get(1)
                dsv = bank(bds).rearrange("p (h t) -> p h t", h=4)
                for h in range(4):
                    S.op("pe", lambda e, h=h, dsv=dsv, pb=pb: e.matmul(dsv[:, h, :], lhsT=kp_bf[pb][:, h, :], rhs=v_sb[pb][:, h * 128:(h + 1) * 128], start=True, stop=True), r=[r_kp[pb], r_v[pb]], w=rds)
                if t == 0:
                    S.op("dve", lambda e, dsv=dsv: e.tensor_copy(out=S_f, in_=dsv), r=rds, w=[r_Sf])
                else:
                    for h in range(4):
                        S.op("dve", lambda e, h=h, dsv=dsv: e.scalar_tensor_tensor(out=S_f[:, h, :], in0=S_f[:, h, :], scalar=tabs["gCp"][h], in1=dsv[:, h, :], op0=ALU.mult, op1=ALU.add), r=rds + [r_Sf], w=[r_Sf])
                if t < NT - 1:
                    S.op("pool", lambda e: e.tensor_copy(out=S_b, in_=S_f), r=[r_Sf], w=[r_Sb])
                else:
                    out_toks.append(S.dma("sp", lambda e: e.dma_start(out=retp.rearrange("h k v -> k h v"), in_=S_f), r=[r_Sf]))
                for h in range(4):
                    S.op("dve", lambda e, h=h, ov=ov: e.bn_stats(out=gst[:, h * 6:(h + 1) * 6], in_=ov[:, h, :]), r=ro, w=[r_gst])
                    S.op("dve", lambda e, h=h: e.bn_aggr(out=gst[:, 24 + 2 * h:26 + 2 * h], in_=gst[:, h * 6:(h + 1) * 6]), r=[r_gst], w=[r_gst])
                mvv = gst[:, 24:32].rearrange("p (h two) -> p h two", two=2)
                S.op("act", lambda e, pb=pb: e.activation(out=scr[pb][:, 0:4], in_=mvv[:, :, 1], func=AF.Sqrt, bias=eps_gn, scale=1.0), r=[r_gst, r_eps], w=[r_scr[pb]])
                S.op("dve", lambda e, pb=pb: e.reciprocal(out=scr[pb][:, 0:4], in_=scr[pb][:, 0:4]), r=[r_scr[pb]], w=[r_scr[pb]])
                S.op("dve", lambda e, pb=pb: e.scalar_tensor_tensor(out=scr[pb][:, 4:8], in0=mvv[:, :, 0], scalar=-1.0, in1=scr[pb][:, 0:4], op0=ALU.mult, op1=ALU.mult), r=[r_gst, r_scr[pb]], w=[r_scr[pb]])
                for h in range(4):
                    S.op("act", lambda e, h=h, ov=ov, pb=pb: e.activation(out=on1[:, h * 128:(h + 1) * 128], in_=ov[:, h, :], func=AF.Identity, bias=scr[pb][:, 4 + h:5 + h], scale=scr[pb][:, h:h + 1]), r=ro + [r_scr[pb]], w=[r_on1])
                S.op("pool", lambda e, pb=pb: e.tensor_tensor(out=ret_bf[pb], in0=on1, in1=sgg[pb], op=ALU.mult), r=[r_on1, r_sgg[pb]], w=[r_ret[pb]])
                brt, rrt = ring.get(1)
                rtv = bankbf(brt)[:, 0:512].rearrange("p (h t) -> p h t", h=4)
                for h in range(4):
                    S.op("pe", lambda e, h=h, rtv=rtv, pb=pb: e.transpose(out=rtv[:, h, :], in_=ret_bf[pb][:, h * 128:(h + 1) * 128], identity=ident_b[:]), r=[r_ret[pb], r_idb], w=rrt)
                S.op("dve", lambda e, rtv=rtv, pb=pb: e.tensor_copy(out=catT[pb][:, 4:8, :], in_=rtv), r=rrt, w=[r_cat[pb]])
                by, ry = ring.get(2)
                for hf in range(2):
                    for c in range(8):
                        S.op("pe", lambda e, hf=hf, c=c, by=by, pb=pb: e.matmul(bank(by + hf), lhsT=catT[pb][:, c, :], rhs=w_o_sb[:, c, hf * 512:(hf + 1) * 512], start=(c == 0), stop=(c == 7)), r=[r_cat[pb], r_wo], w=[ry[hf]])
                S.op("dve", lambda e, by=by, pb=pb, t=t: e.scalar_tensor_tensor(out=z[pb], in0=X[:, t, :], scalar=ALPHA, in1=bank(by, 2), op0=ALU.mult, op1=ALU.add), r=[rX[t]] + ry, w=[r_z[pb]])
                layer_norm_tile(z[pb], r_z[pb], X[:, t, :], rX[t], 128, scr[pb], r_scr[pb], lnt0, r_lnt0)

            l0A(0)
            for t in range(NT):
                if t + 1 < NT:
                    l0A(t + 1)
                l0B(t)

            def ffn(layer, final):
                A.at(0)
                gated, r_gated = A.get(NJ * 1024, BF16, "gated", nres=NJ); gated = gated.rearrange("p (j n) -> p j n", j=NJ)
                x1T, r_x1T = A.get(8 * 1024, BF16, "x1T", nres=8); x1T = x1T.rearrange("p (k n) -> p k n", k=8)
                cb0_, r_cb0_ = A.get(512, F32, "cbuf"); cbuf = [cb0_, cb0_]; r_cbuf = [r_cb0_, r_cb0_]
                sb0_, r_sb0_ = A.get(512, F32, "sbf"); sbf = [sb0_, sb0_]; r_sbf = [r_sb0_, r_sb0_]
                zf0, r_zf0 = A.get(D, F32, "zf"); zf = [zf0, zf0]; r_zf = [r_zf0, r_zf0]
                scf, r_scf = dbl(16, F32, "scf")
                cwb, r_cw = A.get(NJ * 4, F32, "cw"); cw = cwb[:, 0:NJ * 3].rearrange("p (j c) -> p j c", j=NJ); cb = cwb[:, NJ * 3:NJ * 4]
                hist, r_hist = A.get(NJ * 2, F32, "hist"); hist = hist.rearrange("p (j c) -> p j c", j=NJ)
                cst, r_cst = A.get(NJ * 2, F32, "cst"); cst = cst.rearrange("p (j c) -> p j c", j=NJ)
                cso, r_cso = A.get(512, F32, "cso")
                lnt, r_lnt = A.get(2 * D, F32, "lntf")
                wup = []
                r_wup = []
                for i in range(2):
                    v, r = A.get(8 * 256, BF16, f"wup{i}")
                    wup.append(v.rearrange("p (k n) -> p k n", k=8)); r_wup.append(r)
                wdn, r_wdn = A.get(NJ * 1024, BF16, "wdn"); wdn = wdn.rearrange("p (j n) -> p j n", j=NJ)
                if with_sample:
                    xs1T, r_xs1T = A.get(128, BF16, "xs1T"); xs1T = xs1T.rearrange("p (k t) -> p k t", k=8)
                    aext, r_aext = A.get(NJ * 24, F32, "aext"); aext = aext.rearrange("p (j b r) -> p j b r", j=NJ, b=4)
                    b_s, r_bs = A.get(NJ * 16, F32, "b_s"); b_s = b_s.rearrange("p (j t) -> p j t", j=NJ)
                    scT, r_scT = A.get(NJ * 8, F32, "scT"); scT = scT.rearrange("p (j r) -> p j r", j=NJ)
                    c_s, r_cs = A.get(NJ * 16, F32, "c_s"); c_s = c_s.rearrange("p (j b l) -> p j b l", j=NJ, b=4)
                    t_s, r_ts = A.get(NJ * 16, F32, "t_s"); t_s = t_s.rearrange("p (j b l) -> p j b l", j=NJ, b=4)
                    gated_s, r_gs = A.get(NJ * 16, BF16, "gated_s"); gated_s = gated_s.rearrange("p (j t) -> p j t", j=NJ)
                print("ffn arena bytes", A.cur)
                S.dma("sp", lambda e: e.dma_start(out=cw, in_=conv_w[layer]), w=[r_cw])
                S.dma("sp", lambda e: e.dma_start(out=cb, in_=conv_b[layer]), w=[r_cw])
                load_ln(lnt, r_lnt, ln_ffn_g[layer], ln_ffn_b[layer])
                wdn_v = w_down[layer].rearrange("(j p) n -> p j n", p=128)
                wdn_loaded = [False]

                def load_wdn():
                    for jj in range(0, NJ, 2):
                        S.dma("pool", lambda e, jj=jj: e.dma_start(out=wdn[:, jj:jj + 2, :], in_=wdn_v[:, jj:jj + 2, :]), w=[r_wdn])

                def load_wup(j, slot):
                    S.dma("pool", lambda e, j=j, slot=slot: e.dma_start(out=wup[slot], in_=w_up[layer, j]), w=[r_wup[slot]])

                if with_sample:
                    transpose_x(XS[:], r_XS, xs1T, r_xs1T, npart=NS)
                    for jg in range(0, NJ, 4):
                        n = min(4, NJ - jg)
                        S.dma("sp", lambda e, jg=jg, n=n: e.dma_start(out=cso[0:8, 0:n * 128], in_=sconv[layer][:, jg * 128:(jg + n) * 128]), w=[r_cso])
                        bst, rst = ring.get(1)
                        for i_ in range(n):
                            S.op("pe", lambda e, i_=i_, bst=bst: e.transpose(out=bank(bst)[:, i_ * 8:(i_ + 1) * 8], in_=cso[0:8, i_ * 128:(i_ + 1) * 128], identity=ident_f[0:8, 0:8]), r=[r_cso, r_idf], w=rst)
                        S.op("act", lambda e, jg=jg, n=n, bst=bst: e.copy(out=scT[:, jg:jg + n, :], in_=bank(bst)[:, 0:n * 8].rearrange("p (j r) -> p j r", j=n)), r=rst, w=[r_scT])
                    S.op("pool", lambda e: e.tensor_copy(out=aext[:, :, :, 0:2], in_=scT.rearrange("p j (b r) -> p j b r", b=4)), r=[r_scT], w=[r_aext])
                cnt = 0
                for half in range(2):
                    for tt in range(8):
                        t = half * 8 + tt
                        transpose_x(X[:, t, :], rX[t], x1T[:, :, tt * 128:(tt + 1) * 128], r_x1T[tt])
                    load_wup(0, cnt % 2)
                    for j in range(NJ):
                        slot = cnt % 2
                        if j + 1 < NJ:
                            load_wup(j + 1, (cnt + 1) % 2)
                        elif half == 0:
                            pass
                        if half == 0 and j == 1:
                            load_wdn()
                        cnt += 1
                        for grp in range(2):
                            ba, ra = ring.get(1)
                            bb_, rb = ring.get(1)
                            cols = slice(grp * 512, (grp + 1) * 512)
                            for (bx, rx, c0) in ((ba, ra, 0), (bb_, rb, 128)):
                                for k in range(8):
                                    S.op("pe", lambda e, bx=bx, k=k, c0=c0, slot=slot, cols=cols: e.matmul(bank(bx), lhsT=wup[slot][:, k, c0:c0 + 128], rhs=x1T[:, k, cols], start=(k == 0), stop=(k == 7)),
                                         r=[r_wup[slot]] + r_x1T[grp * 4:grp * 4 + 4], w=rx)
                            pa = bank(ba)
                            cp = (j * 2 + grp) % 2
                            c_ = cbuf[cp]
                            s_ = sbf[cp]
                            S.op("act", lambda e, pa=pa, c_=c_, j=j: e.activation(out=c_, in_=pa, func=AF.Identity, bias=cb[:, j:j + 1], scale=cw[:, j, 2:3]), r=ra + [r_cw], w=[r_cbuf[cp]])
                            S.op("dve", lambda e, pa=pa, c_=c_, j=j: e.scalar_tensor_tensor(out=c_[:, 1:512], in0=pa[:, 0:511], scalar=cw[:, j, 1:2], in1=c_[:, 1:512], op0=ALU.mult, op1=ALU.add), r=ra + [r_cw, r_cbuf[cp]], w=[r_cbuf[cp]])
                            S.op("dve", lambda e, pa=pa, c_=c_, j=j: e.scalar_tensor_tensor(out=c_[:, 2:512], in0=pa[:, 0:510], scalar=cw[:, j, 0:1], in1=c_[:, 2:512], op0=ALU.mult, op1=ALU.add), r=ra + [r_cw, r_cbuf[cp]], w=[r_cbuf[cp]])
                            if not (half == 0 and grp == 0):
                                S.op("dve", lambda e, c_=c_, j=j: e.scalar_tensor_tensor(out=c_[:, 0:1], in0=hist[:, j, 1:2], scalar=cw[:, j, 1:2], in1=c_[:, 0:1], op0=ALU.mult, op1=ALU.add), r=[r_hist, r_cw, r_cbuf[cp]], w=[r_cbuf[cp]])
                                S.op("dve", lambda e, c_=c_, j=j: e.scalar_tensor_tensor(out=c_[:, 0:2], in0=hist[:, j, 0:2], scalar=cw[:, j, 0:1], in1=c_[:, 0:2], op0=ALU.mult, op1=ALU.add), r=[r_hist, r_cw, r_cbuf[cp]], w=[r_cbuf[cp]])
                            if half == 1 and grp == 1:
                                S.op("act", lambda e, pa=pa, j=j: e.copy(out=cst[:, j, :], in_=pa[:, 510:512]), r=ra, w=[r_cst])
                            else:
                                S.op("act", lambda e, pa=pa, j=j: e.copy(out=hist[:, j, :], in_=pa[:, 510:512]), r=ra, w=[r_hist])
                            S.op("act", lambda e, c_=c_, s_=s_: e.activation(out=s_, in_=c_, func=AF.Silu), r=[r_cbuf[cp]], w=[r_sbf[cp]])
                            S.op("dve", lambda e, s_=s_, bb_=bb_, j=j, cols=cols: e.tensor_tensor(out=gated[:, j, cols], in0=s_, in1=bank(bb_), op=ALU.mult), r=[r_sbf[cp]] + rb, w=[r_gated[j]])
                        if with_sample and half == 0:
                            bsa, rsa = ring.get(1)
                            for (c0, col) in ((0, 0), (128, 16)):
                                for k in range(8):
                                    S.op("pe", lambda e, k=k, c0=c0, col=col, slot=slot, bsa=bsa: e.matmul(bank(bsa)[:, col:col + 16], lhsT=wup[slot][:, k, c0:c0 + 128], rhs=xs1T[:, k, :], start=(k == 0), stop=(k == 7)), r=[r_wup[slot], r_xs1T], w=rsa)
                            S.op("act", lambda e, j=j, bsa=bsa: e.copy(out=aext[:, j, :, 2:6], in_=bank(bsa)[:, 0:16].rearrange("p (b l) -> p b l", b=4)), r=rsa, w=[r_aext])
                            S.op("dve", lambda e, j=j, bsa=bsa: e.tensor_copy(out=b_s[:, j, :], in_=bank(bsa)[:, 16:32]), r=rsa, w=[r_bs])
                    if with_sample and half == 0:
                        def cwb(c):
                            return cw[:, :, c:c + 1].unsqueeze(3).to_broadcast([128, NJ, 4, 4])
                        S.op("dve", lambda e: e.tensor_tensor(out=c_s, in0=aext[:, :, :, 2:6], in1=cwb(2), op=ALU.mult), r=[r_aext, r_cw], w=[r_cs])
                        S.op("pool", lambda e: e.tensor_tensor(out=t_s, in0=aext[:, :, :, 1:5], in1=cwb(1), op=ALU.mult), r=[r_aext, r_cw], w=[r_ts])
                        S.op("dve", lambda e: e.tensor_tensor(out=c_s, in0=c_s, in1=t_s, op=ALU.add), r=[r_cs, r_ts], w=[r_cs])
                        S.op("pool", lambda e: e.tensor_tensor(out=t_s, in0=aext[:, :, :, 0:4], in1=cwb(0), op=ALU.mult), r=[r_aext, r_cw], w=[r_ts])
                        S.op("dve", lambda e: e.tensor_tensor(out=c_s, in0=c_s, in1=t_s, op=ALU.add), r=[r_cs, r_ts], w=[r_cs])
                        S.op("dve", lambda e: e.tensor_tensor(out=c_s, in0=c_s, in1=cb.unsqueeze(2).unsqueeze(3).to_broadcast([128, NJ, 4, 4]), op=ALU.add), r=[r_cs, r_cw], w=[r_cs])
                        S.op("act", lambda e: e.activation(out=c_s, in_=c_s, func=AF.Silu), r=[r_cs], w=[r_cs])
                        S.op("dve", lambda e: e.tensor_tensor(out=gated_s, in0=c_s.rearrange("p j b l -> p j (b l)"), in1=b_s, op=ALU.mult), r=[r_cs, r_bs], w=[r_gs])
                        S.op("pool", lambda e: e.tensor_copy(out=scT.rearrange("p j (b r) -> p j b r", b=4), in_=aext[:, :, :, 4:6]), r=[r_aext], w=[r_scT])
                        for jg in range(0, NJ, 4):
                            n = min(4, NJ - jg)
                            bst, rst = ring.get(1)
                            for i_ in range(n):
                                S.op("pe", lambda e, i_=i_, jg=jg, bst=bst: e.transpose(out=bank(bst)[0:8, i_ * 128:(i_ + 1) * 128], in_=scT[:, jg + i_, :], identity=ident_f[:]), r=[r_scT, r_idf], w=rst)
                            S.op("act", lambda e, n=n, bst=bst: e.copy(out=cso[0:8, 0:n * 128], in_=bank(bst)[0:8, 0:n * 128]), r=rst, w=[r_cso])
                            out_toks.append(S.dma("sp", lambda e, jg=jg, n=n: e.dma_start(out=convs[layer][:, jg * 128:(jg + n) * 128], in_=cso[0:8, 0:n * 128]), r=[r_cso]))
                    for tt in range(8):
                        t = half * 8 + tt
                        pb = tt % 2
                        by, ry = ring.get(2)
                        for hf in range(2):
                            for j in range(NJ):
                                S.op("pe", lambda e, hf=hf, j=j, by=by, tt=tt: e.matmul(bank(by + hf), lhsT=gated[:, j, tt * 128:(tt + 1) * 128], rhs=wdn[:, j, hf * 512:(hf + 1) * 512], start=(j == 0), stop=(j == NJ - 1)),
                                     r=[r_gated[j], r_wdn], w=[ry[hf]])
                        S.op("dve", lambda e, by=by, pb=pb, t=t: e.scalar_tensor_tensor(out=zf[pb], in0=X[:, t, :], scalar=ALPHA, in1=bank(by, 2), op0=ALU.mult, op1=ALU.add), r=[rX[t]] + ry, w=[r_zf[pb]])
                        layer_norm_tile(zf[pb], r_zf[pb], X[:, t, :], rX[t], 128, scf[pb], r_scf[pb], lnt, r_lnt)
                        if final:
                            out_toks.append(S.dma("sp", lambda e, t=t: e.dma_start(out=yp[t * 128:(t + 1) * 128, :], in_=X[:, t, :]), r=[rX[t]]))
                if with_sample:
                    bys, rys = ring.get(2)
                    for hf in range(2):
                        for j in range(NJ):
                            S.op("pe", lambda e, hf=hf, j=j, bys=bys: e.matmul(bank(bys + hf)[0:NS, :], lhsT=gated_s[:, j, :], rhs=wdn[:, j, hf * 512:(hf + 1) * 512], start=(j == 0), stop=(j == NJ - 1)), r=[r_gs, r_wdn], w=[rys[hf]])
                    S.op("dve", lambda e, bys=bys: e.scalar_tensor_tensor(out=zf[0][0:NS, :], in0=XS[:], scalar=ALPHA, in1=bank(bys, 2)[0:NS, :], op0=ALU.mult, op1=ALU.add), r=[r_XS] + rys, w=[r_zf[0]])
                    layer_norm_tile(zf[0][0:NS, :], r_zf[0], XS[:], r_XS, NS, scf[0], r_scf[0], lnt, r_lnt)
                    if final:
                        out_toks.append(S.dma("sp", lambda e: e.dma_start(out=ys, in_=XS[:]), r=[r_XS]))
                bc, rc = ring.get(1)
                for j in range(NJ):
                    S.op("pe", lambda e, j=j, bc=bc: e.transpose(out=bank(bc)[0:2, (j % 4) * 128:(j % 4 + 1) * 128], in_=cst[:, j, :], identity=ident_f[:]), r=[r_cst, r_idf], w=rc)
                    if j % 4 == 3 or j == NJ - 1:
                        j0 = j - j % 4
                        n = j - j0 + 1
                        S.op("act", lambda e, bc=bc, n=n: e.copy(out=cso[0:2, 0:n * 128], in_=bank(bc)[0:2, 0:n * 128]), r=rc, w=[r_cso])
                        out_toks.append(S.dma("sp", lambda e, j0=j0, n=n: e.dma_start(out=convp[layer][:, j0 * 128:(j0 + n) * 128], in_=cso[0:2, 0:n * 128]), r=[r_cso]))
                        if j != NJ - 1:
                            bc, rc = ring.get(1)


            def sample_l0():
                A.at(L0_BUF)
                xsT, r_xsT = A.get(128, BF16, "xsT"); xsT = xsT.rearrange("p (k t) -> p k t", k=8)
                tb, r_tb = A.get(68, F32, "stab")
                c2s = tb[:, 0:16]; sss = tb[:, 16:32]
                decTs = tb[0:4, 32:48]
                qdecs = tb[:, 48:64].rearrange("p (h l) -> p h l", h=4)
                kdecs = tb[0:4, 64:68]
                S.dma("sp", lambda e: e.dma_start(out=c2s, in_=tdr["c2s"]), w=[r_tb])
                S.dma("sp", lambda e: e.dma_start(out=sss, in_=tdr["sss"]), w=[r_tb])
                S.dma("sp", lambda e: e.dma_start(out=decTs, in_=tdr["decTs"].rearrange("m h l -> m (h l)")), w=[r_tb])
                S.dma("sp", lambda e: e.dma_start(out=qdecs, in_=tdr["qdecs"]), w=[r_tb])
                S.dma("sp", lambda e: e.dma_start(out=kdecs, in_=tdr["kdecs"]), w=[r_tb])
                sp_sb, r_sp = A.get(512, F32, "sp_sb")
                S.dma("sp", lambda e: e.dma_start(out=sp_sb[0:60, :], in_=spool), w=[r_sp])
                S0f, r_S0f = A.get(16 * 128, F32, "S0f"); S0f = S0f.rearrange("p (n v) -> p n v", n=16)
                S0b, r_S0b = A.get(16 * 128, BF16, "S0b"); S0b = S0b.rearrange("p (n v) -> p n v", n=16)
                S.dma("sp", lambda e: e.dma_start(out=S0f, in_=sret.rearrange("n k v -> k n v")), w=[r_S0f])
                S.op("pool", lambda e: e.tensor_copy(out=S0b, in_=S0f), r=[r_S0f], w=[r_S0b])
                for bb in range(SB):
                    out_toks.append(S.dma("sp", lambda e, bb=bb: e.dma_start(out=pools[bb * 15:bb * 15 + 11, :], in_=spool[bb * 15 + 4:bb * 15 + 15, :]), semkey="d2d"))
                transpose_x(XS[:], r_XS, xsT, r_xsT, npart=NS)
                bq_, rq_ = ring.get(1)
                pqk = bank(bq_)[:, 0:192].rearrange("p (g t) -> p g t", g=12)
                for g in range(12):
                    for k in range(8):
                        S.op("pe", lambda e, g=g, k=k: e.matmul(pqk[:, g, :], lhsT=w_in_sb[:, k, g * 128:(g + 1) * 128], rhs=xsT[:, k, :], start=(k == 0), stop=(k == 7)), r=[r_win, r_xsT], w=rq_)
                bu_, ru_ = ring.get(1)
                for k in range(8):
                    S.op("pe", lambda e, k=k: e.matmul(bank(bu_)[0:NS, :], lhsT=xsT[:, k, :], rhs=w_in_sb[:, k, 0:512], start=(k == 0), stop=(k == 7)), r=[r_win, r_xsT], w=ru_)
                utok, r_utok = A.get(512, F32, "utok")
                S.op("act", lambda e: e.copy(out=utok[0:NS, :], in_=bank(bu_)[0:NS, :]), r=ru_, w=[r_utok])
                for bb in range(SB):
                    out_toks.append(S.dma("sp", lambda e, bb=bb: e.dma_start(out=pools[bb * 15 + 11:bb * 15 + 15, :], in_=utok[bb * 4:bb * 4 + 4, :]), r=[r_utok]))
                bh_, rh_ = ring.get(1)
                for g in range(4):
                    S.op("pe", lambda e, g=g: e.transpose(out=bank(bh_)[:, g * 64:g * 64 + 60], in_=sp_sb[0:60, g * 128:(g + 1) * 128], identity=ident_f[0:60, 0:60]), r=[r_sp, r_idf], w=rh_)
                ue, r_ue = A.get(16 * 19, F32, "ues"); ue4 = ue.rearrange("p (g b r) -> p g b r", g=4, b=4); ue3 = ue.rearrange("p (n r) -> p n r", n=16)
                hv = bank(bh_)[:, 0:256].rearrange("p (g c) -> p g c", g=4)[:, :, 0:60].rearrange("p g (b r) -> p g b r", b=4)
                for g in range(4):
                    S.op("act", lambda e, g=g: e.copy(out=ue4[:, g, :, 0:15], in_=hv[:, g]), r=rh_, w=[r_ue])
                    S.op("act", lambda e, g=g: e.copy(out=ue4[:, g, :, 15:19], in_=pqk[:, g, :].rearrange("p (b l) -> p b l", b=4)), r=rq_, w=[r_ue])
                tAs, r_tAs = A.get(16 * 19, F32, "tAs"); tAs = tAs.rearrange("p (n r) -> p n r", n=16)
                tBs, r_tBs = A.get(16 * 19, F32, "tBs"); tBs = tBs.rearrange("p (n r) -> p n r", n=16)
                S.op("pool", lambda e: e.tensor_tensor(out=tAs[:, :, 1:19], in0=ue3[:, :, 1:19], in1=ue3[:, :, 0:18], op=ALU.add), r=[r_ue], w=[r_tAs])
                S.op("pool", lambda e: e.tensor_tensor(out=tBs[:, 4:16, 3:19], in0=tAs[:, 4:16, 3:19], in1=tAs[:, 4:16, 1:17], op=ALU.add), r=[r_tAs], w=[r_tBs])
                S.op("pool", lambda e: e.tensor_tensor(out=tAs[:, 8:16, 7:19], in0=tBs[:, 8:16, 7:19], in1=tBs[:, 8:16, 3:15], op=ALU.add), r=[r_tBs], w=[r_tAs])
                S.op("pool", lambda e: e.tensor_tensor(out=tBs[:, 12:16, 15:19], in0=tAs[:, 12:16, 15:19], in1=tAs[:, 12:16, 7:11], op=ALU.add), r=[r_tAs], w=[r_tBs])
                pooleds, r_pooleds = A.get(64, BF16, "pooleds"); pooleds = pooleds.rearrange("p (g b l) -> p g b l", g=4, b=4)
                fins = (tAs, tBs, tAs, tBs)
                for g in range(4):
                    S.op("dve", lambda e, g=g: e.scalar_tensor_tensor(out=pooleds[:, g], in0=fins[g][:, g * 4:(g + 1) * 4, 15:19], scalar=1.0 / POOL_W[g], in1=ue3[:, g * 4:(g + 1) * 4, 15:19], op0=ALU.mult, op1=ALU.subtract), r=[r_tAs, r_tBs, r_ue], w=[r_pooleds])
                catTs, r_catTs = A.get(128, BF16, "catTs"); catTs = catTs.rearrange("p (c t) -> p c t", c=8)
                bpm_, rpm_ = ring.get(1)
                pms = bank(bpm_)[:, 0:64].rearrange("p (g t) -> p g t", g=4)
                for g in range(4):
                    S.op("pe", lambda e, g=g: e.matmul(pms[:, g, :], lhsT=poolw_sb[:, g, :], rhs=pooleds[:, g].rearrange("p b l -> p (b l)"), start=True, stop=True), r=[r_poolw, r_pooleds], w=rpm_)
                for g in range(4):
                    S.op("act", lambda e, g=g: e.activation(out=catTs[:, g, :], in_=pms[:, g, :], func=AF.Identity, bias=0.0, scale=pscale[:, g:g + 1]), r=rpm_ + [r_c0], w=[r_catTs])
                qkfs, r_qkfs = A.get(128, F32, "qkfs"); qkfs = qkfs.rearrange("p (h t) -> p h t", h=8)
                rAs, r_rAs = A.get(128, F32, "rAs"); rAs = rAs.rearrange("p (h t) -> p h t", h=8)
                rBs, r_rBs = A.get(128, F32, "rBs"); rBs = rBs.rearrange("p (h t) -> p h t", h=8)
                qkTs, r_qkTs = A.get(128, BF16, "qkTs"); qkTs = qkTs.rearrange("p (h t) -> p h t", h=8)
                qdTs, r_qdTs = A.get(64, BF16, "qdTs"); qdTs = qdTs.rearrange("p (h t) -> p h t", h=4)
                S.op("act", lambda e: e.copy(out=qkfs, in_=pqk[:, 4:12, :]), r=rq_, w=[r_qkfs])
                S.op("dve", lambda e: e.tensor_tensor(out=rAs, in0=qkfs, in1=c2s.unsqueeze(1).to_broadcast([128, 8, 16]), op=ALU.mult), r=[r_qkfs, r_tb], w=[r_rAs])
                S.op("pool", lambda e: e.tensor_tensor(out=rBs[0:64], in0=qkfs[64:128], in1=sss[64:128].unsqueeze(1).to_broadcast([64, 8, 16]), op=ALU.mult), r=[r_qkfs, r_tb], w=[r_rBs])
                S.op("pool", lambda e: e.tensor_tensor(out=rBs[64:128], in0=qkfs[0:64], in1=sss[0:64].unsqueeze(1).to_broadcast([64, 8, 16]), op=ALU.mult), r=[r_qkfs, r_tb], w=[r_rBs])
                S.op("dve", lambda e: e.tensor_tensor(out=qkTs, in0=rAs, in1=rBs, op=ALU.add), r=[r_rAs, r_rBs], w=[r_qkTs])
                S.op("pool", lambda e: e.tensor_tensor(out=qdTs.rearrange("p h (b l) -> p h b l", b=4), in0=qkTs[:, 0:4, :].rearrange("p h (b l) -> p h b l", b=4), in1=qdecs.unsqueeze(2).to_broadcast([128, 4, 4, 4]), op=ALU.mult), r=[r_qkTs, r_tb], w=[r_qdTs])
                v_s, r_vs = A.get(SB * 512, BF16, "v_s"); v_s = v_s.rearrange("p (b n) -> p b n", b=SB)
                sTs, r_sTs = A.get(64, BF16, "sTs"); sTs = sTs.rearrange("p (b h l) -> p b h l", b=4, h=4)
                kps, r_kps = A.get(512, BF16, "kps"); kps = kps.rearrange("p (h d) -> p h d", h=4)
                S.op("pool", lambda e: e.memset(v_s, 0.0), w=[r_vs])
                S.op("pool", lambda e: e.memset(sTs, 0.0), w=[r_sTs])
                S.op("pool", lambda e: e.memset(kps, 0.0), w=[r_kps])
                sggs, r_sggs = A.get(SB * 512, F32, "sggs"); sggs = sggs.rearrange("p (b n) -> p b n", b=SB)
                for bb in range(SB):
                    bv_, rv_ = ring.get(1)
                    bg_, rg_ = ring.get(1)
                    for (bx, rx, c0) in ((bv_, rv_, 1536), (bg_, rg_, 2048)):
                        for k in range(8):
                            S.op("pe", lambda e, bx=bx, k=k, c0=c0, bb=bb: e.matmul(bank(bx)[0:4, :], lhsT=xsT[:, k, bb * 4:bb * 4 + 4], rhs=w_in_sb[:, k, c0:c0 + 512], start=(k == 0), stop=(k == 7)), r=[r_win, r_xsT], w=rx)
                    S.op("act", lambda e, bv_=bv_, bb=bb: e.copy(out=v_s[0:4, bb, :], in_=bank(bv_)[0:4, :]), r=rv_, w=[r_vs])
                    S.op("act", lambda e, bg_=bg_, bb=bb: e.activation(out=sggs[0:4, bb, :], in_=bank(bg_)[0:4, :], func=AF.Silu), r=rg_, w=[r_sggs])
                    S.op("pool", lambda e, bb=bb: e.tensor_tensor(out=sggs[0:4, bb, :], in0=sggs[0:4, bb, :], in1=gng[0:4], op=ALU.mult), r=[r_sggs, r_c0], w=[r_sggs])
                bs_, rs_ = ring.get(1)
                for bb in range(SB):
                    for h in range(4):
                        c0 = (bb * 4 + h) * 4
                        S.op("pe", lambda e, bb=bb, h=h, c0=c0: e.matmul(bank(bs_)[0:4, c0:c0 + 4], lhsT=qkTs[:, 4 + h, bb * 4:bb * 4 + 4], rhs=qkTs[:, h, bb * 4:bb * 4 + 4], start=True, stop=True), r=[r_qkTs], w=rs_)
                S.op("dve", lambda e: e.tensor_tensor(out=sTs[0:4].rearrange("p b h l -> p b (h l)"), in0=bank(bs_)[0:4, 0:64].rearrange("p (b n) -> p b n", b=4), in1=decTs.unsqueeze(1).to_broadcast([4, 4, 16]), op=ALU.mult), r=rs_ + [r_tb], w=[r_sTs])
                Snew, r_Snew = dbl(512, F32, "Snew", "p (h v) -> p h v", h=4)
                gsts, r_gsts = A.get(40, F32, "gsts")
                on1s, r_on1s = A.get(512, F32, "on1s")
                rets_bf, r_retsb = A.get(512, BF16, "retsb")
                for bb in range(SB):
                    bo_, ro_ = ring.get(1)
                    ovs = bank(bo_)[0:4, :].rearrange("p (h t) -> p h t", h=4)
                    for h in range(4):
                        S.op("pe", lambda e, h=h, bb=bb, ovs=ovs: e.matmul(ovs[:, h, :], lhsT=sTs[:, bb, h, :], rhs=v_s[:, bb, h * 128:(h + 1) * 128], start=True, stop=False), r=[r_sTs, r_vs], w=ro_)
                        S.op("pe", lambda e, h=h, bb=bb, ovs=ovs: e.matmul(ovs[:, h, :], lhsT=qdTs[:, h, bb * 4:bb * 4 + 4], rhs=S0b[:, bb * 4 + h, :], start=False, stop=True), r=[r_qdTs, r_S0b], w=ro_)
                    bkt_, rkt_ = ring.get(1)
                    ktvs = bankbf(bkt_)[0:4, 0:512].rearrange("p (h t) -> p h t", h=4)
                    for h in range(4):
                        S.op("pe", lambda e, h=h, bb=bb, ktvs=ktvs: e.transpose(out=ktvs[:, h, :], in_=qkTs[:, 4 + h, bb * 4:bb * 4 + 4], identity=ident_b[:]), r=[r_qkTs, r_idb], w=rkt_)
                    for h in range(4):
                        S.op("act", lambda e, h=h, ktvs=ktvs: e.activation(out=kps[0:4, h, :], in_=ktvs[:, h, :], func=AF.Identity, bias=0.0, scale=kdecs[:, h:h + 1]), r=rkt_ + [r_tb], w=[r_kps])
                    bds_, rds_ = ring.get(1)
                    dss = bank(bds_).rearrange("p (h t) -> p h t", h=4)
                    for h in range(4):
                        S.op("pe", lambda e, h=h, bb=bb, dss=dss: e.matmul(dss[:, h, :], lhsT=kps[:, h, :], rhs=v_s[:, bb, h * 128:(h + 1) * 128], start=True, stop=True), r=[r_kps, r_vs], w=rds_)
                    sn = Snew[bb % 2]
                    for h in range(4):
                        S.op("dve", lambda e, h=h, bb=bb, dss=dss, sn=sn: e.scalar_tensor_tensor(out=sn[:, h, :], in0=S0f[:, bb * 4 + h, :], scalar=tabs["gCs"][h], in1=dss[:, h, :], op0=ALU.mult, op1=ALU.add), r=rds_ + [r_S0f], w=[r_Snew[bb % 2]])
                    out_toks.append(S.dma("sp", lambda e, bb=bb, sn=sn: e.dma_start(out=rets[bb * 4:(bb + 1) * 4].rearrange("h k v -> k h v"), in_=sn), r=[r_Snew[bb % 2]]))
                    for h in range(4):
                        S.op("dve", lambda e, h=h, ovs=ovs: e.bn_stats(out=gsts[0:4, h * 6:(h + 1) * 6], in_=ovs[:, h, :]), r=ro_, w=[r_gsts])
                        S.op("dve", lambda e, h=h: e.bn_aggr(out=gsts[0:4, 24 + 2 * h:26 + 2 * h], in_=gsts[0:4, h * 6:(h + 1) * 6]), r=[r_gsts], w=[r_gsts])
                    mvs = gsts[0:4, 24:32].rearrange("p (h two) -> p h two", two=2)
                    S.op("act", lambda e: e.activation(out=gsts[0:4, 32:36], in_=mvs[:, :, 1], func=AF.Sqrt, bias=eps_gn[0:4], scale=1.0), r=[r_gsts, r_eps], w=[r_gsts])
                    S.op("dve", lambda e: e.reciprocal(out=gsts[0:4, 32:36], in_=gsts[0:4, 32:36]), r=[r_gsts], w=[r_gsts])
                    S.op("dve", lambda e: e.scalar_tensor_tensor(out=gsts[0:4, 36:40], in0=mvs[:, :, 0], scalar=-1.0, in1=gsts[0:4, 32:36], op0=ALU.mult, op1=ALU.mult), r=[r_gsts], w=[r_gsts])
                    for h in range(4):
                        S.op("act", lambda e, h=h, ovs=ovs: e.activation(out=on1s[0:4, h * 128:(h + 1) * 128], in_=ovs[:, h, :], func=AF.Identity, bias=gsts[0:4, 36 + h:37 + h], scale=gsts[0:4, 32 + h:33 + h]), r=ro_ + [r_gsts], w=[r_on1s])
                    S.op("pool", lambda e, bb=bb: e.tensor_tensor(out=rets_bf[0:4, :], in0=on1s[0:4, :], in1=sggs[0:4, bb, :], op=ALU.mult), r=[r_on1s, r_sggs], w=[r_retsb])
                    brt_, rrt_ = ring.get(1)
                    rtvs = bankbf(brt_)[:, 0:16].rearrange("p (h t) -> p h t", h=4)
                    for h in range(4):
                        S.op("pe", lambda e, h=h, rtvs=rtvs: e.transpose(out=rtvs[:, h, :], in_=rets_bf[0:4, h * 128:(h + 1) * 128], identity=ident_b[0:4, 0:4]), r=[r_retsb, r_idb], w=rrt_)
                    S.op("dve", lambda e, rtvs=rtvs, bb=bb: e.tensor_copy(out=catTs[:, 4:8, bb * 4:bb * 4 + 4], in_=rtvs), r=rrt_, w=[r_catTs])
                by_, ry_ = ring.get(2)
                for hf in range(2):
                    for c in range(8):
                        S.op("pe", lambda e, hf=hf, c=c: e.matmul(bank(by_ + hf)[0:NS, :], lhsT=catTs[:, c, :], rhs=w_o_sb[:, c, hf * 512:(hf + 1) * 512], start=(c == 0), stop=(c == 7)), r=[r_catTs, r_wo], w=[ry_[hf]])
                zs, r_zs = A.get(D, F32, "zs")
                scs, r_scs = A.get(16, F32, "scs")
                S.op("dve", lambda e: e.scalar_tensor_tensor(out=zs[0:NS, :], in0=XS[:], scalar=ALPHA, in1=bank(by_, 2)[0:NS, :], op0=ALU.mult, op1=ALU.add), r=[r_XS] + ry_, w=[r_zs])
                layer_norm_tile(zs[0:NS, :], r_zs, XS[:], r_XS, NS, scs, r_scs, lnt0, r_lnt0)

            if with_sample:
                sample_l0()
            chk('l0')
            ffn(0, False)
            chk('ffn0')

            A.at(0)
            wdq_sb, r_wdq = A.get(8 * 512, BF16, "wdq"); wdq_sb = wdq_sb.rearrange("p (k n) -> p k n", k=8)
            wuq_sb, r_wuq = A.get(4 * 1536, BF16, "wuq"); wuq_sb = wuq_sb.rearrange("p (k n) -> p k n", k=4)
            wdkv_sb, r_wdkv = A.get(8 * 320, BF16, "wdkv"); wdkv_sb = wdkv_sb.rearrange("p (k n) -> p k n", k=8)
            wukT_sb, r_wukT = A.get(8 * 256, BF16, "wukT"); wukT_sb = wukT_sb.rearrange("p (h c) -> p h c", h=8)
            wuv_sb, r_wuv = A.get(2 * 1024, BF16, "wuv"); wuv_sb = wuv_sb.rearrange("p (k n) -> p k n", k=2)
            wo1_sb, r_wo1 = A.get(8 * 1024, BF16, "wo1"); wo1_sb = wo1_sb.rearrange("p (k n) -> p k n", k=8)
            c1buf, r_c1 = A.get(512 + 256, F32, "c1")
            qng = c1buf[:, 0:512]
            kvng = c1buf[:, 512:768]
            lnt1, r_lnt1 = A.get(2 * D, F32, "lnt1")
            ckvT, r_ckvT = A.get(2 * SEQ, BF16, "ckvT", nres=NT); ckvT = ckvT.rearrange("p (k n) -> p k n", k=2)
            ckvK, r_ckvK = A.get(NT * 256, BF16, "ckvK", nres=NT); ckvK = ckvK.rearrange("p (t c) -> p t c", t=NT)
            kpeT, r_kpeT = A.get(SEQ, BF16, "kpeT", nres=NT)
            S.dma("pool", lambda e: e.dma_start(out=wdq_sb, in_=w_dq.rearrange("(k p) n -> p k n", p=128)), w=[r_wdq])
            S.dma("pool", lambda e: e.dma_start(out=wdkv_sb, in_=w_dkv.rearrange("(k p) n -> p k n", p=128)), w=[r_wdkv])
            S.dma("pool", lambda e: e.dma_start(out=wuq_sb, in_=w_uq.rearrange("(k p) n -> p k n", p=128)), w=[r_wuq])
            S.dma("pool", lambda e: e.dma_start(out=wukT_sb, in_=w_ukT), w=[r_wukT])
            S.dma("pool", lambda e: e.dma_start(out=wuv_sb, in_=w_uv.rearrange("(k p) n -> p k n", p=128)), w=[r_wuv])
            S.dma("pool", lambda e: e.dma_start(out=wo1_sb, in_=w_o1.rearrange("(k p) n -> p k n", p=128)), w=[r_wo1])
            S.dma("sp", lambda e: e.dma_start(out=qng, in_=qn_g.partition_broadcast(128)), w=[r_c1])
            S.dma("sp", lambda e: e.dma_start(out=kvng, in_=kvn_g.partition_broadcast(128)), w=[r_c1])
            load_ln(lnt1, r_lnt1, ln_mix_g[1], ln_mix_b[1])
            L1_BUF = A.cur

            xT1, r_xT1 = dbl(1024, BF16, "xT1", "p (k t) -> p k t", k=8)
            rt1, r_rt1 = dbl(128, F32, "rt1")
            sc1, r_sc1 = dbl(16, F32, "sc1")
            cqn, r_cqn = dbl(512, BF16, "cqn")
            cqnT, r_cqnT = dbl(512, BF16, "cqnT", "p (k t) -> p k t", k=4)
            ckv_f, r_ckvf = dbl(256, F32, "ckv_f")
            kpe_f, r_kpef = dbl(64, F32, "kpe_f")
            kpe_t, r_kpet = A.get(128, F32, "kpe_t")
            kpe_b, r_kpeb = dbl(128, BF16, "kpe_b")
            qnT0_, r_qnT0_ = A.get(1024, BF16, "qnT"); qnT0_ = qnT0_.rearrange("p (h t) -> p h t", h=8); qnT = [qnT0_, qnT0_]; r_qnT = [r_qnT0_, r_qnT0_]
            qpe_t, r_qpet = A.get(1024, F32, "qpe_t")
            junk, r_junk = qpe_t[:, 0:512], r_qpet
            qpe_b, r_qpeb = dbl(512, BF16, "qpe_b")
            qpeT, r_qpeT = dbl(1024, BF16, "qpeT", "p (h t) -> p h t", h=8)
            qlT, r_qlT = dbl(2048, BF16, "qlT", "p (h k t) -> p h k t", h=8, k=2)
            p_sb, r_p = dbl(SEQ, BF16, "p_sb")
            pT_sb, r_pT = dbl(SEQ, BF16, "pT_sb")
            st1, r_st1 = dbl(8, F32, "st1")
            ol_b, r_olb = dbl(256, BF16, "ol_b")
            olT, r_olT = dbl(256, BF16, "olT", "p (k t) -> p k t", k=2)
            oT10_, r_oT10_ = A.get(1024, BF16, "oT1"); oT10_ = oT10_.rearrange("p (h t) -> p h t", h=8); oT1 = [oT10_, oT10_]; r_oT1 = [r_oT10_, r_oT10_]
            z10, r_z10 = A.get(D, F32, "z1"); z1 = [z10, z10]; r_z1 = [r_z10, r_z10]
            print("layer1 arena bytes", A.cur)

            for i_ in range(2):
                S.op("pool", lambda e, i_=i_: e.memset(qpeT[i_], 0.0), w=[r_qpeT[i_]])
            def l1A(t):
                pb = t % 2
                tok = slice(t * 128, (t + 1) * 128)
                S.dma("sp", lambda e, pb=pb, t=t: e.dma_start(out=rt1[pb][:, 0:64], in_=tdr["cc1"][:, t, :]), w=[r_rt1[pb]])
                S.dma("sp", lambda e, pb=pb, t=t: e.dma_start(out=rt1[pb][:, 64:128], in_=tdr["ss1"][:, t, :]), w=[r_rt1[pb]])
                transpose_x(X[:, t, :], rX[t], xT1[pb], r_xT1[pb])
                bcq, rcq = ring.get(1)
                bkv, rkv = ring.get(1)
                for k in range(8):
                    S.op("pe", lambda e, k=k, pb=pb, bcq=bcq: e.matmul(bank(bcq), lhsT=xT1[pb][:, k, :], rhs=wdq_sb[:, k, :], start=(k == 0), stop=(k == 7)), r=[r_xT1[pb], r_wdq], w=rcq)
                for k in range(8):
                    S.op("pe", lambda e, k=k, pb=pb, bkv=bkv: e.matmul(bank(bkv)[:, 0:320], lhsT=xT1[pb][:, k, :], rhs=wdkv_sb[:, k, :], start=(k == 0), stop=(k == 7)), r=[r_xT1[pb], r_wdkv], w=rkv)
                sc = sc1[pb]
                S.op("act", lambda e, bcq=bcq, sc=sc: e.activation(out=junk, in_=bank(bcq), func=AF.Square, accum_out=sc[:, 0:1]), r=rcq, w=[r_junk, r_sc1[pb]])
                S.op("act", lambda e, sc=sc: e.activation(out=sc[:, 1:2], in_=sc[:, 0:1], func=AF.Sqrt, bias=eps_rms, scale=1.0 / 512), r=[r_sc1[pb], r_eps], w=[r_sc1[pb]])
                S.op("dve", lambda e, sc=sc: e.reciprocal(out=sc[:, 1:2], in_=sc[:, 1:2]), r=[r_sc1[pb]], w=[r_sc1[pb]])
                S.op("dve", lambda e, bcq=bcq, sc=sc, pb=pb: e.scalar_tensor_tensor(out=cqn[pb], in0=bank(bcq), scalar=sc[:, 1:2], in1=qng, op0=ALU.mult, op1=ALU.mult), r=rcq + [r_sc1[pb], r_c1], w=[r_cqn[pb]])
                S.op("act", lambda e, bkv=bkv, sc=sc: e.activation(out=junk[:, 0:256], in_=bank(bkv)[:, 0:256], func=AF.Square, accum_out=sc[:, 2:3]), r=rkv, w=[r_junk, r_sc1[pb]])
                S.op("act", lambda e, sc=sc: e.activation(out=sc[:, 3:4], in_=sc[:, 2:3], func=AF.Sqrt, bias=eps_rms, scale=1.0 / 256), r=[r_sc1[pb], r_eps], w=[r_sc1[pb]])
                S.op("dve", lambda e, sc=sc: e.reciprocal(out=sc[:, 3:4], in_=sc[:, 3:4]), r=[r_sc1[pb]], w=[r_sc1[pb]])
                S.op("dve", lambda e, bkv=bkv, sc=sc, pb=pb: e.scalar_tensor_tensor(out=ckv_f[pb], in0=bank(bkv)[:, 0:256], scalar=sc[:, 3:4], in1=kvng, op0=ALU.mult, op1=ALU.mult), r=rkv + [r_sc1[pb], r_c1], w=[r_ckvf[pb]])
                out_toks.append(S.dma("sp", lambda e, pb=pb, tok=tok: e.dma_start(out=ckvp[tok, :], in_=ckv_f[pb]), r=[r_ckvf[pb]]))
                S.op("pool", lambda e, pb=pb, t=t: e.tensor_copy(out=ckvK[:, t, :], in_=ckv_f[pb]), r=[r_ckvf[pb]], w=[r_ckvK[t]])
                S.op("dve", lambda e, bkv=bkv, pb=pb: e.tensor_tensor(out=kpe_t[:, 0:64], in0=bank(bkv)[:, 256:320], in1=rt1[pb][:, 0:64], op=ALU.mult), r=rkv + [r_rt1[pb]], w=[r_kpet])
                S.op("dve", lambda e, bkv=bkv, pb=pb: e.tensor_tensor(out=kpe_t[:, 64:96], in0=bank(bkv)[:, 288:320], in1=rt1[pb][:, 64:96], op=ALU.mult), r=rkv + [r_rt1[pb]], w=[r_kpet])
                S.op("dve", lambda e, bkv=bkv, pb=pb: e.tensor_tensor(out=kpe_t[:, 96:128], in0=bank(bkv)[:, 256:288], in1=rt1[pb][:, 96:128], op=ALU.mult), r=rkv + [r_rt1[pb]], w=[r_kpet])
                S.op("pool", lambda e, pb=pb: e.tensor_tensor(out=kpe_f[pb], in0=kpe_t[:, 0:64], in1=kpe_t[:, 64:128], op=ALU.add), r=[r_kpet], w=[r_kpef[pb]])
                out_toks.append(S.dma("sp", lambda e, pb=pb, tok=tok: e.dma_start(out=kpep[tok, :], in_=kpe_f[pb]), r=[r_kpef[pb]]))
                S.op("pool", lambda e, pb=pb: e.tensor_copy(out=kpe_b[pb][:, 0:64], in_=kpe_f[pb]), r=[r_kpef[pb]], w=[r_kpeb[pb]])
                S.op("pool", lambda e, pb=pb: e.tensor_copy(out=kpe_b[pb][:, 64:128], in_=kpe_f[pb]), r=[r_kpef[pb]], w=[r_kpeb[pb]])
                btr, rtr = ring.get(2)
                trv = bankbf(btr, 2)
                for k in range(4):
                    S.op("pe", lambda e, k=k, trv=trv, pb=pb: e.transpose(out=trv[:, k * 128:(k + 1) * 128], in_=cqn[pb][:, k * 128:(k + 1) * 128], identity=ident_b[:]), r=[r_cqn[pb], r_idb], w=rtr)
                for k in range(2):
                    S.op("pe", lambda e, k=k, trv=trv, t=t: e.transpose(out=trv[:, 1024 + k * 128:1024 + (k + 1) * 128], in_=ckvK[:, t, k * 128:(k + 1) * 128], identity=ident_b[:]), r=[r_ckvK[t], r_idb], w=rtr)
                S.op("pe", lambda e, trv=trv, pb=pb: e.transpose(out=trv[:, 1280:1408], in_=kpe_b[pb], identity=ident_b[:]), r=[r_kpeb[pb], r_idb], w=rtr)
                S.op("act", lambda e, trv=trv, pb=pb: e.copy(out=cqnT[pb], in_=trv[:, 0:512].rearrange("p (k t) -> p k t", k=4)), r=rtr, w=[r_cqnT[pb]])
                S.op("dve", lambda e, trv=trv, tok=tok, t=t: e.tensor_copy(out=ckvT[:, :, tok], in_=trv[:, 1024:1280].rearrange("p (k t) -> p k t", k=2)), r=rtr, w=[r_ckvT[t]])
                S.op("dve", lambda e, trv=trv, tok=tok, t=t: e.tensor_copy(out=kpeT[:, tok], in_=trv[:, 1280:1408]), r=rtr, w=[r_kpeT[t]])
                for hh in range(2):
                    bqn, rqn = ring.get(1)
                    qv = bank(bqn).rearrange("p (h t) -> p h t", h=4)
                    for h4 in range(4):
                        h = hh * 4 + h4
                        for k in range(4):
                            S.op("pe", lambda e, qv=qv, h4=h4, h=h, k=k, pb=pb: e.matmul(qv[:, h4, :], lhsT=wuq_sb[:, k, h * 128:(h + 1) * 128], rhs=cqnT[pb][:, k, :], start=(k == 0), stop=(k == 3)), r=[r_wuq, r_cqnT[pb]], w=rqn)
                    evac("act" if hh == 0 else "dve", qnT[pb][:, hh * 4:hh * 4 + 4, :], qv, rqn, [r_qnT[pb]])
                bqp, rqp = ring.get(1)
                for k in range(4):
                    S.op("pe", lambda e, k=k, bqp=bqp, pb=pb: e.matmul(bank(bqp), lhsT=cqnT[pb][:, k, :], rhs=wuq_sb[:, k, 1024:1536], start=(k == 0), stop=(k == 3)), r=[r_wuq, r_cqnT[pb]], w=rqp)
                qpv = bank(bqp).rearrange("p (h d) -> p h d", h=8)
                qA = qpe_t[:, 0:512].rearrange("p (h d) -> p h d", h=8)
                qB = qpe_t[:, 512:1024].rearrange("p (h d) -> p h d", h=8)
                ccb = rt1[pb][:, 0:64].unsqueeze(1)
                ssb1 = rt1[pb][:, 64:96].unsqueeze(1)
                ssb2 = rt1[pb][:, 96:128].unsqueeze(1)
                S.op("dve", lambda e, qpv=qpv, ccb=ccb: e.tensor_tensor(out=qA, in0=qpv, in1=ccb.to_broadcast([128, 8, 64]), op=ALU.mult), r=rqp + [r_rt1[pb]], w=[r_qpet])
                S.op("dve", lambda e, qpv=qpv, ssb1=ssb1: e.tensor_tensor(out=qB[:, :, 0:32], in0=qpv[:, :, 32:64], in1=ssb1.to_broadcast([128, 8, 32]), op=ALU.mult), r=rqp + [r_rt1[pb]], w=[r_qpet])
                S.op("dve", lambda e, qpv=qpv, ssb2=ssb2: e.tensor_tensor(out=qB[:, :, 32:64], in0=qpv[:, :, 0:32], in1=ssb2.to_broadcast([128, 8, 32]), op=ALU.mult), r=rqp + [r_rt1[pb]], w=[r_qpet])
                S.op("pool", lambda e, pb=pb: e.tensor_tensor(out=qpe_b[pb], in0=qpe_t[:, 0:512], in1=qpe_t[:, 512:1024], op=ALU.add), r=[r_qpet], w=[r_qpeb[pb]])
                btq, rtq = ring.get(1)
                tqv = bankbf(btq)[:, 0:512].rearrange("p (a t) -> p a t", a=4)
                for a in range(4):
                    S.op("pe", lambda e, a=a, tqv=tqv, pb=pb: e.transpose(out=tqv[:, a, :], in_=qpe_b[pb][:, a * 128:(a + 1) * 128], identity=ident_b[:]), r=[r_qpeb[pb], r_idb], w=rtq)
                qz = qpeT[pb].rearrange("p (a two) t -> p a two t", two=2)
                S.op("act", lambda e, tqv=tqv, qz=qz: e.copy(out=qz[0:64, :, 0, :], in_=tqv[0:64]), r=rtq, w=[r_qpeT[pb]])
                S.op("dve", lambda e, tqv=tqv, qz=qz: e.tensor_copy(out=qz[64:128, :, 1, :], in_=tqv[64:128]), r=rtq, w=[r_qpeT[pb]])
                for hh in range(4):
                    bql, rql = ring.get(1)
                    qlv = bank(bql).rearrange("p (h k t) -> p h k t", h=2, k=2)
                    for h2 in range(2):
                        h = hh * 2 + h2
                        for k in range(2):
                            S.op("pe", lambda e, qlv=qlv, h2=h2, h=h, k=k, pb=pb: e.matmul(qlv[:, h2, k, :], lhsT=wukT_sb[:, h, k * 128:(k + 1) * 128], rhs=qnT[pb][:, h, :], start=True, stop=True), r=[r_wukT, r_qnT[pb]], w=rql)
                    evac("act" if hh % 2 == 0 else "dve", qlT[pb][:, hh * 2:hh * 2 + 2, :, :], qlv, rql, [r_qlT[pb]])

            l1ctx = {}
            def l1P1a(t, h):
                pb = t % 2
                hp = h % 2
                nk = (t + 1) * 128
                nb = (nk + 511) // 512
                nbanks = 1 if nb == 1 else (2 if nb == 2 else 4)
                bs, rs = ring.get(nbanks)
                sfull = bank(bs, nbanks)
                for gi in range(nb):
                    k0 = gi * 512
                    kn = min(512, nk - k0)
                    ks = slice(k0, k0 + kn)
                    kres = r_ckvT[gi * 4:gi * 4 + (kn + 127) // 128]
                    kpres = r_kpeT[gi * 4:gi * 4 + (kn + 127) // 128]
                    last_diag = (gi == nb - 1)
                    parts = [(ks, False)] if not last_diag else ([(slice(k0, nk - 128), False)] if nk - 128 > k0 else []) + [(slice(nk - 128, nk), True)]
                    for (ks, dg) in parts:
                        S.op("pe", lambda e, sfull=sfull, ks=ks, h=h, pb=pb: e.matmul(sfull[:, ks], lhsT=qlT[pb][:, h, 0, :], rhs=ckvT[:, 0, ks], start=True, stop=False), r=[r_qlT[pb]] + kres, w=[rs[gi]])
                        S.op("pe", lambda e, sfull=sfull, ks=ks, h=h, pb=pb: e.matmul(sfull[:, ks], lhsT=qlT[pb][:, h, 1, :], rhs=ckvT[:, 1, ks], start=False, stop=False), r=[r_qlT[pb]] + kres, w=[rs[gi]])
                        S.op("pe", lambda e, sfull=sfull, ks=ks, h=h, pb=pb, dg=dg: e.matmul(sfull[:, ks], lhsT=qpeT[pb][:, h, :], rhs=kpeT[:, ks], start=False, stop=(not dg)), r=[r_qpeT[pb]] + kpres, w=[rs[gi]])
                        if dg:
                            S.op("pe", lambda e, sfull=sfull, ks=ks: e.matmul(sfull[:, ks], lhsT=ident_b[:], rhs=maskd[:], start=False, stop=True), r=[r_idb, r_maskd], w=[rs[gi]])
                stt = st1[hp]
                chs = [(c0_, min(nk, c0_ + 1024)) for c0_ in range(0, nk, 1024)]
                for ci, (a_, b_) in enumerate(chs):
                    S.op("dve", lambda e, sfull=sfull, a_=a_, b_=b_, ci=ci, stt=stt: e.reduce_max(out=stt[:, 4 + ci:5 + ci], in_=sfull[:, a_:b_], axis=AX.X), r=rs[0:nb], w=[r_st1[hp]])
                if len(chs) == 2:
                    S.op("dve", lambda e, stt=stt: e.tensor_tensor(out=stt[:, 4:5], in0=stt[:, 4:5], in1=stt[:, 5:6], op=ALU.max), r=[r_st1[hp]], w=[r_st1[hp]])
                S.op("dve", lambda e, stt=stt: e.tensor_scalar(out=stt[:, 1:2], in0=stt[:, 4:5], scalar1=-MLA_SCALE, scalar2=None, op0=ALU.mult), r=[r_st1[hp]], w=[r_st1[hp]])
                l1ctx[(t, h)] = (sfull, rs, nb, chs)

            def l1P1b(t, h):
                pb = t % 2
                hp = h % 2
                nk = (t + 1) * 128
                stt = st1[hp]
                sfull, rs, nb, chs = l1ctx[(t, h)]
                for ci, (a_, b_) in enumerate(chs):
                    S.op("act", lambda e, sfull=sfull, a_=a_, b_=b_, ci=ci, stt=stt, hp=hp: e.activation(out=p_sb[hp][:, a_:b_], in_=sfull[:, a_:b_], func=AF.Exp, bias=stt[:, 1:2], scale=MLA_SCALE, accum_out=stt[:, 6 + ci:7 + ci]), r=rs[0:nb] + [r_st1[hp]], w=[r_p[hp], r_st1[hp]])

            def l1P2a(t, h):
                pb = t % 2
                hp = h % 2
                nk = (t + 1) * 128
                stt = st1[hp]
                nkt = t + 1
                for g0 in range(0, nkt, 4):
                    bpt, rpt = ring.get(1)
                    ptv = bankbf(bpt)
                    n_ = min(4, nkt - g0)
                    for i_ in range(n_):
                        S.op("pe", lambda e, i_=i_, g0=g0, ptv=ptv, hp=hp: e.transpose(out=ptv[:, i_ * 128:(i_ + 1) * 128], in_=p_sb[hp][:, (g0 + i_) * 128:(g0 + i_ + 1) * 128], identity=ident_b[:]), r=[r_p[hp], r_idb], w=rpt)
                    evac("act" if (g0 // 4) % 2 == 0 else "dve", pT_sb[hp][:, g0 * 128:(g0 + n_) * 128], ptv[:, 0:n_ * 128], rpt, [r_pT[hp]])

            def l1P2b(t, h):
                pb = t % 2
                hp = h % 2
                nk = (t + 1) * 128
                stt = st1[hp]
                nkt = t + 1
                bol, rol = ring.get(1)
                for kt in range(nkt):
                    S.op("pe", lambda e, kt=kt, bol=bol, hp=hp: e.matmul(bank(bol)[:, 0:256], lhsT=pT_sb[hp][:, kt * 128:(kt + 1) * 128], rhs=ckvK[:, kt, :], start=(kt == 0), stop=(kt == nkt - 1)), r=[r_pT[hp], r_ckvK[kt]], w=rol)
                chs = [(c0_, min(nk, c0_ + 1024)) for c0_ in range(0, nk, 1024)]
                if len(chs) == 2:
                    S.op("dve", lambda e, stt=stt: e.tensor_tensor(out=stt[:, 6:7], in0=stt[:, 6:7], in1=stt[:, 7:8], op=ALU.add), r=[r_st1[hp]], w=[r_st1[hp]])
                S.op("dve", lambda e, stt=stt: e.reciprocal(out=stt[:, 3:4], in_=stt[:, 6:7]), r=[r_st1[hp]], w=[r_st1[hp]])
                S.op("dve", lambda e, bol=bol, stt=stt, hp=hp: e.tensor_scalar(out=ol_b[hp], in0=bank(bol)[:, 0:256], scalar1=stt[:, 3:4], scalar2=None, op0=ALU.mult), r=rol + [r_st1[hp]], w=[r_olb[hp]])
                bot, rot = ring.get(1)
                otv = bankbf(bot)[:, 0:256].rearrange("p (k t) -> p k t", k=2)
                for k in range(2):
                    S.op("pe", lambda e, k=k, otv=otv, hp=hp: e.transpose(out=otv[:, k, :], in_=ol_b[hp][:, k * 128:(k + 1) * 128], identity=ident_b[:]), r=[r_olb[hp], r_idb], w=rot)
                S.op("dve", lambda e, otv=otv, hp=hp: e.tensor_copy(out=olT[hp], in_=otv), r=rot, w=[r_olT[hp]])
                boh, roh = ring.get(1)
                for k in range(2):
                    S.op("pe", lambda e, k=k, boh=boh, h=h, hp=hp: e.matmul(bank(boh)[:, 0:128], lhsT=wuv_sb[:, k, h * 128:(h + 1) * 128], rhs=olT[hp][:, k, :], start=(k == 0), stop=(k == 1)), r=[r_wuv, r_olT[hp]], w=roh)
                evac("act" if h % 2 == 0 else "dve", oT1[pb][:, h, :], bank(boh)[:, 0:128], roh, [r_oT1[pb]])


            def l1out(t):
                pb = t % 2
                by, ry = ring.get(2)
                for hf in range(2):
                    for c in range(8):
                        S.op("pe", lambda e, hf=hf, c=c, by=by, pb=pb: e.matmul(bank(by + hf), lhsT=oT1[pb][:, c, :], rhs=wo1_sb[:, c, hf * 512:(hf + 1) * 512], start=(c == 0), stop=(c == 7)), r=[r_oT1[pb], r_wo1], w=[ry[hf]])
                S.op("dve", lambda e, by=by, pb=pb, t=t: e.scalar_tensor_tensor(out=z1[pb], in0=X[:, t, :], scalar=ALPHA, in1=bank(by, 2), op0=ALU.mult, op1=ALU.add), r=[rX[t]] + ry, w=[r_z1[pb]])
                layer_norm_tile(z1[pb], r_z1[pb], X[:, t, :], rX[t], 128, sc1[pb], r_sc1[pb], lnt1, r_lnt1)


            l1A(0)
            for t in range(NT):
                l1P1a(t, 0)
                l1P1b(t, 0)
                for h in range(1, 8):
                    l1P2a(t, h - 1)
                    l1P1a(t, h)
                    l1P1b(t, h)
                    l1P2b(t, h - 1)
                    if h == 4 and t + 1 < NT:
                        l1A(t + 1)
                l1P2a(t, 7)
                l1P2b(t, 7)
                l1out(t)
                chk(f'l1t{t}')

            def sample_l1():
                A.at(L1_BUF)
                NSTEP = NPAGES * 128 // 1024
                tb1, r_tb1 = A.get(128, F32, "tb1")
                cc1s = tb1[0:NS, 0:64]; ss1s = tb1[0:NS, 64:128]
                S.dma("sp", lambda e: e.dma_start(out=cc1s, in_=tdr["cc1s"]), w=[r_tb1])
                S.dma("sp", lambda e: e.dma_start(out=ss1s, in_=tdr["ss1s"]), w=[r_tb1])
                msk, r_msk = A.get(SB * NS, BF16, "msk")
                S.dma("pool", lambda e: e.dma_start(out=msk[0:32, :], in_=tdr["masksb"]), w=[r_msk])
                mskv = msk.rearrange("p (b t) -> p b t", b=SB)
                pt_sb, r_pt = A.get(SB, I32, "pt_sb")
                S.dma("sp", lambda e: e.dma_start(out=pt_sb, in_=ptab), w=[r_pt])
                xsT1, r_xsT1 = A.get(128, BF16, "xsT1"); xsT1 = xsT1.rearrange("p (k t) -> p k t", k=8)
                transpose_x(XS[:], r_XS, xsT1, r_xsT1, npart=NS)
                bcq, rcq = ring.get(1)
                bkv, rkv = ring.get(1)
                for k in range(8):
                    S.op("pe", lambda e, k=k: e.matmul(bank(bcq)[0:NS, :], lhsT=xsT1[:, k, :], rhs=wdq_sb[:, k, :], start=(k == 0), stop=(k == 7)), r=[r_xsT1, r_wdq], w=rcq)
                for k in range(8):
                    S.op("pe", lambda e, k=k: e.matmul(bank(bkv)[0:NS, 0:320], lhsT=xsT1[:, k, :], rhs=wdkv_sb[:, k, :], start=(k == 0), stop=(k == 7)), r=[r_xsT1, r_wdkv], w=rkv)
                jk, r_jk = A.get(512, F32, "jk")
                scq, r_scq = A.get(16, F32, "scq")
                cqns, r_cqns = A.get(512, BF16, "cqns")
                ckvsf, r_ckvsf = A.get(256, F32, "ckvsf")
                ckvsb, r_ckvsb = A.get(256, BF16, "ckvsb")
                kpet, r_kpets = A.get(128, F32, "kpets")
                kpesf, r_kpesf = A.get(64, F32, "kpesf")
                kpesb, r_kpesb = A.get(128, BF16, "kpesb")
                sc = scq[0:NS]
                S.op("act", lambda e: e.activation(out=jk[0:NS, :], in_=bank(bcq)[0:NS, :], func=AF.Square, accum_out=sc[:, 0:1]), r=rcq, w=[r_jk, r_scq])
                S.op("act", lambda e: e.activation(out=sc[:, 1:2], in_=sc[:, 0:1], func=AF.Sqrt, bias=eps_rms[0:NS], scale=1.0 / 512), r=[r_scq, r_eps], w=[r_scq])
                S.op("dve", lambda e: e.reciprocal(out=sc[:, 1:2], in_=sc[:, 1:2]), r=[r_scq], w=[r_scq])
                S.op("dve", lambda e: e.scalar_tensor_tensor(out=cqns[0:NS, :], in0=bank(bcq)[0:NS, :], scalar=sc[:, 1:2], in1=qng[0:NS], op0=ALU.mult, op1=ALU.mult), r=rcq + [r_scq, r_c1], w=[r_cqns])
                S.op("act", lambda e: e.activation(out=jk[0:NS, 0:256], in_=bank(bkv)[0:NS, 0:256], func=AF.Square, accum_out=sc[:, 2:3]), r=rkv, w=[r_jk, r_scq])
                S.op("act", lambda e: e.activation(out=sc[:, 3:4], in_=sc[:, 2:3], func=AF.Sqrt, bias=eps_rms[0:NS], scale=1.0 / 256), r=[r_scq, r_eps], w=[r_scq])
                S.op("dve", lambda e: e.reciprocal(out=sc[:, 3:4], in_=sc[:, 3:4]), r=[r_scq], w=[r_scq])
                S.op("dve", lambda e: e.scalar_tensor_tensor(out=ckvsf[0:NS, :], in0=bank(bkv)[0:NS, 0:256], scalar=sc[:, 3:4], in1=kvng[0:NS], op0=ALU.mult, op1=ALU.mult), r=rkv + [r_scq, r_c1], w=[r_ckvsf])
                out_toks.append(S.dma("sp", lambda e: e.dma_start(out=ckvs, in_=ckvsf[0:NS, :]), r=[r_ckvsf]))
                S.op("pool", lambda e: e.tensor_copy(out=ckvsb[0:NS, :], in_=ckvsf[0:NS, :]), r=[r_ckvsf], w=[r_ckvsb])
                S.op("dve", lambda e: e.tensor_tensor(out=kpet[0:NS, 0:64], in0=bank(bkv)[0:NS, 256:320], in1=cc1s, op=ALU.mult), r=rkv + [r_tb1], w=[r_kpets])
                S.op("dve", lambda e: e.tensor_tensor(out=kpet[0:NS, 64:96], in0=bank(bkv)[0:NS, 288:320], in1=ss1s[:, 0:32], op=ALU.mult), r=rkv + [r_tb1], w=[r_kpets])
                S.op("dve", lambda e: e.tensor_tensor(out=kpet[0:NS, 96:128], in0=bank(bkv)[0:NS, 256:288], in1=ss1s[:, 32:64], op=ALU.mult), r=rkv + [r_tb1], w=[r_kpets])
                S.op("pool", lambda e: e.tensor_tensor(out=kpesf[0:NS, :], in0=kpet[0:NS, 0:64], in1=kpet[0:NS, 64:128], op=ALU.add), r=[r_kpets], w=[r_kpesf])
                out_toks.append(S.dma("sp", lambda e: e.dma_start(out=kpes, in_=kpesf[0:NS, :]), r=[r_kpesf]))
                S.op("pool", lambda e: e.tensor_copy(out=kpesb[0:NS, 0:64], in_=kpesf[0:NS, :]), r=[r_kpesf], w=[r_kpesb])
                S.op("pool", lambda e: e.tensor_copy(out=kpesb[0:NS, 64:128], in_=kpesf[0:NS, :]), r=[r_kpesf], w=[r_kpesb])
                btr, rtr = ring.get(1)
                trv = bankbf(btr)
                for k in range(4):
                    S.op("pe", lambda e, k=k: e.transpose(out=trv[:, k * 16:(k + 1) * 16], in_=cqns[0:NS, k * 128:(k + 1) * 128], identity=ident_b[0:NS, 0:NS]), r=[r_cqns, r_idb], w=rtr)
                for k in range(2):
                    S.op("pe", lambda e, k=k: e.transpose(out=trv[:, 64 + k * 16:64 + (k + 1) * 16], in_=ckvsb[0:NS, k * 128:(k + 1) * 128], identity=ident_b[0:NS, 0:NS]), r=[r_ckvsb, r_idb], w=rtr)
                S.op("pe", lambda e: e.transpose(out=trv[:, 96:112], in_=kpesb[0:NS, :], identity=ident_b[0:NS, 0:NS]), r=[r_kpesb, r_idb], w=rtr)
                trs, r_trs = A.get(112, BF16, "trs")
                S.op("act", lambda e: e.copy(out=trs, in_=trv[:, 0:112]), r=rtr, w=[r_trs])
                cqnTs = trs[:, 0:64].rearrange("p (k t) -> p k t", k=4)
                ckvTs = trs[:, 64:96].rearrange("p (k t) -> p k t", k=2)
                kpeTs = trs[:, 96:112]
                bqn, rqn = ring.get(1)
                qnv = bank(bqn)[:, 0:128].rearrange("p (h t) -> p h t", h=8)
                for h in range(8):
                    for k in range(4):
                        S.op("pe", lambda e, h=h, k=k: e.matmul(qnv[:, h, :], lhsT=wuq_sb[:, k, h * 128:(h + 1) * 128], rhs=cqnTs[:, k, :], start=(k == 0), stop=(k == 3)), r=[r_wuq, r_trs], w=rqn)
                qnTs, r_qnTs = A.get(128, BF16, "qnTs"); qnTs = qnTs.rearrange("p (h t) -> p h t", h=8)
                S.op("act", lambda e: e.copy(out=qnTs, in_=qnv), r=rqn, w=[r_qnTs])
                bqp, rqp = ring.get(1)
                for k in range(4):
                    S.op("pe", lambda e, k=k: e.matmul(bank(bqp)[0:NS, :], lhsT=cqnTs[:, k, :], rhs=wuq_sb[:, k, 1024:1536], start=(k == 0), stop=(k == 3)), r=[r_wuq, r_trs], w=rqp)
                qpv = bank(bqp)[0:NS, :].rearrange("p (h d) -> p h d", h=8)
                qpt, r_qpt = A.get(1024, F32, "qpt")
                qA = qpt[0:NS, 0:512].rearrange("p (h d) -> p h d", h=8)
                qB = qpt[0:NS, 512:1024].rearrange("p (h d) -> p h d", h=8)
                S.op("dve", lambda e: e.tensor_tensor(out=qA, in0=qpv, in1=cc1s.unsqueeze(1).to_broadcast([NS, 8, 64]), op=ALU.mult), r=rqp + [r_tb1], w=[r_qpt])
                S.op("dve", lambda e: e.tensor_tensor(out=qB[:, :, 0:32], in0=qpv[:, :, 32:64], in1=ss1s[:, 0:32].unsqueeze(1).to_broadcast([NS, 8, 32]), op=ALU.mult), r=rqp + [r_tb1], w=[r_qpt])
                S.op("dve", lambda e: e.tensor_tensor(out=qB[:, :, 32:64], in0=qpv[:, :, 0:32], in1=ss1s[:, 32:64].unsqueeze(1).to_broadcast([NS, 8, 32]), op=ALU.mult), r=rqp + [r_tb1], w=[r_qpt])
                qpsb, r_qpsb = A.get(512, BF16, "qpsb")
                S.op("pool", lambda e: e.tensor_tensor(out=qpsb[0:NS, :], in0=qpt[0:NS, 0:512], in1=qpt[0:NS, 512:1024], op=ALU.add), r=[r_qpt], w=[r_qpsb])
                btq, rtq = ring.get(1)
                tqv = bankbf(btq)[:, 0:64].rearrange("p (a t) -> p a t", a=4)
                for a in range(4):
                    S.op("pe", lambda e, a=a: e.transpose(out=tqv[:, a, :], in_=qpsb[0:NS, a * 128:(a + 1) * 128], identity=ident_b[0:NS, 0:NS]), r=[r_qpsb, r_idb], w=rtq)
                qpTs, r_qpTs = A.get(SB * 32, BF16, "qpTs")
                S.op("pool", lambda e: e.memset(qpTs, 0.0), w=[r_qpTs])
                qpT5 = qpTs.rearrange("p (b a two l) -> p b a two l", b=4, a=4, two=2)
                tq4 = tqv.rearrange("p a (b l) -> p a b l", b=4)
                for bb in range(SB):
                    S.op("act", lambda e, bb=bb: e.copy(out=qpT5[0:64, bb, :, 0, :], in_=tq4[0:64, :, bb, :]), r=rtq, w=[r_qpTs])
                    S.op("dve", lambda e, bb=bb: e.tensor_copy(out=qpT5[64:128, bb, :, 1, :], in_=tq4[64:128, :, bb, :]), r=rtq, w=[r_qpTs])
                qpTb = qpTs.rearrange("p (b n) -> p b n", b=4)
                bql, rql = ring.get(1)
                qlv = bank(bql)[:, 0:256].rearrange("p (h k t) -> p h k t", h=8, k=2)
                for h in range(8):
                    for k in range(2):
                        S.op("pe", lambda e, h=h, k=k: e.matmul(qlv[:, h, k, :], lhsT=wukT_sb[:, h, k * 128:(k + 1) * 128], rhs=qnTs[:, h, :], start=True, stop=True), r=[r_wukT, r_qnTs], w=rql)
                qlTs, r_qlTs = A.get(2 * SB * 32, BF16, "qlTs")
                ql5 = qlTs.rearrange("p (k b h l) -> p k b h l", k=2, b=4, h=8)
                qlv5 = qlv.rearrange("p h k (b l) -> p h k b l", b=4)
                for k in range(2):
                    for bb in range(SB):
                        S.op("act" if (k + bb) % 2 == 0 else "dve",
                             (lambda e, k=k, bb=bb: e.copy(out=ql5[:, k, bb], in_=qlv5[:, :, k, bb, :])) if (k + bb) % 2 == 0 else
                             (lambda e, k=k, bb=bb: e.tensor_copy(out=ql5[:, k, bb], in_=qlv5[:, :, k, bb, :])), r=rql, w=[r_qlTs])
                qlTb = qlTs.rearrange("p (k b n) -> p k b n", k=2, b=4)
                gk = []; r_gk = []; gp = []; r_gp = []
                for i_ in range(3):
                    v_, r_ = A.get(2048, BF16, f"gk{i_}"); gk.append(v_.rearrange("p (s c) -> p s c", s=8)); r_gk.append(r_)
                    v_, r_ = A.get(512, BF16, f"gp{i_}"); gp.append(v_.rearrange("p (s c) -> p s c", s=8)); r_gp.append(r_)
                gp2, r_gp2 = dbl(1024, BF16, "gp2", "p (s c) -> p s c", s=8)
                ckTc, r_ckTc = dbl(2048, BF16, "ckTc", "p (k n) -> p k n", k=2)
                kpTc, r_kpTc = dbl(1024, BF16, "kpTc")
                p_s, r_ps = dbl(1024, BF16, "p_s")
                pTc, r_pTc = dbl(256, BF16, "pTc", "p (i t) -> p i t", i=8)
                Oacc, r_Oacc = dbl(256, F32, "Oacc")
                ms, r_ms = dbl(8, F32, "ms")
                olb, r_olb = A.get(256, BF16, "olb")
                olTs, r_olTs = A.get(64, BF16, "olTs"); olTs = olTs.rearrange("p (k n) -> p k n", k=2)
                oTs, r_oTs = A.get(128, BF16, "oTs"); oTs = oTs.rearrange("p (h t) -> p h t", h=8)
                print("sample l1 arena bytes", A.cur)
                sctx = {}
                def sI(bb):
                    mb_ = ms[bb % 2][0:32]
                    oa = Oacc[bb % 2][0:32]
                    r_m = r_ms[bb % 2]
                    r_o = r_Oacc[bb % 2]
                    bsn, rsn = ring.get(1)
                    sn = bank(bsn)[0:32, 0:NS]
                    S.op("pe", lambda e, bb=bb, sn=sn: e.matmul(sn, lhsT=qlTb[:, 0, bb, :], rhs=ckvTs[:, 0, :], start=True, stop=False), r=[r_qlTs, r_trs], w=rsn)
                    S.op("pe", lambda e, bb=bb, sn=sn: e.matmul(sn, lhsT=qlTb[:, 1, bb, :], rhs=ckvTs[:, 1, :], start=False, stop=False), r=[r_qlTs, r_trs], w=rsn)
                    S.op("pe", lambda e, bb=bb, sn=sn: e.matmul(sn, lhsT=qpTb[:, bb, :], rhs=kpeTs, start=False, stop=False), r=[r_qpTs, r_trs], w=rsn)
                    S.op("pe", lambda e, bb=bb, sn=sn: e.matmul(sn, lhsT=ident_b[0:32, 0:32], rhs=mskv[0:32, bb, :], start=False, stop=True), r=[r_idb, r_msk], w=rsn)
                    S.op("dve", lambda e, sn=sn, mb_=mb_: e.reduce_max(out=mb_[:, 0:1], in_=sn, axis=AX.X), r=rsn, w=[r_m])
                    S.op("dve", lambda e, mb_=mb_: e.tensor_scalar(out=mb_[:, 5:6], in0=mb_[:, 0:1], scalar1=-MLA_SCALE, scalar2=None, op0=ALU.mult), r=[r_m], w=[r_m])
                    psl = p_s[0][0:32]
                    S.op("act", lambda e, sn=sn, mb_=mb_, psl=psl: e.activation(out=psl[:, 0:NS], in_=sn, func=AF.Exp, bias=mb_[:, 5:6], scale=MLA_SCALE, accum_out=mb_[:, 1:2]), r=rsn + [r_m], w=[r_ps[0], r_m])
                    bpt, rpt = ring.get(1)
                    S.op("pe", lambda e, psl=psl, bpt=bpt: e.transpose(out=bankbf(bpt)[0:NS, 0:32], in_=psl[:, 0:NS], identity=ident_b[0:32, 0:32]), r=[r_ps[0], r_idb], w=rpt)
                    S.op("act", lambda e, bpt=bpt: e.copy(out=pTc[0][0:NS, 0, :], in_=bankbf(bpt)[0:NS, 0:32]), r=rpt, w=[r_pTc[0]])
                    bo_, ro_ = ring.get(1)
                    S.op("pe", lambda e, bo_=bo_: e.matmul(bank(bo_)[0:32, 0:256], lhsT=pTc[0][0:NS, 0, :], rhs=ckvsb[0:NS, :], start=True, stop=True), r=[r_pTc[0], r_ckvsb], w=ro_)
                    S.op("dve", lambda e, bo_=bo_, oa=oa: e.tensor_copy(out=oa, in_=bank(bo_)[0:32, 0:256]), r=ro_, w=[r_o])

                def sG(bb, stp, g3):
                    S.dma("pool", lambda e, g3=g3, bb=bb, stp=stp: e.indirect_dma_start(out=gk[g3].rearrange("p s c -> p (s c)"), out_offset=None, in_=cckv[:, :], element_offset=stp * 2048,
                                                                                         in_offset=bass.IndirectOffsetOnAxis(ap=pt_sb[:, bb:bb + 1], axis=0)), r=[r_pt], w=[r_gk[g3]])
                    S.dma("pool", lambda e, g3=g3, bb=bb, stp=stp: e.indirect_dma_start(out=gp[g3].rearrange("p s c -> p (s c)"), out_offset=None, in_=ckpe[:, :], element_offset=stp * 512,
                                                                                         in_offset=bass.IndirectOffsetOnAxis(ap=pt_sb[:, bb:bb + 1], axis=0)), r=[r_pt], w=[r_gp[g3]])

                def sD(bb, stp, gs, g3):
                    S.op("act", lambda e, gs=gs, g3=g3: e.copy(out=gp2[gs][:, :, 0:64], in_=gp[g3]), r=[r_gp[g3]], w=[r_gp2[gs]])
                    S.op("dve", lambda e, gs=gs, g3=g3: e.tensor_copy(out=gp2[gs][:, :, 64:128], in_=gp[g3]), r=[r_gp[g3]], w=[r_gp2[gs]])

                def sT(bb, stp, gs, g3):
                    for hf in range(2):
                        sl = hf * 4
                        bA, rA = ring.get(1)
                        bB, rB = ring.get(1)
                        bC, rC = ring.get(1)
                        for i_ in range(4):
                            S.op("pe", lambda e, i_=i_, sl=sl, g3=g3, bA=bA: e.transpose(out=bankbf(bA)[:, i_ * 128:(i_ + 1) * 128], in_=gk[g3][:, sl + i_, 0:128], identity=ident_b[:]), r=[r_gk[g3], r_idb], w=rA)
                        for i_ in range(4):
                            S.op("pe", lambda e, i_=i_, sl=sl, g3=g3, bB=bB: e.transpose(out=bankbf(bB)[:, i_ * 128:(i_ + 1) * 128], in_=gk[g3][:, sl + i_, 128:256], identity=ident_b[:]), r=[r_gk[g3], r_idb], w=rB)
                        for i_ in range(4):
                            S.op("pe", lambda e, i_=i_, sl=sl, gs=gs, bC=bC: e.transpose(out=bankbf(bC)[:, i_ * 128:(i_ + 1) * 128], in_=gp2[gs][:, sl + i_, :], identity=ident_b[:]), r=[r_gp2[gs], r_idb], w=rC)
                        cs_ = slice(hf * 512, (hf + 1) * 512)
                        S.op("act", lambda e, gs=gs, bA=bA, cs_=cs_: e.copy(out=ckTc[gs][:, 0, cs_], in_=bankbf(bA)[:, 0:512]), r=rA, w=[r_ckTc[gs]])
                        S.op("dve", lambda e, gs=gs, bB=bB, cs_=cs_: e.tensor_copy(out=ckTc[gs][:, 1, cs_], in_=bankbf(bB)[:, 0:512]), r=rB, w=[r_ckTc[gs]])
                        if hf == 0:
                            S.op("act", lambda e, gs=gs, bC=bC, cs_=cs_: e.copy(out=kpTc[gs][:, cs_], in_=bankbf(bC)[:, 0:512]), r=rC, w=[r_kpTc[gs]])
                        else:
                            S.op("dve", lambda e, gs=gs, bC=bC, cs_=cs_: e.tensor_copy(out=kpTc[gs][:, cs_], in_=bankbf(bC)[:, 0:512]), r=rC, w=[r_kpTc[gs]])

                def sU1(bb, stp, gs):
                    mb_ = ms[bb % 2][0:32]
                    oa = Oacc[bb % 2][0:32]
                    r_m = r_ms[bb % 2]
                    r_o = r_Oacc[bb % 2]
                    bs2, rs2 = ring.get(2)
                    sfull = bank(bs2, 2)[0:32, :]
                    for hf in range(2):
                        cs_ = slice(hf * 512, (hf + 1) * 512)
                        S.op("pe", lambda e, bb=bb, gs=gs, cs_=cs_, sfull=sfull: e.matmul(sfull[:, cs_], lhsT=qlTb[:, 0, bb, :], rhs=ckTc[gs][:, 0, cs_], start=True, stop=False), r=[r_qlTs, r_ckTc[gs]], w=[rs2[hf]])
                        S.op("pe", lambda e, bb=bb, gs=gs, cs_=cs_, sfull=sfull: e.matmul(sfull[:, cs_], lhsT=qlTb[:, 1, bb, :], rhs=ckTc[gs][:, 1, cs_], start=False, stop=False), r=[r_qlTs, r_ckTc[gs]], w=[rs2[hf]])
                        S.op("pe", lambda e, bb=bb, gs=gs, cs_=cs_, sfull=sfull: e.matmul(sfull[:, cs_], lhsT=qpTb[:, bb, :], rhs=kpTc[gs][:, cs_], start=False, stop=True), r=[r_qpTs, r_kpTc[gs]], w=[rs2[hf]])
                    S.op("dve", lambda e, sfull=sfull, mb_=mb_: e.reduce_max(out=mb_[:, 2:3], in_=sfull, axis=AX.X), r=rs2, w=[r_m])
                    S.op("dve", lambda e, mb_=mb_: e.tensor_tensor(out=mb_[:, 3:4], in0=mb_[:, 0:1], in1=mb_[:, 2:3], op=ALU.max), r=[r_m], w=[r_m])
                    S.op("dve", lambda e, mb_=mb_: e.tensor_tensor(out=mb_[:, 4:5], in0=mb_[:, 0:1], in1=mb_[:, 3:4], op=ALU.subtract), r=[r_m], w=[r_m])
                    S.op("act", lambda e, mb_=mb_: e.activation(out=mb_[:, 4:5], in_=mb_[:, 4:5], func=AF.Exp, scale=MLA_SCALE), r=[r_m], w=[r_m])
                    S.op("dve", lambda e, mb_=mb_: e.tensor_scalar(out=mb_[:, 5:6], in0=mb_[:, 3:4], scalar1=-MLA_SCALE, scalar2=None, op0=ALU.mult), r=[r_m], w=[r_m])
                    pq = gs
                    psl = p_s[pq][0:32]
                    S.op("act", lambda e, sfull=sfull, mb_=mb_, psl=psl: e.activation(out=psl, in_=sfull, func=AF.Exp, bias=mb_[:, 5:6], scale=MLA_SCALE, accum_out=mb_[:, 6:7]), r=rs2 + [r_m], w=[r_ps[pq], r_m])
                    S.op("dve", lambda e, mb_=mb_: e.scalar_tensor_tensor(out=mb_[:, 1:2], in0=mb_[:, 1:2], scalar=mb_[:, 4:5], in1=mb_[:, 6:7], op0=ALU.mult, op1=ALU.add), r=[r_m], w=[r_m])
                    S.op("dve", lambda e, mb_=mb_: e.tensor_copy(out=mb_[:, 0:1], in_=mb_[:, 3:4]), r=[r_m], w=[r_m])

                def sU2(bb, stp, gs, g3):
                    mb_ = ms[bb % 2][0:32]
                    oa = Oacc[bb % 2][0:32]
                    r_m = r_ms[bb % 2]
                    r_o = r_Oacc[bb % 2]
                    pq = gs
                    psl = p_s[pq][0:32]
                    bpt, rpt = ring.get(1)
                    ptv = bankbf(bpt)[:, 0:256].rearrange("p (i t) -> p i t", i=8)
                    for i_ in range(8):
                        S.op("pe", lambda e, i_=i_, psl=psl, ptv=ptv: e.transpose(out=ptv[:, i_, :], in_=psl[:, i_ * 128:(i_ + 1) * 128], identity=ident_b[0:32, 0:32]), r=[r_ps[pq], r_idb], w=rpt)
                    S.op("act", lambda e, ptv=ptv, pq=pq: e.copy(out=pTc[pq], in_=ptv), r=rpt, w=[r_pTc[pq]])
                    bo_, ro_ = ring.get(1)
                    for i_ in range(8):
                        S.op("pe", lambda e, i_=i_, bo_=bo_, pq=pq, g3=g3: e.matmul(bank(bo_)[0:32, 0:256], lhsT=pTc[pq][:, i_, :], rhs=gk[g3][:, i_, :], start=(i_ == 0), stop=(i_ == 7)), r=[r_pTc[pq], r_gk[g3]], w=ro_)
                    S.op("dve", lambda e, bo_=bo_, oa=oa, mb_=mb_: e.scalar_tensor_tensor(out=oa, in0=oa, scalar=mb_[:, 4:5], in1=bank(bo_)[0:32, 0:256], op0=ALU.mult, op1=ALU.add), r=ro_ + [r_o, r_m], w=[r_o])


                def sF(bb):
                    mb_ = ms[bb % 2][0:32]
                    oa = Oacc[bb % 2][0:32]
                    r_m = r_ms[bb % 2]
                    r_o = r_Oacc[bb % 2]
                    S.op("dve", lambda e, mb_=mb_: e.reciprocal(out=mb_[:, 7:8], in_=mb_[:, 1:2]), r=[r_m], w=[r_m])
                    S.op("act", lambda e, oa=oa, mb_=mb_: e.activation(out=olb[0:32, :], in_=oa, func=AF.Identity, bias=0.0, scale=mb_[:, 7:8]), r=[r_o, r_m], w=[r_olb])
                    bot, rot = ring.get(1)
                    otv = bankbf(bot)[:, 0:64].rearrange("p (k n) -> p k n", k=2)
                    for k in range(2):
                        S.op("pe", lambda e, k=k, otv=otv: e.transpose(out=otv[:, k, :], in_=olb[0:32, k * 128:(k + 1) * 128], identity=ident_b[0:32, 0:32]), r=[r_olb, r_idb], w=rot)
                    S.op("dve", lambda e, otv=otv: e.tensor_copy(out=olTs, in_=otv), r=rot, w=[r_olTs])
                    boh, roh = ring.get(1)
                    ohv = bank(boh)[:, 0:32].rearrange("p (h l) -> p h l", h=8)
                    for h in range(8):
                        for k in range(2):
                            S.op("pe", lambda e, h=h, k=k, ohv=ohv: e.matmul(ohv[:, h, :], lhsT=wuv_sb[:, k, h * 128:(h + 1) * 128], rhs=olTs[:, k, h * 4:(h + 1) * 4], start=(k == 0), stop=(k == 1)), r=[r_wuv, r_olTs], w=roh)
                    S.op("act", lambda e, ohv=ohv, bb=bb: e.copy(out=oTs[:, :, bb * 4:bb * 4 + 4], in_=ohv), r=roh, w=[r_oTs])

                NTOT = SB * NSTEP
                sG(0, 0, 0)
                sG(0, 1, 1)
                sD(0, 0, 0, 0)
                sT(0, 0, 0, 0)
                for idx in range(NTOT):
                    bb, stp = divmod(idx, NSTEP)
                    if idx + 2 < NTOT:
                        b3, s3 = divmod(idx + 2, NSTEP)
                        sG(b3, s3, (idx + 2) % 3)
                    if stp == 0:
                        sI(bb)
                    if idx + 1 < NTOT:
                        sD(0, 0, (idx + 1) % 2, (idx + 1) % 3)
                    sU1(bb, stp, idx % 2)
                    if idx + 1 < NTOT:
                        b2, s2 = divmod(idx + 1, NSTEP)
                        sT(b2, s2, (idx + 1) % 2, (idx + 1) % 3)
                    sU2(bb, stp, idx % 2, idx % 3)
                    if stp == NSTEP - 1:
                        sF(bb)
                by_, ry_ = ring.get(2)
                for hf in range(2):
                    for c in range(8):
                        S.op("pe", lambda e, hf=hf, c=c: e.matmul(bank(by_ + hf)[0:NS, :], lhsT=oTs[:, c, :], rhs=wo1_sb[:, c, hf * 512:(hf + 1) * 512], start=(c == 0), stop=(c == 7)), r=[r_oTs, r_wo1], w=[ry_[hf]])
                zs1, r_zs1 = A.get(D, F32, "zs1")
                scs1, r_scs1 = A.get(16, F32, "scs1")
                S.op("dve", lambda e: e.scalar_tensor_tensor(out=zs1[0:NS, :], in0=XS[:], scalar=ALPHA, in1=bank(by_, 2)[0:NS, :], op0=ALU.mult, op1=ALU.add), r=[r_XS] + ry_, w=[r_zs1])
                layer_norm_tile(zs1[0:NS, :], r_zs1, XS[:], r_XS, NS, scs1, r_scs1, lnt1, r_lnt1)

            if with_sample:
                sample_l1()
            chk('l1')
            ffn(1, True)


        except _Stop:
            print('stopped at', stop)
        S.finish(out_toks)
        S.emit()
        print("ops", S.stats())
    return nc


_CACHE = {}


def _prep_shared(inp, tabs):
    f = lambda a: np.ascontiguousarray(np.asarray(a, dtype=np.float32))
    sh = {}
    sh["w_in"] = f(inp["w_in_even"][0])
    sh["pool_w"] = f(np.transpose(inp["pool_w"][0], (1, 0, 2)))
    sh["pool_scale"] = f(inp["pool_scale"][0].reshape(4, 128).T)
    sh["gn_g"] = f(inp["ret_gn_g"][0])
    sh["w_o0"] = f(inp["w_o_even"][0])
    sh["w_dq"] = f(inp["w_dq"][0])
    sh["qn_g"] = f(inp["q_norm_g"][0])
    wuq = np.asarray(inp["w_uq"][0]).reshape(512, 8, 192)
    sh["w_uq"] = f(np.concatenate([wuq[:, :, :128].reshape(512, 1024), wuq[:, :, 128:].reshape(512, 512)], axis=1))
    sh["w_dkv"] = f(inp["w_dkv"][0])
    sh["kvn_g"] = f(inp["kv_norm_g"][0])
    sh["w_ukT"] = f(np.transpose(inp["w_uk"][0], (2, 1, 0)))
    sh["w_uv"] = f(np.asarray(inp["w_uv"][0]).reshape(256, 1024))
    sh["w_o1"] = f(inp["w_o_mla"][0])
    wup = np.asarray(inp["w_up"])
    a = wup[:, :, :DFF].reshape(DEPTH, 8, 128, NJ, 128)
    b = wup[:, :, DFF:].reshape(DEPTH, 8, 128, NJ, 128)
    ab = np.concatenate([a, b], axis=4)
    sh["w_up"] = f(np.transpose(ab, (0, 3, 2, 1, 4)))
    sh["w_down"] = f(inp["w_down"])
    sh["conv_w"] = f(np.transpose(np.asarray(inp["conv_w"]).reshape(DEPTH, 3, NJ, 128), (0, 3, 2, 1)))
    sh["conv_b"] = f(np.transpose(np.asarray(inp["conv_b"]).reshape(DEPTH, NJ, 128), (0, 2, 1)))
    for k in ("ln_mix_g", "ln_mix_b", "ln_ffn_g", "ln_ffn_b"):
        sh[k] = f(inp[k])
    sh["cckv"] = np.asarray(inp["cache_ckv"], dtype=np.float32).reshape(5120, 128 * 256)
    sh["ckpe"] = np.asarray(inp["cache_kpe"], dtype=np.float32).reshape(5120, 128 * 64)
    for k, v in tabs.items():
        if isinstance(v, np.ndarray):
            sh["t_" + k] = v
    return sh


def kernel(**inp):
    tabs = host_tables()
    if "nc" not in _CACHE:
        _CACHE["nc"] = build_program(tabs)
    nc = _CACHE["nc"]
    sh = _prep_shared(inp, tabs)
    f = lambda a: np.ascontiguousarray(np.asarray(a, dtype=np.float32))
    in_maps = []
    for c in range(8):
        m = dict(sh)
        bs = slice(SB * c, SB * (c + 1))
        m["xp"] = f(inp["x_prompt"][c])
        m["xs"] = f(np.asarray(inp["x_sample"][bs]).reshape(NS, D))
        m["spool"] = f(np.asarray(inp["state_pool"][0, bs]).reshape(SB * 15, 512))
        m["sret"] = f(np.asarray(inp["state_ret"][0, bs]).reshape(SB * RH, 128, 128))
        m["sconv"] = f(np.asarray(inp["state_conv"][:, bs]).reshape(DEPTH, SB * 2, DFF))
        m["ptab"] = np.ascontiguousarray(np.asarray(inp["page_table"][bs]).T.astype(np.int32))
        in_maps.append(m)
    res = run_bass_kernel_spmd(nc, in_maps, core_ids=list(range(8)))
    R = res.results
    cat = lambda k: np.stack([np.asarray(R[c][k]) for c in range(8)])
    y_p = cat("yp")
    y_s = cat("ys").reshape(32, SL, D)
    pool_p = cat("poolp")[None]
    pool_s = cat("pools").reshape(1, 32, 15, 512)
    ret_p = cat("retp")[None]
    ret_s = cat("rets").reshape(1, 32, RH, 128, 128)
    ckv_p = cat("ckvp")[None]
    ckv_s = cat("ckvs").reshape(1, 32, SL, 256)
    kpe_p = cat("kpep")[None]
    kpe_s = cat("kpes").reshape(1, 32, SL, 64)
    conv_p = np.transpose(cat("convp"), (1, 0, 2, 3))
    conv_s = np.transpose(cat("convs").reshape(8, DEPTH, SB, 2, DFF), (1, 0, 2, 3, 4)).reshape(DEPTH, 32, 2, DFF)
    outs = (y_p, y_s, pool_p, pool_s, ret_p, ret_s, ckv_p, ckv_s, kpe_p, kpe_s, conv_p, conv_s)
    return tuple(np.ascontiguousarray(o.astype(np.float32)) for o in outs)
```

```python
import contextlib
import math
import numpy as np
import concourse.bass as bass
import concourse.mybir as mybir
from concourse.bass_utils import run_bass_kernel_spmd

F32 = mybir.dt.float32
BF16 = mybir.dt.bfloat16
I32 = mybir.dt.int32
AF = mybir.ActivationFunctionType
ALU = mybir.AluOpType
AX = mybir.AxisListType

ALL_Q = ("pe", "act", "dve", "pool", "sp")
SAME_ENGINE_SYNC = True

D = 1024
SEQ = 2048
NT = SEQ // 128
DEPTH = 2
NS = 16
SB = 4
SL = 4
PAST = 16384
NPAGES = 128
POOL_W = (2, 4, 8, 16)
RH = 4
DFF = 2816
NJ = DFF // 128
ALPHA = (2.0 * DEPTH) ** 0.25
LN_EPS = 1e-5
RMS_EPS = 1e-6
GN_EPS = 1e-6
MLA_SCALE = (128 + 64) ** -0.5
KSCALE = 128 ** -0.5
NEG = -30000.0


class Res:
    __slots__ = ("name", "w", "r")

    def __init__(self, name):
        self.name = name
        self.w = None
        self.r = []


class Op:
    __slots__ = ("q", "fn", "waits", "signal", "dma", "idx", "deps", "sdep", "succ", "nrem", "ready", "cost", "lat", "fin", "orig", "done", "phase")

    def __init__(self, q, fn, dma=None):
        self.q = q
        self.fn = fn
        self.waits = []
        self.signal = False
        self.dma = dma
        self.idx = None
        self.deps = []
        self.sdep = None
        self.succ = []
        self.nrem = 0
        self.ready = 0.0
        self.cost = 0.1
        self.lat = 0.0
        self.fin = 0.0
        self.orig = 0
        self.done = False
        self.phase = 0


class _Rec:
    def __init__(self):
        self.info = ("", None)

    def __getattr__(self, name):
        def f(*a, **k):
            out = k.get("out", a[0] if a else None)
            object.__setattr__(self, "info", (name, out))
            return self
        return f


import os as _os
RESCHED_PHASES = set(_os.environ.get("RESCHED", "l0,ffn0,ffn1").split(","))


class Sched:
    def __init__(self, nc, stack):
        self.nc = nc
        self.stack = stack
        self.all = []
        self.ops = {q: [] for q in ALL_Q}
        self.dma_sems = {}
        self.dma_last = {}
        self.eng_sems = {}
        self.nres = 0
        self.phase = 0
        self.phase_names = ['init']

    def res(self, name=None):
        self.nres += 1
        return Res(name or f"r{self.nres}")

    def set_phase(self, name):
        self.phase_names.append(name)
        self.phase = len(self.phase_names) - 1

    def _cost(self, o):
        try:
            rec = _Rec()
            o.fn(rec)
            name, out = rec.info
            shp = list(out.shape)
            n = 1
            for d in shp[1:]:
                n *= d
        except Exception:
            name, n = "", 256
        q = o.q
        if o.dma is not None:
            o.cost = 1.2 if q == "pool" else 0.15
            o.lat = 2.5 + n * shp[0] * 3.0 / 250e3 if name else 3.0
        elif q == "pe":
            o.cost = 0.05 + n / 1900.0
        elif q == "act":
            o.cost = 0.22 + n / 1000.0
        elif q == "dve":
            o.cost = 0.2 + n / 900.0
        else:
            o.cost = 0.35 + n / 450.0

    def _deps(self, o, r, w, tok):
        seen = set()
        for res in r:
            if res.w is not None and id(res.w[-1]) not in seen:
                seen.add(id(res.w[-1])); o.deps.append(res.w)
        for res in w:
            if res.w is not None and id(res.w[-1]) not in seen:
                seen.add(id(res.w[-1])); o.deps.append(res.w)
            for t in res.r:
                if id(t[-1]) not in seen:
                    seen.add(id(t[-1])); o.deps.append(t)
        for res in r:
            res.r.append(tok)
        for res in w:
            res.w = tok
            res.r = []

    def op(self, q, fn, r=(), w=()):
        o = Op(q, fn)
        o.orig = len(self.all)
        o.phase = self.phase
        self._deps(o, r, w, ("eng", o))
        self.all.append(o)
        return o

    def dma(self, q, fn, r=(), w=(), semkey=None):
        res0 = (list(w) + list(r))[0] if (w or r) else None
        key = semkey if semkey is not None else id(res0)
        if key not in self.dma_sems:
            self.dma_sems[key] = [None, 0]
        ent = self.dma_sems[key]
        ent[1] += 16
        o = Op(q, fn, dma=(key, ent[1]))
        o.orig = len(self.all)
        o.phase = self.phase
        o.sdep = self.dma_last.get(key)
        self.dma_last[key] = o
        tok = ("dma", key, ent[1], o)
        self._deps(o, r, w, tok)
        self.all.append(o)
        return tok

    def finish(self, out_tokens):
        o = Op("sp", None)
        o.orig = len(self.all)
        o.phase = self.phase
        seen = set()
        for t in out_tokens:
            if id(t[-1]) not in seen:
                seen.add(id(t[-1])); o.deps.append(t)
        self.all.append(o)

    def _schedule(self):
        ops = self.all
        for o in ops:
            if o.fn is not None:
                self._cost(o)
            dset = set(id(t[-1]) for t in o.deps)
            preds = [(t[-1], True) for t in o.deps]
            if o.sdep is not None and id(o.sdep) not in dset:
                preds.append((o.sdep, False))
            o.nrem = len(preds)
            for p, kind in preds:
                p.succ.append((o, kind))
        tfree = {q: 0.0 for q in ALL_Q}

        def place(o, st):
            self.ops[o.q].append(o)
            tfree[o.q] = st + o.cost
            o.fin = st + o.cost + o.lat
            o.done = True
            for s, kind in o.succ:
                t = o.fin if kind else st
                if t > s.ready:
                    s.ready = t
                s.nrem -= 1

        nph = len(self.phase_names)
        byph = [[] for _ in range(nph)]
        for o in ops:
            byph[o.phase].append(o)
        for ph in range(nph):
            plist = byph[ph]
            if self.phase_names[ph] not in RESCHED_PHASES:
                for o in plist:
                    place(o, max(tfree[o.q], o.ready))
                continue
            pending = set(id(o) for o in plist)
            ready = {q: [] for q in ALL_Q}
            inready = set()
            for o in plist:
                if o.nrem == 0:
                    ready[o.q].append(o); inready.add(id(o))
            nleft = len(plist)
            while nleft:
                best = None
                for q in ALL_Q:
                    lst = ready[q]
                    if not lst:
                        continue
                    tf = tfree[q]
                    cand = None
                    for o in lst:
                        st = o.ready if o.ready > tf else tf
                        key = (st, o.orig)
                        if cand is None or key < cand[0]:
                            cand = (key, o)
                    if best is None or cand[0] < best[0]:
                        best = cand
                (st, _), o = best
                ready[o.q].remove(o)
                place(o, st)
                nleft -= 1
                for s, kind in o.succ:
                    if s.nrem == 0 and id(s) in pending and id(s) not in inready and not s.done:
                        ready[s.q].append(s); inready.add(id(s))

    def _add_wait(self, op, tok, waited, dwaited):
        q = op.q
        if tok[0] == "eng":
            src = tok[1]
            sq, sidx = src.q, src.idx
            if sq == q and (q == "pe" or q == "sp" or not SAME_ENGINE_SYNC):
                return
            if waited[q].get(sq, -1) >= sidx:
                return
            waited[q][sq] = sidx
            op.waits.append(("eng", sq, sidx))
            src.signal = True
        else:
            _, key, val, _src = tok
            if dwaited[q].get(key, -1) >= val:
                return
            dwaited[q][key] = val
            op.waits.append(("dma", key, val))

    def emit(self):
        nc = self.nc
        st = self.stack
        self._schedule()
        waited = {q: {} for q in ALL_Q}
        dwaited = {q: {} for q in ALL_Q}
        for q in ALL_Q:
            for i, o in enumerate(self.ops[q]):
                o.idx = i
        for q in ALL_Q:
            for o in self.ops[q]:
                best = {}
                for t in o.deps:
                    if t[0] == "eng":
                        k = ("eng", t[1].q)
                        v = t[1].idx
                    else:
                        k = ("dma", t[1])
                        v = t[2]
                    if k not in best or best[k][0] < v:
                        best[k] = (v, t)
                for v, t in best.values():
                    self._add_wait(o, t, waited, dwaited)
        for q in ALL_Q:
            self.eng_sems[q] = st.enter_context(nc.semaphore(f"es_{q}"))
        for i, key in enumerate(self.dma_sems):
            self.dma_sems[key][0] = st.enter_context(nc.semaphore(f"ds_{i}"))
        cnt = {}
        for q in ALL_Q:
            c = 0
            arr = []
            for o in self.ops[q]:
                if o.signal:
                    c += 1
                arr.append(c)
            cnt[q] = arr
        block = st.enter_context(nc.Block())

        def run(q, eng):
            for o in self.ops[q]:
                for t in o.waits:
                    if t[0] == "eng":
                        eng.wait_ge(self.eng_sems[t[1]], cnt[t[1]][t[2]])
                    else:
                        eng.wait_ge(self.dma_sems[t[1]][0], t[2])
                if o.fn is None:
                    continue
                ins = o.fn(eng)
                if o.dma is not None:
                    ins.then_inc(self.dma_sems[o.dma[0]][0], 16)
                elif o.signal:
                    ins.then_inc(self.eng_sems[q], 1)

        @block.tensor
        def _(eng):
            run("pe", eng)

        @block.scalar
        def _(eng):
            run("act", eng)

        @block.vector
        def _(eng):
            run("dve", eng)

        @block.gpsimd
        def _(eng):
            run("pool", eng)

        @block.sync
        def _(eng):
            run("sp", eng)

    def stats(self):
        return {q: (len(self.ops[q]), sum(len(o.waits) for o in self.ops[q])) for q in ALL_Q}


def _rope_tables(half, pos):
    inv = (np.float32(10000.0) ** (-(np.arange(half, dtype=np.float32)) / np.float32(half))).astype(np.float32)
    ang = (pos.astype(np.float32)[:, None] * inv[None, :]).astype(np.float32)
    return np.cos(ang).astype(np.float32), np.sin(ang).astype(np.float32)


def host_tables():
    t = {}
    pos_p = np.arange(SEQ, dtype=np.float32)
    pos_s = (PAST + np.arange(SL)).astype(np.float32)
    c, s = _rope_tables(64, pos_p)
    t["c2p"] = np.ascontiguousarray(np.concatenate([c, c], 1).T)
    t["ssp"] = np.ascontiguousarray(np.concatenate([s, -s], 1).T)
    c, s = _rope_tables(64, pos_s)
    t["c2s"] = np.ascontiguousarray(np.tile(np.concatenate([c, c], 1).T, (1, SB)))
    t["sss"] = np.ascontiguousarray(np.tile(np.concatenate([s, -s], 1).T, (1, SB)))
    c, s = _rope_tables(32, pos_p)
    cc = np.concatenate([c, c], 1).reshape(NT, 128, 64).transpose(1, 0, 2)
    ss = np.concatenate([-s, s], 1).reshape(NT, 128, 64).transpose(1, 0, 2)
    t["cc1"] = np.ascontiguousarray(cc)
    t["ss1"] = np.ascontiguousarray(ss)
    c, s = _rope_tables(32, pos_s)
    t["cc1s"] = np.ascontiguousarray(np.tile(np.concatenate([c, c], 1), (SB, 1)))
    t["ss1s"] = np.ascontiguousarray(np.tile(np.concatenate([-s, s], 1), (SB, 1)))
    lg = np.log(np.float32(1.0) - np.float32(2.0) ** (np.float32(-5.0) - np.arange(RH, dtype=np.float32))).astype(np.float32)
    for name, C in (("p", 128), ("s", SL)):
        idx = np.arange(C, dtype=np.float32)
        diff = idx[:, None] - idx[None, :]
        dec = np.where(diff >= 0, np.exp(np.maximum(diff, 0.0)[None] * lg[:, None, None]), 0.0).astype(np.float32)
        t["decT" + name] = np.ascontiguousarray((dec * np.float32(KSCALE)).transpose(2, 0, 1))
        qd = np.exp((idx + 1.0)[:, None] * lg[None, :]).astype(np.float32)
        t["qdec" + name] = np.ascontiguousarray(np.broadcast_to(qd.T[None], (128, RH, C)))
        kd = np.exp((C - 1.0 - idx)[:, None] * lg[None, :]).astype(np.float32) * np.float32(KSCALE)
        t["kdec" + name] = np.ascontiguousarray(kd)
        t["gC" + name] = [float(np.exp(np.float32(C) * lg[h])) for h in range(RH)]
    ic = np.zeros((128, 4, 15), np.float32)
    for g, w in enumerate(POOL_W):
        ic[:, g, :] = 1.0 / np.minimum(float(w), np.arange(15) + 1.0)
    t["invcnt"] = ic
    qi = np.arange(128)
    t["maskd"] = np.where(qi[None, :] <= qi[:, None], 0.0, NEG).astype(np.float32)
    t["ident"] = np.eye(128, dtype=np.float32)
    r = np.arange(32) % 4
    t["masks"] = np.where(np.arange(4)[None, :] <= r[:, None], 0.0, NEG).astype(np.float32)
    mb = np.full((32, SB, NS), NEG, np.float32)
    for bb in range(SB):
        for l2 in range(SL):
            mb[:, bb, bb * SL + l2] = np.where(l2 <= r, 0.0, NEG)
    t["masksb"] = np.ascontiguousarray(mb.reshape(32, SB * NS))
    return t


ARENA_BYTES = 142144


class PsumRing:
    def __init__(self, S):
        self.res = [S.res(f"bank{i}") for i in range(8)]
        self.nxt = 0

    def get(self, n=1):
        if n == 2 and self.nxt % 2:
            self.nxt = (self.nxt + 1) % 8
        if n == 4 and self.nxt % 4:
            self.nxt = (self.nxt + 4 - self.nxt % 4) % 8
        b = self.nxt
        self.nxt = (self.nxt + n) % 8
        return b, [self.res[b + i] for i in range(n)]


class Arena:
    def __init__(self, S, buf):
        self.S = S
        self.buf = buf
        self.hist = []
        self.cur = 0

    def at(self, off):
        assert off % 4 == 0
        self.cur = off

    def get(self, nelem, dt=BF16, name=None, nres=None):
        size = 2 if dt == BF16 else 4
        nb = (nelem * size + 3) // 4 * 4
        s, e = self.cur, self.cur + nb
        assert e <= ARENA_BYTES, (name, e)
        self.cur = e
        pend = []
        for (s2, e2, r2) in self.hist:
            if s2 < e and s < e2:
                if r2.w is not None:
                    pend.append(r2.w)
                pend.extend(r2.r)
        seen = set()
        p2 = []
        for t in pend:
            if id(t[-1]) not in seen:
                seen.add(id(t[-1])); p2.append(t)
        pend = p2
        rs = []
        for i in range(nres or 1):
            r = self.S.res(f"{name}_{i}")
            r.r = list(pend)
            self.hist.append((s, e, r))
            rs.append(r)
        v = self.buf[:, s // 2:s // 2 + nb // 2]
        if dt != BF16:
            v = v.bitcast(dt)
        return v[:, 0:nelem], (rs if nres else rs[0])


class _Stop(Exception):
    pass


def build_program(tabs, with_sample=True, stop=None):
    nc = bass.Bass("TRN2", target_bir_lowering=False)

    def din(name, shape, dt=F32):
        return nc.dram_tensor(name, list(shape), dt, kind="ExternalInput").ap()

    def dout(name, shape):
        return nc.dram_tensor(name, list(shape), F32, kind="ExternalOutput").ap()

    xp = din("xp", [SEQ, D])
    xs = din("xs", [NS, D])
    spool = din("spool", [SB * 15, 512])
    sret = din("sret", [SB * RH, 128, 128])
    sconv = din("sconv", [DEPTH, SB * 2, DFF])
    ptab = din("ptab", [128, SB], I32)
    if with_sample:
        cckv = din("cckv", [5120, 128 * 256])
        ckpe = din("ckpe", [5120, 128 * 64])
    w_in = din("w_in", [D, 2560])
    pool_w = din("pool_w", [128, 4, 128])
    pool_scale = din("pool_scale", [128, 4])
    gn_g = din("gn_g", [512])
    w_o0 = din("w_o0", [D, D])
    w_dq = din("w_dq", [D, 512])
    qn_g = din("qn_g", [512])
    w_uq = din("w_uq", [512, 1536])
    w_dkv = din("w_dkv", [D, 320])
    kvn_g = din("kvn_g", [256])
    w_ukT = din("w_ukT", [128, 8, 256])
    w_uv = din("w_uv", [256, 1024])
    w_o1 = din("w_o1", [D, D])
    w_up = din("w_up", [DEPTH, NJ, 128, 8, 256])
    w_down = din("w_down", [DEPTH, DFF, D])
    conv_w = din("conv_w", [DEPTH, 128, NJ, 3])
    conv_b = din("conv_b", [DEPTH, 128, NJ])
    ln_mix_g = din("ln_mix_g", [DEPTH, D])
    ln_mix_b = din("ln_mix_b", [DEPTH, D])
    ln_ffn_g = din("ln_ffn_g", [DEPTH, D])
    ln_ffn_b = din("ln_ffn_b", [DEPTH, D])
    tdr = {k: din("t_" + k, v.shape) for k, v in tabs.items() if isinstance(v, np.ndarray)}

    yp = dout("yp", [SEQ, D])
    ys = dout("ys", [NS, D])
    poolp = dout("poolp", [15, 512])
    pools = dout("pools", [SB * 15, 512])
    retp = dout("retp", [RH, 128, 128])
    rets = dout("rets", [SB * RH, 128, 128])
    ckvp = dout("ckvp", [SEQ, 256])
    ckvs = dout("ckvs", [NS, 256])
    kpep = dout("kpep", [SEQ, 64])
    kpes = dout("kpes", [NS, 64])
    convp = dout("convp", [DEPTH, 2, DFF])
    convs = dout("convs", [DEPTH, SB * 2, DFF])

    out_toks = []
    st = contextlib.ExitStack()
    with st:
        S = Sched(nc, st)
        _n = [0]

        def sb(shape, dt, name):
            return st.enter_context(nc.sbuf_tensor(name, list(shape), dt))

        P = st.enter_context(nc.psum_tensor("P", [128, 8, 512], F32))
        ring = PsumRing(S)

        def bank(b, n=1):
            return P[:, b:b + n, :].rearrange("p a b -> p (a b)")

        def bankbf(b, n=1):
            return P[:, b:b + n, :].rearrange("p a b -> p (a b)").bitcast(BF16)

        X = sb([128, NT, D], F32, "X")
        rX = [S.res(f"X{t}") for t in range(NT)]
        BIG = sb([128, ARENA_BYTES // 2], BF16, "BIG")
        A = Arena(S, BIG)
        ident_f = sb([128, 128], F32, "ident_f"); r_idf = S.res()
        ident_b = sb([128, 128], BF16, "ident_b"); r_idb = S.res()
        maskd = sb([128, 128], BF16, "maskd"); r_maskd = S.res()
        eps_t = sb([128, 4], F32, "eps_t"); r_eps = S.res()
        XS = sb([NS, D], F32, "XS"); r_XS = S.res()
        S.dma("sp", lambda e: e.dma_start(out=ident_f[:], in_=tdr["ident"]), w=[r_idf])
        S.dma("pool", lambda e: e.dma_start(out=ident_b[:], in_=tdr["ident"]), w=[r_idb])
        S.dma("pool", lambda e: e.dma_start(out=maskd[:], in_=tdr["maskd"]), w=[r_maskd])
        S.op("pool", lambda e: e.memset(eps_t[:, 0:1], LN_EPS), w=[r_eps])
        S.op("pool", lambda e: e.memset(eps_t[:, 1:2], RMS_EPS), w=[r_eps])
        S.op("pool", lambda e: e.memset(eps_t[:, 2:3], GN_EPS), w=[r_eps])
        eps_ln, eps_rms, eps_gn = eps_t[:, 0:1], eps_t[:, 1:2], eps_t[:, 2:3]

        def evac(q, out_ap, in_ap, r, w):
            if q == "act":
                S.op("act", lambda e: e.copy(out=out_ap, in_=in_ap), r=r, w=w)
            else:
                S.op(q, lambda e: e.tensor_copy(out=out_ap, in_=in_ap), r=r, w=w)

        def load_ln(lnt, r_lnt, g_ap, b_ap):
            S.dma("sp", lambda e: e.dma_start(out=lnt[:, 0:D], in_=g_ap.partition_broadcast(128)), w=[r_lnt])
            S.dma("sp", lambda e: e.dma_start(out=lnt[:, D:2 * D], in_=b_ap.partition_broadcast(128)), w=[r_lnt])

        def layer_norm_tile(z_ap, z_res, out_ap, out_res, npart, scr, scr_res, lnt, r_lnt):
            for c in range(2):
                S.op("dve", lambda e, c=c: e.bn_stats(out=scr[0:npart, c * 6:(c + 1) * 6], in_=z_ap[:, c * 512:(c + 1) * 512]), r=[z_res], w=[scr_res])
            mv = scr[0:npart, 12:14]
            S.op("dve", lambda e: e.bn_aggr(out=mv, in_=scr[0:npart, 0:12]), r=[scr_res], w=[scr_res])
            sd = scr[0:npart, 14:15]
            S.op("act", lambda e: e.activation(out=sd, in_=scr[0:npart, 13:14], func=AF.Sqrt, bias=eps_ln[0:npart], scale=1.0), r=[scr_res, r_eps], w=[scr_res])
            S.op("dve", lambda e: e.reciprocal(out=sd, in_=sd), r=[scr_res], w=[scr_res])
            nmr = scr[0:npart, 15:16]
            S.op("dve", lambda e: e.tensor_scalar(out=nmr, in0=scr[0:npart, 12:13], scalar1=sd, scalar2=-1.0, op0=ALU.mult, op1=ALU.mult), r=[scr_res], w=[scr_res])
            S.op("act", lambda e: e.activation(out=z_ap, in_=z_ap, func=AF.Identity, bias=nmr, scale=sd), r=[scr_res, z_res], w=[z_res])
            S.op("pool", lambda e: e.tensor_tensor(out=z_ap, in0=z_ap, in1=lnt[0:npart, 0:D], op=ALU.mult), r=[z_res, r_lnt], w=[z_res])
            S.op("pool", lambda e: e.tensor_tensor(out=out_ap, in0=z_ap, in1=lnt[0:npart, D:2 * D], op=ALU.add), r=[z_res, r_lnt], w=[out_res])

        def transpose_x(src_ap, src_res, dst_ap, dst_res, npart=128):
            for half in range(2):
                b, br = ring.get(1)
                pv = bank(b)[:, 0:4 * npart].rearrange("p (a b) -> p a b", a=4)
                for c in range(4):
                    cc = half * 4 + c
                    S.op("pe", lambda e, c=c, cc=cc, pv=pv: e.transpose(out=pv[:, c, :], in_=src_ap[:, cc * 128:(cc + 1) * 128], identity=ident_f[0:npart, 0:npart]),
                         r=[src_res, r_idf], w=br)
                evac("act" if half == 0 else "dve", dst_ap[:, half * 4:half * 4 + 4, :], pv, br, [dst_res])

        def chk(tag):
            if stop == tag:
                raise _Stop()

        try:
            for t in range(NT):
                S.dma("sp", lambda e, t=t: e.dma_start(out=X[:, t, :], in_=xp[t * 128:(t + 1) * 128, :]), w=[rX[t]])
            S.dma("sp", lambda e: e.dma_start(out=XS[:], in_=xs), w=[r_XS])

            S.set_phase('l0')
            A.at(0)
            w_in_sb, r_win = A.get(8 * 2560, BF16, "w_in"); w_in_sb = w_in_sb.rearrange("p (k n) -> p k n", k=8)
            w_o_sb, r_wo = A.get(8 * 1024, BF16, "w_o0"); w_o_sb = w_o_sb.rearrange("p (k n) -> p k n", k=8)
            poolw_sb, r_poolw = A.get(512, BF16, "poolw"); poolw_sb = poolw_sb.rearrange("p (g d) -> p g d", g=4)
            c0buf, r_c0 = A.get(1024 + 1024 + 4 + 60 + 4 + 512, F32, "c0")
            decT = c0buf[:, 0:512].rearrange("p (h l) -> p h l", h=4)
            qdec = c0buf[:, 512:1024].rearrange("p (h l) -> p h l", h=4)
            kdec = c0buf[:, 1024:1028]
            invcnt = c0buf[:, 1028:1088].rearrange("p (g t) -> p g t", g=4)
            pscale = c0buf[:, 1088:1092]
            gng = c0buf[:, 1092:1604]
            lnt0, r_lnt0 = A.get(2 * D, F32, "lnt0")
            w_in_v = w_in.rearrange("(k p) n -> p k n", p=128)
            for hh in range(2):
                S.dma("pool", lambda e, hh=hh: e.dma_start(out=w_in_sb[:, :, hh * 1280:(hh + 1) * 1280], in_=w_in_v[:, :, hh * 1280:(hh + 1) * 1280]), w=[r_win])
            S.dma("pool", lambda e: e.dma_start(out=poolw_sb, in_=pool_w), w=[r_poolw])
            S.dma("pool", lambda e: e.dma_start(out=w_o_sb, in_=w_o0.rearrange("(k p) n -> p k n", p=128)), w=[r_wo])
            S.dma("sp", lambda e: e.dma_start(out=decT, in_=tdr["decTp"]), w=[r_c0])
            S.dma("sp", lambda e: e.dma_start(out=qdec, in_=tdr["qdecp"]), w=[r_c0])
            S.dma("sp", lambda e: e.dma_start(out=kdec, in_=tdr["kdecp"]), w=[r_c0])
            S.dma("sp", lambda e: e.dma_start(out=invcnt, in_=tdr["invcnt"]), w=[r_c0])
            S.dma("sp", lambda e: e.dma_start(out=pscale, in_=pool_scale), w=[r_c0])
            S.dma("sp", lambda e: e.dma_start(out=gng, in_=gn_g.partition_broadcast(128)), w=[r_c0])
            load_ln(lnt0, r_lnt0, ln_mix_g[0], ln_mix_b[0])
            L0_BUF = A.cur

            def dbl(nelem, dt, name, shape=None, **kw):
                out = []
                for i in range(2):
                    v, r = A.get(nelem, dt, f"{name}{i}")
                    if shape:
                        v = v.rearrange(shape, **kw)
                    out.append((v, r))
                return [x[0] for x in out], [x[1] for x in out]

            xT, r_xT = dbl(1024, BF16, "xT", "p (k t) -> p k t", k=8)
            ropet, r_ropet = dbl(256, F32, "ropet")
            uext, r_uext = dbl(4 * 143 + 1, F32, "uext")
            uext = [u[:, 0:572].rearrange("p (g t) -> p g t", g=4) for u in uext]
            tA, r_tA = A.get(573, F32, "tA"); tA = tA[:, 0:572].rearrange("p (g t) -> p g t", g=4)
            tB, r_tB = A.get(573, F32, "tB"); tB = tB[:, 0:572].rearrange("p (g t) -> p g t", g=4)
            pooledT, r_pooled = dbl(512, BF16, "pooledT", "p (g t) -> p g t", g=4)
            qk_f, r_qkf = A.get(1024, F32, "qk_f"); qk_f = qk_f.rearrange("p (h t) -> p h t", h=8)
            ropeA, r_ropeA = A.get(1024, F32, "ropeA"); ropeA = ropeA.rearrange("p (h t) -> p h t", h=8)
            ropeB, r_ropeB = A.get(1024, F32, "ropeB"); ropeB = ropeB.rearrange("p (h t) -> p h t", h=8)
            qkT, r_qkT = dbl(1024, BF16, "qkT", "p (h t) -> p h t", h=8)
            qdT, r_qdT = dbl(512, BF16, "qdT", "p (h t) -> p h t", h=4)
            v_sb, r_v = dbl(512, BF16, "v")
            sgg, r_sgg = dbl(512, F32, "sgg")
            sT_bf, r_sT = dbl(512, BF16, "sT", "p (h t) -> p h t", h=4)
            kp_bf, r_kp = dbl(512, BF16, "kp", "p (h t) -> p h t", h=4)
            S_f, r_Sf = A.get(512, F32, "S_f"); S_f = S_f.rearrange("p (h t) -> p h t", h=4)
            S_b, r_Sb = A.get(512, BF16, "S_b"); S_b = S_b.rearrange("p (h t) -> p h t", h=4)
            on1, r_on1 = A.get(512, F32, "on1")
            ret_bf, r_ret = dbl(512, BF16, "ret")
            catT, r_cat = dbl(1024, BF16, "catT", "p (c t) -> p c t", c=8)
            z0_, r_z0_ = A.get(D, F32, "z"); z = [z0_, z0_]; r_z = [r_z0_, r_z0_]
            scr, r_scr = dbl(16, F32, "scr")
            gst, r_gst = A.get(32, F32, "gst")
            poolo, r_poolo = A.get(512, F32, "poolo")
            L0_END = A.cur
            print("layer0 arena bytes", L0_END)

            S.op("pool", lambda e: e.memset(uext[1][:, :, 128:143], 0.0), w=[r_uext[1]])

            def l0A(t):
                pb = t % 2
                tok = slice(t * 128, (t + 1) * 128)
                S.dma("sp", lambda e, pb=pb, tok=tok: e.dma_start(out=ropet[pb][:, 0:128], in_=tdr["c2p"][:, tok]), w=[r_ropet[pb]])
                S.dma("sp", lambda e, pb=pb, tok=tok: e.dma_start(out=ropet[pb][:, 128:256], in_=tdr["ssp"][:, tok]), w=[r_ropet[pb]])
                transpose_x(X[:, t, :], rX[t], xT[pb], r_xT[pb])
                bu, ru = ring.get(1)
                bq, rq = ring.get(1)
                bk, rk = ring.get(1)
                bv, rv = ring.get(1)
                bg, rg = ring.get(1)
                for (bb, rr, c0) in ((bu, ru, 0), (bq, rq, 512), (bk, rk, 1024)):
                    pv = bank(bb).rearrange("p (h t) -> p h t", h=4)
                    for h in range(4):
                        for k in range(8):
                            S.op("pe", lambda e, pv=pv, h=h, k=k, c0=c0, pb=pb: e.matmul(pv[:, h, :], lhsT=w_in_sb[:, k, c0 + h * 128:c0 + (h + 1) * 128], rhs=xT[pb][:, k, :], start=(k == 0), stop=(k == 7)),
                                 r=[r_win, r_xT[pb]], w=rr)
                for (bb, rr, c0) in ((bv, rv, 1536), (bg, rg, 2048)):
                    for k in range(8):
                        S.op("pe", lambda e, bb=bb, k=k, c0=c0, pb=pb: e.matmul(bank(bb), lhsT=xT[pb][:, k, :], rhs=w_in_sb[:, k, c0:c0 + 512], start=(k == 0), stop=(k == 7)),
                             r=[r_win, r_xT[pb]], w=rr)
                ue = uext[pb]
                S.op("act", lambda e, ue=ue, bu=bu: e.copy(out=ue[:, :, 15:143], in_=bank(bu).rearrange("p (h t) -> p h t", h=4)), r=ru, w=[r_uext[pb]])
                S.op("pool", lambda e, ue=ue, pb=pb: e.tensor_copy(out=ue[:, :, 0:15], in_=uext[1 - pb][:, :, 128:143]), r=[r_uext[1 - pb]], w=[r_uext[pb]])
                S.op("pool", lambda e, ue=ue: e.tensor_tensor(out=tA[:, :, 1:143], in0=ue[:, :, 1:143], in1=ue[:, :, 0:142], op=ALU.add), r=[r_uext[pb]], w=[r_tA])
                S.op("pool", lambda e: e.tensor_tensor(out=tB[:, 1:4, 3:143], in0=tA[:, 1:4, 3:143], in1=tA[:, 1:4, 1:141], op=ALU.add), r=[r_tA], w=[r_tB])
                S.op("pool", lambda e: e.tensor_tensor(out=tA[:, 2:4, 7:143], in0=tB[:, 2:4, 7:143], in1=tB[:, 2:4, 3:139], op=ALU.add), r=[r_tB], w=[r_tA])
                S.op("pool", lambda e: e.tensor_tensor(out=tB[:, 3:4, 15:143], in0=tA[:, 3:4, 15:143], in1=tA[:, 3:4, 7:135], op=ALU.add), r=[r_tA], w=[r_tB])
                fin = (tA, tB, tA, tB)
                for g in range(4):
                    S.op("dve", lambda e, g=g, ue=ue, pb=pb: e.scalar_tensor_tensor(out=pooledT[pb][:, g, :], in0=fin[g][:, g, 15:143], scalar=1.0 / POOL_W[g], in1=ue[:, g, 15:143], op0=ALU.mult, op1=ALU.subtract),
                         r=[r_tA, r_tB, r_uext[pb]], w=[r_pooled[pb]])
                if t == 0:
                    for g in range(4):
                        S.op("dve", lambda e, g=g: e.tensor_tensor(out=ropeA[:, g, 0:15], in0=fin[g][:, g, 15:30], in1=invcnt[:, g, :], op=ALU.mult), r=[r_tA, r_tB, r_c0], w=[r_ropeA])
                        S.op("dve", lambda e, g=g, ue=ue, pb=pb: e.tensor_tensor(out=pooledT[pb][:, g, 0:15], in0=ropeA[:, g, 0:15], in1=ue[:, g, 15:30], op=ALU.subtract), r=[r_ropeA, r_uext[pb]], w=[r_pooled[pb]])
                bpm, rpm = ring.get(1)
                pmv = bank(bpm).rearrange("p (h t) -> p h t", h=4)
                for g in range(4):
                    S.op("pe", lambda e, g=g, pmv=pmv, pb=pb: e.matmul(pmv[:, g, :], lhsT=poolw_sb[:, g, :], rhs=pooledT[pb][:, g, :], start=True, stop=True), r=[r_poolw, r_pooled[pb]], w=rpm)
                for g in range(4):
                    S.op("act", lambda e, g=g, pmv=pmv, pb=pb: e.activation(out=catT[pb][:, g, :], in_=pmv[:, g, :], func=AF.Identity, bias=0.0, scale=pscale[:, g:g + 1]), r=rpm + [r_c0], w=[r_cat[pb]])
                if t == NT - 1:
                    bpo, rpo = ring.get(1)
                    for g in range(4):
                        S.op("pe", lambda e, g=g, ue=ue, bpo=bpo: e.transpose(out=bank(bpo)[0:15, g * 128:(g + 1) * 128], in_=ue[:, g, 128:143], identity=ident_f[:]), r=[r_uext[pb], r_idf], w=rpo)
                    S.op("act", lambda e, bpo=bpo: e.copy(out=poolo[0:15, :], in_=bank(bpo)[0:15, :]), r=rpo, w=[r_poolo])
                    out_toks.append(S.dma("sp", lambda e: e.dma_start(out=poolp, in_=poolo[0:15, :]), r=[r_poolo]))
                S.op("act", lambda e, bq=bq: e.copy(out=qk_f[:, 0:4, :], in_=bank(bq).rearrange("p (h t) -> p h t", h=4)), r=rq, w=[r_qkf])
                S.op("act", lambda e, bk=bk: e.copy(out=qk_f[:, 4:8, :], in_=bank(bk).rearrange("p (h t) -> p h t", h=4)), r=rk, w=[r_qkf])
                c2b = ropet[pb][:, 0:128].unsqueeze(1)
                ssb = ropet[pb][:, 128:256].unsqueeze(1)
                S.op("dve", lambda e, c2b=c2b: e.tensor_tensor(out=ropeA, in0=qk_f, in1=c2b.to_broadcast([128, 8, 128]), op=ALU.mult), r=[r_qkf, r_ropet[pb]], w=[r_ropeA])
                S.op("pool", lambda e, ssb=ssb: e.tensor_tensor(out=ropeB[0:64], in0=qk_f[64:128], in1=ssb[64:128].to_broadcast([64, 8, 128]), op=ALU.mult), r=[r_qkf, r_ropet[pb]], w=[r_ropeB])
                S.op("pool", lambda e, ssb=ssb: e.tensor_tensor(out=ropeB[64:128], in0=qk_f[0:64], in1=ssb[0:64].to_broadcast([64, 8, 128]), op=ALU.mult), r=[r_qkf, r_ropet[pb]], w=[r_ropeB])
                S.op("dve", lambda e, pb=pb: e.tensor_tensor(out=qkT[pb], in0=ropeA, in1=ropeB, op=ALU.add), r=[r_ropeA, r_ropeB], w=[r_qkT[pb]])
                if t > 0:
                    S.op("pool", lambda e, pb=pb: e.tensor_tensor(out=qdT[pb], in0=qkT[pb][:, 0:4, :], in1=qdec, op=ALU.mult), r=[r_qkT[pb], r_c0], w=[r_qdT[pb]])
                S.op("act", lambda e, pb=pb, bv=bv: e.copy(out=v_sb[pb], in_=bank(bv)), r=rv, w=[r_v[pb]])
                S.op("act", lambda e, pb=pb, bg=bg: e.activation(out=sgg[pb], in_=bank(bg), func=AF.Silu), r=rg, w=[r_sgg[pb]])
                S.op("pool", lambda e, pb=pb: e.tensor_tensor(out=sgg[pb], in0=sgg[pb], in1=gng, op=ALU.mult), r=[r_sgg[pb], r_c0], w=[r_sgg[pb]])
            def l0B(t):
                pb = t % 2
                bs, rs = ring.get(1)
                sv = bank(bs).rearrange("p (h t) -> p h t", h=4)
                for h in range(4):
                    S.op("pe", lambda e, h=h, sv=sv, pb=pb: e.matmul(sv[:, h, :], lhsT=qkT[pb][:, 4 + h, :], rhs=qkT[pb][:, h, :], start=True, stop=True), r=[r_qkT[pb]], w=rs)
                S.op("dve", lambda e, sv=sv, pb=pb: e.tensor_tensor(out=sT_bf[pb], in0=sv, in1=decT, op=ALU.mult), r=rs + [r_c0], w=[r_sT[pb]])
                bo, ro = ring.get(1)
                ov = bank(bo).rearrange("p (h t) -> p h t", h=4)
                for h in range(4):
                    S.op("pe", lambda e, h=h, ov=ov, pb=pb, t=t: e.matmul(ov[:, h, :], lhsT=sT_bf[pb][:, h, :], rhs=v_sb[pb][:, h * 128:(h + 1) * 128], start=True, stop=(t == 0)), r=[r_sT[pb], r_v[pb]], w=ro)
                    if t > 0:
                        S.op("pe", lambda e, h=h, ov=ov, pb=pb: e.matmul(ov[:, h, :], lhsT=qdT[pb][:, h, :], rhs=S_b[:, h, :], start=False, stop=True), r=[r_qdT[pb], r_Sb], w=ro)
                bkt, rkt = ring.get(1)
                ktv = bankbf(bkt)[:, 0:512].rearrange("p (h t) -> p h t", h=4)
                for h in range(4):
                    S.op("pe", lambda e, h=h, ktv=ktv, pb=pb: e.transpose(out=ktv[:, h, :], in_=qkT[pb][:, 4 + h, :], identity=ident_b[:]), r=[r_qkT[pb], r_idb], w=rkt)
                for h in range(4):
                    S.op("act", lambda e, h=h, ktv=ktv, pb=pb: e.activation(out=kp_bf[pb][:, h, :], in_=ktv[:, h, :], func=AF.Identity, bias=0.0, scale=kdec[:, h:h + 1]), r=rkt + [r_c0], w=[r_kp[pb]])
                bds, rds = ring.get(1)
                dsv = bank(bds).rearrange("p (h t) -> p h t", h=4)
                for h in range(4):
                    S.op("pe", lambda e, h=h, dsv=dsv, pb=pb: e.matmul(dsv[:, h, :], lhsT=kp_bf[pb][:, h, :], rhs=v_sb[pb][:, h * 128:(h + 1) * 128], start=True, stop=True), r=[r_kp[pb], r_v[pb]], w=rds)
                if t == 0:
                    S.op("dve", lambda e, dsv=dsv: e.tensor_copy(out=S_f, in_=dsv), r=rds, w=[r_Sf])
                else:
                    for h in range(4):
                        S.op("dve", lambda e, h=h, dsv=dsv: e.scalar_tensor_tensor(out=S_f[:, h, :], in0=S_f[:, h, :], scalar=tabs["gCp"][h], in1=dsv[:, h, :], op0=ALU.mult, op1=ALU.add), r=rds + [r_Sf], w=[r_Sf])
                if t < NT - 1:
                    S.op("pool", lambda e: e.tensor_copy(out=S_b, in_=S_f), r=[r_Sf], w=[r_Sb])
                else:
                    out_toks.append(S.dma("sp", lambda e: e.dma_start(out=retp.rearrange("h k v -> k h v"), in_=S_f), r=[r_Sf]))
                for h in range(4):
                    S.op("dve", lambda e, h=h, ov=ov: e.bn_stats(out=gst[:, h * 6:(h + 1) * 6], in_=ov[:, h, :]), r=ro, w=[r_gst])
                    S.op("dve", lambda e, h=h: e.bn_aggr(out=gst[:, 24 + 2 * h:26 + 2 * h], in_=gst[:, h * 6:(h + 1) * 6]), r=[r_gst], w=[r_gst])
                mvv = gst[:, 24:32].rearrange("p (h two) -> p h two", two=2)
                S.op("act", lambda e, pb=pb: e.activation(out=scr[pb][:, 0:4], in_=mvv[:, :, 1], func=AF.Sqrt, bias=eps_gn, scale=1.0), r=[r_gst, r_eps], w=[r_scr[pb]])
                S.op("dve", lambda e, pb=pb: e.reciprocal(out=scr[pb][:, 0:4], in_=scr[pb][:, 0:4]), r=[r_scr[pb]], w=[r_scr[pb]])
                S.op("dve", lambda e, pb=pb: e.scalar_tensor_tensor(out=scr[pb][:, 4:8], in0=mvv[:, :, 0], scalar=-1.0, in1=scr[pb][:, 0:4], op0=ALU.mult, op1=ALU.mult), r=[r_gst, r_scr[pb]], w=[r_scr[pb]])
                for h in range(4):
                    S.op("act", lambda e, h=h, ov=ov, pb=pb: e.activation(out=on1[:, h * 128:(h + 1) * 128], in_=ov[:, h, :], func=AF.Identity, bias=scr[pb][:, 4 + h:5 + h], scale=scr[pb][:, h:h + 1]), r=ro + [r_scr[pb]], w=[r_on1])
                S.op("pool", lambda e, pb=pb: e.tensor_tensor(out=ret_bf[pb], in0=on1, in1=sgg[pb], op=ALU.mult), r=[r_on1, r_sgg[pb]], w=[r_ret[pb]])
                brt, rrt = ring.get(1)
                rtv = bankbf(brt)[:, 0:512].rearrange("p (h t) -> p h t", h=4)
                for h in range(4):
                    S.op("pe", lambda e, h=h, rtv=rtv, pb=pb: e.transpose(out=rtv[:, h, :], in_=ret_bf[pb][:, h * 128:(h + 1) * 128], identity=ident_b[:]), r=[r_ret[pb], r_idb], w=rrt)
                S.op("dve", lambda e, rtv=rtv, pb=pb: e.tensor_copy(out=catT[pb][:, 4:8, :], in_=rtv), r=rrt, w=[r_cat[pb]])
                by, ry = ring.get(2)
                for hf in range(2):
                    for c in range(8):
                        S.op("pe", lambda e, hf=hf, c=c, by=by, pb=pb: e.matmul(bank(by + hf), lhsT=catT[pb][:, c, :], rhs=w_o_sb[:, c, hf * 512:(hf + 1) * 512], start=(c == 0), stop=(c == 7)), r=[r_cat[pb], r_wo], w=[ry[hf]])
                S.op("dve", lambda e, by=by, pb=pb, t=t: e.scalar_tensor_tensor(out=z[pb], in0=X[:, t, :], scalar=ALPHA, in1=bank(by, 2), op0=ALU.mult, op1=ALU.add), r=[rX[t]] + ry, w=[r_z[pb]])
                layer_norm_tile(z[pb], r_z[pb], X[:, t, :], rX[t], 128, scr[pb], r_scr[pb], lnt0, r_lnt0)

            l0A(0)
            for t in range(NT):
                if t + 1 < NT:
                    l0A(t + 1)
                l0B(t)

            def ffn(layer, final):
                A.at(0)
                gated, r_gated = A.get(NJ * 1024, BF16, "gated", nres=NJ); gated = gated.rearrange("p (j n) -> p j n", j=NJ)
                x1T, r_x1T = A.get(8 * 1024, BF16, "x1T", nres=8); x1T = x1T.rearrange("p (k n) -> p k n", k=8)
                cb0_, r_cb0_ = A.get(512, F32, "cbuf"); cbuf = [cb0_, cb0_]; r_cbuf = [r_cb0_, r_cb0_]
                sb0_, r_sb0_ = A.get(512, F32, "sbf"); sbf = [sb0_, sb0_]; r_sbf = [r_sb0_, r_sb0_]
                zf0, r_zf0 = A.get(D, F32, "zf"); zf = [zf0, zf0]; r_zf = [r_zf0, r_zf0]
                scf, r_scf = dbl(16, F32, "scf")
                cwb, r_cw = A.get(NJ * 4, F32, "cw"); cw = cwb[:, 0:NJ * 3].rearrange("p (j c) -> p j c", j=NJ); cb = cwb[:, NJ * 3:NJ * 4]
                hist, r_hist = A.get(NJ * 2, F32, "hist"); hist = hist.rearrange("p (j c) -> p j c", j=NJ)
                cst, r_cst = A.get(NJ * 2, F32, "cst"); cst = cst.rearrange("p (j c) -> p j c", j=NJ)
                cso, r_cso = A.get(512, F32, "cso")
                lnt, r_lnt = A.get(2 * D, F32, "lntf")
                wup = []
                r_wup = []
                for i in range(2):
                    v, r = A.get(8 * 256, BF16, f"wup{i}")
                    wup.append(v.rearrange("p (k n) -> p k n", k=8)); r_wup.append(r)
                wdn, r_wdn = A.get(NJ * 1024, BF16, "wdn"); wdn = wdn.rearrange("p (j n) -> p j n", j=NJ)
                if with_sample:
                    xs1T, r_xs1T = A.get(128, BF16, "xs1T"); xs1T = xs1T.rearrange("p (k t) -> p k t", k=8)
                    aext, r_aext = A.get(NJ * 24, F32, "aext"); aext = aext.rearrange("p (j b r) -> p j b r", j=NJ, b=4)
                    b_s, r_bs = A.get(NJ * 16, F32, "b_s"); b_s = b_s.rearrange("p (j t) -> p j t", j=NJ)
                    scT, r_scT = A.get(NJ * 8, F32, "scT"); scT = scT.rearrange("p (j r) -> p j r", j=NJ)
                    c_s, r_cs = A.get(NJ * 16, F32, "c_s"); c_s = c_s.rearrange("p (j b l) -> p j b l", j=NJ, b=4)
                    t_s, r_ts = A.get(NJ * 16, F32, "t_s"); t_s = t_s.rearrange("p (j b l) -> p j b l", j=NJ, b=4)
                    gated_s, r_gs = A.get(NJ * 16, BF16, "gated_s"); gated_s = gated_s.rearrange("p (j t) -> p j t", j=NJ)
                print("ffn arena bytes", A.cur)
                S.dma("sp", lambda e: e.dma_start(out=cw, in_=conv_w[layer]), w=[r_cw])
                S.dma("sp", lambda e: e.dma_start(out=cb, in_=conv_b[layer]), w=[r_cw])
                load_ln(lnt, r_lnt, ln_ffn_g[layer], ln_ffn_b[layer])
                wdn_v = w_down[layer].rearrange("(j p) n -> p j n", p=128)
                wdn_loaded = [False]

                def load_wdn():
                    for jj in range(0, NJ, 2):
                        S.dma("pool", lambda e, jj=jj: e.dma_start(out=wdn[:, jj:jj + 2, :], in_=wdn_v[:, jj:jj + 2, :]), w=[r_wdn])

                def load_wup(j, slot):
                    S.dma("pool", lambda e, j=j, slot=slot: e.dma_start(out=wup[slot], in_=w_up[layer, j]), w=[r_wup[slot]])

                if with_sample:
                    transpose_x(XS[:], r_XS, xs1T, r_xs1T, npart=NS)
                    for jg in range(0, NJ, 4):
                        n = min(4, NJ - jg)
                        S.dma("sp", lambda e, jg=jg, n=n: e.dma_start(out=cso[0:8, 0:n * 128], in_=sconv[layer][:, jg * 128:(jg + n) * 128]), w=[r_cso])
                        bst, rst = ring.get(1)
                        for i_ in range(n):
                            S.op("pe", lambda e, i_=i_, bst=bst: e.transpose(out=bank(bst)[:, i_ * 8:(i_ + 1) * 8], in_=cso[0:8, i_ * 128:(i_ + 1) * 128], identity=ident_f[0:8, 0:8]), r=[r_cso, r_idf], w=rst)
                        S.op("act", lambda e, jg=jg, n=n, bst=bst: e.copy(out=scT[:, jg:jg + n, :], in_=bank(bst)[:, 0:n * 8].rearrange("p (j r) -> p j r", j=n)), r=rst, w=[r_scT])
                    S.op("pool", lambda e: e.tensor_copy(out=aext[:, :, :, 0:2], in_=scT.rearrange("p j (b r) -> p j b r", b=4)), r=[r_scT], w=[r_aext])
                cnt = 0
                for half in range(2):
                    for tt in range(8):
                        t = half * 8 + tt
                        transpose_x(X[:, t, :], rX[t], x1T[:, :, tt * 128:(tt + 1) * 128], r_x1T[tt])
                    load_wup(0, cnt % 2)
                    for j in range(NJ):
                        slot = cnt % 2
                        if j + 1 < NJ:
                            load_wup(j + 1, (cnt + 1) % 2)
                        elif half == 0:
                            pass
                        if half == 0 and j == 1:
                            load_wdn()
                        cnt += 1
                        for grp in range(2):
                            ba, ra = ring.get(1)
                            bb_, rb = ring.get(1)
                            cols = slice(grp * 512, (grp + 1) * 512)
                            for (bx, rx, c0) in ((ba, ra, 0), (bb_, rb, 128)):
                                for k in range(8):
                                    S.op("pe", lambda e, bx=bx, k=k, c0=c0, slot=slot, cols=cols: e.matmul(bank(bx), lhsT=wup[slot][:, k, c0:c0 + 128], rhs=x1T[:, k, cols], start=(k == 0), stop=(k == 7)),
                                         r=[r_wup[slot]] + r_x1T[grp * 4:grp * 4 + 4], w=rx)
                            pa = bank(ba)
                            cp = (j * 2 + grp) % 2
                            c_ = cbuf[cp]
                            s_ = sbf[cp]
                            S.op("act", lambda e, pa=pa, c_=c_, j=j: e.activation(out=c_, in_=pa, func=AF.Identity, bias=cb[:, j:j + 1], scale=cw[:, j, 2:3]), r=ra + [r_cw], w=[r_cbuf[cp]])
                            S.op("dve", lambda e, pa=pa, c_=c_, j=j: e.scalar_tensor_tensor(out=c_[:, 1:512], in0=pa[:, 0:511], scalar=cw[:, j, 1:2], in1=c_[:, 1:512], op0=ALU.mult, op1=ALU.add), r=ra + [r_cw, r_cbuf[cp]], w=[r_cbuf[cp]])
                            S.op("dve", lambda e, pa=pa, c_=c_, j=j: e.scalar_tensor_tensor(out=c_[:, 2:512], in0=pa[:, 0:510], scalar=cw[:, j, 0:1], in1=c_[:, 2:512], op0=ALU.mult, op1=ALU.add), r=ra + [r_cw, r_cbuf[cp]], w=[r_cbuf[cp]])
                            if not (half == 0 and grp == 0):
                                S.op("dve", lambda e, c_=c_, j=j: e.scalar_tensor_tensor(out=c_[:, 0:1], in0=hist[:, j, 1:2], scalar=cw[:, j, 1:2], in1=c_[:, 0:1], op0=ALU.mult, op1=ALU.add), r=[r_hist, r_cw, r_cbuf[cp]], w=[r_cbuf[cp]])
                                S.op("dve", lambda e, c_=c_, j=j: e.scalar_tensor_tensor(out=c_[:, 0:2], in0=hist[:, j, 0:2], scalar=cw[:, j, 0:1], in1=c_[:, 0:2], op0=ALU.mult, op1=ALU.add), r=[r_hist, r_cw, r_cbuf[cp]], w=[r_cbuf[cp]])
                            if half == 1 and grp == 1:
                                S.op("act", lambda e, pa=pa, j=j: e.copy(out=cst[:, j, :], in_=pa[:, 510:512]), r=ra, w=[r_cst])
                            else:
                                S.op("act", lambda e, pa=pa, j=j: e.copy(out=hist[:, j, :], in_=pa[:, 510:512]), r=ra, w=[r_hist])
                            S.op("act", lambda e, c_=c_, s_=s_: e.activation(out=s_, in_=c_, func=AF.Silu), r=[r_cbuf[cp]], w=[r_sbf[cp]])
                            S.op("dve", lambda e, s_=s_, bb_=bb_, j=j, cols=cols: e.tensor_tensor(out=gated[:, j, cols], in0=s_, in1=bank(bb_), op=ALU.mult), r=[r_sbf[cp]] + rb, w=[r_gated[j]])
                        if with_sample and half == 0:
                            bsa, rsa = ring.get(1)
                            for (c0, col) in ((0, 0), (128, 16)):
                                for k in range(8):
                                    S.op("pe", lambda e, k=k, c0=c0, col=col, slot=slot, bsa=bsa: e.matmul(bank(bsa)[:, col:col + 16], lhsT=wup[slot][:, k, c0:c0 + 128], rhs=xs1T[:, k, :], start=(k == 0), stop=(k == 7)), r=[r_wup[slot], r_xs1T], w=rsa)
                            S.op("act", lambda e, j=j, bsa=bsa: e.copy(out=aext[:, j, :, 2:6], in_=bank(bsa)[:, 0:16].rearrange("p (b l) -> p b l", b=4)), r=rsa, w=[r_aext])
                            S.op("dve", lambda e, j=j, bsa=bsa: e.tensor_copy(out=b_s[:, j, :], in_=bank(bsa)[:, 16:32]), r=rsa, w=[r_bs])
                    if with_sample and half == 0:
                        def cwb(c):
                            return cw[:, :, c:c + 1].unsqueeze(3).to_broadcast([128, NJ, 4, 4])
                        S.op("dve", lambda e: e.tensor_tensor(out=c_s, in0=aext[:, :, :, 2:6], in1=cwb(2), op=ALU.mult), r=[r_aext, r_cw], w=[r_cs])
                        S.op("pool", lambda e: e.tensor_tensor(out=t_s, in0=aext[:, :, :, 1:5], in1=cwb(1), op=ALU.mult), r=[r_aext, r_cw], w=[r_ts])
                        S.op("dve", lambda e: e.tensor_tensor(out=c_s, in0=c_s, in1=t_s, op=ALU.add), r=[r_cs, r_ts], w=[r_cs])
                        S.op("pool", lambda e: e.tensor_tensor(out=t_s, in0=aext[:, :, :, 0:4], in1=cwb(0), op=ALU.mult), r=[r_aext, r_cw], w=[r_ts])
                        S.op("dve", lambda e: e.tensor_tensor(out=c_s, in0=c_s, in1=t_s, op=ALU.add), r=[r_cs, r_ts], w=[r_cs])
                        S.op("dve", lambda e: e.tensor_tensor(out=c_s, in0=c_s, in1=cb.unsqueeze(2).unsqueeze(3).to_broadcast([128, NJ, 4, 4]), op=ALU.add), r=[r_cs, r_cw], w=[r_cs])
                        S.op("act", lambda e: e.activation(out=c_s, in_=c_s, func=AF.Silu), r=[r_cs], w=[r_cs])
                        S.op("dve", lambda e: e.tensor_tensor(out=gated_s, in0=c_s.rearrange("p j b l -> p j (b l)"), in1=b_s, op=ALU.mult), r=[r_cs, r_bs], w=[r_gs])
                        S.op("pool", lambda e: e.tensor_copy(out=scT.rearrange("p j (b r) -> p j b r", b=4), in_=aext[:, :, :, 4:6]), r=[r_aext], w=[r_scT])
                        for jg in range(0, NJ, 4):
                            n = min(4, NJ - jg)
                            bst, rst = ring.get(1)
                            for i_ in range(n):
                                S.op("pe", lambda e, i_=i_, jg=jg, bst=bst: e.transpose(out=bank(bst)[0:8, i_ * 128:(i_ + 1) * 128], in_=scT[:, jg + i_, :], identity=ident_f[:]), r=[r_scT, r_idf], w=rst)
                            S.op("act", lambda e, n=n, bst=bst: e.copy(out=cso[0:8, 0:n * 128], in_=bank(bst)[0:8, 0:n * 128]), r=rst, w=[r_cso])
                            out_toks.append(S.dma("sp", lambda e, jg=jg, n=n: e.dma_start(out=convs[layer][:, jg * 128:(jg + n) * 128], in_=cso[0:8, 0:n * 128]), r=[r_cso]))
                    for tt in range(8):
                        t = half * 8 + tt
                        pb = tt % 2
                        by, ry = ring.get(2)
                        for hf in range(2):
                            for j in range(NJ):
                                S.op("pe", lambda e, hf=hf, j=j, by=by, tt=tt: e.matmul(bank(by + hf), lhsT=gated[:, j, tt * 128:(tt + 1) * 128], rhs=wdn[:, j, hf * 512:(hf + 1) * 512], start=(j == 0), stop=(j == NJ - 1)),
                                     r=[r_gated[j], r_wdn], w=[ry[hf]])
                        S.op("dve", lambda e, by=by, pb=pb, t=t: e.scalar_tensor_tensor(out=zf[pb], in0=X[:, t, :], scalar=ALPHA, in1=bank(by, 2), op0=ALU.mult, op1=ALU.add), r=[rX[t]] + ry, w=[r_zf[pb]])
                        layer_norm_tile(zf[pb], r_zf[pb], X[:, t, :], rX[t], 128, scf[pb], r_scf[pb], lnt, r_lnt)
                        if final:
                            out_toks.append(S.dma("sp", lambda e, t=t: e.dma_start(out=yp[t * 128:(t + 1) * 128, :], in_=X[:, t, :]), r=[rX[t]]))
                if with_sample:
                    bys, rys = ring.get(2)
                    for hf in range(2):
                        for j in range(NJ):
                            S.op("pe", lambda e, hf=hf, j=j, bys=bys: e.matmul(bank(bys + hf)[0:NS, :], lhsT=gated_s[:, j, :], rhs=wdn[:, j, hf * 512:(hf + 1) * 512], start=(j == 0), stop=(j == NJ - 1)), r=[r_gs, r_wdn], w=[rys[hf]])
                    S.op("dve", lambda e, bys=bys: e.scalar_tensor_tensor(out=zf[0][0:NS, :], in0=XS[:], scalar=ALPHA, in1=bank(bys, 2)[0:NS, :], op0=ALU.mult, op1=ALU.add), r=[r_XS] + rys, w=[r_zf[0]])
                    layer_norm_tile(zf[0][0:NS, :], r_zf[0], XS[:], r_XS, NS, scf[0], r_scf[0], lnt, r_lnt)
                    if final:
                        out_toks.append(S.dma("sp", lambda e: e.dma_start(out=ys, in_=XS[:]), r=[r_XS]))
                bc, rc = ring.get(1)
                for j in range(NJ):
                    S.op("pe", lambda e, j=j, bc=bc: e.transpose(out=bank(bc)[0:2, (j % 4) * 128:(j % 4 + 1) * 128], in_=cst[:, j, :], identity=ident_f[:]), r=[r_cst, r_idf], w=rc)
                    if j % 4 == 3 or j == NJ - 1:
                        j0 = j - j % 4
                        n = j - j0 + 1
                        S.op("act", lambda e, bc=bc, n=n: e.copy(out=cso[0:2, 0:n * 128], in_=bank(bc)[0:2, 0:n * 128]), r=rc, w=[r_cso])
                        out_toks.append(S.dma("sp", lambda e, j0=j0, n=n: e.dma_start(out=convp[layer][:, j0 * 128:(j0 + n) * 128], in_=cso[0:2, 0:n * 128]), r=[r_cso]))
                        if j != NJ - 1:
                            bc, rc = ring.get(1)


            def sample_l0():
                A.at(L0_BUF)
                xsT, r_xsT = A.get(128, BF16, "xsT"); xsT = xsT.rearrange("p (k t) -> p k t", k=8)
                tb, r_tb = A.get(68, F32, "stab")
                c2s = tb[:, 0:16]; sss = tb[:, 16:32]
                decTs = tb[0:4, 32:48]
                qdecs = tb[:, 48:64].rearrange("p (h l) -> p h l", h=4)
                kdecs = tb[0:4, 64:68]
                S.dma("sp", lambda e: e.dma_start(out=c2s, in_=tdr["c2s"]), w=[r_tb])
                S.dma("sp", lambda e: e.dma_start(out=sss, in_=tdr["sss"]), w=[r_tb])
                S.dma("sp", lambda e: e.dma_start(out=decTs, in_=tdr["decTs"].rearrange("m h l -> m (h l)")), w=[r_tb])
                S.dma("sp", lambda e: e.dma_start(out=qdecs, in_=tdr["qdecs"]), w=[r_tb])
                S.dma("sp", lambda e: e.dma_start(out=kdecs, in_=tdr["kdecs"]), w=[r_tb])
                sp_sb, r_sp = A.get(512, F32, "sp_sb")
                S.dma("sp", lambda e: e.dma_start(out=sp_sb[0:60, :], in_=spool), w=[r_sp])
                S0f, r_S0f = A.get(16 * 128, F32, "S0f"); S0f = S0f.rearrange("p (n v) -> p n v", n=16)
                S0b, r_S0b = A.get(16 * 128, BF16, "S0b"); S0b = S0b.rearrange("p (n v) -> p n v", n=16)
                S.dma("sp", lambda e: e.dma_start(out=S0f, in_=sret.rearrange("n k v -> k n v")), w=[r_S0f])
                S.op("pool", lambda e: e.tensor_copy(out=S0b, in_=S0f), r=[r_S0f], w=[r_S0b])
                for bb in range(SB):
                    out_toks.append(S.dma("sp", lambda e, bb=bb: e.dma_start(out=pools[bb * 15:bb * 15 + 11, :], in_=spool[bb * 15 + 4:bb * 15 + 15, :]), semkey="d2d"))
                transpose_x(XS[:], r_XS, xsT, r_xsT, npart=NS)
                bq_, rq_ = ring.get(1)
                pqk = bank(bq_)[:, 0:192].rearrange("p (g t) -> p g t", g=12)
                for g in range(12):
                    for k in range(8):
                        S.op("pe", lambda e, g=g, k=k: e.matmul(pqk[:, g, :], lhsT=w_in_sb[:, k, g * 128:(g + 1) * 128], rhs=xsT[:, k, :], start=(k == 0), stop=(k == 7)), r=[r_win, r_xsT], w=rq_)
                bu_, ru_ = ring.get(1)
                for k in range(8):
                    S.op("pe", lambda e, k=k: e.matmul(bank(bu_)[0:NS, :], lhsT=xsT[:, k, :], rhs=w_in_sb[:, k, 0:512], start=(k == 0), stop=(k == 7)), r=[r_win, r_xsT], w=ru_)
                utok, r_utok = A.get(512, F32, "utok")
                S.op("act", lambda e: e.copy(out=utok[0:NS, :], in_=bank(bu_)[0:NS, :]), r=ru_, w=[r_utok])
                for bb in range(SB):
                    out_toks.append(S.dma("sp", lambda e, bb=bb: e.dma_start(out=pools[bb * 15 + 11:bb * 15 + 15, :], in_=utok[bb * 4:bb * 4 + 4, :]), r=[r_utok]))
                bh_, rh_ = ring.get(1)
                for g in range(4):
                    S.op("pe", lambda e, g=g: e.transpose(out=bank(bh_)[:, g * 64:g * 64 + 60], in_=sp_sb[0:60, g * 128:(g + 1) * 128], identity=ident_f[0:60, 0:60]), r=[r_sp, r_idf], w=rh_)
                ue, r_ue = A.get(16 * 19, F32, "ues"); ue4 = ue.rearrange("p (g b r) -> p g b r", g=4, b=4); ue3 = ue.rearrange("p (n r) -> p n r", n=16)
                hv = bank(bh_)[:, 0:256].rearrange("p (g c) -> p g c", g=4)[:, :, 0:60].rearrange("p g (b r) -> p g b r", b=4)
                for g in range(4):
                    S.op("act", lambda e, g=g: e.copy(out=ue4[:, g, :, 0:15], in_=hv[:, g]), r=rh_, w=[r_ue])
                    S.op("act", lambda e, g=g: e.copy(out=ue4[:, g, :, 15:19], in_=pqk[:, g, :].rearrange("p (b l) -> p b l", b=4)), r=rq_, w=[r_ue])
                tAs, r_tAs = A.get(16 * 19, F32, "tAs"); tAs = tAs.rearrange("p (n r) -> p n r", n=16)
                tBs, r_tBs = A.get(16 * 19, F32, "tBs"); tBs = tBs.rearrange("p (n r) -> p n r", n=16)
                S.op("pool", lambda e: e.tensor_tensor(out=tAs[:, :, 1:19], in0=ue3[:, :, 1:19], in1=ue3[:, :, 0:18], op=ALU.add), r=[r_ue], w=[r_tAs])
                S.op("pool", lambda e: e.tensor_tensor(out=tBs[:, 4:16, 3:19], in0=tAs[:, 4:16, 3:19], in1=tAs[:, 4:16, 1:17], op=ALU.add), r=[r_tAs], w=[r_tBs])
                S.op("pool", lambda e: e.tensor_tensor(out=tAs[:, 8:16, 7:19], in0=tBs[:, 8:16, 7:19], in1=tBs[:, 8:16, 3:15], op=ALU.add), r=[r_tBs], w=[r_tAs])
                S.op("pool", lambda e: e.tensor_tensor(out=tBs[:, 12:16, 15:19], in0=tAs[:, 12:16, 15:19], in1=tAs[:, 12:16, 7:11], op=ALU.add), r=[r_tAs], w=[r_tBs])
                pooleds, r_pooleds = A.get(64, BF16, "pooleds"); pooleds = pooleds.rearrange("p (g b l) -> p g b l", g=4, b=4)
                fins = (tAs, tBs, tAs, tBs)
                for g in range(4):
                    S.op("dve", lambda e, g=g: e.scalar_tensor_tensor(out=pooleds[:, g], in0=fins[g][:, g * 4:(g + 1) * 4, 15:19], scalar=1.0 / POOL_W[g], in1=ue3[:, g * 4:(g + 1) * 4, 15:19], op0=ALU.mult, op1=ALU.subtract), r=[r_tAs, r_tBs, r_ue], w=[r_pooleds])
                catTs, r_catTs = A.get(128, BF16, "catTs"); catTs = catTs.rearrange("p (c t) -> p c t", c=8)
                bpm_, rpm_ = ring.get(1)
                pms = bank(bpm_)[:, 0:64].rearrange("p (g t) -> p g t", g=4)
                for g in range(4):
                    S.op("pe", lambda e, g=g: e.matmul(pms[:, g, :], lhsT=poolw_sb[:, g, :], rhs=pooleds[:, g].rearrange("p b l -> p (b l)"), start=True, stop=True), r=[r_poolw, r_pooleds], w=rpm_)
                for g in range(4):
                    S.op("act", lambda e, g=g: e.activation(out=catTs[:, g, :], in_=pms[:, g, :], func=AF.Identity, bias=0.0, scale=pscale[:, g:g + 1]), r=rpm_ + [r_c0], w=[r_catTs])
                qkfs, r_qkfs = A.get(128, F32, "qkfs"); qkfs = qkfs.rearrange("p (h t) -> p h t", h=8)
                rAs, r_rAs = A.get(128, F32, "rAs"); rAs = rAs.rearrange("p (h t) -> p h t", h=8)
                rBs, r_rBs = A.get(128, F32, "rBs"); rBs = rBs.rearrange("p (h t) -> p h t", h=8)
                qkTs, r_qkTs = A.get(128, BF16, "qkTs"); qkTs = qkTs.rearrange("p (h t) -> p h t", h=8)
                qdTs, r_qdTs = A.get(64, BF16, "qdTs"); qdTs = qdTs.rearrange("p (h t) -> p h t", h=4)
                S.op("act", lambda e: e.copy(out=qkfs, in_=pqk[:, 4:12, :]), r=rq_, w=[r_qkfs])
                S.op("dve", lambda e: e.tensor_tensor(out=rAs, in0=qkfs, in1=c2s.unsqueeze(1).to_broadcast([128, 8, 16]), op=ALU.mult), r=[r_qkfs, r_tb], w=[r_rAs])
                S.op("pool", lambda e: e.tensor_tensor(out=rBs[0:64], in0=qkfs[64:128], in1=sss[64:128].unsqueeze(1).to_broadcast([64, 8, 16]), op=ALU.mult), r=[r_qkfs, r_tb], w=[r_rBs])
                S.op("pool", lambda e: e.tensor_tensor(out=rBs[64:128], in0=qkfs[0:64], in1=sss[0:64].unsqueeze(1).to_broadcast([64, 8, 16]), op=ALU.mult), r=[r_qkfs, r_tb], w=[r_rBs])
                S.op("dve", lambda e: e.tensor_tensor(out=qkTs, in0=rAs, in1=rBs, op=ALU.add), r=[r_rAs, r_rBs], w=[r_qkTs])
                S.op("pool", lambda e: e.tensor_tensor(out=qdTs.rearrange("p h (b l) -> p h b l", b=4), in0=qkTs[:, 0:4, :].rearrange("p h (b l) -> p h b l", b=4), in1=qdecs.unsqueeze(2).to_broadcast([128, 4, 4, 4]), op=ALU.mult), r=[r_qkTs, r_tb], w=[r_qdTs])
                v_s, r_vs = A.get(SB * 512, BF16, "v_s"); v_s = v_s.rearrange("p (b n) -> p b n", b=SB)
                sTs, r_sTs = A.get(64, BF16, "sTs"); sTs = sTs.rearrange("p (b h l) -> p b h l", b=4, h=4)
                kps, r_kps = A.get(512, BF16, "kps"); kps = kps.rearrange("p (h d) -> p h d", h=4)
                S.op("pool", lambda e: e.memset(v_s, 0.0), w=[r_vs])
                S.op("pool", lambda e: e.memset(sTs, 0.0), w=[r_sTs])
                S.op("pool", lambda e: e.memset(kps, 0.0), w=[r_kps])
                sggs, r_sggs = A.get(SB * 512, F32, "sggs"); sggs = sggs.rearrange("p (b n) -> p b n", b=SB)
                for bb in range(SB):
                    bv_, rv_ = ring.get(1)
                    bg_, rg_ = ring.get(1)
                    for (bx, rx, c0) in ((bv_, rv_, 1536), (bg_, rg_, 2048)):
                        for k in range(8):
                            S.op("pe", lambda e, bx=bx, k=k, c0=c0, bb=bb: e.matmul(bank(bx)[0:4, :], lhsT=xsT[:, k, bb * 4:bb * 4 + 4], rhs=w_in_sb[:, k, c0:c0 + 512], start=(k == 0), stop=(k == 7)), r=[r_win, r_xsT], w=rx)
                    S.op("act", lambda e, bv_=bv_, bb=bb: e.copy(out=v_s[0:4, bb, :], in_=bank(bv_)[0:4, :]), r=rv_, w=[r_vs])
                    S.op("act", lambda e, bg_=bg_, bb=bb: e.activation(out=sggs[0:4, bb, :], in_=bank(bg_)[0:4, :], func=AF.Silu), r=rg_, w=[r_sggs])
                    S.op("pool", lambda e, bb=bb: e.tensor_tensor(out=sggs[0:4, bb, :], in0=sggs[0:4, bb, :], in1=gng[0:4], op=ALU.mult), r=[r_sggs, r_c0], w=[r_sggs])
                bs_, rs_ = ring.get(1)
                for bb in range(SB):
                    for h in range(4):
                        c0 = (bb * 4 + h) * 4
                        S.op("pe", lambda e, bb=bb, h=h, c0=c0: e.matmul(bank(bs_)[0:4, c0:c0 + 4], lhsT=qkTs[:, 4 + h, bb * 4:bb * 4 + 4], rhs=qkTs[:, h, bb * 4:bb * 4 + 4], start=True, stop=True), r=[r_qkTs], w=rs_)
                S.op("dve", lambda e: e.tensor_tensor(out=sTs[0:4].rearrange("p b h l -> p b (h l)"), in0=bank(bs_)[0:4, 0:64].rearrange("p (b n) -> p b n", b=4), in1=decTs.unsqueeze(1).to_broadcast([4, 4, 16]), op=ALU.mult), r=rs_ + [r_tb], w=[r_sTs])
                Snew, r_Snew = dbl(512, F32, "Snew", "p (h v) -> p h v", h=4)
                gsts, r_gsts = A.get(40, F32, "gsts")
                on1s, r_on1s = A.get(512, F32, "on1s")
                rets_bf, r_retsb = A.get(512, BF16, "retsb")
                for bb in range(SB):
                    bo_, ro_ = ring.get(1)
                    ovs = bank(bo_)[0:4, :].rearrange("p (h t) -> p h t", h=4)
                    for h in range(4):
                        S.op("pe", lambda e, h=h, bb=bb, ovs=ovs: e.matmul(ovs[:, h, :], lhsT=sTs[:, bb, h, :], rhs=v_s[:, bb, h * 128:(h + 1) * 128], start=True, stop=False), r=[r_sTs, r_vs], w=ro_)
                        S.op("pe", lambda e, h=h, bb=bb, ovs=ovs: e.matmul(ovs[:, h, :], lhsT=qdTs[:, h, bb * 4:bb * 4 + 4], rhs=S0b[:, bb * 4 + h, :], start=False, stop=True), r=[r_qdTs, r_S0b], w=ro_)
                    bkt_, rkt_ = ring.get(1)
                    ktvs = bankbf(bkt_)[0:4, 0:512].rearrange("p (h t) -> p h t", h=4)
                    for h in range(4):
                        S.op("pe", lambda e, h=h, bb=bb, ktvs=ktvs: e.transpose(out=ktvs[:, h, :], in_=qkTs[:, 4 + h, bb * 4:bb * 4 + 4], identity=ident_b[:]), r=[r_qkTs, r_idb], w=rkt_)
                    for h in range(4):
                        S.op("act", lambda e, h=h, ktvs=ktvs: e.activation(out=kps[0:4, h, :], in_=ktvs[:, h, :], func=AF.Identity, bias=0.0, scale=kdecs[:, h:h + 1]), r=rkt_ + [r_tb], w=[r_kps])
                    bds_, rds_ = ring.get(1)
                    dss = bank(bds_).rearrange("p (h t) -> p h t", h=4)
                    for h in range(4):
                        S.op("pe", lambda e, h=h, bb=bb, dss=dss: e.matmul(dss[:, h, :], lhsT=kps[:, h, :], rhs=v_s[:, bb, h * 128:(h + 1) * 128], start=True, stop=True), r=[r_kps, r_vs], w=rds_)
                    sn = Snew[bb % 2]
                    for h in range(4):
                        S.op("dve", lambda e, h=h, bb=bb, dss=dss, sn=sn: e.scalar_tensor_tensor(out=sn[:, h, :], in0=S0f[:, bb * 4 + h, :], scalar=tabs["gCs"][h], in1=dss[:, h, :], op0=ALU.mult, op1=ALU.add), r=rds_ + [r_S0f], w=[r_Snew[bb % 2]])
                    out_toks.append(S.dma("sp", lambda e, bb=bb, sn=sn: e.dma_start(out=rets[bb * 4:(bb + 1) * 4].rearrange("h k v -> k h v"), in_=sn), r=[r_Snew[bb % 2]]))
                    for h in range(4):
                        S.op("dve", lambda e, h=h, ovs=ovs: e.bn_stats(out=gsts[0:4, h * 6:(h + 1) * 6], in_=ovs[:, h, :]), r=ro_, w=[r_gsts])
                        S.op("dve", lambda e, h=h: e.bn_aggr(out=gsts[0:4, 24 + 2 * h:26 + 2 * h], in_=gsts[0:4, h * 6:(h + 1) * 6]), r=[r_gsts], w=[r_gsts])
                    mvs = gsts[0:4, 24:32].rearrange("p (h two) -> p h two", two=2)
                    S.op("act", lambda e: e.activation(out=gsts[0:4, 32:36], in_=mvs[:, :, 1], func=AF.Sqrt, bias=eps_gn[0:4], scale=1.0), r=[r_gsts, r_eps], w=[r_gsts])
                    S.op("dve", lambda e: e.reciprocal(out=gsts[0:4, 32:36], in_=gsts[0:4, 32:36]), r=[r_gsts], w=[r_gsts])
                    S.op("dve", lambda e: e.scalar_tensor_tensor(out=gsts[0:4, 36:40], in0=mvs[:, :, 0], scalar=-1.0, in1=gsts[0:4, 32:36], op0=ALU.mult, op1=ALU.mult), r=[r_gsts], w=[r_gsts])
                    for h in range(4):
                        S.op("act", lambda e, h=h, ovs=ovs: e.activation(out=on1s[0:4, h * 128:(h + 1) * 128], in_=ovs[:, h, :], func=AF.Identity, bias=gsts[0:4, 36 + h:37 + h], scale=gsts[0:4, 32 + h:33 + h]), r=ro_ + [r_gsts], w=[r_on1s])
                    S.op("pool", lambda e, bb=bb: e.tensor_tensor(out=rets_bf[0:4, :], in0=on1s[0:4, :], in1=sggs[0:4, bb, :], op=ALU.mult), r=[r_on1s, r_sggs], w=[r_retsb])
                    brt_, rrt_ = ring.get(1)
                    rtvs = bankbf(brt_)[:, 0:16].rearrange("p (h t) -> p h t", h=4)
                    for h in range(4):
                        S.op("pe", lambda e, h=h, rtvs=rtvs: e.transpose(out=rtvs[:, h, :], in_=rets_bf[0:4, h * 128:(h + 1) * 128], identity=ident_b[0:4, 0:4]), r=[r_retsb, r_idb], w=rrt_)
                    S.op("dve", lambda e, rtvs=rtvs, bb=bb: e.tensor_copy(out=catTs[:, 4:8, bb * 4:bb * 4 + 4], in_=rtvs), r=rrt_, w=[r_catTs])
                by_, ry_ = ring.get(2)
                for hf in range(2):
                    for c in range(8):
                        S.op("pe", lambda e, hf=hf, c=c: e.matmul(bank(by_ + hf)[0:NS, :], lhsT=catTs[:, c, :], rhs=w_o_sb[:, c, hf * 512:(hf + 1) * 512], start=(c == 0), stop=(c == 7)), r=[r_catTs, r_wo], w=[ry_[hf]])
                zs, r_zs = A.get(D, F32, "zs")
                scs, r_scs = A.get(16, F32, "scs")
                S.op("dve", lambda e: e.scalar_tensor_tensor(out=zs[0:NS, :], in0=XS[:], scalar=ALPHA, in1=bank(by_, 2)[0:NS, :], op0=ALU.mult, op1=ALU.add), r=[r_XS] + ry_, w=[r_zs])
                layer_norm_tile(zs[0:NS, :], r_zs, XS[:], r_XS, NS, scs, r_scs, lnt0, r_lnt0)

            if with_sample:
                sample_l0()
            chk('l0')
            S.set_phase('ffn0')
            ffn(0, False)
            chk('ffn0')
            S.set_phase('l1')

            A.at(0)
            wdq_sb, r_wdq = A.get(8 * 512, BF16, "wdq"); wdq_sb = wdq_sb.rearrange("p (k n) -> p k n", k=8)
            wuq_sb, r_wuq = A.get(4 * 1536, BF16, "wuq"); wuq_sb = wuq_sb.rearrange("p (k n) -> p k n", k=4)
            wdkv_sb, r_wdkv = A.get(8 * 320, BF16, "wdkv"); wdkv_sb = wdkv_sb.rearrange("p (k n) -> p k n", k=8)
            wukT_sb, r_wukT = A.get(8 * 256, BF16, "wukT"); wukT_sb = wukT_sb.rearrange("p (h c) -> p h c", h=8)
            wuv_sb, r_wuv = A.get(2 * 1024, BF16, "wuv"); wuv_sb = wuv_sb.rearrange("p (k n) -> p k n", k=2)
            wo1_sb, r_wo1 = A.get(8 * 1024, BF16, "wo1"); wo1_sb = wo1_sb.rearrange("p (k n) -> p k n", k=8)
            c1buf, r_c1 = A.get(512 + 256, F32, "c1")
            qng = c1buf[:, 0:512]
            kvng = c1buf[:, 512:768]
            lnt1, r_lnt1 = A.get(2 * D, F32, "lnt1")
            ckvT, r_ckvT = A.get(2 * SEQ, BF16, "ckvT", nres=NT); ckvT = ckvT.rearrange("p (k n) -> p k n", k=2)
            ckvK, r_ckvK = A.get(NT * 256, BF16, "ckvK", nres=NT); ckvK = ckvK.rearrange("p (t c) -> p t c", t=NT)
            kpeT, r_kpeT = A.get(SEQ, BF16, "kpeT", nres=NT)
            S.dma("pool", lambda e: e.dma_start(out=wdq_sb, in_=w_dq.rearrange("(k p) n -> p k n", p=128)), w=[r_wdq])
            S.dma("pool", lambda e: e.dma_start(out=wdkv_sb, in_=w_dkv.rearrange("(k p) n -> p k n", p=128)), w=[r_wdkv])
            S.dma("pool", lambda e: e.dma_start(out=wuq_sb, in_=w_uq.rearrange("(k p) n -> p k n", p=128)), w=[r_wuq])
            S.dma("pool", lambda e: e.dma_start(out=wukT_sb, in_=w_ukT), w=[r_wukT])
            S.dma("pool", lambda e: e.dma_start(out=wuv_sb, in_=w_uv.rearrange("(k p) n -> p k n", p=128)), w=[r_wuv])
            S.dma("pool", lambda e: e.dma_start(out=wo1_sb, in_=w_o1.rearrange("(k p) n -> p k n", p=128)), w=[r_wo1])
            S.dma("sp", lambda e: e.dma_start(out=qng, in_=qn_g.partition_broadcast(128)), w=[r_c1])
            S.dma("sp", lambda e: e.dma_start(out=kvng, in_=kvn_g.partition_broadcast(128)), w=[r_c1])
            load_ln(lnt1, r_lnt1, ln_mix_g[1], ln_mix_b[1])
            L1_BUF = A.cur

            xT1, r_xT1 = dbl(1024, BF16, "xT1", "p (k t) -> p k t", k=8)
            rt1, r_rt1 = dbl(128, F32, "rt1")
            sc1, r_sc1 = dbl(16, F32, "sc1")
            cqn, r_cqn = dbl(512, BF16, "cqn")
            cqnT, r_cqnT = dbl(512, BF16, "cqnT", "p (k t) -> p k t", k=4)
            ckv_f, r_ckvf = dbl(256, F32, "ckv_f")
            kpe_f, r_kpef = dbl(64, F32, "kpe_f")
            kpe_t, r_kpet = A.get(128, F32, "kpe_t")
            kpe_b, r_kpeb = dbl(128, BF16, "kpe_b")
            qnT0_, r_qnT0_ = A.get(1024, BF16, "qnT"); qnT0_ = qnT0_.rearrange("p (h t) -> p h t", h=8); qnT = [qnT0_, qnT0_]; r_qnT = [r_qnT0_, r_qnT0_]
            qpe_t, r_qpet = A.get(1024, F32, "qpe_t")
            junk, r_junk = qpe_t[:, 0:512], r_qpet
            qpe_b, r_qpeb = dbl(512, BF16, "qpe_b")
            qpeT, r_qpeT = dbl(1024, BF16, "qpeT", "p (h t) -> p h t", h=8)
            qlT, r_qlT = dbl(2048, BF16, "qlT", "p (h k t) -> p h k t", h=8, k=2)
            p_sb, r_p = dbl(SEQ, BF16, "p_sb")
            pT_sb, r_pT = dbl(SEQ, BF16, "pT_sb")
            st1, r_st1 = dbl(8, F32, "st1")
            ol_b, r_olb = dbl(256, BF16, "ol_b")
            olT, r_olT = dbl(256, BF16, "olT", "p (k t) -> p k t", k=2)
            oT10_, r_oT10_ = A.get(1024, BF16, "oT1"); oT10_ = oT10_.rearrange("p (h t) -> p h t", h=8); oT1 = [oT10_, oT10_]; r_oT1 = [r_oT10_, r_oT10_]
            z10, r_z10 = A.get(D, F32, "z1"); z1 = [z10, z10]; r_z1 = [r_z10, r_z10]
            print("layer1 arena bytes", A.cur)

            for i_ in range(2):
                S.op("pool", lambda e, i_=i_: e.memset(qpeT[i_], 0.0), w=[r_qpeT[i_]])
            def l1A(t):
                pb = t % 2
                tok = slice(t * 128, (t + 1) * 128)
                S.dma("sp", lambda e, pb=pb, t=t: e.dma_start(out=rt1[pb][:, 0:64], in_=tdr["cc1"][:, t, :]), w=[r_rt1[pb]])
                S.dma("sp", lambda e, pb=pb, t=t: e.dma_start(out=rt1[pb][:, 64:128], in_=tdr["ss1"][:, t, :]), w=[r_rt1[pb]])
                transpose_x(X[:, t, :], rX[t], xT1[pb], r_xT1[pb])
                bcq, rcq = ring.get(1)
                bkv, rkv = ring.get(1)
                for k in range(8):
                    S.op("pe", lambda e, k=k, pb=pb, bcq=bcq: e.matmul(bank(bcq), lhsT=xT1[pb][:, k, :], rhs=wdq_sb[:, k, :], start=(k == 0), stop=(k == 7)), r=[r_xT1[pb], r_wdq], w=rcq)
                for k in range(8):
                    S.op("pe", lambda e, k=k, pb=pb, bkv=bkv: e.matmul(bank(bkv)[:, 0:320], lhsT=xT1[pb][:, k, :], rhs=wdkv_sb[:, k, :], start=(k == 0), stop=(k == 7)), r=[r_xT1[pb], r_wdkv], w=rkv)
                sc = sc1[pb]
                S.op("act", lambda e, bcq=bcq, sc=sc: e.activation(out=junk, in_=bank(bcq), func=AF.Square, accum_out=sc[:, 0:1]), r=rcq, w=[r_junk, r_sc1[pb]])
                S.op("act", lambda e, sc=sc: e.activation(out=sc[:, 1:2], in_=sc[:, 0:1], func=AF.Sqrt, bias=eps_rms, scale=1.0 / 512), r=[r_sc1[pb], r_eps], w=[r_sc1[pb]])
                S.op("dve", lambda e, sc=sc: e.reciprocal(out=sc[:, 1:2], in_=sc[:, 1:2]), r=[r_sc1[pb]], w=[r_sc1[pb]])
                S.op("dve", lambda e, bcq=bcq, sc=sc, pb=pb: e.scalar_tensor_tensor(out=cqn[pb], in0=bank(bcq), scalar=sc[:, 1:2], in1=qng, op0=ALU.mult, op1=ALU.mult), r=rcq + [r_sc1[pb], r_c1], w=[r_cqn[pb]])
                S.op("act", lambda e, bkv=bkv, sc=sc: e.activation(out=junk[:, 0:256], in_=bank(bkv)[:, 0:256], func=AF.Square, accum_out=sc[:, 2:3]), r=rkv, w=[r_junk, r_sc1[pb]])
                S.op("act", lambda e, sc=sc: e.activation(out=sc[:, 3:4], in_=sc[:, 2:3], func=AF.Sqrt, bias=eps_rms, scale=1.0 / 256), r=[r_sc1[pb], r_eps], w=[r_sc1[pb]])
                S.op("dve", lambda e, sc=sc: e.reciprocal(out=sc[:, 3:4], in_=sc[:, 3:4]), r=[r_sc1[pb]], w=[r_sc1[pb]])
                S.op("dve", lambda e, bkv=bkv, sc=sc, pb=pb: e.scalar_tensor_tensor(out=ckv_f[pb], in0=bank(bkv)[:, 0:256], scalar=sc[:, 3:4], in1=kvng, op0=ALU.mult, op1=ALU.mult), r=rkv + [r_sc1[pb], r_c1], w=[r_ckvf[pb]])
                out_toks.append(S.dma("sp", lambda e, pb=pb, tok=tok: e.dma_start(out=ckvp[tok, :], in_=ckv_f[pb]), r=[r_ckvf[pb]]))
                S.op("pool", lambda e, pb=pb, t=t: e.tensor_copy(out=ckvK[:, t, :], in_=ckv_f[pb]), r=[r_ckvf[pb]], w=[r_ckvK[t]])
                S.op("dve", lambda e, bkv=bkv, pb=pb: e.tensor_tensor(out=kpe_t[:, 0:64], in0=bank(bkv)[:, 256:320], in1=rt1[pb][:, 0:64], op=ALU.mult), r=rkv + [r_rt1[pb]], w=[r_kpet])
                S.op("dve", lambda e, bkv=bkv, pb=pb: e.tensor_tensor(out=kpe_t[:, 64:96], in0=bank(bkv)[:, 288:320], in1=rt1[pb][:, 64:96], op=ALU.mult), r=rkv + [r_rt1[pb]], w=[r_kpet])
                S.op("dve", lambda e, bkv=bkv, pb=pb: e.tensor_tensor(out=kpe_t[:, 96:128], in0=bank(bkv)[:, 256:288], in1=rt1[pb][:, 96:128], op=ALU.mult), r=rkv + [r_rt1[pb]], w=[r_kpet])
                S.op("pool", lambda e, pb=pb: e.tensor_tensor(out=kpe_f[pb], in0=kpe_t[:, 0:64], in1=kpe_t[:, 64:128], op=ALU.add), r=[r_kpet], w=[r_kpef[pb]])
                out_toks.append(S.dma("sp", lambda e, pb=pb, tok=tok: e.dma_start(out=kpep[tok, :], in_=kpe_f[pb]), r=[r_kpef[pb]]))
                S.op("pool", lambda e, pb=pb: e.tensor_copy(out=kpe_b[pb][:, 0:64], in_=kpe_f[pb]), r=[r_kpef[pb]], w=[r_kpeb[pb]])
                S.op("pool", lambda e, pb=pb: e.tensor_copy(out=kpe_b[pb][:, 64:128], in_=kpe_f[pb]), r=[r_kpef[pb]], w=[r_kpeb[pb]])
                btr, rtr = ring.get(2)
                trv = bankbf(btr, 2)
                for k in range(4):
                    S.op("pe", lambda e, k=k, trv=trv, pb=pb: e.transpose(out=trv[:, k * 128:(k + 1) * 128], in_=cqn[pb][:, k * 128:(k + 1) * 128], identity=ident_b[:]), r=[r_cqn[pb], r_idb], w=rtr)
                for k in range(2):
                    S.op("pe", lambda e, k=k, trv=trv, t=t: e.transpose(out=trv[:, 1024 + k * 128:1024 + (k + 1) * 128], in_=ckvK[:, t, k * 128:(k + 1) * 128], identity=ident_b[:]), r=[r_ckvK[t], r_idb], w=rtr)
                S.op("pe", lambda e, trv=trv, pb=pb: e.transpose(out=trv[:, 1280:1408], in_=kpe_b[pb], identity=ident_b[:]), r=[r_kpeb[pb], r_idb], w=rtr)
                S.op("act", lambda e, trv=trv, pb=pb: e.copy(out=cqnT[pb], in_=trv[:, 0:512].rearrange("p (k t) -> p k t", k=4)), r=rtr, w=[r_cqnT[pb]])
                S.op("dve", lambda e, trv=trv, tok=tok, t=t: e.tensor_copy(out=ckvT[:, :, tok], in_=trv[:, 1024:1280].rearrange("p (k t) -> p k t", k=2)), r=rtr, w=[r_ckvT[t]])
                S.op("dve", lambda e, trv=trv, tok=tok, t=t: e.tensor_copy(out=kpeT[:, tok], in_=trv[:, 1280:1408]), r=rtr, w=[r_kpeT[t]])
                for hh in range(2):
                    bqn, rqn = ring.get(1)
                    qv = bank(bqn).rearrange("p (h t) -> p h t", h=4)
                    for h4 in range(4):
                        h = hh * 4 + h4
                        for k in range(4):
                            S.op("pe", lambda e, qv=qv, h4=h4, h=h, k=k, pb=pb: e.matmul(qv[:, h4, :], lhsT=wuq_sb[:, k, h * 128:(h + 1) * 128], rhs=cqnT[pb][:, k, :], start=(k == 0), stop=(k == 3)), r=[r_wuq, r_cqnT[pb]], w=rqn)
                    evac("act" if hh == 0 else "dve", qnT[pb][:, hh * 4:hh * 4 + 4, :], qv, rqn, [r_qnT[pb]])
                bqp, rqp = ring.get(1)
                for k in range(4):
                    S.op("pe", lambda e, k=k, bqp=bqp, pb=pb: e.matmul(bank(bqp), lhsT=cqnT[pb][:, k, :], rhs=wuq_sb[:, k, 1024:1536], start=(k == 0), stop=(k == 3)), r=[r_wuq, r_cqnT[pb]], w=rqp)
                qpv = bank(bqp).rearrange("p (h d) -> p h d", h=8)
                qA = qpe_t[:, 0:512].rearrange("p (h d) -> p h d", h=8)
                qB = qpe_t[:, 512:1024].rearrange("p (h d) -> p h d", h=8)
                ccb = rt1[pb][:, 0:64].unsqueeze(1)
                ssb1 = rt1[pb][:, 64:96].unsqueeze(1)
                ssb2 = rt1[pb][:, 96:128].unsqueeze(1)
                S.op("dve", lambda e, qpv=qpv, ccb=ccb: e.tensor_tensor(out=qA, in0=qpv, in1=ccb.to_broadcast([128, 8, 64]), op=ALU.mult), r=rqp + [r_rt1[pb]], w=[r_qpet])
                S.op("dve", lambda e, qpv=qpv, ssb1=ssb1: e.tensor_tensor(out=qB[:, :, 0:32], in0=qpv[:, :, 32:64], in1=ssb1.to_broadcast([128, 8, 32]), op=ALU.mult), r=rqp + [r_rt1[pb]], w=[r_qpet])
                S.op("dve", lambda e, qpv=qpv, ssb2=ssb2: e.tensor_tensor(out=qB[:, :, 32:64], in0=qpv[:, :, 0:32], in1=ssb2.to_broadcast([128, 8, 32]), op=ALU.mult), r=rqp + [r_rt1[pb]], w=[r_qpet])
                S.op("pool", lambda e, pb=pb: e.tensor_tensor(out=qpe_b[pb], in0=qpe_t[:, 0:512], in1=qpe_t[:, 512:1024], op=ALU.add), r=[r_qpet], w=[r_qpeb[pb]])
                btq, rtq = ring.get(1)
                tqv = bankbf(btq)[:, 0:512].rearrange("p (a t) -> p a t", a=4)
                for a in range(4):
                    S.op("pe", lambda e, a=a, tqv=tqv, pb=pb: e.transpose(out=tqv[:, a, :], in_=qpe_b[pb][:, a * 128:(a + 1) * 128], identity=ident_b[:]), r=[r_qpeb[pb], r_idb], w=rtq)
                qz = qpeT[pb].rearrange("p (a two) t -> p a two t", two=2)
                S.op("act", lambda e, tqv=tqv, qz=qz: e.copy(out=qz[0:64, :, 0, :], in_=tqv[0:64]), r=rtq, w=[r_qpeT[pb]])
                S.op("dve", lambda e, tqv=tqv, qz=qz: e.tensor_copy(out=qz[64:128, :, 1, :], in_=tqv[64:128]), r=rtq, w=[r_qpeT[pb]])
                for hh in range(4):
                    bql, rql = ring.get(1)
                    qlv = bank(bql).rearrange("p (h k t) -> p h k t", h=2, k=2)
                    for h2 in range(2):
                        h = hh * 2 + h2
                        for k in range(2):
                            S.op("pe", lambda e, qlv=qlv, h2=h2, h=h, k=k, pb=pb: e.matmul(qlv[:, h2, k, :], lhsT=wukT_sb[:, h, k * 128:(k + 1) * 128], rhs=qnT[pb][:, h, :], start=True, stop=True), r=[r_wukT, r_qnT[pb]], w=rql)
                    evac("act" if hh % 2 == 0 else "dve", qlT[pb][:, hh * 2:hh * 2 + 2, :, :], qlv, rql, [r_qlT[pb]])

            l1ctx = {}
            def l1P1a(t, h):
                pb = t % 2
                hp = h % 2
                nk = (t + 1) * 128
                nb = (nk + 511) // 512
                nbanks = 1 if nb == 1 else (2 if nb == 2 else 4)
                bs, rs = ring.get(nbanks)
                sfull = bank(bs, nbanks)
                for gi in range(nb):
                    k0 = gi * 512
                    kn = min(512, nk - k0)
                    ks = slice(k0, k0 + kn)
                    kres = r_ckvT[gi * 4:gi * 4 + (kn + 127) // 128]
                    kpres = r_kpeT[gi * 4:gi * 4 + (kn + 127) // 128]
                    last_diag = (gi == nb - 1)
                    parts = [(ks, False)] if not last_diag else ([(slice(k0, nk - 128), False)] if nk - 128 > k0 else []) + [(slice(nk - 128, nk), True)]
                    for (ks, dg) in parts:
                        S.op("pe", lambda e, sfull=sfull, ks=ks, h=h, pb=pb: e.matmul(sfull[:, ks], lhsT=qlT[pb][:, h, 0, :], rhs=ckvT[:, 0, ks], start=True, stop=False), r=[r_qlT[pb]] + kres, w=[rs[gi]])
                        S.op("pe", lambda e, sfull=sfull, ks=ks, h=h, pb=pb: e.matmul(sfull[:, ks], lhsT=qlT[pb][:, h, 1, :], rhs=ckvT[:, 1, ks], start=False, stop=False), r=[r_qlT[pb]] + kres, w=[rs[gi]])
                        S.op("pe", lambda e, sfull=sfull, ks=ks, h=h, pb=pb, dg=dg: e.matmul(sfull[:, ks], lhsT=qpeT[pb][:, h, :], rhs=kpeT[:, ks], start=False, stop=(not dg)), r=[r_qpeT[pb]] + kpres, w=[rs[gi]])
                        if dg:
                            S.op("pe", lambda e, sfull=sfull, ks=ks: e.matmul(sfull[:, ks], lhsT=ident_b[:], rhs=maskd[:], start=False, stop=True), r=[r_idb, r_maskd], w=[rs[gi]])
                stt = st1[hp]
                chs = [(c0_, min(nk, c0_ + 1024)) for c0_ in range(0, nk, 1024)]
                for ci, (a_, b_) in enumerate(chs):
                    S.op("dve", lambda e, sfull=sfull, a_=a_, b_=b_, ci=ci, stt=stt: e.reduce_max(out=stt[:, 4 + ci:5 + ci], in_=sfull[:, a_:b_], axis=AX.X), r=rs[0:nb], w=[r_st1[hp]])
                if len(chs) == 2:
                    S.op("dve", lambda e, stt=stt: e.tensor_tensor(out=stt[:, 4:5], in0=stt[:, 4:5], in1=stt[:, 5:6], op=ALU.max), r=[r_st1[hp]], w=[r_st1[hp]])
                S.op("dve", lambda e, stt=stt: e.tensor_scalar(out=stt[:, 1:2], in0=stt[:, 4:5], scalar1=-MLA_SCALE, scalar2=None, op0=ALU.mult), r=[r_st1[hp]], w=[r_st1[hp]])
                l1ctx[(t, h)] = (sfull, rs, nb, chs)

            def l1P1b(t, h):
                pb = t % 2
                hp = h % 2
                nk = (t + 1) * 128
                stt = st1[hp]
                sfull, rs, nb, chs = l1ctx[(t, h)]
                for ci, (a_, b_) in enumerate(chs):
                    S.op("act", lambda e, sfull=sfull, a_=a_, b_=b_, ci=ci, stt=stt, hp=hp: e.activation(out=p_sb[hp][:, a_:b_], in_=sfull[:, a_:b_], func=AF.Exp, bias=stt[:, 1:2], scale=MLA_SCALE, accum_out=stt[:, 6 + ci:7 + ci]), r=rs[0:nb] + [r_st1[hp]], w=[r_p[hp], r_st1[hp]])

            def l1P2a(t, h):
                pb = t % 2
                hp = h % 2
                nk = (t + 1) * 128
                stt = st1[hp]
                nkt = t + 1
                for g0 in range(0, nkt, 4):
                    bpt, rpt = ring.get(1)
                    ptv = bankbf(bpt)
                    n_ = min(4, nkt - g0)
                    for i_ in range(n_):
                        S.op("pe", lambda e, i_=i_, g0=g0, ptv=ptv, hp=hp: e.transpose(out=ptv[:, i_ * 128:(i_ + 1) * 128], in_=p_sb[hp][:, (g0 + i_) * 128:(g0 + i_ + 1) * 128], identity=ident_b[:]), r=[r_p[hp], r_idb], w=rpt)
                    evac("act" if (g0 // 4) % 2 == 0 else "dve", pT_sb[hp][:, g0 * 128:(g0 + n_) * 128], ptv[:, 0:n_ * 128], rpt, [r_pT[hp]])

            def l1P2b(t, h):
                pb = t % 2
                hp = h % 2
                nk = (t + 1) * 128
                stt = st1[hp]
                nkt = t + 1
                bol, rol = ring.get(1)
                for kt in range(nkt):
                    S.op("pe", lambda e, kt=kt, bol=bol, hp=hp: e.matmul(bank(bol)[:, 0:256], lhsT=pT_sb[hp][:, kt * 128:(kt + 1) * 128], rhs=ckvK[:, kt, :], start=(kt == 0), stop=(kt == nkt - 1)), r=[r_pT[hp], r_ckvK[kt]], w=rol)
                chs = [(c0_, min(nk, c0_ + 1024)) for c0_ in range(0, nk, 1024)]
                if len(chs) == 2:
                    S.op("dve", lambda e, stt=stt: e.tensor_tensor(out=stt[:, 6:7], in0=stt[:, 6:7], in1=stt[:, 7:8], op=ALU.add), r=[r_st1[hp]], w=[r_st1[hp]])
                S.op("dve", lambda e, stt=stt: e.reciprocal(out=stt[:, 3:4], in_=stt[:, 6:7]), r=[r_st1[hp]], w=[r_st1[hp]])
                S.op("dve", lambda e, bol=bol, stt=stt, hp=hp: e.tensor_scalar(out=ol_b[hp], in0=bank(bol)[:, 0:256], scalar1=stt[:, 3:4], scalar2=None, op0=ALU.mult), r=rol + [r_st1[hp]], w=[r_olb[hp]])
                bot, rot = ring.get(1)
                otv = bankbf(bot)[:, 0:256].rearrange("p (k t) -> p k t", k=2)
                for k in range(2):
                    S.op("pe", lambda e, k=k, otv=otv, hp=hp: e.transpose(out=otv[:, k, :], in_=ol_b[hp][:, k * 128:(k + 1) * 128], identity=ident_b[:]), r=[r_olb[hp], r_idb], w=rot)
                S.op("dve", lambda e, otv=otv, hp=hp: e.tensor_copy(out=olT[hp], in_=otv), r=rot, w=[r_olT[hp]])
                boh, roh = ring.get(1)
                for k in range(2):
                    S.op("pe", lambda e, k=k, boh=boh, h=h, hp=hp: e.matmul(bank(boh)[:, 0:128], lhsT=wuv_sb[:, k, h * 128:(h + 1) * 128], rhs=olT[hp][:, k, :], start=(k == 0), stop=(k == 1)), r=[r_wuv, r_olT[hp]], w=roh)
                evac("act" if h % 2 == 0 else "dve", oT1[pb][:, h, :], bank(boh)[:, 0:128], roh, [r_oT1[pb]])


            def l1out(t):
                pb = t % 2
                by, ry = ring.get(2)
                for hf in range(2):
                    for c in range(8):
                        S.op("pe", lambda e, hf=hf, c=c, by=by, pb=pb: e.matmul(bank(by + hf), lhsT=oT1[pb][:, c, :], rhs=wo1_sb[:, c, hf * 512:(hf + 1) * 512], start=(c == 0), stop=(c == 7)), r=[r_oT1[pb], r_wo1], w=[ry[hf]])
                S.op("dve", lambda e, by=by, pb=pb, t=t: e.scalar_tensor_tensor(out=z1[pb], in0=X[:, t, :], scalar=ALPHA, in1=bank(by, 2), op0=ALU.mult, op1=ALU.add), r=[rX[t]] + ry, w=[r_z1[pb]])
                layer_norm_tile(z1[pb], r_z1[pb], X[:, t, :], rX[t], 128, sc1[pb], r_sc1[pb], lnt1, r_lnt1)


            l1A(0)
            for t in range(NT):
                l1P1a(t, 0)
                l1P1b(t, 0)
                for h in range(1, 8):
                    l1P2a(t, h - 1)
                    l1P1a(t, h)
                    l1P1b(t, h)
                    l1P2b(t, h - 1)
                    if h == 4 and t + 1 < NT:
                        l1A(t + 1)
                l1P2a(t, 7)
                l1P2b(t, 7)
                l1out(t)
                chk(f'l1t{t}')

            def sample_l1():
                A.at(L1_BUF)
                NSTEP = NPAGES * 128 // 1024
                tb1, r_tb1 = A.get(128, F32, "tb1")
                cc1s = tb1[0:NS, 0:64]; ss1s = tb1[0:NS, 64:128]
                S.dma("sp", lambda e: e.dma_start(out=cc1s, in_=tdr["cc1s"]), w=[r_tb1])
                S.dma("sp", lambda e: e.dma_start(out=ss1s, in_=tdr["ss1s"]), w=[r_tb1])
                msk, r_msk = A.get(SB * NS, BF16, "msk")
                S.dma("pool", lambda e: e.dma_start(out=msk[0:32, :], in_=tdr["masksb"]), w=[r_msk])
                mskv = msk.rearrange("p (b t) -> p b t", b=SB)
                pt_sb, r_pt = A.get(SB, I32, "pt_sb")
                S.dma("sp", lambda e: e.dma_start(out=pt_sb, in_=ptab), w=[r_pt])
                xsT1, r_xsT1 = A.get(128, BF16, "xsT1"); xsT1 = xsT1.rearrange("p (k t) -> p k t", k=8)
                transpose_x(XS[:], r_XS, xsT1, r_xsT1, npart=NS)
                bcq, rcq = ring.get(1)
                bkv, rkv = ring.get(1)
                for k in range(8):
                    S.op("pe", lambda e, k=k: e.matmul(bank(bcq)[0:NS, :], lhsT=xsT1[:, k, :], rhs=wdq_sb[:, k, :], start=(k == 0), stop=(k == 7)), r=[r_xsT1, r_wdq], w=rcq)
                for k in range(8):
                    S.op("pe", lambda e, k=k: e.matmul(bank(bkv)[0:NS, 0:320], lhsT=xsT1[:, k, :], rhs=wdkv_sb[:, k, :], start=(k == 0), stop=(k == 7)), r=[r_xsT1, r_wdkv], w=rkv)
                jk, r_jk = A.get(512, F32, "jk")
                scq, r_scq = A.get(16, F32, "scq")
                cqns, r_cqns = A.get(512, BF16, "cqns")
                ckvsf, r_ckvsf = A.get(256, F32, "ckvsf")
                ckvsb, r_ckvsb = A.get(256, BF16, "ckvsb")
                kpet, r_kpets = A.get(128, F32, "kpets")
                kpesf, r_kpesf = A.get(64, F32, "kpesf")
                kpesb, r_kpesb = A.get(128, BF16, "kpesb")
                sc = scq[0:NS]
                S.op("act", lambda e: e.activation(out=jk[0:NS, :], in_=bank(bcq)[0:NS, :], func=AF.Square, accum_out=sc[:, 0:1]), r=rcq, w=[r_jk, r_scq])
                S.op("act", lambda e: e.activation(out=sc[:, 1:2], in_=sc[:, 0:1], func=AF.Sqrt, bias=eps_rms[0:NS], scale=1.0 / 512), r=[r_scq, r_eps], w=[r_scq])
                S.op("dve", lambda e: e.reciprocal(out=sc[:, 1:2], in_=sc[:, 1:2]), r=[r_scq], w=[r_scq])
                S.op("dve", lambda e: e.scalar_tensor_tensor(out=cqns[0:NS, :], in0=bank(bcq)[0:NS, :], scalar=sc[:, 1:2], in1=qng[0:NS], op0=ALU.mult, op1=ALU.mult), r=rcq + [r_scq, r_c1], w=[r_cqns])
                S.op("act", lambda e: e.activation(out=jk[0:NS, 0:256], in_=bank(bkv)[0:NS, 0:256], func=AF.Square, accum_out=sc[:, 2:3]), r=rkv, w=[r_jk, r_scq])
                S.op("act", lambda e: e.activation(out=sc[:, 3:4], in_=sc[:, 2:3], func=AF.Sqrt, bias=eps_rms[0:NS], scale=1.0 / 256), r=[r_scq, r_eps], w=[r_scq])
                S.op("dve", lambda e: e.reciprocal(out=sc[:, 3:4], in_=sc[:, 3:4]), r=[r_scq], w=[r_scq])
                S.op("dve", lambda e: e.scalar_tensor_tensor(out=ckvsf[0:NS, :], in0=bank(bkv)[0:NS, 0:256], scalar=sc[:, 3:4], in1=kvng[0:NS], op0=ALU.mult, op1=ALU.mult), r=rkv + [r_scq, r_c1], w=[r_ckvsf])
                out_toks.append(S.dma("sp", lambda e: e.dma_start(out=ckvs, in_=ckvsf[0:NS, :]), r=[r_ckvsf]))
                S.op("pool", lambda e: e.tensor_copy(out=ckvsb[0:NS, :], in_=ckvsf[0:NS, :]), r=[r_ckvsf], w=[r_ckvsb])
                S.op("dve", lambda e: e.tensor_tensor(out=kpet[0:NS, 0:64], in0=bank(bkv)[0:NS, 256:320], in1=cc1s, op=ALU.mult), r=rkv + [r_tb1], w=[r_kpets])
                S.op("dve", lambda e: e.tensor_tensor(out=kpet[0:NS, 64:96], in0=bank(bkv)[0:NS, 288:320], in1=ss1s[:, 0:32], op=ALU.mult), r=rkv + [r_tb1], w=[r_kpets])
                S.op("dve", lambda e: e.tensor_tensor(out=kpet[0:NS, 96:128], in0=bank(bkv)[0:NS, 256:288], in1=ss1s[:, 32:64], op=ALU.mult), r=rkv + [r_tb1], w=[r_kpets])
                S.op("pool", lambda e: e.tensor_tensor(out=kpesf[0:NS, :], in0=kpet[0:NS, 0:64], in1=kpet[0:NS, 64:128], op=ALU.add), r=[r_kpets], w=[r_kpesf])
                out_toks.append(S.dma("sp", lambda e: e.dma_start(out=kpes, in_=kpesf[0:NS, :]), r=[r_kpesf]))
                S.op("pool", lambda e: e.tensor_copy(out=kpesb[0:NS, 0:64], in_=kpesf[0:NS, :]), r=[r_kpesf], w=[r_kpesb])
                S.op("pool", lambda e: e.tensor_copy(out=kpesb[0:NS, 64:128], in_=kpesf[0:NS, :]), r=[r_kpesf], w=[r_kpesb])
                btr, rtr = ring.get(1)
                trv = bankbf(btr)
                for k in range(4):
                    S.op("pe", lambda e, k=k: e.transpose(out=trv[:, k * 16:(k + 1) * 16], in_=cqns[0:NS, k * 128:(k + 1) * 128], identity=ident_b[0:NS, 0:NS]), r=[r_cqns, r_idb], w=rtr)
                for k in range(2):
                    S.op("pe", lambda e, k=k: e.transpose(out=trv[:, 64 + k * 16:64 + (k + 1) * 16], in_=ckvsb[0:NS, k * 128:(k + 1) * 128], identity=ident_b[0:NS, 0:NS]), r=[r_ckvsb, r_idb], w=rtr)
                S.op("pe", lambda e: e.transpose(out=trv[:, 96:112], in_=kpesb[0:NS, :], identity=ident_b[0:NS, 0:NS]), r=[r_kpesb, r_idb], w=rtr)
                trs, r_trs = A.get(112, BF16, "trs")
                S.op("act", lambda e: e.copy(out=trs, in_=trv[:, 0:112]), r=rtr, w=[r_trs])
                cqnTs = trs[:, 0:64].rearrange("p (k t) -> p k t", k=4)
                ckvTs = trs[:, 64:96].rearrange("p (k t) -> p k t", k=2)
                kpeTs = trs[:, 96:112]
                bqn, rqn = ring.get(1)
                qnv = bank(bqn)[:, 0:128].rearrange("p (h t) -> p h t", h=8)
                for h in range(8):
                    for k in range(4):
                        S.op("pe", lambda e, h=h, k=k: e.matmul(qnv[:, h, :], lhsT=wuq_sb[:, k, h * 128:(h + 1) * 128], rhs=cqnTs[:, k, :], start=(k == 0), stop=(k == 3)), r=[r_wuq, r_trs], w=rqn)
                qnTs, r_qnTs = A.get(128, BF16, "qnTs"); qnTs = qnTs.rearrange("p (h t) -> p h t", h=8)
                S.op("act", lambda e: e.copy(out=qnTs, in_=qnv), r=rqn, w=[r_qnTs])
                bqp, rqp = ring.get(1)
                for k in range(4):
                    S.op("pe", lambda e, k=k: e.matmul(bank(bqp)[0:NS, :], lhsT=cqnTs[:, k, :], rhs=wuq_sb[:, k, 1024:1536], start=(k == 0), stop=(k == 3)), r=[r_wuq, r_trs], w=rqp)
                qpv = bank(bqp)[0:NS, :].rearrange("p (h d) -> p h d", h=8)
                qpt, r_qpt = A.get(1024, F32, "qpt")
                qA = qpt[0:NS, 0:512].rearrange("p (h d) -> p h d", h=8)
                qB = qpt[0:NS, 512:1024].rearrange("p (h d) -> p h d", h=8)
                S.op("dve", lambda e: e.tensor_tensor(out=qA, in0=qpv, in1=cc1s.unsqueeze(1).to_broadcast([NS, 8, 64]), op=ALU.mult), r=rqp + [r_tb1], w=[r_qpt])
                S.op("dve", lambda e: e.tensor_tensor(out=qB[:, :, 0:32], in0=qpv[:, :, 32:64], in1=ss1s[:, 0:32].unsqueeze(1).to_broadcast([NS, 8, 32]), op=ALU.mult), r=rqp + [r_tb1], w=[r_qpt])
                S.op("dve", lambda e: e.tensor_tensor(out=qB[:, :, 32:64], in0=qpv[:, :, 0:32], in1=ss1s[:, 32:64].unsqueeze(1).to_broadcast([NS, 8, 32]), op=ALU.mult), r=rqp + [r_tb1], w=[r_qpt])
                qpsb, r_qpsb = A.get(512, BF16, "qpsb")
                S.op("pool", lambda e: e.tensor_tensor(out=qpsb[0:NS, :], in0=qpt[0:NS, 0:512], in1=qpt[0:NS, 512:1024], op=ALU.add), r=[r_qpt], w=[r_qpsb])
                btq, rtq = ring.get(1)
                tqv = bankbf(btq)[:, 0:64].rearrange("p (a t) -> p a t", a=4)
                for a in range(4):
                    S.op("pe", lambda e, a=a: e.transpose(out=tqv[:, a, :], in_=qpsb[0:NS, a * 128:(a + 1) * 128], identity=ident_b[0:NS, 0:NS]), r=[r_qpsb, r_idb], w=rtq)
                qpTs, r_qpTs = A.get(SB * 32, BF16, "qpTs")
                S.op("pool", lambda e: e.memset(qpTs, 0.0), w=[r_qpTs])
                qpT5 = qpTs.rearrange("p (b a two l) -> p b a two l", b=4, a=4, two=2)
                tq4 = tqv.rearrange("p a (b l) -> p a b l", b=4)
                for bb in range(SB):
                    S.op("act", lambda e, bb=bb: e.copy(out=qpT5[0:64, bb, :, 0, :], in_=tq4[0:64, :, bb, :]), r=rtq, w=[r_qpTs])
                    S.op("dve", lambda e, bb=bb: e.tensor_copy(out=qpT5[64:128, bb, :, 1, :], in_=tq4[64:128, :, bb, :]), r=rtq, w=[r_qpTs])
                qpTb = qpTs.rearrange("p (b n) -> p b n", b=4)
                bql, rql = ring.get(1)
                qlv = bank(bql)[:, 0:256].rearrange("p (h k t) -> p h k t", h=8, k=2)
                for h in range(8):
                    for k in range(2):
                        S.op("pe", lambda e, h=h, k=k: e.matmul(qlv[:, h, k, :], lhsT=wukT_sb[:, h, k * 128:(k + 1) * 128], rhs=qnTs[:, h, :], start=True, stop=True), r=[r_wukT, r_qnTs], w=rql)
                qlTs, r_qlTs = A.get(2 * SB * 32, BF16, "qlTs")
                ql5 = qlTs.rearrange("p (k b h l) -> p k b h l", k=2, b=4, h=8)
                qlv5 = qlv.rearrange("p h k (b l) -> p h k b l", b=4)
                for k in range(2):
                    for bb in range(SB):
                        S.op("act" if (k + bb) % 2 == 0 else "dve",
                             (lambda e, k=k, bb=bb: e.copy(out=ql5[:, k, bb], in_=qlv5[:, :, k, bb, :])) if (k + bb) % 2 == 0 else
                             (lambda e, k=k, bb=bb: e.tensor_copy(out=ql5[:, k, bb], in_=qlv5[:, :, k, bb, :])), r=rql, w=[r_qlTs])
                qlTb = qlTs.rearrange("p (k b n) -> p k b n", k=2, b=4)
                gk = []; r_gk = []; gp = []; r_gp = []
                for i_ in range(3):
                    v_, r_ = A.get(2048, BF16, f"gk{i_}"); gk.append(v_.rearrange("p (s c) -> p s c", s=8)); r_gk.append(r_)
                    v_, r_ = A.get(512, BF16, f"gp{i_}"); gp.append(v_.rearrange("p (s c) -> p s c", s=8)); r_gp.append(r_)
                gp2, r_gp2 = dbl(1024, BF16, "gp2", "p (s c) -> p s c", s=8)
                ckTc, r_ckTc = dbl(2048, BF16, "ckTc", "p (k n) -> p k n", k=2)
                kpTc, r_kpTc = dbl(1024, BF16, "kpTc")
                p_s, r_ps = dbl(1024, BF16, "p_s")
                pTc, r_pTc = dbl(256, BF16, "pTc", "p (i t) -> p i t", i=8)
                Oacc, r_Oacc = dbl(256, F32, "Oacc")
                ms, r_ms = dbl(8, F32, "ms")
                olb, r_olb = A.get(256, BF16, "olb")
                olTs, r_olTs = A.get(64, BF16, "olTs"); olTs = olTs.rearrange("p (k n) -> p k n", k=2)
                oTs, r_oTs = A.get(128, BF16, "oTs"); oTs = oTs.rearrange("p (h t) -> p h t", h=8)
                print("sample l1 arena bytes", A.cur)
                sctx = {}
                def sI(bb):
                    mb_ = ms[bb % 2][0:32]
                    oa = Oacc[bb % 2][0:32]
                    r_m = r_ms[bb % 2]
                    r_o = r_Oacc[bb % 2]
                    bsn, rsn = ring.get(1)
                    sn = bank(bsn)[0:32, 0:NS]
                    S.op("pe", lambda e, bb=bb, sn=sn: e.matmul(sn, lhsT=qlTb[:, 0, bb, :], rhs=ckvTs[:, 0, :], start=True, stop=False), r=[r_qlTs, r_trs], w=rsn)
                    S.op("pe", lambda e, bb=bb, sn=sn: e.matmul(sn, lhsT=qlTb[:, 1, bb, :], rhs=ckvTs[:, 1, :], start=False, stop=False), r=[r_qlTs, r_trs], w=rsn)
                    S.op("pe", lambda e, bb=bb, sn=sn: e.matmul(sn, lhsT=qpTb[:, bb, :], rhs=kpeTs, start=False, stop=False), r=[r_qpTs, r_trs], w=rsn)
                    S.op("pe", lambda e, bb=bb, sn=sn: e.matmul(sn, lhsT=ident_b[0:32, 0:32], rhs=mskv[0:32, bb, :], start=False, stop=True), r=[r_idb, r_msk], w=rsn)
                    S.op("dve", lambda e, sn=sn, mb_=mb_: e.reduce_max(out=mb_[:, 0:1], in_=sn, axis=AX.X), r=rsn, w=[r_m])
                    S.op("dve", lambda e, mb_=mb_: e.tensor_scalar(out=mb_[:, 5:6], in0=mb_[:, 0:1], scalar1=-MLA_SCALE, scalar2=None, op0=ALU.mult), r=[r_m], w=[r_m])
                    psl = p_s[0][0:32]
                    S.op("act", lambda e, sn=sn, mb_=mb_, psl=psl: e.activation(out=psl[:, 0:NS], in_=sn, func=AF.Exp, bias=mb_[:, 5:6], scale=MLA_SCALE, accum_out=mb_[:, 1:2]), r=rsn + [r_m], w=[r_ps[0], r_m])
                    bpt, rpt = ring.get(1)
                    S.op("pe", lambda e, psl=psl, bpt=bpt: e.transpose(out=bankbf(bpt)[0:NS, 0:32], in_=psl[:, 0:NS], identity=ident_b[0:32, 0:32]), r=[r_ps[0], r_idb], w=rpt)
                    S.op("act", lambda e, bpt=bpt: e.copy(out=pTc[0][0:NS, 0, :], in_=bankbf(bpt)[0:NS, 0:32]), r=rpt, w=[r_pTc[0]])
                    bo_, ro_ = ring.get(1)
                    S.op("pe", lambda e, bo_=bo_: e.matmul(bank(bo_)[0:32, 0:256], lhsT=pTc[0][0:NS, 0, :], rhs=ckvsb[0:NS, :], start=True, stop=True), r=[r_pTc[0], r_ckvsb], w=ro_)
                    S.op("dve", lambda e, bo_=bo_, oa=oa: e.tensor_copy(out=oa, in_=bank(bo_)[0:32, 0:256]), r=ro_, w=[r_o])

                def sG(bb, stp, g3):
                    S.dma("pool", lambda e, g3=g3, bb=bb, stp=stp: e.indirect_dma_start(out=gk[g3].rearrange("p s c -> p (s c)"), out_offset=None, in_=cckv[:, :], element_offset=stp * 2048,
                                                                                         in_offset=bass.IndirectOffsetOnAxis(ap=pt_sb[:, bb:bb + 1], axis=0)), r=[r_pt], w=[r_gk[g3]])
                    S.dma("pool", lambda e, g3=g3, bb=bb, stp=stp: e.indirect_dma_start(out=gp[g3].rearrange("p s c -> p (s c)"), out_offset=None, in_=ckpe[:, :], element_offset=stp * 512,
                                                                                         in_offset=bass.IndirectOffsetOnAxis(ap=pt_sb[:, bb:bb + 1], axis=0)), r=[r_pt], w=[r_gp[g3]])

                def sD(bb, stp, gs, g3):
                    S.op("act", lambda e, gs=gs, g3=g3: e.copy(out=gp2[gs][:, :, 0:64], in_=gp[g3]), r=[r_gp[g3]], w=[r_gp2[gs]])
                    S.op("dve", lambda e, gs=gs, g3=g3: e.tensor_copy(out=gp2[gs][:, :, 64:128], in_=gp[g3]), r=[r_gp[g3]], w=[r_gp2[gs]])

                def sT(bb, stp, gs, g3):
                    for hf in range(2):
                        sl = hf * 4
                        bA, rA = ring.get(1)
                        bB, rB = ring.get(1)
                        bC, rC = ring.get(1)
                        for i_ in range(4):
                            S.op("pe", lambda e, i_=i_, sl=sl, g3=g3, bA=bA: e.transpose(out=bankbf(bA)[:, i_ * 128:(i_ + 1) * 128], in_=gk[g3][:, sl + i_, 0:128], identity=ident_b[:]), r=[r_gk[g3], r_idb], w=rA)
                        for i_ in range(4):
                            S.op("pe", lambda e, i_=i_, sl=sl, g3=g3, bB=bB: e.transpose(out=bankbf(bB)[:, i_ * 128:(i_ + 1) * 128], in_=gk[g3][:, sl + i_, 128:256], identity=ident_b[:]), r=[r_gk[g3], r_idb], w=rB)
                        for i_ in range(4):
                            S.op("pe", lambda e, i_=i_, sl=sl, gs=gs, bC=bC: e.transpose(out=bankbf(bC)[:, i_ * 128:(i_ + 1) * 128], in_=gp2[gs][:, sl + i_, :], identity=ident_b[:]), r=[r_gp2[gs], r_idb], w=rC)
                        cs_ = slice(hf * 512, (hf + 1) * 512)
                        S.op("act", lambda e, gs=gs, bA=bA, cs_=cs_: e.copy(out=ckTc[gs][:, 0, cs_], in_=bankbf(bA)[:, 0:512]), r=rA, w=[r_ckTc[gs]])
                        S.op("dve", lambda e, gs=gs, bB=bB, cs_=cs_: e.tensor_copy(out=ckTc[gs][:, 1, cs_], in_=bankbf(bB)[:, 0:512]), r=rB, w=[r_ckTc[gs]])
                        if hf == 0:
                            S.op("act", lambda e, gs=gs, bC=bC, cs_=cs_: e.copy(out=kpTc[gs][:, cs_], in_=bankbf(bC)[:, 0:512]), r=rC, w=[r_kpTc[gs]])
                        else:
                            S.op("dve", lambda e, gs=gs, bC=bC, cs_=cs_: e.tensor_copy(out=kpTc[gs][:, cs_], in_=bankbf(bC)[:, 0:512]), r=rC, w=[r_kpTc[gs]])

                def sU1(bb, stp, gs):
                    mb_ = ms[bb % 2][0:32]
                    oa = Oacc[bb % 2][0:32]
                    r_m = r_ms[bb % 2]
                    r_o = r_Oacc[bb % 2]
                    bs2, rs2 = ring.get(2)
                    sfull = bank(bs2, 2)[0:32, :]
                    for hf in range(2):
                        cs_ = slice(hf * 512, (hf + 1) * 512)
                        S.op("pe", lambda e, bb=bb, gs=gs, cs_=cs_, sfull=sfull: e.matmul(sfull[:, cs_], lhsT=qlTb[:, 0, bb, :], rhs=ckTc[gs][:, 0, cs_], start=True, stop=False), r=[r_qlTs, r_ckTc[gs]], w=[rs2[hf]])
                        S.op("pe", lambda e, bb=bb, gs=gs, cs_=cs_, sfull=sfull: e.matmul(sfull[:, cs_], lhsT=qlTb[:, 1, bb, :], rhs=ckTc[gs][:, 1, cs_], start=False, stop=False), r=[r_qlTs, r_ckTc[gs]], w=[rs2[hf]])
                        S.op("pe", lambda e, bb=bb, gs=gs, cs_=cs_, sfull=sfull: e.matmul(sfull[:, cs_], lhsT=qpTb[:, bb, :], rhs=kpTc[gs][:, cs_], start=False, stop=True), r=[r_qpTs, r_kpTc[gs]], w=[rs2[hf]])
                    S.op("dve", lambda e, sfull=sfull, mb_=mb_: e.reduce_max(out=mb_[:, 2:3], in_=sfull, axis=AX.X), r=rs2, w=[r_m])
                    S.op("dve", lambda e, mb_=mb_: e.tensor_tensor(out=mb_[:, 3:4], in0=mb_[:, 0:1], in1=mb_[:, 2:3], op=ALU.max), r=[r_m], w=[r_m])
                    S.op("dve", lambda e, mb_=mb_: e.tensor_tensor(out=mb_[:, 4:5], in0=mb_[:, 0:1], in1=mb_[:, 3:4], op=ALU.subtract), r=[r_m], w=[r_m])
                    S.op("act", lambda e, mb_=mb_: e.activation(out=mb_[:, 4:5], in_=mb_[:, 4:5], func=AF.Exp, scale=MLA_SCALE), r=[r_m], w=[r_m])
                    S.op("dve", lambda e, mb_=mb_: e.tensor_scalar(out=mb_[:, 5:6], in0=mb_[:, 3:4], scalar1=-MLA_SCALE, scalar2=None, op0=ALU.mult), r=[r_m], w=[r_m])
                    pq = gs
                    psl = p_s[pq][0:32]
                    S.op("act", lambda e, sfull=sfull, mb_=mb_, psl=psl: e.activation(out=psl, in_=sfull, func=AF.Exp, bias=mb_[:, 5:6], scale=MLA_SCALE, accum_out=mb_[:, 6:7]), r=rs2 + [r_m], w=[r_ps[pq], r_m])
                    S.op("dve", lambda e, mb_=mb_: e.scalar_tensor_tensor(out=mb_[:, 1:2], in0=mb_[:, 1:2], scalar=mb_[:, 4:5], in1=mb_[:, 6:7], op0=ALU.mult, op1=ALU.add), r=[r_m], w=[r_m])
                    S.op("dve", lambda e, mb_=mb_: e.tensor_copy(out=mb_[:, 0:1], in_=mb_[:, 3:4]), r=[r_m], w=[r_m])

                def sU2(bb, stp, gs, g3):
                    mb_ = ms[bb % 2][0:32]
                    oa = Oacc[bb % 2][0:32]
                    r_m = r_ms[bb % 2]
                    r_o = r_Oacc[bb % 2]
                    pq = gs
                    psl = p_s[pq][0:32]
                    bpt, rpt = ring.get(1)
                    ptv = bankbf(bpt)[:, 0:256].rearrange("p (i t) -> p i t", i=8)
                    for i_ in range(8):
                        S.op("pe", lambda e, i_=i_, psl=psl, ptv=ptv: e.transpose(out=ptv[:, i_, :], in_=psl[:, i_ * 128:(i_ + 1) * 128], identity=ident_b[0:32, 0:32]), r=[r_ps[pq], r_idb], w=rpt)
                    S.op("act", lambda e, ptv=ptv, pq=pq: e.copy(out=pTc[pq], in_=ptv), r=rpt, w=[r_pTc[pq]])
                    bo_, ro_ = ring.get(1)
                    for i_ in range(8):
                        S.op("pe", lambda e, i_=i_, bo_=bo_, pq=pq, g3=g3: e.matmul(bank(bo_)[0:32, 0:256], lhsT=pTc[pq][:, i_, :], rhs=gk[g3][:, i_, :], start=(i_ == 0), stop=(i_ == 7)), r=[r_pTc[pq], r_gk[g3]], w=ro_)
                    S.op("dve", lambda e, bo_=bo_, oa=oa, mb_=mb_: e.scalar_tensor_tensor(out=oa, in0=oa, scalar=mb_[:, 4:5], in1=bank(bo_)[0:32, 0:256], op0=ALU.mult, op1=ALU.add), r=ro_ + [r_o, r_m], w=[r_o])


                def sF(bb):
                    mb_ = ms[bb % 2][0:32]
                    oa = Oacc[bb % 2][0:32]
                    r_m = r_ms[bb % 2]
                    r_o = r_Oacc[bb % 2]
                    S.op("dve", lambda e, mb_=mb_: e.reciprocal(out=mb_[:, 7:8], in_=mb_[:, 1:2]), r=[r_m], w=[r_m])
                    S.op("act", lambda e, oa=oa, mb_=mb_: e.activation(out=olb[0:32, :], in_=oa, func=AF.Identity, bias=0.0, scale=mb_[:, 7:8]), r=[r_o, r_m], w=[r_olb])
                    bot, rot = ring.get(1)
                    otv = bankbf(bot)[:, 0:64].rearrange("p (k n) -> p k n", k=2)
                    for k in range(2):
                        S.op("pe", lambda e, k=k, otv=otv: e.transpose(out=otv[:, k, :], in_=olb[0:32, k * 128:(k + 1) * 128], identity=ident_b[0:32, 0:32]), r=[r_olb, r_idb], w=rot)
                    S.op("dve", lambda e, otv=otv: e.tensor_copy(out=olTs, in_=otv), r=rot, w=[r_olTs])
                    boh, roh = ring.get(1)
                    ohv = bank(boh)[:, 0:32].rearrange("p (h l) -> p h l", h=8)
                    for h in range(8):
                        for k in range(2):
                            S.op("pe", lambda e, h=h, k=k, ohv=ohv: e.matmul(ohv[:, h, :], lhsT=wuv_sb[:, k, h * 128:(h + 1) * 128], rhs=olTs[:, k, h * 4:(h + 1) * 4], start=(k == 0), stop=(k == 1)), r=[r_wuv, r_olTs], w=roh)
                    S.op("act", lambda e, ohv=ohv, bb=bb: e.copy(out=oTs[:, :, bb * 4:bb * 4 + 4], in_=ohv), r=roh, w=[r_oTs])

                NTOT = SB * NSTEP
                sG(0, 0, 0)
                sG(0, 1, 1)
                sD(0, 0, 0, 0)
                sT(0, 0, 0, 0)
                for idx in range(NTOT):
                    bb, stp = divmod(idx, NSTEP)
                    if idx + 2 < NTOT:
                        b3, s3 = divmod(idx + 2, NSTEP)
                        sG(b3, s3, (idx + 2) % 3)
                    if stp == 0:
                        sI(bb)
                    if idx + 1 < NTOT:
                        sD(0, 0, (idx + 1) % 2, (idx + 1) % 3)
                    sU1(bb, stp, idx % 2)
                    if idx + 1 < NTOT:
                        b2, s2 = divmod(idx + 1, NSTEP)
                        sT(b2, s2, (idx + 1) % 2, (idx + 1) % 3)
                    sU2(bb, stp, idx % 2, idx % 3)
                    if stp == NSTEP - 1:
                        sF(bb)
                by_, ry_ = ring.get(2)
                for hf in range(2):
                    for c in range(8):
                        S.op("pe", lambda e, hf=hf, c=c: e.matmul(bank(by_ + hf)[0:NS, :], lhsT=oTs[:, c, :], rhs=wo1_sb[:, c, hf * 512:(hf + 1) * 512], start=(c == 0), stop=(c == 7)), r=[r_oTs, r_wo1], w=[ry_[hf]])
                zs1, r_zs1 = A.get(D, F32, "zs1")
                scs1, r_scs1 = A.get(16, F32, "scs1")
                S.op("dve", lambda e: e.scalar_tensor_tensor(out=zs1[0:NS, :], in0=XS[:], scalar=ALPHA, in1=bank(by_, 2)[0:NS, :], op0=ALU.mult, op1=ALU.add), r=[r_XS] + ry_, w=[r_zs1])
                layer_norm_tile(zs1[0:NS, :], r_zs1, XS[:], r_XS, NS, scs1, r_scs1, lnt1, r_lnt1)

            S.set_phase('s1')
            if with_sample:
                sample_l1()
            chk('l1')
            S.set_phase('ffn1')
            ffn(1, True)


        except _Stop:
            print('stopped at', stop)
        S.finish(out_toks)
        S.emit()
        print("ops", S.stats())
    return nc


_CACHE = {}


def _prep_shared(inp, tabs):
    f = lambda a: np.ascontiguousarray(np.asarray(a, dtype=np.float32))
    sh = {}
    sh["w_in"] = f(inp["w_in_even"][0])
    sh["pool_w"] = f(np.transpose(inp["pool_w"][0], (1, 0, 2)))
    sh["pool_scale"] = f(inp["pool_scale"][0].reshape(4, 128).T)
    sh["gn_g"] = f(inp["ret_gn_g"][0])
    sh["w_o0"] = f(inp["w_o_even"][0])
    sh["w_dq"] = f(inp["w_dq"][0])
    sh["qn_g"] = f(inp["q_norm_g"][0])
    wuq = np.asarray(inp["w_uq"][0]).reshape(512, 8, 192)
    sh["w_uq"] = f(np.concatenate([wuq[:, :, :128].reshape(512, 1024), wuq[:, :, 128:].reshape(512, 512)], axis=1))
    sh["w_dkv"] = f(inp["w_dkv"][0])
    sh["kvn_g"] = f(inp["kv_norm_g"][0])
    sh["w_ukT"] = f(np.transpose(inp["w_uk"][0], (2, 1, 0)))
    sh["w_uv"] = f(np.asarray(inp["w_uv"][0]).reshape(256, 1024))
    sh["w_o1"] = f(inp["w_o_mla"][0])
    wup = np.asarray(inp["w_up"])
    a = wup[:, :, :DFF].reshape(DEPTH, 8, 128, NJ, 128)
    b = wup[:, :, DFF:].reshape(DEPTH, 8, 128, NJ, 128)
    ab = np.concatenate([a, b], axis=4)
    sh["w_up"] = f(np.transpose(ab, (0, 3, 2, 1, 4)))
    sh["w_down"] = f(inp["w_down"])
    sh["conv_w"] = f(np.transpose(np.asarray(inp["conv_w"]).reshape(DEPTH, 3, NJ, 128), (0, 3, 2, 1)))
    sh["conv_b"] = f(np.transpose(np.asarray(inp["conv_b"]).reshape(DEPTH, NJ, 128), (0, 2, 1)))
    for k in ("ln_mix_g", "ln_mix_b", "ln_ffn_g", "ln_ffn_b"):
        sh[k] = f(inp[k])
    sh["cckv"] = np.asarray(inp["cache_ckv"], dtype=np.float32).reshape(5120, 128 * 256)
    sh["ckpe"] = np.asarray(inp["cache_kpe"], dtype=np.float32).reshape(5120, 128 * 64)
    for k, v in tabs.items():
        if isinstance(v, np.ndarray):
            sh["t_" + k] = v
    return sh


def kernel(**inp):
    tabs = host_tables()
    if "nc" not in _CACHE:
        _CACHE["nc"] = build_program(tabs)
    nc = _CACHE["nc"]
    sh = _prep_shared(inp, tabs)
    f = lambda a: np.ascontiguousarray(np.asarray(a, dtype=np.float32))
    in_maps = []
    for c in range(8):
        m = dict(sh)
        bs = slice(SB * c, SB * (c + 1))
        m["xp"] = f(inp["x_prompt"][c])
        m["xs"] = f(np.asarray(inp["x_sample"][bs]).reshape(NS, D))
        m["spool"] = f(np.asarray(inp["state_pool"][0, bs]).reshape(SB * 15, 512))
        m["sret"] = f(np.asarray(inp["state_ret"][0, bs]).reshape(SB * RH, 128, 128))
        m["sconv"] = f(np.asarray(inp["state_conv"][:, bs]).reshape(DEPTH, SB * 2, DFF))
        m["ptab"] = np.ascontiguousarray(np.asarray(inp["page_table"][bs]).T.astype(np.int32))
        in_maps.append(m)
    res = run_bass_kernel_spmd(nc, in_maps, core_ids=list(range(8)))
    R = res.results
    cat = lambda k: np.stack([np.asarray(R[c][k]) for c in range(8)])
    y_p = cat("yp")
    y_s = cat("ys").reshape(32, SL, D)
    pool_p = cat("poolp")[None]
    pool_s = cat("pools").reshape(1, 32, 15, 512)
    ret_p = cat("retp")[None]
    ret_s = cat("rets").reshape(1, 32, RH, 128, 128)
    ckv_p = cat("ckvp")[None]
    ckv_s = cat("ckvs").reshape(1, 32, SL, 256)
    kpe_p = cat("kpep")[None]
    kpe_s = cat("kpes").reshape(1, 32, SL, 64)
    conv_p = np.transpose(cat("convp"), (1, 0, 2, 3))
    conv_s = np.transpose(cat("convs").reshape(8, DEPTH, SB, 2, DFF), (1, 0, 2, 3, 4)).reshape(DEPTH, 32, 2, DFF)
    outs = (y_p, y_s, pool_p, pool_s, ret_p, ret_s, ckv_p, ckv_s, kpe_p, kpe_s, conv_p, conv_s)
    return tuple(np.ascontiguousarray(o.astype(np.float32)) for o in outs)
```

```python
import contextlib
import math
import numpy as np
import concourse.bass as bass
import concourse.mybir as mybir
from concourse.bass_utils import run_bass_kernel_spmd

F32 = mybir.dt.float32
BF16 = mybir.dt.bfloat16
I32 = mybir.dt.int32
AF = mybir.ActivationFunctionType
ALU = mybir.AluOpType
AX = mybir.AxisListType

ALL_Q = ("pe", "act", "dve", "pool", "sp")
SAME_ENGINE_SYNC = True

D = 1024
SEQ = 2048
NT = SEQ // 128
DEPTH = 2
NS = 16
SB = 4
SL = 4
PAST = 16384
NPAGES = 128
POOL_W = (2, 4, 8, 16)
RH = 4
DFF = 2816
NJ = DFF // 128
ALPHA = (2.0 * DEPTH) ** 0.25
LN_EPS = 1e-5
RMS_EPS = 1e-6
GN_EPS = 1e-6
MLA_SCALE = (128 + 64) ** -0.5
KSCALE = 128 ** -0.5
NEG = -30000.0


class Res:
    __slots__ = ("name", "w", "r")

    def __init__(self, name):
        self.name = name
        self.w = None
        self.r = []


class Op:
    __slots__ = ("q", "fn", "waits", "signal", "dma", "idx", "deps", "sdep", "succ", "nrem", "ready", "cost", "lat", "fin", "orig", "done", "phase")

    def __init__(self, q, fn, dma=None):
        self.q = q
        self.fn = fn
        self.waits = []
        self.signal = False
        self.dma = dma
        self.idx = None
        self.deps = []
        self.sdep = None
        self.succ = []
        self.nrem = 0
        self.ready = 0.0
        self.cost = 0.1
        self.lat = 0.0
        self.fin = 0.0
        self.orig = 0
        self.done = False
        self.phase = 0


class _Rec:
    def __init__(self):
        self.info = ("", None)

    def __getattr__(self, name):
        def f(*a, **k):
            out = k.get("out", a[0] if a else None)
            object.__setattr__(self, "info", (name, out))
            return self
        return f


RESCHED_PHASES = {"l0", "ffn0", "ffn1"}
KEEP_Q = ()


class Sched:
    def __init__(self, nc, stack):
        self.nc = nc
        self.stack = stack
        self.all = []
        self.ops = {q: [] for q in ALL_Q}
        self.dma_sems = {}
        self.dma_last = {}
        self.eng_sems = {}
        self.nres = 0
        self.phase = 0
        self.phase_names = ['init']

    def res(self, name=None):
        self.nres += 1
        return Res(name or f"r{self.nres}")

    def set_phase(self, name):
        self.phase_names.append(name)
        self.phase = len(self.phase_names) - 1

    def _cost(self, o):
        try:
            rec = _Rec()
            o.fn(rec)
            name, out = rec.info
            shp = list(out.shape)
            n = 1
            for d in shp[1:]:
                n *= d
        except Exception:
            name, n = "", 256
        q = o.q
        if o.dma is not None:
            o.cost = 1.2 if q == "pool" else 0.15
            o.lat = 2.5 + n * shp[0] * 3.0 / 250e3 if name else 3.0
        elif q == "pe":
            o.cost = 0.05 + n / 1900.0
        elif q == "act":
            o.cost = 0.22 + n / 1000.0
        elif q == "dve":
            o.cost = 0.2 + n / 900.0
        else:
            o.cost = 0.35 + n / 450.0

    def _deps(self, o, r, w, tok):
        seen = set()
        for res in r:
            if res.w is not None and id(res.w[-1]) not in seen:
                seen.add(id(res.w[-1])); o.deps.append(res.w)
        for res in w:
            if res.w is not None and id(res.w[-1]) not in seen:
                seen.add(id(res.w[-1])); o.deps.append(res.w)
            for t in res.r:
                if id(t[-1]) not in seen:
                    seen.add(id(t[-1])); o.deps.append(t)
        for res in r:
            res.r.append(tok)
        for res in w:
            res.w = tok
            res.r = []

    def op(self, q, fn, r=(), w=()):
        o = Op(q, fn)
        o.orig = len(self.all)
        o.phase = self.phase
        self._deps(o, r, w, ("eng", o))
        self.all.append(o)
        return o

    def dma(self, q, fn, r=(), w=(), semkey=None):
        res0 = (list(w) + list(r))[0] if (w or r) else None
        key = semkey if semkey is not None else id(res0)
        if key not in self.dma_sems:
            self.dma_sems[key] = [None, 0]
        ent = self.dma_sems[key]
        ent[1] += 16
        o = Op(q, fn, dma=(key, ent[1]))
        o.orig = len(self.all)
        o.phase = self.phase
        o.sdep = self.dma_last.get(key)
        self.dma_last[key] = o
        tok = ("dma", key, ent[1], o)
        self._deps(o, r, w, tok)
        self.all.append(o)
        return tok

    def finish(self, out_tokens):
        o = Op("sp", None)
        o.orig = len(self.all)
        o.phase = self.phase
        seen = set()
        for t in out_tokens:
            if id(t[-1]) not in seen:
                seen.add(id(t[-1])); o.deps.append(t)
        self.all.append(o)

    def _schedule(self):
        ops = self.all
        for o in ops:
            if o.fn is not None:
                self._cost(o)
            dset = set(id(t[-1]) for t in o.deps)
            preds = [(t[-1], True) for t in o.deps]
            if o.sdep is not None and id(o.sdep) not in dset:
                preds.append((o.sdep, False))
            o.nrem = len(preds)
            for p, kind in preds:
                p.succ.append((o, kind))
        tfree = {q: 0.0 for q in ALL_Q}

        def place(o, st):
            self.ops[o.q].append(o)
            tfree[o.q] = st + o.cost
            o.fin = st + o.cost + o.lat
            o.done = True
            for s, kind in o.succ:
                t = o.fin if kind else st
                if t > s.ready:
                    s.ready = t
                s.nrem -= 1

        nph = len(self.phase_names)
        byph = [[] for _ in range(nph)]
        for o in ops:
            byph[o.phase].append(o)
        for ph in range(nph):
            plist = byph[ph]
            if self.phase_names[ph] not in RESCHED_PHASES:
                for o in plist:
                    place(o, max(tfree[o.q], o.ready))
                continue
            pending = set(id(o) for o in plist)
            keepq = KEEP_Q if self.phase_names[ph] in ("l1", "s1") else ()
            nxt = {q: [o for o in plist if o.q == q] for q in keepq}
            nxt_i = {q: 0 for q in keepq}
            ready = {q: [] for q in ALL_Q}
            inready = set()
            for o in plist:
                if o.nrem == 0:
                    ready[o.q].append(o); inready.add(id(o))
            nleft = len(plist)
            while nleft:
                best = None
                for q in ALL_Q:
                    lst = ready[q]
                    if not lst:
                        continue
                    tf = tfree[q]
                    cand = None
                    if q in keepq:
                        o = nxt[q][nxt_i[q]]
                        if o in lst:
                            st = o.ready if o.ready > tf else tf
                            cand = ((st, o.orig), o)
                    else:
                        for o in lst:
                            st = o.ready if o.ready > tf else tf
                            key = (st, o.orig)
                            if cand is None or key < cand[0]:
                                cand = (key, o)
                    if cand is not None and (best is None or cand[0] < best[0]):
                        best = cand
                (st, _), o = best
                ready[o.q].remove(o)
                if o.q in keepq:
                    nxt_i[o.q] += 1
                place(o, st)
                nleft -= 1
                for s, kind in o.succ:
                    if s.nrem == 0 and id(s) in pending and id(s) not in inready and not s.done:
                        ready[s.q].append(s); inready.add(id(s))

    def _add_wait(self, op, tok, waited, dwaited):
        q = op.q
        if tok[0] == "eng":
            src = tok[1]
            sq, sidx = src.q, src.idx
            if sq == q and (q == "pe" or q == "sp" or not SAME_ENGINE_SYNC):
                return
            if waited[q].get(sq, -1) >= sidx:
                return
            waited[q][sq] = sidx
            op.waits.append(("eng", sq, sidx))
            src.signal = True
        else:
            _, key, val, _src = tok
            if dwaited[q].get(key, -1) >= val:
                return
            dwaited[q][key] = val
            op.waits.append(("dma", key, val))

    def emit(self):
        nc = self.nc
        st = self.stack
        self._schedule()
        waited = {q: {} for q in ALL_Q}
        dwaited = {q: {} for q in ALL_Q}
        for q in ALL_Q:
            for i, o in enumerate(self.ops[q]):
                o.idx = i
        for q in ALL_Q:
            for o in self.ops[q]:
                best = {}
                for t in o.deps:
                    if t[0] == "eng":
                        k = ("eng", t[1].q)
                        v = t[1].idx
                    else:
                        k = ("dma", t[1])
                        v = t[2]
                    if k not in best or best[k][0] < v:
                        best[k] = (v, t)
                for v, t in best.values():
                    self._add_wait(o, t, waited, dwaited)
        for q in ALL_Q:
            self.eng_sems[q] = st.enter_context(nc.semaphore(f"es_{q}"))
        for i, key in enumerate(self.dma_sems):
            self.dma_sems[key][0] = st.enter_context(nc.semaphore(f"ds_{i}"))
        cnt = {}
        for q in ALL_Q:
            c = 0
            arr = []
            for o in self.ops[q]:
                if o.signal:
                    c += 1
                arr.append(c)
            cnt[q] = arr
        block = st.enter_context(nc.Block())

        def run(q, eng):
            for o in self.ops[q]:
                for t in o.waits:
                    if t[0] == "eng":
                        eng.wait_ge(self.eng_sems[t[1]], cnt[t[1]][t[2]])
                    else:
                        eng.wait_ge(self.dma_sems[t[1]][0], t[2])
                if o.fn is None:
                    continue
                ins = o.fn(eng)
                if o.dma is not None:
                    ins.then_inc(self.dma_sems[o.dma[0]][0], 16)
                elif o.signal:
                    ins.then_inc(self.eng_sems[q], 1)

        @block.tensor
        def _(eng):
            run("pe", eng)

        @block.scalar
        def _(eng):
            run("act", eng)

        @block.vector
        def _(eng):
            run("dve", eng)

        @block.gpsimd
        def _(eng):
            run("pool", eng)

        @block.sync
        def _(eng):
            run("sp", eng)

    def stats(self):
        return {q: (len(self.ops[q]), sum(len(o.waits) for o in self.ops[q])) for q in ALL_Q}


def _rope_tables(half, pos):
    inv = (np.float32(10000.0) ** (-(np.arange(half, dtype=np.float32)) / np.float32(half))).astype(np.float32)
    ang = (pos.astype(np.float32)[:, None] * inv[None, :]).astype(np.float32)
    return np.cos(ang).astype(np.float32), np.sin(ang).astype(np.float32)


def host_tables():
    t = {}
    pos_p = np.arange(SEQ, dtype=np.float32)
    pos_s = (PAST + np.arange(SL)).astype(np.float32)
    c, s = _rope_tables(64, pos_p)
    t["c2p"] = np.ascontiguousarray(np.concatenate([c, c], 1).T)
    t["ssp"] = np.ascontiguousarray(np.concatenate([s, -s], 1).T)
    c, s = _rope_tables(64, pos_s)
    t["c2s"] = np.ascontiguousarray(np.tile(np.concatenate([c, c], 1).T, (1, SB)))
    t["sss"] = np.ascontiguousarray(np.tile(np.concatenate([s, -s], 1).T, (1, SB)))
    c, s = _rope_tables(32, pos_p)
    cc = np.concatenate([c, c], 1).reshape(NT, 128, 64).transpose(1, 0, 2)
    ss = np.concatenate([-s, s], 1).reshape(NT, 128, 64).transpose(1, 0, 2)
    t["cc1"] = np.ascontiguousarray(cc)
    t["ss1"] = np.ascontiguousarray(ss)
    c, s = _rope_tables(32, pos_s)
    t["cc1s"] = np.ascontiguousarray(np.tile(np.concatenate([c, c], 1), (SB, 1)))
    t["ss1s"] = np.ascontiguousarray(np.tile(np.concatenate([-s, s], 1), (SB, 1)))
    lg = np.log(np.float32(1.0) - np.float32(2.0) ** (np.float32(-5.0) - np.arange(RH, dtype=np.float32))).astype(np.float32)
    for name, C in (("p", 128), ("s", SL)):
        idx = np.arange(C, dtype=np.float32)
        diff = idx[:, None] - idx[None, :]
        dec = np.where(diff >= 0, np.exp(np.maximum(diff, 0.0)[None] * lg[:, None, None]), 0.0).astype(np.float32)
        t["decT" + name] = np.ascontiguousarray((dec * np.float32(KSCALE)).transpose(2, 0, 1))
        qd = np.exp((idx + 1.0)[:, None] * lg[None, :]).astype(np.float32)
        t["qdec" + name] = np.ascontiguousarray(np.broadcast_to(qd.T[None], (128, RH, C)))
        kd = np.exp((C - 1.0 - idx)[:, None] * lg[None, :]).astype(np.float32) * np.float32(KSCALE)
        t["kdec" + name] = np.ascontiguousarray(kd)
        t["gC" + name] = [float(np.exp(np.float32(C) * lg[h])) for h in range(RH)]
    ic = np.zeros((128, 4, 15), np.float32)
    for g, w in enumerate(POOL_W):
        ic[:, g, :] = 1.0 / np.minimum(float(w), np.arange(15) + 1.0)
    t["invcnt"] = ic
    qi = np.arange(128)
    t["maskd"] = np.where(qi[None, :] <= qi[:, None], 0.0, NEG).astype(np.float32)
    t["ident"] = np.eye(128, dtype=np.float32)
    r = np.arange(32) % 4
    t["masks"] = np.where(np.arange(4)[None, :] <= r[:, None], 0.0, NEG).astype(np.float32)
    mb = np.full((32, SB, NS), NEG, np.float32)
    for bb in range(SB):
        for l2 in range(SL):
            mb[:, bb, bb * SL + l2] = np.where(l2 <= r, 0.0, NEG)
    t["masksb"] = np.ascontiguousarray(mb.reshape(32, SB * NS))
    return t


ARENA_BYTES = 142144


class PsumRing:
    def __init__(self, S):
        self.res = [S.res(f"bank{i}") for i in range(8)]
        self.nxt = 0

    def get(self, n=1):
        if n == 2 and self.nxt % 2:
            self.nxt = (self.nxt + 1) % 8
        if n == 4 and self.nxt % 4:
            self.nxt = (self.nxt + 4 - self.nxt % 4) % 8
        b = self.nxt
        self.nxt = (self.nxt + n) % 8
        return b, [self.res[b + i] for i in range(n)]


class Arena:
    def __init__(self, S, buf):
        self.S = S
        self.buf = buf
        self.hist = []
        self.cur = 0

    def at(self, off):
        assert off % 4 == 0
        self.cur = off

    def get(self, nelem, dt=BF16, name=None, nres=None):
        size = 2 if dt == BF16 else 4
        nb = (nelem * size + 3) // 4 * 4
        s, e = self.cur, self.cur + nb
        assert e <= ARENA_BYTES, (name, e)
        self.cur = e
        pend = []
        for (s2, e2, r2) in self.hist:
            if s2 < e and s < e2:
                if r2.w is not None:
                    pend.append(r2.w)
                pend.extend(r2.r)
        seen = set()
        p2 = []
        for t in pend:
            if id(t[-1]) not in seen:
                seen.add(id(t[-1])); p2.append(t)
        pend = p2
        rs = []
        for i in range(nres or 1):
            r = self.S.res(f"{name}_{i}")
            r.r = list(pend)
            self.hist.append((s, e, r))
            rs.append(r)
        v = self.buf[:, s // 2:s // 2 + nb // 2]
        if dt != BF16:
            v = v.bitcast(dt)
        return v[:, 0:nelem], (rs if nres else rs[0])


class _Stop(Exception):
    pass


def build_program(tabs, with_sample=True, stop=None):
    nc = bass.Bass("TRN2", target_bir_lowering=False)

    def din(name, shape, dt=F32):
        return nc.dram_tensor(name, list(shape), dt, kind="ExternalInput").ap()

    def dout(name, shape):
        return nc.dram_tensor(name, list(shape), F32, kind="ExternalOutput").ap()

    xp = din("xp", [SEQ, D])
    xs = din("xs", [NS, D])
    spool = din("spool", [SB * 15, 512])
    sret = din("sret", [SB * RH, 128, 128])
    sconv = din("sconv", [DEPTH, SB * 2, DFF])
    ptab = din("ptab", [128, SB], I32)
    if with_sample:
        cckv = din("cckv", [5120, 128 * 256])
        ckpe = din("ckpe", [5120, 128 * 64])
    w_in = din("w_in", [D, 2560])
    pool_w = din("pool_w", [128, 4, 128])
    pool_scale = din("pool_scale", [128, 4])
    gn_g = din("gn_g", [512])
    w_o0 = din("w_o0", [D, D])
    w_dq = din("w_dq", [D, 512])
    qn_g = din("qn_g", [512])
    w_uq = din("w_uq", [512, 1536])
    w_dkv = din("w_dkv", [D, 320])
    kvn_g = din("kvn_g", [256])
    w_ukT = din("w_ukT", [128, 8, 256])
    w_uv = din("w_uv", [256, 1024])
    w_o1 = din("w_o1", [D, D])
    w_up = din("w_up", [DEPTH, NJ, 128, 8, 256])
    w_down = din("w_down", [DEPTH, DFF, D])
    conv_w = din("conv_w", [DEPTH, 128, NJ, 3])
    conv_b = din("conv_b", [DEPTH, 128, NJ])
    ln_mix_g = din("ln_mix_g", [DEPTH, D])
    ln_mix_b = din("ln_mix_b", [DEPTH, D])
    ln_ffn_g = din("ln_ffn_g", [DEPTH, D])
    ln_ffn_b = din("ln_ffn_b", [DEPTH, D])
    tdr = {k: din("t_" + k, v.shape) for k, v in tabs.items() if isinstance(v, np.ndarray)}

    yp = dout("yp", [SEQ, D])
    ys = dout("ys", [NS, D])
    poolp = dout("poolp", [15, 512])
    pools = dout("pools", [SB * 15, 512])
    retp = dout("retp", [RH, 128, 128])
    rets = dout("rets", [SB * RH, 128, 128])
    ckvp = dout("ckvp", [SEQ, 256])
    ckvs = dout("ckvs", [NS, 256])
    kpep = dout("kpep", [SEQ, 64])
    kpes = dout("kpes", [NS, 64])
    convp = dout("convp", [DEPTH, 2, DFF])
    convs = dout("convs", [DEPTH, SB * 2, DFF])

    out_toks = []
    st = contextlib.ExitStack()
    with st:
        S = Sched(nc, st)
        _n = [0]

        def sb(shape, dt, name):
            return st.enter_context(nc.sbuf_tensor(name, list(shape), dt))

        P = st.enter_context(nc.psum_tensor("P", [128, 8, 512], F32))
        ring = PsumRing(S)

        def bank(b, n=1):
            return P[:, b:b + n, :].rearrange("p a b -> p (a b)")

        def bankbf(b, n=1):
            return P[:, b:b + n, :].rearrange("p a b -> p (a b)").bitcast(BF16)

        X = sb([128, NT, D], F32, "X")
        rX = [S.res(f"X{t}") for t in range(NT)]
        BIG = sb([128, ARENA_BYTES // 2], BF16, "BIG")
        A = Arena(S, BIG)
        ident_f = sb([128, 128], F32, "ident_f"); r_idf = S.res()
        ident_b = sb([128, 128], BF16, "ident_b"); r_idb = S.res()
        maskd = sb([128, 128], BF16, "maskd"); r_maskd = S.res()
        eps_t = sb([128, 4], F32, "eps_t"); r_eps = S.res()
        XS = sb([NS, D], F32, "XS"); r_XS = S.res()
        S.dma("sp", lambda e: e.dma_start(out=ident_f[:], in_=tdr["ident"]), w=[r_idf])
        S.dma("pool", lambda e: e.dma_start(out=ident_b[:], in_=tdr["ident"]), w=[r_idb])
        S.dma("pool", lambda e: e.dma_start(out=maskd[:], in_=tdr["maskd"]), w=[r_maskd])
        S.op("pool", lambda e: e.memset(eps_t[:, 0:1], LN_EPS), w=[r_eps])
        S.op("pool", lambda e: e.memset(eps_t[:, 1:2], RMS_EPS), w=[r_eps])
        S.op("pool", lambda e: e.memset(eps_t[:, 2:3], GN_EPS), w=[r_eps])
        eps_ln, eps_rms, eps_gn = eps_t[:, 0:1], eps_t[:, 1:2], eps_t[:, 2:3]

        def evac(q, out_ap, in_ap, r, w):
            if q == "act":
                S.op("act", lambda e: e.copy(out=out_ap, in_=in_ap), r=r, w=w)
            else:
                S.op(q, lambda e: e.tensor_copy(out=out_ap, in_=in_ap), r=r, w=w)

        def load_ln(lnt, r_lnt, g_ap, b_ap):
            S.dma("sp", lambda e: e.dma_start(out=lnt[:, 0:D], in_=g_ap.partition_broadcast(128)), w=[r_lnt])
            S.dma("sp", lambda e: e.dma_start(out=lnt[:, D:2 * D], in_=b_ap.partition_broadcast(128)), w=[r_lnt])

        def layer_norm_tile(z_ap, z_res, out_ap, out_res, npart, scr, scr_res, lnt, r_lnt):
            for c in range(2):
                S.op("dve", lambda e, c=c: e.bn_stats(out=scr[0:npart, c * 6:(c + 1) * 6], in_=z_ap[:, c * 512:(c + 1) * 512]), r=[z_res], w=[scr_res])
            mv = scr[0:npart, 12:14]
            S.op("dve", lambda e: e.bn_aggr(out=mv, in_=scr[0:npart, 0:12]), r=[scr_res], w=[scr_res])
            sd = scr[0:npart, 14:15]
            S.op("act", lambda e: e.activation(out=sd, in_=scr[0:npart, 13:14], func=AF.Sqrt, bias=eps_ln[0:npart], scale=1.0), r=[scr_res, r_eps], w=[scr_res])
            S.op("dve", lambda e: e.reciprocal(out=sd, in_=sd), r=[scr_res], w=[scr_res])
            nmr = scr[0:npart, 15:16]
            S.op("dve", lambda e: e.tensor_scalar(out=nmr, in0=scr[0:npart, 12:13], scalar1=sd, scalar2=-1.0, op0=ALU.mult, op1=ALU.mult), r=[scr_res], w=[scr_res])
            S.op("act", lambda e: e.activation(out=z_ap, in_=z_ap, func=AF.Identity, bias=nmr, scale=sd), r=[scr_res, z_res], w=[z_res])
            S.op("pool", lambda e: e.tensor_tensor(out=z_ap, in0=z_ap, in1=lnt[0:npart, 0:D], op=ALU.mult), r=[z_res, r_lnt], w=[z_res])
            S.op("pool", lambda e: e.tensor_tensor(out=out_ap, in0=z_ap, in1=lnt[0:npart, D:2 * D], op=ALU.add), r=[z_res, r_lnt], w=[out_res])

        def transpose_x(src_ap, src_res, dst_ap, dst_res, npart=128):
            for half in range(2):
                b, br = ring.get(1)
                pv = bank(b)[:, 0:4 * npart].rearrange("p (a b) -> p a b", a=4)
                for c in range(4):
                    cc = half * 4 + c
                    S.op("pe", lambda e, c=c, cc=cc, pv=pv: e.transpose(out=pv[:, c, :], in_=src_ap[:, cc * 128:(cc + 1) * 128], identity=ident_f[0:npart, 0:npart]),
                         r=[src_res, r_idf], w=br)
                evac("act" if half == 0 else "dve", dst_ap[:, half * 4:half * 4 + 4, :], pv, br, [dst_res])

        def chk(tag):
            if stop == tag:
                raise _Stop()

        try:
            for t in range(NT):
                S.dma("sp", lambda e, t=t: e.dma_start(out=X[:, t, :], in_=xp[t * 128:(t + 1) * 128, :]), w=[rX[t]])
            S.dma("sp", lambda e: e.dma_start(out=XS[:], in_=xs), w=[r_XS])

            S.set_phase('l0')
            A.at(0)
            w_in_sb, r_win = A.get(8 * 2560, BF16, "w_in"); w_in_sb = w_in_sb.rearrange("p (k n) -> p k n", k=8)
            w_o_sb, r_wo = A.get(8 * 1024, BF16, "w_o0"); w_o_sb = w_o_sb.rearrange("p (k n) -> p k n", k=8)
            poolw_sb, r_poolw = A.get(512, BF16, "poolw"); poolw_sb = poolw_sb.rearrange("p (g d) -> p g d", g=4)
            c0buf, r_c0 = A.get(1024 + 1024 + 4 + 60 + 4 + 512, F32, "c0")
            decT = c0buf[:, 0:512].rearrange("p (h l) -> p h l", h=4)
            qdec = c0buf[:, 512:1024].rearrange("p (h l) -> p h l", h=4)
            kdec = c0buf[:, 1024:1028]
            invcnt = c0buf[:, 1028:1088].rearrange("p (g t) -> p g t", g=4)
            pscale = c0buf[:, 1088:1092]
            gng = c0buf[:, 1092:1604]
            lnt0, r_lnt0 = A.get(2 * D, F32, "lnt0")
            w_in_v = w_in.rearrange("(k p) n -> p k n", p=128)
            for hh in range(2):
                S.dma("pool", lambda e, hh=hh: e.dma_start(out=w_in_sb[:, :, hh * 1280:(hh + 1) * 1280], in_=w_in_v[:, :, hh * 1280:(hh + 1) * 1280]), w=[r_win])
            S.dma("pool", lambda e: e.dma_start(out=poolw_sb, in_=pool_w), w=[r_poolw])
            S.dma("pool", lambda e: e.dma_start(out=w_o_sb, in_=w_o0.rearrange("(k p) n -> p k n", p=128)), w=[r_wo])
            S.dma("sp", lambda e: e.dma_start(out=decT, in_=tdr["decTp"]), w=[r_c0])
            S.dma("sp", lambda e: e.dma_start(out=qdec, in_=tdr["qdecp"]), w=[r_c0])
            S.dma("sp", lambda e: e.dma_start(out=kdec, in_=tdr["kdecp"]), w=[r_c0])
            S.dma("sp", lambda e: e.dma_start(out=invcnt, in_=tdr["invcnt"]), w=[r_c0])
            S.dma("sp", lambda e: e.dma_start(out=pscale, in_=pool_scale), w=[r_c0])
            S.dma("sp", lambda e: e.dma_start(out=gng, in_=gn_g.partition_broadcast(128)), w=[r_c0])
            load_ln(lnt0, r_lnt0, ln_mix_g[0], ln_mix_b[0])
            L0_BUF = A.cur

            def dbl(nelem, dt, name, shape=None, **kw):
                out = []
                for i in range(2):
                    v, r = A.get(nelem, dt, f"{name}{i}")
                    if shape:
                        v = v.rearrange(shape, **kw)
                    out.append((v, r))
                return [x[0] for x in out], [x[1] for x in out]

            xT, r_xT = dbl(1024, BF16, "xT", "p (k t) -> p k t", k=8)
            ropet, r_ropet = dbl(256, F32, "ropet")
            uext, r_uext = dbl(4 * 143 + 1, F32, "uext")
            uext = [u[:, 0:572].rearrange("p (g t) -> p g t", g=4) for u in uext]
            tA, r_tA = A.get(573, F32, "tA"); tA = tA[:, 0:572].rearrange("p (g t) -> p g t", g=4)
            tB, r_tB = A.get(573, F32, "tB"); tB = tB[:, 0:572].rearrange("p (g t) -> p g t", g=4)
            pooledT, r_pooled = dbl(512, BF16, "pooledT", "p (g t) -> p g t", g=4)
            qk_f, r_qkf = A.get(1024, F32, "qk_f"); qk_f = qk_f.rearrange("p (h t) -> p h t", h=8)
            ropeA, r_ropeA = A.get(1024, F32, "ropeA"); ropeA = ropeA.rearrange("p (h t) -> p h t", h=8)
            ropeB, r_ropeB = A.get(1024, F32, "ropeB"); ropeB = ropeB.rearrange("p (h t) -> p h t", h=8)
            qkT, r_qkT = dbl(1024, BF16, "qkT", "p (h t) -> p h t", h=8)
            qdT, r_qdT = dbl(512, BF16, "qdT", "p (h t) -> p h t", h=4)
            v_sb, r_v = dbl(512, BF16, "v")
            sgg, r_sgg = dbl(512, F32, "sgg")
            sT_bf, r_sT = dbl(512, BF16, "sT", "p (h t) -> p h t", h=4)
            kp_bf, r_kp = dbl(512, BF16, "kp", "p (h t) -> p h t", h=4)
            S_f, r_Sf = A.get(512, F32, "S_f"); S_f = S_f.rearrange("p (h t) -> p h t", h=4)
            S_b, r_Sb = A.get(512, BF16, "S_b"); S_b = S_b.rearrange("p (h t) -> p h t", h=4)
            on1, r_on1 = A.get(512, F32, "on1")
            ret_bf, r_ret = dbl(512, BF16, "ret")
            catT, r_cat = dbl(1024, BF16, "catT", "p (c t) -> p c t", c=8)
            z0_, r_z0_ = A.get(D, F32, "z"); z = [z0_, z0_]; r_z = [r_z0_, r_z0_]
            scr, r_scr = dbl(16, F32, "scr")
            gst, r_gst = A.get(32, F32, "gst")
            poolo, r_poolo = A.get(512, F32, "poolo")
            L0_END = A.cur
            print("layer0 arena bytes", L0_END)

            S.op("pool", lambda e: e.memset(uext[1][:, :, 128:143], 0.0), w=[r_uext[1]])

            def l0A(t):
                pb = t % 2
                tok = slice(t * 128, (t + 1) * 128)
                S.dma("sp", lambda e, pb=pb, tok=tok: e.dma_start(out=ropet[pb][:, 0:128], in_=tdr["c2p"][:, tok]), w=[r_ropet[pb]])
                S.dma("sp", lambda e, pb=pb, tok=tok: e.dma_start(out=ropet[pb][:, 128:256], in_=tdr["ssp"][:, tok]), w=[r_ropet[pb]])
                transpose_x(X[:, t, :], rX[t], xT[pb], r_xT[pb])
                bu, ru = ring.get(1)
                bq, rq = ring.get(1)
                bk, rk = ring.get(1)
                bv, rv = ring.get(1)
                bg, rg = ring.get(1)
                for (bb, rr, c0) in ((bu, ru, 0), (bq, rq, 512), (bk, rk, 1024)):
                    pv = bank(bb).rearrange("p (h t) -> p h t", h=4)
                    for h in range(4):
                        for k in range(8):
                            S.op("pe", lambda e, pv=pv, h=h, k=k, c0=c0, pb=pb: e.matmul(pv[:, h, :], lhsT=w_in_sb[:, k, c0 + h * 128:c0 + (h + 1) * 128], rhs=xT[pb][:, k, :], start=(k == 0), stop=(k == 7)),
                                 r=[r_win, r_xT[pb]], w=rr)
                for (bb, rr, c0) in ((bv, rv, 1536), (bg, rg, 2048)):
                    for k in range(8):
                        S.op("pe", lambda e, bb=bb, k=k, c0=c0, pb=pb: e.matmul(bank(bb), lhsT=xT[pb][:, k, :], rhs=w_in_sb[:, k, c0:c0 + 512], start=(k == 0), stop=(k == 7)),
                             r=[r_win, r_xT[pb]], w=rr)
                ue = uext[pb]
                S.op("act", lambda e, ue=ue, bu=bu: e.copy(out=ue[:, :, 15:143], in_=bank(bu).rearrange("p (h t) -> p h t", h=4)), r=ru, w=[r_uext[pb]])
                S.op("pool", lambda e, ue=ue, pb=pb: e.tensor_copy(out=ue[:, :, 0:15], in_=uext[1 - pb][:, :, 128:143]), r=[r_uext[1 - pb]], w=[r_uext[pb]])
                S.op("pool", lambda e, ue=ue: e.tensor_tensor(out=tA[:, :, 1:143], in0=ue[:, :, 1:143], in1=ue[:, :, 0:142], op=ALU.add), r=[r_uext[pb]], w=[r_tA])
                S.op("pool", lambda e: e.tensor_tensor(out=tB[:, 1:4, 3:143], in0=tA[:, 1:4, 3:143], in1=tA[:, 1:4, 1:141], op=ALU.add), r=[r_tA], w=[r_tB])
                S.op("pool", lambda e: e.tensor_tensor(out=tA[:, 2:4, 7:143], in0=tB[:, 2:4, 7:143], in1=tB[:, 2:4, 3:139], op=ALU.add), r=[r_tB], w=[r_tA])
                S.op("pool", lambda e: e.tensor_tensor(out=tB[:, 3:4, 15:143], in0=tA[:, 3:4, 15:143], in1=tA[:, 3:4, 7:135], op=ALU.add), r=[r_tA], w=[r_tB])
                fin = (tA, tB, tA, tB)
                for g in range(4):
                    S.op("dve", lambda e, g=g, ue=ue, pb=pb: e.scalar_tensor_tensor(out=pooledT[pb][:, g, :], in0=fin[g][:, g, 15:143], scalar=1.0 / POOL_W[g], in1=ue[:, g, 15:143], op0=ALU.mult, op1=ALU.subtract),
                         r=[r_tA, r_tB, r_uext[pb]], w=[r_pooled[pb]])
                if t == 0:
                    for g in range(4):
                        S.op("dve", lambda e, g=g: e.tensor_tensor(out=ropeA[:, g, 0:15], in0=fin[g][:, g, 15:30], in1=invcnt[:, g, :], op=ALU.mult), r=[r_tA, r_tB, r_c0], w=[r_ropeA])
                        S.op("dve", lambda e, g=g, ue=ue, pb=pb: e.tensor_tensor(out=pooledT[pb][:, g, 0:15], in0=ropeA[:, g, 0:15], in1=ue[:, g, 15:30], op=ALU.subtract), r=[r_ropeA, r_uext[pb]], w=[r_pooled[pb]])
                bpm, rpm = ring.get(1)
                pmv = bank(bpm).rearrange("p (h t) -> p h t", h=4)
                for g in range(4):
                    S.op("pe", lambda e, g=g, pmv=pmv, pb=pb: e.matmul(pmv[:, g, :], lhsT=poolw_sb[:, g, :], rhs=pooledT[pb][:, g, :], start=True, stop=True), r=[r_poolw, r_pooled[pb]], w=rpm)
                for g in range(4):
                    S.op("act", lambda e, g=g, pmv=pmv, pb=pb: e.activation(out=catT[pb][:, g, :], in_=pmv[:, g, :], func=AF.Identity, bias=0.0, scale=pscale[:, g:g + 1]), r=rpm + [r_c0], w=[r_cat[pb]])
                if t == NT - 1:
                    bpo, rpo = ring.get(1)
                    for g in range(4):
                        S.op("pe", lambda e, g=g, ue=ue, bpo=bpo: e.transpose(out=bank(bpo)[0:15, g * 128:(g + 1) * 128], in_=ue[:, g, 128:143], identity=ident_f[:]), r=[r_uext[pb], r_idf], w=rpo)
                    S.op("act", lambda e, bpo=bpo: e.copy(out=poolo[0:15, :], in_=bank(bpo)[0:15, :]), r=rpo, w=[r_poolo])
                    out_toks.append(S.dma("sp", lambda e: e.dma_start(out=poolp, in_=poolo[0:15, :]), r=[r_poolo]))
                S.op("act", lambda e, bq=bq: e.copy(out=qk_f[:, 0:4, :], in_=bank(bq).rearrange("p (h t) -> p h t", h=4)), r=rq, w=[r_qkf])
                S.op("act", lambda e, bk=bk: e.copy(out=qk_f[:, 4:8, :], in_=bank(bk).rearrange("p (h t) -> p h t", h=4)), r=rk, w=[r_qkf])
                c2b = ropet[pb][:, 0:128].unsqueeze(1)
                ssb = ropet[pb][:, 128:256].unsqueeze(1)
                S.op("dve", lambda e, c2b=c2b: e.tensor_tensor(out=ropeA, in0=qk_f, in1=c2b.to_broadcast([128, 8, 128]), op=ALU.mult), r=[r_qkf, r_ropet[pb]], w=[r_ropeA])
                S.op("pool", lambda e, ssb=ssb: e.tensor_tensor(out=ropeB[0:64], in0=qk_f[64:128], in1=ssb[64:128].to_broadcast([64, 8, 128]), op=ALU.mult), r=[r_qkf, r_ropet[pb]], w=[r_ropeB])
                S.op("pool", lambda e, ssb=ssb: e.tensor_tensor(out=ropeB[64:128], in0=qk_f[0:64], in1=ssb[0:64].to_broadcast([64, 8, 128]), op=ALU.mult), r=[r_qkf, r_ropet[pb]], w=[r_ropeB])
                S.op("dve", lambda e, pb=pb: e.tensor_tensor(out=qkT[pb], in0=ropeA, in1=ropeB, op=ALU.add), r=[r_ropeA, r_ropeB], w=[r_qkT[pb]])
                if t > 0:
                    S.op("pool", lambda e, pb=pb: e.tensor_tensor(out=qdT[pb], in0=qkT[pb][:, 0:4, :], in1=qdec, op=ALU.mult), r=[r_qkT[pb], r_c0], w=[r_qdT[pb]])
                S.op("act", lambda e, pb=pb, bv=bv: e.copy(out=v_sb[pb], in_=bank(bv)), r=rv, w=[r_v[pb]])
                S.op("act", lambda e, pb=pb, bg=bg: e.activation(out=sgg[pb], in_=bank(bg), func=AF.Silu), r=rg, w=[r_sgg[pb]])
                S.op("pool", lambda e, pb=pb: e.tensor_tensor(out=sgg[pb], in0=sgg[pb], in1=gng, op=ALU.mult), r=[r_sgg[pb], r_c0], w=[r_sgg[pb]])
            def l0B(t):
                pb = t % 2
                bs, rs = ring.get(1)
                sv = bank(bs).rearrange("p (h t) -> p h t", h=4)
                for h in range(4):
                    S.op("pe", lambda e, h=h, sv=sv, pb=pb: e.matmul(sv[:, h, :], lhsT=qkT[pb][:, 4 + h, :], rhs=qkT[pb][:, h, :], start=True, stop=True), r=[r_qkT[pb]], w=rs)
                S.op("dve", lambda e, sv=sv, pb=pb: e.tensor_tensor(out=sT_bf[pb], in0=sv, in1=decT, op=ALU.mult), r=rs + [r_c0], w=[r_sT[pb]])
                bo, ro = ring.get(1)
                ov = bank(bo).rearrange("p (h t) -> p h t", h=4)
                for h in range(4):
                    S.op("pe", lambda e, h=h, ov=ov, pb=pb, t=t: e.matmul(ov[:, h, :], lhsT=sT_bf[pb][:, h, :], rhs=v_sb[pb][:, h * 128:(h + 1) * 128], start=True, stop=(t == 0)), r=[r_sT[pb], r_v[pb]], w=ro)
                    if t > 0:
                        S.op("pe", lambda e, h=h, ov=ov, pb=pb: e.matmul(ov[:, h, :], lhsT=qdT[pb][:, h, :], rhs=S_b[:, h, :], start=False, stop=True), r=[r_qdT[pb], r_Sb], w=ro)
                bkt, rkt = ring.get(1)
                ktv = bankbf(bkt)[:, 0:512].rearrange("p (h t) -> p h t", h=4)
                for h in range(4):
                    S.op("pe", lambda e, h=h, ktv=ktv, pb=pb: e.transpose(out=ktv[:, h, :], in_=qkT[pb][:, 4 + h, :], identity=ident_b[:]), r=[r_qkT[pb], r_idb], w=rkt)
                for h in range(4):
                    S.op("act", lambda e, h=h, ktv=ktv, pb=pb: e.activation(out=kp_bf[pb][:, h, :], in_=ktv[:, h, :], func=AF.Identity, bias=0.0, scale=kdec[:, h:h + 1]), r=rkt + [r_c0], w=[r_kp[pb]])
                bds, rds = ring.get(1)
                dsv = bank(bds).rearrange("p (h t) -> p h t", h=4)
                for h in range(4):
                    S.op("pe", lambda e, h=h, dsv=dsv, pb=pb: e.matmul(dsv[:, h, :], lhsT=kp_bf[pb][:, h, :], rhs=v_sb[pb][:, h * 128:(h + 1) * 128], start=True, stop=True), r=[r_kp[pb], r_v[pb]], w=rds)
                if t == 0:
                    S.op("dve", lambda e, dsv=dsv: e.tensor_copy(out=S_f, in_=dsv), r=rds, w=[r_Sf])
                else:
                    for h in range(4):
                        S.op("dve", lambda e, h=h, dsv=dsv: e.scalar_tensor_tensor(out=S_f[:, h, :], in0=S_f[:, h, :], scalar=tabs["gCp"][h], in1=dsv[:, h, :], op0=ALU.mult, op1=ALU.add), r=rds + [r_Sf], w=[r_Sf])
                if t < NT - 1:
                    S.op("pool", lambda e: e.tensor_copy(out=S_b, in_=S_f), r=[r_Sf], w=[r_Sb])
                else:
                    out_toks.append(S.dma("sp", lambda e: e.dma_start(out=retp.rearrange("h k v -> k h v"), in_=S_f), r=[r_Sf]))
                for h in range(4):
                    S.op("dve", lambda e, h=h, ov=ov: e.bn_stats(out=gst[:, h * 6:(h + 1) * 6], in_=ov[:, h, :]), r=ro, w=[r_gst])
                    S.op("dve", lambda e, h=h: e.bn_aggr(out=gst[:, 24 + 2 * h:26 + 2 * h], in_=gst[:, h * 6:(h + 1) * 6]), r=[r_gst], w=[r_gst])
                mvv = gst[:, 24:32].rearrange("p (h two) -> p h two", two=2)
                S.op("act", lambda e, pb=pb: e.activation(out=scr[pb][:, 0:4], in_=mvv[:, :, 1], func=AF.Sqrt, bias=eps_gn, scale=1.0), r=[r_gst, r_eps], w=[r_scr[pb]])
                S.op("dve", lambda e, pb=pb: e.reciprocal(out=scr[pb][:, 0:4], in_=scr[pb][:, 0:4]), r=[r_scr[pb]], w=[r_scr[pb]])
                S.op("dve", lambda e, pb=pb: e.scalar_tensor_tensor(out=scr[pb][:, 4:8], in0=mvv[:, :, 0], scalar=-1.0, in1=scr[pb][:, 0:4], op0=ALU.mult, op1=ALU.mult), r=[r_gst, r_scr[pb]], w=[r_scr[pb]])
                for h in range(4):
                    S.op("act", lambda e, h=h, ov=ov, pb=pb: e.activation(out=on1[:, h * 128:(h + 1) * 128], in_=ov[:, h, :], func=AF.Identity, bias=scr[pb][:, 4 + h:5 + h], scale=scr[pb][:, h:h + 1]), r=ro + [r_scr[pb]], w=[r_on1])
                S.op("pool", lambda e, pb=pb: e.tensor_tensor(out=ret_bf[pb], in0=on1, in1=sgg[pb], op=ALU.mult), r=[r_on1, r_sgg[pb]], w=[r_ret[pb]])
                brt, rrt = ring.get(1)
                rtv = bankbf(brt)[:, 0:512].rearrange("p (h t) -> p h t", h=4)
                for h in range(4):
                    S.op("pe", lambda e, h=h, rtv=rtv, pb=pb: e.transpose(out=rtv[:, h, :], in_=ret_bf[pb][:, h * 128:(h + 1) * 128], identity=ident_b[:]), r=[r_ret[pb], r_idb], w=rrt)
                S.op("dve", lambda e, rtv=rtv, pb=pb: e.tensor_copy(out=catT[pb][:, 4:8, :], in_=rtv), r=rrt, w=[r_cat[pb]])
                by, ry = ring.get(2)
                for hf in range(2):
                    for c in range(8):
                        S.op("pe", lambda e, hf=hf, c=c, by=by, pb=pb: e.matmul(bank(by + hf), lhsT=catT[pb][:, c, :], rhs=w_o_sb[:, c, hf * 512:(hf + 1) * 512], start=(c == 0), stop=(c == 7)), r=[r_cat[pb], r_wo], w=[ry[hf]])
                S.op("dve", lambda e, by=by, pb=pb, t=t: e.scalar_tensor_tensor(out=z[pb], in0=X[:, t, :], scalar=ALPHA, in1=bank(by, 2), op0=ALU.mult, op1=ALU.add), r=[rX[t]] + ry, w=[r_z[pb]])
                layer_norm_tile(z[pb], r_z[pb], X[:, t, :], rX[t], 128, scr[pb], r_scr[pb], lnt0, r_lnt0)

            l0A(0)
            for t in range(NT):
                if t + 1 < NT:
                    l0A(t + 1)
                l0B(t)

            def ffn(layer, final):
                A.at(0)
                gated, r_gated = A.get(NJ * 1024, BF16, "gated", nres=NJ); gated = gated.rearrange("p (j n) -> p j n", j=NJ)
                x1T, r_x1T = A.get(8 * 1024, BF16, "x1T", nres=8); x1T = x1T.rearrange("p (k n) -> p k n", k=8)
                cb0_, r_cb0_ = A.get(512, F32, "cbuf"); cbuf = [cb0_, cb0_]; r_cbuf = [r_cb0_, r_cb0_]
                sb0_, r_sb0_ = A.get(512, F32, "sbf"); sbf = [sb0_, sb0_]; r_sbf = [r_sb0_, r_sb0_]
                zf0, r_zf0 = A.get(D, F32, "zf"); zf = [zf0, zf0]; r_zf = [r_zf0, r_zf0]
                scf, r_scf = dbl(16, F32, "scf")
                cwb, r_cw = A.get(NJ * 4, F32, "cw"); cw = cwb[:, 0:NJ * 3].rearrange("p (j c) -> p j c", j=NJ); cb = cwb[:, NJ * 3:NJ * 4]
                hist, r_hist = A.get(NJ * 2, F32, "hist"); hist = hist.rearrange("p (j c) -> p j c", j=NJ)
                cst, r_cst = A.get(NJ * 2, F32, "cst"); cst = cst.rearrange("p (j c) -> p j c", j=NJ)
                cso, r_cso = A.get(512, F32, "cso")
                lnt, r_lnt = A.get(2 * D, F32, "lntf")
                wup = []
                r_wup = []
                for i in range(2):
                    v, r = A.get(8 * 256, BF16, f"wup{i}")
                    wup.append(v.rearrange("p (k n) -> p k n", k=8)); r_wup.append(r)
                wdn, r_wdn = A.get(NJ * 1024, BF16, "wdn"); wdn = wdn.rearrange("p (j n) -> p j n", j=NJ)
                if with_sample:
                    xs1T, r_xs1T = A.get(128, BF16, "xs1T"); xs1T = xs1T.rearrange("p (k t) -> p k t", k=8)
                    aext, r_aext = A.get(NJ * 24, F32, "aext"); aext = aext.rearrange("p (j b r) -> p j b r", j=NJ, b=4)
                    b_s, r_bs = A.get(NJ * 16, F32, "b_s"); b_s = b_s.rearrange("p (j t) -> p j t", j=NJ)
                    scT, r_scT = A.get(NJ * 8, F32, "scT"); scT = scT.rearrange("p (j r) -> p j r", j=NJ)
                    c_s, r_cs = A.get(NJ * 16, F32, "c_s"); c_s = c_s.rearrange("p (j b l) -> p j b l", j=NJ, b=4)
                    t_s, r_ts = A.get(NJ * 16, F32, "t_s"); t_s = t_s.rearrange("p (j b l) -> p j b l", j=NJ, b=4)
                    gated_s, r_gs = A.get(NJ * 16, BF16, "gated_s"); gated_s = gated_s.rearrange("p (j t) -> p j t", j=NJ)
                print("ffn arena bytes", A.cur)
                S.dma("sp", lambda e: e.dma_start(out=cw, in_=conv_w[layer]), w=[r_cw])
                S.dma("sp", lambda e: e.dma_start(out=cb, in_=conv_b[layer]), w=[r_cw])
                load_ln(lnt, r_lnt, ln_ffn_g[layer], ln_ffn_b[layer])
                wdn_v = w_down[layer].rearrange("(j p) n -> p j n", p=128)
                wdn_loaded = [False]

                def load_wdn():
                    for jj in range(0, NJ, 2):
                        S.dma("pool", lambda e, jj=jj: e.dma_start(out=wdn[:, jj:jj + 2, :], in_=wdn_v[:, jj:jj + 2, :]), w=[r_wdn])

                def load_wup(j, slot):
                    S.dma("pool", lambda e, j=j, slot=slot: e.dma_start(out=wup[slot], in_=w_up[layer, j]), w=[r_wup[slot]])

                if with_sample:
                    transpose_x(XS[:], r_XS, xs1T, r_xs1T, npart=NS)
                    for jg in range(0, NJ, 4):
                        n = min(4, NJ - jg)
                        S.dma("sp", lambda e, jg=jg, n=n: e.dma_start(out=cso[0:8, 0:n * 128], in_=sconv[layer][:, jg * 128:(jg + n) * 128]), w=[r_cso])
                        bst, rst = ring.get(1)
                        for i_ in range(n):
                            S.op("pe", lambda e, i_=i_, bst=bst: e.transpose(out=bank(bst)[:, i_ * 8:(i_ + 1) * 8], in_=cso[0:8, i_ * 128:(i_ + 1) * 128], identity=ident_f[0:8, 0:8]), r=[r_cso, r_idf], w=rst)
                        S.op("act", lambda e, jg=jg, n=n, bst=bst: e.copy(out=scT[:, jg:jg + n, :], in_=bank(bst)[:, 0:n * 8].rearrange("p (j r) -> p j r", j=n)), r=rst, w=[r_scT])
                    S.op("pool", lambda e: e.tensor_copy(out=aext[:, :, :, 0:2], in_=scT.rearrange("p j (b r) -> p j b r", b=4)), r=[r_scT], w=[r_aext])
                cnt = 0
                for half in range(2):
                    for tt in range(8):
                        t = half * 8 + tt
                        transpose_x(X[:, t, :], rX[t], x1T[:, :, tt * 128:(tt + 1) * 128], r_x1T[tt])
                    load_wup(0, cnt % 2)
                    for j in range(NJ):
                        slot = cnt % 2
                        if j + 1 < NJ:
                            load_wup(j + 1, (cnt + 1) % 2)
                        elif half == 0:
                            pass
                        if half == 0 and j == 1:
                            load_wdn()
                        cnt += 1
                        for grp in range(2):
                            ba, ra = ring.get(1)
                            bb_, rb = ring.get(1)
                            cols = slice(grp * 512, (grp + 1) * 512)
                            for (bx, rx, c0) in ((ba, ra, 0), (bb_, rb, 128)):
                                for k in range(8):
                                    S.op("pe", lambda e, bx=bx, k=k, c0=c0, slot=slot, cols=cols: e.matmul(bank(bx), lhsT=wup[slot][:, k, c0:c0 + 128], rhs=x1T[:, k, cols], start=(k == 0), stop=(k == 7)),
                                         r=[r_wup[slot]] + r_x1T[grp * 4:grp * 4 + 4], w=rx)
                            pa = bank(ba)
                            cp = (j * 2 + grp) % 2
                            c_ = cbuf[cp]
                            s_ = sbf[cp]
                            S.op("act", lambda e, pa=pa, c_=c_, j=j: e.activation(out=c_, in_=pa, func=AF.Identity, bias=cb[:, j:j + 1], scale=cw[:, j, 2:3]), r=ra + [r_cw], w=[r_cbuf[cp]])
                            S.op("dve", lambda e, pa=pa, c_=c_, j=j: e.scalar_tensor_tensor(out=c_[:, 1:512], in0=pa[:, 0:511], scalar=cw[:, j, 1:2], in1=c_[:, 1:512], op0=ALU.mult, op1=ALU.add), r=ra + [r_cw, r_cbuf[cp]], w=[r_cbuf[cp]])
                            S.op("dve", lambda e, pa=pa, c_=c_, j=j: e.scalar_tensor_tensor(out=c_[:, 2:512], in0=pa[:, 0:510], scalar=cw[:, j, 0:1], in1=c_[:, 2:512], op0=ALU.mult, op1=ALU.add), r=ra + [r_cw, r_cbuf[cp]], w=[r_cbuf[cp]])
                            if not (half == 0 and grp == 0):
                                S.op("dve", lambda e, c_=c_, j=j: e.scalar_tensor_tensor(out=c_[:, 0:1], in0=hist[:, j, 1:2], scalar=cw[:, j, 1:2], in1=c_[:, 0:1], op0=ALU.mult, op1=ALU.add), r=[r_hist, r_cw, r_cbuf[cp]], w=[r_cbuf[cp]])
                                S.op("dve", lambda e, c_=c_, j=j: e.scalar_tensor_tensor(out=c_[:, 0:2], in0=hist[:, j, 0:2], scalar=cw[:, j, 0:1], in1=c_[:, 0:2], op0=ALU.mult, op1=ALU.add), r=[r_hist, r_cw, r_cbuf[cp]], w=[r_cbuf[cp]])
                            if half == 1 and grp == 1:
                                S.op("act", lambda e, pa=pa, j=j: e.copy(out=cst[:, j, :], in_=pa[:, 510:512]), r=ra, w=[r_cst])
                            else:
                                S.op("act", lambda e, pa=pa, j=j: e.copy(out=hist[:, j, :], in_=pa[:, 510:512]), r=ra, w=[r_hist])
                            S.op("act", lambda e, c_=c_, s_=s_: e.activation(out=s_, in_=c_, func=AF.Silu), r=[r_cbuf[cp]], w=[r_sbf[cp]])
                            S.op("dve", lambda e, s_=s_, bb_=bb_, j=j, cols=cols: e.tensor_tensor(out=gated[:, j, cols], in0=s_, in1=bank(bb_), op=ALU.mult), r=[r_sbf[cp]] + rb, w=[r_gated[j]])
                        if with_sample and half == 0:
                            bsa, rsa = ring.get(1)
                            for (c0, col) in ((0, 0), (128, 16)):
                                for k in range(8):
                                    S.op("pe", lambda e, k=k, c0=c0, col=col, slot=slot, bsa=bsa: e.matmul(bank(bsa)[:, col:col + 16], lhsT=wup[slot][:, k, c0:c0 + 128], rhs=xs1T[:, k, :], start=(k == 0), stop=(k == 7)), r=[r_wup[slot], r_xs1T], w=rsa)
                            S.op("act", lambda e, j=j, bsa=bsa: e.copy(out=aext[:, j, :, 2:6], in_=bank(bsa)[:, 0:16].rearrange("p (b l) -> p b l", b=4)), r=rsa, w=[r_aext])
                            S.op("dve", lambda e, j=j, bsa=bsa: e.tensor_copy(out=b_s[:, j, :], in_=bank(bsa)[:, 16:32]), r=rsa, w=[r_bs])
                    if with_sample and half == 0:
                        def cwb(c):
                            return cw[:, :, c:c + 1].unsqueeze(3).to_broadcast([128, NJ, 4, 4])
                        S.op("dve", lambda e: e.tensor_tensor(out=c_s, in0=aext[:, :, :, 2:6], in1=cwb(2), op=ALU.mult), r=[r_aext, r_cw], w=[r_cs])
                        S.op("pool", lambda e: e.tensor_tensor(out=t_s, in0=aext[:, :, :, 1:5], in1=cwb(1), op=ALU.mult), r=[r_aext, r_cw], w=[r_ts])
                        S.op("dve", lambda e: e.tensor_tensor(out=c_s, in0=c_s, in1=t_s, op=ALU.add), r=[r_cs, r_ts], w=[r_cs])
                        S.op("pool", lambda e: e.tensor_tensor(out=t_s, in0=aext[:, :, :, 0:4], in1=cwb(0), op=ALU.mult), r=[r_aext, r_cw], w=[r_ts])
                        S.op("dve", lambda e: e.tensor_tensor(out=c_s, in0=c_s, in1=t_s, op=ALU.add), r=[r_cs, r_ts], w=[r_cs])
                        S.op("dve", lambda e: e.tensor_tensor(out=c_s, in0=c_s, in1=cb.unsqueeze(2).unsqueeze(3).to_broadcast([128, NJ, 4, 4]), op=ALU.add), r=[r_cs, r_cw], w=[r_cs])
                        S.op("act", lambda e: e.activation(out=c_s, in_=c_s, func=AF.Silu), r=[r_cs], w=[r_cs])
                        S.op("dve", lambda e: e.tensor_tensor(out=gated_s, in0=c_s.rearrange("p j b l -> p j (b l)"), in1=b_s, op=ALU.mult), r=[r_cs, r_bs], w=[r_gs])
                        S.op("pool", lambda e: e.tensor_copy(out=scT.rearrange("p j (b r) -> p j b r", b=4), in_=aext[:, :, :, 4:6]), r=[r_aext], w=[r_scT])
                        for jg in range(0, NJ, 4):
                            n = min(4, NJ - jg)
                            bst, rst = ring.get(1)
                            for i_ in range(n):
                                S.op("pe", lambda e, i_=i_, jg=jg, bst=bst: e.transpose(out=bank(bst)[0:8, i_ * 128:(i_ + 1) * 128], in_=scT[:, jg + i_, :], identity=ident_f[:]), r=[r_scT, r_idf], w=rst)
                            S.op("act", lambda e, n=n, bst=bst: e.copy(out=cso[0:8, 0:n * 128], in_=bank(bst)[0:8, 0:n * 128]), r=rst, w=[r_cso])
                            out_toks.append(S.dma("sp", lambda e, jg=jg, n=n: e.dma_start(out=convs[layer][:, jg * 128:(jg + n) * 128], in_=cso[0:8, 0:n * 128]), r=[r_cso]))
                    for tt in range(8):
                        t = half * 8 + tt
                        pb = tt % 2
                        by, ry = ring.get(2)
                        for hf in range(2):
                            for j in range(NJ):
                                S.op("pe", lambda e, hf=hf, j=j, by=by, tt=tt: e.matmul(bank(by + hf), lhsT=gated[:, j, tt * 128:(tt + 1) * 128], rhs=wdn[:, j, hf * 512:(hf + 1) * 512], start=(j == 0), stop=(j == NJ - 1)),
                                     r=[r_gated[j], r_wdn], w=[ry[hf]])
                        S.op("dve", lambda e, by=by, pb=pb, t=t: e.scalar_tensor_tensor(out=zf[pb], in0=X[:, t, :], scalar=ALPHA, in1=bank(by, 2), op0=ALU.mult, op1=ALU.add), r=[rX[t]] + ry, w=[r_zf[pb]])
                        layer_norm_tile(zf[pb], r_zf[pb], X[:, t, :], rX[t], 128, scf[pb], r_scf[pb], lnt, r_lnt)
                        if final:
                            out_toks.append(S.dma("sp", lambda e, t=t: e.dma_start(out=yp[t * 128:(t + 1) * 128, :], in_=X[:, t, :]), r=[rX[t]]))
                if with_sample:
                    bys, rys = ring.get(2)
                    for hf in range(2):
                        for j in range(NJ):
                            S.op("pe", lambda e, hf=hf, j=j, bys=bys: e.matmul(bank(bys + hf)[0:NS, :], lhsT=gated_s[:, j, :], rhs=wdn[:, j, hf * 512:(hf + 1) * 512], start=(j == 0), stop=(j == NJ - 1)), r=[r_gs, r_wdn], w=[rys[hf]])
                    S.op("dve", lambda e, bys=bys: e.scalar_tensor_tensor(out=zf[0][0:NS, :], in0=XS[:], scalar=ALPHA, in1=bank(bys, 2)[0:NS, :], op0=ALU.mult, op1=ALU.add), r=[r_XS] + rys, w=[r_zf[0]])
                    layer_norm_tile(zf[0][0:NS, :], r_zf[0], XS[:], r_XS, NS, scf[0], r_scf[0], lnt, r_lnt)
                    if final:
                        out_toks.append(S.dma("sp", lambda e: e.dma_start(out=ys, in_=XS[:]), r=[r_XS]))
                bc, rc = ring.get(1)
                for j in range(NJ):
                    S.op("pe", lambda e, j=j, bc=bc: e.transpose(out=bank(bc)[0:2, (j % 4) * 128:(j % 4 + 1) * 128], in_=cst[:, j, :], identity=ident_f[:]), r=[r_cst, r_idf], w=rc)
                    if j % 4 == 3 or j == NJ - 1:
                        j0 = j - j % 4
                        n = j - j0 + 1
                        S.op("act", lambda e, bc=bc, n=n: e.copy(out=cso[0:2, 0:n * 128], in_=bank(bc)[0:2, 0:n * 128]), r=rc, w=[r_cso])
                        out_toks.append(S.dma("sp", lambda e, j0=j0, n=n: e.dma_start(out=convp[layer][:, j0 * 128:(j0 + n) * 128], in_=cso[0:2, 0:n * 128]), r=[r_cso]))
                        if j != NJ - 1:
                            bc, rc = ring.get(1)


            def sample_l0():
                A.at(L0_BUF)
                xsT, r_xsT = A.get(128, BF16, "xsT"); xsT = xsT.rearrange("p (k t) -> p k t", k=8)
                tb, r_tb = A.get(68, F32, "stab")
                c2s = tb[:, 0:16]; sss = tb[:, 16:32]
                decTs = tb[0:4, 32:48]
                qdecs = tb[:, 48:64].rearrange("p (h l) -> p h l", h=4)
                kdecs = tb[0:4, 64:68]
                S.dma("sp", lambda e: e.dma_start(out=c2s, in_=tdr["c2s"]), w=[r_tb])
                S.dma("sp", lambda e: e.dma_start(out=sss, in_=tdr["sss"]), w=[r_tb])
                S.dma("sp", lambda e: e.dma_start(out=decTs, in_=tdr["decTs"].rearrange("m h l -> m (h l)")), w=[r_tb])
                S.dma("sp", lambda e: e.dma_start(out=qdecs, in_=tdr["qdecs"]), w=[r_tb])
                S.dma("sp", lambda e: e.dma_start(out=kdecs, in_=tdr["kdecs"]), w=[r_tb])
                sp_sb, r_sp = A.get(512, F32, "sp_sb")
                S.dma("sp", lambda e: e.dma_start(out=sp_sb[0:60, :], in_=spool), w=[r_sp])
                S0f, r_S0f = A.get(16 * 128, F32, "S0f"); S0f = S0f.rearrange("p (n v) -> p n v", n=16)
                S0b, r_S0b = A.get(16 * 128, BF16, "S0b"); S0b = S0b.rearrange("p (n v) -> p n v", n=16)
                S.dma("sp", lambda e: e.dma_start(out=S0f, in_=sret.rearrange("n k v -> k n v")), w=[r_S0f])
                S.op("pool", lambda e: e.tensor_copy(out=S0b, in_=S0f), r=[r_S0f], w=[r_S0b])
                for bb in range(SB):
                    out_toks.append(S.dma("sp", lambda e, bb=bb: e.dma_start(out=pools[bb * 15:bb * 15 + 11, :], in_=spool[bb * 15 + 4:bb * 15 + 15, :]), semkey="d2d"))
                transpose_x(XS[:], r_XS, xsT, r_xsT, npart=NS)
                bq_, rq_ = ring.get(1)
                pqk = bank(bq_)[:, 0:192].rearrange("p (g t) -> p g t", g=12)
                for g in range(12):
                    for k in range(8):
                        S.op("pe", lambda e, g=g, k=k: e.matmul(pqk[:, g, :], lhsT=w_in_sb[:, k, g * 128:(g + 1) * 128], rhs=xsT[:, k, :], start=(k == 0), stop=(k == 7)), r=[r_win, r_xsT], w=rq_)
                bu_, ru_ = ring.get(1)
                for k in range(8):
                    S.op("pe", lambda e, k=k: e.matmul(bank(bu_)[0:NS, :], lhsT=xsT[:, k, :], rhs=w_in_sb[:, k, 0:512], start=(k == 0), stop=(k == 7)), r=[r_win, r_xsT], w=ru_)
                utok, r_utok = A.get(512, F32, "utok")
                S.op("act", lambda e: e.copy(out=utok[0:NS, :], in_=bank(bu_)[0:NS, :]), r=ru_, w=[r_utok])
                for bb in range(SB):
                    out_toks.append(S.dma("sp", lambda e, bb=bb: e.dma_start(out=pools[bb * 15 + 11:bb * 15 + 15, :], in_=utok[bb * 4:bb * 4 + 4, :]), r=[r_utok]))
                bh_, rh_ = ring.get(1)
                for g in range(4):
                    S.op("pe", lambda e, g=g: e.transpose(out=bank(bh_)[:, g * 64:g * 64 + 60], in_=sp_sb[0:60, g * 128:(g + 1) * 128], identity=ident_f[0:60, 0:60]), r=[r_sp, r_idf], w=rh_)
                ue, r_ue = A.get(16 * 19, F32, "ues"); ue4 = ue.rearrange("p (g b r) -> p g b r", g=4, b=4); ue3 = ue.rearrange("p (n r) -> p n r", n=16)
                hv = bank(bh_)[:, 0:256].rearrange("p (g c) -> p g c", g=4)[:, :, 0:60].rearrange("p g (b r) -> p g b r", b=4)
                for g in range(4):
                    S.op("act", lambda e, g=g: e.copy(out=ue4[:, g, :, 0:15], in_=hv[:, g]), r=rh_, w=[r_ue])
                    S.op("act", lambda e, g=g: e.copy(out=ue4[:, g, :, 15:19], in_=pqk[:, g, :].rearrange("p (b l) -> p b l", b=4)), r=rq_, w=[r_ue])
                tAs, r_tAs = A.get(16 * 19, F32, "tAs"); tAs = tAs.rearrange("p (n r) -> p n r", n=16)
                tBs, r_tBs = A.get(16 * 19, F32, "tBs"); tBs = tBs.rearrange("p (n r) -> p n r", n=16)
                S.op("pool", lambda e: e.tensor_tensor(out=tAs[:, :, 1:19], in0=ue3[:, :, 1:19], in1=ue3[:, :, 0:18], op=ALU.add), r=[r_ue], w=[r_tAs])
                S.op("pool", lambda e: e.tensor_tensor(out=tBs[:, 4:16, 3:19], in0=tAs[:, 4:16, 3:19], in1=tAs[:, 4:16, 1:17], op=ALU.add), r=[r_tAs], w=[r_tBs])
                S.op("pool", lambda e: e.tensor_tensor(out=tAs[:, 8:16, 7:19], in0=tBs[:, 8:16, 7:19], in1=tBs[:, 8:16, 3:15], op=ALU.add), r=[r_tBs], w=[r_tAs])
                S.op("pool", lambda e: e.tensor_tensor(out=tBs[:, 12:16, 15:19], in0=tAs[:, 12:16, 15:19], in1=tAs[:, 12:16, 7:11], op=ALU.add), r=[r_tAs], w=[r_tBs])
                pooleds, r_pooleds = A.get(64, BF16, "pooleds"); pooleds = pooleds.rearrange("p (g b l) -> p g b l", g=4, b=4)
                fins = (tAs, tBs, tAs, tBs)
                for g in range(4):
                    S.op("dve", lambda e, g=g: e.scalar_tensor_tensor(out=pooleds[:, g], in0=fins[g][:, g * 4:(g + 1) * 4, 15:19], scalar=1.0 / POOL_W[g], in1=ue3[:, g * 4:(g + 1) * 4, 15:19], op0=ALU.mult, op1=ALU.subtract), r=[r_tAs, r_tBs, r_ue], w=[r_pooleds])
                catTs, r_catTs = A.get(128, BF16, "catTs"); catTs = catTs.rearrange("p (c t) -> p c t", c=8)
                bpm_, rpm_ = ring.get(1)
                pms = bank(bpm_)[:, 0:64].rearrange("p (g t) -> p g t", g=4)
                for g in range(4):
                    S.op("pe", lambda e, g=g: e.matmul(pms[:, g, :], lhsT=poolw_sb[:, g, :], rhs=pooleds[:, g].rearrange("p b l -> p (b l)"), start=True, stop=True), r=[r_poolw, r_pooleds], w=rpm_)
                for g in range(4):
                    S.op("act", lambda e, g=g: e.activation(out=catTs[:, g, :], in_=pms[:, g, :], func=AF.Identity, bias=0.0, scale=pscale[:, g:g + 1]), r=rpm_ + [r_c0], w=[r_catTs])
                qkfs, r_qkfs = A.get(128, F32, "qkfs"); qkfs = qkfs.rearrange("p (h t) -> p h t", h=8)
                rAs, r_rAs = A.get(128, F32, "rAs"); rAs = rAs.rearrange("p (h t) -> p h t", h=8)
                rBs, r_rBs = A.get(128, F32, "rBs"); rBs = rBs.rearrange("p (h t) -> p h t", h=8)
                qkTs, r_qkTs = A.get(128, BF16, "qkTs"); qkTs = qkTs.rearrange("p (h t) -> p h t", h=8)
                qdTs, r_qdTs = A.get(64, BF16, "qdTs"); qdTs = qdTs.rearrange("p (h t) -> p h t", h=4)
                S.op("act", lambda e: e.copy(out=qkfs, in_=pqk[:, 4:12, :]), r=rq_, w=[r_qkfs])
                S.op("dve", lambda e: e.tensor_tensor(out=rAs, in0=qkfs, in1=c2s.unsqueeze(1).to_broadcast([128, 8, 16]), op=ALU.mult), r=[r_qkfs, r_tb], w=[r_rAs])
                S.op("pool", lambda e: e.tensor_tensor(out=rBs[0:64], in0=qkfs[64:128], in1=sss[64:128].unsqueeze(1).to_broadcast([64, 8, 16]), op=ALU.mult), r=[r_qkfs, r_tb], w=[r_rBs])
                S.op("pool", lambda e: e.tensor_tensor(out=rBs[64:128], in0=qkfs[0:64], in1=sss[0:64].unsqueeze(1).to_broadcast([64, 8, 16]), op=ALU.mult), r=[r_qkfs, r_tb], w=[r_rBs])
                S.op("dve", lambda e: e.tensor_tensor(out=qkTs, in0=rAs, in1=rBs, op=ALU.add), r=[r_rAs, r_rBs], w=[r_qkTs])
                S.op("pool", lambda e: e.tensor_tensor(out=qdTs.rearrange("p h (b l) -> p h b l", b=4), in0=qkTs[:, 0:4, :].rearrange("p h (b l) -> p h b l", b=4), in1=qdecs.unsqueeze(2).to_broadcast([128, 4, 4, 4]), op=ALU.mult), r=[r_qkTs, r_tb], w=[r_qdTs])
                v_s, r_vs = A.get(SB * 512, BF16, "v_s"); v_s = v_s.rearrange("p (b n) -> p b n", b=SB)
                sTs, r_sTs = A.get(64, BF16, "sTs"); sTs = sTs.rearrange("p (b h l) -> p b h l", b=4, h=4)
                kps, r_kps = A.get(512, BF16, "kps"); kps = kps.rearrange("p (h d) -> p h d", h=4)
                S.op("pool", lambda e: e.memset(v_s, 0.0), w=[r_vs])
                S.op("pool", lambda e: e.memset(sTs, 0.0), w=[r_sTs])
                S.op("pool", lambda e: e.memset(kps, 0.0), w=[r_kps])
                sggs, r_sggs = A.get(SB * 512, F32, "sggs"); sggs = sggs.rearrange("p (b n) -> p b n", b=SB)
                for bb in range(SB):
                    bv_, rv_ = ring.get(1)
                    bg_, rg_ = ring.get(1)
                    for (bx, rx, c0) in ((bv_, rv_, 1536), (bg_, rg_, 2048)):
                        for k in range(8):
                            S.op("pe", lambda e, bx=bx, k=k, c0=c0, bb=bb: e.matmul(bank(bx)[0:4, :], lhsT=xsT[:, k, bb * 4:bb * 4 + 4], rhs=w_in_sb[:, k, c0:c0 + 512], start=(k == 0), stop=(k == 7)), r=[r_win, r_xsT], w=rx)
                    S.op("act", lambda e, bv_=bv_, bb=bb: e.copy(out=v_s[0:4, bb, :], in_=bank(bv_)[0:4, :]), r=rv_, w=[r_vs])
                    S.op("act", lambda e, bg_=bg_, bb=bb: e.activation(out=sggs[0:4, bb, :], in_=bank(bg_)[0:4, :], func=AF.Silu), r=rg_, w=[r_sggs])
                    S.op("pool", lambda e, bb=bb: e.tensor_tensor(out=sggs[0:4, bb, :], in0=sggs[0:4, bb, :], in1=gng[0:4], op=ALU.mult), r=[r_sggs, r_c0], w=[r_sggs])
                bs_, rs_ = ring.get(1)
                for bb in range(SB):
                    for h in range(4):
                        c0 = (bb * 4 + h) * 4
                        S.op("pe", lambda e, bb=bb, h=h, c0=c0: e.matmul(bank(bs_)[0:4, c0:c0 + 4], lhsT=qkTs[:, 4 + h, bb * 4:bb * 4 + 4], rhs=qkTs[:, h, bb * 4:bb * 4 + 4], start=True, stop=True), r=[r_qkTs], w=rs_)
                S.op("dve", lambda e: e.tensor_tensor(out=sTs[0:4].rearrange("p b h l -> p b (h l)"), in0=bank(bs_)[0:4, 0:64].rearrange("p (b n) -> p b n", b=4), in1=decTs.unsqueeze(1).to_broadcast([4, 4, 16]), op=ALU.mult), r=rs_ + [r_tb], w=[r_sTs])
                Snew, r_Snew = dbl(512, F32, "Snew", "p (h v) -> p h v", h=4)
                gsts, r_gsts = A.get(40, F32, "gsts")
                on1s, r_on1s = A.get(512, F32, "on1s")
                rets_bf, r_retsb = A.get(512, BF16, "retsb")
                for bb in range(SB):
                    bo_, ro_ = ring.get(1)
                    ovs = bank(bo_)[0:4, :].rearrange("p (h t) -> p h t", h=4)
                    for h in range(4):
                        S.op("pe", lambda e, h=h, bb=bb, ovs=ovs: e.matmul(ovs[:, h, :], lhsT=sTs[:, bb, h, :], rhs=v_s[:, bb, h * 128:(h + 1) * 128], start=True, stop=False), r=[r_sTs, r_vs], w=ro_)
                        S.op("pe", lambda e, h=h, bb=bb, ovs=ovs: e.matmul(ovs[:, h, :], lhsT=qdTs[:, h, bb * 4:bb * 4 + 4], rhs=S0b[:, bb * 4 + h, :], start=False, stop=True), r=[r_qdTs, r_S0b], w=ro_)
                    bkt_, rkt_ = ring.get(1)
                    ktvs = bankbf(bkt_)[0:4, 0:512].rearrange("p (h t) -> p h t", h=4)
                    for h in range(4):
                        S.op("pe", lambda e, h=h, bb=bb, ktvs=ktvs: e.transpose(out=ktvs[:, h, :], in_=qkTs[:, 4 + h, bb * 4:bb * 4 + 4], identity=ident_b[:]), r=[r_qkTs, r_idb], w=rkt_)
                    for h in range(4):
                        S.op("act", lambda e, h=h, ktvs=ktvs: e.activation(out=kps[0:4, h, :], in_=ktvs[:, h, :], func=AF.Identity, bias=0.0, scale=kdecs[:, h:h + 1]), r=rkt_ + [r_tb], w=[r_kps])
                    bds_, rds_ = ring.get(1)
                    dss = bank(bds_).rearrange("p (h t) -> p h t", h=4)
                    for h in range(4):
                        S.op("pe", lambda e, h=h, bb=bb, dss=dss: e.matmul(dss[:, h, :], lhsT=kps[:, h, :], rhs=v_s[:, bb, h * 128:(h + 1) * 128], start=True, stop=True), r=[r_kps, r_vs], w=rds_)
                    sn = Snew[bb % 2]
                    for h in range(4):
                        S.op("dve", lambda e, h=h, bb=bb, dss=dss, sn=sn: e.scalar_tensor_tensor(out=sn[:, h, :], in0=S0f[:, bb * 4 + h, :], scalar=tabs["gCs"][h], in1=dss[:, h, :], op0=ALU.mult, op1=ALU.add), r=rds_ + [r_S0f], w=[r_Snew[bb % 2]])
                    out_toks.append(S.dma("sp", lambda e, bb=bb, sn=sn: e.dma_start(out=rets[bb * 4:(bb + 1) * 4].rearrange("h k v -> k h v"), in_=sn), r=[r_Snew[bb % 2]]))
                    for h in range(4):
                        S.op("dve", lambda e, h=h, ovs=ovs: e.bn_stats(out=gsts[0:4, h * 6:(h + 1) * 6], in_=ovs[:, h, :]), r=ro_, w=[r_gsts])
                        S.op("dve", lambda e, h=h: e.bn_aggr(out=gsts[0:4, 24 + 2 * h:26 + 2 * h], in_=gsts[0:4, h * 6:(h + 1) * 6]), r=[r_gsts], w=[r_gsts])
                    mvs = gsts[0:4, 24:32].rearrange("p (h two) -> p h two", two=2)
                    S.op("act", lambda e: e.activation(out=gsts[0:4, 32:36], in_=mvs[:, :, 1], func=AF.Sqrt, bias=eps_gn[0:4], scale=1.0), r=[r_gsts, r_eps], w=[r_gsts])
                    S.op("dve", lambda e: e.reciprocal(out=gsts[0:4, 32:36], in_=gsts[0:4, 32:36]), r=[r_gsts], w=[r_gsts])
                    S.op("dve", lambda e: e.scalar_tensor_tensor(out=gsts[0:4, 36:40], in0=mvs[:, :, 0], scalar=-1.0, in1=gsts[0:4, 32:36], op0=ALU.mult, op1=ALU.mult), r=[r_gsts], w=[r_gsts])
                    for h in range(4):
                        S.op("act", lambda e, h=h, ovs=ovs: e.activation(out=on1s[0:4, h * 128:(h + 1) * 128], in_=ovs[:, h, :], func=AF.Identity, bias=gsts[0:4, 36 + h:37 + h], scale=gsts[0:4, 32 + h:33 + h]), r=ro_ + [r_gsts], w=[r_on1s])
                    S.op("pool", lambda e, bb=bb: e.tensor_tensor(out=rets_bf[0:4, :], in0=on1s[0:4, :], in1=sggs[0:4, bb, :], op=ALU.mult), r=[r_on1s, r_sggs], w=[r_retsb])
                    brt_, rrt_ = ring.get(1)
                    rtvs = bankbf(brt_)[:, 0:16].rearrange("p (h t) -> p h t", h=4)
                    for h in range(4):
                        S.op("pe", lambda e, h=h, rtvs=rtvs: e.transpose(out=rtvs[:, h, :], in_=rets_bf[0:4, h * 128:(h + 1) * 128], identity=ident_b[0:4, 0:4]), r=[r_retsb, r_idb], w=rrt_)
                    S.op("dve", lambda e, rtvs=rtvs, bb=bb: e.tensor_copy(out=catTs[:, 4:8, bb * 4:bb * 4 + 4], in_=rtvs), r=rrt_, w=[r_catTs])
                by_, ry_ = ring.get(2)
                for hf in range(2):
                    for c in range(8):
                        S.op("pe", lambda e, hf=hf, c=c: e.matmul(bank(by_ + hf)[0:NS, :], lhsT=catTs[:, c, :], rhs=w_o_sb[:, c, hf * 512:(hf + 1) * 512], start=(c == 0), stop=(c == 7)), r=[r_catTs, r_wo], w=[ry_[hf]])
                zs, r_zs = A.get(D, F32, "zs")
                scs, r_scs = A.get(16, F32, "scs")
                S.op("dve", lambda e: e.scalar_tensor_tensor(out=zs[0:NS, :], in0=XS[:], scalar=ALPHA, in1=bank(by_, 2)[0:NS, :], op0=ALU.mult, op1=ALU.add), r=[r_XS] + ry_, w=[r_zs])
                layer_norm_tile(zs[0:NS, :], r_zs, XS[:], r_XS, NS, scs, r_scs, lnt0, r_lnt0)

            if with_sample:
                sample_l0()
            chk('l0')
            S.set_phase('ffn0')
            ffn(0, False)
            chk('ffn0')
            S.set_phase('l1')

            A.at(0)
            wdq_sb, r_wdq = A.get(8 * 512, BF16, "wdq"); wdq_sb = wdq_sb.rearrange("p (k n) -> p k n", k=8)
            wuq_sb, r_wuq = A.get(4 * 1536, BF16, "wuq"); wuq_sb = wuq_sb.rearrange("p (k n) -> p k n", k=4)
            wdkv_sb, r_wdkv = A.get(8 * 320, BF16, "wdkv"); wdkv_sb = wdkv_sb.rearrange("p (k n) -> p k n", k=8)
            wukT_sb, r_wukT = A.get(8 * 256, BF16, "wukT"); wukT_sb = wukT_sb.rearrange("p (h c) -> p h c", h=8)
            wuv_sb, r_wuv = A.get(2 * 1024, BF16, "wuv"); wuv_sb = wuv_sb.rearrange("p (k n) -> p k n", k=2)
            wo1_sb, r_wo1 = A.get(8 * 1024, BF16, "wo1"); wo1_sb = wo1_sb.rearrange("p (k n) -> p k n", k=8)
            c1buf, r_c1 = A.get(512 + 256, F32, "c1")
            qng = c1buf[:, 0:512]
            kvng = c1buf[:, 512:768]
            lnt1, r_lnt1 = A.get(2 * D, F32, "lnt1")
            ckvT, r_ckvT = A.get(2 * SEQ, BF16, "ckvT", nres=NT); ckvT = ckvT.rearrange("p (k n) -> p k n", k=2)
            ckvK, r_ckvK = A.get(NT * 256, BF16, "ckvK", nres=NT); ckvK = ckvK.rearrange("p (t c) -> p t c", t=NT)
            kpeT, r_kpeT = A.get(SEQ, BF16, "kpeT", nres=NT)
            S.dma("pool", lambda e: e.dma_start(out=wdq_sb, in_=w_dq.rearrange("(k p) n -> p k n", p=128)), w=[r_wdq])
            S.dma("pool", lambda e: e.dma_start(out=wdkv_sb, in_=w_dkv.rearrange("(k p) n -> p k n", p=128)), w=[r_wdkv])
            S.dma("pool", lambda e: e.dma_start(out=wuq_sb, in_=w_uq.rearrange("(k p) n -> p k n", p=128)), w=[r_wuq])
            S.dma("pool", lambda e: e.dma_start(out=wukT_sb, in_=w_ukT), w=[r_wukT])
            S.dma("pool", lambda e: e.dma_start(out=wuv_sb, in_=w_uv.rearrange("(k p) n -> p k n", p=128)), w=[r_wuv])
            S.dma("pool", lambda e: e.dma_start(out=wo1_sb, in_=w_o1.rearrange("(k p) n -> p k n", p=128)), w=[r_wo1])
            S.dma("sp", lambda e: e.dma_start(out=qng, in_=qn_g.partition_broadcast(128)), w=[r_c1])
            S.dma("sp", lambda e: e.dma_start(out=kvng, in_=kvn_g.partition_broadcast(128)), w=[r_c1])
            load_ln(lnt1, r_lnt1, ln_mix_g[1], ln_mix_b[1])
            L1_BUF = A.cur

            xT1, r_xT1 = dbl(1024, BF16, "xT1", "p (k t) -> p k t", k=8)
            rt1, r_rt1 = dbl(128, F32, "rt1")
            sc1, r_sc1 = dbl(16, F32, "sc1")
            cqn, r_cqn = dbl(512, BF16, "cqn")
            cqnT, r_cqnT = dbl(512, BF16, "cqnT", "p (k t) -> p k t", k=4)
            ckv_f, r_ckvf = dbl(256, F32, "ckv_f")
            kpe_f, r_kpef = dbl(64, F32, "kpe_f")
            kpe_t, r_kpet = A.get(128, F32, "kpe_t")
            kpe_b, r_kpeb = dbl(128, BF16, "kpe_b")
            qnT0_, r_qnT0_ = A.get(1024, BF16, "qnT"); qnT0_ = qnT0_.rearrange("p (h t) -> p h t", h=8); qnT = [qnT0_, qnT0_]; r_qnT = [r_qnT0_, r_qnT0_]
            qpe_t, r_qpet = A.get(1024, F32, "qpe_t")
            junk, r_junk = qpe_t[:, 0:512], r_qpet
            qpe_b, r_qpeb = dbl(512, BF16, "qpe_b")
            qpeT, r_qpeT = dbl(1024, BF16, "qpeT", "p (h t) -> p h t", h=8)
            qlT, r_qlT = dbl(2048, BF16, "qlT", "p (h k t) -> p h k t", h=8, k=2)
            p_sb, r_p = dbl(SEQ, BF16, "p_sb")
            pT_sb, r_pT = dbl(SEQ, BF16, "pT_sb")
            st1, r_st1 = dbl(8, F32, "st1")
            ol_b, r_olb = dbl(256, BF16, "ol_b")
            olT, r_olT = dbl(256, BF16, "olT", "p (k t) -> p k t", k=2)
            oT10_, r_oT10_ = A.get(1024, BF16, "oT1"); oT10_ = oT10_.rearrange("p (h t) -> p h t", h=8); oT1 = [oT10_, oT10_]; r_oT1 = [r_oT10_, r_oT10_]
            z10, r_z10 = A.get(D, F32, "z1"); z1 = [z10, z10]; r_z1 = [r_z10, r_z10]
            print("layer1 arena bytes", A.cur)

            for i_ in range(2):
                S.op("pool", lambda e, i_=i_: e.memset(qpeT[i_], 0.0), w=[r_qpeT[i_]])
            def l1A(t):
                pb = t % 2
                tok = slice(t * 128, (t + 1) * 128)
                S.dma("sp", lambda e, pb=pb, t=t: e.dma_start(out=rt1[pb][:, 0:64], in_=tdr["cc1"][:, t, :]), w=[r_rt1[pb]])
                S.dma("sp", lambda e, pb=pb, t=t: e.dma_start(out=rt1[pb][:, 64:128], in_=tdr["ss1"][:, t, :]), w=[r_rt1[pb]])
                transpose_x(X[:, t, :], rX[t], xT1[pb], r_xT1[pb])
                bcq, rcq = ring.get(1)
                bkv, rkv = ring.get(1)
                for k in range(8):
                    S.op("pe", lambda e, k=k, pb=pb, bcq=bcq: e.matmul(bank(bcq), lhsT=xT1[pb][:, k, :], rhs=wdq_sb[:, k, :], start=(k == 0), stop=(k == 7)), r=[r_xT1[pb], r_wdq], w=rcq)
                for k in range(8):
                    S.op("pe", lambda e, k=k, pb=pb, bkv=bkv: e.matmul(bank(bkv)[:, 0:320], lhsT=xT1[pb][:, k, :], rhs=wdkv_sb[:, k, :], start=(k == 0), stop=(k == 7)), r=[r_xT1[pb], r_wdkv], w=rkv)
                sc = sc1[pb]
                S.op("act", lambda e, bcq=bcq, sc=sc: e.activation(out=junk, in_=bank(bcq), func=AF.Square, accum_out=sc[:, 0:1]), r=rcq, w=[r_junk, r_sc1[pb]])
                S.op("act", lambda e, sc=sc: e.activation(out=sc[:, 1:2], in_=sc[:, 0:1], func=AF.Sqrt, bias=eps_rms, scale=1.0 / 512), r=[r_sc1[pb], r_eps], w=[r_sc1[pb]])
                S.op("dve", lambda e, sc=sc: e.reciprocal(out=sc[:, 1:2], in_=sc[:, 1:2]), r=[r_sc1[pb]], w=[r_sc1[pb]])
                S.op("dve", lambda e, bcq=bcq, sc=sc, pb=pb: e.scalar_tensor_tensor(out=cqn[pb], in0=bank(bcq), scalar=sc[:, 1:2], in1=qng, op0=ALU.mult, op1=ALU.mult), r=rcq + [r_sc1[pb], r_c1], w=[r_cqn[pb]])
                S.op("act", lambda e, bkv=bkv, sc=sc: e.activation(out=junk[:, 0:256], in_=bank(bkv)[:, 0:256], func=AF.Square, accum_out=sc[:, 2:3]), r=rkv, w=[r_junk, r_sc1[pb]])
                S.op("act", lambda e, sc=sc: e.activation(out=sc[:, 3:4], in_=sc[:, 2:3], func=AF.Sqrt, bias=eps_rms, scale=1.0 / 256), r=[r_sc1[pb], r_eps], w=[r_sc1[pb]])
                S.op("dve", lambda e, sc=sc: e.reciprocal(out=sc[:, 3:4], in_=sc[:, 3:4]), r=[r_sc1[pb]], w=[r_sc1[pb]])
                S.op("dve", lambda e, bkv=bkv, sc=sc, pb=pb: e.scalar_tensor_tensor(out=ckv_f[pb], in0=bank(bkv)[:, 0:256], scalar=sc[:, 3:4], in1=kvng, op0=ALU.mult, op1=ALU.mult), r=rkv + [r_sc1[pb], r_c1], w=[r_ckvf[pb]])
                out_toks.append(S.dma("sp", lambda e, pb=pb, tok=tok: e.dma_start(out=ckvp[tok, :], in_=ckv_f[pb]), r=[r_ckvf[pb]]))
                S.op("pool", lambda e, pb=pb, t=t: e.tensor_copy(out=ckvK[:, t, :], in_=ckv_f[pb]), r=[r_ckvf[pb]], w=[r_ckvK[t]])
                S.op("dve", lambda e, bkv=bkv, pb=pb: e.tensor_tensor(out=kpe_t[:, 0:64], in0=bank(bkv)[:, 256:320], in1=rt1[pb][:, 0:64], op=ALU.mult), r=rkv + [r_rt1[pb]], w=[r_kpet])
                S.op("dve", lambda e, bkv=bkv, pb=pb: e.tensor_tensor(out=kpe_t[:, 64:96], in0=bank(bkv)[:, 288:320], in1=rt1[pb][:, 64:96], op=ALU.mult), r=rkv + [r_rt1[pb]], w=[r_kpet])
                S.op("dve", lambda e, bkv=bkv, pb=pb: e.tensor_tensor(out=kpe_t[:, 96:128], in0=bank(bkv)[:, 256:288], in1=rt1[pb][:, 96:128], op=ALU.mult), r=rkv + [r_rt1[pb]], w=[r_kpet])
                S.op("pool", lambda e, pb=pb: e.tensor_tensor(out=kpe_f[pb], in0=kpe_t[:, 0:64], in1=kpe_t[:, 64:128], op=ALU.add), r=[r_kpet], w=[r_kpef[pb]])
                out_toks.append(S.dma("sp", lambda e, pb=pb, tok=tok: e.dma_start(out=kpep[tok, :], in_=kpe_f[pb]), r=[r_kpef[pb]]))
                S.op("pool", lambda e, pb=pb: e.tensor_copy(out=kpe_b[pb][:, 0:64], in_=kpe_f[pb]), r=[r_kpef[pb]], w=[r_kpeb[pb]])
                S.op("pool", lambda e, pb=pb: e.tensor_copy(out=kpe_b[pb][:, 64:128], in_=kpe_f[pb]), r=[r_kpef[pb]], w=[r_kpeb[pb]])
                btr, rtr = ring.get(2)
                trv = bankbf(btr, 2)
                for k in range(4):
                    S.op("pe", lambda e, k=k, trv=trv, pb=pb: e.transpose(out=trv[:, k * 128:(k + 1) * 128], in_=cqn[pb][:, k * 128:(k + 1) * 128], identity=ident_b[:]), r=[r_cqn[pb], r_idb], w=rtr)
                for k in range(2):
                    S.op("pe", lambda e, k=k, trv=trv, t=t: e.transpose(out=trv[:, 1024 + k * 128:1024 + (k + 1) * 128], in_=ckvK[:, t, k * 128:(k + 1) * 128], identity=ident_b[:]), r=[r_ckvK[t], r_idb], w=rtr)
                S.op("pe", lambda e, trv=trv, pb=pb: e.transpose(out=trv[:, 1280:1408], in_=kpe_b[pb], identity=ident_b[:]), r=[r_kpeb[pb], r_idb], w=rtr)
                S.op("act", lambda e, trv=trv, pb=pb: e.copy(out=cqnT[pb], in_=trv[:, 0:512].rearrange("p (k t) -> p k t", k=4)), r=rtr, w=[r_cqnT[pb]])
                S.op("dve", lambda e, trv=trv, tok=tok, t=t: e.tensor_copy(out=ckvT[:, :, tok], in_=trv[:, 1024:1280].rearrange("p (k t) -> p k t", k=2)), r=rtr, w=[r_ckvT[t]])
                S.op("dve", lambda e, trv=trv, tok=tok, t=t: e.tensor_copy(out=kpeT[:, tok], in_=trv[:, 1280:1408]), r=rtr, w=[r_kpeT[t]])
                for hh in range(2):
                    bqn, rqn = ring.get(1)
                    qv = bank(bqn).rearrange("p (h t) -> p h t", h=4)
                    for h4 in range(4):
                        h = hh * 4 + h4
                        for k in range(4):
                            S.op("pe", lambda e, qv=qv, h4=h4, h=h, k=k, pb=pb: e.matmul(qv[:, h4, :], lhsT=wuq_sb[:, k, h * 128:(h + 1) * 128], rhs=cqnT[pb][:, k, :], start=(k == 0), stop=(k == 3)), r=[r_wuq, r_cqnT[pb]], w=rqn)
                    evac("act" if hh == 0 else "dve", qnT[pb][:, hh * 4:hh * 4 + 4, :], qv, rqn, [r_qnT[pb]])
                bqp, rqp = ring.get(1)
                for k in range(4):
                    S.op("pe", lambda e, k=k, bqp=bqp, pb=pb: e.matmul(bank(bqp), lhsT=cqnT[pb][:, k, :], rhs=wuq_sb[:, k, 1024:1536], start=(k == 0), stop=(k == 3)), r=[r_wuq, r_cqnT[pb]], w=rqp)
                qpv = bank(bqp).rearrange("p (h d) -> p h d", h=8)
                qA = qpe_t[:, 0:512].rearrange("p (h d) -> p h d", h=8)
                qB = qpe_t[:, 512:1024].rearrange("p (h d) -> p h d", h=8)
                ccb = rt1[pb][:, 0:64].unsqueeze(1)
                ssb1 = rt1[pb][:, 64:96].unsqueeze(1)
                ssb2 = rt1[pb][:, 96:128].unsqueeze(1)
                S.op("dve", lambda e, qpv=qpv, ccb=ccb: e.tensor_tensor(out=qA, in0=qpv, in1=ccb.to_broadcast([128, 8, 64]), op=ALU.mult), r=rqp + [r_rt1[pb]], w=[r_qpet])
                S.op("dve", lambda e, qpv=qpv, ssb1=ssb1: e.tensor_tensor(out=qB[:, :, 0:32], in0=qpv[:, :, 32:64], in1=ssb1.to_broadcast([128, 8, 32]), op=ALU.mult), r=rqp + [r_rt1[pb]], w=[r_qpet])
                S.op("dve", lambda e, qpv=qpv, ssb2=ssb2: e.tensor_tensor(out=qB[:, :, 32:64], in0=qpv[:, :, 0:32], in1=ssb2.to_broadcast([128, 8, 32]), op=ALU.mult), r=rqp + [r_rt1[pb]], w=[r_qpet])
                S.op("pool", lambda e, pb=pb: e.tensor_tensor(out=qpe_b[pb], in0=qpe_t[:, 0:512], in1=qpe_t[:, 512:1024], op=ALU.add), r=[r_qpet], w=[r_qpeb[pb]])
                btq, rtq = ring.get(1)
                tqv = bankbf(btq)[:, 0:512].rearrange("p (a t) -> p a t", a=4)
                for a in range(4):
                    S.op("pe", lambda e, a=a, tqv=tqv, pb=pb: e.transpose(out=tqv[:, a, :], in_=qpe_b[pb][:, a * 128:(a + 1) * 128], identity=ident_b[:]), r=[r_qpeb[pb], r_idb], w=rtq)
                qz = qpeT[pb].rearrange("p (a two) t -> p a two t", two=2)
                S.op("act", lambda e, tqv=tqv, qz=qz: e.copy(out=qz[0:64, :, 0, :], in_=tqv[0:64]), r=rtq, w=[r_qpeT[pb]])
                S.op("dve", lambda e, tqv=tqv, qz=qz: e.tensor_copy(out=qz[64:128, :, 1, :], in_=tqv[64:128]), r=rtq, w=[r_qpeT[pb]])
                for hh in range(4):
                    bql, rql = ring.get(1)
                    qlv = bank(bql).rearrange("p (h k t) -> p h k t", h=2, k=2)
                    for h2 in range(2):
                        h = hh * 2 + h2
                        for k in range(2):
                            S.op("pe", lambda e, qlv=qlv, h2=h2, h=h, k=k, pb=pb: e.matmul(qlv[:, h2, k, :], lhsT=wukT_sb[:, h, k * 128:(k + 1) * 128], rhs=qnT[pb][:, h, :], start=True, stop=True), r=[r_wukT, r_qnT[pb]], w=rql)
                    evac("act" if hh % 2 == 0 else "dve", qlT[pb][:, hh * 2:hh * 2 + 2, :, :], qlv, rql, [r_qlT[pb]])

            l1ctx = {}
            def l1P1a(t, h):
                pb = t % 2
                hp = h % 2
                nk = (t + 1) * 128
                nb = (nk + 511) // 512
                nbanks = 1 if nb == 1 else (2 if nb == 2 else 4)
                bs, rs = ring.get(nbanks)
                sfull = bank(bs, nbanks)
                for gi in range(nb):
                    k0 = gi * 512
                    kn = min(512, nk - k0)
                    ks = slice(k0, k0 + kn)
                    kres = r_ckvT[gi * 4:gi * 4 + (kn + 127) // 128]
                    kpres = r_kpeT[gi * 4:gi * 4 + (kn + 127) // 128]
                    last_diag = (gi == nb - 1)
                    parts = [(ks, False)] if not last_diag else ([(slice(k0, nk - 128), False)] if nk - 128 > k0 else []) + [(slice(nk - 128, nk), True)]
                    for (ks, dg) in parts:
                        S.op("pe", lambda e, sfull=sfull, ks=ks, h=h, pb=pb: e.matmul(sfull[:, ks], lhsT=qlT[pb][:, h, 0, :], rhs=ckvT[:, 0, ks], start=True, stop=False), r=[r_qlT[pb]] + kres, w=[rs[gi]])
                        S.op("pe", lambda e, sfull=sfull, ks=ks, h=h, pb=pb: e.matmul(sfull[:, ks], lhsT=qlT[pb][:, h, 1, :], rhs=ckvT[:, 1, ks], start=False, stop=False), r=[r_qlT[pb]] + kres, w=[rs[gi]])
                        S.op("pe", lambda e, sfull=sfull, ks=ks, h=h, pb=pb, dg=dg: e.matmul(sfull[:, ks], lhsT=qpeT[pb][:, h, :], rhs=kpeT[:, ks], start=False, stop=(not dg)), r=[r_qpeT[pb]] + kpres, w=[rs[gi]])
                        if dg:
                            S.op("pe", lambda e, sfull=sfull, ks=ks: e.matmul(sfull[:, ks], lhsT=ident_b[:], rhs=maskd[:], start=False, stop=True), r=[r_idb, r_maskd], w=[rs[gi]])
                stt = st1[hp]
                chs = [(c0_, min(nk, c0_ + 1024)) for c0_ in range(0, nk, 1024)]
                for ci, (a_, b_) in enumerate(chs):
                    S.op("dve", lambda e, sfull=sfull, a_=a_, b_=b_, ci=ci, stt=stt: e.reduce_max(out=stt[:, 4 + ci:5 + ci], in_=sfull[:, a_:b_], axis=AX.X), r=rs[0:nb], w=[r_st1[hp]])
                if len(chs) == 2:
                    S.op("dve", lambda e, stt=stt: e.tensor_tensor(out=stt[:, 4:5], in0=stt[:, 4:5], in1=stt[:, 5:6], op=ALU.max), r=[r_st1[hp]], w=[r_st1[hp]])
                S.op("dve", lambda e, stt=stt: e.tensor_scalar(out=stt[:, 1:2], in0=stt[:, 4:5], scalar1=-MLA_SCALE, scalar2=None, op0=ALU.mult), r=[r_st1[hp]], w=[r_st1[hp]])
                l1ctx[(t, h)] = (sfull, rs, nb, chs)

            def l1P1b(t, h):
                pb = t % 2
                hp = h % 2
                nk = (t + 1) * 128
                stt = st1[hp]
                sfull, rs, nb, chs = l1ctx[(t, h)]
                for ci, (a_, b_) in enumerate(chs):
                    S.op("act", lambda e, sfull=sfull, a_=a_, b_=b_, ci=ci, stt=stt, hp=hp: e.activation(out=p_sb[hp][:, a_:b_], in_=sfull[:, a_:b_], func=AF.Exp, bias=stt[:, 1:2], scale=MLA_SCALE, accum_out=stt[:, 6 + ci:7 + ci]), r=rs[0:nb] + [r_st1[hp]], w=[r_p[hp], r_st1[hp]])

            def l1P2a(t, h):
                pb = t % 2
                hp = h % 2
                nk = (t + 1) * 128
                stt = st1[hp]
                nkt = t + 1
                for g0 in range(0, nkt, 4):
                    bpt, rpt = ring.get(1)
                    ptv = bankbf(bpt)
                    n_ = min(4, nkt - g0)
                    for i_ in range(n_):
                        S.op("pe", lambda e, i_=i_, g0=g0, ptv=ptv, hp=hp: e.transpose(out=ptv[:, i_ * 128:(i_ + 1) * 128], in_=p_sb[hp][:, (g0 + i_) * 128:(g0 + i_ + 1) * 128], identity=ident_b[:]), r=[r_p[hp], r_idb], w=rpt)
                    evac("act" if (g0 // 4) % 2 == 0 else "dve", pT_sb[hp][:, g0 * 128:(g0 + n_) * 128], ptv[:, 0:n_ * 128], rpt, [r_pT[hp]])

            def l1P2b(t, h):
                pb = t % 2
                hp = h % 2
                nk = (t + 1) * 128
                stt = st1[hp]
                nkt = t + 1
                bol, rol = ring.get(1)
                for kt in range(nkt):
                    S.op("pe", lambda e, kt=kt, bol=bol, hp=hp: e.matmul(bank(bol)[:, 0:256], lhsT=pT_sb[hp][:, kt * 128:(kt + 1) * 128], rhs=ckvK[:, kt, :], start=(kt == 0), stop=(kt == nkt - 1)), r=[r_pT[hp], r_ckvK[kt]], w=rol)
                chs = [(c0_, min(nk, c0_ + 1024)) for c0_ in range(0, nk, 1024)]
                if len(chs) == 2:
                    S.op("dve", lambda e, stt=stt: e.tensor_tensor(out=stt[:, 6:7], in0=stt[:, 6:7], in1=stt[:, 7:8], op=ALU.add), r=[r_st1[hp]], w=[r_st1[hp]])
                S.op("dve", lambda e, stt=stt: e.reciprocal(out=stt[:, 3:4], in_=stt[:, 6:7]), r=[r_st1[hp]], w=[r_st1[hp]])
                S.op("dve", lambda e, bol=bol, stt=stt, hp=hp: e.tensor_scalar(out=ol_b[hp], in0=bank(bol)[:, 0:256], scalar1=stt[:, 3:4], scalar2=None, op0=ALU.mult), r=rol + [r_st1[hp]], w=[r_olb[hp]])
                bot, rot = ring.get(1)
                otv = bankbf(bot)[:, 0:256].rearrange("p (k t) -> p k t", k=2)
                for k in range(2):
                    S.op("pe", lambda e, k=k, otv=otv, hp=hp: e.transpose(out=otv[:, k, :], in_=ol_b[hp][:, k * 128:(k + 1) * 128], identity=ident_b[:]), r=[r_olb[hp], r_idb], w=rot)
                S.op("dve", lambda e, otv=otv, hp=hp: e.tensor_copy(out=olT[hp], in_=otv), r=rot, w=[r_olT[hp]])
                boh, roh = ring.get(1)
                for k in range(2):
                    S.op("pe", lambda e, k=k, boh=boh, h=h, hp=hp: e.matmul(bank(boh)[:, 0:128], lhsT=wuv_sb[:, k, h * 128:(h + 1) * 128], rhs=olT[hp][:, k, :], start=(k == 0), stop=(k == 1)), r=[r_wuv, r_olT[hp]], w=roh)
                evac("act" if h % 2 == 0 else "dve", oT1[pb][:, h, :], bank(boh)[:, 0:128], roh, [r_oT1[pb]])


            def l1out(t):
                pb = t % 2
                by, ry = ring.get(2)
                for hf in range(2):
                    for c in range(8):
                        S.op("pe", lambda e, hf=hf, c=c, by=by, pb=pb: e.matmul(bank(by + hf), lhsT=oT1[pb][:, c, :], rhs=wo1_sb[:, c, hf * 512:(hf + 1) * 512], start=(c == 0), stop=(c == 7)), r=[r_oT1[pb], r_wo1], w=[ry[hf]])
                S.op("dve", lambda e, by=by, pb=pb, t=t: e.scalar_tensor_tensor(out=z1[pb], in0=X[:, t, :], scalar=ALPHA, in1=bank(by, 2), op0=ALU.mult, op1=ALU.add), r=[rX[t]] + ry, w=[r_z1[pb]])
                layer_norm_tile(z1[pb], r_z1[pb], X[:, t, :], rX[t], 128, sc1[pb], r_sc1[pb], lnt1, r_lnt1)


            l1A(0)
            for t in range(NT):
                l1P1a(t, 0)
                l1P1b(t, 0)
                for h in range(1, 8):
                    l1P2a(t, h - 1)
                    l1P1a(t, h)
                    l1P1b(t, h)
                    l1P2b(t, h - 1)
                    if h == 4 and t + 1 < NT:
                        l1A(t + 1)
                l1P2a(t, 7)
                l1P2b(t, 7)
                l1out(t)
                chk(f'l1t{t}')

            def sample_l1():
                A.at(L1_BUF)
                NSTEP = NPAGES * 128 // 1024
                tb1, r_tb1 = A.get(128, F32, "tb1")
                cc1s = tb1[0:NS, 0:64]; ss1s = tb1[0:NS, 64:128]
                S.dma("sp", lambda e: e.dma_start(out=cc1s, in_=tdr["cc1s"]), w=[r_tb1])
                S.dma("sp", lambda e: e.dma_start(out=ss1s, in_=tdr["ss1s"]), w=[r_tb1])
                msk, r_msk = A.get(SB * NS, BF16, "msk")
                S.dma("pool", lambda e: e.dma_start(out=msk[0:32, :], in_=tdr["masksb"]), w=[r_msk])
                mskv = msk.rearrange("p (b t) -> p b t", b=SB)
                pt_sb, r_pt = A.get(SB, I32, "pt_sb")
                S.dma("sp", lambda e: e.dma_start(out=pt_sb, in_=ptab), w=[r_pt])
                xsT1, r_xsT1 = A.get(128, BF16, "xsT1"); xsT1 = xsT1.rearrange("p (k t) -> p k t", k=8)
                transpose_x(XS[:], r_XS, xsT1, r_xsT1, npart=NS)
                bcq, rcq = ring.get(1)
                bkv, rkv = ring.get(1)
                for k in range(8):
                    S.op("pe", lambda e, k=k: e.matmul(bank(bcq)[0:NS, :], lhsT=xsT1[:, k, :], rhs=wdq_sb[:, k, :], start=(k == 0), stop=(k == 7)), r=[r_xsT1, r_wdq], w=rcq)
                for k in range(8):
                    S.op("pe", lambda e, k=k: e.matmul(bank(bkv)[0:NS, 0:320], lhsT=xsT1[:, k, :], rhs=wdkv_sb[:, k, :], start=(k == 0), stop=(k == 7)), r=[r_xsT1, r_wdkv], w=rkv)
                jk, r_jk = A.get(512, F32, "jk")
                scq, r_scq = A.get(16, F32, "scq")
                cqns, r_cqns = A.get(512, BF16, "cqns")
                ckvsf, r_ckvsf = A.get(256, F32, "ckvsf")
                ckvsb, r_ckvsb = A.get(256, BF16, "ckvsb")
                kpet, r_kpets = A.get(128, F32, "kpets")
                kpesf, r_kpesf = A.get(64, F32, "kpesf")
                kpesb, r_kpesb = A.get(128, BF16, "kpesb")
                sc = scq[0:NS]
                S.op("act", lambda e: e.activation(out=jk[0:NS, :], in_=bank(bcq)[0:NS, :], func=AF.Square, accum_out=sc[:, 0:1]), r=rcq, w=[r_jk, r_scq])
                S.op("act", lambda e: e.activation(out=sc[:, 1:2], in_=sc[:, 0:1], func=AF.Sqrt, bias=eps_rms[0:NS], scale=1.0 / 512), r=[r_scq, r_eps], w=[r_scq])
                S.op("dve", lambda e: e.reciprocal(out=sc[:, 1:2], in_=sc[:, 1:2]), r=[r_scq], w=[r_scq])
                S.op("dve", lambda e: e.scalar_tensor_tensor(out=cqns[0:NS, :], in0=bank(bcq)[0:NS, :], scalar=sc[:, 1:2], in1=qng[0:NS], op0=ALU.mult, op1=ALU.mult), r=rcq + [r_scq, r_c1], w=[r_cqns])
                S.op("act", lambda e: e.activation(out=jk[0:NS, 0:256], in_=bank(bkv)[0:NS, 0:256], func=AF.Square, accum_out=sc[:, 2:3]), r=rkv, w=[r_jk, r_scq])
                S.op("act", lambda e: e.activation(out=sc[:, 3:4], in_=sc[:, 2:3], func=AF.Sqrt, bias=eps_rms[0:NS], scale=1.0 / 256), r=[r_scq, r_eps], w=[r_scq])
                S.op("dve", lambda e: e.reciprocal(out=sc[:, 3:4], in_=sc[:, 3:4]), r=[r_scq], w=[r_scq])
                S.op("dve", lambda e: e.scalar_tensor_tensor(out=ckvsf[0:NS, :], in0=bank(bkv)[0:NS, 0:256], scalar=sc[:, 3:4], in1=kvng[0:NS], op0=ALU.mult, op1=ALU.mult), r=rkv + [r_scq, r_c1], w=[r_ckvsf])
                out_toks.append(S.dma("sp", lambda e: e.dma_start(out=ckvs, in_=ckvsf[0:NS, :]), r=[r_ckvsf]))
                S.op("pool", lambda e: e.tensor_copy(out=ckvsb[0:NS, :], in_=ckvsf[0:NS, :]), r=[r_ckvsf], w=[r_ckvsb])
                S.op("dve", lambda e: e.tensor_tensor(out=kpet[0:NS, 0:64], in0=bank(bkv)[0:NS, 256:320], in1=cc1s, op=ALU.mult), r=rkv + [r_tb1], w=[r_kpets])
                S.op("dve", lambda e: e.tensor_tensor(out=kpet[0:NS, 64:96], in0=bank(bkv)[0:NS, 288:320], in1=ss1s[:, 0:32], op=ALU.mult), r=rkv + [r_tb1], w=[r_kpets])
                S.op("dve", lambda e: e.tensor_tensor(out=kpet[0:NS, 96:128], in0=bank(bkv)[0:NS, 256:288], in1=ss1s[:, 32:64], op=ALU.mult), r=rkv + [r_tb1], w=[r_kpets])
                S.op("pool", lambda e: e.tensor_tensor(out=kpesf[0:NS, :], in0=kpet[0:NS, 0:64], in1=kpet[0:NS, 64:128], op=ALU.add), r=[r_kpets], w=[r_kpesf])
                out_toks.append(S.dma("sp", lambda e: e.dma_start(out=kpes, in_=kpesf[0:NS, :]), r=[r_kpesf]))
                S.op("pool", lambda e: e.tensor_copy(out=kpesb[0:NS, 0:64], in_=kpesf[0:NS, :]), r=[r_kpesf], w=[r_kpesb])
                S.op("pool", lambda e: e.tensor_copy(out=kpesb[0:NS, 64:128], in_=kpesf[0:NS, :]), r=[r_kpesf], w=[r_kpesb])
                btr, rtr = ring.get(1)
                trv = bankbf(btr)
                for k in range(4):
                    S.op("pe", lambda e, k=k: e.transpose(out=trv[:, k * 16:(k + 1) * 16], in_=cqns[0:NS, k * 128:(k + 1) * 128], identity=ident_b[0:NS, 0:NS]), r=[r_cqns, r_idb], w=rtr)
                for k in range(2):
                    S.op("pe", lambda e, k=k: e.transpose(out=trv[:, 64 + k * 16:64 + (k + 1) * 16], in_=ckvsb[0:NS, k * 128:(k + 1) * 128], identity=ident_b[0:NS, 0:NS]), r=[r_ckvsb, r_idb], w=rtr)
                S.op("pe", lambda e: e.transpose(out=trv[:, 96:112], in_=kpesb[0:NS, :], identity=ident_b[0:NS, 0:NS]), r=[r_kpesb, r_idb], w=rtr)
                trs, r_trs = A.get(112, BF16, "trs")
                S.op("act", lambda e: e.copy(out=trs, in_=trv[:, 0:112]), r=rtr, w=[r_trs])
                cqnTs = trs[:, 0:64].rearrange("p (k t) -> p k t", k=4)
                ckvTs = trs[:, 64:96].rearrange("p (k t) -> p k t", k=2)
                kpeTs = trs[:, 96:112]
                bqn, rqn = ring.get(1)
                qnv = bank(bqn)[:, 0:128].rearrange("p (h t) -> p h t", h=8)
                for h in range(8):
                    for k in range(4):
                        S.op("pe", lambda e, h=h, k=k: e.matmul(qnv[:, h, :], lhsT=wuq_sb[:, k, h * 128:(h + 1) * 128], rhs=cqnTs[:, k, :], start=(k == 0), stop=(k == 3)), r=[r_wuq, r_trs], w=rqn)
                qnTs, r_qnTs = A.get(128, BF16, "qnTs"); qnTs = qnTs.rearrange("p (h t) -> p h t", h=8)
                S.op("act", lambda e: e.copy(out=qnTs, in_=qnv), r=rqn, w=[r_qnTs])
                bqp, rqp = ring.get(1)
                for k in range(4):
                    S.op("pe", lambda e, k=k: e.matmul(bank(bqp)[0:NS, :], lhsT=cqnTs[:, k, :], rhs=wuq_sb[:, k, 1024:1536], start=(k == 0), stop=(k == 3)), r=[r_wuq, r_trs], w=rqp)
                qpv = bank(bqp)[0:NS, :].rearrange("p (h d) -> p h d", h=8)
                qpt, r_qpt = A.get(1024, F32, "qpt")
                qA = qpt[0:NS, 0:512].rearrange("p (h d) -> p h d", h=8)
                qB = qpt[0:NS, 512:1024].rearrange("p (h d) -> p h d", h=8)
                S.op("dve", lambda e: e.tensor_tensor(out=qA, in0=qpv, in1=cc1s.unsqueeze(1).to_broadcast([NS, 8, 64]), op=ALU.mult), r=rqp + [r_tb1], w=[r_qpt])
                S.op("dve", lambda e: e.tensor_tensor(out=qB[:, :, 0:32], in0=qpv[:, :, 32:64], in1=ss1s[:, 0:32].unsqueeze(1).to_broadcast([NS, 8, 32]), op=ALU.mult), r=rqp + [r_tb1], w=[r_qpt])
                S.op("dve", lambda e: e.tensor_tensor(out=qB[:, :, 32:64], in0=qpv[:, :, 0:32], in1=ss1s[:, 32:64].unsqueeze(1).to_broadcast([NS, 8, 32]), op=ALU.mult), r=rqp + [r_tb1], w=[r_qpt])
                qpsb, r_qpsb = A.get(512, BF16, "qpsb")
                S.op("pool", lambda e: e.tensor_tensor(out=qpsb[0:NS, :], in0=qpt[0:NS, 0:512], in1=qpt[0:NS, 512:1024], op=ALU.add), r=[r_qpt], w=[r_qpsb])
                btq, rtq = ring.get(1)
                tqv = bankbf(btq)[:, 0:64].rearrange("p (a t) -> p a t", a=4)
                for a in range(4):
                    S.op("pe", lambda e, a=a: e.transpose(out=tqv[:, a, :], in_=qpsb[0:NS, a * 128:(a + 1) * 128], identity=ident_b[0:NS, 0:NS]), r=[r_qpsb, r_idb], w=rtq)
                qpTs, r_qpTs = A.get(SB * 32, BF16, "qpTs")
                S.op("pool", lambda e: e.memset(qpTs, 0.0), w=[r_qpTs])
                qpT5 = qpTs.rearrange("p (b a two l) -> p b a two l", b=4, a=4, two=2)
                tq4 = tqv.rearrange("p a (b l) -> p a b l", b=4)
                for bb in range(SB):
                    S.op("act", lambda e, bb=bb: e.copy(out=qpT5[0:64, bb, :, 0, :], in_=tq4[0:64, :, bb, :]), r=rtq, w=[r_qpTs])
                    S.op("dve", lambda e, bb=bb: e.tensor_copy(out=qpT5[64:128, bb, :, 1, :], in_=tq4[64:128, :, bb, :]), r=rtq, w=[r_qpTs])
                qpTb = qpTs.rearrange("p (b n) -> p b n", b=4)
                bql, rql = ring.get(1)
                qlv = bank(bql)[:, 0:256].rearrange("p (h k t) -> p h k t", h=8, k=2)
                for h in range(8):
                    for k in range(2):
                        S.op("pe", lambda e, h=h, k=k: e.matmul(qlv[:, h, k, :], lhsT=wukT_sb[:, h, k * 128:(k + 1) * 128], rhs=qnTs[:, h, :], start=True, stop=True), r=[r_wukT, r_qnTs], w=rql)
                qlTs, r_qlTs = A.get(2 * SB * 32, BF16, "qlTs")
                ql5 = qlTs.rearrange("p (k b h l) -> p k b h l", k=2, b=4, h=8)
                qlv5 = qlv.rearrange("p h k (b l) -> p h k b l", b=4)
                for k in range(2):
                    for bb in range(SB):
                        S.op("act" if (k + bb) % 2 == 0 else "dve",
                             (lambda e, k=k, bb=bb: e.copy(out=ql5[:, k, bb], in_=qlv5[:, :, k, bb, :])) if (k + bb) % 2 == 0 else
                             (lambda e, k=k, bb=bb: e.tensor_copy(out=ql5[:, k, bb], in_=qlv5[:, :, k, bb, :])), r=rql, w=[r_qlTs])
                qlTb = qlTs.rearrange("p (k b n) -> p k b n", k=2, b=4)
                gk = []; r_gk = []; gp = []; r_gp = []
                for i_ in range(3):
                    v_, r_ = A.get(2048, BF16, f"gk{i_}"); gk.append(v_.rearrange("p (s c) -> p s c", s=8)); r_gk.append(r_)
                    v_, r_ = A.get(512, BF16, f"gp{i_}"); gp.append(v_.rearrange("p (s c) -> p s c", s=8)); r_gp.append(r_)
                gp2, r_gp2 = dbl(1024, BF16, "gp2", "p (s c) -> p s c", s=8)
                ckTc, r_ckTc = dbl(2048, BF16, "ckTc", "p (k n) -> p k n", k=2)
                kpTc, r_kpTc = dbl(1024, BF16, "kpTc")
                p_s, r_ps = dbl(1024, BF16, "p_s")
                pTc, r_pTc = dbl(256, BF16, "pTc", "p (i t) -> p i t", i=8)
                Oacc, r_Oacc = dbl(256, F32, "Oacc")
                ms, r_ms = dbl(8, F32, "ms")
                olb, r_olb = A.get(256, BF16, "olb")
                olTs, r_olTs = A.get(64, BF16, "olTs"); olTs = olTs.rearrange("p (k n) -> p k n", k=2)
                oTs, r_oTs = A.get(128, BF16, "oTs"); oTs = oTs.rearrange("p (h t) -> p h t", h=8)
                print("sample l1 arena bytes", A.cur)
                sctx = {}
                def sI(bb):
                    mb_ = ms[bb % 2][0:32]
                    oa = Oacc[bb % 2][0:32]
                    r_m = r_ms[bb % 2]
                    r_o = r_Oacc[bb % 2]
                    bsn, rsn = ring.get(1)
                    sn = bank(bsn)[0:32, 0:NS]
                    S.op("pe", lambda e, bb=bb, sn=sn: e.matmul(sn, lhsT=qlTb[:, 0, bb, :], rhs=ckvTs[:, 0, :], start=True, stop=False), r=[r_qlTs, r_trs], w=rsn)
                    S.op("pe", lambda e, bb=bb, sn=sn: e.matmul(sn, lhsT=qlTb[:, 1, bb, :], rhs=ckvTs[:, 1, :], start=False, stop=False), r=[r_qlTs, r_trs], w=rsn)
                    S.op("pe", lambda e, bb=bb, sn=sn: e.matmul(sn, lhsT=qpTb[:, bb, :], rhs=kpeTs, start=False, stop=False), r=[r_qpTs, r_trs], w=rsn)
                    S.op("pe", lambda e, bb=bb, sn=sn: e.matmul(sn, lhsT=ident_b[0:32, 0:32], rhs=mskv[0:32, bb, :], start=False, stop=True), r=[r_idb, r_msk], w=rsn)
                    S.op("dve", lambda e, sn=sn, mb_=mb_: e.reduce_max(out=mb_[:, 0:1], in_=sn, axis=AX.X), r=rsn, w=[r_m])
                    S.op("dve", lambda e, mb_=mb_: e.tensor_scalar(out=mb_[:, 5:6], in0=mb_[:, 0:1], scalar1=-MLA_SCALE, scalar2=None, op0=ALU.mult), r=[r_m], w=[r_m])
                    psl = p_s[0][0:32]
                    S.op("act", lambda e, sn=sn, mb_=mb_, psl=psl: e.activation(out=psl[:, 0:NS], in_=sn, func=AF.Exp, bias=mb_[:, 5:6], scale=MLA_SCALE, accum_out=mb_[:, 1:2]), r=rsn + [r_m], w=[r_ps[0], r_m])
                    bpt, rpt = ring.get(1)
                    S.op("pe", lambda e, psl=psl, bpt=bpt: e.transpose(out=bankbf(bpt)[0:NS, 0:32], in_=psl[:, 0:NS], identity=ident_b[0:32, 0:32]), r=[r_ps[0], r_idb], w=rpt)
                    S.op("act", lambda e, bpt=bpt: e.copy(out=pTc[0][0:NS, 0, :], in_=bankbf(bpt)[0:NS, 0:32]), r=rpt, w=[r_pTc[0]])
                    bo_, ro_ = ring.get(1)
                    S.op("pe", lambda e, bo_=bo_: e.matmul(bank(bo_)[0:32, 0:256], lhsT=pTc[0][0:NS, 0, :], rhs=ckvsb[0:NS, :], start=True, stop=True), r=[r_pTc[0], r_ckvsb], w=ro_)
                    S.op("dve", lambda e, bo_=bo_, oa=oa: e.tensor_copy(out=oa, in_=bank(bo_)[0:32, 0:256]), r=ro_, w=[r_o])

                def sG(bb, stp, g3):
                    S.dma("pool", lambda e, g3=g3, bb=bb, stp=stp: e.indirect_dma_start(out=gk[g3].rearrange("p s c -> p (s c)"), out_offset=None, in_=cckv[:, :], element_offset=stp * 2048,
                                                                                         in_offset=bass.IndirectOffsetOnAxis(ap=pt_sb[:, bb:bb + 1], axis=0)), r=[r_pt], w=[r_gk[g3]])
                    S.dma("pool", lambda e, g3=g3, bb=bb, stp=stp: e.indirect_dma_start(out=gp[g3].rearrange("p s c -> p (s c)"), out_offset=None, in_=ckpe[:, :], element_offset=stp * 512,
                                                                                         in_offset=bass.IndirectOffsetOnAxis(ap=pt_sb[:, bb:bb + 1], axis=0)), r=[r_pt], w=[r_gp[g3]])

                def sD(bb, stp, gs, g3):
                    S.op("act", lambda e, gs=gs, g3=g3: e.copy(out=gp2[gs][:, :, 0:64], in_=gp[g3]), r=[r_gp[g3]], w=[r_gp2[gs]])
                    S.op("dve", lambda e, gs=gs, g3=g3: e.tensor_copy(out=gp2[gs][:, :, 64:128], in_=gp[g3]), r=[r_gp[g3]], w=[r_gp2[gs]])

                def sT(bb, stp, gs, g3):
                    for hf in range(2):
                        sl = hf * 4
                        bA, rA = ring.get(1)
                        bB, rB = ring.get(1)
                        bC, rC = ring.get(1)
                        for i_ in range(4):
                            S.op("pe", lambda e, i_=i_, sl=sl, g3=g3, bA=bA: e.transpose(out=bankbf(bA)[:, i_ * 128:(i_ + 1) * 128], in_=gk[g3][:, sl + i_, 0:128], identity=ident_b[:]), r=[r_gk[g3], r_idb], w=rA)
                        for i_ in range(4):
                            S.op("pe", lambda e, i_=i_, sl=sl, g3=g3, bB=bB: e.transpose(out=bankbf(bB)[:, i_ * 128:(i_ + 1) * 128], in_=gk[g3][:, sl + i_, 128:256], identity=ident_b[:]), r=[r_gk[g3], r_idb], w=rB)
                        for i_ in range(4):
                            S.op("pe", lambda e, i_=i_, sl=sl, gs=gs, bC=bC: e.transpose(out=bankbf(bC)[:, i_ * 128:(i_ + 1) * 128], in_=gp2[gs][:, sl + i_, :], identity=ident_b[:]), r=[r_gp2[gs], r_idb], w=rC)
                        cs_ = slice(hf * 512, (hf + 1) * 512)
                        S.op("act", lambda e, gs=gs, bA=bA, cs_=cs_: e.copy(out=ckTc[gs][:, 0, cs_], in_=bankbf(bA)[:, 0:512]), r=rA, w=[r_ckTc[gs]])
                        S.op("dve", lambda e, gs=gs, bB=bB, cs_=cs_: e.tensor_copy(out=ckTc[gs][:, 1, cs_], in_=bankbf(bB)[:, 0:512]), r=rB, w=[r_ckTc[gs]])
                        if hf == 0:
                            S.op("act", lambda e, gs=gs, bC=bC, cs_=cs_: e.copy(out=kpTc[gs][:, cs_], in_=bankbf(bC)[:, 0:512]), r=rC, w=[r_kpTc[gs]])
                        else:
                            S.op("dve", lambda e, gs=gs, bC=bC, cs_=cs_: e.tensor_copy(out=kpTc[gs][:, cs_], in_=bankbf(bC)[:, 0:512]), r=rC, w=[r_kpTc[gs]])

                def sU1(bb, stp, gs):
                    mb_ = ms[bb % 2][0:32]
                    oa = Oacc[bb % 2][0:32]
                    r_m = r_ms[bb % 2]
                    r_o = r_Oacc[bb % 2]
                    bs2, rs2 = ring.get(2)
                    sfull = bank(bs2, 2)[0:32, :]
                    for hf in range(2):
                        cs_ = slice(hf * 512, (hf + 1) * 512)
                        S.op("pe", lambda e, bb=bb, gs=gs, cs_=cs_, sfull=sfull: e.matmul(sfull[:, cs_], lhsT=qlTb[:, 0, bb, :], rhs=ckTc[gs][:, 0, cs_], start=True, stop=False), r=[r_qlTs, r_ckTc[gs]], w=[rs2[hf]])
                        S.op("pe", lambda e, bb=bb, gs=gs, cs_=cs_, sfull=sfull: e.matmul(sfull[:, cs_], lhsT=qlTb[:, 1, bb, :], rhs=ckTc[gs][:, 1, cs_], start=False, stop=False), r=[r_qlTs, r_ckTc[gs]], w=[rs2[hf]])
                        S.op("pe", lambda e, bb=bb, gs=gs, cs_=cs_, sfull=sfull: e.matmul(sfull[:, cs_], lhsT=qpTb[:, bb, :], rhs=kpTc[gs][:, cs_], start=False, stop=True), r=[r_qpTs, r_kpTc[gs]], w=[rs2[hf]])
                    S.op("dve", lambda e, sfull=sfull, mb_=mb_: e.reduce_max(out=mb_[:, 2:3], in_=sfull, axis=AX.X), r=rs2, w=[r_m])
                    S.op("dve", lambda e, mb_=mb_: e.tensor_tensor(out=mb_[:, 3:4], in0=mb_[:, 0:1], in1=mb_[:, 2:3], op=ALU.max), r=[r_m], w=[r_m])
                    S.op("dve", lambda e, mb_=mb_: e.tensor_tensor(out=mb_[:, 4:5], in0=mb_[:, 0:1], in1=mb_[:, 3:4], op=ALU.subtract), r=[r_m], w=[r_m])
                    S.op("act", lambda e, mb_=mb_: e.activation(out=mb_[:, 4:5], in_=mb_[:, 4:5], func=AF.Exp, scale=MLA_SCALE), r=[r_m], w=[r_m])
                    S.op("dve", lambda e, mb_=mb_: e.tensor_scalar(out=mb_[:, 5:6], in0=mb_[:, 3:4], scalar1=-MLA_SCALE, scalar2=None, op0=ALU.mult), r=[r_m], w=[r_m])
                    pq = gs
                    psl = p_s[pq][0:32]
                    S.op("act", lambda e, sfull=sfull, mb_=mb_, psl=psl: e.activation(out=psl, in_=sfull, func=AF.Exp, bias=mb_[:, 5:6], scale=MLA_SCALE, accum_out=mb_[:, 6:7]), r=rs2 + [r_m], w=[r_ps[pq], r_m])
                    S.op("dve", lambda e, mb_=mb_: e.scalar_tensor_tensor(out=mb_[:, 1:2], in0=mb_[:, 1:2], scalar=mb_[:, 4:5], in1=mb_[:, 6:7], op0=ALU.mult, op1=ALU.add), r=[r_m], w=[r_m])
                    S.op("dve", lambda e, mb_=mb_: e.tensor_copy(out=mb_[:, 0:1], in_=mb_[:, 3:4]), r=[r_m], w=[r_m])

                def sU2(bb, stp, gs, g3):
                    mb_ = ms[bb % 2][0:32]
                    oa = Oacc[bb % 2][0:32]
                    r_m = r_ms[bb % 2]
                    r_o = r_Oacc[bb % 2]
                    pq = gs
                    psl = p_s[pq][0:32]
                    bpt, rpt = ring.get(1)
                    ptv = bankbf(bpt)[:, 0:256].rearrange("p (i t) -> p i t", i=8)
                    for i_ in range(8):
                        S.op("pe", lambda e, i_=i_, psl=psl, ptv=ptv: e.transpose(out=ptv[:, i_, :], in_=psl[:, i_ * 128:(i_ + 1) * 128], identity=ident_b[0:32, 0:32]), r=[r_ps[pq], r_idb], w=rpt)
                    S.op("act", lambda e, ptv=ptv, pq=pq: e.copy(out=pTc[pq], in_=ptv), r=rpt, w=[r_pTc[pq]])
                    bo_, ro_ = ring.get(1)
                    for i_ in range(8):
                        S.op("pe", lambda e, i_=i_, bo_=bo_, pq=pq, g3=g3: e.matmul(bank(bo_)[0:32, 0:256], lhsT=pTc[pq][:, i_, :], rhs=gk[g3][:, i_, :], start=(i_ == 0), stop=(i_ == 7)), r=[r_pTc[pq], r_gk[g3]], w=ro_)
                    S.op("dve", lambda e, bo_=bo_, oa=oa, mb_=mb_: e.scalar_tensor_tensor(out=oa, in0=oa, scalar=mb_[:, 4:5], in1=bank(bo_)[0:32, 0:256], op0=ALU.mult, op1=ALU.add), r=ro_ + [r_o, r_m], w=[r_o])


                def sF(bb):
                    mb_ = ms[bb % 2][0:32]
                    oa = Oacc[bb % 2][0:32]
                    r_m = r_ms[bb % 2]
                    r_o = r_Oacc[bb % 2]
                    S.op("dve", lambda e, mb_=mb_: e.reciprocal(out=mb_[:, 7:8], in_=mb_[:, 1:2]), r=[r_m], w=[r_m])
                    S.op("act", lambda e, oa=oa, mb_=mb_: e.activation(out=olb[0:32, :], in_=oa, func=AF.Identity, bias=0.0, scale=mb_[:, 7:8]), r=[r_o, r_m], w=[r_olb])
                    bot, rot = ring.get(1)
                    otv = bankbf(bot)[:, 0:64].rearrange("p (k n) -> p k n", k=2)
                    for k in range(2):
                        S.op("pe", lambda e, k=k, otv=otv: e.transpose(out=otv[:, k, :], in_=olb[0:32, k * 128:(k + 1) * 128], identity=ident_b[0:32, 0:32]), r=[r_olb, r_idb], w=rot)
                    S.op("dve", lambda e, otv=otv: e.tensor_copy(out=olTs, in_=otv), r=rot, w=[r_olTs])
                    boh, roh = ring.get(1)
                    ohv = bank(boh)[:, 0:32].rearrange("p (h l) -> p h l", h=8)
                    for h in range(8):
                        for k in range(2):
                            S.op("pe", lambda e, h=h, k=k, ohv=ohv: e.matmul(ohv[:, h, :], lhsT=wuv_sb[:, k, h * 128:(h + 1) * 128], rhs=olTs[:, k, h * 4:(h + 1) * 4], start=(k == 0), stop=(k == 1)), r=[r_wuv, r_olTs], w=roh)
                    S.op("act", lambda e, ohv=ohv, bb=bb: e.copy(out=oTs[:, :, bb * 4:bb * 4 + 4], in_=ohv), r=roh, w=[r_oTs])

                NTOT = SB * NSTEP
                sG(0, 0, 0)
                sG(0, 1, 1)
                sD(0, 0, 0, 0)
                sT(0, 0, 0, 0)
                for idx in range(NTOT):
                    bb, stp = divmod(idx, NSTEP)
                    if idx + 2 < NTOT:
                        b3, s3 = divmod(idx + 2, NSTEP)
                        sG(b3, s3, (idx + 2) % 3)
                    if stp == 0:
                        sI(bb)
                    if idx + 1 < NTOT:
                        sD(0, 0, (idx + 1) % 2, (idx + 1) % 3)
                    sU1(bb, stp, idx % 2)
                    if idx + 1 < NTOT:
                        b2, s2 = divmod(idx + 1, NSTEP)
                        sT(b2, s2, (idx + 1) % 2, (idx + 1) % 3)
                    sU2(bb, stp, idx % 2, idx % 3)
                    if stp == NSTEP - 1:
                        sF(bb)
                by_, ry_ = ring.get(2)
                for hf in range(2):
                    for c in range(8):
                        S.op("pe", lambda e, hf=hf, c=c: e.matmul(bank(by_ + hf)[0:NS, :], lhsT=oTs[:, c, :], rhs=wo1_sb[:, c, hf * 512:(hf + 1) * 512], start=(c == 0), stop=(c == 7)), r=[r_oTs, r_wo1], w=[ry_[hf]])
                zs1, r_zs1 = A.get(D, F32, "zs1")
                scs1, r_scs1 = A.get(16, F32, "scs1")
                S.op("dve", lambda e: e.scalar_tensor_tensor(out=zs1[0:NS, :], in0=XS[:], scalar=ALPHA, in1=bank(by_, 2)[0:NS, :], op0=ALU.mult, op1=ALU.add), r=[r_XS] + ry_, w=[r_zs1])
                layer_norm_tile(zs1[0:NS, :], r_zs1, XS[:], r_XS, NS, scs1, r_scs1, lnt1, r_lnt1)

            S.set_phase('s1')
            if with_sample:
                sample_l1()
            chk('l1')
            S.set_phase('ffn1')
            ffn(1, True)


        except _Stop:
            print('stopped at', stop)
        S.finish(out_toks)
        S.emit()
        print("ops", S.stats())
    return nc


_CACHE = {}


def _prep_shared(inp, tabs):
    f = lambda a: np.ascontiguousarray(np.asarray(a, dtype=np.float32))
    sh = {}
    sh["w_in"] = f(inp["w_in_even"][0])
    sh["pool_w"] = f(np.transpose(inp["pool_w"][0], (1, 0, 2)))
    sh["pool_scale"] = f(inp["pool_scale"][0].reshape(4, 128).T)
    sh["gn_g"] = f(inp["ret_gn_g"][0])
    sh["w_o0"] = f(inp["w_o_even"][0])
    sh["w_dq"] = f(inp["w_dq"][0])
    sh["qn_g"] = f(inp["q_norm_g"][0])
    wuq = np.asarray(inp["w_uq"][0]).reshape(512, 8, 192)
    sh["w_uq"] = f(np.concatenate([wuq[:, :, :128].reshape(512, 1024), wuq[:, :, 128:].reshape(512, 512)], axis=1))
    sh["w_dkv"] = f(inp["w_dkv"][0])
    sh["kvn_g"] = f(inp["kv_norm_g"][0])
    sh["w_ukT"] = f(np.transpose(inp["w_uk"][0], (2, 1, 0)))
    sh["w_uv"] = f(np.asarray(inp["w_uv"][0]).reshape(256, 1024))
    sh["w_o1"] = f(inp["w_o_mla"][0])
    wup = np.asarray(inp["w_up"])
    a = wup[:, :, :DFF].reshape(DEPTH, 8, 128, NJ, 128)
    b = wup[:, :, DFF:].reshape(DEPTH, 8, 128, NJ, 128)
    ab = np.concatenate([a, b], axis=4)
    sh["w_up"] = f(np.transpose(ab, (0, 3, 2, 1, 4)))
    sh["w_down"] = f(inp["w_down"])
    sh["conv_w"] = f(np.transpose(np.asarray(inp["conv_w"]).reshape(DEPTH, 3, NJ, 128), (0, 3, 2, 1)))
    sh["conv_b"] = f(np.transpose(np.asarray(inp["conv_b"]).reshape(DEPTH, NJ, 128), (0, 2, 1)))
    for k in ("ln_mix_g", "ln_mix_b", "ln_ffn_g", "ln_ffn_b"):
        sh[k] = f(inp[k])
    sh["cckv"] = np.asarray(inp["cache_ckv"], dtype=np.float32).reshape(5120, 128 * 256)
    sh["ckpe"] = np.asarray(inp["cache_kpe"], dtype=np.float32).reshape(5120, 128 * 64)
    for k, v in tabs.items():
        if isinstance(v, np.ndarray):
            sh["t_" + k] = v
    return sh


def kernel(**inp):
    tabs = host_tables()
    if "nc" not in _CACHE:
        _CACHE["nc"] = build_program(tabs)
    nc = _CACHE["nc"]
    sh = _prep_shared(inp, tabs)
    f = lambda a: np.ascontiguousarray(np.asarray(a, dtype=np.float32))
    in_maps = []
    for c in range(8):
        m = dict(sh)
        bs = slice(SB * c, SB * (c + 1))
        m["xp"] = f(inp["x_prompt"][c])
        m["xs"] = f(np.asarray(inp["x_sample"][bs]).reshape(NS, D))
        m["spool"] = f(np.asarray(inp["state_pool"][0, bs]).reshape(SB * 15, 512))
        m["sret"] = f(np.asarray(inp["state_ret"][0, bs]).reshape(SB * RH, 128, 128))
        m["sconv"] = f(np.asarray(inp["state_conv"][:, bs]).reshape(DEPTH, SB * 2, DFF))
        m["ptab"] = np.ascontiguousarray(np.asarray(inp["page_table"][bs]).T.astype(np.int32))
        in_maps.append(m)
    res = run_bass_kernel_spmd(nc, in_maps, core_ids=list(range(8)))
    R = res.results
    cat = lambda k: np.stack([np.asarray(R[c][k]) for c in range(8)])
    y_p = cat("yp")
    y_s = cat("ys").reshape(32, SL, D)
    pool_p = cat("poolp")[None]
    pool_s = cat("pools").reshape(1, 32, 15, 512)
    ret_p = cat("retp")[None]
    ret_s = cat("rets").reshape(1, 32, RH, 128, 128)
    ckv_p = cat("ckvp")[None]
    ckv_s = cat("ckvs").reshape(1, 32, SL, 256)
    kpe_p = cat("kpep")[None]
    kpe_s = cat("kpes").reshape(1, 32, SL, 64)
    conv_p = np.transpose(cat("convp"), (1, 0, 2, 3))
    conv_s = np.transpose(cat("convs").reshape(8, DEPTH, SB, 2, DFF), (1, 0, 2, 3, 4)).reshape(DEPTH, 32, 2, DFF)
    outs = (y_p, y_s, pool_p, pool_s, ret_p, ret_s, ckv_p, ckv_s, kpe_p, kpe_s, conv_p, conv_s)
    return tuple(np.ascontiguousarray(o.astype(np.float32)) for o in outs)
```

```python
import contextlib
import math
import numpy as np
import concourse.bass as bass
import concourse.mybir as mybir
from concourse.bass_utils import run_bass_kernel_spmd

F32 = mybir.dt.float32
BF16 = mybir.dt.bfloat16
I32 = mybir.dt.int32
AF = mybir.ActivationFunctionType
ALU = mybir.AluOpType
AX = mybir.AxisListType

ALL_Q = ("pe", "act", "dve", "pool", "sp")
SAME_ENGINE_SYNC = True

D = 1024
SEQ = 2048
NT = SEQ // 128
DEPTH = 2
NS = 16
SB = 4
SL = 4
PAST = 16384
NPAGES = 128
POOL_W = (2, 4, 8, 16)
RH = 4
DFF = 2816
NJ = DFF // 128
ALPHA = (2.0 * DEPTH) ** 0.25
LN_EPS = 1e-5
RMS_EPS = 1e-6
GN_EPS = 1e-6
MLA_SCALE = (128 + 64) ** -0.5
KSCALE = 128 ** -0.5
NEG = -30000.0


class Res:
    __slots__ = ("name", "w", "r")

    def __init__(self, name):
        self.name = name
        self.w = None
        self.r = []


class Op:
    __slots__ = ("q", "fn", "waits", "signal", "dma", "idx", "deps", "sdep", "succ", "nrem", "ready", "cost", "lat", "fin", "orig", "done", "phase")

    def __init__(self, q, fn, dma=None):
        self.q = q
        self.fn = fn
        self.waits = []
        self.signal = False
        self.dma = dma
        self.idx = None
        self.deps = []
        self.sdep = None
        self.succ = []
        self.nrem = 0
        self.ready = 0.0
        self.cost = 0.1
        self.lat = 0.0
        self.fin = 0.0
        self.orig = 0
        self.done = False
        self.phase = 0


class _Rec:
    def __init__(self):
        self.info = ("", None)

    def __getattr__(self, name):
        def f(*a, **k):
            out = k.get("out", a[0] if a else None)
            object.__setattr__(self, "info", (name, out))
            return self
        return f


RESCHED_PHASES = {"l0", "ffn0", "ffn1"}
KEEP_Q = ()


class Sched:
    def __init__(self, nc, stack):
        self.nc = nc
        self.stack = stack
        self.all = []
        self.ops = {q: [] for q in ALL_Q}
        self.dma_sems = {}
        self.dma_last = {}
        self.eng_sems = {}
        self.nres = 0
        self.phase = 0
        self.phase_names = ['init']

    def res(self, name=None):
        self.nres += 1
        return Res(name or f"r{self.nres}")

    def set_phase(self, name):
        self.phase_names.append(name)
        self.phase = len(self.phase_names) - 1

    def _cost(self, o):
        try:
            rec = _Rec()
            o.fn(rec)
            name, out = rec.info
            shp = list(out.shape)
            n = 1
            for d in shp[1:]:
                n *= d
        except Exception:
            name, n = "", 256
        q = o.q
        if o.dma is not None:
            o.cost = 1.2 if q == "pool" else 0.15
            o.lat = 2.5 + n * shp[0] * 3.0 / 250e3 if name else 3.0
        elif q == "pe":
            o.cost = 0.05 + n / 1900.0
        elif q == "act":
            o.cost = 0.22 + n / 1000.0
        elif q == "dve":
            o.cost = 0.2 + n / 900.0
        else:
            o.cost = 0.35 + n / 450.0

    def _deps(self, o, r, w, tok):
        seen = set()
        for res in r:
            if res.w is not None and id(res.w[-1]) not in seen:
                seen.add(id(res.w[-1])); o.deps.append(res.w)
        for res in w:
            if res.w is not None and id(res.w[-1]) not in seen:
                seen.add(id(res.w[-1])); o.deps.append(res.w)
            for t in res.r:
                if id(t[-1]) not in seen:
                    seen.add(id(t[-1])); o.deps.append(t)
        for res in r:
            res.r.append(tok)
        for res in w:
            res.w = tok
            res.r = []

    def op(self, q, fn, r=(), w=()):
        o = Op(q, fn)
        o.orig = len(self.all)
        o.phase = self.phase
        self._deps(o, r, w, ("eng", o))
        self.all.append(o)
        return o

    def dma(self, q, fn, r=(), w=(), semkey=None):
        res0 = (list(w) + list(r))[0] if (w or r) else None
        key = semkey if semkey is not None else id(res0)
        if key not in self.dma_sems:
            self.dma_sems[key] = [None, 0]
        ent = self.dma_sems[key]
        ent[1] += 16
        o = Op(q, fn, dma=(key, ent[1]))
        o.orig = len(self.all)
        o.phase = self.phase
        o.sdep = self.dma_last.get(key)
        self.dma_last[key] = o
        tok = ("dma", key, ent[1], o)
        self._deps(o, r, w, tok)
        self.all.append(o)
        return tok

    def finish(self, out_tokens):
        o = Op("sp", None)
        o.orig = len(self.all)
        o.phase = self.phase
        seen = set()
        for t in out_tokens:
            if id(t[-1]) not in seen:
                seen.add(id(t[-1])); o.deps.append(t)
        self.all.append(o)

    def _schedule(self):
        ops = self.all
        for o in ops:
            if o.fn is not None:
                self._cost(o)
            dset = set(id(t[-1]) for t in o.deps)
            preds = [(t[-1], True) for t in o.deps]
            if o.sdep is not None and id(o.sdep) not in dset:
                preds.append((o.sdep, False))
            o.nrem = len(preds)
            for p, kind in preds:
                p.succ.append((o, kind))
        tfree = {q: 0.0 for q in ALL_Q}

        def place(o, st):
            self.ops[o.q].append(o)
            tfree[o.q] = st + o.cost
            o.fin = st + o.cost + o.lat
            o.done = True
            for s, kind in o.succ:
                t = o.fin if kind else st
                if t > s.ready:
                    s.ready = t
                s.nrem -= 1

        nph = len(self.phase_names)
        byph = [[] for _ in range(nph)]
        for o in ops:
            byph[o.phase].append(o)
        for ph in range(nph):
            plist = byph[ph]
            if self.phase_names[ph] not in RESCHED_PHASES:
                for o in plist:
                    place(o, max(tfree[o.q], o.ready))
                continue
            pending = set(id(o) for o in plist)
            keepq = KEEP_Q if self.phase_names[ph] in ("l1", "s1") else ()
            nxt = {q: [o for o in plist if o.q == q] for q in keepq}
            nxt_i = {q: 0 for q in keepq}
            ready = {q: [] for q in ALL_Q}
            inready = set()
            for o in plist:
                if o.nrem == 0:
                    ready[o.q].append(o); inready.add(id(o))
            nleft = len(plist)
            while nleft:
                best = None
                for q in ALL_Q:
                    lst = ready[q]
                    if not lst:
                        continue
                    tf = tfree[q]
                    cand = None
                    if q in keepq:
                        o = nxt[q][nxt_i[q]]
                        if o in lst:
                            st = o.ready if o.ready > tf else tf
                            cand = ((st, o.orig), o)
                    else:
                        for o in lst:
                            st = o.ready if o.ready > tf else tf
                            key = (st, o.orig)
                            if cand is None or key < cand[0]:
                                cand = (key, o)
                    if cand is not None and (best is None or cand[0] < best[0]):
                        best = cand
                (st, _), o = best
                ready[o.q].remove(o)
                if o.q in keepq:
                    nxt_i[o.q] += 1
                place(o, st)
                nleft -= 1
                for s, kind in o.succ:
                    if s.nrem == 0 and id(s) in pending and id(s) not in inready and not s.done:
                        ready[s.q].append(s); inready.add(id(s))

    def _add_wait(self, op, tok, waited, dwaited):
        q = op.q
        if tok[0] == "eng":
            src = tok[1]
            sq, sidx = src.q, src.idx
            if sq == q and (q == "pe" or q == "sp" or not SAME_ENGINE_SYNC):
                return
            if waited[q].get(sq, -1) >= sidx:
                return
            waited[q][sq] = sidx
            op.waits.append(("eng", sq, sidx))
            src.signal = True
        else:
            _, key, val, _src = tok
            if dwaited[q].get(key, -1) >= val:
                return
            dwaited[q][key] = val
            op.waits.append(("dma", key, val))

    def emit(self):
        nc = self.nc
        st = self.stack
        self._schedule()
        waited = {q: {} for q in ALL_Q}
        dwaited = {q: {} for q in ALL_Q}
        for q in ALL_Q:
            for i, o in enumerate(self.ops[q]):
                o.idx = i
        for q in ALL_Q:
            for o in self.ops[q]:
                best = {}
                for t in o.deps:
                    if t[0] == "eng":
                        k = ("eng", t[1].q)
                        v = t[1].idx
                    else:
                        k = ("dma", t[1])
                        v = t[2]
                    if k not in best or best[k][0] < v:
                        best[k] = (v, t)
                for v, t in best.values():
                    self._add_wait(o, t, waited, dwaited)
        for q in ALL_Q:
            self.eng_sems[q] = st.enter_context(nc.semaphore(f"es_{q}"))
        for i, key in enumerate(self.dma_sems):
            self.dma_sems[key][0] = st.enter_context(nc.semaphore(f"ds_{i}"))
        cnt = {}
        for q in ALL_Q:
            c = 0
            arr = []
            for o in self.ops[q]:
                if o.signal:
                    c += 1
                arr.append(c)
            cnt[q] = arr
        block = st.enter_context(nc.Block())

        def run(q, eng):
            for o in self.ops[q]:
                for t in o.waits:
                    if t[0] == "eng":
                        eng.wait_ge(self.eng_sems[t[1]], cnt[t[1]][t[2]])
                    else:
                        eng.wait_ge(self.dma_sems[t[1]][0], t[2])
                if o.fn is None:
                    continue
                ins = o.fn(eng)
                if o.dma is not None:
                    ins.then_inc(self.dma_sems[o.dma[0]][0], 16)
                elif o.signal:
                    ins.then_inc(self.eng_sems[q], 1)

        @block.tensor
        def _(eng):
            run("pe", eng)

        @block.scalar
        def _(eng):
            run("act", eng)

        @block.vector
        def _(eng):
            run("dve", eng)

        @block.gpsimd
        def _(eng):
            run("pool", eng)

        @block.sync
        def _(eng):
            run("sp", eng)

    def stats(self):
        return {q: (len(self.ops[q]), sum(len(o.waits) for o in self.ops[q])) for q in ALL_Q}


def _rope_tables(half, pos):
    inv = (np.float32(10000.0) ** (-(np.arange(half, dtype=np.float32)) / np.float32(half))).astype(np.float32)
    ang = (pos.astype(np.float32)[:, None] * inv[None, :]).astype(np.float32)
    return np.cos(ang).astype(np.float32), np.sin(ang).astype(np.float32)


def host_tables():
    t = {}
    pos_p = np.arange(SEQ, dtype=np.float32)
    pos_s = (PAST + np.arange(SL)).astype(np.float32)
    c, s = _rope_tables(64, pos_p)
    t["c2p"] = np.ascontiguousarray(np.concatenate([c, c], 1).T)
    t["ssp"] = np.ascontiguousarray(np.concatenate([s, -s], 1).T)
    c, s = _rope_tables(64, pos_s)
    t["c2s"] = np.ascontiguousarray(np.tile(np.concatenate([c, c], 1).T, (1, SB)))
    t["sss"] = np.ascontiguousarray(np.tile(np.concatenate([s, -s], 1).T, (1, SB)))
    c, s = _rope_tables(32, pos_p)
    cc = np.concatenate([c, c], 1).reshape(NT, 128, 64).transpose(1, 0, 2)
    ss = np.concatenate([-s, s], 1).reshape(NT, 128, 64).transpose(1, 0, 2)
    t["cc1"] = np.ascontiguousarray(cc)
    t["ss1"] = np.ascontiguousarray(ss)
    c, s = _rope_tables(32, pos_s)
    t["cc1s"] = np.ascontiguousarray(np.tile(np.concatenate([c, c], 1), (SB, 1)))
    t["ss1s"] = np.ascontiguousarray(np.tile(np.concatenate([-s, s], 1), (SB, 1)))
    lg = np.log(np.float32(1.0) - np.float32(2.0) ** (np.float32(-5.0) - np.arange(RH, dtype=np.float32))).astype(np.float32)
    for name, C in (("p", 128), ("s", SL)):
        idx = np.arange(C, dtype=np.float32)
        diff = idx[:, None] - idx[None, :]
        dec = np.where(diff >= 0, np.exp(np.maximum(diff, 0.0)[None] * lg[:, None, None]), 0.0).astype(np.float32)
        t["decT" + name] = np.ascontiguousarray((dec * np.float32(KSCALE)).transpose(2, 0, 1))
        qd = np.exp((idx + 1.0)[:, None] * lg[None, :]).astype(np.float32)
        t["qdec" + name] = np.ascontiguousarray(np.broadcast_to(qd.T[None], (128, RH, C)))
        kd = np.exp((C - 1.0 - idx)[:, None] * lg[None, :]).astype(np.float32) * np.float32(KSCALE)
        t["kdec" + name] = np.ascontiguousarray(kd)
        t["gC" + name] = [float(np.exp(np.float32(C) * lg[h])) for h in range(RH)]
    ic = np.zeros((128, 4, 15), np.float32)
    for g, w in enumerate(POOL_W):
        ic[:, g, :] = 1.0 / np.minimum(float(w), np.arange(15) + 1.0)
    t["invcnt"] = ic
    qi = np.arange(128)
    t["maskd"] = np.where(qi[None, :] <= qi[:, None], 0.0, NEG).astype(np.float32)
    t["ident"] = np.eye(128, dtype=np.float32)
    r = np.arange(32) % 4
    t["masks"] = np.where(np.arange(4)[None, :] <= r[:, None], 0.0, NEG).astype(np.float32)
    mb = np.full((32, SB, NS), NEG, np.float32)
    for bb in range(SB):
        for l2 in range(SL):
            mb[:, bb, bb * SL + l2] = np.where(l2 <= r, 0.0, NEG)
    t["masksb"] = np.ascontiguousarray(mb.reshape(32, SB * NS))
    return t


ARENA_BYTES = 142144


class PsumRing:
    def __init__(self, S):
        self.res = [S.res(f"bank{i}") for i in range(8)]
        self.nxt = 0

    def get(self, n=1):
        if n == 2 and self.nxt % 2:
            self.nxt = (self.nxt + 1) % 8
        if n == 4 and self.nxt % 4:
            self.nxt = (self.nxt + 4 - self.nxt % 4) % 8
        b = self.nxt
        self.nxt = (self.nxt + n) % 8
        return b, [self.res[b + i] for i in range(n)]


class Arena:
    def __init__(self, S, buf):
        self.S = S
        self.buf = buf
        self.hist = []
        self.cur = 0

    def at(self, off):
        assert off % 4 == 0
        self.cur = off

    def get(self, nelem, dt=BF16, name=None, nres=None):
        size = 2 if dt == BF16 else 4
        nb = (nelem * size + 3) // 4 * 4
        s, e = self.cur, self.cur + nb
        assert e <= ARENA_BYTES, (name, e)
        self.cur = e
        pend = []
        for (s2, e2, r2) in self.hist:
            if s2 < e and s < e2:
                if r2.w is not None:
                    pend.append(r2.w)
                pend.extend(r2.r)
        seen = set()
        p2 = []
        for t in pend:
            if id(t[-1]) not in seen:
                seen.add(id(t[-1])); p2.append(t)
        pend = p2
        rs = []
        for i in range(nres or 1):
            r = self.S.res(f"{name}_{i}")
            r.r = list(pend)
            self.hist.append((s, e, r))
            rs.append(r)
        v = self.buf[:, s // 2:s // 2 + nb // 2]
        if dt != BF16:
            v = v.bitcast(dt)
        return v[:, 0:nelem], (rs if nres else rs[0])


class _Stop(Exception):
    pass


def build_program(tabs, with_sample=True, stop=None):
    nc = bass.Bass("TRN2", target_bir_lowering=False)

    def din(name, shape, dt=F32):
        return nc.dram_tensor(name, list(shape), dt, kind="ExternalInput").ap()

    def dout(name, shape):
        return nc.dram_tensor(name, list(shape), F32, kind="ExternalOutput").ap()

    xp = din("xp", [SEQ, D])
    xs = din("xs", [NS, D])
    spool = din("spool", [SB * 15, 512])
    sret = din("sret", [SB * RH, 128, 128])
    sconv = din("sconv", [DEPTH, SB * 2, DFF])
    ptab = din("ptab", [128, SB], I32)
    if with_sample:
        cckv = din("cckv", [5120, 128 * 256])
        ckpe = din("ckpe", [5120, 128 * 64])
    w_in = din("w_in", [D, 2560])
    pool_w = din("pool_w", [128, 4, 128])
    pool_scale = din("pool_scale", [128, 4])
    gn_g = din("gn_g", [512])
    w_o0 = din("w_o0", [D, D])
    w_dq = din("w_dq", [D, 512])
    qn_g = din("qn_g", [512])
    w_uq = din("w_uq", [512, 1536])
    w_dkv = din("w_dkv", [D, 320])
    kvn_g = din("kvn_g", [256])
    w_ukT = din("w_ukT", [128, 8, 256])
    w_uv = din("w_uv", [256, 1024])
    w_o1 = din("w_o1", [D, D])
    w_up = din("w_up", [DEPTH, NJ, 128, 8, 256])
    w_down = din("w_down", [DEPTH, DFF, D])
    conv_w = din("conv_w", [DEPTH, 128, NJ, 3])
    conv_b = din("conv_b", [DEPTH, 128, NJ])
    ln_mix_g = din("ln_mix_g", [DEPTH, D])
    ln_mix_b = din("ln_mix_b", [DEPTH, D])
    ln_ffn_g = din("ln_ffn_g", [DEPTH, D])
    ln_ffn_b = din("ln_ffn_b", [DEPTH, D])
    tdr = {k: din("t_" + k, v.shape) for k, v in tabs.items() if isinstance(v, np.ndarray)}

    yp = dout("yp", [SEQ, D])
    ys = dout("ys", [NS, D])
    poolp = dout("poolp", [15, 512])
    pools = dout("pools", [SB * 15, 512])
    retp = dout("retp", [RH, 128, 128])
    rets = dout("rets", [SB * RH, 128, 128])
    ckvp = dout("ckvp", [SEQ, 256])
    ckvs = dout("ckvs", [NS, 256])
    kpep = dout("kpep", [SEQ, 64])
    kpes = dout("kpes", [NS, 64])
    convp = dout("convp", [DEPTH, 2, DFF])
    convs = dout("convs", [DEPTH, SB * 2, DFF])

    out_toks = []
    st = contextlib.ExitStack()
    with st:
        S = Sched(nc, st)
        _n = [0]

        def sb(shape, dt, name):
            return st.enter_context(nc.sbuf_tensor(name, list(shape), dt))

        P = st.enter_context(nc.psum_tensor("P", [128, 8, 512], F32))
        ring = PsumRing(S)

        def bank(b, n=1):
            return P[:, b:b + n, :].rearrange("p a b -> p (a b)")

        def bankbf(b, n=1):
            return P[:, b:b + n, :].rearrange("p a b -> p (a b)").bitcast(BF16)

        X = sb([128, NT, D], F32, "X")
        rX = [S.res(f"X{t}") for t in range(NT)]
        BIG = sb([128, ARENA_BYTES // 2], BF16, "BIG")
        A = Arena(S, BIG)
        ident_f = sb([128, 128], F32, "ident_f"); r_idf = S.res()
        ident_b = sb([128, 128], BF16, "ident_b"); r_idb = S.res()
        maskd = sb([128, 128], BF16, "maskd"); r_maskd = S.res()
        eps_t = sb([128, 4], F32, "eps_t"); r_eps = S.res()
        XS = sb([NS, D], F32, "XS"); r_XS = S.res()
        S.dma("sp", lambda e: e.dma_start(out=ident_f[:], in_=tdr["ident"]), w=[r_idf])
        S.dma("pool", lambda e: e.dma_start(out=ident_b[:], in_=tdr["ident"]), w=[r_idb])
        S.dma("pool", lambda e: e.dma_start(out=maskd[:], in_=tdr["maskd"]), w=[r_maskd])
        S.op("pool", lambda e: e.memset(eps_t[:, 0:1], LN_EPS), w=[r_eps])
        S.op("pool", lambda e: e.memset(eps_t[:, 1:2], RMS_EPS), w=[r_eps])
        S.op("pool", lambda e: e.memset(eps_t[:, 2:3], GN_EPS), w=[r_eps])
        eps_ln, eps_rms, eps_gn = eps_t[:, 0:1], eps_t[:, 1:2], eps_t[:, 2:3]

        def evac(q, out_ap, in_ap, r, w):
            if q == "act":
                S.op("act", lambda e: e.copy(out=out_ap, in_=in_ap), r=r, w=w)
            else:
                S.op(q, lambda e: e.tensor_copy(out=out_ap, in_=in_ap), r=r, w=w)

        def load_ln(lnt, r_lnt, g_ap, b_ap):
            S.dma("sp", lambda e: e.dma_start(out=lnt[:, 0:D], in_=g_ap.partition_broadcast(128)), w=[r_lnt])
            S.dma("sp", lambda e: e.dma_start(out=lnt[:, D:2 * D], in_=b_ap.partition_broadcast(128)), w=[r_lnt])

        def layer_norm_tile(z_ap, z_res, out_ap, out_res, npart, scr, scr_res, lnt, r_lnt):
            for c in range(2):
                S.op("dve", lambda e, c=c: e.bn_stats(out=scr[0:npart, c * 6:(c + 1) * 6], in_=z_ap[:, c * 512:(c + 1) * 512]), r=[z_res], w=[scr_res])
            mv = scr[0:npart, 12:14]
            S.op("dve", lambda e: e.bn_aggr(out=mv, in_=scr[0:npart, 0:12]), r=[scr_res], w=[scr_res])
            sd = scr[0:npart, 14:15]
            S.op("act", lambda e: e.activation(out=sd, in_=scr[0:npart, 13:14], func=AF.Sqrt, bias=eps_ln[0:npart], scale=1.0), r=[scr_res, r_eps], w=[scr_res])
            S.op("dve", lambda e: e.reciprocal(out=sd, in_=sd), r=[scr_res], w=[scr_res])
            nmr = scr[0:npart, 15:16]
            S.op("dve", lambda e: e.tensor_scalar(out=nmr, in0=scr[0:npart, 12:13], scalar1=sd, scalar2=-1.0, op0=ALU.mult, op1=ALU.mult), r=[scr_res], w=[scr_res])
            S.op("act", lambda e: e.activation(out=z_ap, in_=z_ap, func=AF.Identity, bias=nmr, scale=sd), r=[scr_res, z_res], w=[z_res])
            S.op("pool", lambda e: e.tensor_tensor(out=z_ap, in0=z_ap, in1=lnt[0:npart, 0:D], op=ALU.mult), r=[z_res, r_lnt], w=[z_res])
            S.op("pool", lambda e: e.tensor_tensor(out=out_ap, in0=z_ap, in1=lnt[0:npart, D:2 * D], op=ALU.add), r=[z_res, r_lnt], w=[out_res])

        def transpose_x(src_ap, src_res, dst_ap, dst_res, npart=128):
            for half in range(2):
                b, br = ring.get(1)
                pv = bank(b)[:, 0:4 * npart].rearrange("p (a b) -> p a b", a=4)
                for c in range(4):
                    cc = half * 4 + c
                    S.op("pe", lambda e, c=c, cc=cc, pv=pv: e.transpose(out=pv[:, c, :], in_=src_ap[:, cc * 128:(cc + 1) * 128], identity=ident_f[0:npart, 0:npart]),
                         r=[src_res, r_idf], w=br)
                evac("act" if half == 0 else "dve", dst_ap[:, half * 4:half * 4 + 4, :], pv, br, [dst_res])

        def chk(tag):
            if stop == tag:
                raise _Stop()

        try:
            for t in range(NT):
                S.dma("sp", lambda e, t=t: e.dma_start(out=X[:, t, :], in_=xp[t * 128:(t + 1) * 128, :]), w=[rX[t]])
            S.dma("sp", lambda e: e.dma_start(out=XS[:], in_=xs), w=[r_XS])

            S.set_phase('l0')
            A.at(0)
            w_in_sb, r_win = A.get(8 * 2560, BF16, "w_in"); w_in_sb = w_in_sb.rearrange("p (k n) -> p k n", k=8)
            w_o_sb, r_wo = A.get(8 * 1024, BF16, "w_o0"); w_o_sb = w_o_sb.rearrange("p (k n) -> p k n", k=8)
            poolw_sb, r_poolw = A.get(512, BF16, "poolw"); poolw_sb = poolw_sb.rearrange("p (g d) -> p g d", g=4)
            c0buf, r_c0 = A.get(1024 + 1024 + 4 + 60 + 4 + 512, F32, "c0")
            decT = c0buf[:, 0:512].rearrange("p (h l) -> p h l", h=4)
            qdec = c0buf[:, 512:1024].rearrange("p (h l) -> p h l", h=4)
            kdec = c0buf[:, 1024:1028]
            invcnt = c0buf[:, 1028:1088].rearrange("p (g t) -> p g t", g=4)
            pscale = c0buf[:, 1088:1092]
            gng = c0buf[:, 1092:1604]
            lnt0, r_lnt0 = A.get(2 * D, F32, "lnt0")
            w_in_v = w_in.rearrange("(k p) n -> p k n", p=128)
            for hh in range(2):
                S.dma("pool", lambda e, hh=hh: e.dma_start(out=w_in_sb[:, :, hh * 1280:(hh + 1) * 1280], in_=w_in_v[:, :, hh * 1280:(hh + 1) * 1280]), w=[r_win])
            S.dma("pool", lambda e: e.dma_start(out=poolw_sb, in_=pool_w), w=[r_poolw])
            S.dma("pool", lambda e: e.dma_start(out=w_o_sb, in_=w_o0.rearrange("(k p) n -> p k n", p=128)), w=[r_wo])
            S.dma("sp", lambda e: e.dma_start(out=decT, in_=tdr["decTp"]), w=[r_c0])
            S.dma("sp", lambda e: e.dma_start(out=qdec, in_=tdr["qdecp"]), w=[r_c0])
            S.dma("sp", lambda e: e.dma_start(out=kdec, in_=tdr["kdecp"]), w=[r_c0])
            S.dma("sp", lambda e: e.dma_start(out=invcnt, in_=tdr["invcnt"]), w=[r_c0])
            S.dma("sp", lambda e: e.dma_start(out=pscale, in_=pool_scale), w=[r_c0])
            S.dma("sp", lambda e: e.dma_start(out=gng, in_=gn_g.partition_broadcast(128)), w=[r_c0])
            load_ln(lnt0, r_lnt0, ln_mix_g[0], ln_mix_b[0])
            L0_BUF = A.cur

            def dbl(nelem, dt, name, shape=None, **kw):
                out = []
                for i in range(2):
                    v, r = A.get(nelem, dt, f"{name}{i}")
                    if shape:
                        v = v.rearrange(shape, **kw)
                    out.append((v, r))
                return [x[0] for x in out], [x[1] for x in out]

            xT, r_xT = dbl(1024, BF16, "xT", "p (k t) -> p k t", k=8)
            ropet, r_ropet = dbl(256, F32, "ropet")
            uext, r_uext = dbl(4 * 143 + 1, F32, "uext")
            uext = [u[:, 0:572].rearrange("p (g t) -> p g t", g=4) for u in uext]
            tA, r_tA = A.get(573, F32, "tA"); tA = tA[:, 0:572].rearrange("p (g t) -> p g t", g=4)
            tB, r_tB = A.get(573, F32, "tB"); tB = tB[:, 0:572].rearrange("p (g t) -> p g t", g=4)
            pooledT, r_pooled = dbl(512, BF16, "pooledT", "p (g t) -> p g t", g=4)
            qk_f, r_qkf = A.get(1024, F32, "qk_f"); qk_f = qk_f.rearrange("p (h t) -> p h t", h=8)
            ropeA, r_ropeA = A.get(1024, F32, "ropeA"); ropeA = ropeA.rearrange("p (h t) -> p h t", h=8)
            ropeB, r_ropeB = A.get(1024, F32, "ropeB"); ropeB = ropeB.rearrange("p (h t) -> p h t", h=8)
            qkT, r_qkT = dbl(1024, BF16, "qkT", "p (h t) -> p h t", h=8)
            qdT, r_qdT = dbl(512, BF16, "qdT", "p (h t) -> p h t", h=4)
            v_sb, r_v = dbl(512, BF16, "v")
            sgg, r_sgg = dbl(512, F32, "sgg")
            sT_bf, r_sT = dbl(512, BF16, "sT", "p (h t) -> p h t", h=4)
            kp_bf, r_kp = dbl(512, BF16, "kp", "p (h t) -> p h t", h=4)
            S_f, r_Sf = A.get(512, F32, "S_f"); S_f = S_f.rearrange("p (h t) -> p h t", h=4)
            S_b, r_Sb = A.get(512, BF16, "S_b"); S_b = S_b.rearrange("p (h t) -> p h t", h=4)
            on1, r_on1 = A.get(512, F32, "on1")
            ret_bf, r_ret = dbl(512, BF16, "ret")
            catT, r_cat = dbl(1024, BF16, "catT", "p (c t) -> p c t", c=8)
            z0_, r_z0_ = A.get(D, F32, "z"); z = [z0_, z0_]; r_z = [r_z0_, r_z0_]
            scr, r_scr = dbl(16, F32, "scr")
            gst, r_gst = A.get(32, F32, "gst")
            poolo, r_poolo = A.get(512, F32, "poolo")
            L0_END = A.cur
            print("layer0 arena bytes", L0_END)

            S.op("pool", lambda e: e.memset(uext[1][:, :, 128:143], 0.0), w=[r_uext[1]])

            def l0A(t):
                pb = t % 2
                tok = slice(t * 128, (t + 1) * 128)
                S.dma("sp", lambda e, pb=pb, tok=tok: e.dma_start(out=ropet[pb][:, 0:128], in_=tdr["c2p"][:, tok]), w=[r_ropet[pb]])
                S.dma("sp", lambda e, pb=pb, tok=tok: e.dma_start(out=ropet[pb][:, 128:256], in_=tdr["ssp"][:, tok]), w=[r_ropet[pb]])
                transpose_x(X[:, t, :], rX[t], xT[pb], r_xT[pb])
                bu, ru = ring.get(1)
                bq, rq = ring.get(1)
                bk, rk = ring.get(1)
                bv, rv = ring.get(1)
                bg, rg = ring.get(1)
                for (bb, rr, c0) in ((bu, ru, 0), (bq, rq, 512), (bk, rk, 1024)):
                    pv = bank(bb).rearrange("p (h t) -> p h t", h=4)
                    for h in range(4):
                        for k in range(8):
                            S.op("pe", lambda e, pv=pv, h=h, k=k, c0=c0, pb=pb: e.matmul(pv[:, h, :], lhsT=w_in_sb[:, k, c0 + h * 128:c0 + (h + 1) * 128], rhs=xT[pb][:, k, :], start=(k == 0), stop=(k == 7)),
                                 r=[r_win, r_xT[pb]], w=rr)
                for (bb, rr, c0) in ((bv, rv, 1536), (bg, rg, 2048)):
                    for k in range(8):
                        S.op("pe", lambda e, bb=bb, k=k, c0=c0, pb=pb: e.matmul(bank(bb), lhsT=xT[pb][:, k, :], rhs=w_in_sb[:, k, c0:c0 + 512], start=(k == 0), stop=(k == 7)),
                             r=[r_win, r_xT[pb]], w=rr)
                ue = uext[pb]
                S.op("act", lambda e, ue=ue, bu=bu: e.copy(out=ue[:, :, 15:143], in_=bank(bu).rearrange("p (h t) -> p h t", h=4)), r=ru, w=[r_uext[pb]])
                S.op("pool", lambda e, ue=ue, pb=pb: e.tensor_copy(out=ue[:, :, 0:15], in_=uext[1 - pb][:, :, 128:143]), r=[r_uext[1 - pb]], w=[r_uext[pb]])
                S.op("pool", lambda e, ue=ue: e.tensor_tensor(out=tA[:, :, 1:143], in0=ue[:, :, 1:143], in1=ue[:, :, 0:142], op=ALU.add), r=[r_uext[pb]], w=[r_tA])
                S.op("pool", lambda e: e.tensor_tensor(out=tB[:, 1:4, 3:143], in0=tA[:, 1:4, 3:143], in1=tA[:, 1:4, 1:141], op=ALU.add), r=[r_tA], w=[r_tB])
                S.op("pool", lambda e: e.tensor_tensor(out=tA[:, 2:4, 7:143], in0=tB[:, 2:4, 7:143], in1=tB[:, 2:4, 3:139], op=ALU.add), r=[r_tB], w=[r_tA])
                S.op("pool", lambda e: e.tensor_tensor(out=tB[:, 3:4, 15:143], in0=tA[:, 3:4, 15:143], in1=tA[:, 3:4, 7:135], op=ALU.add), r=[r_tA], w=[r_tB])
                fin = (tA, tB, tA, tB)
                for g in range(4):
                    S.op("dve", lambda e, g=g, ue=ue, pb=pb: e.scalar_tensor_tensor(out=pooledT[pb][:, g, :], in0=fin[g][:, g, 15:143], scalar=1.0 / POOL_W[g], in1=ue[:, g, 15:143], op0=ALU.mult, op1=ALU.subtract),
                         r=[r_tA, r_tB, r_uext[pb]], w=[r_pooled[pb]])
                if t == 0:
                    for g in range(4):
                        S.op("dve", lambda e, g=g: e.tensor_tensor(out=ropeA[:, g, 0:15], in0=fin[g][:, g, 15:30], in1=invcnt[:, g, :], op=ALU.mult), r=[r_tA, r_tB, r_c0], w=[r_ropeA])
                        S.op("dve", lambda e, g=g, ue=ue, pb=pb: e.tensor_tensor(out=pooledT[pb][:, g, 0:15], in0=ropeA[:, g, 0:15], in1=ue[:, g, 15:30], op=ALU.subtract), r=[r_ropeA, r_uext[pb]], w=[r_pooled[pb]])
                bpm, rpm = ring.get(1)
                pmv = bank(bpm).rearrange("p (h t) -> p h t", h=4)
                for g in range(4):
                    S.op("pe", lambda e, g=g, pmv=pmv, pb=pb: e.matmul(pmv[:, g, :], lhsT=poolw_sb[:, g, :], rhs=pooledT[pb][:, g, :], start=True, stop=True), r=[r_poolw, r_pooled[pb]], w=rpm)
                for g in range(4):
                    S.op("act", lambda e, g=g, pmv=pmv, pb=pb: e.activation(out=catT[pb][:, g, :], in_=pmv[:, g, :], func=AF.Identity, bias=0.0, scale=pscale[:, g:g + 1]), r=rpm + [r_c0], w=[r_cat[pb]])
                if t == NT - 1:
                    bpo, rpo = ring.get(1)
                    for g in range(4):
                        S.op("pe", lambda e, g=g, ue=ue, bpo=bpo: e.transpose(out=bank(bpo)[0:15, g * 128:(g + 1) * 128], in_=ue[:, g, 128:143], identity=ident_f[:]), r=[r_uext[pb], r_idf], w=rpo)
                    S.op("act", lambda e, bpo=bpo: e.copy(out=poolo[0:15, :], in_=bank(bpo)[0:15, :]), r=rpo, w=[r_poolo])
                    out_toks.append(S.dma("sp", lambda e: e.dma_start(out=poolp, in_=poolo[0:15, :]), r=[r_poolo]))
                S.op("act", lambda e, bq=bq: e.copy(out=qk_f[:, 0:4, :], in_=bank(bq).rearrange("p (h t) -> p h t", h=4)), r=rq, w=[r_qkf])
                S.op("act", lambda e, bk=bk: e.copy(out=qk_f[:, 4:8, :], in_=bank(bk).rearrange("p (h t) -> p h t", h=4)), r=rk, w=[r_qkf])
                c2b = ropet[pb][:, 0:128].unsqueeze(1)
                ssb = ropet[pb][:, 128:256].unsqueeze(1)
                S.op("dve", lambda e, c2b=c2b: e.tensor_tensor(out=ropeA, in0=qk_f, in1=c2b.to_broadcast([128, 8, 128]), op=ALU.mult), r=[r_qkf, r_ropet[pb]], w=[r_ropeA])
                S.op("pool", lambda e, ssb=ssb: e.tensor_tensor(out=ropeB[0:64], in0=qk_f[64:128], in1=ssb[64:128].to_broadcast([64, 8, 128]), op=ALU.mult), r=[r_qkf, r_ropet[pb]], w=[r_ropeB])
                S.op("pool", lambda e, ssb=ssb: e.tensor_tensor(out=ropeB[64:128], in0=qk_f[0:64], in1=ssb[0:64].to_broadcast([64, 8, 128]), op=ALU.mult), r=[r_qkf, r_ropet[pb]], w=[r_ropeB])
                S.op("dve", lambda e, pb=pb: e.tensor_tensor(out=qkT[pb], in0=ropeA, in1=ropeB, op=ALU.add), r=[r_ropeA, r_ropeB], w=[r_qkT[pb]])
                if t > 0:
                    S.op("pool", lambda e, pb=pb: e.tensor_tensor(out=qdT[pb], in0=qkT[pb][:, 0:4, :], in1=qdec, op=ALU.mult), r=[r_qkT[pb], r_c0], w=[r_qdT[pb]])
                S.op("act", lambda e, pb=pb, bv=bv: e.copy(out=v_sb[pb], in_=bank(bv)), r=rv, w=[r_v[pb]])
                S.op("act", lambda e, pb=pb, bg=bg: e.activation(out=sgg[pb], in_=bank(bg), func=AF.Silu), r=rg, w=[r_sgg[pb]])
                S.op("pool", lambda e, pb=pb: e.tensor_tensor(out=sgg[pb], in0=sgg[pb], in1=gng, op=ALU.mult), r=[r_sgg[pb], r_c0], w=[r_sgg[pb]])
            def l0B(t):
                pb = t % 2
                bs, rs = ring.get(1)
                sv = bank(bs).rearrange("p (h t) -> p h t", h=4)
                for h in range(4):
                    S.op("pe", lambda e, h=h, sv=sv, pb=pb: e.matmul(sv[:, h, :], lhsT=qkT[pb][:, 4 + h, :], rhs=qkT[pb][:, h, :], start=True, stop=True), r=[r_qkT[pb]], w=rs)
                S.op("dve", lambda e, sv=sv, pb=pb: e.tensor_tensor(out=sT_bf[pb], in0=sv, in1=decT, op=ALU.mult), r=rs + [r_c0], w=[r_sT[pb]])
                bo, ro = ring.get(1)
                ov = bank(bo).rearrange("p (h t) -> p h t", h=4)
                for h in range(4):
                    S.op("pe", lambda e, h=h, ov=ov, pb=pb, t=t: e.matmul(ov[:, h, :], lhsT=sT_bf[pb][:, h, :], rhs=v_sb[pb][:, h * 128:(h + 1) * 128], start=True, stop=(t == 0)), r=[r_sT[pb], r_v[pb]], w=ro)
                    if t > 0:
                        S.op("pe", lambda e, h=h, ov=ov, pb=pb: e.matmul(ov[:, h, :], lhsT=qdT[pb][:, h, :], rhs=S_b[:, h, :], start=False, stop=True), r=[r_qdT[pb], r_Sb], w=ro)
                bkt, rkt = ring.get(1)
                ktv = bankbf(bkt)[:, 0:512].rearrange("p (h t) -> p h t", h=4)
                for h in range(4):
                    S.op("pe", lambda e, h=h, ktv=ktv, pb=pb: e.transpose(out=ktv[:, h, :], in_=qkT[pb][:, 4 + h, :], identity=ident_b[:]), r=[r_qkT[pb], r_idb], w=rkt)
                for h in range(4):
                    S.op("act", lambda e, h=h, ktv=ktv, pb=pb: e.activation(out=kp_bf[pb][:, h, :], in_=ktv[:, h, :], func=AF.Identity, bias=0.0, scale=kdec[:, h:h + 1]), r=rkt + [r_c0], w=[r_kp[pb]])
                bds, rds = ring.get(1)
                dsv = bank(bds).rearrange("p (h t) -> p h t", h=4)
                for h in range(4):
                    S.op("pe", lambda e, h=h, dsv=dsv, pb=pb: e.matmul(dsv[:, h, :], lhsT=kp_bf[pb][:, h, :], rhs=v_sb[pb][:, h * 128:(h + 1) * 128], start=True, stop=True), r=[r_kp[pb], r_v[pb]], w=rds)
                if t == 0:
                    S.op("dve", lambda e, dsv=dsv: e.tensor_copy(out=S_f, in_=dsv), r=rds, w=[r_Sf])
                else:
                    for h in range(4):
                        S.op("dve", lambda e, h=h, dsv=dsv: e.scalar_tensor_tensor(out=S_f[:, h, :], in0=S_f[:, h, :], scalar=tabs["gCp"][h], in1=dsv[:, h, :], op0=ALU.mult, op1=ALU.add), r=rds + [r_Sf], w=[r_Sf])
                if t < NT - 1:
                    S.op("pool", lambda e: e.tensor_copy(out=S_b, in_=S_f), r=[r_Sf], w=[r_Sb])
                else:
                    out_toks.append(S.dma("sp", lambda e: e.dma_start(out=retp.rearrange("h k v -> k h v"), in_=S_f), r=[r_Sf]))
                for h in range(4):
                    S.op("dve", lambda e, h=h, ov=ov: e.bn_stats(out=gst[:, h * 6:(h + 1) * 6], in_=ov[:, h, :]), r=ro, w=[r_gst])
                    S.op("dve", lambda e, h=h: e.bn_aggr(out=gst[:, 24 + 2 * h:26 + 2 * h], in_=gst[:, h * 6:(h + 1) * 6]), r=[r_gst], w=[r_gst])
                mvv = gst[:, 24:32].rearrange("p (h two) -> p h two", two=2)
                S.op("act", lambda e, pb=pb: e.activation(out=scr[pb][:, 0:4], in_=mvv[:, :, 1], func=AF.Sqrt, bias=eps_gn, scale=1.0), r=[r_gst, r_eps], w=[r_scr[pb]])
                S.op("dve", lambda e, pb=pb: e.reciprocal(out=scr[pb][:, 0:4], in_=scr[pb][:, 0:4]), r=[r_scr[pb]], w=[r_scr[pb]])
                S.op("dve", lambda e, pb=pb: e.scalar_tensor_tensor(out=scr[pb][:, 4:8], in0=mvv[:, :, 0], scalar=-1.0, in1=scr[pb][:, 0:4], op0=ALU.mult, op1=ALU.mult), r=[r_gst, r_scr[pb]], w=[r_scr[pb]])
                for h in range(4):
                    S.op("act", lambda e, h=h, ov=ov, pb=pb: e.activation(out=on1[:, h * 128:(h + 1) * 128], in_=ov[:, h, :], func=AF.Identity, bias=scr[pb][:, 4 + h:5 + h], scale=scr[pb][:, h:h + 1]), r=ro + [r_scr[pb]], w=[r_on1])
                S.op("pool", lambda e, pb=pb: e.tensor_tensor(out=ret_bf[pb], in0=on1, in1=sgg[pb], op=ALU.mult), r=[r_on1, r_sgg[pb]], w=[r_ret[pb]])
                brt, rrt = ring.get(1)
                rtv = bankbf(brt)[:, 0:512].rearrange("p (h t) -> p h t", h=4)
                for h in range(4):
                    S.op("pe", lambda e, h=h, rtv=rtv, pb=pb: e.transpose(out=rtv[:, h, :], in_=ret_bf[pb][:, h * 128:(h + 1) * 128], identity=ident_b[:]), r=[r_ret[pb], r_idb], w=rrt)
                S.op("dve", lambda e, rtv=rtv, pb=pb: e.tensor_copy(out=catT[pb][:, 4:8, :], in_=rtv), r=rrt, w=[r_cat[pb]])
                by, ry = ring.get(2)
                for hf in range(2):
                    for c in range(8):
                        S.op("pe", lambda e, hf=hf, c=c, by=by, pb=pb: e.matmul(bank(by + hf), lhsT=catT[pb][:, c, :], rhs=w_o_sb[:, c, hf * 512:(hf + 1) * 512], start=(c == 0), stop=(c == 7)), r=[r_cat[pb], r_wo], w=[ry[hf]])
                S.op("dve", lambda e, by=by, pb=pb, t=t: e.scalar_tensor_tensor(out=z[pb], in0=X[:, t, :], scalar=ALPHA, in1=bank(by, 2), op0=ALU.mult, op1=ALU.add), r=[rX[t]] + ry, w=[r_z[pb]])
                layer_norm_tile(z[pb], r_z[pb], X[:, t, :], rX[t], 128, scr[pb], r_scr[pb], lnt0, r_lnt0)

            l0A(0)
            for t in range(NT):
                if t + 1 < NT:
                    l0A(t + 1)
                l0B(t)

            def ffn(layer, final):
                A.at(0)
                gated, r_gated = A.get(NJ * 1024, BF16, "gated", nres=NJ); gated = gated.rearrange("p (j n) -> p j n", j=NJ)
                x1T, r_x1T = A.get(8 * 1024, BF16, "x1T", nres=8); x1T = x1T.rearrange("p (k n) -> p k n", k=8)
                cb0_, r_cb0_ = A.get(512, F32, "cbuf"); cbuf = [cb0_, cb0_]; r_cbuf = [r_cb0_, r_cb0_]
                sb0_, r_sb0_ = A.get(512, F32, "sbf"); sbf = [sb0_, sb0_]; r_sbf = [r_sb0_, r_sb0_]
                zf0, r_zf0 = A.get(D, F32, "zf"); zf = [zf0, zf0]; r_zf = [r_zf0, r_zf0]
                scf, r_scf = dbl(16, F32, "scf")
                cwb, r_cw = A.get(NJ * 4, F32, "cw"); cw = cwb[:, 0:NJ * 3].rearrange("p (j c) -> p j c", j=NJ); cb = cwb[:, NJ * 3:NJ * 4]
                hist, r_hist = A.get(NJ * 2, F32, "hist"); hist = hist.rearrange("p (j c) -> p j c", j=NJ)
                cst, r_cst = A.get(NJ * 2, F32, "cst"); cst = cst.rearrange("p (j c) -> p j c", j=NJ)
                cso, r_cso = A.get(512, F32, "cso")
                lnt, r_lnt = A.get(2 * D, F32, "lntf")
                wup = []
                r_wup = []
                for i in range(2):
                    v, r = A.get(8 * 256, BF16, f"wup{i}")
                    wup.append(v.rearrange("p (k n) -> p k n", k=8)); r_wup.append(r)
                wdn, r_wdn = A.get(NJ * 1024, BF16, "wdn"); wdn = wdn.rearrange("p (j n) -> p j n", j=NJ)
                if with_sample:
                    xs1T, r_xs1T = A.get(128, BF16, "xs1T"); xs1T = xs1T.rearrange("p (k t) -> p k t", k=8)
                    aext, r_aext = A.get(NJ * 24, F32, "aext"); aext = aext.rearrange("p (j b r) -> p j b r", j=NJ, b=4)
                    b_s, r_bs = A.get(NJ * 16, F32, "b_s"); b_s = b_s.rearrange("p (j t) -> p j t", j=NJ)
                    scT, r_scT = A.get(NJ * 8, F32, "scT"); scT = scT.rearrange("p (j r) -> p j r", j=NJ)
                    c_s, r_cs = A.get(NJ * 16, F32, "c_s"); c_s = c_s.rearrange("p (j b l) -> p j b l", j=NJ, b=4)
                    t_s, r_ts = A.get(NJ * 16, F32, "t_s"); t_s = t_s.rearrange("p (j b l) -> p j b l", j=NJ, b=4)
                    gated_s, r_gs = A.get(NJ * 16, BF16, "gated_s"); gated_s = gated_s.rearrange("p (j t) -> p j t", j=NJ)
                print("ffn arena bytes", A.cur)
                S.dma("sp", lambda e: e.dma_start(out=cw, in_=conv_w[layer]), w=[r_cw])
                S.dma("sp", lambda e: e.dma_start(out=cb, in_=conv_b[layer]), w=[r_cw])
                load_ln(lnt, r_lnt, ln_ffn_g[layer], ln_ffn_b[layer])
                wdn_v = w_down[layer].rearrange("(j p) n -> p j n", p=128)
                wdn_loaded = [False]

                def load_wdn():
                    for jj in range(0, NJ, 2):
                        S.dma("pool", lambda e, jj=jj: e.dma_start(out=wdn[:, jj:jj + 2, :], in_=wdn_v[:, jj:jj + 2, :]), w=[r_wdn])

                def load_wup(j, slot):
                    S.dma("pool", lambda e, j=j, slot=slot: e.dma_start(out=wup[slot], in_=w_up[layer, j]), w=[r_wup[slot]])

                if with_sample:
                    transpose_x(XS[:], r_XS, xs1T, r_xs1T, npart=NS)
                    for jg in range(0, NJ, 4):
                        n = min(4, NJ - jg)
                        S.dma("sp", lambda e, jg=jg, n=n: e.dma_start(out=cso[0:8, 0:n * 128], in_=sconv[layer][:, jg * 128:(jg + n) * 128]), w=[r_cso])
                        bst, rst = ring.get(1)
                        for i_ in range(n):
                            S.op("pe", lambda e, i_=i_, bst=bst: e.transpose(out=bank(bst)[:, i_ * 8:(i_ + 1) * 8], in_=cso[0:8, i_ * 128:(i_ + 1) * 128], identity=ident_f[0:8, 0:8]), r=[r_cso, r_idf], w=rst)
                        S.op("act", lambda e, jg=jg, n=n, bst=bst: e.copy(out=scT[:, jg:jg + n, :], in_=bank(bst)[:, 0:n * 8].rearrange("p (j r) -> p j r", j=n)), r=rst, w=[r_scT])
                    S.op("pool", lambda e: e.tensor_copy(out=aext[:, :, :, 0:2], in_=scT.rearrange("p j (b r) -> p j b r", b=4)), r=[r_scT], w=[r_aext])
                cnt = 0
                for half in range(2):
                    for tt in range(8):
                        t = half * 8 + tt
                        transpose_x(X[:, t, :], rX[t], x1T[:, :, tt * 128:(tt + 1) * 128], r_x1T[tt])
                    load_wup(0, cnt % 2)
                    for j in range(NJ):
                        slot = cnt % 2
                        if j + 1 < NJ:
                            load_wup(j + 1, (cnt + 1) % 2)
                        elif half == 0:
                            pass
                        if half == 0 and j == 1:
                            load_wdn()
                        cnt += 1
                        for grp in range(2):
                            ba, ra = ring.get(1)
                            bb_, rb = ring.get(1)
                            cols = slice(grp * 512, (grp + 1) * 512)
                            for (bx, rx, c0) in ((ba, ra, 0), (bb_, rb, 128)):
                                for k in range(8):
                                    S.op("pe", lambda e, bx=bx, k=k, c0=c0, slot=slot, cols=cols: e.matmul(bank(bx), lhsT=wup[slot][:, k, c0:c0 + 128], rhs=x1T[:, k, cols], start=(k == 0), stop=(k == 7)),
                                         r=[r_wup[slot]] + r_x1T[grp * 4:grp * 4 + 4], w=rx)
                            pa = bank(ba)
                            cp = (j * 2 + grp) % 2
                            c_ = cbuf[cp]
                            s_ = sbf[cp]
                            S.op("act", lambda e, pa=pa, c_=c_, j=j: e.activation(out=c_, in_=pa, func=AF.Identity, bias=cb[:, j:j + 1], scale=cw[:, j, 2:3]), r=ra + [r_cw], w=[r_cbuf[cp]])
                            S.op("dve", lambda e, pa=pa, c_=c_, j=j: e.scalar_tensor_tensor(out=c_[:, 1:512], in0=pa[:, 0:511], scalar=cw[:, j, 1:2], in1=c_[:, 1:512], op0=ALU.mult, op1=ALU.add), r=ra + [r_cw, r_cbuf[cp]], w=[r_cbuf[cp]])
                            S.op("dve", lambda e, pa=pa, c_=c_, j=j: e.scalar_tensor_tensor(out=c_[:, 2:512], in0=pa[:, 0:510], scalar=cw[:, j, 0:1], in1=c_[:, 2:512], op0=ALU.mult, op1=ALU.add), r=ra + [r_cw, r_cbuf[cp]], w=[r_cbuf[cp]])
                            if not (half == 0 and grp == 0):
                                S.op("dve", lambda e, c_=c_, j=j: e.scalar_tensor_tensor(out=c_[:, 0:1], in0=hist[:, j, 1:2], scalar=cw[:, j, 1:2], in1=c_[:, 0:1], op0=ALU.mult, op1=ALU.add), r=[r_hist, r_cw, r_cbuf[cp]], w=[r_cbuf[cp]])
                                S.op("dve", lambda e, c_=c_, j=j: e.scalar_tensor_tensor(out=c_[:, 0:2], in0=hist[:, j, 0:2], scalar=cw[:, j, 0:1], in1=c_[:, 0:2], op0=ALU.mult, op1=ALU.add), r=[r_hist, r_cw, r_cbuf[cp]], w=[r_cbuf[cp]])
                            if half == 1 and grp == 1:
                                S.op("act", lambda e, pa=pa, j=j: e.copy(out=cst[:, j, :], in_=pa[:, 510:512]), r=ra, w=[r_cst])
                            else:
                                S.op("act", lambda e, pa=pa, j=j: e.copy(out=hist[:, j, :], in_=pa[:, 510:512]), r=ra, w=[r_hist])
                            S.op("act", lambda e, c_=c_, s_=s_: e.activation(out=s_, in_=c_, func=AF.Silu), r=[r_cbuf[cp]], w=[r_sbf[cp]])
                            S.op("dve", lambda e, s_=s_, bb_=bb_, j=j, cols=cols: e.tensor_tensor(out=gated[:, j, cols], in0=s_, in1=bank(bb_), op=ALU.mult), r=[r_sbf[cp]] + rb, w=[r_gated[j]])
                        if with_sample and half == 0:
                            bsa, rsa = ring.get(1)
                            for (c0, col) in ((0, 0), (128, 16)):
                                for k in range(8):
                                    S.op("pe", lambda e, k=k, c0=c0, col=col, slot=slot, bsa=bsa: e.matmul(bank(bsa)[:, col:col + 16], lhsT=wup[slot][:, k, c0:c0 + 128], rhs=xs1T[:, k, :], start=(k == 0), stop=(k == 7)), r=[r_wup[slot], r_xs1T], w=rsa)
                            S.op("act", lambda e, j=j, bsa=bsa: e.copy(out=aext[:, j, :, 2:6], in_=bank(bsa)[:, 0:16].rearrange("p (b l) -> p b l", b=4)), r=rsa, w=[r_aext])
                            S.op("dve", lambda e, j=j, bsa=bsa: e.tensor_copy(out=b_s[:, j, :], in_=bank(bsa)[:, 16:32]), r=rsa, w=[r_bs])
                    if with_sample and half == 0:
                        def cwb(c):
                            return cw[:, :, c:c + 1].unsqueeze(3).to_broadcast([128, NJ, 4, 4])
                        S.op("dve", lambda e: e.tensor_tensor(out=c_s, in0=aext[:, :, :, 2:6], in1=cwb(2), op=ALU.mult), r=[r_aext, r_cw], w=[r_cs])
                        S.op("pool", lambda e: e.tensor_tensor(out=t_s, in0=aext[:, :, :, 1:5], in1=cwb(1), op=ALU.mult), r=[r_aext, r_cw], w=[r_ts])
                        S.op("dve", lambda e: e.tensor_tensor(out=c_s, in0=c_s, in1=t_s, op=ALU.add), r=[r_cs, r_ts], w=[r_cs])
                        S.op("pool", lambda e: e.tensor_tensor(out=t_s, in0=aext[:, :, :, 0:4], in1=cwb(0), op=ALU.mult), r=[r_aext, r_cw], w=[r_ts])
                        S.op("dve", lambda e: e.tensor_tensor(out=c_s, in0=c_s, in1=t_s, op=ALU.add), r=[r_cs, r_ts], w=[r_cs])
                        S.op("dve", lambda e: e.tensor_tensor(out=c_s, in0=c_s, in1=cb.unsqueeze(2).unsqueeze(3).to_broadcast([128, NJ, 4, 4]), op=ALU.add), r=[r_cs, r_cw], w=[r_cs])
                        S.op("act", lambda e: e.activation(out=c_s, in_=c_s, func=AF.Silu), r=[r_cs], w=[r_cs])
                        S.op("dve", lambda e: e.tensor_tensor(out=gated_s, in0=c_s.rearrange("p j b l -> p j (b l)"), in1=b_s, op=ALU.mult), r=[r_cs, r_bs], w=[r_gs])
                        S.op("pool", lambda e: e.tensor_copy(out=scT.rearrange("p j (b r) -> p j b r", b=4), in_=aext[:, :, :, 4:6]), r=[r_aext], w=[r_scT])
                        for jg in range(0, NJ, 4):
                            n = min(4, NJ - jg)
                            bst, rst = ring.get(1)
                            for i_ in range(n):
                                S.op("pe", lambda e, i_=i_, jg=jg, bst=bst: e.transpose(out=bank(bst)[0:8, i_ * 128:(i_ + 1) * 128], in_=scT[:, jg + i_, :], identity=ident_f[:]), r=[r_scT, r_idf], w=rst)
                            S.op("act", lambda e, n=n, bst=bst: e.copy(out=cso[0:8, 0:n * 128], in_=bank(bst)[0:8, 0:n * 128]), r=rst, w=[r_cso])
                            out_toks.append(S.dma("sp", lambda e, jg=jg, n=n: e.dma_start(out=convs[layer][:, jg * 128:(jg + n) * 128], in_=cso[0:8, 0:n * 128]), r=[r_cso]))
                    for tt in range(8):
                        t = half * 8 + tt
                        pb = tt % 2
                        by, ry = ring.get(2)
                        for hf in range(2):
                            for j in range(NJ):
                                S.op("pe", lambda e, hf=hf, j=j, by=by, tt=tt: e.matmul(bank(by + hf), lhsT=gated[:, j, tt * 128:(tt + 1) * 128], rhs=wdn[:, j, hf * 512:(hf + 1) * 512], start=(j == 0), stop=(j == NJ - 1)),
                                     r=[r_gated[j], r_wdn], w=[ry[hf]])
                        S.op("dve", lambda e, by=by, pb=pb, t=t: e.scalar_tensor_tensor(out=zf[pb], in0=X[:, t, :], scalar=ALPHA, in1=bank(by, 2), op0=ALU.mult, op1=ALU.add), r=[rX[t]] + ry, w=[r_zf[pb]])
                        layer_norm_tile(zf[pb], r_zf[pb], X[:, t, :], rX[t], 128, scf[pb], r_scf[pb], lnt, r_lnt)
                        if final:
                            out_toks.append(S.dma("sp", lambda e, t=t: e.dma_start(out=yp[t * 128:(t + 1) * 128, :], in_=X[:, t, :]), r=[rX[t]]))
                if with_sample:
                    bys, rys = ring.get(2)
                    for hf in range(2):
                        for j in range(NJ):
                            S.op("pe", lambda e, hf=hf, j=j, bys=bys: e.matmul(bank(bys + hf)[0:NS, :], lhsT=gated_s[:, j, :], rhs=wdn[:, j, hf * 512:(hf + 1) * 512], start=(j == 0), stop=(j == NJ - 1)), r=[r_gs, r_wdn], w=[rys[hf]])
                    S.op("dve", lambda e, bys=bys: e.scalar_tensor_tensor(out=zf[0][0:NS, :], in0=XS[:], scalar=ALPHA, in1=bank(bys, 2)[0:NS, :], op0=ALU.mult, op1=ALU.add), r=[r_XS] + rys, w=[r_zf[0]])
                    layer_norm_tile(zf[0][0:NS, :], r_zf[0], XS[:], r_XS, NS, scf[0], r_scf[0], lnt, r_lnt)
                    if final:
                        out_toks.append(S.dma("sp", lambda e: e.dma_start(out=ys, in_=XS[:]), r=[r_XS]))
                bc, rc = ring.get(1)
                for j in range(NJ):
                    S.op("pe", lambda e, j=j, bc=bc: e.transpose(out=bank(bc)[0:2, (j % 4) * 128:(j % 4 + 1) * 128], in_=cst[:, j, :], identity=ident_f[:]), r=[r_cst, r_idf], w=rc)
                    if j % 4 == 3 or j == NJ - 1:
                        j0 = j - j % 4
                        n = j - j0 + 1
                        S.op("act", lambda e, bc=bc, n=n: e.copy(out=cso[0:2, 0:n * 128], in_=bank(bc)[0:2, 0:n * 128]), r=rc, w=[r_cso])
                        out_toks.append(S.dma("sp", lambda e, j0=j0, n=n: e.dma_start(out=convp[layer][:, j0 * 128:(j0 + n) * 128], in_=cso[0:2, 0:n * 128]), r=[r_cso]))
                        if j != NJ - 1:
                            bc, rc = ring.get(1)


            def sample_l0():
                A.at(L0_BUF)
                xsT, r_xsT = A.get(128, BF16, "xsT"); xsT = xsT.rearrange("p (k t) -> p k t", k=8)
                tb, r_tb = A.get(68, F32, "stab")
                c2s = tb[:, 0:16]; sss = tb[:, 16:32]
                decTs = tb[0:4, 32:48]
                qdecs = tb[:, 48:64].rearrange("p (h l) -> p h l", h=4)
                kdecs = tb[0:4, 64:68]
                S.dma("sp", lambda e: e.dma_start(out=c2s, in_=tdr["c2s"]), w=[r_tb])
                S.dma("sp", lambda e: e.dma_start(out=sss, in_=tdr["sss"]), w=[r_tb])
                S.dma("sp", lambda e: e.dma_start(out=decTs, in_=tdr["decTs"].rearrange("m h l -> m (h l)")), w=[r_tb])
                S.dma("sp", lambda e: e.dma_start(out=qdecs, in_=tdr["qdecs"]), w=[r_tb])
                S.dma("sp", lambda e: e.dma_start(out=kdecs, in_=tdr["kdecs"]), w=[r_tb])
                sp_sb, r_sp = A.get(512, F32, "sp_sb")
                S.dma("sp", lambda e: e.dma_start(out=sp_sb[0:60, :], in_=spool), w=[r_sp])
                S0f, r_S0f = A.get(16 * 128, F32, "S0f"); S0f = S0f.rearrange("p (n v) -> p n v", n=16)
                S0b, r_S0b = A.get(16 * 128, BF16, "S0b"); S0b = S0b.rearrange("p (n v) -> p n v", n=16)
                S.dma("sp", lambda e: e.dma_start(out=S0f, in_=sret.rearrange("n k v -> k n v")), w=[r_S0f])
                S.op("pool", lambda e: e.tensor_copy(out=S0b, in_=S0f), r=[r_S0f], w=[r_S0b])
                for bb in range(SB):
                    out_toks.append(S.dma("sp", lambda e, bb=bb: e.dma_start(out=pools[bb * 15:bb * 15 + 11, :], in_=spool[bb * 15 + 4:bb * 15 + 15, :]), semkey="d2d"))
                transpose_x(XS[:], r_XS, xsT, r_xsT, npart=NS)
                bq_, rq_ = ring.get(1)
                pqk = bank(bq_)[:, 0:192].rearrange("p (g t) -> p g t", g=12)
                for g in range(12):
                    for k in range(8):
                        S.op("pe", lambda e, g=g, k=k: e.matmul(pqk[:, g, :], lhsT=w_in_sb[:, k, g * 128:(g + 1) * 128], rhs=xsT[:, k, :], start=(k == 0), stop=(k == 7)), r=[r_win, r_xsT], w=rq_)
                bu_, ru_ = ring.get(1)
                for k in range(8):
                    S.op("pe", lambda e, k=k: e.matmul(bank(bu_)[0:NS, :], lhsT=xsT[:, k, :], rhs=w_in_sb[:, k, 0:512], start=(k == 0), stop=(k == 7)), r=[r_win, r_xsT], w=ru_)
                utok, r_utok = A.get(512, F32, "utok")
                S.op("act", lambda e: e.copy(out=utok[0:NS, :], in_=bank(bu_)[0:NS, :]), r=ru_, w=[r_utok])
                for bb in range(SB):
                    out_toks.append(S.dma("sp", lambda e, bb=bb: e.dma_start(out=pools[bb * 15 + 11:bb * 15 + 15, :], in_=utok[bb * 4:bb * 4 + 4, :]), r=[r_utok]))
                bh_, rh_ = ring.get(1)
                for g in range(4):
                    S.op("pe", lambda e, g=g: e.transpose(out=bank(bh_)[:, g * 64:g * 64 + 60], in_=sp_sb[0:60, g * 128:(g + 1) * 128], identity=ident_f[0:60, 0:60]), r=[r_sp, r_idf], w=rh_)
                ue, r_ue = A.get(16 * 19, F32, "ues"); ue4 = ue.rearrange("p (g b r) -> p g b r", g=4, b=4); ue3 = ue.rearrange("p (n r) -> p n r", n=16)
                hv = bank(bh_)[:, 0:256].rearrange("p (g c) -> p g c", g=4)[:, :, 0:60].rearrange("p g (b r) -> p g b r", b=4)
                for g in range(4):
                    S.op("act", lambda e, g=g: e.copy(out=ue4[:, g, :, 0:15], in_=hv[:, g]), r=rh_, w=[r_ue])
                    S.op("act", lambda e, g=g: e.copy(out=ue4[:, g, :, 15:19], in_=pqk[:, g, :].rearrange("p (b l) -> p b l", b=4)), r=rq_, w=[r_ue])
                tAs, r_tAs = A.get(16 * 19, F32, "tAs"); tAs = tAs.rearrange("p (n r) -> p n r", n=16)
                tBs, r_tBs = A.get(16 * 19, F32, "tBs"); tBs = tBs.rearrange("p (n r) -> p n r", n=16)
                S.op("pool", lambda e: e.tensor_tensor(out=tAs[:, :, 1:19], in0=ue3[:, :, 1:19], in1=ue3[:, :, 0:18], op=ALU.add), r=[r_ue], w=[r_tAs])
                S.op("pool", lambda e: e.tensor_tensor(out=tBs[:, 4:16, 3:19], in0=tAs[:, 4:16, 3:19], in1=tAs[:, 4:16, 1:17], op=ALU.add), r=[r_tAs], w=[r_tBs])
                S.op("pool", lambda e: e.tensor_tensor(out=tAs[:, 8:16, 7:19], in0=tBs[:, 8:16, 7:19], in1=tBs[:, 8:16, 3:15], op=ALU.add), r=[r_tBs], w=[r_tAs])
                S.op("pool", lambda e: e.tensor_tensor(out=tBs[:, 12:16, 15:19], in0=tAs[:, 12:16, 15:19], in1=tAs[:, 12:16, 7:11], op=ALU.add), r=[r_tAs], w=[r_tBs])
                pooleds, r_pooleds = A.get(64, BF16, "pooleds"); pooleds = pooleds.rearrange("p (g b l) -> p g b l", g=4, b=4)
                fins = (tAs, tBs, tAs, tBs)
                for g in range(4):
                    S.op("dve", lambda e, g=g: e.scalar_tensor_tensor(out=pooleds[:, g], in0=fins[g][:, g * 4:(g + 1) * 4, 15:19], scalar=1.0 / POOL_W[g], in1=ue3[:, g * 4:(g + 1) * 4, 15:19], op0=ALU.mult, op1=ALU.subtract), r=[r_tAs, r_tBs, r_ue], w=[r_pooleds])
                catTs, r_catTs = A.get(128, BF16, "catTs"); catTs = catTs.rearrange("p (c t) -> p c t", c=8)
                bpm_, rpm_ = ring.get(1)
                pms = bank(bpm_)[:, 0:64].rearrange("p (g t) -> p g t", g=4)
                for g in range(4):
                    S.op("pe", lambda e, g=g: e.matmul(pms[:, g, :], lhsT=poolw_sb[:, g, :], rhs=pooleds[:, g].rearrange("p b l -> p (b l)"), start=True, stop=True), r=[r_poolw, r_pooleds], w=rpm_)
                for g in range(4):
                    S.op("act", lambda e, g=g: e.activation(out=catTs[:, g, :], in_=pms[:, g, :], func=AF.Identity, bias=0.0, scale=pscale[:, g:g + 1]), r=rpm_ + [r_c0], w=[r_catTs])
                qkfs, r_qkfs = A.get(128, F32, "qkfs"); qkfs = qkfs.rearrange("p (h t) -> p h t", h=8)
                rAs, r_rAs = A.get(128, F32, "rAs"); rAs = rAs.rearrange("p (h t) -> p h t", h=8)
                rBs, r_rBs = A.get(128, F32, "rBs"); rBs = rBs.rearrange("p (h t) -> p h t", h=8)
                qkTs, r_qkTs = A.get(128, BF16, "qkTs"); qkTs = qkTs.rearrange("p (h t) -> p h t", h=8)
                qdTs, r_qdTs = A.get(64, BF16, "qdTs"); qdTs = qdTs.rearrange("p (h t) -> p h t", h=4)
                S.op("act", lambda e: e.copy(out=qkfs, in_=pqk[:, 4:12, :]), r=rq_, w=[r_qkfs])
                S.op("dve", lambda e: e.tensor_tensor(out=rAs, in0=qkfs, in1=c2s.unsqueeze(1).to_broadcast([128, 8, 16]), op=ALU.mult), r=[r_qkfs, r_tb], w=[r_rAs])
                S.op("pool", lambda e: e.tensor_tensor(out=rBs[0:64], in0=qkfs[64:128], in1=sss[64:128].unsqueeze(1).to_broadcast([64, 8, 16]), op=ALU.mult), r=[r_qkfs, r_tb], w=[r_rBs])
                S.op("pool", lambda e: e.tensor_tensor(out=rBs[64:128], in0=qkfs[0:64], in1=sss[0:64].unsqueeze(1).to_broadcast([64, 8, 16]), op=ALU.mult), r=[r_qkfs, r_tb], w=[r_rBs])
                S.op("dve", lambda e: e.tensor_tensor(out=qkTs, in0=rAs, in1=rBs, op=ALU.add), r=[r_rAs, r_rBs], w=[r_qkTs])
                S.op("pool", lambda e: e.tensor_tensor(out=qdTs.rearrange("p h (b l) -> p h b l", b=4), in0=qkTs[:, 0:4, :].rearrange("p h (b l) -> p h b l", b=4), in1=qdecs.unsqueeze(2).to_broadcast([128, 4, 4, 4]), op=ALU.mult), r=[r_qkTs, r_tb], w=[r_qdTs])
                v_s, r_vs = A.get(SB * 512, BF16, "v_s"); v_s = v_s.rearrange("p (b n) -> p b n", b=SB)
                sTs, r_sTs = A.get(64, BF16, "sTs"); sTs = sTs.rearrange("p (b h l) -> p b h l", b=4, h=4)
                kps, r_kps = A.get(512, BF16, "kps"); kps = kps.rearrange("p (h d) -> p h d", h=4)
                S.op("pool", lambda e: e.memset(v_s, 0.0), w=[r_vs])
                S.op("pool", lambda e: e.memset(sTs, 0.0), w=[r_sTs])
                S.op("pool", lambda e: e.memset(kps, 0.0), w=[r_kps])
                sggs, r_sggs = A.get(SB * 512, F32, "sggs"); sggs = sggs.rearrange("p (b n) -> p b n", b=SB)
                for bb in range(SB):
                    bv_, rv_ = ring.get(1)
                    bg_, rg_ = ring.get(1)
                    for (bx, rx, c0) in ((bv_, rv_, 1536), (bg_, rg_, 2048)):
                        for k in range(8):
                            S.op("pe", lambda e, bx=bx, k=k, c0=c0, bb=bb: e.matmul(bank(bx)[0:4, :], lhsT=xsT[:, k, bb * 4:bb * 4 + 4], rhs=w_in_sb[:, k, c0:c0 + 512], start=(k == 0), stop=(k == 7)), r=[r_win, r_xsT], w=rx)
                    S.op("act", lambda e, bv_=bv_, bb=bb: e.copy(out=v_s[0:4, bb, :], in_=bank(bv_)[0:4, :]), r=rv_, w=[r_vs])
                    S.op("act", lambda e, bg_=bg_, bb=bb: e.activation(out=sggs[0:4, bb, :], in_=bank(bg_)[0:4, :], func=AF.Silu), r=rg_, w=[r_sggs])
                    S.op("pool", lambda e, bb=bb: e.tensor_tensor(out=sggs[0:4, bb, :], in0=sggs[0:4, bb, :], in1=gng[0:4], op=ALU.mult), r=[r_sggs, r_c0], w=[r_sggs])
                bs_, rs_ = ring.get(1)
                for bb in range(SB):
                    for h in range(4):
                        c0 = (bb * 4 + h) * 4
                        S.op("pe", lambda e, bb=bb, h=h, c0=c0: e.matmul(bank(bs_)[0:4, c0:c0 + 4], lhsT=qkTs[:, 4 + h, bb * 4:bb * 4 + 4], rhs=qkTs[:, h, bb * 4:bb * 4 + 4], start=True, stop=True), r=[r_qkTs], w=rs_)
                S.op("dve", lambda e: e.tensor_tensor(out=sTs[0:4].rearrange("p b h l -> p b (h l)"), in0=bank(bs_)[0:4, 0:64].rearrange("p (b n) -> p b n", b=4), in1=decTs.unsqueeze(1).to_broadcast([4, 4, 16]), op=ALU.mult), r=rs_ + [r_tb], w=[r_sTs])
                Snew, r_Snew = dbl(512, F32, "Snew", "p (h v) -> p h v", h=4)
                gsts, r_gsts = A.get(40, F32, "gsts")
                on1s, r_on1s = A.get(512, F32, "on1s")
                rets_bf, r_retsb = A.get(512, BF16, "retsb")
                for bb in range(SB):
                    bo_, ro_ = ring.get(1)
                    ovs = bank(bo_)[0:4, :].rearrange("p (h t) -> p h t", h=4)
                    for h in range(4):
                        S.op("pe", lambda e, h=h, bb=bb, ovs=ovs: e.matmul(ovs[:, h, :], lhsT=sTs[:, bb, h, :], rhs=v_s[:, bb, h * 128:(h + 1) * 128], start=True, stop=False), r=[r_sTs, r_vs], w=ro_)
                        S.op("pe", lambda e, h=h, bb=bb, ovs=ovs: e.matmul(ovs[:, h, :], lhsT=qdTs[:, h, bb * 4:bb * 4 + 4], rhs=S0b[:, bb * 4 + h, :], start=False, stop=True), r=[r_qdTs, r_S0b], w=ro_)
                    bkt_, rkt_ = ring.get(1)
                    ktvs = bankbf(bkt_)[0:4, 0:512].rearrange("p (h t) -> p h t", h=4)
                    for h in range(4):
                        S.op("pe", lambda e, h=h, bb=bb, ktvs=ktvs: e.transpose(out=ktvs[:, h, :], in_=qkTs[:, 4 + h, bb * 4:bb * 4 + 4], identity=ident_b[:]), r=[r_qkTs, r_idb], w=rkt_)
                    for h in range(4):
                        S.op("act", lambda e, h=h, ktvs=ktvs: e.activation(out=kps[0:4, h, :], in_=ktvs[:, h, :], func=AF.Identity, bias=0.0, scale=kdecs[:, h:h + 1]), r=rkt_ + [r_tb], w=[r_kps])
                    bds_, rds_ = ring.get(1)
                    dss = bank(bds_).rearrange("p (h t) -> p h t", h=4)
                    for h in range(4):
                        S.op("pe", lambda e, h=h, bb=bb, dss=dss: e.matmul(dss[:, h, :], lhsT=kps[:, h, :], rhs=v_s[:, bb, h * 128:(h + 1) * 128], start=True, stop=True), r=[r_kps, r_vs], w=rds_)
                    sn = Snew[bb % 2]
                    for h in range(4):
                        S.op("dve", lambda e, h=h, bb=bb, dss=dss, sn=sn: e.scalar_tensor_tensor(out=sn[:, h, :], in0=S0f[:, bb * 4 + h, :], scalar=tabs["gCs"][h], in1=dss[:, h, :], op0=ALU.mult, op1=ALU.add), r=rds_ + [r_S0f], w=[r_Snew[bb % 2]])
                    out_toks.append(S.dma("sp", lambda e, bb=bb, sn=sn: e.dma_start(out=rets[bb * 4:(bb + 1) * 4].rearrange("h k v -> k h v"), in_=sn), r=[r_Snew[bb % 2]]))
                    for h in range(4):
                        S.op("dve", lambda e, h=h, ovs=ovs: e.bn_stats(out=gsts[0:4, h * 6:(h + 1) * 6], in_=ovs[:, h, :]), r=ro_, w=[r_gsts])
                        S.op("dve", lambda e, h=h: e.bn_aggr(out=gsts[0:4, 24 + 2 * h:26 + 2 * h], in_=gsts[0:4, h * 6:(h + 1) * 6]), r=[r_gsts], w=[r_gsts])
                    mvs = gsts[0:4, 24:32].rearrange("p (h two) -> p h two", two=2)
                    S.op("act", lambda e: e.activation(out=gsts[0:4, 32:36], in_=mvs[:, :, 1], func=AF.Sqrt, bias=eps_gn[0:4], scale=1.0), r=[r_gsts, r_eps], w=[r_gsts])
                    S.op("dve", lambda e: e.reciprocal(out=gsts[0:4, 32:36], in_=gsts[0:4, 32:36]), r=[r_gsts], w=[r_gsts])
                    S.op("dve", lambda e: e.scalar_tensor_tensor(out=gsts[0:4, 36:40], in0=mvs[:, :, 0], scalar=-1.0, in1=gsts[0:4, 32:36], op0=ALU.mult, op1=ALU.mult), r=[r_gsts], w=[r_gsts])
                    for h in range(4):
                        S.op("act", lambda e, h=h, ovs=ovs: e.activation(out=on1s[0:4, h * 128:(h + 1) * 128], in_=ovs[:, h, :], func=AF.Identity, bias=gsts[0:4, 36 + h:37 + h], scale=gsts[0:4, 32 + h:33 + h]), r=ro_ + [r_gsts], w=[r_on1s])
                    S.op("pool", lambda e, bb=bb: e.tensor_tensor(out=rets_bf[0:4, :], in0=on1s[0:4, :], in1=sggs[0:4, bb, :], op=ALU.mult), r=[r_on1s, r_sggs], w=[r_retsb])
                    brt_, rrt_ = ring.get(1)
                    rtvs = bankbf(brt_)[:, 0:16].rearrange("p (h t) -> p h t", h=4)
                    for h in range(4):
                        S.op("pe", lambda e, h=h, rtvs=rtvs: e.transpose(out=rtvs[:, h, :], in_=rets_bf[0:4, h * 128:(h + 1) * 128], identity=ident_b[0:4, 0:4]), r=[r_retsb, r_idb], w=rrt_)
                    S.op("dve", lambda e, rtvs=rtvs, bb=bb: e.tensor_copy(out=catTs[:, 4:8, bb * 4:bb * 4 + 4], in_=rtvs), r=rrt_, w=[r_catTs])
                by_, ry_ = ring.get(2)
                for hf in range(2):
                    for c in range(8):
                        S.op("pe", lambda e, hf=hf, c=c: e.matmul(bank(by_ + hf)[0:NS, :], lhsT=catTs[:, c, :], rhs=w_o_sb[:, c, hf * 512:(hf + 1) * 512], start=(c == 0), stop=(c == 7)), r=[r_catTs, r_wo], w=[ry_[hf]])
                zs, r_zs = A.get(D, F32, "zs")
                scs, r_scs = A.get(16, F32, "scs")
                S.op("dve", lambda e: e.scalar_tensor_tensor(out=zs[0:NS, :], in0=XS[:], scalar=ALPHA, in1=bank(by_, 2)[0:NS, :], op0=ALU.mult, op1=ALU.add), r=[r_XS] + ry_, w=[r_zs])
                layer_norm_tile(zs[0:NS, :], r_zs, XS[:], r_XS, NS, scs, r_scs, lnt0, r_lnt0)

            if with_sample:
                sample_l0()
            chk('l0')
            S.set_phase('ffn0')
            ffn(0, False)
            chk('ffn0')
            S.set_phase('l1')

            A.at(0)
            wdq_sb, r_wdq = A.get(8 * 512, BF16, "wdq"); wdq_sb = wdq_sb.rearrange("p (k n) -> p k n", k=8)
            wuq_sb, r_wuq = A.get(4 * 1536, BF16, "wuq"); wuq_sb = wuq_sb.rearrange("p (k n) -> p k n", k=4)
            wdkv_sb, r_wdkv = A.get(8 * 320, BF16, "wdkv"); wdkv_sb = wdkv_sb.rearrange("p (k n) -> p k n", k=8)
            wukT_sb, r_wukT = A.get(8 * 256, BF16, "wukT"); wukT_sb = wukT_sb.rearrange("p (h c) -> p h c", h=8)
            wuv_sb, r_wuv = A.get(2 * 1024, BF16, "wuv"); wuv_sb = wuv_sb.rearrange("p (k n) -> p k n", k=2)
            wo1_sb, r_wo1 = A.get(8 * 1024, BF16, "wo1"); wo1_sb = wo1_sb.rearrange("p (k n) -> p k n", k=8)
            c1buf, r_c1 = A.get(512 + 256, F32, "c1")
            qng = c1buf[:, 0:512]
            kvng = c1buf[:, 512:768]
            lnt1, r_lnt1 = A.get(2 * D, F32, "lnt1")
            ckvT, r_ckvT = A.get(2 * SEQ, BF16, "ckvT", nres=NT); ckvT = ckvT.rearrange("p (k n) -> p k n", k=2)
            ckvK, r_ckvK = A.get(NT * 256, BF16, "ckvK", nres=NT); ckvK = ckvK.rearrange("p (t c) -> p t c", t=NT)
            kpeT, r_kpeT = A.get(SEQ, BF16, "kpeT", nres=NT)
            S.dma("pool", lambda e: e.dma_start(out=wdq_sb, in_=w_dq.rearrange("(k p) n -> p k n", p=128)), w=[r_wdq])
            S.dma("pool", lambda e: e.dma_start(out=wdkv_sb, in_=w_dkv.rearrange("(k p) n -> p k n", p=128)), w=[r_wdkv])
            S.dma("pool", lambda e: e.dma_start(out=wuq_sb, in_=w_uq.rearrange("(k p) n -> p k n", p=128)), w=[r_wuq])
            S.dma("pool", lambda e: e.dma_start(out=wukT_sb, in_=w_ukT), w=[r_wukT])
            S.dma("pool", lambda e: e.dma_start(out=wuv_sb, in_=w_uv.rearrange("(k p) n -> p k n", p=128)), w=[r_wuv])
            S.dma("pool", lambda e: e.dma_start(out=wo1_sb, in_=w_o1.rearrange("(k p) n -> p k n", p=128)), w=[r_wo1])
            S.dma("sp", lambda e: e.dma_start(out=qng, in_=qn_g.partition_broadcast(128)), w=[r_c1])
            S.dma("sp", lambda e: e.dma_start(out=kvng, in_=kvn_g.partition_broadcast(128)), w=[r_c1])
            load_ln(lnt1, r_lnt1, ln_mix_g[1], ln_mix_b[1])
            L1_BUF = A.cur

            xT1, r_xT1 = dbl(1024, BF16, "xT1", "p (k t) -> p k t", k=8)
            rt1, r_rt1 = dbl(128, F32, "rt1")
            sc1, r_sc1 = dbl(16, F32, "sc1")
            cqn, r_cqn = dbl(512, BF16, "cqn")
            cqnT, r_cqnT = dbl(512, BF16, "cqnT", "p (k t) -> p k t", k=4)
            ckv_f, r_ckvf = dbl(256, F32, "ckv_f")
            kpe_f, r_kpef = dbl(64, F32, "kpe_f")
            kpe_t, r_kpet = A.get(128, F32, "kpe_t")
            kpe_b, r_kpeb = dbl(128, BF16, "kpe_b")
            qnT0_, r_qnT0_ = A.get(1024, BF16, "qnT"); qnT0_ = qnT0_.rearrange("p (h t) -> p h t", h=8); qnT = [qnT0_, qnT0_]; r_qnT = [r_qnT0_, r_qnT0_]
            qpe_t, r_qpet = A.get(1024, F32, "qpe_t")
            junk, r_junk = qpe_t[:, 0:512], r_qpet
            qpe_b, r_qpeb = dbl(512, BF16, "qpe_b")
            qpeT, r_qpeT = dbl(1024, BF16, "qpeT", "p (h t) -> p h t", h=8)
            qlT, r_qlT = dbl(2048, BF16, "qlT", "p (h k t) -> p h k t", h=8, k=2)
            p_sb, r_p = dbl(SEQ, BF16, "p_sb")
            pT_sb, r_pT = dbl(SEQ, BF16, "pT_sb")
            st1, r_st1 = dbl(8, F32, "st1")
            ol_b, r_olb = dbl(256, BF16, "ol_b")
            olT, r_olT = dbl(256, BF16, "olT", "p (k t) -> p k t", k=2)
            oT10_, r_oT10_ = A.get(1024, BF16, "oT1"); oT10_ = oT10_.rearrange("p (h t) -> p h t", h=8); oT1 = [oT10_, oT10_]; r_oT1 = [r_oT10_, r_oT10_]
            z10, r_z10 = A.get(D, F32, "z1"); z1 = [z10, z10]; r_z1 = [r_z10, r_z10]
            print("layer1 arena bytes", A.cur)

            for i_ in range(2):
                S.op("pool", lambda e, i_=i_: e.memset(qpeT[i_], 0.0), w=[r_qpeT[i_]])
            def l1A(t):
                pb = t % 2
                tok = slice(t * 128, (t + 1) * 128)
                S.dma("sp", lambda e, pb=pb, t=t: e.dma_start(out=rt1[pb][:, 0:64], in_=tdr["cc1"][:, t, :]), w=[r_rt1[pb]])
                S.dma("sp", lambda e, pb=pb, t=t: e.dma_start(out=rt1[pb][:, 64:128], in_=tdr["ss1"][:, t, :]), w=[r_rt1[pb]])
                transpose_x(X[:, t, :], rX[t], xT1[pb], r_xT1[pb])
                bcq, rcq = ring.get(1)
                bkv, rkv = ring.get(1)
                for k in range(8):
                    S.op("pe", lambda e, k=k, pb=pb, bcq=bcq: e.matmul(bank(bcq), lhsT=xT1[pb][:, k, :], rhs=wdq_sb[:, k, :], start=(k == 0), stop=(k == 7)), r=[r_xT1[pb], r_wdq], w=rcq)
                for k in range(8):
                    S.op("pe", lambda e, k=k, pb=pb, bkv=bkv: e.matmul(bank(bkv)[:, 0:320], lhsT=xT1[pb][:, k, :], rhs=wdkv_sb[:, k, :], start=(k == 0), stop=(k == 7)), r=[r_xT1[pb], r_wdkv], w=rkv)
                sc = sc1[pb]
                S.op("act", lambda e, bcq=bcq, sc=sc: e.activation(out=junk, in_=bank(bcq), func=AF.Square, accum_out=sc[:, 0:1]), r=rcq, w=[r_junk, r_sc1[pb]])
                S.op("act", lambda e, sc=sc: e.activation(out=sc[:, 1:2], in_=sc[:, 0:1], func=AF.Sqrt, bias=eps_rms, scale=1.0 / 512), r=[r_sc1[pb], r_eps], w=[r_sc1[pb]])
                S.op("dve", lambda e, sc=sc: e.reciprocal(out=sc[:, 1:2], in_=sc[:, 1:2]), r=[r_sc1[pb]], w=[r_sc1[pb]])
                S.op("dve", lambda e, bcq=bcq, sc=sc, pb=pb: e.scalar_tensor_tensor(out=cqn[pb], in0=bank(bcq), scalar=sc[:, 1:2], in1=qng, op0=ALU.mult, op1=ALU.mult), r=rcq + [r_sc1[pb], r_c1], w=[r_cqn[pb]])
                S.op("act", lambda e, bkv=bkv, sc=sc: e.activation(out=junk[:, 0:256], in_=bank(bkv)[:, 0:256], func=AF.Square, accum_out=sc[:, 2:3]), r=rkv, w=[r_junk, r_sc1[pb]])
                S.op("act", lambda e, sc=sc: e.activation(out=sc[:, 3:4], in_=sc[:, 2:3], func=AF.Sqrt, bias=eps_rms, scale=1.0 / 256), r=[r_sc1[pb], r_eps], w=[r_sc1[pb]])
                S.op("dve", lambda e, sc=sc: e.reciprocal(out=sc[:, 3:4], in_=sc[:, 3:4]), r=[r_sc1[pb]], w=[r_sc1[pb]])
                S.op("dve", lambda e, bkv=bkv, sc=sc, pb=pb: e.scalar_tensor_tensor(out=ckv_f[pb], in0=bank(bkv)[:, 0:256], scalar=sc[:, 3:4], in1=kvng, op0=ALU.mult, op1=ALU.mult), r=rkv + [r_sc1[pb], r_c1], w=[r_ckvf[pb]])
                out_toks.append(S.dma("sp", lambda e, pb=pb, tok=tok: e.dma_start(out=ckvp[tok, :], in_=ckv_f[pb]), r=[r_ckvf[pb]]))
                S.op("pool", lambda e, pb=pb, t=t: e.tensor_copy(out=ckvK[:, t, :], in_=ckv_f[pb]), r=[r_ckvf[pb]], w=[r_ckvK[t]])
                S.op("dve", lambda e, bkv=bkv, pb=pb: e.tensor_tensor(out=kpe_t[:, 0:64], in0=bank(bkv)[:, 256:320], in1=rt1[pb][:, 0:64], op=ALU.mult), r=rkv + [r_rt1[pb]], w=[r_kpet])
                S.op("dve", lambda e, bkv=bkv, pb=pb: e.tensor_tensor(out=kpe_t[:, 64:96], in0=bank(bkv)[:, 288:320], in1=rt1[pb][:, 64:96], op=ALU.mult), r=rkv + [r_rt1[pb]], w=[r_kpet])
                S.op("dve", lambda e, bkv=bkv, pb=pb: e.tensor_tensor(out=kpe_t[:, 96:128], in0=bank(bkv)[:, 256:288], in1=rt1[pb][:, 96:128], op=ALU.mult), r=rkv + [r_rt1[pb]], w=[r_kpet])
                S.op("pool", lambda e, pb=pb: e.tensor_tensor(out=kpe_f[pb], in0=kpe_t[:, 0:64], in1=kpe_t[:, 64:128], op=ALU.add), r=[r_kpet], w=[r_kpef[pb]])
                out_toks.append(S.dma("sp", lambda e, pb=pb, tok=tok: e.dma_start(out=kpep[tok, :], in_=kpe_f[pb]), r=[r_kpef[pb]]))
                S.op("pool", lambda e, pb=pb: e.tensor_copy(out=kpe_b[pb][:, 0:64], in_=kpe_f[pb]), r=[r_kpef[pb]], w=[r_kpeb[pb]])
                S.op("pool", lambda e, pb=pb: e.tensor_copy(out=kpe_b[pb][:, 64:128], in_=kpe_f[pb]), r=[r_kpef[pb]], w=[r_kpeb[pb]])
                btr, rtr = ring.get(2)
                trv = bankbf(btr, 2)
                for k in range(4):
                    S.op("pe", lambda e, k=k, trv=trv, pb=pb: e.transpose(out=trv[:, k * 128:(k + 1) * 128], in_=cqn[pb][:, k * 128:(k + 1) * 128], identity=ident_b[:]), r=[r_cqn[pb], r_idb], w=rtr)
                for k in range(2):
                    S.op("pe", lambda e, k=k, trv=trv, t=t: e.transpose(out=trv[:, 1024 + k * 128:1024 + (k + 1) * 128], in_=ckvK[:, t, k * 128:(k + 1) * 128], identity=ident_b[:]), r=[r_ckvK[t], r_idb], w=rtr)
                S.op("pe", lambda e, trv=trv, pb=pb: e.transpose(out=trv[:, 1280:1408], in_=kpe_b[pb], identity=ident_b[:]), r=[r_kpeb[pb], r_idb], w=rtr)
                S.op("act", lambda e, trv=trv, pb=pb: e.copy(out=cqnT[pb], in_=trv[:, 0:512].rearrange("p (k t) -> p k t", k=4)), r=rtr, w=[r_cqnT[pb]])
                S.op("dve", lambda e, trv=trv, tok=tok, t=t: e.tensor_copy(out=ckvT[:, :, tok], in_=trv[:, 1024:1280].rearrange("p (k t) -> p k t", k=2)), r=rtr, w=[r_ckvT[t]])
                S.op("dve", lambda e, trv=trv, tok=tok, t=t: e.tensor_copy(out=kpeT[:, tok], in_=trv[:, 1280:1408]), r=rtr, w=[r_kpeT[t]])
                for hh in range(2):
                    bqn, rqn = ring.get(1)
                    qv = bank(bqn).rearrange("p (h t) -> p h t", h=4)
                    for h4 in range(4):
                        h = hh * 4 + h4
                        for k in range(4):
                            S.op("pe", lambda e, qv=qv, h4=h4, h=h, k=k, pb=pb: e.matmul(qv[:, h4, :], lhsT=wuq_sb[:, k, h * 128:(h + 1) * 128], rhs=cqnT[pb][:, k, :], start=(k == 0), stop=(k == 3)), r=[r_wuq, r_cqnT[pb]], w=rqn)
                    evac("act" if hh == 0 else "dve", qnT[pb][:, hh * 4:hh * 4 + 4, :], qv, rqn, [r_qnT[pb]])
                bqp, rqp = ring.get(1)
                for k in range(4):
                    S.op("pe", lambda e, k=k, bqp=bqp, pb=pb: e.matmul(bank(bqp), lhsT=cqnT[pb][:, k, :], rhs=wuq_sb[:, k, 1024:1536], start=(k == 0), stop=(k == 3)), r=[r_wuq, r_cqnT[pb]], w=rqp)
                qpv = bank(bqp).rearrange("p (h d) -> p h d", h=8)
                qA = qpe_t[:, 0:512].rearrange("p (h d) -> p h d", h=8)
                qB = qpe_t[:, 512:1024].rearrange("p (h d) -> p h d", h=8)
                ccb = rt1[pb][:, 0:64].unsqueeze(1)
                ssb1 = rt1[pb][:, 64:96].unsqueeze(1)
                ssb2 = rt1[pb][:, 96:128].unsqueeze(1)
                S.op("dve", lambda e, qpv=qpv, ccb=ccb: e.tensor_tensor(out=qA, in0=qpv, in1=ccb.to_broadcast([128, 8, 64]), op=ALU.mult), r=rqp + [r_rt1[pb]], w=[r_qpet])
                S.op("dve", lambda e, qpv=qpv, ssb1=ssb1: e.tensor_tensor(out=qB[:, :, 0:32], in0=qpv[:, :, 32:64], in1=ssb1.to_broadcast([128, 8, 32]), op=ALU.mult), r=rqp + [r_rt1[pb]], w=[r_qpet])
                S.op("dve", lambda e, qpv=qpv, ssb2=ssb2: e.tensor_tensor(out=qB[:, :, 32:64], in0=qpv[:, :, 0:32], in1=ssb2.to_broadcast([128, 8, 32]), op=ALU.mult), r=rqp + [r_rt1[pb]], w=[r_qpet])
                S.op("pool", lambda e, pb=pb: e.tensor_tensor(out=qpe_b[pb], in0=qpe_t[:, 0:512], in1=qpe_t[:, 512:1024], op=ALU.add), r=[r_qpet], w=[r_qpeb[pb]])
                btq, rtq = ring.get(1)
                tqv = bankbf(btq)[:, 0:512].rearrange("p (a t) -> p a t", a=4)
                for a in range(4):
                    S.op("pe", lambda e, a=a, tqv=tqv, pb=pb: e.transpose(out=tqv[:, a, :], in_=qpe_b[pb][:, a * 128:(a + 1) * 128], identity=ident_b[:]), r=[r_qpeb[pb], r_idb], w=rtq)
                qz = qpeT[pb].rearrange("p (a two) t -> p a two t", two=2)
                S.op("act", lambda e, tqv=tqv, qz=qz: e.copy(out=qz[0:64, :, 0, :], in_=tqv[0:64]), r=rtq, w=[r_qpeT[pb]])
                S.op("dve", lambda e, tqv=tqv, qz=qz: e.tensor_copy(out=qz[64:128, :, 1, :], in_=tqv[64:128]), r=rtq, w=[r_qpeT[pb]])
                for hh in range(4):
                    bql, rql = ring.get(1)
                    qlv = bank(bql).rearrange("p (h k t) -> p h k t", h=2, k=2)
                    for h2 in range(2):
                        h = hh * 2 + h2
                        for k in range(2):
                            S.op("pe", lambda e, qlv=qlv, h2=h2, h=h, k=k, pb=pb: e.matmul(qlv[:, h2, k, :], lhsT=wukT_sb[:, h, k * 128:(k + 1) * 128], rhs=qnT[pb][:, h, :], start=True, stop=True), r=[r_wukT, r_qnT[pb]], w=rql)
                    evac("act" if hh % 2 == 0 else "dve", qlT[pb][:, hh * 2:hh * 2 + 2, :, :], qlv, rql, [r_qlT[pb]])

            l1ctx = {}
            def l1P1a(t, h):
                pb = t % 2
                hp = h % 2
                nk = (t + 1) * 128
                nb = (nk + 511) // 512
                nbanks = 1 if nb == 1 else (2 if nb == 2 else 4)
                bs, rs = ring.get(nbanks)
                sfull = bank(bs, nbanks)
                stt = st1[hp]
                chs = [(c0_, min(nk, c0_ + 1024)) for c0_ in range(0, nk, 1024)]
                for ci, (a_, b_) in enumerate(chs):
                    for gi in range(a_ // 512, (b_ + 511) // 512):
                        k0 = gi * 512
                        kn = min(512, nk - k0)
                        ks = slice(k0, k0 + kn)
                        kres = r_ckvT[gi * 4:gi * 4 + (kn + 127) // 128]
                        kpres = r_kpeT[gi * 4:gi * 4 + (kn + 127) // 128]
                        last_diag = (gi == nb - 1)
                        parts = [(ks, False)] if not last_diag else ([(slice(k0, nk - 128), False)] if nk - 128 > k0 else []) + [(slice(nk - 128, nk), True)]
                        for (ks, dg) in parts:
                            S.op("pe", lambda e, sfull=sfull, ks=ks, h=h, pb=pb: e.matmul(sfull[:, ks], lhsT=qlT[pb][:, h, 0, :], rhs=ckvT[:, 0, ks], start=True, stop=False), r=[r_qlT[pb]] + kres, w=[rs[gi]])
                            S.op("pe", lambda e, sfull=sfull, ks=ks, h=h, pb=pb: e.matmul(sfull[:, ks], lhsT=qlT[pb][:, h, 1, :], rhs=ckvT[:, 1, ks], start=False, stop=False), r=[r_qlT[pb]] + kres, w=[rs[gi]])
                            S.op("pe", lambda e, sfull=sfull, ks=ks, h=h, pb=pb, dg=dg: e.matmul(sfull[:, ks], lhsT=qpeT[pb][:, h, :], rhs=kpeT[:, ks], start=False, stop=(not dg)), r=[r_qpeT[pb]] + kpres, w=[rs[gi]])
                            if dg:
                                S.op("pe", lambda e, sfull=sfull, ks=ks: e.matmul(sfull[:, ks], lhsT=ident_b[:], rhs=maskd[:], start=False, stop=True), r=[r_idb, r_maskd], w=[rs[gi]])
                    S.op("dve", lambda e, sfull=sfull, a_=a_, b_=b_, ci=ci, stt=stt: e.reduce_max(out=stt[:, 4 + ci:5 + ci], in_=sfull[:, a_:b_], axis=AX.X), r=rs[a_ // 512:(b_ + 511) // 512], w=[r_st1[hp]])
                if len(chs) == 2:
                    S.op("dve", lambda e, stt=stt: e.tensor_tensor(out=stt[:, 4:5], in0=stt[:, 4:5], in1=stt[:, 5:6], op=ALU.max), r=[r_st1[hp]], w=[r_st1[hp]])
                S.op("dve", lambda e, stt=stt: e.tensor_scalar(out=stt[:, 1:2], in0=stt[:, 4:5], scalar1=-MLA_SCALE, scalar2=None, op0=ALU.mult), r=[r_st1[hp]], w=[r_st1[hp]])
                l1ctx[(t, h)] = (sfull, rs, nb, chs)

            def l1P1b(t, h):
                pb = t % 2
                hp = h % 2
                nk = (t + 1) * 128
                stt = st1[hp]
                sfull, rs, nb, chs = l1ctx[(t, h)]
                for ci, (a_, b_) in enumerate(chs):
                    S.op("act", lambda e, sfull=sfull, a_=a_, b_=b_, ci=ci, stt=stt, hp=hp: e.activation(out=p_sb[hp][:, a_:b_], in_=sfull[:, a_:b_], func=AF.Exp, bias=stt[:, 1:2], scale=MLA_SCALE, accum_out=stt[:, 6 + ci:7 + ci]), r=rs[0:nb] + [r_st1[hp]], w=[r_p[hp], r_st1[hp]])

            def l1P2a(t, h):
                pb = t % 2
                hp = h % 2
                nk = (t + 1) * 128
                stt = st1[hp]
                nkt = t + 1
                for g0 in range(0, nkt, 4):
                    bpt, rpt = ring.get(1)
                    ptv = bankbf(bpt)
                    n_ = min(4, nkt - g0)
                    for i_ in range(n_):
                        S.op("pe", lambda e, i_=i_, g0=g0, ptv=ptv, hp=hp: e.transpose(out=ptv[:, i_ * 128:(i_ + 1) * 128], in_=p_sb[hp][:, (g0 + i_) * 128:(g0 + i_ + 1) * 128], identity=ident_b[:]), r=[r_p[hp], r_idb], w=rpt)
                    evac("act" if (g0 // 4) % 2 == 0 else "dve", pT_sb[hp][:, g0 * 128:(g0 + n_) * 128], ptv[:, 0:n_ * 128], rpt, [r_pT[hp]])

            def l1P2b(t, h):
                pb = t % 2
                hp = h % 2
                nk = (t + 1) * 128
                stt = st1[hp]
                nkt = t + 1
                bol, rol = ring.get(1)
                for kt in range(nkt):
                    S.op("pe", lambda e, kt=kt, bol=bol, hp=hp: e.matmul(bank(bol)[:, 0:256], lhsT=pT_sb[hp][:, kt * 128:(kt + 1) * 128], rhs=ckvK[:, kt, :], start=(kt == 0), stop=(kt == nkt - 1)), r=[r_pT[hp], r_ckvK[kt]], w=rol)
                chs = [(c0_, min(nk, c0_ + 1024)) for c0_ in range(0, nk, 1024)]
                if len(chs) == 2:
                    S.op("dve", lambda e, stt=stt: e.tensor_tensor(out=stt[:, 6:7], in0=stt[:, 6:7], in1=stt[:, 7:8], op=ALU.add), r=[r_st1[hp]], w=[r_st1[hp]])
                S.op("dve", lambda e, stt=stt: e.reciprocal(out=stt[:, 3:4], in_=stt[:, 6:7]), r=[r_st1[hp]], w=[r_st1[hp]])
                S.op("dve", lambda e, bol=bol, stt=stt, hp=hp: e.tensor_scalar(out=ol_b[hp], in0=bank(bol)[:, 0:256], scalar1=stt[:, 3:4], scalar2=None, op0=ALU.mult), r=rol + [r_st1[hp]], w=[r_olb[hp]])
                bot, rot = ring.get(1)
                otv = bankbf(bot)[:, 0:256].rearrange("p (k t) -> p k t", k=2)
                for k in range(2):
                    S.op("pe", lambda e, k=k, otv=otv, hp=hp: e.transpose(out=otv[:, k, :], in_=ol_b[hp][:, k * 128:(k + 1) * 128], identity=ident_b[:]), r=[r_olb[hp], r_idb], w=rot)
                S.op("dve", lambda e, otv=otv, hp=hp: e.tensor_copy(out=olT[hp], in_=otv), r=rot, w=[r_olT[hp]])
                boh, roh = ring.get(1)
                for k in range(2):
                    S.op("pe", lambda e, k=k, boh=boh, h=h, hp=hp: e.matmul(bank(boh)[:, 0:128], lhsT=wuv_sb[:, k, h * 128:(h + 1) * 128], rhs=olT[hp][:, k, :], start=(k == 0), stop=(k == 1)), r=[r_wuv, r_olT[hp]], w=roh)
                evac("act" if h % 2 == 0 else "dve", oT1[pb][:, h, :], bank(boh)[:, 0:128], roh, [r_oT1[pb]])


            def l1out(t):
                pb = t % 2
                by, ry = ring.get(2)
                for hf in range(2):
                    for c in range(8):
                        S.op("pe", lambda e, hf=hf, c=c, by=by, pb=pb: e.matmul(bank(by + hf), lhsT=oT1[pb][:, c, :], rhs=wo1_sb[:, c, hf * 512:(hf + 1) * 512], start=(c == 0), stop=(c == 7)), r=[r_oT1[pb], r_wo1], w=[ry[hf]])
                S.op("dve", lambda e, by=by, pb=pb, t=t: e.scalar_tensor_tensor(out=z1[pb], in0=X[:, t, :], scalar=ALPHA, in1=bank(by, 2), op0=ALU.mult, op1=ALU.add), r=[rX[t]] + ry, w=[r_z1[pb]])
                layer_norm_tile(z1[pb], r_z1[pb], X[:, t, :], rX[t], 128, sc1[pb], r_sc1[pb], lnt1, r_lnt1)


            l1A(0)
            for t in range(NT):
                l1P1a(t, 0)
                l1P1b(t, 0)
                for h in range(1, 8):
                    l1P2a(t, h - 1)
                    l1P1a(t, h)
                    l1P1b(t, h)
                    l1P2b(t, h - 1)
                    if h == 4 and t + 1 < NT:
                        l1A(t + 1)
                l1P2a(t, 7)
                l1P2b(t, 7)
                l1out(t)
                chk(f'l1t{t}')

            def sample_l1():
                A.at(L1_BUF)
                NSTEP = NPAGES * 128 // 1024
                tb1, r_tb1 = A.get(128, F32, "tb1")
                cc1s = tb1[0:NS, 0:64]; ss1s = tb1[0:NS, 64:128]
                S.dma("sp", lambda e: e.dma_start(out=cc1s, in_=tdr["cc1s"]), w=[r_tb1])
                S.dma("sp", lambda e: e.dma_start(out=ss1s, in_=tdr["ss1s"]), w=[r_tb1])
                msk, r_msk = A.get(SB * NS, BF16, "msk")
                S.dma("pool", lambda e: e.dma_start(out=msk[0:32, :], in_=tdr["masksb"]), w=[r_msk])
                mskv = msk.rearrange("p (b t) -> p b t", b=SB)
                pt_sb, r_pt = A.get(SB, I32, "pt_sb")
                S.dma("sp", lambda e: e.dma_start(out=pt_sb, in_=ptab), w=[r_pt])
                xsT1, r_xsT1 = A.get(128, BF16, "xsT1"); xsT1 = xsT1.rearrange("p (k t) -> p k t", k=8)
                transpose_x(XS[:], r_XS, xsT1, r_xsT1, npart=NS)
                bcq, rcq = ring.get(1)
                bkv, rkv = ring.get(1)
                for k in range(8):
                    S.op("pe", lambda e, k=k: e.matmul(bank(bcq)[0:NS, :], lhsT=xsT1[:, k, :], rhs=wdq_sb[:, k, :], start=(k == 0), stop=(k == 7)), r=[r_xsT1, r_wdq], w=rcq)
                for k in range(8):
                    S.op("pe", lambda e, k=k: e.matmul(bank(bkv)[0:NS, 0:320], lhsT=xsT1[:, k, :], rhs=wdkv_sb[:, k, :], start=(k == 0), stop=(k == 7)), r=[r_xsT1, r_wdkv], w=rkv)
                jk, r_jk = A.get(512, F32, "jk")
                scq, r_scq = A.get(16, F32, "scq")
                cqns, r_cqns = A.get(512, BF16, "cqns")
                ckvsf, r_ckvsf = A.get(256, F32, "ckvsf")
                ckvsb, r_ckvsb = A.get(256, BF16, "ckvsb")
                kpet, r_kpets = A.get(128, F32, "kpets")
                kpesf, r_kpesf = A.get(64, F32, "kpesf")
                kpesb, r_kpesb = A.get(128, BF16, "kpesb")
                sc = scq[0:NS]
                S.op("act", lambda e: e.activation(out=jk[0:NS, :], in_=bank(bcq)[0:NS, :], func=AF.Square, accum_out=sc[:, 0:1]), r=rcq, w=[r_jk, r_scq])
                S.op("act", lambda e: e.activation(out=sc[:, 1:2], in_=sc[:, 0:1], func=AF.Sqrt, bias=eps_rms[0:NS], scale=1.0 / 512), r=[r_scq, r_eps], w=[r_scq])
                S.op("dve", lambda e: e.reciprocal(out=sc[:, 1:2], in_=sc[:, 1:2]), r=[r_scq], w=[r_scq])
                S.op("dve", lambda e: e.scalar_tensor_tensor(out=cqns[0:NS, :], in0=bank(bcq)[0:NS, :], scalar=sc[:, 1:2], in1=qng[0:NS], op0=ALU.mult, op1=ALU.mult), r=rcq + [r_scq, r_c1], w=[r_cqns])
                S.op("act", lambda e: e.activation(out=jk[0:NS, 0:256], in_=bank(bkv)[0:NS, 0:256], func=AF.Square, accum_out=sc[:, 2:3]), r=rkv, w=[r_jk, r_scq])
                S.op("act", lambda e: e.activation(out=sc[:, 3:4], in_=sc[:, 2:3], func=AF.Sqrt, bias=eps_rms[0:NS], scale=1.0 / 256), r=[r_scq, r_eps], w=[r_scq])
                S.op("dve", lambda e: e.reciprocal(out=sc[:, 3:4], in_=sc[:, 3:4]), r=[r_scq], w=[r_scq])
                S.op("dve", lambda e: e.scalar_tensor_tensor(out=ckvsf[0:NS, :], in0=bank(bkv)[0:NS, 0:256], scalar=sc[:, 3:4], in1=kvng[0:NS], op0=ALU.mult, op1=ALU.mult), r=rkv + [r_scq, r_c1], w=[r_ckvsf])
                out_toks.append(S.dma("sp", lambda e: e.dma_start(out=ckvs, in_=ckvsf[0:NS, :]), r=[r_ckvsf]))
                S.op("pool", lambda e: e.tensor_copy(out=ckvsb[0:NS, :], in_=ckvsf[0:NS, :]), r=[r_ckvsf], w=[r_ckvsb])
                S.op("dve", lambda e: e.tensor_tensor(out=kpet[0:NS, 0:64], in0=bank(bkv)[0:NS, 256:320], in1=cc1s, op=ALU.mult), r=rkv + [r_tb1], w=[r_kpets])
                S.op("dve", lambda e: e.tensor_tensor(out=kpet[0:NS, 64:96], in0=bank(bkv)[0:NS, 288:320], in1=ss1s[:, 0:32], op=ALU.mult), r=rkv + [r_tb1], w=[r_kpets])
                S.op("dve", lambda e: e.tensor_tensor(out=kpet[0:NS, 96:128], in0=bank(bkv)[0:NS, 256:288], in1=ss1s[:, 32:64], op=ALU.mult), r=rkv + [r_tb1], w=[r_kpets])
                S.op("pool", lambda e: e.tensor_tensor(out=kpesf[0:NS, :], in0=kpet[0:NS, 0:64], in1=kpet[0:NS, 64:128], op=ALU.add), r=[r_kpets], w=[r_kpesf])
                out_toks.append(S.dma("sp", lambda e: e.dma_start(out=kpes, in_=kpesf[0:NS, :]), r=[r_kpesf]))
                S.op("pool", lambda e: e.tensor_copy(out=kpesb[0:NS, 0:64], in_=kpesf[0:NS, :]), r=[r_kpesf], w=[r_kpesb])
                S.op("pool", lambda e: e.tensor_copy(out=kpesb[0:NS, 64:128], in_=kpesf[0:NS, :]), r=[r_kpesf], w=[r_kpesb])
                btr, rtr = ring.get(1)
                trv = bankbf(btr)
                for k in range(4):
                    S.op("pe", lambda e, k=k: e.transpose(out=trv[:, k * 16:(k + 1) * 16], in_=cqns[0:NS, k * 128:(k + 1) * 128], identity=ident_b[0:NS, 0:NS]), r=[r_cqns, r_idb], w=rtr)
                for k in range(2):
                    S.op("pe", lambda e, k=k: e.transpose(out=trv[:, 64 + k * 16:64 + (k + 1) * 16], in_=ckvsb[0:NS, k * 128:(k + 1) * 128], identity=ident_b[0:NS, 0:NS]), r=[r_ckvsb, r_idb], w=rtr)
                S.op("pe", lambda e: e.transpose(out=trv[:, 96:112], in_=kpesb[0:NS, :], identity=ident_b[0:NS, 0:NS]), r=[r_kpesb, r_idb], w=rtr)
                trs, r_trs = A.get(112, BF16, "trs")
                S.op("act", lambda e: e.copy(out=trs, in_=trv[:, 0:112]), r=rtr, w=[r_trs])
                cqnTs = trs[:, 0:64].rearrange("p (k t) -> p k t", k=4)
                ckvTs = trs[:, 64:96].rearrange("p (k t) -> p k t", k=2)
                kpeTs = trs[:, 96:112]
                bqn, rqn = ring.get(1)
                qnv = bank(bqn)[:, 0:128].rearrange("p (h t) -> p h t", h=8)
                for h in range(8):
                    for k in range(4):
                        S.op("pe", lambda e, h=h, k=k: e.matmul(qnv[:, h, :], lhsT=wuq_sb[:, k, h * 128:(h + 1) * 128], rhs=cqnTs[:, k, :], start=(k == 0), stop=(k == 3)), r=[r_wuq, r_trs], w=rqn)
                qnTs, r_qnTs = A.get(128, BF16, "qnTs"); qnTs = qnTs.rearrange("p (h t) -> p h t", h=8)
                S.op("act", lambda e: e.copy(out=qnTs, in_=qnv), r=rqn, w=[r_qnTs])
                bqp, rqp = ring.get(1)
                for k in range(4):
                    S.op("pe", lambda e, k=k: e.matmul(bank(bqp)[0:NS, :], lhsT=cqnTs[:, k, :], rhs=wuq_sb[:, k, 1024:1536], start=(k == 0), stop=(k == 3)), r=[r_wuq, r_trs], w=rqp)
                qpv = bank(bqp)[0:NS, :].rearrange("p (h d) -> p h d", h=8)
                qpt, r_qpt = A.get(1024, F32, "qpt")
                qA = qpt[0:NS, 0:512].rearrange("p (h d) -> p h d", h=8)
                qB = qpt[0:NS, 512:1024].rearrange("p (h d) -> p h d", h=8)
                S.op("dve", lambda e: e.tensor_tensor(out=qA, in0=qpv, in1=cc1s.unsqueeze(1).to_broadcast([NS, 8, 64]), op=ALU.mult), r=rqp + [r_tb1], w=[r_qpt])
                S.op("dve", lambda e: e.tensor_tensor(out=qB[:, :, 0:32], in0=qpv[:, :, 32:64], in1=ss1s[:, 0:32].unsqueeze(1).to_broadcast([NS, 8, 32]), op=ALU.mult), r=rqp + [r_tb1], w=[r_qpt])
                S.op("dve", lambda e: e.tensor_tensor(out=qB[:, :, 32:64], in0=qpv[:, :, 0:32], in1=ss1s[:, 32:64].unsqueeze(1).to_broadcast([NS, 8, 32]), op=ALU.mult), r=rqp + [r_tb1], w=[r_qpt])
                qpsb, r_qpsb = A.get(512, BF16, "qpsb")
                S.op("pool", lambda e: e.tensor_tensor(out=qpsb[0:NS, :], in0=qpt[0:NS, 0:512], in1=qpt[0:NS, 512:1024], op=ALU.add), r=[r_qpt], w=[r_qpsb])
                btq, rtq = ring.get(1)
                tqv = bankbf(btq)[:, 0:64].rearrange("p (a t) -> p a t", a=4)
                for a in range(4):
                    S.op("pe", lambda e, a=a: e.transpose(out=tqv[:, a, :], in_=qpsb[0:NS, a * 128:(a + 1) * 128], identity=ident_b[0:NS, 0:NS]), r=[r_qpsb, r_idb], w=rtq)
                qpTs, r_qpTs = A.get(SB * 32, BF16, "qpTs")
                S.op("pool", lambda e: e.memset(qpTs, 0.0), w=[r_qpTs])
                qpT5 = qpTs.rearrange("p (b a two l) -> p b a two l", b=4, a=4, two=2)
                tq4 = tqv.rearrange("p a (b l) -> p a b l", b=4)
                for bb in range(SB):
                    S.op("act", lambda e, bb=bb: e.copy(out=qpT5[0:64, bb, :, 0, :], in_=tq4[0:64, :, bb, :]), r=rtq, w=[r_qpTs])
                    S.op("dve", lambda e, bb=bb: e.tensor_copy(out=qpT5[64:128, bb, :, 1, :], in_=tq4[64:128, :, bb, :]), r=rtq, w=[r_qpTs])
                qpTb = qpTs.rearrange("p (b n) -> p b n", b=4)
                bql, rql = ring.get(1)
                qlv = bank(bql)[:, 0:256].rearrange("p (h k t) -> p h k t", h=8, k=2)
                for h in range(8):
                    for k in range(2):
                        S.op("pe", lambda e, h=h, k=k: e.matmul(qlv[:, h, k, :], lhsT=wukT_sb[:, h, k * 128:(k + 1) * 128], rhs=qnTs[:, h, :], start=True, stop=True), r=[r_wukT, r_qnTs], w=rql)
                qlTs, r_qlTs = A.get(2 * SB * 32, BF16, "qlTs")
                ql5 = qlTs.rearrange("p (k b h l) -> p k b h l", k=2, b=4, h=8)
                qlv5 = qlv.rearrange("p h k (b l) -> p h k b l", b=4)
                for k in range(2):
                    for bb in range(SB):
                        S.op("act" if (k + bb) % 2 == 0 else "dve",
                             (lambda e, k=k, bb=bb: e.copy(out=ql5[:, k, bb], in_=qlv5[:, :, k, bb, :])) if (k + bb) % 2 == 0 else
                             (lambda e, k=k, bb=bb: e.tensor_copy(out=ql5[:, k, bb], in_=qlv5[:, :, k, bb, :])), r=rql, w=[r_qlTs])
                qlTb = qlTs.rearrange("p (k b n) -> p k b n", k=2, b=4)
                gk = []; r_gk = []; gp = []; r_gp = []
                for i_ in range(3):
                    v_, r_ = A.get(2048, BF16, f"gk{i_}"); gk.append(v_.rearrange("p (s c) -> p s c", s=8)); r_gk.append(r_)
                    v_, r_ = A.get(512, BF16, f"gp{i_}"); gp.append(v_.rearrange("p (s c) -> p s c", s=8)); r_gp.append(r_)
                gp2, r_gp2 = dbl(1024, BF16, "gp2", "p (s c) -> p s c", s=8)
                ckTc, r_ckTc = dbl(2048, BF16, "ckTc", "p (k n) -> p k n", k=2)
                kpTc, r_kpTc = dbl(1024, BF16, "kpTc")
                p_s, r_ps = dbl(1024, BF16, "p_s")
                pTc, r_pTc = dbl(256, BF16, "pTc", "p (i t) -> p i t", i=8)
                Oacc, r_Oacc = dbl(256, F32, "Oacc")
                ms, r_ms = dbl(8, F32, "ms")
                olb, r_olb = A.get(256, BF16, "olb")
                olTs, r_olTs = A.get(64, BF16, "olTs"); olTs = olTs.rearrange("p (k n) -> p k n", k=2)
                oTs, r_oTs = A.get(128, BF16, "oTs"); oTs = oTs.rearrange("p (h t) -> p h t", h=8)
                print("sample l1 arena bytes", A.cur)
                sctx = {}
                def sI(bb):
                    mb_ = ms[bb % 2][0:32]
                    oa = Oacc[bb % 2][0:32]
                    r_m = r_ms[bb % 2]
                    r_o = r_Oacc[bb % 2]
                    bsn, rsn = ring.get(1)
                    sn = bank(bsn)[0:32, 0:NS]
                    S.op("pe", lambda e, bb=bb, sn=sn: e.matmul(sn, lhsT=qlTb[:, 0, bb, :], rhs=ckvTs[:, 0, :], start=True, stop=False), r=[r_qlTs, r_trs], w=rsn)
                    S.op("pe", lambda e, bb=bb, sn=sn: e.matmul(sn, lhsT=qlTb[:, 1, bb, :], rhs=ckvTs[:, 1, :], start=False, stop=False), r=[r_qlTs, r_trs], w=rsn)
                    S.op("pe", lambda e, bb=bb, sn=sn: e.matmul(sn, lhsT=qpTb[:, bb, :], rhs=kpeTs, start=False, stop=False), r=[r_qpTs, r_trs], w=rsn)
                    S.op("pe", lambda e, bb=bb, sn=sn: e.matmul(sn, lhsT=ident_b[0:32, 0:32], rhs=mskv[0:32, bb, :], start=False, stop=True), r=[r_idb, r_msk], w=rsn)
                    S.op("dve", lambda e, sn=sn, mb_=mb_: e.reduce_max(out=mb_[:, 0:1], in_=sn, axis=AX.X), r=rsn, w=[r_m])
                    S.op("dve", lambda e, mb_=mb_: e.tensor_scalar(out=mb_[:, 5:6], in0=mb_[:, 0:1], scalar1=-MLA_SCALE, scalar2=None, op0=ALU.mult), r=[r_m], w=[r_m])
                    psl = p_s[0][0:32]
                    S.op("act", lambda e, sn=sn, mb_=mb_, psl=psl: e.activation(out=psl[:, 0:NS], in_=sn, func=AF.Exp, bias=mb_[:, 5:6], scale=MLA_SCALE, accum_out=mb_[:, 1:2]), r=rsn + [r_m], w=[r_ps[0], r_m])
                    bpt, rpt = ring.get(1)
                    S.op("pe", lambda e, psl=psl, bpt=bpt: e.transpose(out=bankbf(bpt)[0:NS, 0:32], in_=psl[:, 0:NS], identity=ident_b[0:32, 0:32]), r=[r_ps[0], r_idb], w=rpt)
                    S.op("act", lambda e, bpt=bpt: e.copy(out=pTc[0][0:NS, 0, :], in_=bankbf(bpt)[0:NS, 0:32]), r=rpt, w=[r_pTc[0]])
                    bo_, ro_ = ring.get(1)
                    S.op("pe", lambda e, bo_=bo_: e.matmul(bank(bo_)[0:32, 0:256], lhsT=pTc[0][0:NS, 0, :], rhs=ckvsb[0:NS, :], start=True, stop=True), r=[r_pTc[0], r_ckvsb], w=ro_)
                    S.op("dve", lambda e, bo_=bo_, oa=oa: e.tensor_copy(out=oa, in_=bank(bo_)[0:32, 0:256]), r=ro_, w=[r_o])

                def sG(bb, stp, g3):
                    S.dma("pool", lambda e, g3=g3, bb=bb, stp=stp: e.indirect_dma_start(out=gk[g3].rearrange("p s c -> p (s c)"), out_offset=None, in_=cckv[:, :], element_offset=stp * 2048,
                                                                                         in_offset=bass.IndirectOffsetOnAxis(ap=pt_sb[:, bb:bb + 1], axis=0)), r=[r_pt], w=[r_gk[g3]])
                    S.dma("pool", lambda e, g3=g3, bb=bb, stp=stp: e.indirect_dma_start(out=gp[g3].rearrange("p s c -> p (s c)"), out_offset=None, in_=ckpe[:, :], element_offset=stp * 512,
                                                                                         in_offset=bass.IndirectOffsetOnAxis(ap=pt_sb[:, bb:bb + 1], axis=0)), r=[r_pt], w=[r_gp[g3]])

                def sD(bb, stp, gs, g3):
                    S.op("act", lambda e, gs=gs, g3=g3: e.copy(out=gp2[gs][:, :, 0:64], in_=gp[g3]), r=[r_gp[g3]], w=[r_gp2[gs]])
                    S.op("dve", lambda e, gs=gs, g3=g3: e.tensor_copy(out=gp2[gs][:, :, 64:128], in_=gp[g3]), r=[r_gp[g3]], w=[r_gp2[gs]])

                def sT(bb, stp, gs, g3):
                    for hf in range(2):
                        sl = hf * 4
                        bA, rA = ring.get(1)
                        bB, rB = ring.get(1)
                        bC, rC = ring.get(1)
                        for i_ in range(4):
                            S.op("pe", lambda e, i_=i_, sl=sl, g3=g3, bA=bA: e.transpose(out=bankbf(bA)[:, i_ * 128:(i_ + 1) * 128], in_=gk[g3][:, sl + i_, 0:128], identity=ident_b[:]), r=[r_gk[g3], r_idb], w=rA)
                        for i_ in range(4):
                            S.op("pe", lambda e, i_=i_, sl=sl, g3=g3, bB=bB: e.transpose(out=bankbf(bB)[:, i_ * 128:(i_ + 1) * 128], in_=gk[g3][:, sl + i_, 128:256], identity=ident_b[:]), r=[r_gk[g3], r_idb], w=rB)
                        for i_ in range(4):
                            S.op("pe", lambda e, i_=i_, sl=sl, gs=gs, bC=bC: e.transpose(out=bankbf(bC)[:, i_ * 128:(i_ + 1) * 128], in_=gp2[gs][:, sl + i_, :], identity=ident_b[:]), r=[r_gp2[gs], r_idb], w=rC)
                        cs_ = slice(hf * 512, (hf + 1) * 512)
                        S.op("act", lambda e, gs=gs, bA=bA, cs_=cs_: e.copy(out=ckTc[gs][:, 0, cs_], in_=bankbf(bA)[:, 0:512]), r=rA, w=[r_ckTc[gs]])
                        S.op("dve", lambda e, gs=gs, bB=bB, cs_=cs_: e.tensor_copy(out=ckTc[gs][:, 1, cs_], in_=bankbf(bB)[:, 0:512]), r=rB, w=[r_ckTc[gs]])
                        if hf == 0:
                            S.op("act", lambda e, gs=gs, bC=bC, cs_=cs_: e.copy(out=kpTc[gs][:, cs_], in_=bankbf(bC)[:, 0:512]), r=rC, w=[r_kpTc[gs]])
                        else:
                            S.op("dve", lambda e, gs=gs, bC=bC, cs_=cs_: e.tensor_copy(out=kpTc[gs][:, cs_], in_=bankbf(bC)[:, 0:512]), r=rC, w=[r_kpTc[gs]])

                def sU1(bb, stp, gs):
                    mb_ = ms[bb % 2][0:32]
                    oa = Oacc[bb % 2][0:32]
                    r_m = r_ms[bb % 2]
                    r_o = r_Oacc[bb % 2]
                    bs2, rs2 = ring.get(2)
                    sfull = bank(bs2, 2)[0:32, :]
                    for hf in range(2):
                        cs_ = slice(hf * 512, (hf + 1) * 512)
                        S.op("pe", lambda e, bb=bb, gs=gs, cs_=cs_, sfull=sfull: e.matmul(sfull[:, cs_], lhsT=qlTb[:, 0, bb, :], rhs=ckTc[gs][:, 0, cs_], start=True, stop=False), r=[r_qlTs, r_ckTc[gs]], w=[rs2[hf]])
                        S.op("pe", lambda e, bb=bb, gs=gs, cs_=cs_, sfull=sfull: e.matmul(sfull[:, cs_], lhsT=qlTb[:, 1, bb, :], rhs=ckTc[gs][:, 1, cs_], start=False, stop=False), r=[r_qlTs, r_ckTc[gs]], w=[rs2[hf]])
                        S.op("pe", lambda e, bb=bb, gs=gs, cs_=cs_, sfull=sfull: e.matmul(sfull[:, cs_], lhsT=qpTb[:, bb, :], rhs=kpTc[gs][:, cs_], start=False, stop=True), r=[r_qpTs, r_kpTc[gs]], w=[rs2[hf]])
                    S.op("dve", lambda e, sfull=sfull, mb_=mb_: e.reduce_max(out=mb_[:, 2:3], in_=sfull, axis=AX.X), r=rs2, w=[r_m])
                    S.op("dve", lambda e, mb_=mb_: e.tensor_tensor(out=mb_[:, 3:4], in0=mb_[:, 0:1], in1=mb_[:, 2:3], op=ALU.max), r=[r_m], w=[r_m])
                    S.op("dve", lambda e, mb_=mb_: e.tensor_tensor(out=mb_[:, 4:5], in0=mb_[:, 0:1], in1=mb_[:, 3:4], op=ALU.subtract), r=[r_m], w=[r_m])
                    S.op("act", lambda e, mb_=mb_: e.activation(out=mb_[:, 4:5], in_=mb_[:, 4:5], func=AF.Exp, scale=MLA_SCALE), r=[r_m], w=[r_m])
                    S.op("dve", lambda e, mb_=mb_: e.tensor_scalar(out=mb_[:, 5:6], in0=mb_[:, 3:4], scalar1=-MLA_SCALE, scalar2=None, op0=ALU.mult), r=[r_m], w=[r_m])
                    pq = gs
                    psl = p_s[pq][0:32]
                    S.op("act", lambda e, sfull=sfull, mb_=mb_, psl=psl: e.activation(out=psl, in_=sfull, func=AF.Exp, bias=mb_[:, 5:6], scale=MLA_SCALE, accum_out=mb_[:, 6:7]), r=rs2 + [r_m], w=[r_ps[pq], r_m])
                    S.op("dve", lambda e, mb_=mb_: e.scalar_tensor_tensor(out=mb_[:, 1:2], in0=mb_[:, 1:2], scalar=mb_[:, 4:5], in1=mb_[:, 6:7], op0=ALU.mult, op1=ALU.add), r=[r_m], w=[r_m])
                    S.op("dve", lambda e, mb_=mb_: e.tensor_copy(out=mb_[:, 0:1], in_=mb_[:, 3:4]), r=[r_m], w=[r_m])

                def sU2(bb, stp, gs, g3):
                    mb_ = ms[bb % 2][0:32]
                    oa = Oacc[bb % 2][0:32]
                    r_m = r_ms[bb % 2]
                    r_o = r_Oacc[bb % 2]
                    pq = gs
                    psl = p_s[pq][0:32]
                    bpt, rpt = ring.get(1)
                    ptv = bankbf(bpt)[:, 0:256].rearrange("p (i t) -> p i t", i=8)
                    for i_ in range(8):
                        S.op("pe", lambda e, i_=i_, psl=psl, ptv=ptv: e.transpose(out=ptv[:, i_, :], in_=psl[:, i_ * 128:(i_ + 1) * 128], identity=ident_b[0:32, 0:32]), r=[r_ps[pq], r_idb], w=rpt)
                    S.op("act", lambda e, ptv=ptv, pq=pq: e.copy(out=pTc[pq], in_=ptv), r=rpt, w=[r_pTc[pq]])
                    bo_, ro_ = ring.get(1)
                    for i_ in range(8):
                        S.op("pe", lambda e, i_=i_, bo_=bo_, pq=pq, g3=g3: e.matmul(bank(bo_)[0:32, 0:256], lhsT=pTc[pq][:, i_, :], rhs=gk[g3][:, i_, :], start=(i_ == 0), stop=(i_ == 7)), r=[r_pTc[pq], r_gk[g3]], w=ro_)
                    S.op("dve", lambda e, bo_=bo_, oa=oa, mb_=mb_: e.scalar_tensor_tensor(out=oa, in0=oa, scalar=mb_[:, 4:5], in1=bank(bo_)[0:32, 0:256], op0=ALU.mult, op1=ALU.add), r=ro_ + [r_o, r_m], w=[r_o])


                def sF(bb):
                    mb_ = ms[bb % 2][0:32]
                    oa = Oacc[bb % 2][0:32]
                    r_m = r_ms[bb % 2]
                    r_o = r_Oacc[bb % 2]
                    S.op("dve", lambda e, mb_=mb_: e.reciprocal(out=mb_[:, 7:8], in_=mb_[:, 1:2]), r=[r_m], w=[r_m])
                    S.op("act", lambda e, oa=oa, mb_=mb_: e.activation(out=olb[0:32, :], in_=oa, func=AF.Identity, bias=0.0, scale=mb_[:, 7:8]), r=[r_o, r_m], w=[r_olb])
                    bot, rot = ring.get(1)
                    otv = bankbf(bot)[:, 0:64].rearrange("p (k n) -> p k n", k=2)
                    for k in range(2):
                        S.op("pe", lambda e, k=k, otv=otv: e.transpose(out=otv[:, k, :], in_=olb[0:32, k * 128:(k + 1) * 128], identity=ident_b[0:32, 0:32]), r=[r_olb, r_idb], w=rot)
                    S.op("dve", lambda e, otv=otv: e.tensor_copy(out=olTs, in_=otv), r=rot, w=[r_olTs])
                    boh, roh = ring.get(1)
                    ohv = bank(boh)[:, 0:32].rearrange("p (h l) -> p h l", h=8)
                    for h in range(8):
                        for k in range(2):
                            S.op("pe", lambda e, h=h, k=k, ohv=ohv: e.matmul(ohv[:, h, :], lhsT=wuv_sb[:, k, h * 128:(h + 1) * 128], rhs=olTs[:, k, h * 4:(h + 1) * 4], start=(k == 0), stop=(k == 1)), r=[r_wuv, r_olTs], w=roh)
                    S.op("act", lambda e, ohv=ohv, bb=bb: e.copy(out=oTs[:, :, bb * 4:bb * 4 + 4], in_=ohv), r=roh, w=[r_oTs])

                NTOT = SB * NSTEP
                sG(0, 0, 0)
                sG(0, 1, 1)
                sD(0, 0, 0, 0)
                sT(0, 0, 0, 0)
                for idx in range(NTOT):
                    bb, stp = divmod(idx, NSTEP)
                    if idx + 2 < NTOT:
                        b3, s3 = divmod(idx + 2, NSTEP)
                        sG(b3, s3, (idx + 2) % 3)
                    if stp == 0:
                        sI(bb)
                    if idx + 1 < NTOT:
                        sD(0, 0, (idx + 1) % 2, (idx + 1) % 3)
                    sU1(bb, stp, idx % 2)
                    if idx + 1 < NTOT:
                        b2, s2 = divmod(idx + 1, NSTEP)
                        sT(b2, s2, (idx + 1) % 2, (idx + 1) % 3)
                    sU2(bb, stp, idx % 2, idx % 3)
                    if stp == NSTEP - 1:
                        sF(bb)
                by_, ry_ = ring.get(2)
                for hf in range(2):
                    for c in range(8):
                        S.op("pe", lambda e, hf=hf, c=c: e.matmul(bank(by_ + hf)[0:NS, :], lhsT=oTs[:, c, :], rhs=wo1_sb[:, c, hf * 512:(hf + 1) * 512], start=(c == 0), stop=(c == 7)), r=[r_oTs, r_wo1], w=[ry_[hf]])
                zs1, r_zs1 = A.get(D, F32, "zs1")
                scs1, r_scs1 = A.get(16, F32, "scs1")
                S.op("dve", lambda e: e.scalar_tensor_tensor(out=zs1[0:NS, :], in0=XS[:], scalar=ALPHA, in1=bank(by_, 2)[0:NS, :], op0=ALU.mult, op1=ALU.add), r=[r_XS] + ry_, w=[r_zs1])
                layer_norm_tile(zs1[0:NS, :], r_zs1, XS[:], r_XS, NS, scs1, r_scs1, lnt1, r_lnt1)

            S.set_phase('s1')
            if with_sample:
                sample_l1()
            chk('l1')
            S.set_phase('ffn1')
            ffn(1, True)


        except _Stop:
            print('stopped at', stop)
        S.finish(out_toks)
        S.emit()
        print("ops", S.stats())
    return nc


_CACHE = {}


def _prep_shared(inp, tabs):
    f = lambda a: np.ascontiguousarray(np.asarray(a, dtype=np.float32))
    sh = {}
    sh["w_in"] = f(inp["w_in_even"][0])
    sh["pool_w"] = f(np.transpose(inp["pool_w"][0], (1, 0, 2)))
    sh["pool_scale"] = f(inp["pool_scale"][0].reshape(4, 128).T)
    sh["gn_g"] = f(inp["ret_gn_g"][0])
    sh["w_o0"] = f(inp["w_o_even"][0])
    sh["w_dq"] = f(inp["w_dq"][0])
    sh["qn_g"] = f(inp["q_norm_g"][0])
    wuq = np.asarray(inp["w_uq"][0]).reshape(512, 8, 192)
    sh["w_uq"] = f(np.concatenate([wuq[:, :, :128].reshape(512, 1024), wuq[:, :, 128:].reshape(512, 512)], axis=1))
    sh["w_dkv"] = f(inp["w_dkv"][0])
    sh["kvn_g"] = f(inp["kv_norm_g"][0])
    sh["w_ukT"] = f(np.transpose(inp["w_uk"][0], (2, 1, 0)))
    sh["w_uv"] = f(np.asarray(inp["w_uv"][0]).reshape(256, 1024))
    sh["w_o1"] = f(inp["w_o_mla"][0])
    wup = np.asarray(inp["w_up"])
    a = wup[:, :, :DFF].reshape(DEPTH, 8, 128, NJ, 128)
    b = wup[:, :, DFF:].reshape(DEPTH, 8, 128, NJ, 128)
    ab = np.concatenate([a, b], axis=4)
    sh["w_up"] = f(np.transpose(ab, (0, 3, 2, 1, 4)))
    sh["w_down"] = f(inp["w_down"])
    sh["conv_w"] = f(np.transpose(np.asarray(inp["conv_w"]).reshape(DEPTH, 3, NJ, 128), (0, 3, 2, 1)))
    sh["conv_b"] = f(np.transpose(np.asarray(inp["conv_b"]).reshape(DEPTH, NJ, 128), (0, 2, 1)))
    for k in ("ln_mix_g", "ln_mix_b", "ln_ffn_g", "ln_ffn_b"):
        sh[k] = f(inp[k])
    sh["cckv"] = np.asarray(inp["cache_ckv"], dtype=np.float32).reshape(5120, 128 * 256)
    sh["ckpe"] = np.asarray(inp["cache_kpe"], dtype=np.float32).reshape(5120, 128 * 64)
    for k, v in tabs.items():
        if isinstance(v, np.ndarray):
            sh["t_" + k] = v
    return sh


def kernel(**inp):
    tabs = host_tables()
    if "nc" not in _CACHE:
        _CACHE["nc"] = build_program(tabs)
    nc = _CACHE["nc"]
    sh = _prep_shared(inp, tabs)
    f = lambda a: np.ascontiguousarray(np.asarray(a, dtype=np.float32))
    in_maps = []
    for c in range(8):
        m = dict(sh)
        bs = slice(SB * c, SB * (c + 1))
        m["xp"] = f(inp["x_prompt"][c])
        m["xs"] = f(np.asarray(inp["x_sample"][bs]).reshape(NS, D))
        m["spool"] = f(np.asarray(inp["state_pool"][0, bs]).reshape(SB * 15, 512))
        m["sret"] = f(np.asarray(inp["state_ret"][0, bs]).reshape(SB * RH, 128, 128))
        m["sconv"] = f(np.asarray(inp["state_conv"][:, bs]).reshape(DEPTH, SB * 2, DFF))
        m["ptab"] = np.ascontiguousarray(np.asarray(inp["page_table"][bs]).T.astype(np.int32))
        in_maps.append(m)
    res = run_bass_kernel_spmd(nc, in_maps, core_ids=list(range(8)))
    R = res.results
    cat = lambda k: np.stack([np.asarray(R[c][k]) for c in range(8)])
    y_p = cat("yp")
    y_s = cat("ys").reshape(32, SL, D)
    pool_p = cat("poolp")[None]
    pool_s = cat("pools").reshape(1, 32, 15, 512)
    ret_p = cat("retp")[None]
    ret_s = cat("rets").reshape(1, 32, RH, 128, 128)
    ckv_p = cat("ckvp")[None]
    ckv_s = cat("ckvs").reshape(1, 32, SL, 256)
    kpe_p = cat("kpep")[None]
    kpe_s = cat("kpes").reshape(1, 32, SL, 64)
    conv_p = np.transpose(cat("convp"), (1, 0, 2, 3))
    conv_s = np.transpose(cat("convs").reshape(8, DEPTH, SB, 2, DFF), (1, 0, 2, 3, 4)).reshape(DEPTH, 32, 2, DFF)
    outs = (y_p, y_s, pool_p, pool_s, ret_p, ret_s, ckv_p, ckv_s, kpe_p, kpe_s, conv_p, conv_s)
    return tuple(np.ascontiguousarray(o.astype(np.float32)) for o in outs)
```

```python
import contextlib
import math
import numpy as np
import concourse.bass as bass
import concourse.mybir as mybir
from concourse.bass_utils import run_bass_kernel_spmd

F32 = mybir.dt.float32
BF16 = mybir.dt.bfloat16
I32 = mybir.dt.int32
AF = mybir.ActivationFunctionType
ALU = mybir.AluOpType
AX = mybir.AxisListType

ALL_Q = ("pe", "act", "dve", "pool", "sp")
SAME_ENGINE_SYNC = True

D = 1024
SEQ = 2048
NT = SEQ // 128
DEPTH = 2
NS = 16
SB = 4
SL = 4
PAST = 16384
NPAGES = 128
POOL_W = (2, 4, 8, 16)
RH = 4
DFF = 2816
NJ = DFF // 128
ALPHA = (2.0 * DEPTH) ** 0.25
LN_EPS = 1e-5
RMS_EPS = 1e-6
GN_EPS = 1e-6
MLA_SCALE = (128 + 64) ** -0.5
KSCALE = 128 ** -0.5
NEG = -30000.0


class Res:
    __slots__ = ("name", "w", "r")

    def __init__(self, name):
        self.name = name
        self.w = None
        self.r = []


class Op:
    __slots__ = ("q", "fn", "waits", "signal", "dma", "idx", "deps", "sdep", "succ", "nrem", "ready", "cost", "lat", "fin", "orig", "done", "phase")

    def __init__(self, q, fn, dma=None):
        self.q = q
        self.fn = fn
        self.waits = []
        self.signal = False
        self.dma = dma
        self.idx = None
        self.deps = []
        self.sdep = None
        self.succ = []
        self.nrem = 0
        self.ready = 0.0
        self.cost = 0.1
        self.lat = 0.0
        self.fin = 0.0
        self.orig = 0
        self.done = False
        self.phase = 0


class _Rec:
    def __init__(self):
        self.info = ("", None)

    def __getattr__(self, name):
        def f(*a, **k):
            out = k.get("out", a[0] if a else None)
            object.__setattr__(self, "info", (name, out))
            return self
        return f


RESCHED_PHASES = {"l0", "ffn0", "ffn1"}
KEEP_Q = ()


class Sched:
    def __init__(self, nc, stack):
        self.nc = nc
        self.stack = stack
        self.all = []
        self.ops = {q: [] for q in ALL_Q}
        self.dma_sems = {}
        self.dma_last = {}
        self.eng_sems = {}
        self.nres = 0
        self.phase = 0
        self.phase_names = ['init']

    def res(self, name=None):
        self.nres += 1
        return Res(name or f"r{self.nres}")

    def set_phase(self, name):
        self.phase_names.append(name)
        self.phase = len(self.phase_names) - 1

    def _cost(self, o):
        try:
            rec = _Rec()
            o.fn(rec)
            name, out = rec.info
            shp = list(out.shape)
            n = 1
            for d in shp[1:]:
                n *= d
        except Exception:
            name, n = "", 256
        q = o.q
        if o.dma is not None:
            o.cost = 1.2 if q == "pool" else 0.15
            o.lat = 2.5 + n * shp[0] * 3.0 / 250e3 if name else 3.0
        elif q == "pe":
            o.cost = 0.05 + n / 1900.0
        elif q == "act":
            o.cost = 0.22 + n / 1000.0
        elif q == "dve":
            o.cost = 0.2 + n / 900.0
        else:
            o.cost = 0.35 + n / 450.0

    def _deps(self, o, r, w, tok):
        seen = set()
        for res in r:
            if res.w is not None and id(res.w[-1]) not in seen:
                seen.add(id(res.w[-1])); o.deps.append(res.w)
        for res in w:
            if res.w is not None and id(res.w[-1]) not in seen:
                seen.add(id(res.w[-1])); o.deps.append(res.w)
            for t in res.r:
                if id(t[-1]) not in seen:
                    seen.add(id(t[-1])); o.deps.append(t)
        for res in r:
            res.r.append(tok)
        for res in w:
            res.w = tok
            res.r = []

    def op(self, q, fn, r=(), w=()):
        o = Op(q, fn)
        o.orig = len(self.all)
        o.phase = self.phase
        self._deps(o, r, w, ("eng", o))
        self.all.append(o)
        return o

    def dma(self, q, fn, r=(), w=(), semkey=None):
        res0 = (list(w) + list(r))[0] if (w or r) else None
        key = semkey if semkey is not None else id(res0)
        if key not in self.dma_sems:
            self.dma_sems[key] = [None, 0]
        ent = self.dma_sems[key]
        ent[1] += 16
        o = Op(q, fn, dma=(key, ent[1]))
        o.orig = len(self.all)
        o.phase = self.phase
        o.sdep = self.dma_last.get(key)
        self.dma_last[key] = o
        tok = ("dma", key, ent[1], o)
        self._deps(o, r, w, tok)
        self.all.append(o)
        return tok

    def finish(self, out_tokens):
        o = Op("sp", None)
        o.orig = len(self.all)
        o.phase = self.phase
        seen = set()
        for t in out_tokens:
            if id(t[-1]) not in seen:
                seen.add(id(t[-1])); o.deps.append(t)
        self.all.append(o)

    def _schedule(self):
        ops = self.all
        for o in ops:
            if o.fn is not None:
                self._cost(o)
            dset = set(id(t[-1]) for t in o.deps)
            preds = [(t[-1], True) for t in o.deps]
            if o.sdep is not None and id(o.sdep) not in dset:
                preds.append((o.sdep, False))
            o.nrem = len(preds)
            for p, kind in preds:
                p.succ.append((o, kind))
        tfree = {q: 0.0 for q in ALL_Q}

        def place(o, st):
            self.ops[o.q].append(o)
            tfree[o.q] = st + o.cost
            o.fin = st + o.cost + o.lat
            o.done = True
            for s, kind in o.succ:
                t = o.fin if kind else st
                if t > s.ready:
                    s.ready = t
                s.nrem -= 1

        nph = len(self.phase_names)
        byph = [[] for _ in range(nph)]
        for o in ops:
            byph[o.phase].append(o)
        for ph in range(nph):
            plist = byph[ph]
            if self.phase_names[ph] not in RESCHED_PHASES:
                for o in plist:
                    place(o, max(tfree[o.q], o.ready))
                continue
            pending = set(id(o) for o in plist)
            keepq = KEEP_Q if self.phase_names[ph] in ("l1", "s1") else ()
            nxt = {q: [o for o in plist if o.q == q] for q in keepq}
            nxt_i = {q: 0 for q in keepq}
            ready = {q: [] for q in ALL_Q}
            inready = set()
            for o in plist:
                if o.nrem == 0:
                    ready[o.q].append(o); inready.add(id(o))
            nleft = len(plist)
            while nleft:
                best = None
                for q in ALL_Q:
                    lst = ready[q]
                    if not lst:
                        continue
                    tf = tfree[q]
                    cand = None
                    if q in keepq:
                        o = nxt[q][nxt_i[q]]
                        if o in lst:
                            st = o.ready if o.ready > tf else tf
                            cand = ((st, o.orig), o)
                    else:
                        for o in lst:
                            st = o.ready if o.ready > tf else tf
                            key = (st, o.orig)
                            if cand is None or key < cand[0]:
                                cand = (key, o)
                    if cand is not None and (best is None or cand[0] < best[0]):
                        best = cand
                (st, _), o = best
                ready[o.q].remove(o)
                if o.q in keepq:
                    nxt_i[o.q] += 1
                place(o, st)
                nleft -= 1
                for s, kind in o.succ:
                    if s.nrem == 0 and id(s) in pending and id(s) not in inready and not s.done:
                        ready[s.q].append(s); inready.add(id(s))

    def _add_wait(self, op, tok, waited, dwaited):
        q = op.q
        if tok[0] == "eng":
            src = tok[1]
            sq, sidx = src.q, src.idx
            if sq == q and (q == "pe" or q == "sp" or not SAME_ENGINE_SYNC):
                return
            if waited[q].get(sq, -1) >= sidx:
                return
            waited[q][sq] = sidx
            op.waits.append(("eng", sq, sidx))
            src.signal = True
        else:
            _, key, val, _src = tok
            if dwaited[q].get(key, -1) >= val:
                return
            dwaited[q][key] = val
            op.waits.append(("dma", key, val))

    def emit(self):
        nc = self.nc
        st = self.stack
        self._schedule()
        waited = {q: {} for q in ALL_Q}
        dwaited = {q: {} for q in ALL_Q}
        for q in ALL_Q:
            for i, o in enumerate(self.ops[q]):
                o.idx = i
        for q in ALL_Q:
            for o in self.ops[q]:
                best = {}
                for t in o.deps:
                    if t[0] == "eng":
                        k = ("eng", t[1].q)
                        v = t[1].idx
                    else:
                        k = ("dma", t[1])
                        v = t[2]
                    if k not in best or best[k][0] < v:
                        best[k] = (v, t)
                for v, t in best.values():
                    self._add_wait(o, t, waited, dwaited)
        for q in ALL_Q:
            self.eng_sems[q] = st.enter_context(nc.semaphore(f"es_{q}"))
        for i, key in enumerate(self.dma_sems):
            self.dma_sems[key][0] = st.enter_context(nc.semaphore(f"ds_{i}"))
        cnt = {}
        for q in ALL_Q:
            c = 0
            arr = []
            for o in self.ops[q]:
                if o.signal:
                    c += 1
                arr.append(c)
            cnt[q] = arr
        block = st.enter_context(nc.Block())

        def run(q, eng):
            for o in self.ops[q]:
                for t in o.waits:
                    if t[0] == "eng":
                        eng.wait_ge(self.eng_sems[t[1]], cnt[t[1]][t[2]])
                    else:
                        eng.wait_ge(self.dma_sems[t[1]][0], t[2])
                if o.fn is None:
                    continue
                ins = o.fn(eng)
                if o.dma is not None:
                    ins.then_inc(self.dma_sems[o.dma[0]][0], 16)
                elif o.signal:
                    ins.then_inc(self.eng_sems[q], 1)

        @block.tensor
        def _(eng):
            run("pe", eng)

        @block.scalar
        def _(eng):
            run("act", eng)

        @block.vector
        def _(eng):
            run("dve", eng)

        @block.gpsimd
        def _(eng):
            run("pool", eng)

        @block.sync
        def _(eng):
            run("sp", eng)

    def stats(self):
        return {q: (len(self.ops[q]), sum(len(o.waits) for o in self.ops[q])) for q in ALL_Q}


def _rope_tables(half, pos):
    inv = (np.float32(10000.0) ** (-(np.arange(half, dtype=np.float32)) / np.float32(half))).astype(np.float32)
    ang = (pos.astype(np.float32)[:, None] * inv[None, :]).astype(np.float32)
    return np.cos(ang).astype(np.float32), np.sin(ang).astype(np.float32)


def host_tables():
    t = {}
    pos_p = np.arange(SEQ, dtype=np.float32)
    pos_s = (PAST + np.arange(SL)).astype(np.float32)
    c, s = _rope_tables(64, pos_p)
    t["c2p"] = np.ascontiguousarray(np.concatenate([c, c], 1).T)
    t["ssp"] = np.ascontiguousarray(np.concatenate([s, -s], 1).T)
    c, s = _rope_tables(64, pos_s)
    t["c2s"] = np.ascontiguousarray(np.tile(np.concatenate([c, c], 1).T, (1, SB)))
    t["sss"] = np.ascontiguousarray(np.tile(np.concatenate([s, -s], 1).T, (1, SB)))
    c, s = _rope_tables(32, pos_p)
    cc = np.concatenate([c, c], 1).reshape(NT, 128, 64).transpose(1, 0, 2)
    ss = np.concatenate([-s, s], 1).reshape(NT, 128, 64).transpose(1, 0, 2)
    t["cc1"] = np.ascontiguousarray(cc)
    t["ss1"] = np.ascontiguousarray(ss)
    c, s = _rope_tables(32, pos_s)
    t["cc1s"] = np.ascontiguousarray(np.tile(np.concatenate([c, c], 1), (SB, 1)))
    t["ss1s"] = np.ascontiguousarray(np.tile(np.concatenate([-s, s], 1), (SB, 1)))
    lg = np.log(np.float32(1.0) - np.float32(2.0) ** (np.float32(-5.0) - np.arange(RH, dtype=np.float32))).astype(np.float32)
    for name, C in (("p", 128), ("s", SL)):
        idx = np.arange(C, dtype=np.float32)
        diff = idx[:, None] - idx[None, :]
        dec = np.where(diff >= 0, np.exp(np.maximum(diff, 0.0)[None] * lg[:, None, None]), 0.0).astype(np.float32)
        t["decT" + name] = np.ascontiguousarray((dec * np.float32(KSCALE)).transpose(2, 0, 1))
        qd = np.exp((idx + 1.0)[:, None] * lg[None, :]).astype(np.float32)
        t["qdec" + name] = np.ascontiguousarray(np.broadcast_to(qd.T[None], (128, RH, C)))
        kd = np.exp((C - 1.0 - idx)[:, None] * lg[None, :]).astype(np.float32) * np.float32(KSCALE)
        t["kdec" + name] = np.ascontiguousarray(kd)
        t["gC" + name] = [float(np.exp(np.float32(C) * lg[h])) for h in range(RH)]
    ic = np.zeros((128, 4, 15), np.float32)
    for g, w in enumerate(POOL_W):
        ic[:, g, :] = 1.0 / np.minimum(float(w), np.arange(15) + 1.0)
    t["invcnt"] = ic
    qi = np.arange(128)
    t["maskd"] = np.where(qi[None, :] <= qi[:, None], 0.0, NEG).astype(np.float32)
    t["ident"] = np.eye(128, dtype=np.float32)
    r = np.arange(32) % 4
    t["masks"] = np.where(np.arange(4)[None, :] <= r[:, None], 0.0, NEG).astype(np.float32)
    mb = np.full((32, SB, NS), NEG, np.float32)
    for bb in range(SB):
        for l2 in range(SL):
            mb[:, bb, bb * SL + l2] = np.where(l2 <= r, 0.0, NEG)
    t["masksb"] = np.ascontiguousarray(mb.reshape(32, SB * NS))
    return t


ARENA_BYTES = 142144


class PsumRing:
    def __init__(self, S):
        self.res = [S.res(f"bank{i}") for i in range(8)]
        self.nxt = 0

    def get(self, n=1):
        if n == 2 and self.nxt % 2:
            self.nxt = (self.nxt + 1) % 8
        if n == 4 and self.nxt % 4:
            self.nxt = (self.nxt + 4 - self.nxt % 4) % 8
        b = self.nxt
        self.nxt = (self.nxt + n) % 8
        return b, [self.res[b + i] for i in range(n)]


class Arena:
    def __init__(self, S, buf):
        self.S = S
        self.buf = buf
        self.hist = []
        self.cur = 0

    def at(self, off):
        assert off % 4 == 0
        self.cur = off

    def get(self, nelem, dt=BF16, name=None, nres=None):
        size = 2 if dt == BF16 else 4
        nb = (nelem * size + 3) // 4 * 4
        s, e = self.cur, self.cur + nb
        assert e <= ARENA_BYTES, (name, e)
        self.cur = e
        pend = []
        for (s2, e2, r2) in self.hist:
            if s2 < e and s < e2:
                if r2.w is not None:
                    pend.append(r2.w)
                pend.extend(r2.r)
        seen = set()
        p2 = []
        for t in pend:
            if id(t[-1]) not in seen:
                seen.add(id(t[-1])); p2.append(t)
        pend = p2
        rs = []
        for i in range(nres or 1):
            r = self.S.res(f"{name}_{i}")
            r.r = list(pend)
            self.hist.append((s, e, r))
            rs.append(r)
        v = self.buf[:, s // 2:s // 2 + nb // 2]
        if dt != BF16:
            v = v.bitcast(dt)
        return v[:, 0:nelem], (rs if nres else rs[0])


class _Stop(Exception):
    pass


def build_program(tabs, with_sample=True, stop=None):
    nc = bass.Bass("TRN2", target_bir_lowering=False)

    def din(name, shape, dt=F32):
        return nc.dram_tensor(name, list(shape), dt, kind="ExternalInput").ap()

    def dout(name, shape):
        return nc.dram_tensor(name, list(shape), F32, kind="ExternalOutput").ap()

    xp = din("xp", [SEQ, D])
    xs = din("xs", [NS, D])
    spool = din("spool", [SB * 15, 512])
    sret = din("sret", [SB * RH, 128, 128])
    sconv = din("sconv", [DEPTH, SB * 2, DFF])
    ptab = din("ptab", [128, SB], I32)
    if with_sample:
        cckv = din("cckv", [5120, 128 * 256])
        ckpe = din("ckpe", [5120, 128 * 64])
    w_in = din("w_in", [D, 2560])
    pool_w = din("pool_w", [128, 4, 128])
    pool_scale = din("pool_scale", [128, 4])
    gn_g = din("gn_g", [512])
    w_o0 = din("w_o0", [D, D])
    w_dq = din("w_dq", [D, 512])
    qn_g = din("qn_g", [512])
    w_uq = din("w_uq", [512, 1536])
    w_dkv = din("w_dkv", [D, 320])
    kvn_g = din("kvn_g", [256])
    w_ukT = din("w_ukT", [128, 8, 256])
    w_uv = din("w_uv", [256, 1024])
    w_o1 = din("w_o1", [D, D])
    w_up = din("w_up", [DEPTH, NJ, 128, 8, 256])
    w_down = din("w_down", [DEPTH, DFF, D])
    conv_w = din("conv_w", [DEPTH, 128, NJ, 3])
    conv_b = din("conv_b", [DEPTH, 128, NJ])
    ln_mix_g = din("ln_mix_g", [DEPTH, D])
    ln_mix_b = din("ln_mix_b", [DEPTH, D])
    ln_ffn_g = din("ln_ffn_g", [DEPTH, D])
    ln_ffn_b = din("ln_ffn_b", [DEPTH, D])
    tdr = {k: din("t_" + k, v.shape) for k, v in tabs.items() if isinstance(v, np.ndarray)}

    yp = dout("yp", [SEQ, D])
    ys = dout("ys", [NS, D])
    poolp = dout("poolp", [15, 512])
    pools = dout("pools", [SB * 15, 512])
    retp = dout("retp", [RH, 128, 128])
    rets = dout("rets", [SB * RH, 128, 128])
    ckvp = dout("ckvp", [SEQ, 256])
    ckvs = dout("ckvs", [NS, 256])
    kpep = dout("kpep", [SEQ, 64])
    kpes = dout("kpes", [NS, 64])
    convp = dout("convp", [DEPTH, 2, DFF])
    convs = dout("convs", [DEPTH, SB * 2, DFF])

    out_toks = []
    st = contextlib.ExitStack()
    with st:
        S = Sched(nc, st)
        _n = [0]

        def sb(shape, dt, name):
            return st.enter_context(nc.sbuf_tensor(name, list(shape), dt))

        P = st.enter_context(nc.psum_tensor("P", [128, 8, 512], F32))
        ring = PsumRing(S)

        def bank(b, n=1):
            return P[:, b:b + n, :].rearrange("p a b -> p (a b)")

        def bankbf(b, n=1):
            return P[:, b:b + n, :].rearrange("p a b -> p (a b)").bitcast(BF16)

        X = sb([128, NT, D], F32, "X")
        rX = [S.res(f"X{t}") for t in range(NT)]
        BIG = sb([128, ARENA_BYTES // 2], BF16, "BIG")
        A = Arena(S, BIG)
        ident_f = sb([128, 128], F32, "ident_f"); r_idf = S.res()
        ident_b = sb([128, 128], BF16, "ident_b"); r_idb = S.res()
        maskd = sb([128, 128], BF16, "maskd"); r_maskd = S.res()
        eps_t = sb([128, 4], F32, "eps_t"); r_eps = S.res()
        XS = sb([NS, D], F32, "XS"); r_XS = S.res()
        S.dma("sp", lambda e: e.dma_start(out=ident_f[:], in_=tdr["ident"]), w=[r_idf])
        S.dma("pool", lambda e: e.dma_start(out=ident_b[:], in_=tdr["ident"]), w=[r_idb])
        S.dma("pool", lambda e: e.dma_start(out=maskd[:], in_=tdr["maskd"]), w=[r_maskd])
        S.op("pool", lambda e: e.memset(eps_t[:, 0:1], LN_EPS), w=[r_eps])
        S.op("pool", lambda e: e.memset(eps_t[:, 1:2], RMS_EPS), w=[r_eps])
        S.op("pool", lambda e: e.memset(eps_t[:, 2:3], GN_EPS), w=[r_eps])
        eps_ln, eps_rms, eps_gn = eps_t[:, 0:1], eps_t[:, 1:2], eps_t[:, 2:3]

        def evac(q, out_ap, in_ap, r, w):
            if q == "act":
                S.op("act", lambda e: e.copy(out=out_ap, in_=in_ap), r=r, w=w)
            else:
                S.op(q, lambda e: e.tensor_copy(out=out_ap, in_=in_ap), r=r, w=w)

        def load_ln(lnt, r_lnt, g_ap, b_ap):
            S.dma("sp", lambda e: e.dma_start(out=lnt[:, 0:D], in_=g_ap.partition_broadcast(128)), w=[r_lnt])
            S.dma("sp", lambda e: e.dma_start(out=lnt[:, D:2 * D], in_=b_ap.partition_broadcast(128)), w=[r_lnt])

        def layer_norm_tile(z_ap, z_res, out_ap, out_res, npart, scr, scr_res, lnt, r_lnt):
            for c in range(2):
                S.op("dve", lambda e, c=c: e.bn_stats(out=scr[0:npart, c * 6:(c + 1) * 6], in_=z_ap[:, c * 512:(c + 1) * 512]), r=[z_res], w=[scr_res])
            mv = scr[0:npart, 12:14]
            S.op("dve", lambda e: e.bn_aggr(out=mv, in_=scr[0:npart, 0:12]), r=[scr_res], w=[scr_res])
            sd = scr[0:npart, 14:15]
            S.op("act", lambda e: e.activation(out=sd, in_=scr[0:npart, 13:14], func=AF.Sqrt, bias=eps_ln[0:npart], scale=1.0), r=[scr_res, r_eps], w=[scr_res])
            S.op("dve", lambda e: e.reciprocal(out=sd, in_=sd), r=[scr_res], w=[scr_res])
            nmr = scr[0:npart, 15:16]
            S.op("dve", lambda e: e.tensor_scalar(out=nmr, in0=scr[0:npart, 12:13], scalar1=sd, scalar2=-1.0, op0=ALU.mult, op1=ALU.mult), r=[scr_res], w=[scr_res])
            S.op("act", lambda e: e.activation(out=z_ap, in_=z_ap, func=AF.Identity, bias=nmr, scale=sd), r=[scr_res, z_res], w=[z_res])
            S.op("pool", lambda e: e.tensor_tensor(out=z_ap, in0=z_ap, in1=lnt[0:npart, 0:D], op=ALU.mult), r=[z_res, r_lnt], w=[z_res])
            S.op("pool", lambda e: e.tensor_tensor(out=out_ap, in0=z_ap, in1=lnt[0:npart, D:2 * D], op=ALU.add), r=[z_res, r_lnt], w=[out_res])

        def transpose_x(src_ap, src_res, dst_ap, dst_res, npart=128):
            for half in range(2):
                b, br = ring.get(1)
                pv = bank(b)[:, 0:4 * npart].rearrange("p (a b) -> p a b", a=4)
                for c in range(4):
                    cc = half * 4 + c
                    S.op("pe", lambda e, c=c, cc=cc, pv=pv: e.transpose(out=pv[:, c, :], in_=src_ap[:, cc * 128:(cc + 1) * 128], identity=ident_f[0:npart, 0:npart]),
                         r=[src_res, r_idf], w=br)
                evac("act" if half == 0 else "dve", dst_ap[:, half * 4:half * 4 + 4, :], pv, br, [dst_res])

        def chk(tag):
            if stop == tag:
                raise _Stop()

        try:
            for t in range(NT):
                S.dma("sp", lambda e, t=t: e.dma_start(out=X[:, t, :], in_=xp[t * 128:(t + 1) * 128, :]), w=[rX[t]])
            S.dma("sp", lambda e: e.dma_start(out=XS[:], in_=xs), w=[r_XS])

            S.set_phase('l0')
            A.at(0)
            w_in_sb, r_win = A.get(8 * 2560, BF16, "w_in"); w_in_sb = w_in_sb.rearrange("p (k n) -> p k n", k=8)
            w_o_sb, r_wo = A.get(8 * 1024, BF16, "w_o0"); w_o_sb = w_o_sb.rearrange("p (k n) -> p k n", k=8)
            poolw_sb, r_poolw = A.get(512, BF16, "poolw"); poolw_sb = poolw_sb.rearrange("p (g d) -> p g d", g=4)
            c0buf, r_c0 = A.get(1024 + 1024 + 4 + 60 + 4 + 512, F32, "c0")
            decT = c0buf[:, 0:512].rearrange("p (h l) -> p h l", h=4)
            qdec = c0buf[:, 512:1024].rearrange("p (h l) -> p h l", h=4)
            kdec = c0buf[:, 1024:1028]
            invcnt = c0buf[:, 1028:1088].rearrange("p (g t) -> p g t", g=4)
            pscale = c0buf[:, 1088:1092]
            gng = c0buf[:, 1092:1604]
            lnt0, r_lnt0 = A.get(2 * D, F32, "lnt0")
            w_in_v = w_in.rearrange("(k p) n -> p k n", p=128)
            for hh in range(2):
                S.dma("pool", lambda e, hh=hh: e.dma_start(out=w_in_sb[:, :, hh * 1280:(hh + 1) * 1280], in_=w_in_v[:, :, hh * 1280:(hh + 1) * 1280]), w=[r_win])
            S.dma("pool", lambda e: e.dma_start(out=poolw_sb, in_=pool_w), w=[r_poolw])
            S.dma("pool", lambda e: e.dma_start(out=w_o_sb, in_=w_o0.rearrange("(k p) n -> p k n", p=128)), w=[r_wo])
            S.dma("sp", lambda e: e.dma_start(out=decT, in_=tdr["decTp"]), w=[r_c0])
            S.dma("sp", lambda e: e.dma_start(out=qdec, in_=tdr["qdecp"]), w=[r_c0])
            S.dma("sp", lambda e: e.dma_start(out=kdec, in_=tdr["kdecp"]), w=[r_c0])
            S.dma("sp", lambda e: e.dma_start(out=invcnt, in_=tdr["invcnt"]), w=[r_c0])
            S.dma("sp", lambda e: e.dma_start(out=pscale, in_=pool_scale), w=[r_c0])
            S.dma("sp", lambda e: e.dma_start(out=gng, in_=gn_g.partition_broadcast(128)), w=[r_c0])
            load_ln(lnt0, r_lnt0, ln_mix_g[0], ln_mix_b[0])
            L0_BUF = A.cur

            def dbl(nelem, dt, name, shape=None, **kw):
                out = []
                for i in range(2):
                    v, r = A.get(nelem, dt, f"{name}{i}")
                    if shape:
                        v = v.rearrange(shape, **kw)
                    out.append((v, r))
                return [x[0] for x in out], [x[1] for x in out]

            xT, r_xT = dbl(1024, BF16, "xT", "p (k t) -> p k t", k=8)
            ropet, r_ropet = dbl(256, F32, "ropet")
            uext, r_uext = dbl(4 * 143 + 1, F32, "uext")
            uext = [u[:, 0:572].rearrange("p (g t) -> p g t", g=4) for u in uext]
            tA, r_tA = A.get(573, F32, "tA"); tA = tA[:, 0:572].rearrange("p (g t) -> p g t", g=4)
            tB, r_tB = A.get(573, F32, "tB"); tB = tB[:, 0:572].rearrange("p (g t) -> p g t", g=4)
            pooledT, r_pooled = dbl(512, BF16, "pooledT", "p (g t) -> p g t", g=4)
            qk_f, r_qkf = A.get(1024, F32, "qk_f"); qk_f = qk_f.rearrange("p (h t) -> p h t", h=8)
            ropeA, r_ropeA = A.get(1024, F32, "ropeA"); ropeA = ropeA.rearrange("p (h t) -> p h t", h=8)
            ropeB, r_ropeB = A.get(1024, F32, "ropeB"); ropeB = ropeB.rearrange("p (h t) -> p h t", h=8)
            qkT, r_qkT = dbl(1024, BF16, "qkT", "p (h t) -> p h t", h=8)
            qdT, r_qdT = dbl(512, BF16, "qdT", "p (h t) -> p h t", h=4)
            v_sb, r_v = dbl(512, BF16, "v")
            sgg, r_sgg = dbl(512, F32, "sgg")
            sT_bf, r_sT = dbl(512, BF16, "sT", "p (h t) -> p h t", h=4)
            kp_bf, r_kp = dbl(512, BF16, "kp", "p (h t) -> p h t", h=4)
            S_f, r_Sf = A.get(512, F32, "S_f"); S_f = S_f.rearrange("p (h t) -> p h t", h=4)
            S_b, r_Sb = A.get(512, BF16, "S_b"); S_b = S_b.rearrange("p (h t) -> p h t", h=4)
            on1, r_on1 = A.get(512, F32, "on1")
            ret_bf, r_ret = dbl(512, BF16, "ret")
            catT, r_cat = dbl(1024, BF16, "catT", "p (c t) -> p c t", c=8)
            z0_, r_z0_ = A.get(D, F32, "z"); z = [z0_, z0_]; r_z = [r_z0_, r_z0_]
            scr, r_scr = dbl(16, F32, "scr")
            gst, r_gst = A.get(32, F32, "gst")
            poolo, r_poolo = A.get(512, F32, "poolo")
            L0_END = A.cur
            print("layer0 arena bytes", L0_END)

            S.op("pool", lambda e: e.memset(uext[1][:, :, 128:143], 0.0), w=[r_uext[1]])

            def l0A(t):
                pb = t % 2
                tok = slice(t * 128, (t + 1) * 128)
                S.dma("sp", lambda e, pb=pb, tok=tok: e.dma_start(out=ropet[pb][:, 0:128], in_=tdr["c2p"][:, tok]), w=[r_ropet[pb]])
                S.dma("sp", lambda e, pb=pb, tok=tok: e.dma_start(out=ropet[pb][:, 128:256], in_=tdr["ssp"][:, tok]), w=[r_ropet[pb]])
                transpose_x(X[:, t, :], rX[t], xT[pb], r_xT[pb])
                bu, ru = ring.get(1)
                bq, rq = ring.get(1)
                bk, rk = ring.get(1)
                bv, rv = ring.get(1)
                bg, rg = ring.get(1)
                for (bb, rr, c0) in ((bu, ru, 0), (bq, rq, 512), (bk, rk, 1024)):
                    pv = bank(bb).rearrange("p (h t) -> p h t", h=4)
                    for h in range(4):
                        for k in range(8):
                            S.op("pe", lambda e, pv=pv, h=h, k=k, c0=c0, pb=pb: e.matmul(pv[:, h, :], lhsT=w_in_sb[:, k, c0 + h * 128:c0 + (h + 1) * 128], rhs=xT[pb][:, k, :], start=(k == 0), stop=(k == 7)),
                                 r=[r_win, r_xT[pb]], w=rr)
                for (bb, rr, c0) in ((bv, rv, 1536), (bg, rg, 2048)):
                    for k in range(8):
                        S.op("pe", lambda e, bb=bb, k=k, c0=c0, pb=pb: e.matmul(bank(bb), lhsT=xT[pb][:, k, :], rhs=w_in_sb[:, k, c0:c0 + 512], start=(k == 0), stop=(k == 7)),
                             r=[r_win, r_xT[pb]], w=rr)
                ue = uext[pb]
                S.op("act", lambda e, ue=ue, bu=bu: e.copy(out=ue[:, :, 15:143], in_=bank(bu).rearrange("p (h t) -> p h t", h=4)), r=ru, w=[r_uext[pb]])
                S.op("pool", lambda e, ue=ue, pb=pb: e.tensor_copy(out=ue[:, :, 0:15], in_=uext[1 - pb][:, :, 128:143]), r=[r_uext[1 - pb]], w=[r_uext[pb]])
                S.op("pool", lambda e, ue=ue: e.tensor_tensor(out=tA[:, :, 1:143], in0=ue[:, :, 1:143], in1=ue[:, :, 0:142], op=ALU.add), r=[r_uext[pb]], w=[r_tA])
                S.op("pool", lambda e: e.tensor_tensor(out=tB[:, 1:4, 3:143], in0=tA[:, 1:4, 3:143], in1=tA[:, 1:4, 1:141], op=ALU.add), r=[r_tA], w=[r_tB])
                S.op("pool", lambda e: e.tensor_tensor(out=tA[:, 2:4, 7:143], in0=tB[:, 2:4, 7:143], in1=tB[:, 2:4, 3:139], op=ALU.add), r=[r_tB], w=[r_tA])
                S.op("pool", lambda e: e.tensor_tensor(out=tB[:, 3:4, 15:143], in0=tA[:, 3:4, 15:143], in1=tA[:, 3:4, 7:135], op=ALU.add), r=[r_tA], w=[r_tB])
                fin = (tA, tB, tA, tB)
                for g in range(4):
                    S.op("dve", lambda e, g=g, ue=ue, pb=pb: e.scalar_tensor_tensor(out=pooledT[pb][:, g, :], in0=fin[g][:, g, 15:143], scalar=1.0 / POOL_W[g], in1=ue[:, g, 15:143], op0=ALU.mult, op1=ALU.subtract),
                         r=[r_tA, r_tB, r_uext[pb]], w=[r_pooled[pb]])
                if t == 0:
                    for g in range(4):
                        S.op("dve", lambda e, g=g: e.tensor_tensor(out=ropeA[:, g, 0:15], in0=fin[g][:, g, 15:30], in1=invcnt[:, g, :], op=ALU.mult), r=[r_tA, r_tB, r_c0], w=[r_ropeA])
                        S.op("dve", lambda e, g=g, ue=ue, pb=pb: e.tensor_tensor(out=pooledT[pb][:, g, 0:15], in0=ropeA[:, g, 0:15], in1=ue[:, g, 15:30], op=ALU.subtract), r=[r_ropeA, r_uext[pb]], w=[r_pooled[pb]])
                bpm, rpm = ring.get(1)
                pmv = bank(bpm).rearrange("p (h t) -> p h t", h=4)
                for g in range(4):
                    S.op("pe", lambda e, g=g, pmv=pmv, pb=pb: e.matmul(pmv[:, g, :], lhsT=poolw_sb[:, g, :], rhs=pooledT[pb][:, g, :], start=True, stop=True), r=[r_poolw, r_pooled[pb]], w=rpm)
                for g in range(4):
                    S.op("act", lambda e, g=g, pmv=pmv, pb=pb: e.activation(out=catT[pb][:, g, :], in_=pmv[:, g, :], func=AF.Identity, bias=0.0, scale=pscale[:, g:g + 1]), r=rpm + [r_c0], w=[r_cat[pb]])
                if t == NT - 1:
                    bpo, rpo = ring.get(1)
                    for g in range(4):
                        S.op("pe", lambda e, g=g, ue=ue, bpo=bpo: e.transpose(out=bank(bpo)[0:15, g * 128:(g + 1) * 128], in_=ue[:, g, 128:143], identity=ident_f[:]), r=[r_uext[pb], r_idf], w=rpo)
                    S.op("act", lambda e, bpo=bpo: e.copy(out=poolo[0:15, :], in_=bank(bpo)[0:15, :]), r=rpo, w=[r_poolo])
                    out_toks.append(S.dma("sp", lambda e: e.dma_start(out=poolp, in_=poolo[0:15, :]), r=[r_poolo]))
                S.op("act", lambda e, bq=bq: e.copy(out=qk_f[:, 0:4, :], in_=bank(bq).rearrange("p (h t) -> p h t", h=4)), r=rq, w=[r_qkf])
                S.op("act", lambda e, bk=bk: e.copy(out=qk_f[:, 4:8, :], in_=bank(bk).rearrange("p (h t) -> p h t", h=4)), r=rk, w=[r_qkf])
                c2b = ropet[pb][:, 0:128].unsqueeze(1)
                ssb = ropet[pb][:, 128:256].unsqueeze(1)
                S.op("dve", lambda e, c2b=c2b: e.tensor_tensor(out=ropeA, in0=qk_f, in1=c2b.to_broadcast([128, 8, 128]), op=ALU.mult), r=[r_qkf, r_ropet[pb]], w=[r_ropeA])
                S.op("pool", lambda e, ssb=ssb: e.tensor_tensor(out=ropeB[0:64], in0=qk_f[64:128], in1=ssb[64:128].to_broadcast([64, 8, 128]), op=ALU.mult), r=[r_qkf, r_ropet[pb]], w=[r_ropeB])
                S.op("pool", lambda e, ssb=ssb: e.tensor_tensor(out=ropeB[64:128], in0=qk_f[0:64], in1=ssb[0:64].to_broadcast([64, 8, 128]), op=ALU.mult), r=[r_qkf, r_ropet[pb]], w=[r_ropeB])
                S.op("dve", lambda e, pb=pb: e.tensor_tensor(out=qkT[pb], in0=ropeA, in1=ropeB, op=ALU.add), r=[r_ropeA, r_ropeB], w=[r_qkT[pb]])
                if t > 0:
                    S.op("pool", lambda e, pb=pb: e.tensor_tensor(out=qdT[pb], in0=qkT[pb][:, 0:4, :], in1=qdec, op=ALU.mult), r=[r_qkT[pb], r_c0], w=[r_qdT[pb]])
                S.op("act", lambda e, pb=pb, bv=bv: e.copy(out=v_sb[pb], in_=bank(bv)), r=rv, w=[r_v[pb]])
                S.op("act", lambda e, pb=pb, bg=bg: e.activation(out=sgg[pb], in_=bank(bg), func=AF.Silu), r=rg, w=[r_sgg[pb]])
                S.op("pool", lambda e, pb=pb: e.tensor_tensor(out=sgg[pb], in0=sgg[pb], in1=gng, op=ALU.mult), r=[r_sgg[pb], r_c0], w=[r_sgg[pb]])
            def l0B(t):
                pb = t % 2
                bs, rs = ring.get(1)
                sv = bank(bs).rearrange("p (h t) -> p h t", h=4)
                for h in range(4):
                    S.op("pe", lambda e, h=h, sv=sv, pb=pb: e.matmul(sv[:, h, :], lhsT=qkT[pb][:, 4 + h, :], rhs=qkT[pb][:, h, :], start=True, stop=True), r=[r_qkT[pb]], w=rs)
                S.op("dve", lambda e, sv=sv, pb=pb: e.tensor_tensor(out=sT_bf[pb], in0=sv, in1=decT, op=ALU.mult), r=rs + [r_c0], w=[r_sT[pb]])
                bo, ro = ring.get(1)
                ov = bank(bo).rearrange("p (h t) -> p h t", h=4)
                for h in range(4):
                    S.op("pe", lambda e, h=h, ov=ov, pb=pb, t=t: e.matmul(ov[:, h, :], lhsT=sT_bf[pb][:, h, :], rhs=v_sb[pb][:, h * 128:(h + 1) * 128], start=True, stop=(t == 0)), r=[r_sT[pb], r_v[pb]], w=ro)
                    if t > 0:
                        S.op("pe", lambda e, h=h, ov=ov, pb=pb: e.matmul(ov[:, h, :], lhsT=qdT[pb][:, h, :], rhs=S_b[:, h, :], start=False, stop=True), r=[r_qdT[pb], r_Sb], w=ro)
                bkt, rkt = ring.get(1)
                ktv = bankbf(bkt)[:, 0:512].rearrange("p (h t) -> p h t", h=4)
                for h in range(4):
                    S.op("pe", lambda e, h=h, ktv=ktv, pb=pb: e.transpose(out=ktv[:, h, :], in_=qkT[pb][:, 4 + h, :], identity=ident_b[:]), r=[r_qkT[pb], r_idb], w=rkt)
                for h in range(4):
                    S.op("act", lambda e, h=h, ktv=ktv, pb=pb: e.activation(out=kp_bf[pb][:, h, :], in_=ktv[:, h, :], func=AF.Identity, bias=0.0, scale=kdec[:, h:h + 1]), r=rkt + [r_c0], w=[r_kp[pb]])
                bds, rds = ring.get(1)
                dsv = bank(bds).rearrange("p (h t) -> p h t", h=4)
                for h in range(4):
                    S.op("pe", lambda e, h=h, dsv=dsv, pb=pb: e.matmul(dsv[:, h, :], lhsT=kp_bf[pb][:, h, :], rhs=v_sb[pb][:, h * 128:(h + 1) * 128], start=True, stop=True), r=[r_kp[pb], r_v[pb]], w=rds)
                if t == 0:
                    S.op("dve", lambda e, dsv=dsv: e.tensor_copy(out=S_f, in_=dsv), r=rds, w=[r_Sf])
                else:
                    for h in range(4):
                        S.op("dve", lambda e, h=h, dsv=dsv: e.scalar_tensor_tensor(out=S_f[:, h, :], in0=S_f[:, h, :], scalar=tabs["gCp"][h], in1=dsv[:, h, :], op0=ALU.mult, op1=ALU.add), r=rds + [r_Sf], w=[r_Sf])
                if t < NT - 1:
                    S.op("pool", lambda e: e.tensor_copy(out=S_b, in_=S_f), r=[r_Sf], w=[r_Sb])
                else:
                    out_toks.append(S.dma("sp", lambda e: e.dma_start(out=retp.rearrange("h k v -> k h v"), in_=S_f), r=[r_Sf]))
                for h in range(4):
                    S.op("dve", lambda e, h=h, ov=ov: e.bn_stats(out=gst[:, h * 6:(h + 1) * 6], in_=ov[:, h, :]), r=ro, w=[r_gst])
                    S.op("dve", lambda e, h=h: e.bn_aggr(out=gst[:, 24 + 2 * h:26 + 2 * h], in_=gst[:, h * 6:(h + 1) * 6]), r=[r_gst], w=[r_gst])
                mvv = gst[:, 24:32].rearrange("p (h two) -> p h two", two=2)
                S.op("act", lambda e, pb=pb: e.activation(out=scr[pb][:, 0:4], in_=mvv[:, :, 1], func=AF.Sqrt, bias=eps_gn, scale=1.0), r=[r_gst, r_eps], w=[r_scr[pb]])
                S.op("dve", lambda e, pb=pb: e.reciprocal(out=scr[pb][:, 0:4], in_=scr[pb][:, 0:4]), r=[r_scr[pb]], w=[r_scr[pb]])
                S.op("dve", lambda e, pb=pb: e.scalar_tensor_tensor(out=scr[pb][:, 4:8], in0=mvv[:, :, 0], scalar=-1.0, in1=scr[pb][:, 0:4], op0=ALU.mult, op1=ALU.mult), r=[r_gst, r_scr[pb]], w=[r_scr[pb]])
                for h in range(4):
                    S.op("act", lambda e, h=h, ov=ov, pb=pb: e.activation(out=on1[:, h * 128:(h + 1) * 128], in_=ov[:, h, :], func=AF.Identity, bias=scr[pb][:, 4 + h:5 + h], scale=scr[pb][:, h:h + 1]), r=ro + [r_scr[pb]], w=[r_on1])
                S.op("pool", lambda e, pb=pb: e.tensor_tensor(out=ret_bf[pb], in0=on1, in1=sgg[pb], op=ALU.mult), r=[r_on1, r_sgg[pb]], w=[r_ret[pb]])
                brt, rrt = ring.get(1)
                rtv = bankbf(brt)[:, 0:512].rearrange("p (h t) -> p h t", h=4)
                for h in range(4):
                    S.op("pe", lambda e, h=h, rtv=rtv, pb=pb: e.transpose(out=rtv[:, h, :], in_=ret_bf[pb][:, h * 128:(h + 1) * 128], identity=ident_b[:]), r=[r_ret[pb], r_idb], w=rrt)
                S.op("dve", lambda e, rtv=rtv, pb=pb: e.tensor_copy(out=catT[pb][:, 4:8, :], in_=rtv), r=rrt, w=[r_cat[pb]])
                by, ry = ring.get(2)
                for hf in range(2):
                    for c in range(8):
                        S.op("pe", lambda e, hf=hf, c=c, by=by, pb=pb: e.matmul(bank(by + hf), lhsT=catT[pb][:, c, :], rhs=w_o_sb[:, c, hf * 512:(hf + 1) * 512], start=(c == 0), stop=(c == 7)), r=[r_cat[pb], r_wo], w=[ry[hf]])
                S.op("dve", lambda e, by=by, pb=pb, t=t: e.scalar_tensor_tensor(out=z[pb], in0=X[:, t, :], scalar=ALPHA, in1=bank(by, 2), op0=ALU.mult, op1=ALU.add), r=[rX[t]] + ry, w=[r_z[pb]])
                layer_norm_tile(z[pb], r_z[pb], X[:, t, :], rX[t], 128, scr[pb], r_scr[pb], lnt0, r_lnt0)

            l0A(0)
            for t in range(NT):
                if t + 1 < NT:
                    l0A(t + 1)
                l0B(t)

            def ffn(layer, final):
                A.at(0)
                gated, r_gated = A.get(NJ * 1024, BF16, "gated", nres=NJ); gated = gated.rearrange("p (j n) -> p j n", j=NJ)
                x1T, r_x1T = A.get(8 * 1024, BF16, "x1T", nres=8); x1T = x1T.rearrange("p (k n) -> p k n", k=8)
                cb0_, r_cb0_ = A.get(512, F32, "cbuf"); cbuf = [cb0_, cb0_]; r_cbuf = [r_cb0_, r_cb0_]
                sb0_, r_sb0_ = A.get(512, F32, "sbf"); sbf = [sb0_, sb0_]; r_sbf = [r_sb0_, r_sb0_]
                zf0, r_zf0 = A.get(D, F32, "zf"); zf = [zf0, zf0]; r_zf = [r_zf0, r_zf0]
                scf, r_scf = dbl(16, F32, "scf")
                cwb, r_cw = A.get(NJ * 4, F32, "cw"); cw = cwb[:, 0:NJ * 3].rearrange("p (j c) -> p j c", j=NJ); cb = cwb[:, NJ * 3:NJ * 4]
                hist, r_hist = A.get(NJ * 2, F32, "hist"); hist = hist.rearrange("p (j c) -> p j c", j=NJ)
                cst, r_cst = A.get(NJ * 2, F32, "cst"); cst = cst.rearrange("p (j c) -> p j c", j=NJ)
                cso, r_cso = A.get(512, F32, "cso")
                lnt, r_lnt = A.get(2 * D, F32, "lntf")
                wup = []
                r_wup = []
                for i in range(2):
                    v, r = A.get(8 * 256, BF16, f"wup{i}")
                    wup.append(v.rearrange("p (k n) -> p k n", k=8)); r_wup.append(r)
                wdn, r_wdn = A.get(NJ * 1024, BF16, "wdn"); wdn = wdn.rearrange("p (j n) -> p j n", j=NJ)
                if with_sample:
                    xs1T, r_xs1T = A.get(128, BF16, "xs1T"); xs1T = xs1T.rearrange("p (k t) -> p k t", k=8)
                    aext, r_aext = A.get(NJ * 24, F32, "aext"); aext = aext.rearrange("p (j b r) -> p j b r", j=NJ, b=4)
                    b_s, r_bs = A.get(NJ * 16, F32, "b_s"); b_s = b_s.rearrange("p (j t) -> p j t", j=NJ)
                    scT, r_scT = A.get(NJ * 8, F32, "scT"); scT = scT.rearrange("p (j r) -> p j r", j=NJ)
                    c_s, r_cs = A.get(NJ * 16, F32, "c_s"); c_s = c_s.rearrange("p (j b l) -> p j b l", j=NJ, b=4)
                    t_s, r_ts = A.get(NJ * 16, F32, "t_s"); t_s = t_s.rearrange("p (j b l) -> p j b l", j=NJ, b=4)
                    gated_s, r_gs = A.get(NJ * 16, BF16, "gated_s"); gated_s = gated_s.rearrange("p (j t) -> p j t", j=NJ)
                print("ffn arena bytes", A.cur)
                S.dma("sp", lambda e: e.dma_start(out=cw, in_=conv_w[layer]), w=[r_cw])
                S.dma("sp", lambda e: e.dma_start(out=cb, in_=conv_b[layer]), w=[r_cw])
                load_ln(lnt, r_lnt, ln_ffn_g[layer], ln_ffn_b[layer])
                wdn_v = w_down[layer].rearrange("(j p) n -> p j n", p=128)
                wdn_loaded = [False]

                def load_wdn():
                    for jj in range(0, NJ, 2):
                        S.dma("pool", lambda e, jj=jj: e.dma_start(out=wdn[:, jj:jj + 2, :], in_=wdn_v[:, jj:jj + 2, :]), w=[r_wdn])

                def load_wup(j, slot):
                    S.dma("pool", lambda e, j=j, slot=slot: e.dma_start(out=wup[slot], in_=w_up[layer, j]), w=[r_wup[slot]])

                if with_sample:
                    transpose_x(XS[:], r_XS, xs1T, r_xs1T, npart=NS)
                    for jg in range(0, NJ, 4):
                        n = min(4, NJ - jg)
                        S.dma("sp", lambda e, jg=jg, n=n: e.dma_start(out=cso[0:8, 0:n * 128], in_=sconv[layer][:, jg * 128:(jg + n) * 128]), w=[r_cso])
                        bst, rst = ring.get(1)
                        for i_ in range(n):
                            S.op("pe", lambda e, i_=i_, bst=bst: e.transpose(out=bank(bst)[:, i_ * 8:(i_ + 1) * 8], in_=cso[0:8, i_ * 128:(i_ + 1) * 128], identity=ident_f[0:8, 0:8]), r=[r_cso, r_idf], w=rst)
                        S.op("act", lambda e, jg=jg, n=n, bst=bst: e.copy(out=scT[:, jg:jg + n, :], in_=bank(bst)[:, 0:n * 8].rearrange("p (j r) -> p j r", j=n)), r=rst, w=[r_scT])
                    S.op("pool", lambda e: e.tensor_copy(out=aext[:, :, :, 0:2], in_=scT.rearrange("p j (b r) -> p j b r", b=4)), r=[r_scT], w=[r_aext])
                cnt = 0
                for half in range(2):
                    for tt in range(8):
                        t = half * 8 + tt
                        transpose_x(X[:, t, :], rX[t], x1T[:, :, tt * 128:(tt + 1) * 128], r_x1T[tt])
                    load_wup(0, cnt % 2)
                    for j in range(NJ):
                        slot = cnt % 2
                        if j + 1 < NJ:
                            load_wup(j + 1, (cnt + 1) % 2)
                        elif half == 0:
                            pass
                        if half == 0 and j == 1:
                            load_wdn()
                        cnt += 1
                        for grp in range(2):
                            ba, ra = ring.get(1)
                            bb_, rb = ring.get(1)
                            cols = slice(grp * 512, (grp + 1) * 512)
                            for (bx, rx, c0) in ((ba, ra, 0), (bb_, rb, 128)):
                                for k in range(8):
                                    S.op("pe", lambda e, bx=bx, k=k, c0=c0, slot=slot, cols=cols: e.matmul(bank(bx), lhsT=wup[slot][:, k, c0:c0 + 128], rhs=x1T[:, k, cols], start=(k == 0), stop=(k == 7)),
                                         r=[r_wup[slot]] + r_x1T[grp * 4:grp * 4 + 4], w=rx)
                            pa = bank(ba)
                            cp = (j * 2 + grp) % 2
                            c_ = cbuf[cp]
                            s_ = sbf[cp]
                            S.op("act", lambda e, pa=pa, c_=c_, j=j: e.activation(out=c_, in_=pa, func=AF.Identity, bias=cb[:, j:j + 1], scale=cw[:, j, 2:3]), r=ra + [r_cw], w=[r_cbuf[cp]])
                            S.op("dve", lambda e, pa=pa, c_=c_, j=j: e.scalar_tensor_tensor(out=c_[:, 1:512], in0=pa[:, 0:511], scalar=cw[:, j, 1:2], in1=c_[:, 1:512], op0=ALU.mult, op1=ALU.add), r=ra + [r_cw, r_cbuf[cp]], w=[r_cbuf[cp]])
                            S.op("dve", lambda e, pa=pa, c_=c_, j=j: e.scalar_tensor_tensor(out=c_[:, 2:512], in0=pa[:, 0:510], scalar=cw[:, j, 0:1], in1=c_[:, 2:512], op0=ALU.mult, op1=ALU.add), r=ra + [r_cw, r_cbuf[cp]], w=[r_cbuf[cp]])
                            if not (half == 0 and grp == 0):
                                S.op("dve", lambda e, c_=c_, j=j: e.scalar_tensor_tensor(out=c_[:, 0:1], in0=hist[:, j, 1:2], scalar=cw[:, j, 1:2], in1=c_[:, 0:1], op0=ALU.mult, op1=ALU.add), r=[r_hist, r_cw, r_cbuf[cp]], w=[r_cbuf[cp]])
                                S.op("dve", lambda e, c_=c_, j=j: e.scalar_tensor_tensor(out=c_[:, 0:2], in0=hist[:, j, 0:2], scalar=cw[:, j, 0:1], in1=c_[:, 0:2], op0=ALU.mult, op1=ALU.add), r=[r_hist, r_cw, r_cbuf[cp]], w=[r_cbuf[cp]])
                            if half == 1 and grp == 1:
                                S.op("act", lambda e, pa=pa, j=j: e.copy(out=cst[:, j, :], in_=pa[:, 510:512]), r=ra, w=[r_cst])
                            else:
                                S.op("act", lambda e, pa=pa, j=j: e.copy(out=hist[:, j, :], in_=pa[:, 510:512]), r=ra, w=[r_hist])
                            S.op("act", lambda e, c_=c_, s_=s_: e.activation(out=s_, in_=c_, func=AF.Silu), r=[r_cbuf[cp]], w=[r_sbf[cp]])
                            S.op("dve", lambda e, s_=s_, bb_=bb_, j=j, cols=cols: e.tensor_tensor(out=gated[:, j, cols], in0=s_, in1=bank(bb_), op=ALU.mult), r=[r_sbf[cp]] + rb, w=[r_gated[j]])
                        if with_sample and half == 0:
                            bsa, rsa = ring.get(1)
                            for (c0, col) in ((0, 0), (128, 16)):
                                for k in range(8):
                                    S.op("pe", lambda e, k=k, c0=c0, col=col, slot=slot, bsa=bsa: e.matmul(bank(bsa)[:, col:col + 16], lhsT=wup[slot][:, k, c0:c0 + 128], rhs=xs1T[:, k, :], start=(k == 0), stop=(k == 7)), r=[r_wup[slot], r_xs1T], w=rsa)
                            S.op("act", lambda e, j=j, bsa=bsa: e.copy(out=aext[:, j, :, 2:6], in_=bank(bsa)[:, 0:16].rearrange("p (b l) -> p b l", b=4)), r=rsa, w=[r_aext])
                            S.op("dve", lambda e, j=j, bsa=bsa: e.tensor_copy(out=b_s[:, j, :], in_=bank(bsa)[:, 16:32]), r=rsa, w=[r_bs])
                    if with_sample and half == 0:
                        def cwb(c):
                            return cw[:, :, c:c + 1].unsqueeze(3).to_broadcast([128, NJ, 4, 4])
                        S.op("dve", lambda e: e.tensor_tensor(out=c_s, in0=aext[:, :, :, 2:6], in1=cwb(2), op=ALU.mult), r=[r_aext, r_cw], w=[r_cs])
                        S.op("pool", lambda e: e.tensor_tensor(out=t_s, in0=aext[:, :, :, 1:5], in1=cwb(1), op=ALU.mult), r=[r_aext, r_cw], w=[r_ts])
                        S.op("dve", lambda e: e.tensor_tensor(out=c_s, in0=c_s, in1=t_s, op=ALU.add), r=[r_cs, r_ts], w=[r_cs])
                        S.op("pool", lambda e: e.tensor_tensor(out=t_s, in0=aext[:, :, :, 0:4], in1=cwb(0), op=ALU.mult), r=[r_aext, r_cw], w=[r_ts])
                        S.op("dve", lambda e: e.tensor_tensor(out=c_s, in0=c_s, in1=t_s, op=ALU.add), r=[r_cs, r_ts], w=[r_cs])
                        S.op("dve", lambda e: e.tensor_tensor(out=c_s, in0=c_s, in1=cb.unsqueeze(2).unsqueeze(3).to_broadcast([128, NJ, 4, 4]), op=ALU.add), r=[r_cs, r_cw], w=[r_cs])
                        S.op("act", lambda e: e.activation(out=c_s, in_=c_s, func=AF.Silu), r=[r_cs], w=[r_cs])
                        S.op("dve", lambda e: e.tensor_tensor(out=gated_s, in0=c_s.rearrange("p j b l -> p j (b l)"), in1=b_s, op=ALU.mult), r=[r_cs, r_bs], w=[r_gs])
                        S.op("pool", lambda e: e.tensor_copy(out=scT.rearrange("p j (b r) -> p j b r", b=4), in_=aext[:, :, :, 4:6]), r=[r_aext], w=[r_scT])
                        for jg in range(0, NJ, 4):
                            n = min(4, NJ - jg)
                            bst, rst = ring.get(1)
                            for i_ in range(n):
                                S.op("pe", lambda e, i_=i_, jg=jg, bst=bst: e.transpose(out=bank(bst)[0:8, i_ * 128:(i_ + 1) * 128], in_=scT[:, jg + i_, :], identity=ident_f[:]), r=[r_scT, r_idf], w=rst)
                            S.op("act", lambda e, n=n, bst=bst: e.copy(out=cso[0:8, 0:n * 128], in_=bank(bst)[0:8, 0:n * 128]), r=rst, w=[r_cso])
                            out_toks.append(S.dma("sp", lambda e, jg=jg, n=n: e.dma_start(out=convs[layer][:, jg * 128:(jg + n) * 128], in_=cso[0:8, 0:n * 128]), r=[r_cso]))
                    for tt in range(8):
                        t = half * 8 + tt
                        pb = tt % 2
                        by, ry = ring.get(2)
                        for hf in range(2):
                            for j in range(NJ):
                                S.op("pe", lambda e, hf=hf, j=j, by=by, tt=tt: e.matmul(bank(by + hf), lhsT=gated[:, j, tt * 128:(tt + 1) * 128], rhs=wdn[:, j, hf * 512:(hf + 1) * 512], start=(j == 0), stop=(j == NJ - 1)),
                                     r=[r_gated[j], r_wdn], w=[ry[hf]])
                        S.op("dve", lambda e, by=by, pb=pb, t=t: e.scalar_tensor_tensor(out=zf[pb], in0=X[:, t, :], scalar=ALPHA, in1=bank(by, 2), op0=ALU.mult, op1=ALU.add), r=[rX[t]] + ry, w=[r_zf[pb]])
                        layer_norm_tile(zf[pb], r_zf[pb], X[:, t, :], rX[t], 128, scf[pb], r_scf[pb], lnt, r_lnt)
                        if final:
                            out_toks.append(S.dma("sp", lambda e, t=t: e.dma_start(out=yp[t * 128:(t + 1) * 128, :], in_=X[:, t, :]), r=[rX[t]]))
                if with_sample:
                    bys, rys = ring.get(2)
                    for hf in range(2):
                        for j in range(NJ):
                            S.op("pe", lambda e, hf=hf, j=j, bys=bys: e.matmul(bank(bys + hf)[0:NS, :], lhsT=gated_s[:, j, :], rhs=wdn[:, j, hf * 512:(hf + 1) * 512], start=(j == 0), stop=(j == NJ - 1)), r=[r_gs, r_wdn], w=[rys[hf]])
                    S.op("dve", lambda e, bys=bys: e.scalar_tensor_tensor(out=zf[0][0:NS, :], in0=XS[:], scalar=ALPHA, in1=bank(bys, 2)[0:NS, :], op0=ALU.mult, op1=ALU.add), r=[r_XS] + rys, w=[r_zf[0]])
                    layer_norm_tile(zf[0][0:NS, :], r_zf[0], XS[:], r_XS, NS, scf[0], r_scf[0], lnt, r_lnt)
                    if final:
                        out_toks.append(S.dma("sp", lambda e: e.dma_start(out=ys, in_=XS[:]), r=[r_XS]))
                bc, rc = ring.get(1)
                for j in range(NJ):
                    S.op("pe", lambda e, j=j, bc=bc: e.transpose(out=bank(bc)[0:2, (j % 4) * 128:(j % 4 + 1) * 128], in_=cst[:, j, :], identity=ident_f[:]), r=[r_cst, r_idf], w=rc)
                    if j % 4 == 3 or j == NJ - 1:
                        j0 = j - j % 4
                        n = j - j0 + 1
                        S.op("act", lambda e, bc=bc, n=n: e.copy(out=cso[0:2, 0:n * 128], in_=bank(bc)[0:2, 0:n * 128]), r=rc, w=[r_cso])
                        out_toks.append(S.dma("sp", lambda e, j0=j0, n=n: e.dma_start(out=convp[layer][:, j0 * 128:(j0 + n) * 128], in_=cso[0:2, 0:n * 128]), r=[r_cso]))
                        if j != NJ - 1:
                            bc, rc = ring.get(1)


            def sample_l0():
                A.at(L0_BUF)
                xsT, r_xsT = A.get(128, BF16, "xsT"); xsT = xsT.rearrange("p (k t) -> p k t", k=8)
                tb, r_tb = A.get(68, F32, "stab")
                c2s = tb[:, 0:16]; sss = tb[:, 16:32]
                decTs = tb[0:4, 32:48]
                qdecs = tb[:, 48:64].rearrange("p (h l) -> p h l", h=4)
                kdecs = tb[0:4, 64:68]
                S.dma("sp", lambda e: e.dma_start(out=c2s, in_=tdr["c2s"]), w=[r_tb])
                S.dma("sp", lambda e: e.dma_start(out=sss, in_=tdr["sss"]), w=[r_tb])
                S.dma("sp", lambda e: e.dma_start(out=decTs, in_=tdr["decTs"].rearrange("m h l -> m (h l)")), w=[r_tb])
                S.dma("sp", lambda e: e.dma_start(out=qdecs, in_=tdr["qdecs"]), w=[r_tb])
                S.dma("sp", lambda e: e.dma_start(out=kdecs, in_=tdr["kdecs"]), w=[r_tb])
                sp_sb, r_sp = A.get(512, F32, "sp_sb")
                S.dma("sp", lambda e: e.dma_start(out=sp_sb[0:60, :], in_=spool), w=[r_sp])
                S0f, r_S0f = A.get(16 * 128, F32, "S0f"); S0f = S0f.rearrange("p (n v) -> p n v", n=16)
                S0b, r_S0b = A.get(16 * 128, BF16, "S0b"); S0b = S0b.rearrange("p (n v) -> p n v", n=16)
                S.dma("sp", lambda e: e.dma_start(out=S0f, in_=sret.rearrange("n k v -> k n v")), w=[r_S0f])
                S.op("pool", lambda e: e.tensor_copy(out=S0b, in_=S0f), r=[r_S0f], w=[r_S0b])
                for bb in range(SB):
                    out_toks.append(S.dma("sp", lambda e, bb=bb: e.dma_start(out=pools[bb * 15:bb * 15 + 11, :], in_=spool[bb * 15 + 4:bb * 15 + 15, :]), semkey="d2d"))
                transpose_x(XS[:], r_XS, xsT, r_xsT, npart=NS)
                bq_, rq_ = ring.get(1)
                pqk = bank(bq_)[:, 0:192].rearrange("p (g t) -> p g t", g=12)
                for g in range(12):
                    for k in range(8):
                        S.op("pe", lambda e, g=g, k=k: e.matmul(pqk[:, g, :], lhsT=w_in_sb[:, k, g * 128:(g + 1) * 128], rhs=xsT[:, k, :], start=(k == 0), stop=(k == 7)), r=[r_win, r_xsT], w=rq_)
                bu_, ru_ = ring.get(1)
                for k in range(8):
                    S.op("pe", lambda e, k=k: e.matmul(bank(bu_)[0:NS, :], lhsT=xsT[:, k, :], rhs=w_in_sb[:, k, 0:512], start=(k == 0), stop=(k == 7)), r=[r_win, r_xsT], w=ru_)
                utok, r_utok = A.get(512, F32, "utok")
                S.op("act", lambda e: e.copy(out=utok[0:NS, :], in_=bank(bu_)[0:NS, :]), r=ru_, w=[r_utok])
                for bb in range(SB):
                    out_toks.append(S.dma("sp", lambda e, bb=bb: e.dma_start(out=pools[bb * 15 + 11:bb * 15 + 15, :], in_=utok[bb * 4:bb * 4 + 4, :]), r=[r_utok]))
                bh_, rh_ = ring.get(1)
                for g in range(4):
                    S.op("pe", lambda e, g=g: e.transpose(out=bank(bh_)[:, g * 64:g * 64 + 60], in_=sp_sb[0:60, g * 128:(g + 1) * 128], identity=ident_f[0:60, 0:60]), r=[r_sp, r_idf], w=rh_)
                ue, r_ue = A.get(16 * 19, F32, "ues"); ue4 = ue.rearrange("p (g b r) -> p g b r", g=4, b=4); ue3 = ue.rearrange("p (n r) -> p n r", n=16)
                hv = bank(bh_)[:, 0:256].rearrange("p (g c) -> p g c", g=4)[:, :, 0:60].rearrange("p g (b r) -> p g b r", b=4)
                for g in range(4):
                    S.op("act", lambda e, g=g: e.copy(out=ue4[:, g, :, 0:15], in_=hv[:, g]), r=rh_, w=[r_ue])
                    S.op("act", lambda e, g=g: e.copy(out=ue4[:, g, :, 15:19], in_=pqk[:, g, :].rearrange("p (b l) -> p b l", b=4)), r=rq_, w=[r_ue])
                tAs, r_tAs = A.get(16 * 19, F32, "tAs"); tAs = tAs.rearrange("p (n r) -> p n r", n=16)
                tBs, r_tBs = A.get(16 * 19, F32, "tBs"); tBs = tBs.rearrange("p (n r) -> p n r", n=16)
                S.op("pool", lambda e: e.tensor_tensor(out=tAs[:, :, 1:19], in0=ue3[:, :, 1:19], in1=ue3[:, :, 0:18], op=ALU.add), r=[r_ue], w=[r_tAs])
                S.op("pool", lambda e: e.tensor_tensor(out=tBs[:, 4:16, 3:19], in0=tAs[:, 4:16, 3:19], in1=tAs[:, 4:16, 1:17], op=ALU.add), r=[r_tAs], w=[r_tBs])
                S.op("pool", lambda e: e.tensor_tensor(out=tAs[:, 8:16, 7:19], in0=tBs[:, 8:16, 7:19], in1=tBs[:, 8:16, 3:15], op=ALU.add), r=[r_tBs], w=[r_tAs])
                S.op("pool", lambda e: e.tensor_tensor(out=tBs[:, 12:16, 15:19], in0=tAs[:, 12:16, 15:19], in1=tAs[:, 12:16, 7:11], op=ALU.add), r=[r_tAs], w=[r_tBs])
                pooleds, r_pooleds = A.get(64, BF16, "pooleds"); pooleds = pooleds.rearrange("p (g b l) -> p g b l", g=4, b=4)
                fins = (tAs, tBs, tAs, tBs)
                for g in range(4):
                    S.op("dve", lambda e, g=g: e.scalar_tensor_tensor(out=pooleds[:, g], in0=fins[g][:, g * 4:(g + 1) * 4, 15:19], scalar=1.0 / POOL_W[g], in1=ue3[:, g * 4:(g + 1) * 4, 15:19], op0=ALU.mult, op1=ALU.subtract), r=[r_tAs, r_tBs, r_ue], w=[r_pooleds])
                catTs, r_catTs = A.get(128, BF16, "catTs"); catTs = catTs.rearrange("p (c t) -> p c t", c=8)
                bpm_, rpm_ = ring.get(1)
                pms = bank(bpm_)[:, 0:64].rearrange("p (g t) -> p g t", g=4)
                for g in range(4):
                    S.op("pe", lambda e, g=g: e.matmul(pms[:, g, :], lhsT=poolw_sb[:, g, :], rhs=pooleds[:, g].rearrange("p b l -> p (b l)"), start=True, stop=True), r=[r_poolw, r_pooleds], w=rpm_)
                for g in range(4):
                    S.op("act", lambda e, g=g: e.activation(out=catTs[:, g, :], in_=pms[:, g, :], func=AF.Identity, bias=0.0, scale=pscale[:, g:g + 1]), r=rpm_ + [r_c0], w=[r_catTs])
                qkfs, r_qkfs = A.get(128, F32, "qkfs"); qkfs = qkfs.rearrange("p (h t) -> p h t", h=8)
                rAs, r_rAs = A.get(128, F32, "rAs"); rAs = rAs.rearrange("p (h t) -> p h t", h=8)
                rBs, r_rBs = A.get(128, F32, "rBs"); rBs = rBs.rearrange("p (h t) -> p h t", h=8)
                qkTs, r_qkTs = A.get(128, BF16, "qkTs"); qkTs = qkTs.rearrange("p (h t) -> p h t", h=8)
                qdTs, r_qdTs = A.get(64, BF16, "qdTs"); qdTs = qdTs.rearrange("p (h t) -> p h t", h=4)
                S.op("act", lambda e: e.copy(out=qkfs, in_=pqk[:, 4:12, :]), r=rq_, w=[r_qkfs])
                S.op("dve", lambda e: e.tensor_tensor(out=rAs, in0=qkfs, in1=c2s.unsqueeze(1).to_broadcast([128, 8, 16]), op=ALU.mult), r=[r_qkfs, r_tb], w=[r_rAs])
                S.op("pool", lambda e: e.tensor_tensor(out=rBs[0:64], in0=qkfs[64:128], in1=sss[64:128].unsqueeze(1).to_broadcast([64, 8, 16]), op=ALU.mult), r=[r_qkfs, r_tb], w=[r_rBs])
                S.op("pool", lambda e: e.tensor_tensor(out=rBs[64:128], in0=qkfs[0:64], in1=sss[0:64].unsqueeze(1).to_broadcast([64, 8, 16]), op=ALU.mult), r=[r_qkfs, r_tb], w=[r_rBs])
                S.op("dve", lambda e: e.tensor_tensor(out=qkTs, in0=rAs, in1=rBs, op=ALU.add), r=[r_rAs, r_rBs], w=[r_qkTs])
                S.op("pool", lambda e: e.tensor_tensor(out=qdTs.rearrange("p h (b l) -> p h b l", b=4), in0=qkTs[:, 0:4, :].rearrange("p h (b l) -> p h b l", b=4), in1=qdecs.unsqueeze(2).to_broadcast([128, 4, 4, 4]), op=ALU.mult), r=[r_qkTs, r_tb], w=[r_qdTs])
                v_s, r_vs = A.get(SB * 512, BF16, "v_s"); v_s = v_s.rearrange("p (b n) -> p b n", b=SB)
                sTs, r_sTs = A.get(64, BF16, "sTs"); sTs = sTs.rearrange("p (b h l) -> p b h l", b=4, h=4)
                kps, r_kps = A.get(512, BF16, "kps"); kps = kps.rearrange("p (h d) -> p h d", h=4)
                S.op("pool", lambda e: e.memset(v_s, 0.0), w=[r_vs])
                S.op("pool", lambda e: e.memset(sTs, 0.0), w=[r_sTs])
                S.op("pool", lambda e: e.memset(kps, 0.0), w=[r_kps])
                sggs, r_sggs = A.get(SB * 512, F32, "sggs"); sggs = sggs.rearrange("p (b n) -> p b n", b=SB)
                for bb in range(SB):
                    bv_, rv_ = ring.get(1)
                    bg_, rg_ = ring.get(1)
                    for (bx, rx, c0) in ((bv_, rv_, 1536), (bg_, rg_, 2048)):
                        for k in range(8):
                            S.op("pe", lambda e, bx=bx, k=k, c0=c0, bb=bb: e.matmul(bank(bx)[0:4, :], lhsT=xsT[:, k, bb * 4:bb * 4 + 4], rhs=w_in_sb[:, k, c0:c0 + 512], start=(k == 0), stop=(k == 7)), r=[r_win, r_xsT], w=rx)
                    S.op("act", lambda e, bv_=bv_, bb=bb: e.copy(out=v_s[0:4, bb, :], in_=bank(bv_)[0:4, :]), r=rv_, w=[r_vs])
                    S.op("act", lambda e, bg_=bg_, bb=bb: e.activation(out=sggs[0:4, bb, :], in_=bank(bg_)[0:4, :], func=AF.Silu), r=rg_, w=[r_sggs])
                    S.op("pool", lambda e, bb=bb: e.tensor_tensor(out=sggs[0:4, bb, :], in0=sggs[0:4, bb, :], in1=gng[0:4], op=ALU.mult), r=[r_sggs, r_c0], w=[r_sggs])
                bs_, rs_ = ring.get(1)
                for bb in range(SB):
                    for h in range(4):
                        c0 = (bb * 4 + h) * 4
                        S.op("pe", lambda e, bb=bb, h=h, c0=c0: e.matmul(bank(bs_)[0:4, c0:c0 + 4], lhsT=qkTs[:, 4 + h, bb * 4:bb * 4 + 4], rhs=qkTs[:, h, bb * 4:bb * 4 + 4], start=True, stop=True), r=[r_qkTs], w=rs_)
                S.op("dve", lambda e: e.tensor_tensor(out=sTs[0:4].rearrange("p b h l -> p b (h l)"), in0=bank(bs_)[0:4, 0:64].rearrange("p (b n) -> p b n", b=4), in1=decTs.unsqueeze(1).to_broadcast([4, 4, 16]), op=ALU.mult), r=rs_ + [r_tb], w=[r_sTs])
                Snew, r_Snew = dbl(512, F32, "Snew", "p (h v) -> p h v", h=4)
                gsts, r_gsts = A.get(40, F32, "gsts")
                on1s, r_on1s = A.get(512, F32, "on1s")
                rets_bf, r_retsb = A.get(512, BF16, "retsb")
                for bb in range(SB):
                    bo_, ro_ = ring.get(1)
                    ovs = bank(bo_)[0:4, :].rearrange("p (h t) -> p h t", h=4)
                    for h in range(4):
                        S.op("pe", lambda e, h=h, bb=bb, ovs=ovs: e.matmul(ovs[:, h, :], lhsT=sTs[:, bb, h, :], rhs=v_s[:, bb, h * 128:(h + 1) * 128], start=True, stop=False), r=[r_sTs, r_vs], w=ro_)
                        S.op("pe", lambda e, h=h, bb=bb, ovs=ovs: e.matmul(ovs[:, h, :], lhsT=qdTs[:, h, bb * 4:bb * 4 + 4], rhs=S0b[:, bb * 4 + h, :], start=False, stop=True), r=[r_qdTs, r_S0b], w=ro_)
                    bkt_, rkt_ = ring.get(1)
                    ktvs = bankbf(bkt_)[0:4, 0:512].rearrange("p (h t) -> p h t", h=4)
                    for h in range(4):
                        S.op("pe", lambda e, h=h, bb=bb, ktvs=ktvs: e.transpose(out=ktvs[:, h, :], in_=qkTs[:, 4 + h, bb * 4:bb * 4 + 4], identity=ident_b[:]), r=[r_qkTs, r_idb], w=rkt_)
                    for h in range(4):
                        S.op("act", lambda e, h=h, ktvs=ktvs: e.activation(out=kps[0:4, h, :], in_=ktvs[:, h, :], func=AF.Identity, bias=0.0, scale=kdecs[:, h:h + 1]), r=rkt_ + [r_tb], w=[r_kps])
                    bds_, rds_ = ring.get(1)
                    dss = bank(bds_).rearrange("p (h t) -> p h t", h=4)
                    for h in range(4):
                        S.op("pe", lambda e, h=h, bb=bb, dss=dss: e.matmul(dss[:, h, :], lhsT=kps[:, h, :], rhs=v_s[:, bb, h * 128:(h + 1) * 128], start=True, stop=True), r=[r_kps, r_vs], w=rds_)
                    sn = Snew[bb % 2]
                    for h in range(4):
                        S.op("dve", lambda e, h=h, bb=bb, dss=dss, sn=sn: e.scalar_tensor_tensor(out=sn[:, h, :], in0=S0f[:, bb * 4 + h, :], scalar=tabs["gCs"][h], in1=dss[:, h, :], op0=ALU.mult, op1=ALU.add), r=rds_ + [r_S0f], w=[r_Snew[bb % 2]])
                    out_toks.append(S.dma("sp", lambda e, bb=bb, sn=sn: e.dma_start(out=rets[bb * 4:(bb + 1) * 4].rearrange("h k v -> k h v"), in_=sn), r=[r_Snew[bb % 2]]))
                    for h in range(4):
                        S.op("dve", lambda e, h=h, ovs=ovs: e.bn_stats(out=gsts[0:4, h * 6:(h + 1) * 6], in_=ovs[:, h, :]), r=ro_, w=[r_gsts])
                        S.op("dve", lambda e, h=h: e.bn_aggr(out=gsts[0:4, 24 + 2 * h:26 + 2 * h], in_=gsts[0:4, h * 6:(h + 1) * 6]), r=[r_gsts], w=[r_gsts])
                    mvs = gsts[0:4, 24:32].rearrange("p (h two) -> p h two", two=2)
                    S.op("act", lambda e: e.activation(out=gsts[0:4, 32:36], in_=mvs[:, :, 1], func=AF.Sqrt, bias=eps_gn[0:4], scale=1.0), r=[r_gsts, r_eps], w=[r_gsts])
                    S.op("dve", lambda e: e.reciprocal(out=gsts[0:4, 32:36], in_=gsts[0:4, 32:36]), r=[r_gsts], w=[r_gsts])
                    S.op("dve", lambda e: e.scalar_tensor_tensor(out=gsts[0:4, 36:40], in0=mvs[:, :, 0], scalar=-1.0, in1=gsts[0:4, 32:36], op0=ALU.mult, op1=ALU.mult), r=[r_gsts], w=[r_gsts])
                    for h in range(4):
                        S.op("act", lambda e, h=h, ovs=ovs: e.activation(out=on1s[0:4, h * 128:(h + 1) * 128], in_=ovs[:, h, :], func=AF.Identity, bias=gsts[0:4, 36 + h:37 + h], scale=gsts[0:4, 32 + h:33 + h]), r=ro_ + [r_gsts], w=[r_on1s])
                    S.op("pool", lambda e, bb=bb: e.tensor_tensor(out=rets_bf[0:4, :], in0=on1s[0:4, :], in1=sggs[0:4, bb, :], op=ALU.mult), r=[r_on1s, r_sggs], w=[r_retsb])
                    brt_, rrt_ = ring.get(1)
                    rtvs = bankbf(brt_)[:, 0:16].rearrange("p (h t) -> p h t", h=4)
                    for h in range(4):
                        S.op("pe", lambda e, h=h, rtvs=rtvs: e.transpose(out=rtvs[:, h, :], in_=rets_bf[0:4, h * 128:(h + 1) * 128], identity=ident_b[0:4, 0:4]), r=[r_retsb, r_idb], w=rrt_)
                    S.op("dve", lambda e, rtvs=rtvs, bb=bb: e.tensor_copy(out=catTs[:, 4:8, bb * 4:bb * 4 + 4], in_=rtvs), r=rrt_, w=[r_catTs])
                by_, ry_ = ring.get(2)
                for hf in range(2):
                    for c in range(8):
                        S.op("pe", lambda e, hf=hf, c=c: e.matmul(bank(by_ + hf)[0:NS, :], lhsT=catTs[:, c, :], rhs=w_o_sb[:, c, hf * 512:(hf + 1) * 512], start=(c == 0), stop=(c == 7)), r=[r_catTs, r_wo], w=[ry_[hf]])
                zs, r_zs = A.get(D, F32, "zs")
                scs, r_scs = A.get(16, F32, "scs")
                S.op("dve", lambda e: e.scalar_tensor_tensor(out=zs[0:NS, :], in0=XS[:], scalar=ALPHA, in1=bank(by_, 2)[0:NS, :], op0=ALU.mult, op1=ALU.add), r=[r_XS] + ry_, w=[r_zs])
                layer_norm_tile(zs[0:NS, :], r_zs, XS[:], r_XS, NS, scs, r_scs, lnt0, r_lnt0)

            if with_sample:
                sample_l0()
            chk('l0')
            S.set_phase('ffn0')
            ffn(0, False)
            chk('ffn0')
            S.set_phase('l1')

            A.at(0)
            wdq_sb, r_wdq = A.get(8 * 512, BF16, "wdq"); wdq_sb = wdq_sb.rearrange("p (k n) -> p k n", k=8)
            wuq_sb, r_wuq = A.get(4 * 1536, BF16, "wuq"); wuq_sb = wuq_sb.rearrange("p (k n) -> p k n", k=4)
            wdkv_sb, r_wdkv = A.get(8 * 320, BF16, "wdkv"); wdkv_sb = wdkv_sb.rearrange("p (k n) -> p k n", k=8)
            wukT_sb, r_wukT = A.get(8 * 256, BF16, "wukT"); wukT_sb = wukT_sb.rearrange("p (h c) -> p h c", h=8)
            wuv_sb, r_wuv = A.get(2 * 1024, BF16, "wuv"); wuv_sb = wuv_sb.rearrange("p (k n) -> p k n", k=2)
            wo1_sb, r_wo1 = A.get(8 * 1024, BF16, "wo1"); wo1_sb = wo1_sb.rearrange("p (k n) -> p k n", k=8)
            c1buf, r_c1 = A.get(512 + 256, F32, "c1")
            qng = c1buf[:, 0:512]
            kvng = c1buf[:, 512:768]
            lnt1, r_lnt1 = A.get(2 * D, F32, "lnt1")
            ckvT, r_ckvT = A.get(2 * SEQ, BF16, "ckvT", nres=NT); ckvT = ckvT.rearrange("p (k n) -> p k n", k=2)
            ckvK, r_ckvK = A.get(NT * 256, BF16, "ckvK", nres=NT); ckvK = ckvK.rearrange("p (t c) -> p t c", t=NT)
            kpeT, r_kpeT = A.get(SEQ, BF16, "kpeT", nres=NT)
            S.dma("pool", lambda e: e.dma_start(out=wdq_sb, in_=w_dq.rearrange("(k p) n -> p k n", p=128)), w=[r_wdq])
            S.dma("pool", lambda e: e.dma_start(out=wdkv_sb, in_=w_dkv.rearrange("(k p) n -> p k n", p=128)), w=[r_wdkv])
            S.dma("pool", lambda e: e.dma_start(out=wuq_sb, in_=w_uq.rearrange("(k p) n -> p k n", p=128)), w=[r_wuq])
            S.dma("pool", lambda e: e.dma_start(out=wukT_sb, in_=w_ukT), w=[r_wukT])
            S.dma("pool", lambda e: e.dma_start(out=wuv_sb, in_=w_uv.rearrange("(k p) n -> p k n", p=128)), w=[r_wuv])
            S.dma("pool", lambda e: e.dma_start(out=wo1_sb, in_=w_o1.rearrange("(k p) n -> p k n", p=128)), w=[r_wo1])
            S.dma("sp", lambda e: e.dma_start(out=qng, in_=qn_g.partition_broadcast(128)), w=[r_c1])
            S.dma("sp", lambda e: e.dma_start(out=kvng, in_=kvn_g.partition_broadcast(128)), w=[r_c1])
            load_ln(lnt1, r_lnt1, ln_mix_g[1], ln_mix_b[1])
            L1_BUF = A.cur

            xT1, r_xT1 = dbl(1024, BF16, "xT1", "p (k t) -> p k t", k=8)
            rt1, r_rt1 = dbl(128, F32, "rt1")
            sc1, r_sc1 = dbl(16, F32, "sc1")
            cqn, r_cqn = dbl(512, BF16, "cqn")
            cqnT, r_cqnT = dbl(512, BF16, "cqnT", "p (k t) -> p k t", k=4)
            ckv_f, r_ckvf = dbl(256, F32, "ckv_f")
            kpe_f, r_kpef = dbl(64, F32, "kpe_f")
            kpe_t, r_kpet = A.get(128, F32, "kpe_t")
            kpe_b, r_kpeb = dbl(128, BF16, "kpe_b")
            qnT0_, r_qnT0_ = A.get(1024, BF16, "qnT"); qnT0_ = qnT0_.rearrange("p (h t) -> p h t", h=8); qnT = [qnT0_, qnT0_]; r_qnT = [r_qnT0_, r_qnT0_]
            qpe_t, r_qpet = A.get(1024, F32, "qpe_t")
            junk, r_junk = qpe_t[:, 0:512], r_qpet
            qpe_b, r_qpeb = dbl(512, BF16, "qpe_b")
            qpeT, r_qpeT = dbl(1024, BF16, "qpeT", "p (h t) -> p h t", h=8)
            qlT, r_qlT = dbl(2048, BF16, "qlT", "p (h k t) -> p h k t", h=8, k=2)
            p_sb, r_p = dbl(SEQ, BF16, "p_sb")
            pT_sb, r_pT = dbl(SEQ, BF16, "pT_sb")
            st1, r_st1 = dbl(8, F32, "st1")
            ol_b, r_olb = dbl(256, BF16, "ol_b")
            olT, r_olT = dbl(256, BF16, "olT", "p (k t) -> p k t", k=2)
            oT10_, r_oT10_ = A.get(1024, BF16, "oT1"); oT10_ = oT10_.rearrange("p (h t) -> p h t", h=8); oT1 = [oT10_, oT10_]; r_oT1 = [r_oT10_, r_oT10_]
            z10, r_z10 = A.get(D, F32, "z1"); z1 = [z10, z10]; r_z1 = [r_z10, r_z10]
            print("layer1 arena bytes", A.cur)

            for i_ in range(2):
                S.op("pool", lambda e, i_=i_: e.memset(qpeT[i_], 0.0), w=[r_qpeT[i_]])
            def l1A(t):
                pb = t % 2
                tok = slice(t * 128, (t + 1) * 128)
                S.dma("sp", lambda e, pb=pb, t=t: e.dma_start(out=rt1[pb][:, 0:64], in_=tdr["cc1"][:, t, :]), w=[r_rt1[pb]])
                S.dma("sp", lambda e, pb=pb, t=t: e.dma_start(out=rt1[pb][:, 64:128], in_=tdr["ss1"][:, t, :]), w=[r_rt1[pb]])
                transpose_x(X[:, t, :], rX[t], xT1[pb], r_xT1[pb])
                bcq, rcq = ring.get(1)
                bkv, rkv = ring.get(1)
                for k in range(8):
                    S.op("pe", lambda e, k=k, pb=pb, bcq=bcq: e.matmul(bank(bcq), lhsT=xT1[pb][:, k, :], rhs=wdq_sb[:, k, :], start=(k == 0), stop=(k == 7)), r=[r_xT1[pb], r_wdq], w=rcq)
                for k in range(8):
                    S.op("pe", lambda e, k=k, pb=pb, bkv=bkv: e.matmul(bank(bkv)[:, 0:320], lhsT=xT1[pb][:, k, :], rhs=wdkv_sb[:, k, :], start=(k == 0), stop=(k == 7)), r=[r_xT1[pb], r_wdkv], w=rkv)
                sc = sc1[pb]
                S.op("act", lambda e, bcq=bcq, sc=sc: e.activation(out=junk, in_=bank(bcq), func=AF.Square, accum_out=sc[:, 0:1]), r=rcq, w=[r_junk, r_sc1[pb]])
                S.op("act", lambda e, sc=sc: e.activation(out=sc[:, 1:2], in_=sc[:, 0:1], func=AF.Sqrt, bias=eps_rms, scale=1.0 / 512), r=[r_sc1[pb], r_eps], w=[r_sc1[pb]])
                S.op("dve", lambda e, sc=sc: e.reciprocal(out=sc[:, 1:2], in_=sc[:, 1:2]), r=[r_sc1[pb]], w=[r_sc1[pb]])
                S.op("dve", lambda e, bcq=bcq, sc=sc, pb=pb: e.scalar_tensor_tensor(out=cqn[pb], in0=bank(bcq), scalar=sc[:, 1:2], in1=qng, op0=ALU.mult, op1=ALU.mult), r=rcq + [r_sc1[pb], r_c1], w=[r_cqn[pb]])
                S.op("act", lambda e, bkv=bkv, sc=sc: e.activation(out=junk[:, 0:256], in_=bank(bkv)[:, 0:256], func=AF.Square, accum_out=sc[:, 2:3]), r=rkv, w=[r_junk, r_sc1[pb]])
                S.op("act", lambda e, sc=sc: e.activation(out=sc[:, 3:4], in_=sc[:, 2:3], func=AF.Sqrt, bias=eps_rms, scale=1.0 / 256), r=[r_sc1[pb], r_eps], w=[r_sc1[pb]])
                S.op("dve", lambda e, sc=sc: e.reciprocal(out=sc[:, 3:4], in_=sc[:, 3:4]), r=[r_sc1[pb]], w=[r_sc1[pb]])
                S.op("dve", lambda e, bkv=bkv, sc=sc, pb=pb: e.scalar_tensor_tensor(out=ckv_f[pb], in0=bank(bkv)[:, 0:256], scalar=sc[:, 3:4], in1=kvng, op0=ALU.mult, op1=ALU.mult), r=rkv + [r_sc1[pb], r_c1], w=[r_ckvf[pb]])
                out_toks.append(S.dma("sp", lambda e, pb=pb, tok=tok: e.dma_start(out=ckvp[tok, :], in_=ckv_f[pb]), r=[r_ckvf[pb]]))
                S.op("pool", lambda e, pb=pb, t=t: e.tensor_copy(out=ckvK[:, t, :], in_=ckv_f[pb]), r=[r_ckvf[pb]], w=[r_ckvK[t]])
                S.op("dve", lambda e, bkv=bkv, pb=pb: e.tensor_tensor(out=kpe_t[:, 0:64], in0=bank(bkv)[:, 256:320], in1=rt1[pb][:, 0:64], op=ALU.mult), r=rkv + [r_rt1[pb]], w=[r_kpet])
                S.op("dve", lambda e, bkv=bkv, pb=pb: e.tensor_tensor(out=kpe_t[:, 64:96], in0=bank(bkv)[:, 288:320], in1=rt1[pb][:, 64:96], op=ALU.mult), r=rkv + [r_rt1[pb]], w=[r_kpet])
                S.op("dve", lambda e, bkv=bkv, pb=pb: e.tensor_tensor(out=kpe_t[:, 96:128], in0=bank(bkv)[:, 256:288], in1=rt1[pb][:, 96:128], op=ALU.mult), r=rkv + [r_rt1[pb]], w=[r_kpet])
                S.op("pool", lambda e, pb=pb: e.tensor_tensor(out=kpe_f[pb], in0=kpe_t[:, 0:64], in1=kpe_t[:, 64:128], op=ALU.add), r=[r_kpet], w=[r_kpef[pb]])
                out_toks.append(S.dma("sp", lambda e, pb=pb, tok=tok: e.dma_start(out=kpep[tok, :], in_=kpe_f[pb]), r=[r_kpef[pb]]))
                S.op("pool", lambda e, pb=pb: e.tensor_copy(out=kpe_b[pb][:, 0:64], in_=kpe_f[pb]), r=[r_kpef[pb]], w=[r_kpeb[pb]])
                S.op("pool", lambda e, pb=pb: e.tensor_copy(out=kpe_b[pb][:, 64:128], in_=kpe_f[pb]), r=[r_kpef[pb]], w=[r_kpeb[pb]])
                btr, rtr = ring.get(2)
                trv = bankbf(btr, 2)
                for k in range(4):
                    S.op("pe", lambda e, k=k, trv=trv, pb=pb: e.transpose(out=trv[:, k * 128:(k + 1) * 128], in_=cqn[pb][:, k * 128:(k + 1) * 128], identity=ident_b[:]), r=[r_cqn[pb], r_idb], w=rtr)
                for k in range(2):
                    S.op("pe", lambda e, k=k, trv=trv, t=t: e.transpose(out=trv[:, 1024 + k * 128:1024 + (k + 1) * 128], in_=ckvK[:, t, k * 128:(k + 1) * 128], identity=ident_b[:]), r=[r_ckvK[t], r_idb], w=rtr)
                S.op("pe", lambda e, trv=trv, pb=pb: e.transpose(out=trv[:, 1280:1408], in_=kpe_b[pb], identity=ident_b[:]), r=[r_kpeb[pb], r_idb], w=rtr)
                S.op("act", lambda e, trv=trv, pb=pb: e.copy(out=cqnT[pb], in_=trv[:, 0:512].rearrange("p (k t) -> p k t", k=4)), r=rtr, w=[r_cqnT[pb]])
                S.op("dve", lambda e, trv=trv, tok=tok, t=t: e.tensor_copy(out=ckvT[:, :, tok], in_=trv[:, 1024:1280].rearrange("p (k t) -> p k t", k=2)), r=rtr, w=[r_ckvT[t]])
                S.op("dve", lambda e, trv=trv, tok=tok, t=t: e.tensor_copy(out=kpeT[:, tok], in_=trv[:, 1280:1408]), r=rtr, w=[r_kpeT[t]])
                for hh in range(2):
                    bqn, rqn = ring.get(1)
                    qv = bank(bqn).rearrange("p (h t) -> p h t", h=4)
                    for h4 in range(4):
                        h = hh * 4 + h4
                        for k in range(4):
                            S.op("pe", lambda e, qv=qv, h4=h4, h=h, k=k, pb=pb: e.matmul(qv[:, h4, :], lhsT=wuq_sb[:, k, h * 128:(h + 1) * 128], rhs=cqnT[pb][:, k, :], start=(k == 0), stop=(k == 3)), r=[r_wuq, r_cqnT[pb]], w=rqn)
                    evac("act" if hh == 0 else "dve", qnT[pb][:, hh * 4:hh * 4 + 4, :], qv, rqn, [r_qnT[pb]])
                bqp, rqp = ring.get(1)
                for k in range(4):
                    S.op("pe", lambda e, k=k, bqp=bqp, pb=pb: e.matmul(bank(bqp), lhsT=cqnT[pb][:, k, :], rhs=wuq_sb[:, k, 1024:1536], start=(k == 0), stop=(k == 3)), r=[r_wuq, r_cqnT[pb]], w=rqp)
                qpv = bank(bqp).rearrange("p (h d) -> p h d", h=8)
                qA = qpe_t[:, 0:512].rearrange("p (h d) -> p h d", h=8)
                qB = qpe_t[:, 512:1024].rearrange("p (h d) -> p h d", h=8)
                ccb = rt1[pb][:, 0:64].unsqueeze(1)
                ssb1 = rt1[pb][:, 64:96].unsqueeze(1)
                ssb2 = rt1[pb][:, 96:128].unsqueeze(1)
                S.op("dve", lambda e, qpv=qpv, ccb=ccb: e.tensor_tensor(out=qA, in0=qpv, in1=ccb.to_broadcast([128, 8, 64]), op=ALU.mult), r=rqp + [r_rt1[pb]], w=[r_qpet])
                S.op("dve", lambda e, qpv=qpv, ssb1=ssb1: e.tensor_tensor(out=qB[:, :, 0:32], in0=qpv[:, :, 32:64], in1=ssb1.to_broadcast([128, 8, 32]), op=ALU.mult), r=rqp + [r_rt1[pb]], w=[r_qpet])
                S.op("dve", lambda e, qpv=qpv, ssb2=ssb2: e.tensor_tensor(out=qB[:, :, 32:64], in0=qpv[:, :, 0:32], in1=ssb2.to_broadcast([128, 8, 32]), op=ALU.mult), r=rqp + [r_rt1[pb]], w=[r_qpet])
                S.op("pool", lambda e, pb=pb: e.tensor_tensor(out=qpe_b[pb], in0=qpe_t[:, 0:512], in1=qpe_t[:, 512:1024], op=ALU.add), r=[r_qpet], w=[r_qpeb[pb]])
                btq, rtq = ring.get(1)
                tqv = bankbf(btq)[:, 0:512].rearrange("p (a t) -> p a t", a=4)
                for a in range(4):
                    S.op("pe", lambda e, a=a, tqv=tqv, pb=pb: e.transpose(out=tqv[:, a, :], in_=qpe_b[pb][:, a * 128:(a + 1) * 128], identity=ident_b[:]), r=[r_qpeb[pb], r_idb], w=rtq)
                qz = qpeT[pb].rearrange("p (a two) t -> p a two t", two=2)
                S.op("act", lambda e, tqv=tqv, qz=qz: e.copy(out=qz[0:64, :, 0, :], in_=tqv[0:64]), r=rtq, w=[r_qpeT[pb]])
                S.op("dve", lambda e, tqv=tqv, qz=qz: e.tensor_copy(out=qz[64:128, :, 1, :], in_=tqv[64:128]), r=rtq, w=[r_qpeT[pb]])
                for hh in range(4):
                    bql, rql = ring.get(1)
                    qlv = bank(bql).rearrange("p (h k t) -> p h k t", h=2, k=2)
                    for h2 in range(2):
                        h = hh * 2 + h2
                        for k in range(2):
                            S.op("pe", lambda e, qlv=qlv, h2=h2, h=h, k=k, pb=pb: e.matmul(qlv[:, h2, k, :], lhsT=wukT_sb[:, h, k * 128:(k + 1) * 128], rhs=qnT[pb][:, h, :], start=True, stop=True), r=[r_wukT, r_qnT[pb]], w=rql)
                    evac("act" if hh % 2 == 0 else "dve", qlT[pb][:, hh * 2:hh * 2 + 2, :, :], qlv, rql, [r_qlT[pb]])

            l1ctx = {}
            def l1P1a(t, h):
                pb = t % 2
                hp = h % 2
                nk = (t + 1) * 128
                nb = (nk + 511) // 512
                nbanks = 1 if nb == 1 else (2 if nb == 2 else 4)
                bs, rs = ring.get(nbanks)
                sfull = bank(bs, nbanks)
                stt = st1[hp]
                chs = [(c0_, min(nk, c0_ + 1024)) for c0_ in range(0, nk, 1024)]
                for ci, (a_, b_) in enumerate(chs):
                    for gi in range(a_ // 512, (b_ + 511) // 512):
                        k0 = gi * 512
                        kn = min(512, nk - k0)
                        ks = slice(k0, k0 + kn)
                        kres = r_ckvT[gi * 4:gi * 4 + (kn + 127) // 128]
                        kpres = r_kpeT[gi * 4:gi * 4 + (kn + 127) // 128]
                        last_diag = (gi == nb - 1)
                        parts = [(ks, False)] if not last_diag else ([(slice(k0, nk - 128), False)] if nk - 128 > k0 else []) + [(slice(nk - 128, nk), True)]
                        for (ks, dg) in parts:
                            S.op("pe", lambda e, sfull=sfull, ks=ks, h=h, pb=pb: e.matmul(sfull[:, ks], lhsT=qlT[pb][:, h, 0, :], rhs=ckvT[:, 0, ks], start=True, stop=False), r=[r_qlT[pb]] + kres, w=[rs[gi]])
                            S.op("pe", lambda e, sfull=sfull, ks=ks, h=h, pb=pb: e.matmul(sfull[:, ks], lhsT=qlT[pb][:, h, 1, :], rhs=ckvT[:, 1, ks], start=False, stop=False), r=[r_qlT[pb]] + kres, w=[rs[gi]])
                            S.op("pe", lambda e, sfull=sfull, ks=ks, h=h, pb=pb, dg=dg: e.matmul(sfull[:, ks], lhsT=qpeT[pb][:, h, :], rhs=kpeT[:, ks], start=False, stop=(not dg)), r=[r_qpeT[pb]] + kpres, w=[rs[gi]])
                            if dg:
                                S.op("pe", lambda e, sfull=sfull, ks=ks: e.matmul(sfull[:, ks], lhsT=ident_b[:], rhs=maskd[:], start=False, stop=True), r=[r_idb, r_maskd], w=[rs[gi]])
                    S.op("dve", lambda e, sfull=sfull, a_=a_, b_=b_, ci=ci, stt=stt: e.reduce_max(out=stt[:, 4 + ci:5 + ci], in_=sfull[:, a_:b_], axis=AX.X), r=rs[a_ // 512:(b_ + 511) // 512], w=[r_st1[hp]])
                if len(chs) == 2:
                    S.op("dve", lambda e, stt=stt: e.tensor_tensor(out=stt[:, 4:5], in0=stt[:, 4:5], in1=stt[:, 5:6], op=ALU.max), r=[r_st1[hp]], w=[r_st1[hp]])
                S.op("dve", lambda e, stt=stt: e.tensor_scalar(out=stt[:, 1:2], in0=stt[:, 4:5], scalar1=-MLA_SCALE, scalar2=None, op0=ALU.mult), r=[r_st1[hp]], w=[r_st1[hp]])
                l1ctx[(t, h)] = (sfull, rs, nb, chs)

            def l1P1b(t, h):
                pb = t % 2
                hp = h % 2
                nk = (t + 1) * 128
                stt = st1[hp]
                sfull, rs, nb, chs = l1ctx[(t, h)]
                for ci, (a_, b_) in enumerate(chs):
                    S.op("act", lambda e, sfull=sfull, a_=a_, b_=b_, ci=ci, stt=stt, hp=hp: e.activation(out=p_sb[hp][:, a_:b_], in_=sfull[:, a_:b_], func=AF.Exp, bias=stt[:, 1:2], scale=MLA_SCALE, accum_out=stt[:, 6 + ci:7 + ci]), r=rs[0:nb] + [r_st1[hp]], w=[r_p[hp], r_st1[hp]])

            def l1P2a(t, h):
                pb = t % 2
                hp = h % 2
                nk = (t + 1) * 128
                stt = st1[hp]
                nkt = t + 1
                for g0 in range(0, nkt, 4):
                    bpt, rpt = ring.get(1)
                    ptv = bankbf(bpt)
                    n_ = min(4, nkt - g0)
                    for i_ in range(n_):
                        S.op("pe", lambda e, i_=i_, g0=g0, ptv=ptv, hp=hp: e.transpose(out=ptv[:, i_ * 128:(i_ + 1) * 128], in_=p_sb[hp][:, (g0 + i_) * 128:(g0 + i_ + 1) * 128], identity=ident_b[:]), r=[r_p[hp], r_idb], w=rpt)
                    evac("act" if (g0 // 4) % 2 == 0 else "dve", pT_sb[hp][:, g0 * 128:(g0 + n_) * 128], ptv[:, 0:n_ * 128], rpt, [r_pT[hp]])

            def l1P2b(t, h):
                pb = t % 2
                hp = h % 2
                nk = (t + 1) * 128
                stt = st1[hp]
                nkt = t + 1
                bol, rol = ring.get(1)
                for kt in range(nkt):
                    S.op("pe", lambda e, kt=kt, bol=bol, hp=hp: e.matmul(bank(bol)[:, 0:256], lhsT=pT_sb[hp][:, kt * 128:(kt + 1) * 128], rhs=ckvK[:, kt, :], start=(kt == 0), stop=(kt == nkt - 1)), r=[r_pT[hp], r_ckvK[kt]], w=rol)
                chs = [(c0_, min(nk, c0_ + 1024)) for c0_ in range(0, nk, 1024)]
                if len(chs) == 2:
                    S.op("dve", lambda e, stt=stt: e.tensor_tensor(out=stt[:, 6:7], in0=stt[:, 6:7], in1=stt[:, 7:8], op=ALU.add), r=[r_st1[hp]], w=[r_st1[hp]])
                S.op("dve", lambda e, stt=stt: e.reciprocal(out=stt[:, 3:4], in_=stt[:, 6:7]), r=[r_st1[hp]], w=[r_st1[hp]])
                S.op("dve", lambda e, bol=bol, stt=stt, hp=hp: e.tensor_scalar(out=ol_b[hp], in0=bank(bol)[:, 0:256], scalar1=stt[:, 3:4], scalar2=None, op0=ALU.mult), r=rol + [r_st1[hp]], w=[r_olb[hp]])
                bot, rot = ring.get(1)
                otv = bankbf(bot)[:, 0:256].rearrange("p (k t) -> p k t", k=2)
                for k in range(2):
                    S.op("pe", lambda e, k=k, otv=otv, hp=hp: e.transpose(out=otv[:, k, :], in_=ol_b[hp][:, k * 128:(k + 1) * 128], identity=ident_b[:]), r=[r_olb[hp], r_idb], w=rot)
                S.op("dve", lambda e, otv=otv, hp=hp: e.tensor_copy(out=olT[hp], in_=otv), r=rot, w=[r_olT[hp]])
                boh, roh = ring.get(1)
                for k in range(2):
                    S.op("pe", lambda e, k=k, boh=boh, h=h, hp=hp: e.matmul(bank(boh)[:, 0:128], lhsT=wuv_sb[:, k, h * 128:(h + 1) * 128], rhs=olT[hp][:, k, :], start=(k == 0), stop=(k == 1)), r=[r_wuv, r_olT[hp]], w=roh)
                evac("act" if h % 2 == 0 else "dve", oT1[pb][:, h, :], bank(boh)[:, 0:128], roh, [r_oT1[pb]])


            def l1out(t):
                pb = t % 2
                by, ry = ring.get(2)
                for hf in range(2):
                    for c in range(8):
                        S.op("pe", lambda e, hf=hf, c=c, by=by, pb=pb: e.matmul(bank(by + hf), lhsT=oT1[pb][:, c, :], rhs=wo1_sb[:, c, hf * 512:(hf + 1) * 512], start=(c == 0), stop=(c == 7)), r=[r_oT1[pb], r_wo1], w=[ry[hf]])
                S.op("dve", lambda e, by=by, pb=pb, t=t: e.scalar_tensor_tensor(out=z1[pb], in0=X[:, t, :], scalar=ALPHA, in1=bank(by, 2), op0=ALU.mult, op1=ALU.add), r=[rX[t]] + ry, w=[r_z1[pb]])
                layer_norm_tile(z1[pb], r_z1[pb], X[:, t, :], rX[t], 128, sc1[pb], r_sc1[pb], lnt1, r_lnt1)


            l1A(0)
            for t in range(NT):
                l1P1a(t, 0)
                l1P1b(t, 0)
                for h in range(1, 8):
                    l1P2a(t, h - 1)
                    l1P1a(t, h)
                    l1P1b(t, h)
                    l1P2b(t, h - 1)
                    if h == 4 and t + 1 < NT:
                        l1A(t + 1)
                l1P2a(t, 7)
                l1P2b(t, 7)
                l1out(t)
                chk(f'l1t{t}')

            def sample_l1():
                A.at(L1_BUF)
                NSTEP = NPAGES * 128 // 1024
                tb1, r_tb1 = A.get(128, F32, "tb1")
                cc1s = tb1[0:NS, 0:64]; ss1s = tb1[0:NS, 64:128]
                S.dma("sp", lambda e: e.dma_start(out=cc1s, in_=tdr["cc1s"]), w=[r_tb1])
                S.dma("sp", lambda e: e.dma_start(out=ss1s, in_=tdr["ss1s"]), w=[r_tb1])
                msk, r_msk = A.get(SB * NS, BF16, "msk")
                S.dma("pool", lambda e: e.dma_start(out=msk[0:32, :], in_=tdr["masksb"]), w=[r_msk])
                mskv = msk.rearrange("p (b t) -> p b t", b=SB)
                pt_sb, r_pt = A.get(SB, I32, "pt_sb")
                S.dma("sp", lambda e: e.dma_start(out=pt_sb, in_=ptab), w=[r_pt])
                xsT1, r_xsT1 = A.get(128, BF16, "xsT1"); xsT1 = xsT1.rearrange("p (k t) -> p k t", k=8)
                transpose_x(XS[:], r_XS, xsT1, r_xsT1, npart=NS)
                bcq, rcq = ring.get(1)
                bkv, rkv = ring.get(1)
                for k in range(8):
                    S.op("pe", lambda e, k=k: e.matmul(bank(bcq)[0:NS, :], lhsT=xsT1[:, k, :], rhs=wdq_sb[:, k, :], start=(k == 0), stop=(k == 7)), r=[r_xsT1, r_wdq], w=rcq)
                for k in range(8):
                    S.op("pe", lambda e, k=k: e.matmul(bank(bkv)[0:NS, 0:320], lhsT=xsT1[:, k, :], rhs=wdkv_sb[:, k, :], start=(k == 0), stop=(k == 7)), r=[r_xsT1, r_wdkv], w=rkv)
                jk, r_jk = A.get(512, F32, "jk")
                scq, r_scq = A.get(16, F32, "scq")
                cqns, r_cqns = A.get(512, BF16, "cqns")
                ckvsf, r_ckvsf = A.get(256, F32, "ckvsf")
                ckvsb, r_ckvsb = A.get(256, BF16, "ckvsb")
                kpet, r_kpets = A.get(128, F32, "kpets")
                kpesf, r_kpesf = A.get(64, F32, "kpesf")
                kpesb, r_kpesb = A.get(128, BF16, "kpesb")
                sc = scq[0:NS]
                S.op("act", lambda e: e.activation(out=jk[0:NS, :], in_=bank(bcq)[0:NS, :], func=AF.Square, accum_out=sc[:, 0:1]), r=rcq, w=[r_jk, r_scq])
                S.op("act", lambda e: e.activation(out=sc[:, 1:2], in_=sc[:, 0:1], func=AF.Sqrt, bias=eps_rms[0:NS], scale=1.0 / 512), r=[r_scq, r_eps], w=[r_scq])
                S.op("dve", lambda e: e.reciprocal(out=sc[:, 1:2], in_=sc[:, 1:2]), r=[r_scq], w=[r_scq])
                S.op("dve", lambda e: e.scalar_tensor_tensor(out=cqns[0:NS, :], in0=bank(bcq)[0:NS, :], scalar=sc[:, 1:2], in1=qng[0:NS], op0=ALU.mult, op1=ALU.mult), r=rcq + [r_scq, r_c1], w=[r_cqns])
                S.op("act", lambda e: e.activation(out=jk[0:NS, 0:256], in_=bank(bkv)[0:NS, 0:256], func=AF.Square, accum_out=sc[:, 2:3]), r=rkv, w=[r_jk, r_scq])
                S.op("act", lambda e: e.activation(out=sc[:, 3:4], in_=sc[:, 2:3], func=AF.Sqrt, bias=eps_rms[0:NS], scale=1.0 / 256), r=[r_scq, r_eps], w=[r_scq])
                S.op("dve", lambda e: e.reciprocal(out=sc[:, 3:4], in_=sc[:, 3:4]), r=[r_scq], w=[r_scq])
                S.op("dve", lambda e: e.scalar_tensor_tensor(out=ckvsf[0:NS, :], in0=bank(bkv)[0:NS, 0:256], scalar=sc[:, 3:4], in1=kvng[0:NS], op0=ALU.mult, op1=ALU.mult), r=rkv + [r_scq, r_c1], w=[r_ckvsf])
                out_toks.append(S.dma("sp", lambda e: e.dma_start(out=ckvs, in_=ckvsf[0:NS, :]), r=[r_ckvsf]))
                S.op("pool", lambda e: e.tensor_copy(out=ckvsb[0:NS, :], in_=ckvsf[0:NS, :]), r=[r_ckvsf], w=[r_ckvsb])
                S.op("dve", lambda e: e.tensor_tensor(out=kpet[0:NS, 0:64], in0=bank(bkv)[0:NS, 256:320], in1=cc1s, op=ALU.mult), r=rkv + [r_tb1], w=[r_kpets])
                S.op("dve", lambda e: e.tensor_tensor(out=kpet[0:NS, 64:96], in0=bank(bkv)[0:NS, 288:320], in1=ss1s[:, 0:32], op=ALU.mult), r=rkv + [r_tb1], w=[r_kpets])
                S.op("dve", lambda e: e.tensor_tensor(out=kpet[0:NS, 96:128], in0=bank(bkv)[0:NS, 256:288], in1=ss1s[:, 32:64], op=ALU.mult), r=rkv + [r_tb1], w=[r_kpets])
                S.op("pool", lambda e: e.tensor_tensor(out=kpesf[0:NS, :], in0=kpet[0:NS, 0:64], in1=kpet[0:NS, 64:128], op=ALU.add), r=[r_kpets], w=[r_kpesf])
                out_toks.append(S.dma("sp", lambda e: e.dma_start(out=kpes, in_=kpesf[0:NS, :]), r=[r_kpesf]))
                S.op("pool", lambda e: e.tensor_copy(out=kpesb[0:NS, 0:64], in_=kpesf[0:NS, :]), r=[r_kpesf], w=[r_kpesb])
                S.op("pool", lambda e: e.tensor_copy(out=kpesb[0:NS, 64:128], in_=kpesf[0:NS, :]), r=[r_kpesf], w=[r_kpesb])
                btr, rtr = ring.get(1)
                trv = bankbf(btr)
                for k in range(4):
                    S.op("pe", lambda e, k=k: e.transpose(out=trv[:, k * 16:(k + 1) * 16], in_=cqns[0:NS, k * 128:(k + 1) * 128], identity=ident_b[0:NS, 0:NS]), r=[r_cqns, r_idb], w=rtr)
                for k in range(2):
                    S.op("pe", lambda e, k=k: e.transpose(out=trv[:, 64 + k * 16:64 + (k + 1) * 16], in_=ckvsb[0:NS, k * 128:(k + 1) * 128], identity=ident_b[0:NS, 0:NS]), r=[r_ckvsb, r_idb], w=rtr)
                S.op("pe", lambda e: e.transpose(out=trv[:, 96:112], in_=kpesb[0:NS, :], identity=ident_b[0:NS, 0:NS]), r=[r_kpesb, r_idb], w=rtr)
                trs, r_trs = A.get(112, BF16, "trs")
                S.op("act", lambda e: e.copy(out=trs, in_=trv[:, 0:112]), r=rtr, w=[r_trs])
                cqnTs = trs[:, 0:64].rearrange("p (k t) -> p k t", k=4)
                ckvTs = trs[:, 64:96].rearrange("p (k t) -> p k t", k=2)
                kpeTs = trs[:, 96:112]
                bqn, rqn = ring.get(1)
                qnv = bank(bqn)[:, 0:128].rearrange("p (h t) -> p h t", h=8)
                for h in range(8):
                    for k in range(4):
                        S.op("pe", lambda e, h=h, k=k: e.matmul(qnv[:, h, :], lhsT=wuq_sb[:, k, h * 128:(h + 1) * 128], rhs=cqnTs[:, k, :], start=(k == 0), stop=(k == 3)), r=[r_wuq, r_trs], w=rqn)
                qnTs, r_qnTs = A.get(128, BF16, "qnTs"); qnTs = qnTs.rearrange("p (h t) -> p h t", h=8)
                S.op("act", lambda e: e.copy(out=qnTs, in_=qnv), r=rqn, w=[r_qnTs])
                bqp, rqp = ring.get(1)
                for k in range(4):
                    S.op("pe", lambda e, k=k: e.matmul(bank(bqp)[0:NS, :], lhsT=cqnTs[:, k, :], rhs=wuq_sb[:, k, 1024:1536], start=(k == 0), stop=(k == 3)), r=[r_wuq, r_trs], w=rqp)
                qpv = bank(bqp)[0:NS, :].rearrange("p (h d) -> p h d", h=8)
                qpt, r_qpt = A.get(1024, F32, "qpt")
                qA = qpt[0:NS, 0:512].rearrange("p (h d) -> p h d", h=8)
                qB = qpt[0:NS, 512:1024].rearrange("p (h d) -> p h d", h=8)
                S.op("dve", lambda e: e.tensor_tensor(out=qA, in0=qpv, in1=cc1s.unsqueeze(1).to_broadcast([NS, 8, 64]), op=ALU.mult), r=rqp + [r_tb1], w=[r_qpt])
                S.op("dve", lambda e: e.tensor_tensor(out=qB[:, :, 0:32], in0=qpv[:, :, 32:64], in1=ss1s[:, 0:32].unsqueeze(1).to_broadcast([NS, 8, 32]), op=ALU.mult), r=rqp + [r_tb1], w=[r_qpt])
                S.op("dve", lambda e: e.tensor_tensor(out=qB[:, :, 32:64], in0=qpv[:, :, 0:32], in1=ss1s[:, 32:64].unsqueeze(1).to_broadcast([NS, 8, 32]), op=ALU.mult), r=rqp + [r_tb1], w=[r_qpt])
                qpsb, r_qpsb = A.get(512, BF16, "qpsb")
                S.op("pool", lambda e: e.tensor_tensor(out=qpsb[0:NS, :], in0=qpt[0:NS, 0:512], in1=qpt[0:NS, 512:1024], op=ALU.add), r=[r_qpt], w=[r_qpsb])
                btq, rtq = ring.get(1)
                tqv = bankbf(btq)[:, 0:64].rearrange("p (a t) -> p a t", a=4)
                for a in range(4):
                    S.op("pe", lambda e, a=a: e.transpose(out=tqv[:, a, :], in_=qpsb[0:NS, a * 128:(a + 1) * 128], identity=ident_b[0:NS, 0:NS]), r=[r_qpsb, r_idb], w=rtq)
                qpTs, r_qpTs = A.get(SB * 32, BF16, "qpTs")
                S.op("pool", lambda e: e.memset(qpTs, 0.0), w=[r_qpTs])
                qpT5 = qpTs.rearrange("p (b a two l) -> p b a two l", b=4, a=4, two=2)
                tq4 = tqv.rearrange("p a (b l) -> p a b l", b=4)
                for bb in range(SB):
                    S.op("act", lambda e, bb=bb: e.copy(out=qpT5[0:64, bb, :, 0, :], in_=tq4[0:64, :, bb, :]), r=rtq, w=[r_qpTs])
                    S.op("dve", lambda e, bb=bb: e.tensor_copy(out=qpT5[64:128, bb, :, 1, :], in_=tq4[64:128, :, bb, :]), r=rtq, w=[r_qpTs])
                qpTb = qpTs.rearrange("p (b n) -> p b n", b=4)
                bql, rql = ring.get(1)
                qlv = bank(bql)[:, 0:256].rearrange("p (h k t) -> p h k t", h=8, k=2)
                for h in range(8):
                    for k in range(2):
                        S.op("pe", lambda e, h=h, k=k: e.matmul(qlv[:, h, k, :], lhsT=wukT_sb[:, h, k * 128:(k + 1) * 128], rhs=qnTs[:, h, :], start=True, stop=True), r=[r_wukT, r_qnTs], w=rql)
                qlTs, r_qlTs = A.get(2 * SB * 32, BF16, "qlTs")
                ql5 = qlTs.rearrange("p (k b h l) -> p k b h l", k=2, b=4, h=8)
                qlv5 = qlv.rearrange("p h k (b l) -> p h k b l", b=4)
                for k in range(2):
                    for bb in range(SB):
                        S.op("act" if (k + bb) % 2 == 0 else "dve",
                             (lambda e, k=k, bb=bb: e.copy(out=ql5[:, k, bb], in_=qlv5[:, :, k, bb, :])) if (k + bb) % 2 == 0 else
                             (lambda e, k=k, bb=bb: e.tensor_copy(out=ql5[:, k, bb], in_=qlv5[:, :, k, bb, :])), r=rql, w=[r_qlTs])
                qlTb = qlTs.rearrange("p (k b n) -> p k b n", k=2, b=4)
                gk = []; r_gk = []; gp = []; r_gp = []
                for i_ in range(3):
                    v_, r_ = A.get(2048, BF16, f"gk{i_}"); gk.append(v_.rearrange("p (s c) -> p s c", s=8)); r_gk.append(r_)
                    v_, r_ = A.get(512, BF16, f"gp{i_}"); gp.append(v_.rearrange("p (s c) -> p s c", s=8)); r_gp.append(r_)
                gp2, r_gp2 = dbl(1024, BF16, "gp2", "p (s c) -> p s c", s=8)
                ckTc, r_ckTc = dbl(2048, BF16, "ckTc", "p (k n) -> p k n", k=2)
                kpTc, r_kpTc = dbl(1024, BF16, "kpTc")
                p_s, r_ps = dbl(1024, BF16, "p_s")
                pTc, r_pTc = dbl(256, BF16, "pTc", "p (i t) -> p i t", i=8)
                Oacc, r_Oacc = dbl(256, F32, "Oacc")
                ms, r_ms = dbl(8, F32, "ms")
                olb, r_olb = A.get(256, BF16, "olb")
                olTs, r_olTs = A.get(64, BF16, "olTs"); olTs = olTs.rearrange("p (k n) -> p k n", k=2)
                oTs, r_oTs = A.get(128, BF16, "oTs"); oTs = oTs.rearrange("p (h t) -> p h t", h=8)
                print("sample l1 arena bytes", A.cur)
                sctx = {}
                def sI(bb):
                    mb_ = ms[bb % 2][0:32]
                    oa = Oacc[bb % 2][0:32]
                    r_m = r_ms[bb % 2]
                    r_o = r_Oacc[bb % 2]
                    bsn, rsn = ring.get(1)
                    sn = bank(bsn)[0:32, 0:NS]
                    S.op("pe", lambda e, bb=bb, sn=sn: e.matmul(sn, lhsT=qlTb[:, 0, bb, :], rhs=ckvTs[:, 0, :], start=True, stop=False), r=[r_qlTs, r_trs], w=rsn)
                    S.op("pe", lambda e, bb=bb, sn=sn: e.matmul(sn, lhsT=qlTb[:, 1, bb, :], rhs=ckvTs[:, 1, :], start=False, stop=False), r=[r_qlTs, r_trs], w=rsn)
                    S.op("pe", lambda e, bb=bb, sn=sn: e.matmul(sn, lhsT=qpTb[:, bb, :], rhs=kpeTs, start=False, stop=False), r=[r_qpTs, r_trs], w=rsn)
                    S.op("pe", lambda e, bb=bb, sn=sn: e.matmul(sn, lhsT=ident_b[0:32, 0:32], rhs=mskv[0:32, bb, :], start=False, stop=True), r=[r_idb, r_msk], w=rsn)
                    S.op("dve", lambda e, sn=sn, mb_=mb_: e.reduce_max(out=mb_[:, 2:3], in_=sn, axis=AX.X), r=rsn, w=[r_m])
                    S.op("dve", lambda e, mb_=mb_: e.tensor_scalar(out=mb_[:, 0:1], in0=mb_[:, 2:3], scalar1=-MLA_SCALE, scalar2=None, op0=ALU.mult), r=[r_m], w=[r_m])
                    psl = p_s[0][0:32]
                    S.op("act", lambda e, sn=sn, mb_=mb_, psl=psl: e.activation(out=psl[:, 0:NS], in_=sn, func=AF.Exp, bias=mb_[:, 0:1], scale=MLA_SCALE, accum_out=mb_[:, 1:2]), r=rsn + [r_m], w=[r_ps[0], r_m])
                    bpt, rpt = ring.get(1)
                    S.op("pe", lambda e, psl=psl, bpt=bpt: e.transpose(out=bankbf(bpt)[0:NS, 0:32], in_=psl[:, 0:NS], identity=ident_b[0:32, 0:32]), r=[r_ps[0], r_idb], w=rpt)
                    S.op("act", lambda e, bpt=bpt: e.copy(out=pTc[0][0:NS, 0, :], in_=bankbf(bpt)[0:NS, 0:32]), r=rpt, w=[r_pTc[0]])
                    bo_, ro_ = ring.get(1)
                    S.op("pe", lambda e, bo_=bo_: e.matmul(bank(bo_)[0:32, 0:256], lhsT=pTc[0][0:NS, 0, :], rhs=ckvsb[0:NS, :], start=True, stop=True), r=[r_pTc[0], r_ckvsb], w=ro_)
                    S.op("dve", lambda e, bo_=bo_, oa=oa: e.tensor_copy(out=oa, in_=bank(bo_)[0:32, 0:256]), r=ro_, w=[r_o])

                def sG(bb, stp, g3):
                    S.dma("pool", lambda e, g3=g3, bb=bb, stp=stp: e.indirect_dma_start(out=gk[g3].rearrange("p s c -> p (s c)"), out_offset=None, in_=cckv[:, :], element_offset=stp * 2048,
                                                                                         in_offset=bass.IndirectOffsetOnAxis(ap=pt_sb[:, bb:bb + 1], axis=0)), r=[r_pt], w=[r_gk[g3]])
                    S.dma("pool", lambda e, g3=g3, bb=bb, stp=stp: e.indirect_dma_start(out=gp[g3].rearrange("p s c -> p (s c)"), out_offset=None, in_=ckpe[:, :], element_offset=stp * 512,
                                                                                         in_offset=bass.IndirectOffsetOnAxis(ap=pt_sb[:, bb:bb + 1], axis=0)), r=[r_pt], w=[r_gp[g3]])

                def sD(bb, stp, gs, g3):
                    S.op("act", lambda e, gs=gs, g3=g3: e.copy(out=gp2[gs][:, :, 0:64], in_=gp[g3]), r=[r_gp[g3]], w=[r_gp2[gs]])
                    S.op("dve", lambda e, gs=gs, g3=g3: e.tensor_copy(out=gp2[gs][:, :, 64:128], in_=gp[g3]), r=[r_gp[g3]], w=[r_gp2[gs]])

                def sT(bb, stp, gs, g3):
                    for hf in range(2):
                        sl = hf * 4
                        bA, rA = ring.get(1)
                        bB, rB = ring.get(1)
                        bC, rC = ring.get(1)
                        for i_ in range(4):
                            S.op("pe", lambda e, i_=i_, sl=sl, g3=g3, bA=bA: e.transpose(out=bankbf(bA)[:, i_ * 128:(i_ + 1) * 128], in_=gk[g3][:, sl + i_, 0:128], identity=ident_b[:]), r=[r_gk[g3], r_idb], w=rA)
                        for i_ in range(4):
                            S.op("pe", lambda e, i_=i_, sl=sl, g3=g3, bB=bB: e.transpose(out=bankbf(bB)[:, i_ * 128:(i_ + 1) * 128], in_=gk[g3][:, sl + i_, 128:256], identity=ident_b[:]), r=[r_gk[g3], r_idb], w=rB)
                        for i_ in range(4):
                            S.op("pe", lambda e, i_=i_, sl=sl, gs=gs, bC=bC: e.transpose(out=bankbf(bC)[:, i_ * 128:(i_ + 1) * 128], in_=gp2[gs][:, sl + i_, :], identity=ident_b[:]), r=[r_gp2[gs], r_idb], w=rC)
                        cs_ = slice(hf * 512, (hf + 1) * 512)
                        S.op("act", lambda e, gs=gs, bA=bA, cs_=cs_: e.copy(out=ckTc[gs][:, 0, cs_], in_=bankbf(bA)[:, 0:512]), r=rA, w=[r_ckTc[gs]])
                        S.op("dve", lambda e, gs=gs, bB=bB, cs_=cs_: e.tensor_copy(out=ckTc[gs][:, 1, cs_], in_=bankbf(bB)[:, 0:512]), r=rB, w=[r_ckTc[gs]])
                        if hf == 0:
                            S.op("act", lambda e, gs=gs, bC=bC, cs_=cs_: e.copy(out=kpTc[gs][:, cs_], in_=bankbf(bC)[:, 0:512]), r=rC, w=[r_kpTc[gs]])
                        else:
                            S.op("dve", lambda e, gs=gs, bC=bC, cs_=cs_: e.tensor_copy(out=kpTc[gs][:, cs_], in_=bankbf(bC)[:, 0:512]), r=rC, w=[r_kpTc[gs]])

                def sU1(bb, stp, gs):
                    mb_ = ms[bb % 2][0:32]
                    oa = Oacc[bb % 2][0:32]
                    r_m = r_ms[bb % 2]
                    r_o = r_Oacc[bb % 2]
                    bs2, rs2 = ring.get(2)
                    sfull = bank(bs2, 2)[0:32, :]
                    for hf in range(2):
                        cs_ = slice(hf * 512, (hf + 1) * 512)
                        S.op("pe", lambda e, bb=bb, gs=gs, cs_=cs_, sfull=sfull: e.matmul(sfull[:, cs_], lhsT=qlTb[:, 0, bb, :], rhs=ckTc[gs][:, 0, cs_], start=True, stop=False), r=[r_qlTs, r_ckTc[gs]], w=[rs2[hf]])
                        S.op("pe", lambda e, bb=bb, gs=gs, cs_=cs_, sfull=sfull: e.matmul(sfull[:, cs_], lhsT=qlTb[:, 1, bb, :], rhs=ckTc[gs][:, 1, cs_], start=False, stop=False), r=[r_qlTs, r_ckTc[gs]], w=[rs2[hf]])
                        S.op("pe", lambda e, bb=bb, gs=gs, cs_=cs_, sfull=sfull: e.matmul(sfull[:, cs_], lhsT=qpTb[:, bb, :], rhs=kpTc[gs][:, cs_], start=False, stop=True), r=[r_qpTs, r_kpTc[gs]], w=[rs2[hf]])
                    cur = 0 if stp % 2 == 0 else 3
                    nw = 3 if stp % 2 == 0 else 0
                    S.op("dve", lambda e, sfull=sfull, mb_=mb_: e.reduce_max(out=mb_[:, 2:3], in_=sfull, axis=AX.X), r=rs2, w=[r_m])
                    S.op("dve", lambda e, mb_=mb_, cur=cur, nw=nw: e.tensor_scalar(out=mb_[:, nw:nw + 1], in0=mb_[:, 2:3], scalar1=-MLA_SCALE, scalar2=mb_[:, cur:cur + 1], op0=ALU.mult, op1=ALU.min), r=[r_m], w=[r_m])
                    pq = gs
                    psl = p_s[pq][0:32]
                    S.op("act", lambda e, sfull=sfull, mb_=mb_, psl=psl, nw=nw: e.activation(out=psl, in_=sfull, func=AF.Exp, bias=mb_[:, nw:nw + 1], scale=MLA_SCALE, accum_out=mb_[:, 6:7]), r=rs2 + [r_m], w=[r_ps[pq], r_m])
                    S.op("act", lambda e, mb_=mb_, cur=cur, nw=nw: e.activation(out=mb_[:, 4:5], in_=mb_[:, cur:cur + 1], func=AF.Exp, bias=mb_[:, nw:nw + 1], scale=-1.0), r=[r_m], w=[r_m])
                    S.op("dve", lambda e, mb_=mb_: e.scalar_tensor_tensor(out=mb_[:, 1:2], in0=mb_[:, 1:2], scalar=mb_[:, 4:5], in1=mb_[:, 6:7], op0=ALU.mult, op1=ALU.add), r=[r_m], w=[r_m])

                def sU2(bb, stp, gs, g3):
                    mb_ = ms[bb % 2][0:32]
                    oa = Oacc[bb % 2][0:32]
                    r_m = r_ms[bb % 2]
                    r_o = r_Oacc[bb % 2]
                    pq = gs
                    psl = p_s[pq][0:32]
                    bpt, rpt = ring.get(1)
                    ptv = bankbf(bpt)[:, 0:256].rearrange("p (i t) -> p i t", i=8)
                    for i_ in range(8):
                        S.op("pe", lambda e, i_=i_, psl=psl, ptv=ptv: e.transpose(out=ptv[:, i_, :], in_=psl[:, i_ * 128:(i_ + 1) * 128], identity=ident_b[0:32, 0:32]), r=[r_ps[pq], r_idb], w=rpt)
                    S.op("act", lambda e, ptv=ptv, pq=pq: e.copy(out=pTc[pq], in_=ptv), r=rpt, w=[r_pTc[pq]])
                    bo_, ro_ = ring.get(1)
                    for i_ in range(8):
                        S.op("pe", lambda e, i_=i_, bo_=bo_, pq=pq, g3=g3: e.matmul(bank(bo_)[0:32, 0:256], lhsT=pTc[pq][:, i_, :], rhs=gk[g3][:, i_, :], start=(i_ == 0), stop=(i_ == 7)), r=[r_pTc[pq], r_gk[g3]], w=ro_)
                    S.op("dve", lambda e, bo_=bo_, oa=oa, mb_=mb_: e.scalar_tensor_tensor(out=oa, in0=oa, scalar=mb_[:, 4:5], in1=bank(bo_)[0:32, 0:256], op0=ALU.mult, op1=ALU.add), r=ro_ + [r_o, r_m], w=[r_o])


                def sF(bb):
                    mb_ = ms[bb % 2][0:32]
                    oa = Oacc[bb % 2][0:32]
                    r_m = r_ms[bb % 2]
                    r_o = r_Oacc[bb % 2]
                    S.op("dve", lambda e, mb_=mb_: e.reciprocal(out=mb_[:, 7:8], in_=mb_[:, 1:2]), r=[r_m], w=[r_m])
                    S.op("act", lambda e, oa=oa, mb_=mb_: e.activation(out=olb[0:32, :], in_=oa, func=AF.Identity, bias=0.0, scale=mb_[:, 7:8]), r=[r_o, r_m], w=[r_olb])
                    bot, rot = ring.get(1)
                    otv = bankbf(bot)[:, 0:64].rearrange("p (k n) -> p k n", k=2)
                    for k in range(2):
                        S.op("pe", lambda e, k=k, otv=otv: e.transpose(out=otv[:, k, :], in_=olb[0:32, k * 128:(k + 1) * 128], identity=ident_b[0:32, 0:32]), r=[r_olb, r_idb], w=rot)
                    S.op("dve", lambda e, otv=otv: e.tensor_copy(out=olTs, in_=otv), r=rot, w=[r_olTs])
                    boh, roh = ring.get(1)
                    ohv = bank(boh)[:, 0:32].rearrange("p (h l) -> p h l", h=8)
                    for h in range(8):
                        for k in range(2):
                            S.op("pe", lambda e, h=h, k=k, ohv=ohv: e.matmul(ohv[:, h, :], lhsT=wuv_sb[:, k, h * 128:(h + 1) * 128], rhs=olTs[:, k, h * 4:(h + 1) * 4], start=(k == 0), stop=(k == 1)), r=[r_wuv, r_olTs], w=roh)
                    S.op("act", lambda e, ohv=ohv, bb=bb: e.copy(out=oTs[:, :, bb * 4:bb * 4 + 4], in_=ohv), r=roh, w=[r_oTs])

                NTOT = SB * NSTEP
                sG(0, 0, 0)
                sG(0, 1, 1)
                sD(0, 0, 0, 0)
                sT(0, 0, 0, 0)
                for idx in range(NTOT):
                    bb, stp = divmod(idx, NSTEP)
                    if idx + 2 < NTOT:
                        b3, s3 = divmod(idx + 2, NSTEP)
                        sG(b3, s3, (idx + 2) % 3)
                    if stp == 0:
                        sI(bb)
                    if idx + 1 < NTOT:
                        sD(0, 0, (idx + 1) % 2, (idx + 1) % 3)
                    sU1(bb, stp, idx % 2)
                    if idx + 1 < NTOT:
                        b2, s2 = divmod(idx + 1, NSTEP)
                        sT(b2, s2, (idx + 1) % 2, (idx + 1) % 3)
                    sU2(bb, stp, idx % 2, idx % 3)
                    if stp == NSTEP - 1:
                        sF(bb)
                by_, ry_ = ring.get(2)
                for hf in range(2):
                    for c in range(8):
                        S.op("pe", lambda e, hf=hf, c=c: e.matmul(bank(by_ + hf)[0:NS, :], lhsT=oTs[:, c, :], rhs=wo1_sb[:, c, hf * 512:(hf + 1) * 512], start=(c == 0), stop=(c == 7)), r=[r_oTs, r_wo1], w=[ry_[hf]])
                zs1, r_zs1 = A.get(D, F32, "zs1")
                scs1, r_scs1 = A.get(16, F32, "scs1")
                S.op("dve", lambda e: e.scalar_tensor_tensor(out=zs1[0:NS, :], in0=XS[:], scalar=ALPHA, in1=bank(by_, 2)[0:NS, :], op0=ALU.mult, op1=ALU.add), r=[r_XS] + ry_, w=[r_zs1])
                layer_norm_tile(zs1[0:NS, :], r_zs1, XS[:], r_XS, NS, scs1, r_scs1, lnt1, r_lnt1)

            S.set_phase('s1')
            if with_sample:
                sample_l1()
            chk('l1')
            S.set_phase('ffn1')
            ffn(1, True)


        except _Stop:
            print('stopped at', stop)
        S.finish(out_toks)
        S.emit()
        print("ops", S.stats())
    return nc


_CACHE = {}


def _prep_shared(inp, tabs):
    f = lambda a: np.ascontiguousarray(np.asarray(a, dtype=np.float32))
    sh = {}
    sh["w_in"] = f(inp["w_in_even"][0])
    sh["pool_w"] = f(np.transpose(inp["pool_w"][0], (1, 0, 2)))
    sh["pool_scale"] = f(inp["pool_scale"][0].reshape(4, 128).T)
    sh["gn_g"] = f(inp["ret_gn_g"][0])
    sh["w_o0"] = f(inp["w_o_even"][0])
    sh["w_dq"] = f(inp["w_dq"][0])
    sh["qn_g"] = f(inp["q_norm_g"][0])
    wuq = np.asarray(inp["w_uq"][0]).reshape(512, 8, 192)
    sh["w_uq"] = f(np.concatenate([wuq[:, :, :128].reshape(512, 1024), wuq[:, :, 128:].reshape(512, 512)], axis=1))
    sh["w_dkv"] = f(inp["w_dkv"][0])
    sh["kvn_g"] = f(inp["kv_norm_g"][0])
    sh["w_ukT"] = f(np.transpose(inp["w_uk"][0], (2, 1, 0)))
    sh["w_uv"] = f(np.asarray(inp["w_uv"][0]).reshape(256, 1024))
    sh["w_o1"] = f(inp["w_o_mla"][0])
    wup = np.asarray(inp["w_up"])
    a = wup[:, :, :DFF].reshape(DEPTH, 8, 128, NJ, 128)
    b = wup[:, :, DFF:].reshape(DEPTH, 8, 128, NJ, 128)
    ab = np.concatenate([a, b], axis=4)
    sh["w_up"] = f(np.transpose(ab, (0, 3, 2, 1, 4)))
    sh["w_down"] = f(inp["w_down"])
    sh["conv_w"] = f(np.transpose(np.asarray(inp["conv_w"]).reshape(DEPTH, 3, NJ, 128), (0, 3, 2, 1)))
    sh["conv_b"] = f(np.transpose(np.asarray(inp["conv_b"]).reshape(DEPTH, NJ, 128), (0, 2, 1)))
    for k in ("ln_mix_g", "ln_mix_b", "ln_ffn_g", "ln_ffn_b"):
        sh[k] = f(inp[k])
    sh["cckv"] = np.asarray(inp["cache_ckv"], dtype=np.float32).reshape(5120, 128 * 256)
    sh["ckpe"] = np.asarray(inp["cache_kpe"], dtype=np.float32).reshape(5120, 128 * 64)
    for k, v in tabs.items():
        if isinstance(v, np.ndarray):
            sh["t_" + k] = v
    return sh


def kernel(**inp):
    tabs = host_tables()
    if "nc" not in _CACHE:
        _CACHE["nc"] = build_program(tabs)
    nc = _CACHE["nc"]
    sh = _prep_shared(inp, tabs)
    f = lambda a: np.ascontiguousarray(np.asarray(a, dtype=np.float32))
    in_maps = []
    for c in range(8):
        m = dict(sh)
        bs = slice(SB * c, SB * (c + 1))
        m["xp"] = f(inp["x_prompt"][c])
        m["xs"] = f(np.asarray(inp["x_sample"][bs]).reshape(NS, D))
        m["spool"] = f(np.asarray(inp["state_pool"][0, bs]).reshape(SB * 15, 512))
        m["sret"] = f(np.asarray(inp["state_ret"][0, bs]).reshape(SB * RH, 128, 128))
        m["sconv"] = f(np.asarray(inp["state_conv"][:, bs]).reshape(DEPTH, SB * 2, DFF))
        m["ptab"] = np.ascontiguousarray(np.asarray(inp["page_table"][bs]).T.astype(np.int32))
        in_maps.append(m)
    res = run_bass_kernel_spmd(nc, in_maps, core_ids=list(range(8)))
    R = res.results
    cat = lambda k: np.stack([np.asarray(R[c][k]) for c in range(8)])
    y_p = cat("yp")
    y_s = cat("ys").reshape(32, SL, D)
    pool_p = cat("poolp")[None]
    pool_s = cat("pools").reshape(1, 32, 15, 512)
    ret_p = cat("retp")[None]
    ret_s = cat("rets").reshape(1, 32, RH, 128, 128)
    ckv_p = cat("ckvp")[None]
    ckv_s = cat("ckvs").reshape(1, 32, SL, 256)
    kpe_p = cat("kpep")[None]
    kpe_s = cat("kpes").reshape(1, 32, SL, 64)
    conv_p = np.transpose(cat("convp"), (1, 0, 2, 3))
    conv_s = np.transpose(cat("convs").reshape(8, DEPTH, SB, 2, DFF), (1, 0, 2, 3, 4)).reshape(DEPTH, 32, 2, DFF)
    outs = (y_p, y_s, pool_p, pool_s, ret_p, ret_s, ckv_p, ckv_s, kpe_p, kpe_s, conv_p, conv_s)
    return tuple(np.ascontiguousarray(o.astype(np.float32)) for o in outs)
```

```python
import contextlib
import math
import numpy as np
import concourse.bass as bass
import concourse.mybir as mybir
from concourse.bass_utils import run_bass_kernel_spmd

F32 = mybir.dt.float32
BF16 = mybir.dt.bfloat16
I32 = mybir.dt.int32
AF = mybir.ActivationFunctionType
ALU = mybir.AluOpType
AX = mybir.AxisListType

ALL_Q = ("pe", "act", "dve", "pool", "sp")
SAME_ENGINE_SYNC = True

D = 1024
SEQ = 2048
NT = SEQ // 128
DEPTH = 2
NS = 16
SB = 4
SL = 4
PAST = 16384
NPAGES = 128
POOL_W = (2, 4, 8, 16)
RH = 4
DFF = 2816
NJ = DFF // 128
ALPHA = (2.0 * DEPTH) ** 0.25
LN_EPS = 1e-5
RMS_EPS = 1e-6
GN_EPS = 1e-6
MLA_SCALE = (128 + 64) ** -0.5
KSCALE = 128 ** -0.5
NEG = -30000.0


class Res:
    __slots__ = ("name", "w", "r")

    def __init__(self, name):
        self.name = name
        self.w = None
        self.r = []


class Op:
    __slots__ = ("q", "fn", "waits", "signal", "dma", "idx", "deps", "sdep", "succ", "nrem", "ready", "cost", "lat", "fin", "orig", "done", "phase")

    def __init__(self, q, fn, dma=None):
        self.q = q
        self.fn = fn
        self.waits = []
        self.signal = False
        self.dma = dma
        self.idx = None
        self.deps = []
        self.sdep = None
        self.succ = []
        self.nrem = 0
        self.ready = 0.0
        self.cost = 0.1
        self.lat = 0.0
        self.fin = 0.0
        self.orig = 0
        self.done = False
        self.phase = 0


class _Rec:
    def __init__(self):
        self.info = ("", None)

    def __getattr__(self, name):
        def f(*a, **k):
            out = k.get("out", a[0] if a else None)
            object.__setattr__(self, "info", (name, out))
            return self
        return f


RESCHED_PHASES = {"l0", "ffn0", "ffn1", "s1"}
KEEP_Q = ("dve",)


class Sched:
    def __init__(self, nc, stack):
        self.nc = nc
        self.stack = stack
        self.all = []
        self.ops = {q: [] for q in ALL_Q}
        self.dma_sems = {}
        self.dma_last = {}
        self.eng_sems = {}
        self.nres = 0
        self.phase = 0
        self.phase_names = ['init']

    def res(self, name=None):
        self.nres += 1
        return Res(name or f"r{self.nres}")

    def set_phase(self, name):
        self.phase_names.append(name)
        self.phase = len(self.phase_names) - 1

    def _cost(self, o):
        try:
            rec = _Rec()
            o.fn(rec)
            name, out = rec.info
            shp = list(out.shape)
            n = 1
            for d in shp[1:]:
                n *= d
        except Exception:
            name, n = "", 256
        q = o.q
        if o.dma is not None:
            o.cost = 1.2 if q == "pool" else 0.15
            o.lat = 2.5 + n * shp[0] * 3.0 / 250e3 if name else 3.0
        elif q == "pe":
            o.cost = 0.05 + n / 1900.0
        elif q == "act":
            o.cost = 0.22 + n / 1000.0
        elif q == "dve":
            o.cost = 0.2 + n / 900.0
        else:
            o.cost = 0.35 + n / 450.0

    def _deps(self, o, r, w, tok):
        seen = set()
        for res in r:
            if res.w is not None and id(res.w[-1]) not in seen:
                seen.add(id(res.w[-1])); o.deps.append(res.w)
        for res in w:
            if res.w is not None and id(res.w[-1]) not in seen:
                seen.add(id(res.w[-1])); o.deps.append(res.w)
            for t in res.r:
                if id(t[-1]) not in seen:
                    seen.add(id(t[-1])); o.deps.append(t)
        for res in r:
            res.r.append(tok)
        for res in w:
            res.w = tok
            res.r = []

    def op(self, q, fn, r=(), w=()):
        o = Op(q, fn)
        o.orig = len(self.all)
        o.phase = self.phase
        self._deps(o, r, w, ("eng", o))
        self.all.append(o)
        return o

    def dma(self, q, fn, r=(), w=(), semkey=None):
        res0 = (list(w) + list(r))[0] if (w or r) else None
        key = semkey if semkey is not None else id(res0)
        if key not in self.dma_sems:
            self.dma_sems[key] = [None, 0]
        ent = self.dma_sems[key]
        ent[1] += 16
        o = Op(q, fn, dma=(key, ent[1]))
        o.orig = len(self.all)
        o.phase = self.phase
        o.sdep = self.dma_last.get(key)
        self.dma_last[key] = o
        tok = ("dma", key, ent[1], o)
        self._deps(o, r, w, tok)
        self.all.append(o)
        return tok

    def finish(self, out_tokens):
        o = Op("sp", None)
        o.orig = len(self.all)
        o.phase = self.phase
        seen = set()
        for t in out_tokens:
            if id(t[-1]) not in seen:
                seen.add(id(t[-1])); o.deps.append(t)
        self.all.append(o)

    def _schedule(self):
        ops = self.all
        for o in ops:
            if o.fn is not None:
                self._cost(o)
            dset = set(id(t[-1]) for t in o.deps)
            preds = [(t[-1], True) for t in o.deps]
            if o.sdep is not None and id(o.sdep) not in dset:
                preds.append((o.sdep, False))
            o.nrem = len(preds)
            for p, kind in preds:
                p.succ.append((o, kind))
        tfree = {q: 0.0 for q in ALL_Q}

        def place(o, st):
            self.ops[o.q].append(o)
            tfree[o.q] = st + o.cost
            o.fin = st + o.cost + o.lat
            o.done = True
            for s, kind in o.succ:
                t = o.fin if kind else st
                if t > s.ready:
                    s.ready = t
                s.nrem -= 1

        nph = len(self.phase_names)
        byph = [[] for _ in range(nph)]
        for o in ops:
            byph[o.phase].append(o)
        for ph in range(nph):
            plist = byph[ph]
            if self.phase_names[ph] not in RESCHED_PHASES:
                for o in plist:
                    place(o, max(tfree[o.q], o.ready))
                continue
            pending = set(id(o) for o in plist)
            keepq = KEEP_Q if self.phase_names[ph] in ("l1", "s1") else ()
            nxt = {q: [o for o in plist if o.q == q] for q in keepq}
            nxt_i = {q: 0 for q in keepq}
            ready = {q: [] for q in ALL_Q}
            inready = set()
            for o in plist:
                if o.nrem == 0:
                    ready[o.q].append(o); inready.add(id(o))
            nleft = len(plist)
            while nleft:
                best = None
                for q in ALL_Q:
                    lst = ready[q]
                    if not lst:
                        continue
                    tf = tfree[q]
                    cand = None
                    if q in keepq:
                        o = nxt[q][nxt_i[q]]
                        if o in lst:
                            st = o.ready if o.ready > tf else tf
                            cand = ((st, o.orig), o)
                    else:
                        for o in lst:
                            st = o.ready if o.ready > tf else tf
                            key = (st, o.orig)
                            if cand is None or key < cand[0]:
                                cand = (key, o)
                    if cand is not None and (best is None or cand[0] < best[0]):
                        best = cand
                (st, _), o = best
                ready[o.q].remove(o)
                if o.q in keepq:
                    nxt_i[o.q] += 1
                place(o, st)
                nleft -= 1
                for s, kind in o.succ:
                    if s.nrem == 0 and id(s) in pending and id(s) not in inready and not s.done:
                        ready[s.q].append(s); inready.add(id(s))

    def _add_wait(self, op, tok, waited, dwaited):
        q = op.q
        if tok[0] == "eng":
            src = tok[1]
            sq, sidx = src.q, src.idx
            if sq == q and (q == "pe" or q == "sp" or not SAME_ENGINE_SYNC):
                return
            if waited[q].get(sq, -1) >= sidx:
                return
            waited[q][sq] = sidx
            op.waits.append(("eng", sq, sidx))
            src.signal = True
        else:
            _, key, val, _src = tok
            if dwaited[q].get(key, -1) >= val:
                return
            dwaited[q][key] = val
            op.waits.append(("dma", key, val))

    def emit(self):
        nc = self.nc
        st = self.stack
        self._schedule()
        waited = {q: {} for q in ALL_Q}
        dwaited = {q: {} for q in ALL_Q}
        for q in ALL_Q:
            for i, o in enumerate(self.ops[q]):
                o.idx = i
        for q in ALL_Q:
            for o in self.ops[q]:
                best = {}
                for t in o.deps:
                    if t[0] == "eng":
                        k = ("eng", t[1].q)
                        v = t[1].idx
                    else:
                        k = ("dma", t[1])
                        v = t[2]
                    if k not in best or best[k][0] < v:
                        best[k] = (v, t)
                for v, t in best.values():
                    self._add_wait(o, t, waited, dwaited)
        for q in ALL_Q:
            self.eng_sems[q] = st.enter_context(nc.semaphore(f"es_{q}"))
        for i, key in enumerate(self.dma_sems):
            self.dma_sems[key][0] = st.enter_context(nc.semaphore(f"ds_{i}"))
        cnt = {}
        for q in ALL_Q:
            c = 0
            arr = []
            for o in self.ops[q]:
                if o.signal:
                    c += 1
                arr.append(c)
            cnt[q] = arr
        block = st.enter_context(nc.Block())

        def run(q, eng):
            for o in self.ops[q]:
                for t in o.waits:
                    if t[0] == "eng":
                        eng.wait_ge(self.eng_sems[t[1]], cnt[t[1]][t[2]])
                    else:
                        eng.wait_ge(self.dma_sems[t[1]][0], t[2])
                if o.fn is None:
                    continue
                ins = o.fn(eng)
                if o.dma is not None:
                    ins.then_inc(self.dma_sems[o.dma[0]][0], 16)
                elif o.signal:
                    ins.then_inc(self.eng_sems[q], 1)

        @block.tensor
        def _(eng):
            run("pe", eng)

        @block.scalar
        def _(eng):
            run("act", eng)

        @block.vector
        def _(eng):
            run("dve", eng)

        @block.gpsimd
        def _(eng):
            run("pool", eng)

        @block.sync
        def _(eng):
            run("sp", eng)

    def stats(self):
        return {q: (len(self.ops[q]), sum(len(o.waits) for o in self.ops[q])) for q in ALL_Q}


def _rope_tables(half, pos):
    inv = (np.float32(10000.0) ** (-(np.arange(half, dtype=np.float32)) / np.float32(half))).astype(np.float32)
    ang = (pos.astype(np.float32)[:, None] * inv[None, :]).astype(np.float32)
    return np.cos(ang).astype(np.float32), np.sin(ang).astype(np.float32)


def host_tables():
    t = {}
    pos_p = np.arange(SEQ, dtype=np.float32)
    pos_s = (PAST + np.arange(SL)).astype(np.float32)
    c, s = _rope_tables(64, pos_p)
    t["c2p"] = np.ascontiguousarray(np.concatenate([c, c], 1).T)
    t["ssp"] = np.ascontiguousarray(np.concatenate([s, -s], 1).T)
    c, s = _rope_tables(64, pos_s)
    t["c2s"] = np.ascontiguousarray(np.tile(np.concatenate([c, c], 1).T, (1, SB)))
    t["sss"] = np.ascontiguousarray(np.tile(np.concatenate([s, -s], 1).T, (1, SB)))
    c, s = _rope_tables(32, pos_p)
    cc = np.concatenate([c, c], 1).reshape(NT, 128, 64).transpose(1, 0, 2)
    ss = np.concatenate([-s, s], 1).reshape(NT, 128, 64).transpose(1, 0, 2)
    t["cc1"] = np.ascontiguousarray(cc)
    t["ss1"] = np.ascontiguousarray(ss)
    c, s = _rope_tables(32, pos_s)
    t["cc1s"] = np.ascontiguousarray(np.tile(np.concatenate([c, c], 1), (SB, 1)))
    t["ss1s"] = np.ascontiguousarray(np.tile(np.concatenate([-s, s], 1), (SB, 1)))
    lg = np.log(np.float32(1.0) - np.float32(2.0) ** (np.float32(-5.0) - np.arange(RH, dtype=np.float32))).astype(np.float32)
    for name, C in (("p", 128), ("s", SL)):
        idx = np.arange(C, dtype=np.float32)
        diff = idx[:, None] - idx[None, :]
        dec = np.where(diff >= 0, np.exp(np.maximum(diff, 0.0)[None] * lg[:, None, None]), 0.0).astype(np.float32)
        t["decT" + name] = np.ascontiguousarray((dec * np.float32(KSCALE)).transpose(2, 0, 1))
        qd = np.exp((idx + 1.0)[:, None] * lg[None, :]).astype(np.float32)
        t["qdec" + name] = np.ascontiguousarray(np.broadcast_to(qd.T[None], (128, RH, C)))
        kd = np.exp((C - 1.0 - idx)[:, None] * lg[None, :]).astype(np.float32) * np.float32(KSCALE)
        t["kdec" + name] = np.ascontiguousarray(kd)
        t["gC" + name] = [float(np.exp(np.float32(C) * lg[h])) for h in range(RH)]
    ic = np.zeros((128, 4, 15), np.float32)
    for g, w in enumerate(POOL_W):
        ic[:, g, :] = 1.0 / np.minimum(float(w), np.arange(15) + 1.0)
    t["invcnt"] = ic
    qi = np.arange(128)
    t["maskd"] = np.where(qi[None, :] <= qi[:, None], 0.0, NEG).astype(np.float32)
    t["ident"] = np.eye(128, dtype=np.float32)
    r = np.arange(32) % 4
    t["masks"] = np.where(np.arange(4)[None, :] <= r[:, None], 0.0, NEG).astype(np.float32)
    mb = np.full((32, SB, NS), NEG, np.float32)
    for bb in range(SB):
        for l2 in range(SL):
            mb[:, bb, bb * SL + l2] = np.where(l2 <= r, 0.0, NEG)
    t["masksb"] = np.ascontiguousarray(mb.reshape(32, SB * NS))
    return t


ARENA_BYTES = 142144


class PsumRing:
    def __init__(self, S):
        self.res = [S.res(f"bank{i}") for i in range(8)]
        self.nxt = 0

    def get(self, n=1):
        if n == 2 and self.nxt % 2:
            self.nxt = (self.nxt + 1) % 8
        if n == 4 and self.nxt % 4:
            self.nxt = (self.nxt + 4 - self.nxt % 4) % 8
        b = self.nxt
        self.nxt = (self.nxt + n) % 8
        return b, [self.res[b + i] for i in range(n)]


class Arena:
    def __init__(self, S, buf):
        self.S = S
        self.buf = buf
        self.hist = []
        self.cur = 0

    def at(self, off):
        assert off % 4 == 0
        self.cur = off

    def get(self, nelem, dt=BF16, name=None, nres=None):
        size = 2 if dt == BF16 else 4
        nb = (nelem * size + 3) // 4 * 4
        s, e = self.cur, self.cur + nb
        assert e <= ARENA_BYTES, (name, e)
        self.cur = e
        pend = []
        for (s2, e2, r2) in self.hist:
            if s2 < e and s < e2:
                if r2.w is not None:
                    pend.append(r2.w)
                pend.extend(r2.r)
        seen = set()
        p2 = []
        for t in pend:
            if id(t[-1]) not in seen:
                seen.add(id(t[-1])); p2.append(t)
        pend = p2
        rs = []
        for i in range(nres or 1):
            r = self.S.res(f"{name}_{i}")
            r.r = list(pend)
            self.hist.append((s, e, r))
            rs.append(r)
        v = self.buf[:, s // 2:s // 2 + nb // 2]
        if dt != BF16:
            v = v.bitcast(dt)
        return v[:, 0:nelem], (rs if nres else rs[0])


class _Stop(Exception):
    pass


def build_program(tabs, with_sample=True, stop=None):
    nc = bass.Bass("TRN2", target_bir_lowering=False)

    def din(name, shape, dt=F32):
        return nc.dram_tensor(name, list(shape), dt, kind="ExternalInput").ap()

    def dout(name, shape):
        return nc.dram_tensor(name, list(shape), F32, kind="ExternalOutput").ap()

    xp = din("xp", [SEQ, D])
    xs = din("xs", [NS, D])
    spool = din("spool", [SB * 15, 512])
    sret = din("sret", [SB * RH, 128, 128])
    sconv = din("sconv", [DEPTH, SB * 2, DFF])
    ptab = din("ptab", [128, SB], I32)
    if with_sample:
        cckv = din("cckv", [5120, 128 * 256])
        ckpe = din("ckpe", [5120, 128 * 64])
    w_in = din("w_in", [D, 2560])
    pool_w = din("pool_w", [128, 4, 128])
    pool_scale = din("pool_scale", [128, 4])
    gn_g = din("gn_g", [512])
    w_o0 = din("w_o0", [D, D])
    w_dq = din("w_dq", [D, 512])
    qn_g = din("qn_g", [512])
    w_uq = din("w_uq", [512, 1536])
    w_dkv = din("w_dkv", [D, 320])
    kvn_g = din("kvn_g", [256])
    w_ukT = din("w_ukT", [128, 8, 256])
    w_uv = din("w_uv", [256, 1024])
    w_o1 = din("w_o1", [D, D])
    w_up = din("w_up", [DEPTH, NJ, 128, 8, 256])
    w_down = din("w_down", [DEPTH, DFF, D])
    conv_w = din("conv_w", [DEPTH, 128, NJ, 3])
    conv_b = din("conv_b", [DEPTH, 128, NJ])
    ln_mix_g = din("ln_mix_g", [DEPTH, D])
    ln_mix_b = din("ln_mix_b", [DEPTH, D])
    ln_ffn_g = din("ln_ffn_g", [DEPTH, D])
    ln_ffn_b = din("ln_ffn_b", [DEPTH, D])
    tdr = {k: din("t_" + k, v.shape) for k, v in tabs.items() if isinstance(v, np.ndarray)}

    yp = dout("yp", [SEQ, D])
    ys = dout("ys", [NS, D])
    poolp = dout("poolp", [15, 512])
    pools = dout("pools", [SB * 15, 512])
    retp = dout("retp", [RH, 128, 128])
    rets = dout("rets", [SB * RH, 128, 128])
    ckvp = dout("ckvp", [SEQ, 256])
    ckvs = dout("ckvs", [NS, 256])
    kpep = dout("kpep", [SEQ, 64])
    kpes = dout("kpes", [NS, 64])
    convp = dout("convp", [DEPTH, 2, DFF])
    convs = dout("convs", [DEPTH, SB * 2, DFF])

    out_toks = []
    st = contextlib.ExitStack()
    with st:
        S = Sched(nc, st)
        _n = [0]

        def sb(shape, dt, name):
            return st.enter_context(nc.sbuf_tensor(name, list(shape), dt))

        P = st.enter_context(nc.psum_tensor("P", [128, 8, 512], F32))
        ring = PsumRing(S)

        def bank(b, n=1):
            return P[:, b:b + n, :].rearrange("p a b -> p (a b)")

        def bankbf(b, n=1):
            return P[:, b:b + n, :].rearrange("p a b -> p (a b)").bitcast(BF16)

        X = sb([128, NT, D], F32, "X")
        rX = [S.res(f"X{t}") for t in range(NT)]
        BIG = sb([128, ARENA_BYTES // 2], BF16, "BIG")
        A = Arena(S, BIG)
        ident_f = sb([128, 128], F32, "ident_f"); r_idf = S.res()
        ident_b = sb([128, 128], BF16, "ident_b"); r_idb = S.res()
        maskd = sb([128, 128], BF16, "maskd"); r_maskd = S.res()
        eps_t = sb([128, 4], F32, "eps_t"); r_eps = S.res()
        XS = sb([NS, D], F32, "XS"); r_XS = S.res()
        S.dma("sp", lambda e: e.dma_start(out=ident_f[:], in_=tdr["ident"]), w=[r_idf])
        S.dma("pool", lambda e: e.dma_start(out=ident_b[:], in_=tdr["ident"]), w=[r_idb])
        S.dma("pool", lambda e: e.dma_start(out=maskd[:], in_=tdr["maskd"]), w=[r_maskd])
        S.op("pool", lambda e: e.memset(eps_t[:, 0:1], LN_EPS), w=[r_eps])
        S.op("pool", lambda e: e.memset(eps_t[:, 1:2], RMS_EPS), w=[r_eps])
        S.op("pool", lambda e: e.memset(eps_t[:, 2:3], GN_EPS), w=[r_eps])
        eps_ln, eps_rms, eps_gn = eps_t[:, 0:1], eps_t[:, 1:2], eps_t[:, 2:3]

        def evac(q, out_ap, in_ap, r, w):
            if q == "act":
                S.op("act", lambda e: e.copy(out=out_ap, in_=in_ap), r=r, w=w)
            else:
                S.op(q, lambda e: e.tensor_copy(out=out_ap, in_=in_ap), r=r, w=w)

        def load_ln(lnt, r_lnt, g_ap, b_ap):
            S.dma("sp", lambda e: e.dma_start(out=lnt[:, 0:D], in_=g_ap.partition_broadcast(128)), w=[r_lnt])
            S.dma("sp", lambda e: e.dma_start(out=lnt[:, D:2 * D], in_=b_ap.partition_broadcast(128)), w=[r_lnt])

        def layer_norm_tile(z_ap, z_res, out_ap, out_res, npart, scr, scr_res, lnt, r_lnt):
            for c in range(2):
                S.op("dve", lambda e, c=c: e.bn_stats(out=scr[0:npart, c * 6:(c + 1) * 6], in_=z_ap[:, c * 512:(c + 1) * 512]), r=[z_res], w=[scr_res])
            mv = scr[0:npart, 12:14]
            S.op("dve", lambda e: e.bn_aggr(out=mv, in_=scr[0:npart, 0:12]), r=[scr_res], w=[scr_res])
            sd = scr[0:npart, 14:15]
            S.op("act", lambda e: e.activation(out=sd, in_=scr[0:npart, 13:14], func=AF.Sqrt, bias=eps_ln[0:npart], scale=1.0), r=[scr_res, r_eps], w=[scr_res])
            S.op("dve", lambda e: e.reciprocal(out=sd, in_=sd), r=[scr_res], w=[scr_res])
            nmr = scr[0:npart, 15:16]
            S.op("dve", lambda e: e.tensor_scalar(out=nmr, in0=scr[0:npart, 12:13], scalar1=sd, scalar2=-1.0, op0=ALU.mult, op1=ALU.mult), r=[scr_res], w=[scr_res])
            S.op("act", lambda e: e.activation(out=z_ap, in_=z_ap, func=AF.Identity, bias=nmr, scale=sd), r=[scr_res, z_res], w=[z_res])
            S.op("pool", lambda e: e.tensor_tensor(out=z_ap, in0=z_ap, in1=lnt[0:npart, 0:D], op=ALU.mult), r=[z_res, r_lnt], w=[z_res])
            S.op("pool", lambda e: e.tensor_tensor(out=out_ap, in0=z_ap, in1=lnt[0:npart, D:2 * D], op=ALU.add), r=[z_res, r_lnt], w=[out_res])

        def transpose_x(src_ap, src_res, dst_ap, dst_res, npart=128):
            for half in range(2):
                b, br = ring.get(1)
                pv = bank(b)[:, 0:4 * npart].rearrange("p (a b) -> p a b", a=4)
                for c in range(4):
                    cc = half * 4 + c
                    S.op("pe", lambda e, c=c, cc=cc, pv=pv: e.transpose(out=pv[:, c, :], in_=src_ap[:, cc * 128:(cc + 1) * 128], identity=ident_f[0:npart, 0:npart]),
                         r=[src_res, r_idf], w=br)
                evac("act" if half == 0 else "dve", dst_ap[:, half * 4:half * 4 + 4, :], pv, br, [dst_res])

        def chk(tag):
            if stop == tag:
                raise _Stop()

        try:
            for t in range(NT):
                S.dma("sp", lambda e, t=t: e.dma_start(out=X[:, t, :], in_=xp[t * 128:(t + 1) * 128, :]), w=[rX[t]])
            S.dma("sp", lambda e: e.dma_start(out=XS[:], in_=xs), w=[r_XS])

            S.set_phase('l0')
            A.at(0)
            w_in_sb, r_win = A.get(8 * 2560, BF16, "w_in"); w_in_sb = w_in_sb.rearrange("p (k n) -> p k n", k=8)
            w_o_sb, r_wo = A.get(8 * 1024, BF16, "w_o0"); w_o_sb = w_o_sb.rearrange("p (k n) -> p k n", k=8)
            poolw_sb, r_poolw = A.get(512, BF16, "poolw"); poolw_sb = poolw_sb.rearrange("p (g d) -> p g d", g=4)
            c0buf, r_c0 = A.get(1024 + 1024 + 4 + 60 + 4 + 512, F32, "c0")
            decT = c0buf[:, 0:512].rearrange("p (h l) -> p h l", h=4)
            qdec = c0buf[:, 512:1024].rearrange("p (h l) -> p h l", h=4)
            kdec = c0buf[:, 1024:1028]
            invcnt = c0buf[:, 1028:1088].rearrange("p (g t) -> p g t", g=4)
            pscale = c0buf[:, 1088:1092]
            gng = c0buf[:, 1092:1604]
            lnt0, r_lnt0 = A.get(2 * D, F32, "lnt0")
            w_in_v = w_in.rearrange("(k p) n -> p k n", p=128)
            for hh in range(2):
                S.dma("pool", lambda e, hh=hh: e.dma_start(out=w_in_sb[:, :, hh * 1280:(hh + 1) * 1280], in_=w_in_v[:, :, hh * 1280:(hh + 1) * 1280]), w=[r_win])
            S.dma("pool", lambda e: e.dma_start(out=poolw_sb, in_=pool_w), w=[r_poolw])
            S.dma("pool", lambda e: e.dma_start(out=w_o_sb, in_=w_o0.rearrange("(k p) n -> p k n", p=128)), w=[r_wo])
            S.dma("sp", lambda e: e.dma_start(out=decT, in_=tdr["decTp"]), w=[r_c0])
            S.dma("sp", lambda e: e.dma_start(out=qdec, in_=tdr["qdecp"]), w=[r_c0])
            S.dma("sp", lambda e: e.dma_start(out=kdec, in_=tdr["kdecp"]), w=[r_c0])
            S.dma("sp", lambda e: e.dma_start(out=invcnt, in_=tdr["invcnt"]), w=[r_c0])
            S.dma("sp", lambda e: e.dma_start(out=pscale, in_=pool_scale), w=[r_c0])
            S.dma("sp", lambda e: e.dma_start(out=gng, in_=gn_g.partition_broadcast(128)), w=[r_c0])
            load_ln(lnt0, r_lnt0, ln_mix_g[0], ln_mix_b[0])
            L0_BUF = A.cur

            def dbl(nelem, dt, name, shape=None, **kw):
                out = []
                for i in range(2):
                    v, r = A.get(nelem, dt, f"{name}{i}")
                    if shape:
                        v = v.rearrange(shape, **kw)
                    out.append((v, r))
                return [x[0] for x in out], [x[1] for x in out]

            xT, r_xT = dbl(1024, BF16, "xT", "p (k t) -> p k t", k=8)
            ropet, r_ropet = dbl(256, F32, "ropet")
            uext, r_uext = dbl(4 * 143 + 1, F32, "uext")
            uext = [u[:, 0:572].rearrange("p (g t) -> p g t", g=4) for u in uext]
            tA, r_tA = A.get(573, F32, "tA"); tA = tA[:, 0:572].rearrange("p (g t) -> p g t", g=4)
            tB, r_tB = A.get(573, F32, "tB"); tB = tB[:, 0:572].rearrange("p (g t) -> p g t", g=4)
            pooledT, r_pooled = dbl(512, BF16, "pooledT", "p (g t) -> p g t", g=4)
            qk_f, r_qkf = A.get(1024, F32, "qk_f"); qk_f = qk_f.rearrange("p (h t) -> p h t", h=8)
            ropeA, r_ropeA = A.get(1024, F32, "ropeA"); ropeA = ropeA.rearrange("p (h t) -> p h t", h=8)
            ropeB, r_ropeB = A.get(1024, F32, "ropeB"); ropeB = ropeB.rearrange("p (h t) -> p h t", h=8)
            qkT, r_qkT = dbl(1024, BF16, "qkT", "p (h t) -> p h t", h=8)
            qdT, r_qdT = dbl(512, BF16, "qdT", "p (h t) -> p h t", h=4)
            v_sb, r_v = dbl(512, BF16, "v")
            sgg, r_sgg = dbl(512, F32, "sgg")
            sT_bf, r_sT = dbl(512, BF16, "sT", "p (h t) -> p h t", h=4)
            kp_bf, r_kp = dbl(512, BF16, "kp", "p (h t) -> p h t", h=4)
            S_f, r_Sf = A.get(512, F32, "S_f"); S_f = S_f.rearrange("p (h t) -> p h t", h=4)
            S_b, r_Sb = A.get(512, BF16, "S_b"); S_b = S_b.rearrange("p (h t) -> p h t", h=4)
            on1, r_on1 = A.get(512, F32, "on1")
            ret_bf, r_ret = dbl(512, BF16, "ret")
            catT, r_cat = dbl(1024, BF16, "catT", "p (c t) -> p c t", c=8)
            z0_, r_z0_ = A.get(D, F32, "z"); z = [z0_, z0_]; r_z = [r_z0_, r_z0_]
            scr, r_scr = dbl(16, F32, "scr")
            gst, r_gst = A.get(32, F32, "gst")
            poolo, r_poolo = A.get(512, F32, "poolo")
            L0_END = A.cur
            print("layer0 arena bytes", L0_END)

            S.op("pool", lambda e: e.memset(uext[1][:, :, 128:143], 0.0), w=[r_uext[1]])

            def l0A(t):
                pb = t % 2
                tok = slice(t * 128, (t + 1) * 128)
                S.dma("sp", lambda e, pb=pb, tok=tok: e.dma_start(out=ropet[pb][:, 0:128], in_=tdr["c2p"][:, tok]), w=[r_ropet[pb]])
                S.dma("sp", lambda e, pb=pb, tok=tok: e.dma_start(out=ropet[pb][:, 128:256], in_=tdr["ssp"][:, tok]), w=[r_ropet[pb]])
                transpose_x(X[:, t, :], rX[t], xT[pb], r_xT[pb])
                bu, ru = ring.get(1)
                bq, rq = ring.get(1)
                bk, rk = ring.get(1)
                bv, rv = ring.get(1)
                bg, rg = ring.get(1)
                for (bb, rr, c0) in ((bu, ru, 0), (bq, rq, 512), (bk, rk, 1024)):
                    pv = bank(bb).rearrange("p (h t) -> p h t", h=4)
                    for h in range(4):
                        for k in range(8):
                            S.op("pe", lambda e, pv=pv, h=h, k=k, c0=c0, pb=pb: e.matmul(pv[:, h, :], lhsT=w_in_sb[:, k, c0 + h * 128:c0 + (h + 1) * 128], rhs=xT[pb][:, k, :], start=(k == 0), stop=(k == 7)),
                                 r=[r_win, r_xT[pb]], w=rr)
                for (bb, rr, c0) in ((bv, rv, 1536), (bg, rg, 2048)):
                    for k in range(8):
                        S.op("pe", lambda e, bb=bb, k=k, c0=c0, pb=pb: e.matmul(bank(bb), lhsT=xT[pb][:, k, :], rhs=w_in_sb[:, k, c0:c0 + 512], start=(k == 0), stop=(k == 7)),
                             r=[r_win, r_xT[pb]], w=rr)
                ue = uext[pb]
                S.op("act", lambda e, ue=ue, bu=bu: e.copy(out=ue[:, :, 15:143], in_=bank(bu).rearrange("p (h t) -> p h t", h=4)), r=ru, w=[r_uext[pb]])
                S.op("pool", lambda e, ue=ue, pb=pb: e.tensor_copy(out=ue[:, :, 0:15], in_=uext[1 - pb][:, :, 128:143]), r=[r_uext[1 - pb]], w=[r_uext[pb]])
                S.op("pool", lambda e, ue=ue: e.tensor_tensor(out=tA[:, :, 1:143], in0=ue[:, :, 1:143], in1=ue[:, :, 0:142], op=ALU.add), r=[r_uext[pb]], w=[r_tA])
                S.op("pool", lambda e: e.tensor_tensor(out=tB[:, 1:4, 3:143], in0=tA[:, 1:4, 3:143], in1=tA[:, 1:4, 1:141], op=ALU.add), r=[r_tA], w=[r_tB])
                S.op("pool", lambda e: e.tensor_tensor(out=tA[:, 2:4, 7:143], in0=tB[:, 2:4, 7:143], in1=tB[:, 2:4, 3:139], op=ALU.add), r=[r_tB], w=[r_tA])
                S.op("pool", lambda e: e.tensor_tensor(out=tB[:, 3:4, 15:143], in0=tA[:, 3:4, 15:143], in1=tA[:, 3:4, 7:135], op=ALU.add), r=[r_tA], w=[r_tB])
                fin = (tA, tB, tA, tB)
                for g in range(4):
                    S.op("dve", lambda e, g=g, ue=ue, pb=pb: e.scalar_tensor_tensor(out=pooledT[pb][:, g, :], in0=fin[g][:, g, 15:143], scalar=1.0 / POOL_W[g], in1=ue[:, g, 15:143], op0=ALU.mult, op1=ALU.subtract),
                         r=[r_tA, r_tB, r_uext[pb]], w=[r_pooled[pb]])
                if t == 0:
                    for g in range(4):
                        S.op("dve", lambda e, g=g: e.tensor_tensor(out=ropeA[:, g, 0:15], in0=fin[g][:, g, 15:30], in1=invcnt[:, g, :], op=ALU.mult), r=[r_tA, r_tB, r_c0], w=[r_ropeA])
                        S.op("dve", lambda e, g=g, ue=ue, pb=pb: e.tensor_tensor(out=pooledT[pb][:, g, 0:15], in0=ropeA[:, g, 0:15], in1=ue[:, g, 15:30], op=ALU.subtract), r=[r_ropeA, r_uext[pb]], w=[r_pooled[pb]])
                bpm, rpm = ring.get(1)
                pmv = bank(bpm).rearrange("p (h t) -> p h t", h=4)
                for g in range(4):
                    S.op("pe", lambda e, g=g, pmv=pmv, pb=pb: e.matmul(pmv[:, g, :], lhsT=poolw_sb[:, g, :], rhs=pooledT[pb][:, g, :], start=True, stop=True), r=[r_poolw, r_pooled[pb]], w=rpm)
                for g in range(4):
                    S.op("act", lambda e, g=g, pmv=pmv, pb=pb: e.activation(out=catT[pb][:, g, :], in_=pmv[:, g, :], func=AF.Identity, bias=0.0, scale=pscale[:, g:g + 1]), r=rpm + [r_c0], w=[r_cat[pb]])
                if t == NT - 1:
                    bpo, rpo = ring.get(1)
                    for g in range(4):
                        S.op("pe", lambda e, g=g, ue=ue, bpo=bpo: e.transpose(out=bank(bpo)[0:15, g * 128:(g + 1) * 128], in_=ue[:, g, 128:143], identity=ident_f[:]), r=[r_uext[pb], r_idf], w=rpo)
                    S.op("act", lambda e, bpo=bpo: e.copy(out=poolo[0:15, :], in_=bank(bpo)[0:15, :]), r=rpo, w=[r_poolo])
                    out_toks.append(S.dma("sp", lambda e: e.dma_start(out=poolp, in_=poolo[0:15, :]), r=[r_poolo]))
                S.op("act", lambda e, bq=bq: e.copy(out=qk_f[:, 0:4, :], in_=bank(bq).rearrange("p (h t) -> p h t", h=4)), r=rq, w=[r_qkf])
                S.op("act", lambda e, bk=bk: e.copy(out=qk_f[:, 4:8, :], in_=bank(bk).rearrange("p (h t) -> p h t", h=4)), r=rk, w=[r_qkf])
                c2b = ropet[pb][:, 0:128].unsqueeze(1)
                ssb = ropet[pb][:, 128:256].unsqueeze(1)
                S.op("dve", lambda e, c2b=c2b: e.tensor_tensor(out=ropeA, in0=qk_f, in1=c2b.to_broadcast([128, 8, 128]), op=ALU.mult), r=[r_qkf, r_ropet[pb]], w=[r_ropeA])
                S.op("pool", lambda e, ssb=ssb: e.tensor_tensor(out=ropeB[0:64], in0=qk_f[64:128], in1=ssb[64:128].to_broadcast([64, 8, 128]), op=ALU.mult), r=[r_qkf, r_ropet[pb]], w=[r_ropeB])
                S.op("pool", lambda e, ssb=ssb: e.tensor_tensor(out=ropeB[64:128], in0=qk_f[0:64], in1=ssb[0:64].to_broadcast([64, 8, 128]), op=ALU.mult), r=[r_qkf, r_ropet[pb]], w=[r_ropeB])
                S.op("dve", lambda e, pb=pb: e.tensor_tensor(out=qkT[pb], in0=ropeA, in1=ropeB, op=ALU.add), r=[r_ropeA, r_ropeB], w=[r_qkT[pb]])
                if t > 0:
                    S.op("pool", lambda e, pb=pb: e.tensor_tensor(out=qdT[pb], in0=qkT[pb][:, 0:4, :], in1=qdec, op=ALU.mult), r=[r_qkT[pb], r_c0], w=[r_qdT[pb]])
                S.op("act", lambda e, pb=pb, bv=bv: e.copy(out=v_sb[pb], in_=bank(bv)), r=rv, w=[r_v[pb]])
                S.op("act", lambda e, pb=pb, bg=bg: e.activation(out=sgg[pb], in_=bank(bg), func=AF.Silu), r=rg, w=[r_sgg[pb]])
                S.op("pool", lambda e, pb=pb: e.tensor_tensor(out=sgg[pb], in0=sgg[pb], in1=gng, op=ALU.mult), r=[r_sgg[pb], r_c0], w=[r_sgg[pb]])
            def l0B(t):
                pb = t % 2
                bs, rs = ring.get(1)
                sv = bank(bs).rearrange("p (h t) -> p h t", h=4)
                for h in range(4):
                    S.op("pe", lambda e, h=h, sv=sv, pb=pb: e.matmul(sv[:, h, :], lhsT=qkT[pb][:, 4 + h, :], rhs=qkT[pb][:, h, :], start=True, stop=True), r=[r_qkT[pb]], w=rs)
                S.op("dve", lambda e, sv=sv, pb=pb: e.tensor_tensor(out=sT_bf[pb], in0=sv, in1=decT, op=ALU.mult), r=rs + [r_c0], w=[r_sT[pb]])
                bo, ro = ring.get(1)
                ov = bank(bo).rearrange("p (h t) -> p h t", h=4)
                for h in range(4):
                    S.op("pe", lambda e, h=h, ov=ov, pb=pb, t=t: e.matmul(ov[:, h, :], lhsT=sT_bf[pb][:, h, :], rhs=v_sb[pb][:, h * 128:(h + 1) * 128], start=True, stop=(t == 0)), r=[r_sT[pb], r_v[pb]], w=ro)
                    if t > 0:
                        S.op("pe", lambda e, h=h, ov=ov, pb=pb: e.matmul(ov[:, h, :], lhsT=qdT[pb][:, h, :], rhs=S_b[:, h, :], start=False, stop=True), r=[r_qdT[pb], r_Sb], w=ro)
                bkt, rkt = ring.get(1)
                ktv = bankbf(bkt)[:, 0:512].rearrange("p (h t) -> p h t", h=4)
                for h in range(4):
                    S.op("pe", lambda e, h=h, ktv=ktv, pb=pb: e.transpose(out=ktv[:, h, :], in_=qkT[pb][:, 4 + h, :], identity=ident_b[:]), r=[r_qkT[pb], r_idb], w=rkt)
                for h in range(4):
                    S.op("act", lambda e, h=h, ktv=ktv, pb=pb: e.activation(out=kp_bf[pb][:, h, :], in_=ktv[:, h, :], func=AF.Identity, bias=0.0, scale=kdec[:, h:h + 1]), r=rkt + [r_c0], w=[r_kp[pb]])
                bds, rds = ring.get(1)
                dsv = bank(bds).rearrange("p (h t) -> p h t", h=4)
                for h in range(4):
                    S.op("pe", lambda e, h=h, dsv=dsv, pb=pb: e.matmul(dsv[:, h, :], lhsT=kp_bf[pb][:, h, :], rhs=v_sb[pb][:, h * 128:(h + 1) * 128], start=True, stop=True), r=[r_kp[pb], r_v[pb]], w=rds)
                if t == 0:
                    S.op("dve", lambda e, dsv=dsv: e.tensor_copy(out=S_f, in_=dsv), r=rds, w=[r_Sf])
                else:
                    for h in range(4):
                        S.op("dve", lambda e, h=h, dsv=dsv: e.scalar_tensor_tensor(out=S_f[:, h, :], in0=S_f[:, h, :], scalar=tabs["gCp"][h], in1=dsv[:, h, :], op0=ALU.mult, op1=ALU.add), r=rds + [r_Sf], w=[r_Sf])
                if t < NT - 1:
                    S.op("pool", lambda e: e.tensor_copy(out=S_b, in_=S_f), r=[r_Sf], w=[r_Sb])
                else:
                    out_toks.append(S.dma("sp", lambda e: e.dma_start(out=retp.rearrange("h k v -> k h v"), in_=S_f), r=[r_Sf]))
                for h in range(4):
                    S.op("dve", lambda e, h=h, ov=ov: e.bn_stats(out=gst[:, h * 6:(h + 1) * 6], in_=ov[:, h, :]), r=ro, w=[r_gst])
                    S.op("dve", lambda e, h=h: e.bn_aggr(out=gst[:, 24 + 2 * h:26 + 2 * h], in_=gst[:, h * 6:(h + 1) * 6]), r=[r_gst], w=[r_gst])
                mvv = gst[:, 24:32].rearrange("p (h two) -> p h two", two=2)
                S.op("act", lambda e, pb=pb: e.activation(out=scr[pb][:, 0:4], in_=mvv[:, :, 1], func=AF.Sqrt, bias=eps_gn, scale=1.0), r=[r_gst, r_eps], w=[r_scr[pb]])
                S.op("dve", lambda e, pb=pb: e.reciprocal(out=scr[pb][:, 0:4], in_=scr[pb][:, 0:4]), r=[r_scr[pb]], w=[r_scr[pb]])
                S.op("dve", lambda e, pb=pb: e.scalar_tensor_tensor(out=scr[pb][:, 4:8], in0=mvv[:, :, 0], scalar=-1.0, in1=scr[pb][:, 0:4], op0=ALU.mult, op1=ALU.mult), r=[r_gst, r_scr[pb]], w=[r_scr[pb]])
                for h in range(4):
                    S.op("act", lambda e, h=h, ov=ov, pb=pb: e.activation(out=on1[:, h * 128:(h + 1) * 128], in_=ov[:, h, :], func=AF.Identity, bias=scr[pb][:, 4 + h:5 + h], scale=scr[pb][:, h:h + 1]), r=ro + [r_scr[pb]], w=[r_on1])
                S.op("pool", lambda e, pb=pb: e.tensor_tensor(out=ret_bf[pb], in0=on1, in1=sgg[pb], op=ALU.mult), r=[r_on1, r_sgg[pb]], w=[r_ret[pb]])
                brt, rrt = ring.get(1)
                rtv = bankbf(brt)[:, 0:512].rearrange("p (h t) -> p h t", h=4)
                for h in range(4):
                    S.op("pe", lambda e, h=h, rtv=rtv, pb=pb: e.transpose(out=rtv[:, h, :], in_=ret_bf[pb][:, h * 128:(h + 1) * 128], identity=ident_b[:]), r=[r_ret[pb], r_idb], w=rrt)
                S.op("dve", lambda e, rtv=rtv, pb=pb: e.tensor_copy(out=catT[pb][:, 4:8, :], in_=rtv), r=rrt, w=[r_cat[pb]])
                by, ry = ring.get(2)
                for hf in range(2):
                    for c in range(8):
                        S.op("pe", lambda e, hf=hf, c=c, by=by, pb=pb: e.matmul(bank(by + hf), lhsT=catT[pb][:, c, :], rhs=w_o_sb[:, c, hf * 512:(hf + 1) * 512], start=(c == 0), stop=(c == 7)), r=[r_cat[pb], r_wo], w=[ry[hf]])
                S.op("dve", lambda e, by=by, pb=pb, t=t: e.scalar_tensor_tensor(out=z[pb], in0=X[:, t, :], scalar=ALPHA, in1=bank(by, 2), op0=ALU.mult, op1=ALU.add), r=[rX[t]] + ry, w=[r_z[pb]])
                layer_norm_tile(z[pb], r_z[pb], X[:, t, :], rX[t], 128, scr[pb], r_scr[pb], lnt0, r_lnt0)

            l0A(0)
            for t in range(NT):
                if t + 1 < NT:
                    l0A(t + 1)
                l0B(t)

            def ffn(layer, final):
                A.at(0)
                gated, r_gated = A.get(NJ * 1024, BF16, "gated", nres=NJ); gated = gated.rearrange("p (j n) -> p j n", j=NJ)
                x1T, r_x1T = A.get(8 * 1024, BF16, "x1T", nres=8); x1T = x1T.rearrange("p (k n) -> p k n", k=8)
                cb0_, r_cb0_ = A.get(512, F32, "cbuf"); cbuf = [cb0_, cb0_]; r_cbuf = [r_cb0_, r_cb0_]
                sb0_, r_sb0_ = A.get(512, F32, "sbf"); sbf = [sb0_, sb0_]; r_sbf = [r_sb0_, r_sb0_]
                zf0, r_zf0 = A.get(D, F32, "zf"); zf = [zf0, zf0]; r_zf = [r_zf0, r_zf0]
                scf, r_scf = dbl(16, F32, "scf")
                cwb, r_cw = A.get(NJ * 4, F32, "cw"); cw = cwb[:, 0:NJ * 3].rearrange("p (j c) -> p j c", j=NJ); cb = cwb[:, NJ * 3:NJ * 4]
                hist, r_hist = A.get(NJ * 2, F32, "hist"); hist = hist.rearrange("p (j c) -> p j c", j=NJ)
                cst, r_cst = A.get(NJ * 2, F32, "cst"); cst = cst.rearrange("p (j c) -> p j c", j=NJ)
                cso, r_cso = A.get(512, F32, "cso")
                lnt, r_lnt = A.get(2 * D, F32, "lntf")
                wup = []
                r_wup = []
                for i in range(2):
                    v, r = A.get(8 * 256, BF16, f"wup{i}")
                    wup.append(v.rearrange("p (k n) -> p k n", k=8)); r_wup.append(r)
                wdn, r_wdn = A.get(NJ * 1024, BF16, "wdn"); wdn = wdn.rearrange("p (j n) -> p j n", j=NJ)
                if with_sample:
                    xs1T, r_xs1T = A.get(128, BF16, "xs1T"); xs1T = xs1T.rearrange("p (k t) -> p k t", k=8)
                    aext, r_aext = A.get(NJ * 24, F32, "aext"); aext = aext.rearrange("p (j b r) -> p j b r", j=NJ, b=4)
                    b_s, r_bs = A.get(NJ * 16, F32, "b_s"); b_s = b_s.rearrange("p (j t) -> p j t", j=NJ)
                    scT, r_scT = A.get(NJ * 8, F32, "scT"); scT = scT.rearrange("p (j r) -> p j r", j=NJ)
                    c_s, r_cs = A.get(NJ * 16, F32, "c_s"); c_s = c_s.rearrange("p (j b l) -> p j b l", j=NJ, b=4)
                    t_s, r_ts = A.get(NJ * 16, F32, "t_s"); t_s = t_s.rearrange("p (j b l) -> p j b l", j=NJ, b=4)
                    gated_s, r_gs = A.get(NJ * 16, BF16, "gated_s"); gated_s = gated_s.rearrange("p (j t) -> p j t", j=NJ)
                print("ffn arena bytes", A.cur)
                S.dma("sp", lambda e: e.dma_start(out=cw, in_=conv_w[layer]), w=[r_cw])
                S.dma("sp", lambda e: e.dma_start(out=cb, in_=conv_b[layer]), w=[r_cw])
                load_ln(lnt, r_lnt, ln_ffn_g[layer], ln_ffn_b[layer])
                wdn_v = w_down[layer].rearrange("(j p) n -> p j n", p=128)
                wdn_loaded = [False]

                def load_wdn():
                    for jj in range(0, NJ, 2):
                        S.dma("pool", lambda e, jj=jj: e.dma_start(out=wdn[:, jj:jj + 2, :], in_=wdn_v[:, jj:jj + 2, :]), w=[r_wdn])

                def load_wup(j, slot):
                    S.dma("pool", lambda e, j=j, slot=slot: e.dma_start(out=wup[slot], in_=w_up[layer, j]), w=[r_wup[slot]])

                if with_sample:
                    transpose_x(XS[:], r_XS, xs1T, r_xs1T, npart=NS)
                    for jg in range(0, NJ, 4):
                        n = min(4, NJ - jg)
                        S.dma("sp", lambda e, jg=jg, n=n: e.dma_start(out=cso[0:8, 0:n * 128], in_=sconv[layer][:, jg * 128:(jg + n) * 128]), w=[r_cso])
                        bst, rst = ring.get(1)
                        for i_ in range(n):
                            S.op("pe", lambda e, i_=i_, bst=bst: e.transpose(out=bank(bst)[:, i_ * 8:(i_ + 1) * 8], in_=cso[0:8, i_ * 128:(i_ + 1) * 128], identity=ident_f[0:8, 0:8]), r=[r_cso, r_idf], w=rst)
                        S.op("act", lambda e, jg=jg, n=n, bst=bst: e.copy(out=scT[:, jg:jg + n, :], in_=bank(bst)[:, 0:n * 8].rearrange("p (j r) -> p j r", j=n)), r=rst, w=[r_scT])
                    S.op("pool", lambda e: e.tensor_copy(out=aext[:, :, :, 0:2], in_=scT.rearrange("p j (b r) -> p j b r", b=4)), r=[r_scT], w=[r_aext])
                cnt = 0
                for half in range(2):
                    for tt in range(8):
                        t = half * 8 + tt
                        transpose_x(X[:, t, :], rX[t], x1T[:, :, tt * 128:(tt + 1) * 128], r_x1T[tt])
                    load_wup(0, cnt % 2)
                    for j in range(NJ):
                        slot = cnt % 2
                        if j + 1 < NJ:
                            load_wup(j + 1, (cnt + 1) % 2)
                        elif half == 0:
                            pass
                        if half == 0 and j == 1:
                            load_wdn()
                        cnt += 1
                        for grp in range(2):
                            ba, ra = ring.get(1)
                            bb_, rb = ring.get(1)
                            cols = slice(grp * 512, (grp + 1) * 512)
                            for (bx, rx, c0) in ((ba, ra, 0), (bb_, rb, 128)):
                                for k in range(8):
                                    S.op("pe", lambda e, bx=bx, k=k, c0=c0, slot=slot, cols=cols: e.matmul(bank(bx), lhsT=wup[slot][:, k, c0:c0 + 128], rhs=x1T[:, k, cols], start=(k == 0), stop=(k == 7)),
                                         r=[r_wup[slot]] + r_x1T[grp * 4:grp * 4 + 4], w=rx)
                            pa = bank(ba)
                            cp = (j * 2 + grp) % 2
                            c_ = cbuf[cp]
                            s_ = sbf[cp]
                            S.op("act", lambda e, pa=pa, c_=c_, j=j: e.activation(out=c_, in_=pa, func=AF.Identity, bias=cb[:, j:j + 1], scale=cw[:, j, 2:3]), r=ra + [r_cw], w=[r_cbuf[cp]])
                            S.op("dve", lambda e, pa=pa, c_=c_, j=j: e.scalar_tensor_tensor(out=c_[:, 1:512], in0=pa[:, 0:511], scalar=cw[:, j, 1:2], in1=c_[:, 1:512], op0=ALU.mult, op1=ALU.add), r=ra + [r_cw, r_cbuf[cp]], w=[r_cbuf[cp]])
                            S.op("dve", lambda e, pa=pa, c_=c_, j=j: e.scalar_tensor_tensor(out=c_[:, 2:512], in0=pa[:, 0:510], scalar=cw[:, j, 0:1], in1=c_[:, 2:512], op0=ALU.mult, op1=ALU.add), r=ra + [r_cw, r_cbuf[cp]], w=[r_cbuf[cp]])
                            if not (half == 0 and grp == 0):
                                S.op("dve", lambda e, c_=c_, j=j: e.scalar_tensor_tensor(out=c_[:, 0:1], in0=hist[:, j, 1:2], scalar=cw[:, j, 1:2], in1=c_[:, 0:1], op0=ALU.mult, op1=ALU.add), r=[r_hist, r_cw, r_cbuf[cp]], w=[r_cbuf[cp]])
                                S.op("dve", lambda e, c_=c_, j=j: e.scalar_tensor_tensor(out=c_[:, 0:2], in0=hist[:, j, 0:2], scalar=cw[:, j, 0:1], in1=c_[:, 0:2], op0=ALU.mult, op1=ALU.add), r=[r_hist, r_cw, r_cbuf[cp]], w=[r_cbuf[cp]])
                            if half == 1 and grp == 1:
                                S.op("act", lambda e, pa=pa, j=j: e.copy(out=cst[:, j, :], in_=pa[:, 510:512]), r=ra, w=[r_cst])
                            else:
                                S.op("act", lambda e, pa=pa, j=j: e.copy(out=hist[:, j, :], in_=pa[:, 510:512]), r=ra, w=[r_hist])
                            S.op("act", lambda e, c_=c_, s_=s_: e.activation(out=s_, in_=c_, func=AF.Silu), r=[r_cbuf[cp]], w=[r_sbf[cp]])
                            S.op("dve", lambda e, s_=s_, bb_=bb_, j=j, cols=cols: e.tensor_tensor(out=gated[:, j, cols], in0=s_, in1=bank(bb_), op=ALU.mult), r=[r_sbf[cp]] + rb, w=[r_gated[j]])
                        if with_sample and half == 0:
                            bsa, rsa = ring.get(1)
                            for (c0, col) in ((0, 0), (128, 16)):
                                for k in range(8):
                                    S.op("pe", lambda e, k=k, c0=c0, col=col, slot=slot, bsa=bsa: e.matmul(bank(bsa)[:, col:col + 16], lhsT=wup[slot][:, k, c0:c0 + 128], rhs=xs1T[:, k, :], start=(k == 0), stop=(k == 7)), r=[r_wup[slot], r_xs1T], w=rsa)
                            S.op("act", lambda e, j=j, bsa=bsa: e.copy(out=aext[:, j, :, 2:6], in_=bank(bsa)[:, 0:16].rearrange("p (b l) -> p b l", b=4)), r=rsa, w=[r_aext])
                            S.op("dve", lambda e, j=j, bsa=bsa: e.tensor_copy(out=b_s[:, j, :], in_=bank(bsa)[:, 16:32]), r=rsa, w=[r_bs])
                    if with_sample and half == 0:
                        def cwb(c):
                            return cw[:, :, c:c + 1].unsqueeze(3).to_broadcast([128, NJ, 4, 4])
                        S.op("dve", lambda e: e.tensor_tensor(out=c_s, in0=aext[:, :, :, 2:6], in1=cwb(2), op=ALU.mult), r=[r_aext, r_cw], w=[r_cs])
                        S.op("pool", lambda e: e.tensor_tensor(out=t_s, in0=aext[:, :, :, 1:5], in1=cwb(1), op=ALU.mult), r=[r_aext, r_cw], w=[r_ts])
                        S.op("dve", lambda e: e.tensor_tensor(out=c_s, in0=c_s, in1=t_s, op=ALU.add), r=[r_cs, r_ts], w=[r_cs])
                        S.op("pool", lambda e: e.tensor_tensor(out=t_s, in0=aext[:, :, :, 0:4], in1=cwb(0), op=ALU.mult), r=[r_aext, r_cw], w=[r_ts])
                        S.op("dve", lambda e: e.tensor_tensor(out=c_s, in0=c_s, in1=t_s, op=ALU.add), r=[r_cs, r_ts], w=[r_cs])
                        S.op("dve", lambda e: e.tensor_tensor(out=c_s, in0=c_s, in1=cb.unsqueeze(2).unsqueeze(3).to_broadcast([128, NJ, 4, 4]), op=ALU.add), r=[r_cs, r_cw], w=[r_cs])
                        S.op("act", lambda e: e.activation(out=c_s, in_=c_s, func=AF.Silu), r=[r_cs], w=[r_cs])
                        S.op("dve", lambda e: e.tensor_tensor(out=gated_s, in0=c_s.rearrange("p j b l -> p j (b l)"), in1=b_s, op=ALU.mult), r=[r_cs, r_bs], w=[r_gs])
                        S.op("pool", lambda e: e.tensor_copy(out=scT.rearrange("p j (b r) -> p j b r", b=4), in_=aext[:, :, :, 4:6]), r=[r_aext], w=[r_scT])
                        for jg in range(0, NJ, 4):
                            n = min(4, NJ - jg)
                            bst, rst = ring.get(1)
                            for i_ in range(n):
                                S.op("pe", lambda e, i_=i_, jg=jg, bst=bst: e.transpose(out=bank(bst)[0:8, i_ * 128:(i_ + 1) * 128], in_=scT[:, jg + i_, :], identity=ident_f[:]), r=[r_scT, r_idf], w=rst)
                            S.op("act", lambda e, n=n, bst=bst: e.copy(out=cso[0:8, 0:n * 128], in_=bank(bst)[0:8, 0:n * 128]), r=rst, w=[r_cso])
                            out_toks.append(S.dma("sp", lambda e, jg=jg, n=n: e.dma_start(out=convs[layer][:, jg * 128:(jg + n) * 128], in_=cso[0:8, 0:n * 128]), r=[r_cso]))
                    for tt in range(8):
                        t = half * 8 + tt
                        pb = tt % 2
                        by, ry = ring.get(2)
                        for hf in range(2):
                            for j in range(NJ):
                                S.op("pe", lambda e, hf=hf, j=j, by=by, tt=tt: e.matmul(bank(by + hf), lhsT=gated[:, j, tt * 128:(tt + 1) * 128], rhs=wdn[:, j, hf * 512:(hf + 1) * 512], start=(j == 0), stop=(j == NJ - 1)),
                                     r=[r_gated[j], r_wdn], w=[ry[hf]])
                        S.op("dve", lambda e, by=by, pb=pb, t=t: e.scalar_tensor_tensor(out=zf[pb], in0=X[:, t, :], scalar=ALPHA, in1=bank(by, 2), op0=ALU.mult, op1=ALU.add), r=[rX[t]] + ry, w=[r_zf[pb]])
                        layer_norm_tile(zf[pb], r_zf[pb], X[:, t, :], rX[t], 128, scf[pb], r_scf[pb], lnt, r_lnt)
                        if final:
                            out_toks.append(S.dma("sp", lambda e, t=t: e.dma_start(out=yp[t * 128:(t + 1) * 128, :], in_=X[:, t, :]), r=[rX[t]]))
                if with_sample:
                    bys, rys = ring.get(2)
                    for hf in range(2):
                        for j in range(NJ):
                            S.op("pe", lambda e, hf=hf, j=j, bys=bys: e.matmul(bank(bys + hf)[0:NS, :], lhsT=gated_s[:, j, :], rhs=wdn[:, j, hf * 512:(hf + 1) * 512], start=(j == 0), stop=(j == NJ - 1)), r=[r_gs, r_wdn], w=[rys[hf]])
                    S.op("dve", lambda e, bys=bys: e.scalar_tensor_tensor(out=zf[0][0:NS, :], in0=XS[:], scalar=ALPHA, in1=bank(bys, 2)[0:NS, :], op0=ALU.mult, op1=ALU.add), r=[r_XS] + rys, w=[r_zf[0]])
                    layer_norm_tile(zf[0][0:NS, :], r_zf[0], XS[:], r_XS, NS, scf[0], r_scf[0], lnt, r_lnt)
                    if final:
                        out_toks.append(S.dma("sp", lambda e: e.dma_start(out=ys, in_=XS[:]), r=[r_XS]))
                bc, rc = ring.get(1)
                for j in range(NJ):
                    S.op("pe", lambda e, j=j, bc=bc: e.transpose(out=bank(bc)[0:2, (j % 4) * 128:(j % 4 + 1) * 128], in_=cst[:, j, :], identity=ident_f[:]), r=[r_cst, r_idf], w=rc)
                    if j % 4 == 3 or j == NJ - 1:
                        j0 = j - j % 4
                        n = j - j0 + 1
                        S.op("act", lambda e, bc=bc, n=n: e.copy(out=cso[0:2, 0:n * 128], in_=bank(bc)[0:2, 0:n * 128]), r=rc, w=[r_cso])
                        out_toks.append(S.dma("sp", lambda e, j0=j0, n=n: e.dma_start(out=convp[layer][:, j0 * 128:(j0 + n) * 128], in_=cso[0:2, 0:n * 128]), r=[r_cso]))
                        if j != NJ - 1:
                            bc, rc = ring.get(1)


            def sample_l0():
                A.at(L0_BUF)
                xsT, r_xsT = A.get(128, BF16, "xsT"); xsT = xsT.rearrange("p (k t) -> p k t", k=8)
                tb, r_tb = A.get(68, F32, "stab")
                c2s = tb[:, 0:16]; sss = tb[:, 16:32]
                decTs = tb[0:4, 32:48]
                qdecs = tb[:, 48:64].rearrange("p (h l) -> p h l", h=4)
                kdecs = tb[0:4, 64:68]
                S.dma("sp", lambda e: e.dma_start(out=c2s, in_=tdr["c2s"]), w=[r_tb])
                S.dma("sp", lambda e: e.dma_start(out=sss, in_=tdr["sss"]), w=[r_tb])
                S.dma("sp", lambda e: e.dma_start(out=decTs, in_=tdr["decTs"].rearrange("m h l -> m (h l)")), w=[r_tb])
                S.dma("sp", lambda e: e.dma_start(out=qdecs, in_=tdr["qdecs"]), w=[r_tb])
                S.dma("sp", lambda e: e.dma_start(out=kdecs, in_=tdr["kdecs"]), w=[r_tb])
                sp_sb, r_sp = A.get(512, F32, "sp_sb")
                S.dma("sp", lambda e: e.dma_start(out=sp_sb[0:60, :], in_=spool), w=[r_sp])
                S0f, r_S0f = A.get(16 * 128, F32, "S0f"); S0f = S0f.rearrange("p (n v) -> p n v", n=16)
                S0b, r_S0b = A.get(16 * 128, BF16, "S0b"); S0b = S0b.rearrange("p (n v) -> p n v", n=16)
                S.dma("sp", lambda e: e.dma_start(out=S0f, in_=sret.rearrange("n k v -> k n v")), w=[r_S0f])
                S.op("pool", lambda e: e.tensor_copy(out=S0b, in_=S0f), r=[r_S0f], w=[r_S0b])
                for bb in range(SB):
                    out_toks.append(S.dma("sp", lambda e, bb=bb: e.dma_start(out=pools[bb * 15:bb * 15 + 11, :], in_=spool[bb * 15 + 4:bb * 15 + 15, :]), semkey="d2d"))
                transpose_x(XS[:], r_XS, xsT, r_xsT, npart=NS)
                bq_, rq_ = ring.get(1)
                pqk = bank(bq_)[:, 0:192].rearrange("p (g t) -> p g t", g=12)
                for g in range(12):
                    for k in range(8):
                        S.op("pe", lambda e, g=g, k=k: e.matmul(pqk[:, g, :], lhsT=w_in_sb[:, k, g * 128:(g + 1) * 128], rhs=xsT[:, k, :], start=(k == 0), stop=(k == 7)), r=[r_win, r_xsT], w=rq_)
                bu_, ru_ = ring.get(1)
                for k in range(8):
                    S.op("pe", lambda e, k=k: e.matmul(bank(bu_)[0:NS, :], lhsT=xsT[:, k, :], rhs=w_in_sb[:, k, 0:512], start=(k == 0), stop=(k == 7)), r=[r_win, r_xsT], w=ru_)
                utok, r_utok = A.get(512, F32, "utok")
                S.op("act", lambda e: e.copy(out=utok[0:NS, :], in_=bank(bu_)[0:NS, :]), r=ru_, w=[r_utok])
                for bb in range(SB):
                    out_toks.append(S.dma("sp", lambda e, bb=bb: e.dma_start(out=pools[bb * 15 + 11:bb * 15 + 15, :], in_=utok[bb * 4:bb * 4 + 4, :]), r=[r_utok]))
                bh_, rh_ = ring.get(1)
                for g in range(4):
                    S.op("pe", lambda e, g=g: e.transpose(out=bank(bh_)[:, g * 64:g * 64 + 60], in_=sp_sb[0:60, g * 128:(g + 1) * 128], identity=ident_f[0:60, 0:60]), r=[r_sp, r_idf], w=rh_)
                ue, r_ue = A.get(16 * 19, F32, "ues"); ue4 = ue.rearrange("p (g b r) -> p g b r", g=4, b=4); ue3 = ue.rearrange("p (n r) -> p n r", n=16)
                hv = bank(bh_)[:, 0:256].rearrange("p (g c) -> p g c", g=4)[:, :, 0:60].rearrange("p g (b r) -> p g b r", b=4)
                for g in range(4):
                    S.op("act", lambda e, g=g: e.copy(out=ue4[:, g, :, 0:15], in_=hv[:, g]), r=rh_, w=[r_ue])
                    S.op("act", lambda e, g=g: e.copy(out=ue4[:, g, :, 15:19], in_=pqk[:, g, :].rearrange("p (b l) -> p b l", b=4)), r=rq_, w=[r_ue])
                tAs, r_tAs = A.get(16 * 19, F32, "tAs"); tAs = tAs.rearrange("p (n r) -> p n r", n=16)
                tBs, r_tBs = A.get(16 * 19, F32, "tBs"); tBs = tBs.rearrange("p (n r) -> p n r", n=16)
                S.op("pool", lambda e: e.tensor_tensor(out=tAs[:, :, 1:19], in0=ue3[:, :, 1:19], in1=ue3[:, :, 0:18], op=ALU.add), r=[r_ue], w=[r_tAs])
                S.op("pool", lambda e: e.tensor_tensor(out=tBs[:, 4:16, 3:19], in0=tAs[:, 4:16, 3:19], in1=tAs[:, 4:16, 1:17], op=ALU.add), r=[r_tAs], w=[r_tBs])
                S.op("pool", lambda e: e.tensor_tensor(out=tAs[:, 8:16, 7:19], in0=tBs[:, 8:16, 7:19], in1=tBs[:, 8:16, 3:15], op=ALU.add), r=[r_tBs], w=[r_tAs])
                S.op("pool", lambda e: e.tensor_tensor(out=tBs[:, 12:16, 15:19], in0=tAs[:, 12:16, 15:19], in1=tAs[:, 12:16, 7:11], op=ALU.add), r=[r_tAs], w=[r_tBs])
                pooleds, r_pooleds = A.get(64, BF16, "pooleds"); pooleds = pooleds.rearrange("p (g b l) -> p g b l", g=4, b=4)
                fins = (tAs, tBs, tAs, tBs)
                for g in range(4):
                    S.op("dve", lambda e, g=g: e.scalar_tensor_tensor(out=pooleds[:, g], in0=fins[g][:, g * 4:(g + 1) * 4, 15:19], scalar=1.0 / POOL_W[g], in1=ue3[:, g * 4:(g + 1) * 4, 15:19], op0=ALU.mult, op1=ALU.subtract), r=[r_tAs, r_tBs, r_ue], w=[r_pooleds])
                catTs, r_catTs = A.get(128, BF16, "catTs"); catTs = catTs.rearrange("p (c t) -> p c t", c=8)
                bpm_, rpm_ = ring.get(1)
                pms = bank(bpm_)[:, 0:64].rearrange("p (g t) -> p g t", g=4)
                for g in range(4):
                    S.op("pe", lambda e, g=g: e.matmul(pms[:, g, :], lhsT=poolw_sb[:, g, :], rhs=pooleds[:, g].rearrange("p b l -> p (b l)"), start=True, stop=True), r=[r_poolw, r_pooleds], w=rpm_)
                for g in range(4):
                    S.op("act", lambda e, g=g: e.activation(out=catTs[:, g, :], in_=pms[:, g, :], func=AF.Identity, bias=0.0, scale=pscale[:, g:g + 1]), r=rpm_ + [r_c0], w=[r_catTs])
                qkfs, r_qkfs = A.get(128, F32, "qkfs"); qkfs = qkfs.rearrange("p (h t) -> p h t", h=8)
                rAs, r_rAs = A.get(128, F32, "rAs"); rAs = rAs.rearrange("p (h t) -> p h t", h=8)
                rBs, r_rBs = A.get(128, F32, "rBs"); rBs = rBs.rearrange("p (h t) -> p h t", h=8)
                qkTs, r_qkTs = A.get(128, BF16, "qkTs"); qkTs = qkTs.rearrange("p (h t) -> p h t", h=8)
                qdTs, r_qdTs = A.get(64, BF16, "qdTs"); qdTs = qdTs.rearrange("p (h t) -> p h t", h=4)
                S.op("act", lambda e: e.copy(out=qkfs, in_=pqk[:, 4:12, :]), r=rq_, w=[r_qkfs])
                S.op("dve", lambda e: e.tensor_tensor(out=rAs, in0=qkfs, in1=c2s.unsqueeze(1).to_broadcast([128, 8, 16]), op=ALU.mult), r=[r_qkfs, r_tb], w=[r_rAs])
                S.op("pool", lambda e: e.tensor_tensor(out=rBs[0:64], in0=qkfs[64:128], in1=sss[64:128].unsqueeze(1).to_broadcast([64, 8, 16]), op=ALU.mult), r=[r_qkfs, r_tb], w=[r_rBs])
                S.op("pool", lambda e: e.tensor_tensor(out=rBs[64:128], in0=qkfs[0:64], in1=sss[0:64].unsqueeze(1).to_broadcast([64, 8, 16]), op=ALU.mult), r=[r_qkfs, r_tb], w=[r_rBs])
                S.op("dve", lambda e: e.tensor_tensor(out=qkTs, in0=rAs, in1=rBs, op=ALU.add), r=[r_rAs, r_rBs], w=[r_qkTs])
                S.op("pool", lambda e: e.tensor_tensor(out=qdTs.rearrange("p h (b l) -> p h b l", b=4), in0=qkTs[:, 0:4, :].rearrange("p h (b l) -> p h b l", b=4), in1=qdecs.unsqueeze(2).to_broadcast([128, 4, 4, 4]), op=ALU.mult), r=[r_qkTs, r_tb], w=[r_qdTs])
                v_s, r_vs = A.get(SB * 512, BF16, "v_s"); v_s = v_s.rearrange("p (b n) -> p b n", b=SB)
                sTs, r_sTs = A.get(64, BF16, "sTs"); sTs = sTs.rearrange("p (b h l) -> p b h l", b=4, h=4)
                kps, r_kps = A.get(512, BF16, "kps"); kps = kps.rearrange("p (h d) -> p h d", h=4)
                S.op("pool", lambda e: e.memset(v_s, 0.0), w=[r_vs])
                S.op("pool", lambda e: e.memset(sTs, 0.0), w=[r_sTs])
                S.op("pool", lambda e: e.memset(kps, 0.0), w=[r_kps])
                sggs, r_sggs = A.get(SB * 512, F32, "sggs"); sggs = sggs.rearrange("p (b n) -> p b n", b=SB)
                for bb in range(SB):
                    bv_, rv_ = ring.get(1)
                    bg_, rg_ = ring.get(1)
                    for (bx, rx, c0) in ((bv_, rv_, 1536), (bg_, rg_, 2048)):
                        for k in range(8):
                            S.op("pe", lambda e, bx=bx, k=k, c0=c0, bb=bb: e.matmul(bank(bx)[0:4, :], lhsT=xsT[:, k, bb * 4:bb * 4 + 4], rhs=w_in_sb[:, k, c0:c0 + 512], start=(k == 0), stop=(k == 7)), r=[r_win, r_xsT], w=rx)
                    S.op("act", lambda e, bv_=bv_, bb=bb: e.copy(out=v_s[0:4, bb, :], in_=bank(bv_)[0:4, :]), r=rv_, w=[r_vs])
                    S.op("act", lambda e, bg_=bg_, bb=bb: e.activation(out=sggs[0:4, bb, :], in_=bank(bg_)[0:4, :], func=AF.Silu), r=rg_, w=[r_sggs])
                    S.op("pool", lambda e, bb=bb: e.tensor_tensor(out=sggs[0:4, bb, :], in0=sggs[0:4, bb, :], in1=gng[0:4], op=ALU.mult), r=[r_sggs, r_c0], w=[r_sggs])
                bs_, rs_ = ring.get(1)
                for bb in range(SB):
                    for h in range(4):
                        c0 = (bb * 4 + h) * 4
                        S.op("pe", lambda e, bb=bb, h=h, c0=c0: e.matmul(bank(bs_)[0:4, c0:c0 + 4], lhsT=qkTs[:, 4 + h, bb * 4:bb * 4 + 4], rhs=qkTs[:, h, bb * 4:bb * 4 + 4], start=True, stop=True), r=[r_qkTs], w=rs_)
                S.op("dve", lambda e: e.tensor_tensor(out=sTs[0:4].rearrange("p b h l -> p b (h l)"), in0=bank(bs_)[0:4, 0:64].rearrange("p (b n) -> p b n", b=4), in1=decTs.unsqueeze(1).to_broadcast([4, 4, 16]), op=ALU.mult), r=rs_ + [r_tb], w=[r_sTs])
                Snew, r_Snew = dbl(512, F32, "Snew", "p (h v) -> p h v", h=4)
                gsts, r_gsts = A.get(40, F32, "gsts")
                on1s, r_on1s = A.get(512, F32, "on1s")
                rets_bf, r_retsb = A.get(512, BF16, "retsb")
                for bb in range(SB):
                    bo_, ro_ = ring.get(1)
                    ovs = bank(bo_)[0:4, :].rearrange("p (h t) -> p h t", h=4)
                    for h in range(4):
                        S.op("pe", lambda e, h=h, bb=bb, ovs=ovs: e.matmul(ovs[:, h, :], lhsT=sTs[:, bb, h, :], rhs=v_s[:, bb, h * 128:(h + 1) * 128], start=True, stop=False), r=[r_sTs, r_vs], w=ro_)
                        S.op("pe", lambda e, h=h, bb=bb, ovs=ovs: e.matmul(ovs[:, h, :], lhsT=qdTs[:, h, bb * 4:bb * 4 + 4], rhs=S0b[:, bb * 4 + h, :], start=False, stop=True), r=[r_qdTs, r_S0b], w=ro_)
                    bkt_, rkt_ = ring.get(1)
                    ktvs = bankbf(bkt_)[0:4, 0:512].rearrange("p (h t) -> p h t", h=4)
                    for h in range(4):
                        S.op("pe", lambda e, h=h, bb=bb, ktvs=ktvs: e.transpose(out=ktvs[:, h, :], in_=qkTs[:, 4 + h, bb * 4:bb * 4 + 4], identity=ident_b[:]), r=[r_qkTs, r_idb], w=rkt_)
                    for h in range(4):
                        S.op("act", lambda e, h=h, ktvs=ktvs: e.activation(out=kps[0:4, h, :], in_=ktvs[:, h, :], func=AF.Identity, bias=0.0, scale=kdecs[:, h:h + 1]), r=rkt_ + [r_tb], w=[r_kps])
                    bds_, rds_ = ring.get(1)
                    dss = bank(bds_).rearrange("p (h t) -> p h t", h=4)
                    for h in range(4):
                        S.op("pe", lambda e, h=h, bb=bb, dss=dss: e.matmul(dss[:, h, :], lhsT=kps[:, h, :], rhs=v_s[:, bb, h * 128:(h + 1) * 128], start=True, stop=True), r=[r_kps, r_vs], w=rds_)
                    sn = Snew[bb % 2]
                    for h in range(4):
                        S.op("dve", lambda e, h=h, bb=bb, dss=dss, sn=sn: e.scalar_tensor_tensor(out=sn[:, h, :], in0=S0f[:, bb * 4 + h, :], scalar=tabs["gCs"][h], in1=dss[:, h, :], op0=ALU.mult, op1=ALU.add), r=rds_ + [r_S0f], w=[r_Snew[bb % 2]])
                    out_toks.append(S.dma("sp", lambda e, bb=bb, sn=sn: e.dma_start(out=rets[bb * 4:(bb + 1) * 4].rearrange("h k v -> k h v"), in_=sn), r=[r_Snew[bb % 2]]))
                    for h in range(4):
                        S.op("dve", lambda e, h=h, ovs=ovs: e.bn_stats(out=gsts[0:4, h * 6:(h + 1) * 6], in_=ovs[:, h, :]), r=ro_, w=[r_gsts])
                        S.op("dve", lambda e, h=h: e.bn_aggr(out=gsts[0:4, 24 + 2 * h:26 + 2 * h], in_=gsts[0:4, h * 6:(h + 1) * 6]), r=[r_gsts], w=[r_gsts])
                    mvs = gsts[0:4, 24:32].rearrange("p (h two) -> p h two", two=2)
                    S.op("act", lambda e: e.activation(out=gsts[0:4, 32:36], in_=mvs[:, :, 1], func=AF.Sqrt, bias=eps_gn[0:4], scale=1.0), r=[r_gsts, r_eps], w=[r_gsts])
                    S.op("dve", lambda e: e.reciprocal(out=gsts[0:4, 32:36], in_=gsts[0:4, 32:36]), r=[r_gsts], w=[r_gsts])
                    S.op("dve", lambda e: e.scalar_tensor_tensor(out=gsts[0:4, 36:40], in0=mvs[:, :, 0], scalar=-1.0, in1=gsts[0:4, 32:36], op0=ALU.mult, op1=ALU.mult), r=[r_gsts], w=[r_gsts])
                    for h in range(4):
                        S.op("act", lambda e, h=h, ovs=ovs: e.activation(out=on1s[0:4, h * 128:(h + 1) * 128], in_=ovs[:, h, :], func=AF.Identity, bias=gsts[0:4, 36 + h:37 + h], scale=gsts[0:4, 32 + h:33 + h]), r=ro_ + [r_gsts], w=[r_on1s])
                    S.op("pool", lambda e, bb=bb: e.tensor_tensor(out=rets_bf[0:4, :], in0=on1s[0:4, :], in1=sggs[0:4, bb, :], op=ALU.mult), r=[r_on1s, r_sggs], w=[r_retsb])
                    brt_, rrt_ = ring.get(1)
                    rtvs = bankbf(brt_)[:, 0:16].rearrange("p (h t) -> p h t", h=4)
                    for h in range(4):
                        S.op("pe", lambda e, h=h, rtvs=rtvs: e.transpose(out=rtvs[:, h, :], in_=rets_bf[0:4, h * 128:(h + 1) * 128], identity=ident_b[0:4, 0:4]), r=[r_retsb, r_idb], w=rrt_)
                    S.op("dve", lambda e, rtvs=rtvs, bb=bb: e.tensor_copy(out=catTs[:, 4:8, bb * 4:bb * 4 + 4], in_=rtvs), r=rrt_, w=[r_catTs])
                by_, ry_ = ring.get(2)
                for hf in range(2):
                    for c in range(8):
                        S.op("pe", lambda e, hf=hf, c=c: e.matmul(bank(by_ + hf)[0:NS, :], lhsT=catTs[:, c, :], rhs=w_o_sb[:, c, hf * 512:(hf + 1) * 512], start=(c == 0), stop=(c == 7)), r=[r_catTs, r_wo], w=[ry_[hf]])
                zs, r_zs = A.get(D, F32, "zs")
                scs, r_scs = A.get(16, F32, "scs")
                S.op("dve", lambda e: e.scalar_tensor_tensor(out=zs[0:NS, :], in0=XS[:], scalar=ALPHA, in1=bank(by_, 2)[0:NS, :], op0=ALU.mult, op1=ALU.add), r=[r_XS] + ry_, w=[r_zs])
                layer_norm_tile(zs[0:NS, :], r_zs, XS[:], r_XS, NS, scs, r_scs, lnt0, r_lnt0)

            if with_sample:
                sample_l0()
            chk('l0')
            S.set_phase('ffn0')
            ffn(0, False)
            chk('ffn0')
            S.set_phase('l1')

            A.at(0)
            wdq_sb, r_wdq = A.get(8 * 512, BF16, "wdq"); wdq_sb = wdq_sb.rearrange("p (k n) -> p k n", k=8)
            wuq_sb, r_wuq = A.get(4 * 1536, BF16, "wuq"); wuq_sb = wuq_sb.rearrange("p (k n) -> p k n", k=4)
            wdkv_sb, r_wdkv = A.get(8 * 320, BF16, "wdkv"); wdkv_sb = wdkv_sb.rearrange("p (k n) -> p k n", k=8)
            wukT_sb, r_wukT = A.get(8 * 256, BF16, "wukT"); wukT_sb = wukT_sb.rearrange("p (h c) -> p h c", h=8)
            wuv_sb, r_wuv = A.get(2 * 1024, BF16, "wuv"); wuv_sb = wuv_sb.rearrange("p (k n) -> p k n", k=2)
            wo1_sb, r_wo1 = A.get(8 * 1024, BF16, "wo1"); wo1_sb = wo1_sb.rearrange("p (k n) -> p k n", k=8)
            c1buf, r_c1 = A.get(512 + 256, F32, "c1")
            qng = c1buf[:, 0:512]
            kvng = c1buf[:, 512:768]
            lnt1, r_lnt1 = A.get(2 * D, F32, "lnt1")
            ckvT, r_ckvT = A.get(2 * SEQ, BF16, "ckvT", nres=NT); ckvT = ckvT.rearrange("p (k n) -> p k n", k=2)
            ckvK, r_ckvK = A.get(NT * 256, BF16, "ckvK", nres=NT); ckvK = ckvK.rearrange("p (t c) -> p t c", t=NT)
            kpeT, r_kpeT = A.get(SEQ, BF16, "kpeT", nres=NT)
            S.dma("pool", lambda e: e.dma_start(out=wdq_sb, in_=w_dq.rearrange("(k p) n -> p k n", p=128)), w=[r_wdq])
            S.dma("pool", lambda e: e.dma_start(out=wdkv_sb, in_=w_dkv.rearrange("(k p) n -> p k n", p=128)), w=[r_wdkv])
            S.dma("pool", lambda e: e.dma_start(out=wuq_sb, in_=w_uq.rearrange("(k p) n -> p k n", p=128)), w=[r_wuq])
            S.dma("pool", lambda e: e.dma_start(out=wukT_sb, in_=w_ukT), w=[r_wukT])
            S.dma("pool", lambda e: e.dma_start(out=wuv_sb, in_=w_uv.rearrange("(k p) n -> p k n", p=128)), w=[r_wuv])
            S.dma("pool", lambda e: e.dma_start(out=wo1_sb, in_=w_o1.rearrange("(k p) n -> p k n", p=128)), w=[r_wo1])
            S.dma("sp", lambda e: e.dma_start(out=qng, in_=qn_g.partition_broadcast(128)), w=[r_c1])
            S.dma("sp", lambda e: e.dma_start(out=kvng, in_=kvn_g.partition_broadcast(128)), w=[r_c1])
            load_ln(lnt1, r_lnt1, ln_mix_g[1], ln_mix_b[1])
            L1_BUF = A.cur

            xT1, r_xT1 = dbl(1024, BF16, "xT1", "p (k t) -> p k t", k=8)
            rt1, r_rt1 = dbl(128, F32, "rt1")
            sc1, r_sc1 = dbl(16, F32, "sc1")
            cqn, r_cqn = dbl(512, BF16, "cqn")
            cqnT, r_cqnT = dbl(512, BF16, "cqnT", "p (k t) -> p k t", k=4)
            ckv_f, r_ckvf = dbl(256, F32, "ckv_f")
            kpe_f, r_kpef = dbl(64, F32, "kpe_f")
            kpe_t, r_kpet = A.get(128, F32, "kpe_t")
            kpe_b, r_kpeb = dbl(128, BF16, "kpe_b")
            qnT0_, r_qnT0_ = A.get(1024, BF16, "qnT"); qnT0_ = qnT0_.rearrange("p (h t) -> p h t", h=8); qnT = [qnT0_, qnT0_]; r_qnT = [r_qnT0_, r_qnT0_]
            qpe_t, r_qpet = A.get(1024, F32, "qpe_t")
            junk, r_junk = qpe_t[:, 0:512], r_qpet
            qpe_b, r_qpeb = dbl(512, BF16, "qpe_b")
            qpeT, r_qpeT = dbl(1024, BF16, "qpeT", "p (h t) -> p h t", h=8)
            qlT, r_qlT = dbl(2048, BF16, "qlT", "p (h k t) -> p h k t", h=8, k=2)
            p_sb, r_p = dbl(SEQ, BF16, "p_sb")
            pT_sb, r_pT = dbl(SEQ, BF16, "pT_sb")
            st1, r_st1 = dbl(8, F32, "st1")
            ol_b, r_olb = dbl(256, BF16, "ol_b")
            olT, r_olT = dbl(256, BF16, "olT", "p (k t) -> p k t", k=2)
            oT10_, r_oT10_ = A.get(1024, BF16, "oT1"); oT10_ = oT10_.rearrange("p (h t) -> p h t", h=8); oT1 = [oT10_, oT10_]; r_oT1 = [r_oT10_, r_oT10_]
            z10, r_z10 = A.get(D, F32, "z1"); z1 = [z10, z10]; r_z1 = [r_z10, r_z10]
            print("layer1 arena bytes", A.cur)

            for i_ in range(2):
                S.op("pool", lambda e, i_=i_: e.memset(qpeT[i_], 0.0), w=[r_qpeT[i_]])
            def l1A(t):
                pb = t % 2
                tok = slice(t * 128, (t + 1) * 128)
                S.dma("sp", lambda e, pb=pb, t=t: e.dma_start(out=rt1[pb][:, 0:64], in_=tdr["cc1"][:, t, :]), w=[r_rt1[pb]])
                S.dma("sp", lambda e, pb=pb, t=t: e.dma_start(out=rt1[pb][:, 64:128], in_=tdr["ss1"][:, t, :]), w=[r_rt1[pb]])
                transpose_x(X[:, t, :], rX[t], xT1[pb], r_xT1[pb])
                bcq, rcq = ring.get(1)
                bkv, rkv = ring.get(1)
                for k in range(8):
                    S.op("pe", lambda e, k=k, pb=pb, bcq=bcq: e.matmul(bank(bcq), lhsT=xT1[pb][:, k, :], rhs=wdq_sb[:, k, :], start=(k == 0), stop=(k == 7)), r=[r_xT1[pb], r_wdq], w=rcq)
                for k in range(8):
                    S.op("pe", lambda e, k=k, pb=pb, bkv=bkv: e.matmul(bank(bkv)[:, 0:320], lhsT=xT1[pb][:, k, :], rhs=wdkv_sb[:, k, :], start=(k == 0), stop=(k == 7)), r=[r_xT1[pb], r_wdkv], w=rkv)
                sc = sc1[pb]
                S.op("act", lambda e, bcq=bcq, sc=sc: e.activation(out=junk, in_=bank(bcq), func=AF.Square, accum_out=sc[:, 0:1]), r=rcq, w=[r_junk, r_sc1[pb]])
                S.op("act", lambda e, sc=sc: e.activation(out=sc[:, 1:2], in_=sc[:, 0:1], func=AF.Sqrt, bias=eps_rms, scale=1.0 / 512), r=[r_sc1[pb], r_eps], w=[r_sc1[pb]])
                S.op("dve", lambda e, sc=sc: e.reciprocal(out=sc[:, 1:2], in_=sc[:, 1:2]), r=[r_sc1[pb]], w=[r_sc1[pb]])
                S.op("dve", lambda e, bcq=bcq, sc=sc, pb=pb: e.scalar_tensor_tensor(out=cqn[pb], in0=bank(bcq), scalar=sc[:, 1:2], in1=qng, op0=ALU.mult, op1=ALU.mult), r=rcq + [r_sc1[pb], r_c1], w=[r_cqn[pb]])
                S.op("act", lambda e, bkv=bkv, sc=sc: e.activation(out=junk[:, 0:256], in_=bank(bkv)[:, 0:256], func=AF.Square, accum_out=sc[:, 2:3]), r=rkv, w=[r_junk, r_sc1[pb]])
                S.op("act", lambda e, sc=sc: e.activation(out=sc[:, 3:4], in_=sc[:, 2:3], func=AF.Sqrt, bias=eps_rms, scale=1.0 / 256), r=[r_sc1[pb], r_eps], w=[r_sc1[pb]])
                S.op("dve", lambda e, sc=sc: e.reciprocal(out=sc[:, 3:4], in_=sc[:, 3:4]), r=[r_sc1[pb]], w=[r_sc1[pb]])
                S.op("dve", lambda e, bkv=bkv, sc=sc, pb=pb: e.scalar_tensor_tensor(out=ckv_f[pb], in0=bank(bkv)[:, 0:256], scalar=sc[:, 3:4], in1=kvng, op0=ALU.mult, op1=ALU.mult), r=rkv + [r_sc1[pb], r_c1], w=[r_ckvf[pb]])
                out_toks.append(S.dma("sp", lambda e, pb=pb, tok=tok: e.dma_start(out=ckvp[tok, :], in_=ckv_f[pb]), r=[r_ckvf[pb]]))
                S.op("pool", lambda e, pb=pb, t=t: e.tensor_copy(out=ckvK[:, t, :], in_=ckv_f[pb]), r=[r_ckvf[pb]], w=[r_ckvK[t]])
                S.op("dve", lambda e, bkv=bkv, pb=pb: e.tensor_tensor(out=kpe_t[:, 0:64], in0=bank(bkv)[:, 256:320], in1=rt1[pb][:, 0:64], op=ALU.mult), r=rkv + [r_rt1[pb]], w=[r_kpet])
                S.op("dve", lambda e, bkv=bkv, pb=pb: e.tensor_tensor(out=kpe_t[:, 64:96], in0=bank(bkv)[:, 288:320], in1=rt1[pb][:, 64:96], op=ALU.mult), r=rkv + [r_rt1[pb]], w=[r_kpet])
                S.op("dve", lambda e, bkv=bkv, pb=pb: e.tensor_tensor(out=kpe_t[:, 96:128], in0=bank(bkv)[:, 256:288], in1=rt1[pb][:, 96:128], op=ALU.mult), r=rkv + [r_rt1[pb]], w=[r_kpet])
                S.op("pool", lambda e, pb=pb: e.tensor_tensor(out=kpe_f[pb], in0=kpe_t[:, 0:64], in1=kpe_t[:, 64:128], op=ALU.add), r=[r_kpet], w=[r_kpef[pb]])
                out_toks.append(S.dma("sp", lambda e, pb=pb, tok=tok: e.dma_start(out=kpep[tok, :], in_=kpe_f[pb]), r=[r_kpef[pb]]))
                S.op("pool", lambda e, pb=pb: e.tensor_copy(out=kpe_b[pb][:, 0:64], in_=kpe_f[pb]), r=[r_kpef[pb]], w=[r_kpeb[pb]])
                S.op("pool", lambda e, pb=pb: e.tensor_copy(out=kpe_b[pb][:, 64:128], in_=kpe_f[pb]), r=[r_kpef[pb]], w=[r_kpeb[pb]])
                btr, rtr = ring.get(2)
                trv = bankbf(btr, 2)
                for k in range(4):
                    S.op("pe", lambda e, k=k, trv=trv, pb=pb: e.transpose(out=trv[:, k * 128:(k + 1) * 128], in_=cqn[pb][:, k * 128:(k + 1) * 128], identity=ident_b[:]), r=[r_cqn[pb], r_idb], w=rtr)
                for k in range(2):
                    S.op("pe", lambda e, k=k, trv=trv, t=t: e.transpose(out=trv[:, 1024 + k * 128:1024 + (k + 1) * 128], in_=ckvK[:, t, k * 128:(k + 1) * 128], identity=ident_b[:]), r=[r_ckvK[t], r_idb], w=rtr)
                S.op("pe", lambda e, trv=trv, pb=pb: e.transpose(out=trv[:, 1280:1408], in_=kpe_b[pb], identity=ident_b[:]), r=[r_kpeb[pb], r_idb], w=rtr)
                S.op("act", lambda e, trv=trv, pb=pb: e.copy(out=cqnT[pb], in_=trv[:, 0:512].rearrange("p (k t) -> p k t", k=4)), r=rtr, w=[r_cqnT[pb]])
                S.op("dve", lambda e, trv=trv, tok=tok, t=t: e.tensor_copy(out=ckvT[:, :, tok], in_=trv[:, 1024:1280].rearrange("p (k t) -> p k t", k=2)), r=rtr, w=[r_ckvT[t]])
                S.op("dve", lambda e, trv=trv, tok=tok, t=t: e.tensor_copy(out=kpeT[:, tok], in_=trv[:, 1280:1408]), r=rtr, w=[r_kpeT[t]])
                for hh in range(2):
                    bqn, rqn = ring.get(1)
                    qv = bank(bqn).rearrange("p (h t) -> p h t", h=4)
                    for h4 in range(4):
                        h = hh * 4 + h4
                        for k in range(4):
                            S.op("pe", lambda e, qv=qv, h4=h4, h=h, k=k, pb=pb: e.matmul(qv[:, h4, :], lhsT=wuq_sb[:, k, h * 128:(h + 1) * 128], rhs=cqnT[pb][:, k, :], start=(k == 0), stop=(k == 3)), r=[r_wuq, r_cqnT[pb]], w=rqn)
                    evac("act" if hh == 0 else "dve", qnT[pb][:, hh * 4:hh * 4 + 4, :], qv, rqn, [r_qnT[pb]])
                bqp, rqp = ring.get(1)
                for k in range(4):
                    S.op("pe", lambda e, k=k, bqp=bqp, pb=pb: e.matmul(bank(bqp), lhsT=cqnT[pb][:, k, :], rhs=wuq_sb[:, k, 1024:1536], start=(k == 0), stop=(k == 3)), r=[r_wuq, r_cqnT[pb]], w=rqp)
                qpv = bank(bqp).rearrange("p (h d) -> p h d", h=8)
                qA = qpe_t[:, 0:512].rearrange("p (h d) -> p h d", h=8)
                qB = qpe_t[:, 512:1024].rearrange("p (h d) -> p h d", h=8)
                ccb = rt1[pb][:, 0:64].unsqueeze(1)
                ssb1 = rt1[pb][:, 64:96].unsqueeze(1)
                ssb2 = rt1[pb][:, 96:128].unsqueeze(1)
                S.op("dve", lambda e, qpv=qpv, ccb=ccb: e.tensor_tensor(out=qA, in0=qpv, in1=ccb.to_broadcast([128, 8, 64]), op=ALU.mult), r=rqp + [r_rt1[pb]], w=[r_qpet])
                S.op("dve", lambda e, qpv=qpv, ssb1=ssb1: e.tensor_tensor(out=qB[:, :, 0:32], in0=qpv[:, :, 32:64], in1=ssb1.to_broadcast([128, 8, 32]), op=ALU.mult), r=rqp + [r_rt1[pb]], w=[r_qpet])
                S.op("dve", lambda e, qpv=qpv, ssb2=ssb2: e.tensor_tensor(out=qB[:, :, 32:64], in0=qpv[:, :, 0:32], in1=ssb2.to_broadcast([128, 8, 32]), op=ALU.mult), r=rqp + [r_rt1[pb]], w=[r_qpet])
                S.op("pool", lambda e, pb=pb: e.tensor_tensor(out=qpe_b[pb], in0=qpe_t[:, 0:512], in1=qpe_t[:, 512:1024], op=ALU.add), r=[r_qpet], w=[r_qpeb[pb]])
                btq, rtq = ring.get(1)
                tqv = bankbf(btq)[:, 0:512].rearrange("p (a t) -> p a t", a=4)
                for a in range(4):
                    S.op("pe", lambda e, a=a, tqv=tqv, pb=pb: e.transpose(out=tqv[:, a, :], in_=qpe_b[pb][:, a * 128:(a + 1) * 128], identity=ident_b[:]), r=[r_qpeb[pb], r_idb], w=rtq)
                qz = qpeT[pb].rearrange("p (a two) t -> p a two t", two=2)
                S.op("act", lambda e, tqv=tqv, qz=qz: e.copy(out=qz[0:64, :, 0, :], in_=tqv[0:64]), r=rtq, w=[r_qpeT[pb]])
                S.op("dve", lambda e, tqv=tqv, qz=qz: e.tensor_copy(out=qz[64:128, :, 1, :], in_=tqv[64:128]), r=rtq, w=[r_qpeT[pb]])
                for hh in range(4):
                    bql, rql = ring.get(1)
                    qlv = bank(bql).rearrange("p (h k t) -> p h k t", h=2, k=2)
                    for h2 in range(2):
                        h = hh * 2 + h2
                        for k in range(2):
                            S.op("pe", lambda e, qlv=qlv, h2=h2, h=h, k=k, pb=pb: e.matmul(qlv[:, h2, k, :], lhsT=wukT_sb[:, h, k * 128:(k + 1) * 128], rhs=qnT[pb][:, h, :], start=True, stop=True), r=[r_wukT, r_qnT[pb]], w=rql)
                    evac("act" if hh % 2 == 0 else "dve", qlT[pb][:, hh * 2:hh * 2 + 2, :, :], qlv, rql, [r_qlT[pb]])

            l1ctx = {}
            def l1P1a(t, h):
                pb = t % 2
                hp = h % 2
                nk = (t + 1) * 128
                nb = (nk + 511) // 512
                nbanks = 1 if nb == 1 else (2 if nb == 2 else 4)
                bs, rs = ring.get(nbanks)
                sfull = bank(bs, nbanks)
                stt = st1[hp]
                chs = [(c0_, min(nk, c0_ + 1024)) for c0_ in range(0, nk, 1024)]
                for ci, (a_, b_) in enumerate(chs):
                    for gi in range(a_ // 512, (b_ + 511) // 512):
                        k0 = gi * 512
                        kn = min(512, nk - k0)
                        ks = slice(k0, k0 + kn)
                        kres = r_ckvT[gi * 4:gi * 4 + (kn + 127) // 128]
                        kpres = r_kpeT[gi * 4:gi * 4 + (kn + 127) // 128]
                        last_diag = (gi == nb - 1)
                        parts = [(ks, False)] if not last_diag else ([(slice(k0, nk - 128), False)] if nk - 128 > k0 else []) + [(slice(nk - 128, nk), True)]
                        for (ks, dg) in parts:
                            S.op("pe", lambda e, sfull=sfull, ks=ks, h=h, pb=pb: e.matmul(sfull[:, ks], lhsT=qlT[pb][:, h, 0, :], rhs=ckvT[:, 0, ks], start=True, stop=False), r=[r_qlT[pb]] + kres, w=[rs[gi]])
                            S.op("pe", lambda e, sfull=sfull, ks=ks, h=h, pb=pb: e.matmul(sfull[:, ks], lhsT=qlT[pb][:, h, 1, :], rhs=ckvT[:, 1, ks], start=False, stop=False), r=[r_qlT[pb]] + kres, w=[rs[gi]])
                            S.op("pe", lambda e, sfull=sfull, ks=ks, h=h, pb=pb, dg=dg: e.matmul(sfull[:, ks], lhsT=qpeT[pb][:, h, :], rhs=kpeT[:, ks], start=False, stop=(not dg)), r=[r_qpeT[pb]] + kpres, w=[rs[gi]])
                            if dg:
                                S.op("pe", lambda e, sfull=sfull, ks=ks: e.matmul(sfull[:, ks], lhsT=ident_b[:], rhs=maskd[:], start=False, stop=True), r=[r_idb, r_maskd], w=[rs[gi]])
                    S.op("dve", lambda e, sfull=sfull, a_=a_, b_=b_, ci=ci, stt=stt: e.reduce_max(out=stt[:, 4 + ci:5 + ci], in_=sfull[:, a_:b_], axis=AX.X), r=rs[a_ // 512:(b_ + 511) // 512], w=[r_st1[hp]])
                if len(chs) == 2:
                    S.op("dve", lambda e, stt=stt: e.tensor_tensor(out=stt[:, 4:5], in0=stt[:, 4:5], in1=stt[:, 5:6], op=ALU.max), r=[r_st1[hp]], w=[r_st1[hp]])
                S.op("dve", lambda e, stt=stt: e.tensor_scalar(out=stt[:, 1:2], in0=stt[:, 4:5], scalar1=-MLA_SCALE, scalar2=None, op0=ALU.mult), r=[r_st1[hp]], w=[r_st1[hp]])
                l1ctx[(t, h)] = (sfull, rs, nb, chs)

            def l1P1b(t, h):
                pb = t % 2
                hp = h % 2
                nk = (t + 1) * 128
                stt = st1[hp]
                sfull, rs, nb, chs = l1ctx[(t, h)]
                for ci, (a_, b_) in enumerate(chs):
                    S.op("act", lambda e, sfull=sfull, a_=a_, b_=b_, ci=ci, stt=stt, hp=hp: e.activation(out=p_sb[hp][:, a_:b_], in_=sfull[:, a_:b_], func=AF.Exp, bias=stt[:, 1:2], scale=MLA_SCALE, accum_out=stt[:, 6 + ci:7 + ci]), r=rs[0:nb] + [r_st1[hp]], w=[r_p[hp], r_st1[hp]])

            def l1P2a(t, h):
                pb = t % 2
                hp = h % 2
                nk = (t + 1) * 128
                stt = st1[hp]
                nkt = t + 1
                for g0 in range(0, nkt, 4):
                    bpt, rpt = ring.get(1)
                    ptv = bankbf(bpt)
                    n_ = min(4, nkt - g0)
                    for i_ in range(n_):
                        S.op("pe", lambda e, i_=i_, g0=g0, ptv=ptv, hp=hp: e.transpose(out=ptv[:, i_ * 128:(i_ + 1) * 128], in_=p_sb[hp][:, (g0 + i_) * 128:(g0 + i_ + 1) * 128], identity=ident_b[:]), r=[r_p[hp], r_idb], w=rpt)
                    evac("act" if (g0 // 4) % 2 == 0 else "dve", pT_sb[hp][:, g0 * 128:(g0 + n_) * 128], ptv[:, 0:n_ * 128], rpt, [r_pT[hp]])

            def l1P2b(t, h):
                pb = t % 2
                hp = h % 2
                nk = (t + 1) * 128
                stt = st1[hp]
                nkt = t + 1
                bol, rol = ring.get(1)
                for kt in range(nkt):
                    S.op("pe", lambda e, kt=kt, bol=bol, hp=hp: e.matmul(bank(bol)[:, 0:256], lhsT=pT_sb[hp][:, kt * 128:(kt + 1) * 128], rhs=ckvK[:, kt, :], start=(kt == 0), stop=(kt == nkt - 1)), r=[r_pT[hp], r_ckvK[kt]], w=rol)
                chs = [(c0_, min(nk, c0_ + 1024)) for c0_ in range(0, nk, 1024)]
                if len(chs) == 2:
                    S.op("dve", lambda e, stt=stt: e.tensor_tensor(out=stt[:, 6:7], in0=stt[:, 6:7], in1=stt[:, 7:8], op=ALU.add), r=[r_st1[hp]], w=[r_st1[hp]])
                S.op("dve", lambda e, stt=stt: e.reciprocal(out=stt[:, 3:4], in_=stt[:, 6:7]), r=[r_st1[hp]], w=[r_st1[hp]])
                S.op("dve", lambda e, bol=bol, stt=stt, hp=hp: e.tensor_scalar(out=ol_b[hp], in0=bank(bol)[:, 0:256], scalar1=stt[:, 3:4], scalar2=None, op0=ALU.mult), r=rol + [r_st1[hp]], w=[r_olb[hp]])
                bot, rot = ring.get(1)
                otv = bankbf(bot)[:, 0:256].rearrange("p (k t) -> p k t", k=2)
                for k in range(2):
                    S.op("pe", lambda e, k=k, otv=otv, hp=hp: e.transpose(out=otv[:, k, :], in_=ol_b[hp][:, k * 128:(k + 1) * 128], identity=ident_b[:]), r=[r_olb[hp], r_idb], w=rot)
                S.op("dve", lambda e, otv=otv, hp=hp: e.tensor_copy(out=olT[hp], in_=otv), r=rot, w=[r_olT[hp]])
                boh, roh = ring.get(1)
                for k in range(2):
                    S.op("pe", lambda e, k=k, boh=boh, h=h, hp=hp: e.matmul(bank(boh)[:, 0:128], lhsT=wuv_sb[:, k, h * 128:(h + 1) * 128], rhs=olT[hp][:, k, :], start=(k == 0), stop=(k == 1)), r=[r_wuv, r_olT[hp]], w=roh)
                evac("act" if h % 2 == 0 else "dve", oT1[pb][:, h, :], bank(boh)[:, 0:128], roh, [r_oT1[pb]])


            def l1out(t):
                pb = t % 2
                by, ry = ring.get(2)
                for hf in range(2):
                    for c in range(8):
                        S.op("pe", lambda e, hf=hf, c=c, by=by, pb=pb: e.matmul(bank(by + hf), lhsT=oT1[pb][:, c, :], rhs=wo1_sb[:, c, hf * 512:(hf + 1) * 512], start=(c == 0), stop=(c == 7)), r=[r_oT1[pb], r_wo1], w=[ry[hf]])
                S.op("dve", lambda e, by=by, pb=pb, t=t: e.scalar_tensor_tensor(out=z1[pb], in0=X[:, t, :], scalar=ALPHA, in1=bank(by, 2), op0=ALU.mult, op1=ALU.add), r=[rX[t]] + ry, w=[r_z1[pb]])
                layer_norm_tile(z1[pb], r_z1[pb], X[:, t, :], rX[t], 128, sc1[pb], r_sc1[pb], lnt1, r_lnt1)


            l1A(0)
            for t in range(NT):
                l1P1a(t, 0)
                l1P1b(t, 0)
                for h in range(1, 8):
                    l1P2a(t, h - 1)
                    l1P1a(t, h)
                    l1P1b(t, h)
                    l1P2b(t, h - 1)
                    if h == 4 and t + 1 < NT:
                        l1A(t + 1)
                l1P2a(t, 7)
                l1P2b(t, 7)
                l1out(t)
                chk(f'l1t{t}')

            def sample_l1():
                A.at(L1_BUF)
                NSTEP = NPAGES * 128 // 1024
                tb1, r_tb1 = A.get(128, F32, "tb1")
                cc1s = tb1[0:NS, 0:64]; ss1s = tb1[0:NS, 64:128]
                S.dma("sp", lambda e: e.dma_start(out=cc1s, in_=tdr["cc1s"]), w=[r_tb1])
                S.dma("sp", lambda e: e.dma_start(out=ss1s, in_=tdr["ss1s"]), w=[r_tb1])
                msk, r_msk = A.get(SB * NS, BF16, "msk")
                S.dma("pool", lambda e: e.dma_start(out=msk[0:32, :], in_=tdr["masksb"]), w=[r_msk])
                mskv = msk.rearrange("p (b t) -> p b t", b=SB)
                pt_sb, r_pt = A.get(SB, I32, "pt_sb")
                S.dma("sp", lambda e: e.dma_start(out=pt_sb, in_=ptab), w=[r_pt])
                xsT1, r_xsT1 = A.get(128, BF16, "xsT1"); xsT1 = xsT1.rearrange("p (k t) -> p k t", k=8)
                transpose_x(XS[:], r_XS, xsT1, r_xsT1, npart=NS)
                bcq, rcq = ring.get(1)
                bkv, rkv = ring.get(1)
                for k in range(8):
                    S.op("pe", lambda e, k=k: e.matmul(bank(bcq)[0:NS, :], lhsT=xsT1[:, k, :], rhs=wdq_sb[:, k, :], start=(k == 0), stop=(k == 7)), r=[r_xsT1, r_wdq], w=rcq)
                for k in range(8):
                    S.op("pe", lambda e, k=k: e.matmul(bank(bkv)[0:NS, 0:320], lhsT=xsT1[:, k, :], rhs=wdkv_sb[:, k, :], start=(k == 0), stop=(k == 7)), r=[r_xsT1, r_wdkv], w=rkv)
                jk, r_jk = A.get(512, F32, "jk")
                scq, r_scq = A.get(16, F32, "scq")
                cqns, r_cqns = A.get(512, BF16, "cqns")
                ckvsf, r_ckvsf = A.get(256, F32, "ckvsf")
                ckvsb, r_ckvsb = A.get(256, BF16, "ckvsb")
                kpet, r_kpets = A.get(128, F32, "kpets")
                kpesf, r_kpesf = A.get(64, F32, "kpesf")
                kpesb, r_kpesb = A.get(128, BF16, "kpesb")
                sc = scq[0:NS]
                S.op("act", lambda e: e.activation(out=jk[0:NS, :], in_=bank(bcq)[0:NS, :], func=AF.Square, accum_out=sc[:, 0:1]), r=rcq, w=[r_jk, r_scq])
                S.op("act", lambda e: e.activation(out=sc[:, 1:2], in_=sc[:, 0:1], func=AF.Sqrt, bias=eps_rms[0:NS], scale=1.0 / 512), r=[r_scq, r_eps], w=[r_scq])
                S.op("dve", lambda e: e.reciprocal(out=sc[:, 1:2], in_=sc[:, 1:2]), r=[r_scq], w=[r_scq])
                S.op("dve", lambda e: e.scalar_tensor_tensor(out=cqns[0:NS, :], in0=bank(bcq)[0:NS, :], scalar=sc[:, 1:2], in1=qng[0:NS], op0=ALU.mult, op1=ALU.mult), r=rcq + [r_scq, r_c1], w=[r_cqns])
                S.op("act", lambda e: e.activation(out=jk[0:NS, 0:256], in_=bank(bkv)[0:NS, 0:256], func=AF.Square, accum_out=sc[:, 2:3]), r=rkv, w=[r_jk, r_scq])
                S.op("act", lambda e: e.activation(out=sc[:, 3:4], in_=sc[:, 2:3], func=AF.Sqrt, bias=eps_rms[0:NS], scale=1.0 / 256), r=[r_scq, r_eps], w=[r_scq])
                S.op("dve", lambda e: e.reciprocal(out=sc[:, 3:4], in_=sc[:, 3:4]), r=[r_scq], w=[r_scq])
                S.op("dve", lambda e: e.scalar_tensor_tensor(out=ckvsf[0:NS, :], in0=bank(bkv)[0:NS, 0:256], scalar=sc[:, 3:4], in1=kvng[0:NS], op0=ALU.mult, op1=ALU.mult), r=rkv + [r_scq, r_c1], w=[r_ckvsf])
                out_toks.append(S.dma("sp", lambda e: e.dma_start(out=ckvs, in_=ckvsf[0:NS, :]), r=[r_ckvsf]))
                S.op("pool", lambda e: e.tensor_copy(out=ckvsb[0:NS, :], in_=ckvsf[0:NS, :]), r=[r_ckvsf], w=[r_ckvsb])
                S.op("dve", lambda e: e.tensor_tensor(out=kpet[0:NS, 0:64], in0=bank(bkv)[0:NS, 256:320], in1=cc1s, op=ALU.mult), r=rkv + [r_tb1], w=[r_kpets])
                S.op("dve", lambda e: e.tensor_tensor(out=kpet[0:NS, 64:96], in0=bank(bkv)[0:NS, 288:320], in1=ss1s[:, 0:32], op=ALU.mult), r=rkv + [r_tb1], w=[r_kpets])
                S.op("dve", lambda e: e.tensor_tensor(out=kpet[0:NS, 96:128], in0=bank(bkv)[0:NS, 256:288], in1=ss1s[:, 32:64], op=ALU.mult), r=rkv + [r_tb1], w=[r_kpets])
                S.op("pool", lambda e: e.tensor_tensor(out=kpesf[0:NS, :], in0=kpet[0:NS, 0:64], in1=kpet[0:NS, 64:128], op=ALU.add), r=[r_kpets], w=[r_kpesf])
                out_toks.append(S.dma("sp", lambda e: e.dma_start(out=kpes, in_=kpesf[0:NS, :]), r=[r_kpesf]))
                S.op("pool", lambda e: e.tensor_copy(out=kpesb[0:NS, 0:64], in_=kpesf[0:NS, :]), r=[r_kpesf], w=[r_kpesb])
                S.op("pool", lambda e: e.tensor_copy(out=kpesb[0:NS, 64:128], in_=kpesf[0:NS, :]), r=[r_kpesf], w=[r_kpesb])
                btr, rtr = ring.get(1)
                trv = bankbf(btr)
                for k in range(4):
                    S.op("pe", lambda e, k=k: e.transpose(out=trv[:, k * 16:(k + 1) * 16], in_=cqns[0:NS, k * 128:(k + 1) * 128], identity=ident_b[0:NS, 0:NS]), r=[r_cqns, r_idb], w=rtr)
                for k in range(2):
                    S.op("pe", lambda e, k=k: e.transpose(out=trv[:, 64 + k * 16:64 + (k + 1) * 16], in_=ckvsb[0:NS, k * 128:(k + 1) * 128], identity=ident_b[0:NS, 0:NS]), r=[r_ckvsb, r_idb], w=rtr)
                S.op("pe", lambda e: e.transpose(out=trv[:, 96:112], in_=kpesb[0:NS, :], identity=ident_b[0:NS, 0:NS]), r=[r_kpesb, r_idb], w=rtr)
                trs, r_trs = A.get(112, BF16, "trs")
                S.op("act", lambda e: e.copy(out=trs, in_=trv[:, 0:112]), r=rtr, w=[r_trs])
                cqnTs = trs[:, 0:64].rearrange("p (k t) -> p k t", k=4)
                ckvTs = trs[:, 64:96].rearrange("p (k t) -> p k t", k=2)
                kpeTs = trs[:, 96:112]
                bqn, rqn = ring.get(1)
                qnv = bank(bqn)[:, 0:128].rearrange("p (h t) -> p h t", h=8)
                for h in range(8):
                    for k in range(4):
                        S.op("pe", lambda e, h=h, k=k: e.matmul(qnv[:, h, :], lhsT=wuq_sb[:, k, h * 128:(h + 1) * 128], rhs=cqnTs[:, k, :], start=(k == 0), stop=(k == 3)), r=[r_wuq, r_trs], w=rqn)
                qnTs, r_qnTs = A.get(128, BF16, "qnTs"); qnTs = qnTs.rearrange("p (h t) -> p h t", h=8)
                S.op("act", lambda e: e.copy(out=qnTs, in_=qnv), r=rqn, w=[r_qnTs])
                bqp, rqp = ring.get(1)
                for k in range(4):
                    S.op("pe", lambda e, k=k: e.matmul(bank(bqp)[0:NS, :], lhsT=cqnTs[:, k, :], rhs=wuq_sb[:, k, 1024:1536], start=(k == 0), stop=(k == 3)), r=[r_wuq, r_trs], w=rqp)
                qpv = bank(bqp)[0:NS, :].rearrange("p (h d) -> p h d", h=8)
                qpt, r_qpt = A.get(1024, F32, "qpt")
                qA = qpt[0:NS, 0:512].rearrange("p (h d) -> p h d", h=8)
                qB = qpt[0:NS, 512:1024].rearrange("p (h d) -> p h d", h=8)
                S.op("dve", lambda e: e.tensor_tensor(out=qA, in0=qpv, in1=cc1s.unsqueeze(1).to_broadcast([NS, 8, 64]), op=ALU.mult), r=rqp + [r_tb1], w=[r_qpt])
                S.op("dve", lambda e: e.tensor_tensor(out=qB[:, :, 0:32], in0=qpv[:, :, 32:64], in1=ss1s[:, 0:32].unsqueeze(1).to_broadcast([NS, 8, 32]), op=ALU.mult), r=rqp + [r_tb1], w=[r_qpt])
                S.op("dve", lambda e: e.tensor_tensor(out=qB[:, :, 32:64], in0=qpv[:, :, 0:32], in1=ss1s[:, 32:64].unsqueeze(1).to_broadcast([NS, 8, 32]), op=ALU.mult), r=rqp + [r_tb1], w=[r_qpt])
                qpsb, r_qpsb = A.get(512, BF16, "qpsb")
                S.op("pool", lambda e: e.tensor_tensor(out=qpsb[0:NS, :], in0=qpt[0:NS, 0:512], in1=qpt[0:NS, 512:1024], op=ALU.add), r=[r_qpt], w=[r_qpsb])
                btq, rtq = ring.get(1)
                tqv = bankbf(btq)[:, 0:64].rearrange("p (a t) -> p a t", a=4)
                for a in range(4):
                    S.op("pe", lambda e, a=a: e.transpose(out=tqv[:, a, :], in_=qpsb[0:NS, a * 128:(a + 1) * 128], identity=ident_b[0:NS, 0:NS]), r=[r_qpsb, r_idb], w=rtq)
                qpTs, r_qpTs = A.get(SB * 32, BF16, "qpTs")
                S.op("pool", lambda e: e.memset(qpTs, 0.0), w=[r_qpTs])
                qpT5 = qpTs.rearrange("p (b a two l) -> p b a two l", b=4, a=4, two=2)
                tq4 = tqv.rearrange("p a (b l) -> p a b l", b=4)
                for bb in range(SB):
                    S.op("act", lambda e, bb=bb: e.copy(out=qpT5[0:64, bb, :, 0, :], in_=tq4[0:64, :, bb, :]), r=rtq, w=[r_qpTs])
                    S.op("dve", lambda e, bb=bb: e.tensor_copy(out=qpT5[64:128, bb, :, 1, :], in_=tq4[64:128, :, bb, :]), r=rtq, w=[r_qpTs])
                qpTb = qpTs.rearrange("p (b n) -> p b n", b=4)
                bql, rql = ring.get(1)
                qlv = bank(bql)[:, 0:256].rearrange("p (h k t) -> p h k t", h=8, k=2)
                for h in range(8):
                    for k in range(2):
                        S.op("pe", lambda e, h=h, k=k: e.matmul(qlv[:, h, k, :], lhsT=wukT_sb[:, h, k * 128:(k + 1) * 128], rhs=qnTs[:, h, :], start=True, stop=True), r=[r_wukT, r_qnTs], w=rql)
                qlTs, r_qlTs = A.get(2 * SB * 32, BF16, "qlTs")
                ql5 = qlTs.rearrange("p (k b h l) -> p k b h l", k=2, b=4, h=8)
                qlv5 = qlv.rearrange("p h k (b l) -> p h k b l", b=4)
                for k in range(2):
                    for bb in range(SB):
                        S.op("act" if (k + bb) % 2 == 0 else "dve",
                             (lambda e, k=k, bb=bb: e.copy(out=ql5[:, k, bb], in_=qlv5[:, :, k, bb, :])) if (k + bb) % 2 == 0 else
                             (lambda e, k=k, bb=bb: e.tensor_copy(out=ql5[:, k, bb], in_=qlv5[:, :, k, bb, :])), r=rql, w=[r_qlTs])
                qlTb = qlTs.rearrange("p (k b n) -> p k b n", k=2, b=4)
                gk = []; r_gk = []; gp = []; r_gp = []
                for i_ in range(3):
                    v_, r_ = A.get(2048, BF16, f"gk{i_}"); gk.append(v_.rearrange("p (s c) -> p s c", s=8)); r_gk.append(r_)
                    v_, r_ = A.get(512, BF16, f"gp{i_}"); gp.append(v_.rearrange("p (s c) -> p s c", s=8)); r_gp.append(r_)
                gp2, r_gp2 = dbl(1024, BF16, "gp2", "p (s c) -> p s c", s=8)
                ckTc, r_ckTc = dbl(2048, BF16, "ckTc", "p (k n) -> p k n", k=2)
                kpTc, r_kpTc = dbl(1024, BF16, "kpTc")
                p_s, r_ps = dbl(1024, BF16, "p_s")
                pTc, r_pTc = dbl(256, BF16, "pTc", "p (i t) -> p i t", i=8)
                Oacc, r_Oacc = dbl(256, F32, "Oacc")
                ms, r_ms = dbl(8, F32, "ms")
                olb, r_olb = A.get(256, BF16, "olb")
                olTs, r_olTs = A.get(64, BF16, "olTs"); olTs = olTs.rearrange("p (k n) -> p k n", k=2)
                oTs, r_oTs = A.get(128, BF16, "oTs"); oTs = oTs.rearrange("p (h t) -> p h t", h=8)
                print("sample l1 arena bytes", A.cur)
                sctx = {}
                def sI(bb):
                    mb_ = ms[bb % 2][0:32]
                    oa = Oacc[bb % 2][0:32]
                    r_m = r_ms[bb % 2]
                    r_o = r_Oacc[bb % 2]
                    bsn, rsn = ring.get(1)
                    sn = bank(bsn)[0:32, 0:NS]
                    S.op("pe", lambda e, bb=bb, sn=sn: e.matmul(sn, lhsT=qlTb[:, 0, bb, :], rhs=ckvTs[:, 0, :], start=True, stop=False), r=[r_qlTs, r_trs], w=rsn)
                    S.op("pe", lambda e, bb=bb, sn=sn: e.matmul(sn, lhsT=qlTb[:, 1, bb, :], rhs=ckvTs[:, 1, :], start=False, stop=False), r=[r_qlTs, r_trs], w=rsn)
                    S.op("pe", lambda e, bb=bb, sn=sn: e.matmul(sn, lhsT=qpTb[:, bb, :], rhs=kpeTs, start=False, stop=False), r=[r_qpTs, r_trs], w=rsn)
                    S.op("pe", lambda e, bb=bb, sn=sn: e.matmul(sn, lhsT=ident_b[0:32, 0:32], rhs=mskv[0:32, bb, :], start=False, stop=True), r=[r_idb, r_msk], w=rsn)
                    S.op("dve", lambda e, sn=sn, mb_=mb_: e.reduce_max(out=mb_[:, 2:3], in_=sn, axis=AX.X), r=rsn, w=[r_m])
                    S.op("dve", lambda e, mb_=mb_: e.tensor_scalar(out=mb_[:, 0:1], in0=mb_[:, 2:3], scalar1=-MLA_SCALE, scalar2=None, op0=ALU.mult), r=[r_m], w=[r_m])
                    psl = p_s[0][0:32]
                    S.op("act", lambda e, sn=sn, mb_=mb_, psl=psl: e.activation(out=psl[:, 0:NS], in_=sn, func=AF.Exp, bias=mb_[:, 0:1], scale=MLA_SCALE, accum_out=mb_[:, 1:2]), r=rsn + [r_m], w=[r_ps[0], r_m])
                    bpt, rpt = ring.get(1)
                    S.op("pe", lambda e, psl=psl, bpt=bpt: e.transpose(out=bankbf(bpt)[0:NS, 0:32], in_=psl[:, 0:NS], identity=ident_b[0:32, 0:32]), r=[r_ps[0], r_idb], w=rpt)
                    S.op("act", lambda e, bpt=bpt: e.copy(out=pTc[0][0:NS, 0, :], in_=bankbf(bpt)[0:NS, 0:32]), r=rpt, w=[r_pTc[0]])
                    bo_, ro_ = ring.get(1)
                    S.op("pe", lambda e, bo_=bo_: e.matmul(bank(bo_)[0:32, 0:256], lhsT=pTc[0][0:NS, 0, :], rhs=ckvsb[0:NS, :], start=True, stop=True), r=[r_pTc[0], r_ckvsb], w=ro_)
                    S.op("dve", lambda e, bo_=bo_, oa=oa: e.tensor_copy(out=oa, in_=bank(bo_)[0:32, 0:256]), r=ro_, w=[r_o])

                def sG(bb, stp, g3):
                    S.dma("pool", lambda e, g3=g3, bb=bb, stp=stp: e.indirect_dma_start(out=gk[g3].rearrange("p s c -> p (s c)"), out_offset=None, in_=cckv[:, :], element_offset=stp * 2048,
                                                                                         in_offset=bass.IndirectOffsetOnAxis(ap=pt_sb[:, bb:bb + 1], axis=0)), r=[r_pt], w=[r_gk[g3]])
                    S.dma("pool", lambda e, g3=g3, bb=bb, stp=stp: e.indirect_dma_start(out=gp[g3].rearrange("p s c -> p (s c)"), out_offset=None, in_=ckpe[:, :], element_offset=stp * 512,
                                                                                         in_offset=bass.IndirectOffsetOnAxis(ap=pt_sb[:, bb:bb + 1], axis=0)), r=[r_pt], w=[r_gp[g3]])

                def sD(bb, stp, gs, g3):
                    S.op("act", lambda e, gs=gs, g3=g3: e.copy(out=gp2[gs][:, :, 0:64], in_=gp[g3]), r=[r_gp[g3]], w=[r_gp2[gs]])
                    S.op("dve", lambda e, gs=gs, g3=g3: e.tensor_copy(out=gp2[gs][:, :, 64:128], in_=gp[g3]), r=[r_gp[g3]], w=[r_gp2[gs]])

                def sT(bb, stp, gs, g3):
                    for hf in range(2):
                        sl = hf * 4
                        bA, rA = ring.get(1)
                        bB, rB = ring.get(1)
                        bC, rC = ring.get(1)
                        for i_ in range(4):
                            S.op("pe", lambda e, i_=i_, sl=sl, g3=g3, bA=bA: e.transpose(out=bankbf(bA)[:, i_ * 128:(i_ + 1) * 128], in_=gk[g3][:, sl + i_, 0:128], identity=ident_b[:]), r=[r_gk[g3], r_idb], w=rA)
                        for i_ in range(4):
                            S.op("pe", lambda e, i_=i_, sl=sl, g3=g3, bB=bB: e.transpose(out=bankbf(bB)[:, i_ * 128:(i_ + 1) * 128], in_=gk[g3][:, sl + i_, 128:256], identity=ident_b[:]), r=[r_gk[g3], r_idb], w=rB)
                        for i_ in range(4):
                            S.op("pe", lambda e, i_=i_, sl=sl, gs=gs, bC=bC: e.transpose(out=bankbf(bC)[:, i_ * 128:(i_ + 1) * 128], in_=gp2[gs][:, sl + i_, :], identity=ident_b[:]), r=[r_gp2[gs], r_idb], w=rC)
                        cs_ = slice(hf * 512, (hf + 1) * 512)
                        S.op("act", lambda e, gs=gs, bA=bA, cs_=cs_: e.copy(out=ckTc[gs][:, 0, cs_], in_=bankbf(bA)[:, 0:512]), r=rA, w=[r_ckTc[gs]])
                        S.op("dve", lambda e, gs=gs, bB=bB, cs_=cs_: e.tensor_copy(out=ckTc[gs][:, 1, cs_], in_=bankbf(bB)[:, 0:512]), r=rB, w=[r_ckTc[gs]])
                        if hf == 0:
                            S.op("act", lambda e, gs=gs, bC=bC, cs_=cs_: e.copy(out=kpTc[gs][:, cs_], in_=bankbf(bC)[:, 0:512]), r=rC, w=[r_kpTc[gs]])
                        else:
                            S.op("dve", lambda e, gs=gs, bC=bC, cs_=cs_: e.tensor_copy(out=kpTc[gs][:, cs_], in_=bankbf(bC)[:, 0:512]), r=rC, w=[r_kpTc[gs]])

                def sU1(bb, stp, gs):
                    mb_ = ms[bb % 2][0:32]
                    oa = Oacc[bb % 2][0:32]
                    r_m = r_ms[bb % 2]
                    r_o = r_Oacc[bb % 2]
                    bs2, rs2 = ring.get(2)
                    sfull = bank(bs2, 2)[0:32, :]
                    for hf in range(2):
                        cs_ = slice(hf * 512, (hf + 1) * 512)
                        S.op("pe", lambda e, bb=bb, gs=gs, cs_=cs_, sfull=sfull: e.matmul(sfull[:, cs_], lhsT=qlTb[:, 0, bb, :], rhs=ckTc[gs][:, 0, cs_], start=True, stop=False), r=[r_qlTs, r_ckTc[gs]], w=[rs2[hf]])
                        S.op("pe", lambda e, bb=bb, gs=gs, cs_=cs_, sfull=sfull: e.matmul(sfull[:, cs_], lhsT=qlTb[:, 1, bb, :], rhs=ckTc[gs][:, 1, cs_], start=False, stop=False), r=[r_qlTs, r_ckTc[gs]], w=[rs2[hf]])
                        S.op("pe", lambda e, bb=bb, gs=gs, cs_=cs_, sfull=sfull: e.matmul(sfull[:, cs_], lhsT=qpTb[:, bb, :], rhs=kpTc[gs][:, cs_], start=False, stop=True), r=[r_qpTs, r_kpTc[gs]], w=[rs2[hf]])
                    cur = 0 if stp % 2 == 0 else 3
                    nw = 3 if stp % 2 == 0 else 0
                    S.op("dve", lambda e, sfull=sfull, mb_=mb_: e.reduce_max(out=mb_[:, 2:3], in_=sfull, axis=AX.X), r=rs2, w=[r_m])
                    S.op("dve", lambda e, mb_=mb_, cur=cur, nw=nw: e.tensor_scalar(out=mb_[:, nw:nw + 1], in0=mb_[:, 2:3], scalar1=-MLA_SCALE, scalar2=mb_[:, cur:cur + 1], op0=ALU.mult, op1=ALU.min), r=[r_m], w=[r_m])
                    pq = gs
                    psl = p_s[pq][0:32]
                    S.op("act", lambda e, sfull=sfull, mb_=mb_, psl=psl, nw=nw: e.activation(out=psl, in_=sfull, func=AF.Exp, bias=mb_[:, nw:nw + 1], scale=MLA_SCALE, accum_out=mb_[:, 6:7]), r=rs2 + [r_m], w=[r_ps[pq], r_m])
                    S.op("act", lambda e, mb_=mb_, cur=cur, nw=nw: e.activation(out=mb_[:, 4:5], in_=mb_[:, cur:cur + 1], func=AF.Exp, bias=mb_[:, nw:nw + 1], scale=-1.0), r=[r_m], w=[r_m])
                    S.op("dve", lambda e, mb_=mb_: e.scalar_tensor_tensor(out=mb_[:, 1:2], in0=mb_[:, 1:2], scalar=mb_[:, 4:5], in1=mb_[:, 6:7], op0=ALU.mult, op1=ALU.add), r=[r_m], w=[r_m])

                def sU2(bb, stp, gs, g3):
                    mb_ = ms[bb % 2][0:32]
                    oa = Oacc[bb % 2][0:32]
                    r_m = r_ms[bb % 2]
                    r_o = r_Oacc[bb % 2]
                    pq = gs
                    psl = p_s[pq][0:32]
                    bpt, rpt = ring.get(1)
                    ptv = bankbf(bpt)[:, 0:256].rearrange("p (i t) -> p i t", i=8)
                    for i_ in range(8):
                        S.op("pe", lambda e, i_=i_, psl=psl, ptv=ptv: e.transpose(out=ptv[:, i_, :], in_=psl[:, i_ * 128:(i_ + 1) * 128], identity=ident_b[0:32, 0:32]), r=[r_ps[pq], r_idb], w=rpt)
                    S.op("act", lambda e, ptv=ptv, pq=pq: e.copy(out=pTc[pq], in_=ptv), r=rpt, w=[r_pTc[pq]])
                    bo_, ro_ = ring.get(1)
                    for i_ in range(8):
                        S.op("pe", lambda e, i_=i_, bo_=bo_, pq=pq, g3=g3: e.matmul(bank(bo_)[0:32, 0:256], lhsT=pTc[pq][:, i_, :], rhs=gk[g3][:, i_, :], start=(i_ == 0), stop=(i_ == 7)), r=[r_pTc[pq], r_gk[g3]], w=ro_)
                    S.op("dve", lambda e, bo_=bo_, oa=oa, mb_=mb_: e.scalar_tensor_tensor(out=oa, in0=oa, scalar=mb_[:, 4:5], in1=bank(bo_)[0:32, 0:256], op0=ALU.mult, op1=ALU.add), r=ro_ + [r_o, r_m], w=[r_o])


                def sF(bb):
                    mb_ = ms[bb % 2][0:32]
                    oa = Oacc[bb % 2][0:32]
                    r_m = r_ms[bb % 2]
                    r_o = r_Oacc[bb % 2]
                    S.op("dve", lambda e, mb_=mb_: e.reciprocal(out=mb_[:, 7:8], in_=mb_[:, 1:2]), r=[r_m], w=[r_m])
                    S.op("act", lambda e, oa=oa, mb_=mb_: e.activation(out=olb[0:32, :], in_=oa, func=AF.Identity, bias=0.0, scale=mb_[:, 7:8]), r=[r_o, r_m], w=[r_olb])
                    bot, rot = ring.get(1)
                    otv = bankbf(bot)[:, 0:64].rearrange("p (k n) -> p k n", k=2)
                    for k in range(2):
                        S.op("pe", lambda e, k=k, otv=otv: e.transpose(out=otv[:, k, :], in_=olb[0:32, k * 128:(k + 1) * 128], identity=ident_b[0:32, 0:32]), r=[r_olb, r_idb], w=rot)
                    S.op("dve", lambda e, otv=otv: e.tensor_copy(out=olTs, in_=otv), r=rot, w=[r_olTs])
                    boh, roh = ring.get(1)
                    ohv = bank(boh)[:, 0:32].rearrange("p (h l) -> p h l", h=8)
                    for h in range(8):
                        for k in range(2):
                            S.op("pe", lambda e, h=h, k=k, ohv=ohv: e.matmul(ohv[:, h, :], lhsT=wuv_sb[:, k, h * 128:(h + 1) * 128], rhs=olTs[:, k, h * 4:(h + 1) * 4], start=(k == 0), stop=(k == 1)), r=[r_wuv, r_olTs], w=roh)
                    S.op("act", lambda e, ohv=ohv, bb=bb: e.copy(out=oTs[:, :, bb * 4:bb * 4 + 4], in_=ohv), r=roh, w=[r_oTs])

                NTOT = SB * NSTEP
                sG(0, 0, 0)
                sG(0, 1, 1)
                sD(0, 0, 0, 0)
                sT(0, 0, 0, 0)
                for idx in range(NTOT):
                    bb, stp = divmod(idx, NSTEP)
                    if idx + 2 < NTOT:
                        b3, s3 = divmod(idx + 2, NSTEP)
                        sG(b3, s3, (idx + 2) % 3)
                    if stp == 0:
                        sI(bb)
                    if idx + 1 < NTOT:
                        sD(0, 0, (idx + 1) % 2, (idx + 1) % 3)
                    sU1(bb, stp, idx % 2)
                    if idx + 1 < NTOT:
                        b2, s2 = divmod(idx + 1, NSTEP)
                        sT(b2, s2, (idx + 1) % 2, (idx + 1) % 3)
                    sU2(bb, stp, idx % 2, idx % 3)
                    if stp == NSTEP - 1:
                        sF(bb)
                by_, ry_ = ring.get(2)
                for hf in range(2):
                    for c in range(8):
                        S.op("pe", lambda e, hf=hf, c=c: e.matmul(bank(by_ + hf)[0:NS, :], lhsT=oTs[:, c, :], rhs=wo1_sb[:, c, hf * 512:(hf + 1) * 512], start=(c == 0), stop=(c == 7)), r=[r_oTs, r_wo1], w=[ry_[hf]])
                zs1, r_zs1 = A.get(D, F32, "zs1")
                scs1, r_scs1 = A.get(16, F32, "scs1")
                S.op("dve", lambda e: e.scalar_tensor_tensor(out=zs1[0:NS, :], in0=XS[:], scalar=ALPHA, in1=bank(by_, 2)[0:NS, :], op0=ALU.mult, op1=ALU.add), r=[r_XS] + ry_, w=[r_zs1])
                layer_norm_tile(zs1[0:NS, :], r_zs1, XS[:], r_XS, NS, scs1, r_scs1, lnt1, r_lnt1)

            S.set_phase('s1')
            if with_sample:
                sample_l1()
            chk('l1')
            S.set_phase('ffn1')
            ffn(1, True)


        except _Stop:
            print('stopped at', stop)
        S.finish(out_toks)
        S.emit()
        print("ops", S.stats())
    return nc


_CACHE = {}


def _prep_shared(inp, tabs):
    f = lambda a: np.ascontiguousarray(np.asarray(a, dtype=np.float32))
    sh = {}
    sh["w_in"] = f(inp["w_in_even"][0])
    sh["pool_w"] = f(np.transpose(inp["pool_w"][0], (1, 0, 2)))
    sh["pool_scale"] = f(inp["pool_scale"][0].reshape(4, 128).T)
    sh["gn_g"] = f(inp["ret_gn_g"][0])
    sh["w_o0"] = f(inp["w_o_even"][0])
    sh["w_dq"] = f(inp["w_dq"][0])
    sh["qn_g"] = f(inp["q_norm_g"][0])
    wuq = np.asarray(inp["w_uq"][0]).reshape(512, 8, 192)
    sh["w_uq"] = f(np.concatenate([wuq[:, :, :128].reshape(512, 1024), wuq[:, :, 128:].reshape(512, 512)], axis=1))
    sh["w_dkv"] = f(inp["w_dkv"][0])
    sh["kvn_g"] = f(inp["kv_norm_g"][0])
    sh["w_ukT"] = f(np.transpose(inp["w_uk"][0], (2, 1, 0)))
    sh["w_uv"] = f(np.asarray(inp["w_uv"][0]).reshape(256, 1024))
    sh["w_o1"] = f(inp["w_o_mla"][0])
    wup = np.asarray(inp["w_up"])
    a = wup[:, :, :DFF].reshape(DEPTH, 8, 128, NJ, 128)
    b = wup[:, :, DFF:].reshape(DEPTH, 8, 128, NJ, 128)
    ab = np.concatenate([a, b], axis=4)
    sh["w_up"] = f(np.transpose(ab, (0, 3, 2, 1, 4)))
    sh["w_down"] = f(inp["w_down"])
    sh["conv_w"] = f(np.transpose(np.asarray(inp["conv_w"]).reshape(DEPTH, 3, NJ, 128), (0, 3, 2, 1)))
    sh["conv_b"] = f(np.transpose(np.asarray(inp["conv_b"]).reshape(DEPTH, NJ, 128), (0, 2, 1)))
    for k in ("ln_mix_g", "ln_mix_b", "ln_ffn_g", "ln_ffn_b"):
        sh[k] = f(inp[k])
    sh["cckv"] = np.asarray(inp["cache_ckv"], dtype=np.float32).reshape(5120, 128 * 256)
    sh["ckpe"] = np.asarray(inp["cache_kpe"], dtype=np.float32).reshape(5120, 128 * 64)
    for k, v in tabs.items():
        if isinstance(v, np.ndarray):
            sh["t_" + k] = v
    return sh


def kernel(**inp):
    tabs = host_tables()
    if "nc" not in _CACHE:
        _CACHE["nc"] = build_program(tabs)
    nc = _CACHE["nc"]
    sh = _prep_shared(inp, tabs)
    f = lambda a: np.ascontiguousarray(np.asarray(a, dtype=np.float32))
    in_maps = []
    for c in range(8):
        m = dict(sh)
        bs = slice(SB * c, SB * (c + 1))
        m["xp"] = f(inp["x_prompt"][c])
        m["xs"] = f(np.asarray(inp["x_sample"][bs]).reshape(NS, D))
        m["spool"] = f(np.asarray(inp["state_pool"][0, bs]).reshape(SB * 15, 512))
        m["sret"] = f(np.asarray(inp["state_ret"][0, bs]).reshape(SB * RH, 128, 128))
        m["sconv"] = f(np.asarray(inp["state_conv"][:, bs]).reshape(DEPTH, SB * 2, DFF))
        m["ptab"] = np.ascontiguousarray(np.asarray(inp["page_table"][bs]).T.astype(np.int32))
        in_maps.append(m)
    res = run_bass_kernel_spmd(nc, in_maps, core_ids=list(range(8)))
    R = res.results
    cat = lambda k: np.stack([np.asarray(R[c][k]) for c in range(8)])
    y_p = cat("yp")
    y_s = cat("ys").reshape(32, SL, D)
    pool_p = cat("poolp")[None]
    pool_s = cat("pools").reshape(1, 32, 15, 512)
    ret_p = cat("retp")[None]
    ret_s = cat("rets").reshape(1, 32, RH, 128, 128)
    ckv_p = cat("ckvp")[None]
    ckv_s = cat("ckvs").reshape(1, 32, SL, 256)
    kpe_p = cat("kpep")[None]
    kpe_s = cat("kpes").reshape(1, 32, SL, 64)
    conv_p = np.transpose(cat("convp"), (1, 0, 2, 3))
    conv_s = np.transpose(cat("convs").reshape(8, DEPTH, SB, 2, DFF), (1, 0, 2, 3, 4)).reshape(DEPTH, 32, 2, DFF)
    outs = (y_p, y_s, pool_p, pool_s, ret_p, ret_s, ckv_p, ckv_s, kpe_p, kpe_s, conv_p, conv_s)
    return tuple(np.ascontiguousarray(o.astype(np.float32)) for o in outs)
```
